# Optimizing a Trainium2 kernel written in Bass

```python
import jax, jax.numpy as jnp
from jax import lax
import numpy as np


D_MODEL = 2048
BATCH = 4
SEQ = 2048
DEPTH = 1
DEC_BATCH = 128
DEC_SEQ = 1
PAST_LEN = 16384
PAGE_SIZE = 128

D_MIX = D_MODEL
D_CHUNK = D_MIX // 2
CHUNK_HEADS = 8
CHUNK_HEAD_DIM = D_CHUNK // CHUNK_HEADS
CHUNK_LEN = 128
D_SSM = D_MIX - D_CHUNK
SSM_HEAD_DIM = 64
SSM_HEADS = D_SSM // SSM_HEAD_DIM
SSM_GROUPS = 2
SSM_HEADS_PER_GROUP = SSM_HEADS // SSM_GROUPS
D_STATE = 128
CONV_W = 4
CONV_DIM = D_SSM + 2 * SSM_GROUPS * D_STATE
SSD_CHUNK = 128
D_IN_PROJ = 2 * D_CHUNK + D_SSM + CONV_DIM + SSM_HEADS
D_FF = -(-8 * D_MODEL // (3 * 256)) * 256
EPS = 1e-6

kernel_name = 'hybrid_chunkmlp_ssd_decoder_step'


def rmsnorm(x, g):
    xf = x.astype(jnp.float32)
    y = xf * lax.rsqrt(jnp.mean(xf * xf, axis=-1, keepdims=True) + EPS)
    return (y * g.astype(jnp.float32)).astype(x.dtype)


def chunk_spatial_gate(u, v, w_s, b_s):
    bsz, t = v.shape[:2]
    L = min(CHUNK_LEN, t)
    n_c = -(-t // L)
    pad = n_c * L - t
    vc = jnp.pad(v, ((0, 0), (0, pad), (0, 0), (0, 0))).reshape(bsz, n_c, L, CHUNK_HEADS, CHUNK_HEAD_DIM)
    causal = jnp.tril(jnp.ones((L, L), bool))
    w = jnp.where(causal[None], w_s[:, :L, :L], 0.0).astype(vc.dtype)
    mixed = jnp.einsum('hts,bcshd->bcthd', w, vc) + b_s[:, :L].T.astype(vc.dtype)[None, None, :, :, None]
    mixed = mixed.reshape(bsz, n_c * L, CHUNK_HEADS, CHUNK_HEAD_DIM)[:, :t]
    return u * mixed


def ssd_scan(x, dt, a, bm, cm, h0):
    bsz, t = x.shape[:2]
    Q = min(SSD_CHUNK, t)
    n_c = -(-t // Q)
    pad = n_c * Q - t

    def chunks(z):
        z = jnp.pad(z, [(0, 0), (0, pad)] + [(0, 0)] * (z.ndim - 2))
        return z.reshape((bsz, n_c, Q) + z.shape[2:])

    xc, dtc, bc, cc = chunks(x), chunks(dt), chunks(bm), chunks(cm)
    acum = jnp.moveaxis(jnp.cumsum(dtc * a, axis=2), 2, -1)
    diff = acum[..., :, None] - acum[..., None, :]
    causal = jnp.tril(jnp.ones((Q, Q), bool))
    decay = jnp.exp(jnp.where(causal, diff, -jnp.inf))
    xdt = xc * dtc[..., None]
    cb = jnp.einsum('bctgn,bcsgn->bcgts', cc, bc)
    y_diag = jnp.einsum('bcgets,bcsgep->bctgep', cb[:, :, :, None] * decay, xdt)
    a_last = acum[..., -1]
    to_end = jnp.exp(a_last[..., None] - acum)
    states = jnp.einsum('bcsgn,bcges,bcsgep->bcgepn', bc, to_end, xdt)

    def step(h, inp):
        st, al = inp
        return h * jnp.exp(al)[..., None, None] + st, h

    h_final, h_prev = lax.scan(step, h0, (jnp.moveaxis(states, 1, 0), jnp.moveaxis(a_last, 1, 0)))
    h_prev = jnp.moveaxis(h_prev, 0, 1)
    y_off = jnp.einsum('bctgn,bcgepn,bcget->bctgep', cc, h_prev, jnp.exp(acum))
    y = (y_diag + y_off).reshape(bsz, n_c * Q, SSM_GROUPS, SSM_HEADS_PER_GROUP, SSM_HEAD_DIM)[:, :t]
    return y, h_final


def hybrid_layer(x, conv_prefix, h0, norm_mix_g, w_in, chunk_v_norm_g, chunk_w_s, chunk_b_s,
                 chunk_out_norm_g, ssd_conv_w, ssd_conv_b, ssd_dt_bias, ssd_a_log, ssd_d,
                 ssd_norm_g, w_out, norm_ffn_g, w_gate, w_up, w_down):
    bsz, t, _ = x.shape
    n = rmsnorm(x, norm_mix_g)
    proj = n @ w_in
    splits = [D_CHUNK, 2 * D_CHUNK, 2 * D_CHUNK + D_SSM, 2 * D_CHUNK + D_SSM + CONV_DIM]
    u_raw, v_raw, z, xbc, dt_raw = jnp.split(proj, splits, axis=-1)

    u = jax.nn.gelu(u_raw).reshape(bsz, t, CHUNK_HEADS, CHUNK_HEAD_DIM)
    v_rows = rmsnorm(jax.nn.gelu(v_raw), chunk_v_norm_g)
    y_a = chunk_spatial_gate(u, v_rows.reshape(bsz, t, CHUNK_HEADS, CHUNK_HEAD_DIM), chunk_w_s, chunk_b_s)
    y_a = rmsnorm(y_a.reshape(bsz, t, D_CHUNK), chunk_out_norm_g)

    padded = jnp.concatenate([conv_prefix.astype(xbc.dtype), xbc], axis=1)
    conv = ssd_conv_b
    for k in range(CONV_W):
        conv = conv + padded[:, k:k + t] * ssd_conv_w[k]
    xbc_act = jax.nn.silu(conv)
    new_conv = padded[:, -(CONV_W - 1):]
    xs, bm, cm = jnp.split(xbc_act, [D_SSM, D_SSM + SSM_GROUPS * D_STATE], axis=-1)
    xs = xs.astype(jnp.float32).reshape(bsz, t, SSM_GROUPS, SSM_HEADS_PER_GROUP, SSM_HEAD_DIM)
    bm = bm.astype(jnp.float32).reshape(bsz, t, SSM_GROUPS, D_STATE)
    cm = cm.astype(jnp.float32).reshape(bsz, t, SSM_GROUPS, D_STATE)
    dt = jax.nn.softplus(dt_raw.astype(jnp.float32) + ssd_dt_bias.astype(jnp.float32))
    dt = dt.reshape(bsz, t, SSM_GROUPS, SSM_HEADS_PER_GROUP)
    a = -jnp.exp(ssd_a_log.astype(jnp.float32)).reshape(SSM_GROUPS, SSM_HEADS_PER_GROUP)
    h0g = h0.astype(jnp.float32).reshape(bsz, SSM_GROUPS, SSM_HEADS_PER_GROUP, SSM_HEAD_DIM, D_STATE)
    y_b, h_final = ssd_scan(xs, dt, a, bm, cm, h0g)
    y_b = y_b + ssd_d.astype(jnp.float32).reshape(SSM_GROUPS, SSM_HEADS_PER_GROUP)[..., None] * xs
    y_b = y_b.reshape(bsz, t, D_SSM) * jax.nn.silu(z.astype(jnp.float32))
    y_b = y_b.reshape(bsz, t, SSM_GROUPS, D_SSM // SSM_GROUPS)
    y_b = y_b * lax.rsqrt(jnp.mean(y_b * y_b, axis=-1, keepdims=True) + EPS)
    y_b = (y_b.reshape(bsz, t, D_SSM) * ssd_norm_g.astype(jnp.float32)).astype(x.dtype)

    h = x + jnp.concatenate([y_a, y_b], axis=-1) @ w_out
    m = rmsnorm(h, norm_ffn_g)
    h = h + (jax.nn.silu(m @ w_gate) * (m @ w_up)) @ w_down
    new_ssm = h_final.reshape(bsz, SSM_HEADS, SSM_HEAD_DIM, D_STATE)
    return h, new_conv, new_ssm, v_rows


def setup_inputs(seed: int = 0) -> dict:
    key = jax.random.key(seed)
    ks = jax.random.split(key, 24)
    f32 = jnp.float32

    def nrm(k, shape, scale):
        return jax.random.normal(k, shape, f32) * scale

    dt0 = jnp.exp(jax.random.uniform(ks[12], (DEPTH, SSM_HEADS), f32, np.log(1e-3), np.log(1e-1)))
    return {
        'x_prompt': nrm(ks[0], (BATCH, SEQ, D_MODEL), 1.0),
        'x_sample': nrm(ks[1], (DEC_BATCH, DEC_SEQ, D_MODEL), 1.0),
        'state_conv': nrm(ks[2], (DEPTH, DEC_BATCH, CONV_W - 1, CONV_DIM), 1.0),
        'state_ssm': nrm(ks[3], (DEPTH, DEC_BATCH, SSM_HEADS, SSM_HEAD_DIM, D_STATE), 0.3),
        'norm_mix_g': 1.0 + nrm(ks[4], (DEPTH, D_MODEL), 0.05),
        'w_in': nrm(ks[5], (DEPTH, D_MODEL, D_IN_PROJ), D_MODEL ** -0.5),
        'chunk_v_norm_g': 1.0 + nrm(ks[6], (DEPTH, D_CHUNK), 0.05),
        'chunk_w_s': nrm(ks[7], (DEPTH, CHUNK_HEADS, CHUNK_LEN, CHUNK_LEN), CHUNK_LEN ** -0.5),
        'chunk_b_s': 1.0 + nrm(ks[8], (DEPTH, CHUNK_HEADS, CHUNK_LEN), 0.1),
        'chunk_out_norm_g': 1.0 + nrm(ks[9], (DEPTH, D_CHUNK), 0.05),
        'ssd_conv_w': nrm(ks[10], (DEPTH, CONV_W, CONV_DIM), CONV_W ** -0.5),
        'ssd_conv_b': nrm(ks[11], (DEPTH, CONV_DIM), 0.02),
        'ssd_dt_bias': dt0 + jnp.log(-jnp.expm1(-dt0)),
        'ssd_a_log': jnp.log(jax.random.uniform(ks[13], (DEPTH, SSM_HEADS), f32, 1.0, 16.0)),
        'ssd_d': 1.0 + nrm(ks[14], (DEPTH, SSM_HEADS), 0.1),
        'ssd_norm_g': 1.0 + nrm(ks[15], (DEPTH, D_SSM), 0.05),
        'w_out': nrm(ks[16], (DEPTH, D_MIX, D_MODEL), D_MIX ** -0.5),
        'norm_ffn_g': 1.0 + nrm(ks[17], (DEPTH, D_MODEL), 0.05),
        'w_gate': nrm(ks[18], (DEPTH, D_MODEL, D_FF), D_MODEL ** -0.5),
        'w_up': nrm(ks[19], (DEPTH, D_MODEL, D_FF), D_MODEL ** -0.5),
        'w_down': nrm(ks[20], (DEPTH, D_FF, D_MODEL), D_FF ** -0.5),
        'norm_final_g': 1.0 + nrm(ks[21], (D_MODEL,), 0.05),
    }


def reference(x_prompt, x_sample, state_conv, state_ssm, norm_mix_g, w_in, chunk_v_norm_g,
              chunk_w_s, chunk_b_s, chunk_out_norm_g, ssd_conv_w, ssd_conv_b, ssd_dt_bias,
              ssd_a_log, ssd_d, ssd_norm_g, w_out, norm_ffn_g, w_gate, w_up, w_down, norm_final_g):
    hp, hs = x_prompt, x_sample
    conv_p, ssm_p, conv_s, ssm_s, v_s = [], [], [], [], []
    for l in range(DEPTH):
        params = (norm_mix_g[l], w_in[l], chunk_v_norm_g[l], chunk_w_s[l], chunk_b_s[l],
                  chunk_out_norm_g[l], ssd_conv_w[l], ssd_conv_b[l], ssd_dt_bias[l], ssd_a_log[l],
                  ssd_d[l], ssd_norm_g[l], w_out[l], norm_ffn_g[l], w_gate[l], w_up[l], w_down[l])
        zero_conv = jnp.zeros((hp.shape[0], CONV_W - 1, CONV_DIM), hp.dtype)
        zero_ssm = jnp.zeros((hp.shape[0], SSM_HEADS, SSM_HEAD_DIM, D_STATE), jnp.float32)
        hp, cp, sp, _ = hybrid_layer(hp, zero_conv, zero_ssm, *params)
        hs, cs, ss, vs = hybrid_layer(hs, state_conv[l], state_ssm[l], *params)
        conv_p.append(cp)
        ssm_p.append(sp)
        conv_s.append(cs)
        ssm_s.append(ss)
        v_s.append(vs)
    y_prompt = rmsnorm(hp, norm_final_g)
    y_sample = rmsnorm(hs, norm_final_g)
    return (y_prompt, y_sample, jnp.stack(conv_p), jnp.stack(ssm_p), jnp.stack(conv_s), jnp.stack(ssm_s), jnp.stack(v_s))
```

```python
import numpy as np
from contextlib import ExitStack
import concourse.bass as bass
import concourse.mybir as mybir
from concourse.bass_utils import run_bass_kernel_spmd

F32 = mybir.dt.float32
F32R = mybir.dt.float32r
BF16 = mybir.dt.bfloat16
AF = mybir.ActivationFunctionType
ALU = mybir.AluOpType
AX = mybir.AxisListType

ENGS = ["pe", "act", "dve", "pool", "sp"]
NDMASEM = 12

DEBUG = {}
STOP_AFTER = None
SSD_LEVEL = 9


class Op:
    __slots__ = ("eng", "fn", "r", "w", "dma", "deps", "sig", "sem", "prev_use", "waits", "need_sig", "xdeps")

    def __init__(self, eng, fn, r, w, dma):
        self.eng, self.fn, self.r, self.w, self.dma = eng, fn, tuple(r), tuple(w), dma
        self.deps = set()
        self.xdeps = set()
        self.sig = None
        self.sem = None
        self.prev_use = 0
        self.waits = []
        self.need_sig = False


class Prog:
    def __init__(self):
        self.ops = []
        self.pending_bar = {}

    def op(self, eng, fn, r=(), w=(), dma=False):
        w = list(w) + [k for k in r if isinstance(k, tuple) and k and k[0] == "ps" and k not in w]
        o = Op(eng, fn, r, w, dma)
        if eng in self.pending_bar:
            o.xdeps |= self.pending_bar.pop(eng)
        self.ops.append(o)
        return o

    def barrier(self):
        last = {}
        dmas = []
        for i, o in enumerate(self.ops):
            if o.dma:
                dmas.append(i)
            elif o.fn is not None:
                last[o.eng] = i
        s = set(last.values()) | set(dmas)
        for e in ENGS:
            self.pending_bar[e] = set(s) | self.pending_bar.get(e, set())

    def resolve(self):
        ops = self.ops
        last_w = {}
        readers = {}
        for i, o in enumerate(ops):
            deps = set(o.xdeps)
            for k in o.r:
                if k in last_w:
                    deps.add(last_w[k])
            for k in o.w:
                if k in last_w:
                    deps.add(last_w[k])
                deps.update(readers.get(k, ()))
            deps.discard(i)
            keep = set()
            for d in deps:
                od = ops[d]
                if od.dma:
                    keep.add(d)
                elif od.eng == o.eng and not o.dma:
                    if o.eng == "pe":
                        continue
                    if d in o.xdeps:
                        continue
                    if set(od.w) & set(o.r):
                        keep.add(d)
                else:
                    keep.add(d)
            o.deps = keep
            for d in keep:
                ops[d].need_sig = True
            for k in o.r:
                readers.setdefault(k, []).append(i)
            for k in o.w:
                last_w[k] = i
                readers[k] = []
        cnt = {e: 0 for e in ENGS}
        dma_n = {e: 0 for e in ENGS}
        dma_use = {}
        for i, o in enumerate(ops):
            if o.dma:
                slot = (o.eng, dma_n[o.eng] % NDMASEM)
                dma_n[o.eng] += 1
                u = dma_use.get(slot, 0)
                o.prev_use = u
                dma_use[slot] = u + 1
                o.sem = slot
                o.sig = 16 * (u + 1)
            elif o.need_sig:
                cnt[o.eng] += 1
                o.sem = o.eng
                o.sig = cnt[o.eng]
        seen = {e: {} for e in ENGS}
        for i, o in enumerate(ops):
            sd = seen[o.eng]
            need = {}
            if o.dma and o.prev_use > 0:
                need[o.sem] = 16 * o.prev_use
            for d in o.deps:
                od = ops[d]
                need[od.sem] = max(need.get(od.sem, 0), od.sig)
            o.waits = []
            for s, v in need.items():
                if sd.get(s, 0) >= v:
                    continue
                sd[s] = v
                o.waits.append((s, v))

    def emit(self, sems, block):
        ops = self.ops

        def run(eng_name):
            def body(e):
                for o in ops:
                    if o.eng != eng_name:
                        continue
                    for s, v in o.waits:
                        e.wait_ge(sems[s], v)
                    if o.fn is None:
                        continue
                    ins = o.fn(e)
                    if o.dma:
                        ins.then_inc(sems[o.sem], 16)
                    elif o.sig is not None:
                        ins.then_inc(sems[o.sem], 1)
            return body

        block.sync(run("sp"))
        block.tensor(run("pe"))
        block.scalar(run("act"))
        block.vector(run("dve"))
        block.gpsimd(run("pool"))


DM = 2048
DIN = 4624
DFF = 5632
TM = 1024
NS = 16
T = TM + NS
EPS = 1e-6
COL_U, COL_V, COL_Z, COL_X, COL_B, COL_C, COL_DT = 0, 1024, 2048, 3072, 4096, 4352, 4608

C_ID, C_TRI, C_U, C_ONE = 0, 128, 256, 384
C_CWB, C_GON, C_ALOG, C_DTB, C_D, C_W00, C_B0, C_FLAG = 512, 572, 580, 596, 612, 628, 636, 644
C_DPP, C_SGPP, C_DTBPP, C_ALPP = 648, 656, 664, 665
CSTW = 672
C2_BS, C2_RSEL = 0, 1024
CST2W = 2048

AW = 51000


class Bump:
    def __init__(self, arena, start, end):
        self.arena, self.start, self.end, self.off = arena, start, end, start
        self.peak = start

    def reset(self, to=None):
        self.off = self.start if to is None else to

    def mark(self):
        return self.off

    def _take(self, words):
        o = self.off
        self.off += words
        assert self.off <= self.end, ("arena overflow", self.off, self.end)
        self.peak = max(self.peak, self.off)
        return o

    def f32(self, shape):
        n = int(np.prod(shape))
        o = self._take(n)
        v = self.arena[:, o:o + n]
        return _shape(v, shape)

    def bf16(self, shape):
        n = int(np.prod(shape))
        w = (n + 1) // 2
        o = self._take(w)
        v = self.arena[:, o:o + w].bitcast(BF16)[:, 0:n]
        return _shape(v, shape)


def _shape(v, shape):
    if len(shape) == 1:
        return v
    if len(shape) == 2:
        return v.rearrange("p (a b) -> p a b", a=shape[0])
    if len(shape) == 3:
        return v.rearrange("p (a b c) -> p a b c", a=shape[0], b=shape[1])
    raise ValueError(shape)


def build_program():
    nc = bass.Bass("TRN2", target_bir_lowering=False)

    def din(name, shape):
        return nc.dram_tensor(name, shape, F32, kind="ExternalInput").ap()

    def dout(name, shape):
        return nc.dram_tensor(name, shape, F32, kind="ExternalOutput").ap()

    xm = din("xm", [TM, DM])
    xprev = din("xprev", [TM, DM])
    xsmp = din("xsmp", [NS, DM])
    sconv = din("sconv", [NS, 3, 1536])
    sssm = din("sssm", [NS, 1024, 128])
    w_in = din("w_in", [DM, DIN])
    w_out = din("w_out", [DM, DM])
    w_gate = din("w_gate", [DM, DFF])
    w_up = din("w_up", [DM, DFF])
    w_down = din("w_down", [DFF, DM])
    cst_d = din("cst", [128, CSTW])
    cst2_d = din("cst2", [128, CST2W])
    ws_d = din("ws", [8, 128, 128])
    g_mix = din("g_mix", [DM])
    g_ffn = din("g_ffn", [DM])
    g_fin = din("g_fin", [DM])
    vn_g = din("vn_g", [1024])
    ssd_g = din("ssd_g", [1024])

    y_m = dout("y_m", [TM, DM])
    y_s = dout("y_s", [NS, DM])
    nconv_p = dout("nconv_p", [3, 1536])
    nssm_p = dout("nssm_p", [1024, 128])
    nconv_s = dout("nconv_s", [NS, 3, 1536])
    nssm_s = dout("nssm_s", [NS, 1024, 128])
    v_s = dout("v_s", [NS, 1024])
    dbg_d = {}
    for name, (shape, dty) in DEBUG.items():
        dbg_d[name] = nc.dram_tensor("dbg_" + name, list(shape), dty, kind="ExternalOutput").ap()

    P = Prog()
    with ExitStack() as es:
        arena = es.enter_context(nc.sbuf_tensor("arena", [128, AW], F32))
        rhsE = es.enter_context(nc.sbuf_tensor("rhsE", [128, 2048], F32))
        U32 = es.enter_context(nc.sbuf_tensor("U32", [128, 128], F32))
        pb = [es.enter_context(nc.psum_tensor(f"pb{i}", [128, 512], F32)) for i in range(8)]
        sems = {}
        for e in ENGS:
            sems[e] = es.enter_context(nc.semaphore("s_" + e))
        for e in ("sp", "pool"):
            for i in range(NDMASEM):
                sems[(e, i)] = es.enter_context(nc.semaphore(f"d_{e}_{i}"))
        block = es.enter_context(nc.Block())

        arena = arena[:, :]
        rhsEr = rhsE[:, :].bitcast(F32R)
        U32r = U32[:, :].bitcast(F32R)
        pbf = [p[:, :] for p in pb]
        pbb = [p[:, :].bitcast(BF16) for p in pb]

        fx = Bump(arena, 0, AW)
        CST = fx.f32([CSTW])
        identF = CST[:, C_ID:C_ID + 128]
        triF = CST[:, C_TRI:C_TRI + 128]
        UF = CST[:, C_U:C_U + 128]
        onesF = CST[:, C_ONE:C_ONE + 128]
        cwb = CST[:, C_CWB:C_CWB + 60].rearrange("p (j k) -> p j k", j=12)
        gon_pp = CST[:, C_GON:C_GON + 8]
        alog_bc = CST[:, C_ALOG:C_ALOG + 16]
        dtb_bc = CST[:, C_DTB:C_DTB + 16]
        D_bc = CST[:, C_D:C_D + 16]
        w00_bc = CST[:, C_W00:C_W00 + 8]
        b0_bc = CST[:, C_B0:C_B0 + 8]
        flag = CST[:, C_FLAG:C_FLAG + 1]
        D_pp = CST[:, C_DPP:C_DPP + 8]
        sg_pp = CST[:, C_SGPP:C_SGPP + 8]
        dtb_pp = CST[:, C_DTBPP:C_DTBPP + 1]
        al_pp = CST[:, C_ALPP:C_ALPP + 1]
        identB = fx.bf16([128])
        onesB = fx.bf16([128])
        aneg = fx.f32([16])
        hT = fx.f32([1024])
        hTb = fx.bf16([1024])
        prefix = fx.f32([12, 3])
        ncv = fx.f32([12, 3])
        nT = fx.bf16([16, T])
        YT = fx.bf16([16, T])
        r2_start = fx.off - (16 * T) // 2
        r2_end = fx.off
        S0 = fx.off
        sc = Bump(arena, S0, AW)
        r2 = Bump(arena, r2_start, r2_end)

        KC = ("cst",)

        P.op("sp", lambda e: e.dma_start(out=CST, in_=cst_d[:, :]), w=[KC], dma=True)
        P.op("dve", lambda e: e.tensor_copy(out=identB, in_=identF), r=[KC], w=["identB"])
        P.op("dve", lambda e: e.memset(onesB, 1.0), w=["onesB"])
        P.op("dve", lambda e: e.tensor_copy(out=U32r, in_=UF), r=[KC], w=["U32"])
        P.op("act", lambda e: e.activation(out=aneg, in_=alog_bc, func=AF.Exp), r=[KC], w=["aneg0"])
        P.op("dve", lambda e: e.tensor_scalar(out=aneg, in0=aneg, scalar1=-1.0, scalar2=None, op0=ALU.mult), r=["aneg0"], w=["aneg"])
        P.op("dve", lambda e: e.memset(hT, 0.0), w=["hT"])
        P.op("dve", lambda e: e.memset(hTb, 0.0), w=["hTb"])

        def rms_transpose(xt, kx, rows, xn, kxn, st, kst, gbc, kg, dstT, col0, kdst, width, tag):
            nk = width // 128
            P.op("act", lambda e: e.activation(out=xn[:rows, :], in_=xt[:rows, :], func=AF.Square, accum_out=st[:rows, 0:1]),
                 r=[kx], w=[kxn, kst])
            P.op("act", lambda e: e.activation(out=st[:rows, 1:2], in_=st[:rows, 0:1], func=AF.Ln, scale=1.0 / width, bias=EPS),
                 r=[kst], w=[kst])
            P.op("act", lambda e: e.activation(out=st[:rows, 2:3], in_=st[:rows, 1:2], func=AF.Exp, scale=-0.5),
                 r=[kst], w=[kst])
            P.op("dve", lambda e: e.scalar_tensor_tensor(out=xn[:rows, :], in0=xt[:rows, :], scalar=st[:rows, 2:3], op0=ALU.mult,
                                                         in1=gbc[:rows, :], op1=ALU.mult),
                 r=[kx, kst, kg, kxn], w=[kxn])
            for b8 in range(nk // 8):
                bank = tag % 2 if nk == 8 else b8
                psv = pbb[bank][:, 0:8 * rows].rearrange("p (a b) -> p a b", a=8)

                def tr(e, b8=b8, psv=psv):
                    ins = None
                    for j in range(8):
                        k = b8 * 8 + j
                        ins = e.transpose(out=psv[:, j, :], in_=xn[:rows, k * 128:(k + 1) * 128], identity=identB[:rows, :rows])
                    return ins
                P.op("pe", tr, r=[kxn, "identB"], w=[("ps", bank)])
                dst = dstT[:, b8 * 8:(b8 + 1) * 8, col0:col0 + rows]
                if b8 == 0:
                    P.op("act", lambda e, dst=dst, psv=psv: e.activation(out=dst, in_=psv, func=AF.Copy), r=[("ps", bank)], w=[kdst])
                else:
                    P.op("dve", lambda e, dst=dst, psv=psv: e.tensor_copy(out=dst, in_=psv), r=[("ps", bank)], w=[kdst])

        def mm_group(out, pairs, r, w):
            def fn(e):
                ins = None
                n = len(pairs)
                for i, (l, rh) in enumerate(pairs):
                    ins = e.matmul(out, lhsT=l, rhs=rh, start=(i == 0), stop=(i == n - 1))
                return ins
            P.op("pe", fn, r=r, w=w)

        def load_w(dst, src_ap, key):
            P.op("pool", lambda e: e.dma_start(out=dst, in_=src_ap), w=[key], dma=True)

        def wview(wap, c0, ncols):
            return wap[:, c0:c0 + ncols].rearrange("(kt p) c -> p kt c", p=128)

        def conv_silu(rawpad, krp, acc, kacc, j, ntok, dst, kdst):
            P.op("dve", lambda e: e.tensor_scalar(out=acc[:, 0:ntok], in0=rawpad[:, 0:ntok], scalar1=cwb[:, j, 0:1], scalar2=cwb[:, j, 4:5],
                                                  op0=ALU.mult, op1=ALU.add), r=[krp, KC], w=[kacc])
            for k in (1, 2, 3):
                P.op("dve", lambda e, k=k: e.scalar_tensor_tensor(out=acc[:, 0:ntok], in0=rawpad[:, k:k + ntok], scalar=cwb[:, j, k:k + 1],
                                                                  op0=ALU.mult, in1=acc[:, 0:ntok], op1=ALU.add), r=[krp, kacc, KC], w=[kacc])
            P.op("act", lambda e: e.activation(out=dst, in_=acc[:, 0:ntok], func=AF.Silu), r=[kacc], w=[kdst])

        def ssd_temps(b):
            t = {}
            t["sm"] = b.f32([96])
            t["ex"] = b.f32([48])
            t["LT"] = b.f32([2048])
            t["MT"] = b.bf16([16, 128])
            t["xs_tok"] = b.bf16([16, 64])
            t["xdt"] = b.bf16([16, 64])
            t["xdtw"] = b.bf16([16, 64])
            t["B_tok"] = b.bf16([256])
            t["cbm"] = b.f32([2, 128])
            t["y1"] = b.f32([16, 64])
            t["t2"] = b.f32([16, 64])
            t["yb"] = b.bf16([1024])
            t["gn"] = b.f32([8])
            return t

        def ssd_chunk(t, xT, c, dtraw_c, kdt, main, ztok_c=None, ssdg_bc=None):
            sm, ex = t["sm"], t["ex"]
            cs = slice(c * 128, (c + 1) * 128)
            kx = ("xT",)
            P.op("dve", lambda e: e.tensor_tensor(out=sm[:, 0:16], in0=dtraw_c, in1=dtb_bc, op=ALU.add), r=[kdt, KC], w=["sm0"])
            P.op("act", lambda e: e.activation(out=sm[:, 16:32], in_=sm[:, 0:16], func=AF.Exp), r=["sm0"], w=["sm1"])
            P.op("act", lambda e: e.activation(out=sm[:, 32:48], in_=sm[:, 16:32], func=AF.Ln, bias=1.0), r=["sm1"], w=["dtc"])
            dtc = sm[:, 32:48]
            dta = sm[:, 48:64]
            P.op("dve", lambda e: e.tensor_tensor(out=dta, in0=dtc, in1=aneg, op=ALU.mult), r=["dtc", "aneg"], w=["dta"])

            def small(e):
                e.matmul(pbf[0][:, 0:16], lhsT=UF, rhs=dta, start=True, stop=True)
                ins = e.matmul(pbf[0][:, 16:32], lhsT=onesF, rhs=dta, start=True, stop=True)
                if main:
                    ins = e.matmul(pbf[0][:, 32:48], lhsT=triF, rhs=dta, start=True, stop=True)
                return ins
            P.op("pe", small, r=["dta", KC], w=[("ps", 0)])
            nex = 48 if main else 32
            P.op("act", lambda e: e.activation(out=ex[:, 0:nex], in_=pbf[0][:, 0:nex], func=AF.Exp), r=[("ps", 0)], w=["ex"])
            toend, dec, ea = ex[:, 0:16], ex[:, 16:32], ex[:, 32:48]

            psx = pbb[1][:, 0:1024].rearrange("p (a b) -> p a b", a=8)

            def trx(e):
                ins = None
                for q in range(8):
                    ins = e.transpose(out=psx[:, q, :], in_=xT[:, q, cs], identity=identB)
                return ins
            P.op("pe", trx, r=[kx, "identB"], w=[("ps", 1)])
            psB = pbb[2][:, 0:256].rearrange("p (a b) -> p a b", a=2)

            def trb(e):
                ins = None
                for g in range(2):
                    ins = e.transpose(out=psB[:, g, :], in_=xT[:, 8 + g, cs], identity=identB)
                return ins
            P.op("pe", trb, r=[kx, "identB"], w=[("ps", 2)])
            psx3 = pbb[1][:, 0:1024].rearrange("p (a b) -> p a b", a=16)
            w2 = sm[:, 64:80]
            P.op("dve", lambda e: e.tensor_tensor(out=w2, in0=dtc, in1=toend, op=ALU.mult), r=["dtc", "ex"], w=["w2"])
            P.op("dve", lambda e: e.tensor_tensor(out=t["xdtw"], in0=psx3, in1=w2.unsqueeze(2).broadcast_to([128, 16, 64]), op=ALU.mult),
                 r=[("ps", 1), "w2"], w=["xdtw"])
            P.op("act", lambda e: e.activation(out=t["B_tok"], in_=pbb[2][:, 0:256], func=AF.Copy), r=[("ps", 2)], w=["B_tok"])
            if main and SSD_LEVEL == 0:
                return
            if main:
                P.op("dve", lambda e: e.tensor_tensor(out=t["xdt"], in0=psx3, in1=dtc.unsqueeze(2).broadcast_to([128, 16, 64]), op=ALU.mult),
                     r=[("ps", 1), "dtc"], w=["xdt"])
                P.op("act", lambda e: e.activation(out=t["xs_tok"], in_=psx3, func=AF.Copy), r=[("ps", 1)], w=["xs_tok"])
                for e16 in range(16 if SSD_LEVEL >= 2 else 0):
                    P.op("dve", lambda e, e16=e16: e.tensor_scalar(out=rhsEr[:, e16 * 128:(e16 + 1) * 128], in0=triF, scalar1=dta[:, e16:e16 + 1],
                                                                   scalar2=None, op0=ALU.mult), r=["dta", KC], w=[("rhsE", e16 // 4)])
                for i in range(4 if SSD_LEVEL >= 2 else 0):
                    P.op("pe", lambda e, i=i: e.matmul(pbf[3 + i], lhsT=U32r, rhs=rhsEr[:, i * 512:(i + 1) * 512], start=True, stop=True),
                         r=[("rhsE", i), "U32"], w=[("ps", 3 + i)])
                    P.op("act", lambda e, i=i: e.activation(out=t["LT"][:, i * 512:(i + 1) * 512], in_=pbf[3 + i], func=AF.Exp),
                         r=[("ps", 3 + i)], w=[("LT", i)])
                psc = pbf[7][:, 0:256].rearrange("p (a b) -> p a b", a=2)
                if SSD_LEVEL < 3:
                    return

                def cbf(e):
                    ins = None
                    for g in range(2):
                        ins = e.matmul(psc[:, g, :], lhsT=xT[:, 8 + g, cs], rhs=xT[:, 10 + g, cs], start=True, stop=True)
                    return ins
                P.op("pe", cbf, r=[kx], w=[("ps", 7)])
                P.op("dve", lambda e: e.tensor_tensor(out=t["cbm"], in0=psc, in1=triF.unsqueeze(1).broadcast_to([128, 2, 128]), op=ALU.mult),
                     r=[("ps", 7), KC], w=["cbm"])
                LT3 = t["LT"].rearrange("p (a b) -> p a b", a=16)
                for g in range(2):
                    P.op("dve", lambda e, g=g: e.tensor_tensor(out=t["MT"][:, g * 8:(g + 1) * 8, :], in0=LT3[:, g * 8:(g + 1) * 8, :],
                                                               in1=t["cbm"][:, g:g + 1, :].broadcast_to([128, 8, 128]), op=ALU.mult),
                         r=[("LT", 2 * g), ("LT", 2 * g + 1), "cbm"], w=[("MT", g)])
                if SSD_LEVEL < 4:
                    return
                for g in range(2):
                    def yd(e, g=g):
                        ins = None
                        for j in range(8):
                            e16 = g * 8 + j
                            ins = e.matmul(pbf[3 + g][:, j * 64:(j + 1) * 64], lhsT=t["MT"][:, e16, :], rhs=t["xdt"][:, e16, :], start=True, stop=True)
                        return ins
                    P.op("pe", yd, r=[("MT", g), "xdt"], w=[("ps", 3 + g)])
                    P.op("pe", lambda e, g=g: e.matmul(pbf[5 + g], lhsT=xT[:, 10 + g, cs], rhs=hTb[:, g * 512:(g + 1) * 512], start=True, stop=True),
                         r=[kx, "hTb"], w=[("ps", 5 + g)])
                y1 = t["y1"]
                for g in range(2):
                    y1g = y1[:, g * 8:(g + 1) * 8, :]
                    P.op("dve", lambda e, g=g, y1g=y1g: e.tensor_tensor(out=y1g, in0=pbf[5 + g].rearrange("p (a b) -> p a b", a=8),
                                                                        in1=ea[:, g * 8:(g + 1) * 8].unsqueeze(2).broadcast_to([128, 8, 64]), op=ALU.mult),
                         r=[("ps", 5 + g), "ex"], w=[("y1", g)])
                    P.op("dve", lambda e, g=g, y1g=y1g: e.tensor_tensor(out=y1g, in0=pbf[3 + g].rearrange("p (a b) -> p a b", a=8), in1=y1g, op=ALU.add),
                         r=[("ps", 3 + g), ("y1", g)], w=[("y1", g)])
                P.op("pool", lambda e: e.tensor_tensor(out=t["t2"], in0=t["xs_tok"], in1=D_bc.unsqueeze(2).broadcast_to([128, 16, 64]), op=ALU.mult),
                     r=["xs_tok", KC], w=["t2"])
                P.op("dve", lambda e: e.tensor_tensor(out=y1, in0=y1, in1=t["t2"], op=ALU.add), r=[("y1", 0), ("y1", 1), "t2"], w=[("y1", 0), ("y1", 1)])
                y1f = y1.rearrange("p a b -> p (a b)")
                P.op("dve", lambda e: e.tensor_tensor(out=y1f, in0=y1f, in1=ztok_c, op=ALU.mult), r=[("y1", 0), ("y1", 1), "z_tok"], w=[("y1", 0), ("y1", 1)])
                if SSD_LEVEL < 5:
                    return
                gn = t["gn"]
                t2f = t["t2"].rearrange("p a b -> p (a b)")
                for g in range(2):
                    P.op("act", lambda e, g=g: e.activation(out=t2f[:, g * 512:(g + 1) * 512], in_=y1f[:, g * 512:(g + 1) * 512], func=AF.Square,
                                                            accum_out=gn[:, g:g + 1]), r=[("y1", g)], w=["t2", ("gn", g)])
                P.op("act", lambda e: e.activation(out=gn[:, 2:4], in_=gn[:, 0:2], func=AF.Ln, scale=1.0 / 512, bias=EPS), r=[("gn", 0), ("gn", 1)], w=["gn2"])
                P.op("act", lambda e: e.activation(out=gn[:, 4:6], in_=gn[:, 2:4], func=AF.Exp, scale=-0.5), r=["gn2"], w=["gn4"])
                for g in range(2):
                    P.op("dve", lambda e, g=g: e.scalar_tensor_tensor(out=t["yb"][:, g * 512:(g + 1) * 512], in0=y1f[:, g * 512:(g + 1) * 512],
                                                                      scalar=gn[:, 4 + g:5 + g], op0=ALU.mult, in1=ssdg_bc[:, g * 512:(g + 1) * 512], op1=ALU.mult),
                         r=[("y1", g), "gn4", "ssdg"], w=[("yb", g)])
                if SSD_LEVEL < 6:
                    return
                psy = pbb[2][:, 0:1024].rearrange("p (a b) -> p a b", a=8)

                def try_(e):
                    ins = None
                    for q in range(8):
                        ins = e.transpose(out=psy[:, q, :], in_=t["yb"][:, q * 128:(q + 1) * 128], identity=identB)
                    return ins
                P.op("pe", try_, r=[("yb", 0), ("yb", 1), "identB"], w=[("ps", 2)])
                P.op("act", lambda e: e.activation(out=YT[:, 8:16, cs], in_=psy, func=AF.Copy), r=[("ps", 2)], w=[("YTb", c)])
            sb = (7, 1)
            for g in range(2):
                P.op("pe", lambda e, g=g: e.matmul(pbf[sb[g]], lhsT=t["B_tok"][:, g * 128:(g + 1) * 128], rhs=t["xdtw"].rearrange("p a b -> p (a b)")[:, g * 512:(g + 1) * 512],
                                                   start=True, stop=True), r=["B_tok", "xdtw"], w=[("ps", sb[g])])
            hT3 = hT.rearrange("p (a b) -> p a b", a=16)
            P.op("dve", lambda e: e.tensor_tensor(out=hT3, in0=hT3, in1=dec.unsqueeze(2).broadcast_to([128, 16, 64]), op=ALU.mult), r=["hT", "ex"], w=["hT"])
            for g in range(2):
                P.op("dve", lambda e, g=g: e.tensor_tensor(out=hT[:, g * 512:(g + 1) * 512], in0=pbf[sb[g]], in1=hT[:, g * 512:(g + 1) * 512], op=ALU.add),
                     r=[("ps", sb[g]), "hT"], w=["hT"])
            P.op("act", lambda e: e.activation(out=hTb, in_=hT, func=AF.Copy), r=["hT"], w=["hTb"])

        dbg_ops = []

        def dbg_dump(name, ap, keys):
            if name in dbg_d:
                dbg_ops.append((name, ap, keys))

        def finish():
            P.barrier()
            outs = []
            for name, ap, keys in dbg_ops:
                k = ("dbgout", name)
                P.op("sp", lambda e, name=name, ap=ap: e.dma_start(out=dbg_d[name], in_=ap), r=keys, w=[k], dma=True)
                outs.append(k)
            P.op("sp", None, r=outs + OUTKEYS)
            P.resolve()
            P.emit(sems, block)

        OUTKEYS = []

        sc.reset()
        X = [sc.f32([DM]) for _ in range(2)]
        XN = [sc.bf16([DM]) for _ in range(2)]
        ST = [sc.f32([4]) for _ in range(2)]
        gbc = sc.f32([DM])
        m_common = sc.mark()
        nPT = YT[:, :, 0:TM]
        P.op("sp", lambda e, gbc=gbc: e.dma_start(out=gbc, in_=g_mix.partition_broadcast(128)), w=["gbc"], dma=True)
        for tt in range(8):
            i = tt % 2
            P.op("sp", lambda e, tt=tt, i=i: e.dma_start(out=X[i], in_=xprev[tt * 128:(tt + 1) * 128, :]), w=[("X", i)], dma=True)
            rms_transpose(X[i], ("X", i), 128, XN[i], ("XN", i), ST[i], ("ST", i), gbc, "gbc", nPT, tt * 128, ("nPT", tt), DM, tt)
        Wb = [sc.bf16([16, 512]) for _ in range(2)]
        Wdt = sc.bf16([16, 16])
        xTp = sc.bf16([10, TM])
        dtraw_p = sc.f32([8, 16])
        m_after_dt = sc.mark()
        rawpad = [sc.f32([TM + 4]) for _ in range(2)]
        accb = [sc.f32([TM]) for _ in range(2)]
        blocks = [(COL_X, 512), (COL_X + 512, 512), (COL_B, 512)]
        load_w(Wdt, wview(w_in, COL_DT, 16), "Wdt")
        load_w(Wb[0], wview(w_in, blocks[0][0], blocks[0][1]), ("W", 0))
        for i in range(2):
            P.op("dve", lambda e, rp=rawpad[i]: e.memset(rp[:, 0:3], 0.0), w=[("rawpad", i)])
        jt = 0
        for bi, (c0, ncol) in enumerate(blocks):
            if bi + 1 < len(blocks):
                load_w(Wb[(bi + 1) % 2], wview(w_in, blocks[bi + 1][0], blocks[bi + 1][1]), ("W", (bi + 1) % 2))
            Wv = Wb[bi % 2]
            for jj in range(4):
                j = (c0 - COL_X) // 128 + jj
                rp = rawpad[jt % 2]
                krp = ("rawpad", jt % 2)
                isC = j >= 10
                for nb in range(2):
                    if isC and nb == 0:
                        continue
                    bank = 3 + (jt * 2 + nb) % 4
                    mm_group(pbf[bank], [(Wv[:, k, jj * 128:(jj + 1) * 128], nPT[:, k, nb * 512:(nb + 1) * 512]) for k in range(16)],
                             r=[("W", bi % 2)] + [("nPT", tt) for tt in range(nb * 4, nb * 4 + 4)], w=[("ps", bank)])
                    P.op("act", lambda e, rp=rp, nb=nb, bank=bank: e.activation(out=rp[:, 3 + nb * 512:3 + (nb + 1) * 512], in_=pbf[bank], func=AF.Copy),
                         r=[("ps", bank)], w=[krp])
                P.op("dve", lambda e, rp=rp, j=j: e.tensor_copy(out=prefix[:, j, :], in_=rp[:, TM:TM + 3]), r=[krp], w=["prefix"])
                if not isC:
                    conv_silu(rp, krp, accb[jt % 2], ("acc", jt % 2), j, TM, xTp[:, j, :], ("xT",))
                jt += 1
        for tt in range(8):
            mm_group(pbf[0][:, 0:16], [(nPT[:, k, tt * 128:(tt + 1) * 128], Wdt[:, k, :]) for k in range(16)], r=["Wdt", ("nPT", tt)], w=[("ps", 0)])
            P.op("act", lambda e, tt=tt: e.activation(out=dtraw_p[:, tt, :], in_=pbf[0][:, 0:16], func=AF.Copy), r=[("ps", 0)], w=["dtraw"])
        P.barrier()
        sc.reset(m_after_dt)
        tP = ssd_temps(sc)
        for c in range(8):
            ssd_chunk(tP, xTp, c, dtraw_p[:, c, :], "dtraw", main=False)
        P.op("dve", lambda e: e.tensor_scalar(out=hT, in0=hT, scalar1=flag, scalar2=None, op0=ALU.mult), r=["hT", KC], w=["hT"])
        P.op("act", lambda e: e.activation(out=hTb, in_=hT, func=AF.Copy), r=["hT"], w=["hTb"])
        P.barrier()
        if STOP_AFTER == "P":
            dbg_dump("nPT", YT.rearrange("p a b -> p (a b)"), [("nPT", tt) for tt in range(8)])
            dbg_dump("xTp", xTp.rearrange("p a b -> p (a b)"), [("xT",)])
            dbg_dump("dtraw", dtraw_p.rearrange("p a b -> p (a b)"), ["dtraw"])
            dbg_dump("Wb1", Wb[0].rearrange("p a b -> p (a b)"), [("W", 0)])
            dbg_dump("gbc", gbc, ["gbc"])
            dbg_dump("hT", hT, ["hT"])
            dbg_dump("prefix", prefix.rearrange("p a b -> p (a b)"), ["prefix"])
            finish()
            return nc

        sc.reset(m_common)
        for tt in range(9):
            i = tt % 2
            rows = 128 if tt < 8 else NS
            src = xm[tt * 128:(tt + 1) * 128, :] if tt < 8 else xsmp[:, :]
            P.op("sp", lambda e, i=i, rows=rows, src=src: e.dma_start(out=X[i][:rows, :], in_=src), w=[("X", i)], dma=True)
            rms_transpose(X[i], ("X", i), rows, XN[i], ("XN", i), ST[i], ("ST", i), gbc, "gbc", nT, tt * 128, ("nT", tt), DM, tt)
        P.barrier()
        if STOP_AFTER == "A0":
            dbg_dump("nT", nT.rearrange("p a b -> p (a b)"), [("nT", tt) for tt in range(9)])
            finish()
            return nc
        sc.reset()
        xT = sc.bf16([12, T])
        z_tok = sc.bf16([8, 1024])
        zT_s = sc.f32([8, NS])
        dtraw = sc.f32([8, 16])
        dtT_s = sc.f32([NS])
        ssdg_bc = sc.f32([1024])
        CST2 = sc.f32([CST2W])
        raw_s = sc.f32([12, NS])
        m_A = sc.mark()
        Wb = [sc.bf16([16, 512]) for _ in range(2)]
        Wdt = sc.bf16([16, 16])
        rawpad = [sc.f32([TM + 4]) for _ in range(2)]
        accb = [sc.f32([TM]) for _ in range(2)]
        scTok = sc.f32([3 * 1536])
        scT = sc.f32([36, NS])
        acc_s = sc.f32([12, NS])
        P.op("sp", lambda e: e.dma_start(out=ssdg_bc, in_=ssd_g.partition_broadcast(128)), w=["ssdg"], dma=True)
        P.op("sp", lambda e: e.dma_start(out=CST2, in_=cst2_d[:, :]), w=["cst2"], dma=True)
        P.op("sp", lambda e: e.dma_start(out=scTok[:NS, :], in_=sconv.rearrange("b r c -> b (r c)")), w=["scTok"], dma=True)
        P.op("sp", lambda e: e.dma_start(out=nconv_s[:, 0:2, :], in_=sconv[:, 1:3, :]), w=["o_ncs01"], dma=True)
        OUTKEYS.append("o_ncs01")
        for half in range(2):
            def trs(e, half=half):
                ins = None
                for idx in range(half * 18, half * 18 + 18):
                    r_, j_ = idx // 12, idx % 12
                    ii = idx - half * 18
                    ins = e.transpose(out=pbf[half][:, ii * NS:(ii + 1) * NS], in_=scTok[:NS, r_ * 1536 + j_ * 128:r_ * 1536 + (j_ + 1) * 128],
                                      identity=identF[:NS, :NS])
                return ins
            P.op("pe", trs, r=["scTok", KC], w=[("ps", half)])
            P.op("dve", lambda e, half=half: e.tensor_copy(out=scT[:, half * 18:half * 18 + 18, :].rearrange("p a b -> p (a b)"), in_=pbf[half][:, 0:18 * NS]),
                 r=[("ps", half)], w=["scT"])
        blocksA = [(COL_Z, "z"), (COL_Z + 512, "z"), (COL_X, "x"), (COL_X + 512, "x"), (COL_B, "x")]
        load_w(Wdt, wview(w_in, COL_DT, 16), "Wdt")
        load_w(Wb[0], wview(w_in, blocksA[0][0], 512), ("W", 0))
        jt = 0
        grp = 0
        for bi, (c0, kind) in enumerate(blocksA):
            if bi + 1 < len(blocksA):
                load_w(Wb[(bi + 1) % 2], wview(w_in, blocksA[bi + 1][0], 512), ("W", (bi + 1) % 2))
            Wv = Wb[bi % 2]
            kW = ("W", bi % 2)
            if kind == "z":
                zb = (c0 - COL_Z) // 512
                for tt in range(8):
                    bank = 2 + grp % 4
                    grp += 1
                    mm_group(pbf[bank], [(nT[:, k, tt * 128:(tt + 1) * 128], Wv[:, k, :]) for k in range(16)], r=[kW, ("nT", tt)], w=[("ps", bank)])
                    P.op("act", lambda e, tt=tt, zb=zb, bank=bank: e.activation(out=z_tok[:, tt, zb * 512:(zb + 1) * 512], in_=pbf[bank], func=AF.Silu),
                         r=[("ps", bank)], w=["z_tok"])
                for jj in range(4):
                    bank = 2 + grp % 4
                    grp += 1
                    j = zb * 4 + jj
                    mm_group(pbf[bank][:, 0:NS], [(Wv[:, k, jj * 128:(jj + 1) * 128], nT[:, k, TM:T]) for k in range(16)], r=[kW, ("nT", 8)], w=[("ps", bank)])
                    P.op("act", lambda e, j=j, bank=bank: e.activation(out=zT_s[:, j, :], in_=pbf[bank][:, 0:NS], func=AF.Silu), r=[("ps", bank)], w=["zT_s"])
            else:
                for jj in range(4):
                    j = (c0 - COL_X) // 128 + jj
                    rp = rawpad[jt % 2]
                    krp = ("rawpad", jt % 2)
                    P.op("dve", lambda e, rp=rp, j=j: e.tensor_copy(out=rp[:, 0:3], in_=prefix[:, j, :]), r=["prefix"], w=[krp])
                    for nb in range(3):
                        bank = 2 + grp % 4
                        grp += 1
                        lo, hi = (nb * 512, (nb + 1) * 512) if nb < 2 else (TM, T)
                        n = hi - lo
                        rk = [("nT", tt) for tt in range(nb * 4, nb * 4 + 4)] if nb < 2 else [("nT", 8)]
                        mm_group(pbf[bank][:, 0:n], [(Wv[:, k, jj * 128:(jj + 1) * 128], nT[:, k, lo:hi]) for k in range(16)], r=[kW] + rk, w=[("ps", bank)])
                        if nb < 2:
                            P.op("act", lambda e, rp=rp, lo=lo, hi=hi, bank=bank: e.activation(out=rp[:, 3 + lo:3 + hi], in_=pbf[bank], func=AF.Copy),
                                 r=[("ps", bank)], w=[krp])
                        else:
                            P.op("act", lambda e, j=j, bank=bank: e.activation(out=raw_s[:, j, :], in_=pbf[bank][:, 0:NS], func=AF.Copy),
                                 r=[("ps", bank)], w=["raw_s"])
                    P.op("dve", lambda e, rp=rp, j=j: e.tensor_copy(out=ncv[:, j, :], in_=rp[:, TM:TM + 3]), r=[krp], w=["ncv"])
                    conv_silu(rp, krp, accb[jt % 2], ("acc", jt % 2), j, TM, xT[:, j, 0:TM], ("xT",))
                    P.op("dve", lambda e, j=j: e.tensor_scalar(out=acc_s[:, j, :], in0=raw_s[:, j, :], scalar1=cwb[:, j, 3:4], scalar2=cwb[:, j, 4:5],
                                                               op0=ALU.mult, op1=ALU.add), r=["raw_s", KC], w=["acc_s"])
                    for r_ in range(3):
                        P.op("dve", lambda e, j=j, r_=r_: e.scalar_tensor_tensor(out=acc_s[:, j, :], in0=scT[:, r_ * 12 + j, :], scalar=cwb[:, j, r_:r_ + 1],
                                                                                 op0=ALU.mult, in1=acc_s[:, j, :], op1=ALU.add), r=["scT", "acc_s", KC], w=["acc_s"])
                    jt += 1
        P.op("act", lambda e: e.activation(out=xT[:, :, TM:T], in_=acc_s, func=AF.Silu), r=["acc_s"], w=[("xT",)])
        for tt in range(8):
            mm_group(pbf[0][:, 0:16], [(nT[:, k, tt * 128:(tt + 1) * 128], Wdt[:, k, :]) for k in range(16)], r=["Wdt", ("nT", tt)], w=[("ps", 0)])
            P.op("act", lambda e, tt=tt: e.activation(out=dtraw[:, tt, :], in_=pbf[0][:, 0:16], func=AF.Copy), r=[("ps", 0)], w=["dtraw"])
        mm_group(pbf[1][:NS, 0:NS], [(Wdt[:, k, :], nT[:, k, TM:T]) for k in range(16)], r=["Wdt", ("nT", 8)], w=[("ps", 1)])
        P.op("act", lambda e: e.activation(out=dtT_s[:NS, :], in_=pbf[1][:NS, 0:NS], func=AF.Copy), r=[("ps", 1)], w=["dtT_s"])

        def rows_out(src3, ksrc, R, stage, kst, dram_ap, okey):
            for b3 in range(3):
                def trr(e, b3=b3):
                    ins = None
                    for jj in range(4):
                        j_ = b3 * 4 + jj
                        ins = e.transpose(out=pbf[5 + b3][:R, jj * 128:(jj + 1) * 128], in_=src3[:, j_, :], identity=identF)
                    return ins
                P.op("pe", trr, r=[ksrc, KC], w=[("ps", 5 + b3)])
                P.op("dve", lambda e, b3=b3: e.tensor_copy(out=stage[:R, b3 * 512:(b3 + 1) * 512], in_=pbf[5 + b3][:R, :]), r=[("ps", 5 + b3)], w=[kst, "scTok"])
            P.op("sp", lambda e: e.dma_start(out=dram_ap, in_=stage[:R, 0:1536]), r=[kst], w=[okey], dma=True)
            OUTKEYS.append(okey)
        rows_out(raw_s, "raw_s", NS, scTok[:, 0:1536], "stg0", nconv_s[:, 2, :], "o_ncs2")
        rows_out(ncv, "ncv", 3, scTok[:, 1536:3072], "stg1", nconv_p[:, :], "o_ncp")
        P.barrier()
        if STOP_AFTER == "A1":
            dbg_dump("xT", xT.rearrange("p a b -> p (a b)"), [("xT",)])
            dbg_dump("z_tok", z_tok.rearrange("p a b -> p (a b)"), ["z_tok"])
            dbg_dump("zT_s", zT_s.rearrange("p a b -> p (a b)"), ["zT_s"])
            dbg_dump("dtraw", dtraw.rearrange("p a b -> p (a b)"), ["dtraw"])
            dbg_dump("dtT_s", dtT_s, ["dtT_s"])
            finish()
            return nc

        sc.reset(m_A)
        tA = ssd_temps(sc)
        for c in range(8):
            ssd_chunk(tA, xT, c, dtraw[:, c, :], "dtraw", main=True, ztok_c=z_tok[:, c, :], ssdg_bc=ssdg_bc)
        if STOP_AFTER == "A2a":
            P.barrier()
            dbg_dump("YT", YT.rearrange("p a b -> p (a b)"), [("YTb", c) for c in range(8)])
            finish()
            return nc
        hout = sc.f32([8, 128])
        for half in range(2):
            def trh(e, half=half):
                ins = None
                for jj in range(4):
                    q = half * 4 + jj
                    ins = e.transpose(out=pbf[3 + half][:, jj * 128:(jj + 1) * 128], in_=hT[:, q * 128:(q + 1) * 128], identity=identF)
                return ins
            P.op("pe", trh, r=["hT", KC], w=[("ps", 3 + half)])
            P.op("dve", lambda e, half=half: e.tensor_copy(out=hout[:, half * 4:(half + 1) * 4, :].rearrange("p a b -> p (a b)"), in_=pbf[3 + half]),
                 r=[("ps", 3 + half)], w=["hout"])
        P.op("sp", lambda e: e.dma_start(out=nssm_p.rearrange("(q r) n -> r q n", r=128), in_=hout), r=["hout"], w=["o_nsp"], dma=True)
        OUTKEYS.append("o_nsp")

        if STOP_AFTER == "A2b":
            P.barrier()
            dbg_dump("YT", YT.rearrange("p a b -> p (a b)"), [("YTb", c) for c in range(8)])
            finish()
            return nc
        sm_s = sc.f32([8, NS])
        cat = sc.f32([32])
        rep = sc.f32([8, 32])
        dtx = sc.f32([8, NS])
        ys = sc.f32([8, NS])
        ysq = sc.f32([8, NS])
        rs = sc.f32([2, NS])
        BC_tok = sc.bf16([512])
        selB = sc.bf16([NS, 128])
        hs = [sc.f32([8, 128]) for _ in range(2)]
        tmpb = [sc.f32([128]) for _ in range(2)]
        junk = sc.f32([128])
        Rsel = CST2[:, C2_RSEL:C2_RSEL + 1024]
        x0 = sm_s[:NS, 0, :]; e1 = sm_s[:NS, 1, :]; anp = sm_s[:NS, 2, 0:1]
        P.op("act", lambda e: e.activation(out=e1, in_=dtT_s[:NS, :], func=AF.Exp, bias=dtb_pp[:NS, :]), r=["dtT_s", KC], w=["s_e1"])
        P.op("act", lambda e: e.activation(out=cat[:NS, 16:32], in_=e1, func=AF.Ln, bias=1.0), r=["s_e1"], w=["s_dt"])
        P.op("act", lambda e: e.activation(out=anp, in_=al_pp[:NS, :], func=AF.Exp), r=[KC], w=["s_anp0"])
        P.op("dve", lambda e: e.tensor_scalar(out=anp, in0=anp, scalar1=-1.0, scalar2=None, op0=ALU.mult), r=["s_anp0"], w=["s_anp"])
        P.op("act", lambda e: e.activation(out=cat[:NS, 0:16], in_=cat[:NS, 16:32], func=AF.Exp, scale=anp), r=["s_dt", "s_anp"], w=["s_dA"])

        def repf(e):
            ins = None
            for q in range(8):
                ins = e.matmul(pbf[0][:, q * 32:(q + 1) * 32], lhsT=Rsel[:NS, q * 128:(q + 1) * 128], rhs=cat[:NS, :], start=True, stop=True)
            return ins
        P.op("pe", repf, r=["s_dA", "s_dt", "cst2"], w=[("ps", 0)])
        P.op("dve", lambda e: e.tensor_copy(out=rep.rearrange("p a b -> p (a b)"), in_=pbf[0][:, 0:256]), r=[("ps", 0)], w=["rep"])
        xs_s = xT[:, 0:8, TM:T]
        P.op("dve", lambda e: e.tensor_tensor(out=dtx, in0=rep[:, :, 16:32], in1=xs_s, op=ALU.mult), r=["rep", ("xT",)], w=["dtx"])
        psbc = pbb[1][:NS, 0:512].rearrange("p (a b) -> p a b", a=4)

        def trbc(e):
            ins = None
            for jj in range(4):
                ins = e.transpose(out=psbc[:, jj, :], in_=xT[:, 8 + jj, TM:T], identity=identB)
            return ins
        P.op("pe", trbc, r=[("xT",), "identB"], w=[("ps", 1)])
        P.op("act", lambda e: e.activation(out=BC_tok[:NS, :], in_=pbb[1][:NS, 0:512], func=AF.Copy), r=[("ps", 1)], w=["BC_tok"])
        P.op("dve", lambda e: e.tensor_copy(out=selB[:NS, :, :], in_=identB[:NS, 0:NS].unsqueeze(2).broadcast_to([NS, NS, 128])), r=["identB"], w=["selB"])
        P.op("dve", lambda e: e.memset(ys, 0.0), w=["ys"])
        for b in range(NS):
            hb = hs[b % 2]
            khb = ("hs", b % 2)
            bank = 2 + b % 2
            P.op("sp", lambda e, b=b, hb=hb: e.dma_start(out=hb, in_=sssm[b].rearrange("(q r) n -> r q n", r=128)), w=[khb], dma=True)
            P.op("pe", lambda e, b=b, bank=bank: e.matmul(pbf[bank], lhsT=selB[:NS, b, :], rhs=BC_tok[:NS, :], start=True, stop=True),
                 r=["selB", "BC_tok"], w=[("ps", bank)])
            for q in range(8):
                g = q // 4
                tb = tmpb[q % 2]
                ktb = ("tmpb", q % 2)
                P.op("dve", lambda e, b=b, q=q, g=g, tb=tb, bank=bank: e.tensor_scalar(out=tb, in0=pbf[bank][:, g * 128:(g + 1) * 128], scalar1=dtx[:, q, b:b + 1],
                                                                                       scalar2=None, op0=ALU.mult), r=[("ps", bank), "dtx"], w=[ktb])
                P.op("dve", lambda e, b=b, q=q, tb=tb, hb=hb: e.scalar_tensor_tensor(out=hb[:, q, :], in0=hb[:, q, :], scalar=rep[:, q, b:b + 1], op0=ALU.mult,
                                                                                     in1=tb, op1=ALU.add), r=[khb, "rep", ktb], w=[khb])
                P.op("dve", lambda e, b=b, q=q, g=g, hb=hb, bank=bank: e.scalar_tensor_tensor(out=junk, in0=hb[:, q, :], scalar=1.0, op0=ALU.mult,
                                                                                              in1=pbf[bank][:, 256 + g * 128:256 + (g + 1) * 128], op1=ALU.mult,
                                                                                              accum_out=ys[:, q, b:b + 1]), r=[khb, ("ps", bank)], w=["ys", "junk"])
            ok = ("o_nss", b)
            P.op("sp", lambda e, b=b, hb=hb: e.dma_start(out=nssm_s[b].rearrange("(q r) n -> r q n", r=128), in_=hb), r=[khb], w=[ok], dma=True)
            OUTKEYS.append(ok)
        P.op("dve", lambda e: e.tensor_tensor(out=ysq, in0=xs_s, in1=D_pp.unsqueeze(2).broadcast_to([128, 8, NS]), op=ALU.mult), r=[("xT",), KC], w=["ysq"])
        P.op("dve", lambda e: e.tensor_tensor(out=ys, in0=ys, in1=ysq, op=ALU.add), r=["ys", "ysq"], w=["ys"])
        P.op("dve", lambda e: e.tensor_tensor(out=ys, in0=ys, in1=zT_s, op=ALU.mult), r=["ys", "zT_s"], w=["ys"])
        P.op("dve", lambda e: e.tensor_tensor(out=ysq, in0=ys, in1=ys, op=ALU.mult), r=["ys", "ysq"], w=["ysq"])

        def gsum(e):
            ins = None
            for q in range(8):
                g = q // 4
                ins = e.matmul(pbf[4][:, g * NS:(g + 1) * NS], lhsT=onesF, rhs=ysq[:, q, :], start=(q % 4 == 0), stop=(q % 4 == 3))
            return ins
        P.op("pe", gsum, r=["ysq", KC], w=[("ps", 4)])
        rsf = rs.rearrange("p a b -> p (a b)")
        P.op("act", lambda e: e.activation(out=rsf, in_=pbf[4][:, 0:2 * NS], func=AF.Ln, scale=1.0 / 512, bias=EPS), r=[("ps", 4)], w=["rs0"])
        P.op("act", lambda e: e.activation(out=rsf, in_=rsf, func=AF.Exp, scale=-0.5), r=["rs0"], w=["rs"])
        for g in range(2):
            P.op("dve", lambda e, g=g: e.tensor_tensor(out=ys[:, g * 4:(g + 1) * 4, :], in0=ys[:, g * 4:(g + 1) * 4, :],
                                                       in1=rs[:, g:g + 1, :].broadcast_to([128, 4, NS]), op=ALU.mult), r=["ys", "rs"], w=["ys"])
        P.op("dve", lambda e: e.tensor_tensor(out=YT[:, 8:16, TM:T], in0=ys, in1=sg_pp.unsqueeze(2).broadcast_to([128, 8, NS]), op=ALU.mult),
             r=["ys", KC], w=[("YTb", 8)])
        P.barrier()
        if STOP_AFTER == "A2":
            dbg_dump("YT", YT.rearrange("p a b -> p (a b)"), [("YTb", c) for c in range(9)])
            finish()
            return nc
        sc.reset()
        vng_bc = sc.f32([1024])
        bs_bc = sc.f32([8, 128])
        v_tok = sc.bf16([9, 1024])
        ya = sc.f32([8, T])
        Wb = [sc.bf16([16, 512]) for _ in range(2)]
        WsT = sc.bf16([8, 128])
        wsraw = ya[:, 0, 0:1024].rearrange("p (a b) -> p a b", a=8)
        gv = [sc.f32([1024]) for _ in range(2)]
        vst = [sc.f32([4]) for _ in range(2)]
        ug = [sc.f32([T]) for _ in range(2)]
        _m_tm = sc.mark()
        tmpm = [sc.f32([512]) for _ in range(2)]
        _m_tm2 = sc.mark()
        sc.reset(_m_tm)
        vsf = sc.f32([1024])
        sc.reset(_m_tm2)
        sq = [sc.bf16([T]) for _ in range(2)]
        rstd_bc = sc.f32([T])
        w00I = sc.bf16([8, NS])
        P.op("sp", lambda e: e.dma_start(out=vng_bc, in_=vn_g.partition_broadcast(128)), w=["vng"], dma=True)
        P.op("sp", lambda e: e.dma_start(out=bs_bc.rearrange("p a b -> p (a b)"), in_=cst2_d[:, C2_BS:C2_BS + 1024]), w=["bs_bc"], dma=True)
        P.op("sp", lambda e: e.dma_start(out=wsraw, in_=ws_d.rearrange("h t s -> t h s")), w=["wsraw"], dma=True)
        load_w(Wb[0], wview(w_in, COL_V, 512), ("W", 0))
        load_w(Wb[1], wview(w_in, COL_V + 512, 512), ("W", 1))
        for half in range(2):
            def trw(e, half=half):
                ins = None
                for jj in range(4):
                    h_ = half * 4 + jj
                    ins = e.transpose(out=pbf[half][:, jj * 128:(jj + 1) * 128], in_=wsraw[:, h_, :], identity=identF)
                return ins
            P.op("pe", trw, r=["wsraw", KC], w=[("ps", half)])
            P.op("dve", lambda e, half=half: e.tensor_tensor(out=WsT[:, half * 4:(half + 1) * 4, :], in0=pbf[half].rearrange("p (a b) -> p a b", a=4),
                                                             in1=triF.unsqueeze(1).broadcast_to([128, 4, 128]), op=ALU.mult), r=[("ps", half), KC], w=["WsT"])
        for h_ in range(8):
            P.op("dve", lambda e, h_=h_: e.tensor_scalar(out=w00I[:NS, h_, :], in0=identF[:NS, 0:NS], scalar1=w00_bc[:NS, h_:h_ + 1], scalar2=None, op0=ALU.mult),
                 r=[KC], w=["w00I"])
        for tt in range(9):
            rows = 128 if tt < 8 else NS
            i = tt % 2
            tcols = slice(tt * 128, tt * 128 + rows)
            for zb in range(2):
                bank = 2 + (tt * 2 + zb) % 4
                mm_group(pbf[bank][:rows, :], [(nT[:, k, tcols], Wb[zb][:, k, :]) for k in range(16)], r=[("W", zb), ("nT", tt)], w=[("ps", bank)])
                P.op("act", lambda e, i=i, zb=zb, bank=bank, rows=rows: e.activation(out=gv[i][:rows, zb * 512:(zb + 1) * 512], in_=pbf[bank][:rows, :],
                                                                                    func=AF.Gelu_apprx_tanh), r=[("ps", bank)], w=[("gv", i, zb)])
            P.op("act", lambda e, i=i, rows=rows: e.activation(out=ug[i][:rows, 0:1024], in_=gv[i][:rows, :], func=AF.Square, accum_out=vst[i][:rows, 0:1]),
                 r=[("gv", i, 0), ("gv", i, 1)], w=[("ug", i), ("vst", i)])
            P.op("act", lambda e, i=i, rows=rows: e.activation(out=vst[i][:rows, 1:2], in_=vst[i][:rows, 0:1], func=AF.Ln, scale=1.0 / 1024, bias=EPS),
                 r=[("vst", i)], w=[("vst", i)])
            P.op("act", lambda e, i=i, rows=rows: e.activation(out=vst[i][:rows, 2:3], in_=vst[i][:rows, 1:2], func=AF.Exp, scale=-0.5),
                 r=[("vst", i)], w=[("vst", i)])
            P.op("dve", lambda e, i=i, rows=rows, tt=tt: e.scalar_tensor_tensor(out=v_tok[:rows, tt, :], in0=gv[i][:rows, :], scalar=vst[i][:rows, 2:3], op0=ALU.mult,
                                                                                in1=vng_bc[:rows, :], op1=ALU.mult),
                 r=[("gv", i, 0), ("gv", i, 1), ("vst", i), "vng"], w=[("v_tok", tt)])
            if tt == 8:
                P.op("dve", lambda e, i=i: e.scalar_tensor_tensor(out=vsf[:NS, :], in0=gv[i][:NS, :], scalar=vst[i][:NS, 2:3], op0=ALU.mult,
                                                                  in1=vng_bc[:NS, :], op1=ALU.mult), r=[("gv", i, 0), ("gv", i, 1), ("vst", i), "vng"], w=["vsf", ("tmpm", 0), ("tmpm", 1)])
                P.op("sp", lambda e: e.dma_start(out=v_s[:, :], in_=vsf[:NS, :]), r=["vsf", ("tmpm", 0), ("tmpm", 1)], w=["o_vs"], dma=True)
                OUTKEYS.append("o_vs")
        load_w(Wb[0], wview(w_in, COL_U, 512), ("W", 0))
        load_w(Wb[1], wview(w_in, COL_U + 512, 512), ("W", 1))
        tok_blocks = [(0, 512), (512, 1024), (TM, T)]
        grp = 0
        for h_ in range(8):
            ub, jj = h_ // 4, h_ % 4
            Wv = Wb[ub]
            ui = h_ % 2
            for nb, (lo, hi) in enumerate(tok_blocks):
                bank = grp % 2
                grp += 1
                n = hi - lo
                rk = [("nT", tt) for tt in range(nb * 4, nb * 4 + 4)] if nb < 2 else [("nT", 8)]
                mm_group(pbf[bank][:, 0:n], [(Wv[:, k, jj * 128:(jj + 1) * 128], nT[:, k, lo:hi]) for k in range(16)], r=[("W", ub)] + rk, w=[("ps", bank)])
                P.op("act", lambda e, ui=ui, lo=lo, hi=hi, n=n, bank=bank: e.activation(out=ug[ui][:, lo:hi], in_=pbf[bank][:, 0:n], func=AF.Gelu_apprx_tanh),
                     r=[("ps", bank)], w=[("ug", ui)])
            for half in range(2):
                def mix(e, half=half, h_=h_):
                    ins = None
                    for cc in range(4):
                        c = half * 4 + cc
                        ins = e.matmul(pbf[2 + half][:, cc * 128:(cc + 1) * 128], lhsT=v_tok[:, c, h_ * 128:(h_ + 1) * 128], rhs=WsT[:, h_, :], start=True, stop=True)
                    return ins
                P.op("pe", mix, r=[("v_tok", c) for c in range(half * 4, half * 4 + 4)] + ["WsT"], w=[("ps", 2 + half)])
                tm = tmpm[half]
                P.op("dve", lambda e, half=half, h_=h_, tm=tm: e.tensor_tensor(out=tm.rearrange("p (a b) -> p a b", a=4), in0=pbf[2 + half].rearrange("p (a b) -> p a b", a=4),
                                                                               in1=bs_bc[:, h_:h_ + 1, :].broadcast_to([128, 4, 128]), op=ALU.add),
                     r=[("ps", 2 + half), "bs_bc"], w=[("tmpm", half)])
                P.op("dve", lambda e, half=half, h_=h_, tm=tm, ui=ui: e.tensor_tensor(out=ya[:, h_, half * 512:(half + 1) * 512], in0=tm, in1=ug[ui][:, half * 512:(half + 1) * 512], op=ALU.mult),
                     r=[("tmpm", half), ("ug", ui)], w=[("ya", h_)])
            P.op("pe", lambda e, h_=h_: e.matmul(pbf[4][:, 0:NS], lhsT=v_tok[:NS, 8, h_ * 128:(h_ + 1) * 128], rhs=w00I[:NS, h_, :], start=True, stop=True),
                 r=[("v_tok", 8), "w00I"], w=[("ps", 4)])
            P.op("dve", lambda e, h_=h_, ui=ui: e.scalar_tensor_tensor(out=ya[:, h_, TM:T], in0=pbf[4][:, 0:NS], scalar=b0_bc[:, h_:h_ + 1], op0=ALU.add, in1=ug[ui][:, TM:T], op1=ALU.mult),
                 r=[("ps", 4), ("ug", ui), KC], w=[("ya", h_)])
            si = h_ % 2
            P.op("act", lambda e, h_=h_, si=si: e.activation(out=sq[si], in_=ya[:, h_, :], func=AF.Square), r=[("ya", h_)], w=[("sq", si)])
            for nb, (lo, hi) in enumerate(tok_blocks):
                n = hi - lo
                P.op("pe", lambda e, nb=nb, lo=lo, hi=hi, n=n, si=si, h_=h_: e.matmul(pbf[5 + nb][:, 0:n], lhsT=onesB, rhs=sq[si][:, lo:hi], start=(h_ == 0), stop=(h_ == 7)),
                     r=[("sq", si), "onesB"], w=[("ps", 5 + nb)])
        for nb, (lo, hi) in enumerate(tok_blocks):
            n = hi - lo
            P.op("act", lambda e, nb=nb, lo=lo, hi=hi, n=n: e.activation(out=rstd_bc[:, lo:hi], in_=pbf[5 + nb][:, 0:n], func=AF.Ln, scale=1.0 / 1024, bias=EPS),
                 r=[("ps", 5 + nb)], w=[("rstd", nb)])
            P.op("act", lambda e, lo=lo, hi=hi: e.activation(out=rstd_bc[:, lo:hi], in_=rstd_bc[:, lo:hi], func=AF.Exp, scale=-0.5), r=[("rstd", nb)], w=[("rstd", nb)])
        for h_ in range(8):
            P.op("dve", lambda e, h_=h_: e.scalar_tensor_tensor(out=YT[:, h_, :], in0=ya[:, h_, :], scalar=gon_pp[:, h_:h_ + 1], op0=ALU.mult, in1=rstd_bc, op1=ALU.mult),
                 r=[("ya", h_), ("rstd", 0), ("rstd", 1), ("rstd", 2), KC], w=[("YTa", h_)])
        P.barrier()
        if STOP_AFTER == "A3":
            dbg_dump("YT", YT.rearrange("p a b -> p (a b)"), [("YTa", h_) for h_ in range(8)])
            finish()
            return nc

        sc.reset()
        hres = sc.f32([9, DM])
        Wb = [sc.bf16([16, 512]) for _ in range(2)]
        m_B = sc.mark()
        for tt in range(9):
            rows = 128 if tt < 8 else NS
            src = xm[tt * 128:(tt + 1) * 128, :] if tt < 8 else xsmp[:, :]
            P.op("sp", lambda e, tt=tt, rows=rows, src=src: e.dma_start(out=hres[:rows, tt, :], in_=src), w=[("h", tt)], dma=True)
        load_w(Wb[0], wview(w_out, 0, 512), ("W", 0))
        for cb in range(4):
            if cb + 1 < 4:
                load_w(Wb[(cb + 1) % 2], wview(w_out, (cb + 1) * 512, 512), ("W", (cb + 1) % 2))
            Wv = Wb[cb % 2]
            for tt in range(9):
                rows = 128 if tt < 8 else NS
                tcols = slice(tt * 128, tt * 128 + rows)
                bank = (cb * 9 + tt) % 6
                mm_group(pbf[bank][:rows, :], [(YT[:, k, tcols], Wv[:, k, :]) for k in range(16)], r=[("W", cb % 2)], w=[("ps", bank)])
                P.op("dve", lambda e, rows=rows, tt=tt, cb=cb, bank=bank: e.tensor_tensor(out=hres[:rows, tt, cb * 512:(cb + 1) * 512], in0=pbf[bank][:rows, :],
                                                                                         in1=hres[:rows, tt, cb * 512:(cb + 1) * 512], op=ALU.add),
                     r=[("ps", bank), ("h", tt)], w=[("h", tt)])
        P.barrier()
        if STOP_AFTER == "B":
            dbg_dump("h", hres.rearrange("p a b -> p (a b)"), [("h", tt) for tt in range(9)])
            finish()
            return nc

        sc.reset(m_B - 2 * 4096)
        gbc = sc.f32([DM])
        gbc2 = sc.f32([DM])
        Wg = [sc.bf16([16, 256]) for _ in range(2)]
        Wu = [sc.bf16([16, 256]) for _ in range(2)]
        stmp = [sc.f32([512]) for _ in range(2)]
        ST = [sc.f32([4]) for _ in range(2)]
        r2.reset()
        actb = [r2.bf16([2, T]) for _ in range(2)]
        Wd = [r2.bf16([2, DM]) for _ in range(2)]
        XN = [r2.bf16([DM]) for _ in range(2)]
        mT = nT
        P.op("sp", lambda e: e.dma_start(out=gbc, in_=g_ffn.partition_broadcast(128)), w=["gbc"], dma=True)
        P.op("sp", lambda e: e.dma_start(out=gbc2, in_=g_fin.partition_broadcast(128)), w=["gbc2"], dma=True)
        for tt in range(9):
            rows = 128 if tt < 8 else NS
            i = tt % 2
            rms_transpose(hres[:, tt, :], ("h", tt), rows, XN[i], ("XN", i), ST[i], ("ST", i), gbc, "gbc", mT, tt * 128, ("mT", tt), DM, tt)
        NFB = DFF // 256
        if STOP_AFTER == "C0":
            P.barrier()
            dbg_dump("mT", mT.rearrange("p a b -> p (a b)"), [("mT", tt) for tt in range(9)])
            dbg_dump("XN0", XN[0], [("XN", 0)])
            dbg_dump("ST0", ST[0], [("ST", 0)])
            dbg_dump("gbc", gbc, ["gbc"])
            dbg_dump("h", hres.rearrange("p a b -> p (a b)"), [("h", tt) for tt in range(9)])
            finish()
            return nc

        def load_gu(fb):
            load_w(Wg[fb % 2], wview(w_gate, fb * 256, 256), ("Wg", fb % 2))
            load_w(Wu[fb % 2], wview(w_up, fb * 256, 256), ("Wu", fb % 2))

        def load_d(fb):
            load_w(Wd[fb % 2], w_down[fb * 256:(fb + 1) * 256, :].rearrange("(fl p) c -> p fl c", p=128), ("Wd", fb % 2))

        gctr = [0]

        def GU(fb):
            for fl in range(2):
                for nb, (lo, hi) in enumerate(tok_blocks):
                    n = hi - lo
                    pr = gctr[0] % 2
                    gctr[0] += 1
                    bg, bu = pr * 2, pr * 2 + 1
                    rk = [("mT", tt) for tt in range(nb * 4, nb * 4 + 4)] if nb < 2 else [("mT", 8)]
                    mm_group(pbf[bg][:, 0:n], [(Wg[fb % 2][:, k, fl * 128:(fl + 1) * 128], mT[:, k, lo:hi]) for k in range(16)], r=[("Wg", fb % 2)] + rk, w=[("ps", bg)])
                    mm_group(pbf[bu][:, 0:n], [(Wu[fb % 2][:, k, fl * 128:(fl + 1) * 128], mT[:, k, lo:hi]) for k in range(16)], r=[("Wu", fb % 2)] + rk, w=[("ps", bu)])
                    P.op("act", lambda e, pr=pr, bg=bg, n=n: e.activation(out=stmp[pr][:, 0:n], in_=pbf[bg][:, 0:n], func=AF.Silu), r=[("ps", bg)], w=[("stmp", pr)])
                    P.op("dve", lambda e, pr=pr, bu=bu, n=n, fb=fb, fl=fl, lo=lo, hi=hi: e.tensor_tensor(out=actb[fb % 2][:, fl, lo:hi], in0=pbf[bu][:, 0:n], in1=stmp[pr][:, 0:n], op=ALU.mult),
                         r=[("ps", bu), ("stmp", pr)], w=[("act", fb % 2)])

        dctr = [0]

        def DN(fb):
            for tt in range(9):
                rows = 128 if tt < 8 else NS
                tcols = slice(tt * 128, tt * 128 + rows)
                for cb in range(4):
                    bank = 4 + dctr[0] % 4
                    dctr[0] += 1
                    mm_group(pbf[bank][:rows, :], [(actb[fb % 2][:, fl, tcols], Wd[fb % 2][:, fl, cb * 512:(cb + 1) * 512]) for fl in range(2)],
                             r=[("act", fb % 2), ("Wd", fb % 2)], w=[("ps", bank)])
                    P.op("dve", lambda e, rows=rows, tt=tt, cb=cb, bank=bank: e.tensor_tensor(out=hres[:rows, tt, cb * 512:(cb + 1) * 512], in0=pbf[bank][:rows, :],
                                                                                             in1=hres[:rows, tt, cb * 512:(cb + 1) * 512], op=ALU.add),
                         r=[("ps", bank), ("h", tt)], w=[("h", tt)])

        load_gu(0)
        load_d(0)
        for i in range(NFB + 1):
            if i + 1 < NFB:
                load_gu(i + 1)
            if i < NFB:
                GU(i)
            if i >= 1:
                DN(i - 1)
            if i + 1 < NFB and i >= 0:
                load_d(i + 1) if i >= 1 or True else None
        P.barrier()
        if STOP_AFTER == "C":
            dbg_dump("h", hres.rearrange("p a b -> p (a b)"), [("h", tt) for tt in range(9)])
            finish()
            return nc

        for tt in range(9):
            rows = 128 if tt < 8 else NS
            i = tt % 2
            ht = hres[:, tt, :]
            st = ST[i]
            P.op("act", lambda e, rows=rows, ht=ht, st=st, i=i: e.activation(out=XN[i][:rows, :], in_=ht[:rows, :], func=AF.Square, accum_out=st[:rows, 0:1]),
                 r=[("h", tt)], w=[("XN", i), ("ST", i)])
            P.op("act", lambda e, rows=rows, st=st: e.activation(out=st[:rows, 1:2], in_=st[:rows, 0:1], func=AF.Ln, scale=1.0 / DM, bias=EPS), r=[("ST", i)], w=[("ST", i)])
            P.op("act", lambda e, rows=rows, st=st: e.activation(out=st[:rows, 2:3], in_=st[:rows, 1:2], func=AF.Exp, scale=-0.5), r=[("ST", i)], w=[("ST", i)])
            P.op("dve", lambda e, rows=rows, ht=ht, st=st: e.scalar_tensor_tensor(out=ht[:rows, :], in0=ht[:rows, :], scalar=st[:rows, 2:3], op0=ALU.mult, in1=gbc2[:rows, :], op1=ALU.mult),
                 r=[("h", tt), ("ST", i), "gbc2"], w=[("h", tt)])
            dst = y_m[tt * 128:(tt + 1) * 128, :] if tt < 8 else y_s[:, :]
            ok = ("o_y", tt)
            P.op("sp", lambda e, rows=rows, ht=ht, dst=dst: e.dma_start(out=dst, in_=ht[:rows, :]), r=[("h", tt)], w=[ok], dma=True)
            OUTKEYS.append(ok)
        finish()
        return nc


def _host_consts(inputs, core):
    hf = core % 2
    cst = np.zeros((128, CSTW), np.float32)
    r = np.arange(128)
    cst[:, C_ID:C_ID + 128] = np.eye(128, dtype=np.float32)
    cst[:, C_TRI:C_TRI + 128] = (r[:, None] <= r[None, :]).astype(np.float32)
    cst[:, C_U:C_U + 128] = (r[:, None] > r[None, :]).astype(np.float32)
    cst[:, C_ONE:C_ONE + 128] = 1.0
    cw = np.asarray(inputs["ssd_conv_w"])[0]
    cb = np.asarray(inputs["ssd_conv_b"])[0]
    cwb = np.concatenate([cw, cb[None]], 0)
    cst[:, C_CWB:C_CWB + 60] = cwb.reshape(5, 12, 128).transpose(2, 1, 0).reshape(128, 60)
    cst[:, C_GON:C_GON + 8] = np.asarray(inputs["chunk_out_norm_g"])[0].reshape(8, 128).T
    cst[:, C_ALOG:C_ALOG + 16] = np.asarray(inputs["ssd_a_log"])[0][None, :]
    cst[:, C_DTB:C_DTB + 16] = np.asarray(inputs["ssd_dt_bias"])[0][None, :]
    cst[:, C_D:C_D + 16] = np.asarray(inputs["ssd_d"])[0][None, :]
    cst[:, C_W00:C_W00 + 8] = np.asarray(inputs["chunk_w_s"])[0][:, 0, 0][None, :]
    cst[:, C_B0:C_B0 + 8] = np.asarray(inputs["chunk_b_s"])[0][:, 0][None, :]
    cst[:, C_FLAG] = float(hf)
    cst[:, C_DPP:C_DPP + 8] = np.repeat(np.asarray(inputs["ssd_d"])[0], 64).reshape(8, 128).T
    cst[:, C_SGPP:C_SGPP + 8] = np.asarray(inputs["ssd_norm_g"])[0].reshape(8, 128).T
    cst[0:16, C_DTBPP] = np.asarray(inputs["ssd_dt_bias"])[0]
    cst[0:16, C_ALPP] = np.asarray(inputs["ssd_a_log"])[0]
    cst2 = np.zeros((128, CST2W), np.float32)
    cst2[:, C2_BS:C2_BS + 1024] = np.asarray(inputs["chunk_b_s"])[0].reshape(1, 1024)
    rs = np.zeros((16, 8, 128), np.float32)
    for q in range(8):
        for rr in range(128):
            rs[2 * q + rr // 64, q, rr] = 1.0
    cst2[0:16, C2_RSEL:C2_RSEL + 1024] = rs.reshape(16, 1024)
    return cst, cst2


def make_in_maps(inputs):
    xp = np.asarray(inputs["x_prompt"], np.float32)
    xs = np.asarray(inputs["x_sample"], np.float32)
    sconv = np.asarray(inputs["state_conv"], np.float32)[0]
    sssm = np.asarray(inputs["state_ssm"], np.float32)[0]
    shared = {
        "w_in": np.ascontiguousarray(np.asarray(inputs["w_in"], np.float32)[0]),
        "w_out": np.ascontiguousarray(np.asarray(inputs["w_out"], np.float32)[0]),
        "w_gate": np.ascontiguousarray(np.asarray(inputs["w_gate"], np.float32)[0]),
        "w_up": np.ascontiguousarray(np.asarray(inputs["w_up"], np.float32)[0]),
        "w_down": np.ascontiguousarray(np.asarray(inputs["w_down"], np.float32)[0]),
        "ws": np.ascontiguousarray(np.asarray(inputs["chunk_w_s"], np.float32)[0]),
        "g_mix": np.ascontiguousarray(np.asarray(inputs["norm_mix_g"], np.float32)[0]),
        "g_ffn": np.ascontiguousarray(np.asarray(inputs["norm_ffn_g"], np.float32)[0]),
        "g_fin": np.ascontiguousarray(np.asarray(inputs["norm_final_g"], np.float32)),
        "vn_g": np.ascontiguousarray(np.asarray(inputs["chunk_v_norm_g"], np.float32)[0]),
        "ssd_g": np.ascontiguousarray(np.asarray(inputs["ssd_norm_g"], np.float32)[0]),
    }
    zeros_prev = np.zeros((TM, DM), np.float32)
    maps = []
    for c in range(8):
        b, hf = c // 2, c % 2
        cst, cst2 = _host_consts(inputs, c)
        m = dict(shared)
        m["xm"] = np.ascontiguousarray(xp[b, hf * TM:(hf + 1) * TM])
        m["xprev"] = np.ascontiguousarray(xp[b, 0:TM]) if hf == 1 else zeros_prev
        m["xsmp"] = np.ascontiguousarray(xs[c * NS:(c + 1) * NS, 0])
        m["sconv"] = np.ascontiguousarray(sconv[c * NS:(c + 1) * NS])
        m["sssm"] = np.ascontiguousarray(sssm[c * NS:(c + 1) * NS].reshape(NS, 1024, 128))
        m["cst"] = cst
        m["cst2"] = cst2
        maps.append(m)
    return maps


def kernel(**inputs):
    nc = build_program()
    maps = make_in_maps(inputs)
    res = run_bass_kernel_spmd(nc, maps, core_ids=list(range(8)))
    R = res.results
    y_prompt = np.zeros((4, 2048, DM), np.float32)
    y_sample = np.zeros((128, 1, DM), np.float32)
    ncp = np.zeros((1, 4, 3, 1536), np.float32)
    nsp = np.zeros((1, 4, 16, 64, 128), np.float32)
    ncs = np.zeros((1, 128, 3, 1536), np.float32)
    nss = np.zeros((1, 128, 16, 64, 128), np.float32)
    vs = np.zeros((1, 128, 1, 1024), np.float32)
    for c in range(8):
        b, hf = c // 2, c % 2
        y_prompt[b, hf * TM:(hf + 1) * TM] = R[c]["y_m"]
        y_sample[c * NS:(c + 1) * NS, 0] = R[c]["y_s"]
        if hf == 1:
            ncp[0, b] = R[c]["nconv_p"]
            nsp[0, b] = R[c]["nssm_p"].reshape(16, 64, 128)
        ncs[0, c * NS:(c + 1) * NS] = R[c]["nconv_s"]
        nss[0, c * NS:(c + 1) * NS] = R[c]["nssm_s"].reshape(NS, 16, 64, 128)
        vs[0, c * NS:(c + 1) * NS, 0] = R[c]["v_s"]
    return (y_prompt, y_sample, ncp, nsp, ncs, nss, vs)
```

```python
import numpy as np
from contextlib import ExitStack
import concourse.bass as bass
import concourse.mybir as mybir
from concourse.bass_utils import run_bass_kernel_spmd

F32 = mybir.dt.float32
F32R = mybir.dt.float32r
BF16 = mybir.dt.bfloat16
AF = mybir.ActivationFunctionType
ALU = mybir.AluOpType
AX = mybir.AxisListType

ENGS = ["pe", "act", "dve", "pool", "sp"]
NDMASEM = 12

DEBUG = {}
STOP_AFTER = None
SSD_LEVEL = 9


class Op:
    __slots__ = ("eng", "fn", "r", "w", "dma", "deps", "sig", "sem", "prev_use", "waits", "need_sig", "xdeps")

    def __init__(self, eng, fn, r, w, dma):
        self.eng, self.fn, self.r, self.w, self.dma = eng, fn, tuple(r), tuple(w), dma
        self.deps = set()
        self.xdeps = set()
        self.sig = None
        self.sem = None
        self.prev_use = 0
        self.waits = []
        self.need_sig = False


class Prog:
    def __init__(self):
        self.ops = []
        self.pending_bar = {}

    def op(self, eng, fn, r=(), w=(), dma=False, nobar=False):
        w = list(w) + [k for k in r if isinstance(k, tuple) and k and k[0] == "ps" and k not in w]
        o = Op(eng, fn, r, w, dma)
        if eng in self.pending_bar and not nobar:
            o.xdeps |= self.pending_bar.pop(eng)
        self.ops.append(o)
        return o

    def barrier(self):
        last = {}
        dmas = []
        for i, o in enumerate(self.ops):
            if o.dma:
                dmas.append(i)
            elif o.fn is not None:
                last[o.eng] = i
        s = set(last.values()) | set(dmas)
        for e in ENGS:
            self.pending_bar[e] = set(s) | self.pending_bar.get(e, set())

    def resolve(self):
        ops = self.ops
        last_w = {}
        readers = {}
        for i, o in enumerate(ops):
            deps = set(o.xdeps)
            for k in o.r:
                if k in last_w:
                    deps.add(last_w[k])
            for k in o.w:
                if k in last_w:
                    deps.add(last_w[k])
                deps.update(readers.get(k, ()))
            deps.discard(i)
            keep = set()
            for d in deps:
                od = ops[d]
                if od.dma:
                    keep.add(d)
                elif od.eng == o.eng and not o.dma:
                    if o.eng == "pe":
                        continue
                    if d in o.xdeps:
                        continue
                    if set(od.w) & set(o.r):
                        keep.add(d)
                else:
                    keep.add(d)
            o.deps = keep
            for d in keep:
                ops[d].need_sig = True
            for k in o.r:
                readers.setdefault(k, []).append(i)
            for k in o.w:
                last_w[k] = i
                readers[k] = []
        cnt = {e: 0 for e in ENGS}
        dma_n = {e: 0 for e in ENGS}
        dma_use = {}
        for i, o in enumerate(ops):
            if o.dma:
                slot = (o.eng, dma_n[o.eng] % NDMASEM)
                dma_n[o.eng] += 1
                u = dma_use.get(slot, 0)
                o.prev_use = u
                dma_use[slot] = u + 1
                o.sem = slot
                o.sig = 16 * (u + 1)
            elif o.need_sig:
                cnt[o.eng] += 1
                o.sem = o.eng
                o.sig = cnt[o.eng]
        seen = {e: {} for e in ENGS}
        for i, o in enumerate(ops):
            sd = seen[o.eng]
            need = {}
            if o.dma and o.prev_use > 0:
                need[o.sem] = 16 * o.prev_use
            for d in o.deps:
                od = ops[d]
                need[od.sem] = max(need.get(od.sem, 0), od.sig)
            o.waits = []
            for s, v in need.items():
                if sd.get(s, 0) >= v:
                    continue
                sd[s] = v
                o.waits.append((s, v))

    def emit(self, sems, block):
        ops = self.ops

        def run(eng_name):
            def body(e):
                for o in ops:
                    if o.eng != eng_name:
                        continue
                    for s, v in o.waits:
                        e.wait_ge(sems[s], v)
                    if o.fn is None:
                        continue
                    ins = o.fn(e)
                    if o.dma:
                        ins.then_inc(sems[o.sem], 16)
                    elif o.sig is not None:
                        ins.then_inc(sems[o.sem], 1)
            return body

        block.sync(run("sp"))
        block.tensor(run("pe"))
        block.scalar(run("act"))
        block.vector(run("dve"))
        block.gpsimd(run("pool"))


DM = 2048
DIN = 4624
DFF = 5632
TM = 1024
NS = 16
T = TM + NS
EPS = 1e-6
COL_U, COL_V, COL_Z, COL_X, COL_B, COL_C, COL_DT = 0, 1024, 2048, 3072, 4096, 4352, 4608

C_ID, C_TRI, C_U, C_ONE = 0, 128, 256, 384
C_CWB, C_GON, C_ALOG, C_DTB, C_D, C_W00, C_B0, C_FLAG = 512, 572, 580, 596, 612, 628, 636, 644
C_DPP, C_SGPP, C_DTBPP, C_ALPP = 648, 656, 664, 665
CSTW = 672
C2_BS, C2_RSEL = 0, 1024
CST2W = 2048

AW = 51000


class Bump:
    def __init__(self, arena, start, end):
        self.arena, self.start, self.end, self.off = arena, start, end, start
        self.peak = start

    def reset(self, to=None):
        self.off = self.start if to is None else to

    def mark(self):
        return self.off

    def _take(self, words):
        o = self.off
        self.off += words
        assert self.off <= self.end, ("arena overflow", self.off, self.end)
        self.peak = max(self.peak, self.off)
        return o

    def f32(self, shape):
        n = int(np.prod(shape))
        o = self._take(n)
        v = self.arena[:, o:o + n]
        return _shape(v, shape)

    def bf16(self, shape):
        n = int(np.prod(shape))
        w = (n + 1) // 2
        o = self._take(w)
        v = self.arena[:, o:o + w].bitcast(BF16)[:, 0:n]
        return _shape(v, shape)


def _shape(v, shape):
    if len(shape) == 1:
        return v
    if len(shape) == 2:
        return v.rearrange("p (a b) -> p a b", a=shape[0])
    if len(shape) == 3:
        return v.rearrange("p (a b c) -> p a b c", a=shape[0], b=shape[1])
    raise ValueError(shape)


def build_program():
    nc = bass.Bass("TRN2", target_bir_lowering=False)

    def din(name, shape):
        return nc.dram_tensor(name, shape, F32, kind="ExternalInput").ap()

    def dout(name, shape):
        return nc.dram_tensor(name, shape, F32, kind="ExternalOutput").ap()

    xm = din("xm", [TM, DM])
    xprev = din("xprev", [TM, DM])
    xsmp = din("xsmp", [NS, DM])
    sconv = din("sconv", [NS, 3, 1536])
    sssm = din("sssm", [NS, 1024, 128])
    w_in = din("w_in", [DM, DIN])
    w_out = din("w_out", [DM, DM])
    w_gate = din("w_gate", [DM, DFF])
    w_up = din("w_up", [DM, DFF])
    w_down = din("w_down", [DFF, DM])
    cst_d = din("cst", [128, CSTW])
    cst2_d = din("cst2", [128, CST2W])
    ws_d = din("ws", [8, 128, 128])
    g_mix = din("g_mix", [DM])
    g_ffn = din("g_ffn", [DM])
    g_fin = din("g_fin", [DM])
    vn_g = din("vn_g", [1024])
    ssd_g = din("ssd_g", [1024])

    y_m = dout("y_m", [TM, DM])
    y_s = dout("y_s", [NS, DM])
    nconv_p = dout("nconv_p", [3, 1536])
    nssm_p = dout("nssm_p", [1024, 128])
    nconv_s = dout("nconv_s", [NS, 3, 1536])
    nssm_s = dout("nssm_s", [NS, 1024, 128])
    v_s = dout("v_s", [NS, 1024])
    dbg_d = {}
    for name, (shape, dty) in DEBUG.items():
        dbg_d[name] = nc.dram_tensor("dbg_" + name, list(shape), dty, kind="ExternalOutput").ap()

    P = Prog()
    with ExitStack() as es:
        arena = es.enter_context(nc.sbuf_tensor("arena", [128, AW], F32))
        rhsE = es.enter_context(nc.sbuf_tensor("rhsE", [128, 2048], F32))
        U32 = es.enter_context(nc.sbuf_tensor("U32", [128, 128], F32))
        pb = [es.enter_context(nc.psum_tensor(f"pb{i}", [128, 512], F32)) for i in range(8)]
        sems = {}
        for e in ENGS:
            sems[e] = es.enter_context(nc.semaphore("s_" + e))
        for e in ("sp", "pool"):
            for i in range(NDMASEM):
                sems[(e, i)] = es.enter_context(nc.semaphore(f"d_{e}_{i}"))
        block = es.enter_context(nc.Block())

        arena = arena[:, :]
        rhsEr = rhsE[:, :].bitcast(F32R)
        U32r = U32[:, :].bitcast(F32R)
        pbf = [p[:, :] for p in pb]
        pbb = [p[:, :].bitcast(BF16) for p in pb]

        fx = Bump(arena, 0, AW)
        CST = fx.f32([CSTW])
        identF = CST[:, C_ID:C_ID + 128]
        triF = CST[:, C_TRI:C_TRI + 128]
        UF = CST[:, C_U:C_U + 128]
        onesF = CST[:, C_ONE:C_ONE + 128]
        cwb = CST[:, C_CWB:C_CWB + 60].rearrange("p (j k) -> p j k", j=12)
        gon_pp = CST[:, C_GON:C_GON + 8]
        alog_bc = CST[:, C_ALOG:C_ALOG + 16]
        dtb_bc = CST[:, C_DTB:C_DTB + 16]
        D_bc = CST[:, C_D:C_D + 16]
        w00_bc = CST[:, C_W00:C_W00 + 8]
        b0_bc = CST[:, C_B0:C_B0 + 8]
        flag = CST[:, C_FLAG:C_FLAG + 1]
        D_pp = CST[:, C_DPP:C_DPP + 8]
        sg_pp = CST[:, C_SGPP:C_SGPP + 8]
        dtb_pp = CST[:, C_DTBPP:C_DTBPP + 1]
        al_pp = CST[:, C_ALPP:C_ALPP + 1]
        identB = fx.bf16([128])
        onesB = fx.bf16([128])
        aneg = fx.f32([16])
        hT = fx.f32([1024])
        hTb = fx.bf16([1024])
        prefix = fx.f32([12, 3])
        ncv = fx.f32([12, 3])
        nT = fx.bf16([16, T])
        YT = fx.bf16([16, T])
        r2_start = fx.off - (16 * T) // 2
        r2_end = fx.off
        Wfix = [fx.bf16([16, 512]) for _ in range(2)]
        _wflat = [w_.rearrange("p a b -> p (a b)") for w_ in Wfix]
        WG = [w_[:, 0:4096].rearrange("p (a b) -> p a b", a=16) for w_ in _wflat]
        WU = [w_[:, 4096:8192].rearrange("p (a b) -> p a b", a=16) for w_ in _wflat]
        wq = [0]
        S0 = fx.off
        sc = Bump(arena, S0, AW)
        r2 = Bump(arena, r2_start, r2_end)

        KC = ("cst",)

        P.op("sp", lambda e: e.dma_start(out=CST, in_=cst_d[:, :]), w=[KC], dma=True)
        P.op("dve", lambda e: e.tensor_copy(out=identB, in_=identF), r=[KC], w=["identB"])
        P.op("dve", lambda e: e.memset(onesB, 1.0), w=["onesB"])
        P.op("dve", lambda e: e.tensor_copy(out=U32r, in_=UF), r=[KC], w=["U32"])
        P.op("act", lambda e: e.activation(out=aneg, in_=alog_bc, func=AF.Exp), r=[KC], w=["aneg0"])
        P.op("dve", lambda e: e.tensor_scalar(out=aneg, in0=aneg, scalar1=-1.0, scalar2=None, op0=ALU.mult), r=["aneg0"], w=["aneg"])
        P.op("dve", lambda e: e.memset(hT, 0.0), w=["hT"])
        P.op("dve", lambda e: e.memset(hTb, 0.0), w=["hTb"])

        def rms_transpose(xt, kx, rows, xn, kxn, st, kst, gbc, kg, dstT, col0, kdst, width, tag):
            nk = width // 128
            P.op("act", lambda e: e.activation(out=xn[:rows, :], in_=xt[:rows, :], func=AF.Square, accum_out=st[:rows, 0:1]),
                 r=[kx], w=[kxn, kst])
            P.op("act", lambda e: e.activation(out=st[:rows, 1:2], in_=st[:rows, 0:1], func=AF.Ln, scale=1.0 / width, bias=EPS),
                 r=[kst], w=[kst])
            P.op("act", lambda e: e.activation(out=st[:rows, 2:3], in_=st[:rows, 1:2], func=AF.Exp, scale=-0.5),
                 r=[kst], w=[kst])
            P.op("dve", lambda e: e.scalar_tensor_tensor(out=xn[:rows, :], in0=xt[:rows, :], scalar=st[:rows, 2:3], op0=ALU.mult,
                                                         in1=gbc[:rows, :], op1=ALU.mult),
                 r=[kx, kst, kg, kxn], w=[kxn])
            for b8 in range(nk // 8):
                bank = tag % 2 if nk == 8 else b8
                psv = pbb[bank][:, 0:8 * rows].rearrange("p (a b) -> p a b", a=8)

                def tr(e, b8=b8, psv=psv):
                    ins = None
                    for j in range(8):
                        k = b8 * 8 + j
                        ins = e.transpose(out=psv[:, j, :], in_=xn[:rows, k * 128:(k + 1) * 128], identity=identB[:rows, :rows])
                    return ins
                P.op("pe", tr, r=[kxn, "identB"], w=[("ps", bank)])
                dst = dstT[:, b8 * 8:(b8 + 1) * 8, col0:col0 + rows]
                if b8 == 0:
                    P.op("act", lambda e, dst=dst, psv=psv: e.activation(out=dst, in_=psv, func=AF.Copy), r=[("ps", bank)], w=[kdst])
                else:
                    P.op("dve", lambda e, dst=dst, psv=psv: e.tensor_copy(out=dst, in_=psv), r=[("ps", bank)], w=[kdst])

        def mm_group(out, pairs, r, w):
            def fn(e):
                ins = None
                n = len(pairs)
                for i, (l, rh) in enumerate(pairs):
                    ins = e.matmul(out, lhsT=l, rhs=rh, start=(i == 0), stop=(i == n - 1))
                return ins
            P.op("pe", fn, r=r, w=w)

        def load_w(dst, src_ap, key, nobar=False):
            P.op("pool", lambda e: e.dma_start(out=dst, in_=src_ap), w=[key], dma=True, nobar=nobar)

        def wload(src_ap, ncols=512):
            slot = wq[0] % 2
            wq[0] += 1
            buf = Wfix[slot] if ncols == 512 else Wfix[slot][:, :, 0:ncols]
            load_w(buf, src_ap, ("W", slot), nobar=True)
            return buf, ("W", slot)

        def wview(wap, c0, ncols):
            return wap[:, c0:c0 + ncols].rearrange("(kt p) c -> p kt c", p=128)

        def conv_silu(rawpad, krp, acc, kacc, j, ntok, dst, kdst):
            P.op("dve", lambda e: e.tensor_scalar(out=acc[:, 0:ntok], in0=rawpad[:, 0:ntok], scalar1=cwb[:, j, 0:1], scalar2=cwb[:, j, 4:5],
                                                  op0=ALU.mult, op1=ALU.add), r=[krp, KC], w=[kacc])
            for k in (1, 2, 3):
                P.op("dve", lambda e, k=k: e.scalar_tensor_tensor(out=acc[:, 0:ntok], in0=rawpad[:, k:k + ntok], scalar=cwb[:, j, k:k + 1],
                                                                  op0=ALU.mult, in1=acc[:, 0:ntok], op1=ALU.add), r=[krp, kacc, KC], w=[kacc])
            P.op("act", lambda e: e.activation(out=dst, in_=acc[:, 0:ntok], func=AF.Silu), r=[kacc], w=[kdst])

        def ssd_temps(b):
            t = {}
            t["sm"] = b.f32([96])
            t["ex"] = b.f32([48])
            t["LT"] = b.f32([2048])
            t["MT"] = b.bf16([16, 128])
            t["xs_tok"] = b.bf16([16, 64])
            t["xdt"] = b.bf16([16, 64])
            t["xdtw"] = b.bf16([16, 64])
            t["B_tok"] = b.bf16([256])
            t["cbm"] = b.f32([2, 128])
            t["y1"] = b.f32([16, 64])
            t["t2"] = b.f32([16, 64])
            t["yb"] = b.bf16([1024])
            t["gn"] = b.f32([8])
            return t

        def ssd_chunk(t, xT, c, dtraw_c, kdt, main, ztok_c=None, ssdg_bc=None):
            sm, ex = t["sm"], t["ex"]
            cs = slice(c * 128, (c + 1) * 128)
            kx = ("xT",)
            P.op("dve", lambda e: e.tensor_tensor(out=sm[:, 0:16], in0=dtraw_c, in1=dtb_bc, op=ALU.add), r=[kdt, KC], w=["sm0"])
            P.op("act", lambda e: e.activation(out=sm[:, 16:32], in_=sm[:, 0:16], func=AF.Exp), r=["sm0"], w=["sm1"])
            P.op("act", lambda e: e.activation(out=sm[:, 32:48], in_=sm[:, 16:32], func=AF.Ln, bias=1.0), r=["sm1"], w=["dtc"])
            dtc = sm[:, 32:48]
            dta = sm[:, 48:64]
            P.op("dve", lambda e: e.tensor_tensor(out=dta, in0=dtc, in1=aneg, op=ALU.mult), r=["dtc", "aneg"], w=["dta"])

            def small(e):
                e.matmul(pbf[0][:, 0:16], lhsT=UF, rhs=dta, start=True, stop=True)
                ins = e.matmul(pbf[0][:, 16:32], lhsT=onesF, rhs=dta, start=True, stop=True)
                if main:
                    ins = e.matmul(pbf[0][:, 32:48], lhsT=triF, rhs=dta, start=True, stop=True)
                return ins
            P.op("pe", small, r=["dta", KC], w=[("ps", 0)])
            nex = 48 if main else 32
            P.op("act", lambda e: e.activation(out=ex[:, 0:nex], in_=pbf[0][:, 0:nex], func=AF.Exp), r=[("ps", 0)], w=["ex"])
            toend, dec, ea = ex[:, 0:16], ex[:, 16:32], ex[:, 32:48]

            psx = pbb[1][:, 0:1024].rearrange("p (a b) -> p a b", a=8)

            def trx(e):
                ins = None
                for q in range(8):
                    ins = e.transpose(out=psx[:, q, :], in_=xT[:, q, cs], identity=identB)
                return ins
            P.op("pe", trx, r=[kx, "identB"], w=[("ps", 1)])
            psB = pbb[2][:, 0:256].rearrange("p (a b) -> p a b", a=2)

            def trb(e):
                ins = None
                for g in range(2):
                    ins = e.transpose(out=psB[:, g, :], in_=xT[:, 8 + g, cs], identity=identB)
                return ins
            P.op("pe", trb, r=[kx, "identB"], w=[("ps", 2)])
            psx3 = pbb[1][:, 0:1024].rearrange("p (a b) -> p a b", a=16)
            w2 = sm[:, 64:80]
            P.op("dve", lambda e: e.tensor_tensor(out=w2, in0=dtc, in1=toend, op=ALU.mult), r=["dtc", "ex"], w=["w2"])
            P.op("dve", lambda e: e.tensor_tensor(out=t["xdtw"], in0=psx3, in1=w2.unsqueeze(2).broadcast_to([128, 16, 64]), op=ALU.mult),
                 r=[("ps", 1), "w2"], w=["xdtw"])
            P.op("act", lambda e: e.activation(out=t["B_tok"], in_=pbb[2][:, 0:256], func=AF.Copy), r=[("ps", 2)], w=["B_tok"])
            if main and SSD_LEVEL == 0:
                return
            if main:
                P.op("dve", lambda e: e.tensor_tensor(out=t["xdt"], in0=psx3, in1=dtc.unsqueeze(2).broadcast_to([128, 16, 64]), op=ALU.mult),
                     r=[("ps", 1), "dtc"], w=["xdt"])
                P.op("act", lambda e: e.activation(out=t["xs_tok"], in_=psx3, func=AF.Copy), r=[("ps", 1)], w=["xs_tok"])
                for e16 in range(16 if SSD_LEVEL >= 2 else 0):
                    P.op("dve", lambda e, e16=e16: e.tensor_scalar(out=rhsEr[:, e16 * 128:(e16 + 1) * 128], in0=triF, scalar1=dta[:, e16:e16 + 1],
                                                                   scalar2=None, op0=ALU.mult), r=["dta", KC], w=[("rhsE", e16 // 4)])
                for i in range(4 if SSD_LEVEL >= 2 else 0):
                    P.op("pe", lambda e, i=i: e.matmul(pbf[3 + i], lhsT=U32r, rhs=rhsEr[:, i * 512:(i + 1) * 512], start=True, stop=True),
                         r=[("rhsE", i), "U32"], w=[("ps", 3 + i)])
                    P.op("act", lambda e, i=i: e.activation(out=t["LT"][:, i * 512:(i + 1) * 512], in_=pbf[3 + i], func=AF.Exp),
                         r=[("ps", 3 + i)], w=[("LT", i)])
                psc = pbf[7][:, 0:256].rearrange("p (a b) -> p a b", a=2)
                if SSD_LEVEL < 3:
                    return

                def cbf(e):
                    ins = None
                    for g in range(2):
                        ins = e.matmul(psc[:, g, :], lhsT=xT[:, 8 + g, cs], rhs=xT[:, 10 + g, cs], start=True, stop=True)
                    return ins
                P.op("pe", cbf, r=[kx], w=[("ps", 7)])
                P.op("dve", lambda e: e.tensor_tensor(out=t["cbm"], in0=psc, in1=triF.unsqueeze(1).broadcast_to([128, 2, 128]), op=ALU.mult),
                     r=[("ps", 7), KC], w=["cbm"])
                LT3 = t["LT"].rearrange("p (a b) -> p a b", a=16)
                for g in range(2):
                    P.op("dve", lambda e, g=g: e.tensor_tensor(out=t["MT"][:, g * 8:(g + 1) * 8, :], in0=LT3[:, g * 8:(g + 1) * 8, :],
                                                               in1=t["cbm"][:, g:g + 1, :].broadcast_to([128, 8, 128]), op=ALU.mult),
                         r=[("LT", 2 * g), ("LT", 2 * g + 1), "cbm"], w=[("MT", g)])
                if SSD_LEVEL < 4:
                    return
                for g in range(2):
                    def yd(e, g=g):
                        ins = None
                        for j in range(8):
                            e16 = g * 8 + j
                            ins = e.matmul(pbf[3 + g][:, j * 64:(j + 1) * 64], lhsT=t["MT"][:, e16, :], rhs=t["xdt"][:, e16, :], start=True, stop=True)
                        return ins
                    P.op("pe", yd, r=[("MT", g), "xdt"], w=[("ps", 3 + g)])
                    P.op("pe", lambda e, g=g: e.matmul(pbf[5 + g], lhsT=xT[:, 10 + g, cs], rhs=hTb[:, g * 512:(g + 1) * 512], start=True, stop=True),
                         r=[kx, "hTb"], w=[("ps", 5 + g)])
                y1 = t["y1"]
                for g in range(2):
                    y1g = y1[:, g * 8:(g + 1) * 8, :]
                    P.op("dve", lambda e, g=g, y1g=y1g: e.tensor_tensor(out=y1g, in0=pbf[5 + g].rearrange("p (a b) -> p a b", a=8),
                                                                        in1=ea[:, g * 8:(g + 1) * 8].unsqueeze(2).broadcast_to([128, 8, 64]), op=ALU.mult),
                         r=[("ps", 5 + g), "ex"], w=[("y1", g)])
                    P.op("dve", lambda e, g=g, y1g=y1g: e.tensor_tensor(out=y1g, in0=pbf[3 + g].rearrange("p (a b) -> p a b", a=8), in1=y1g, op=ALU.add),
                         r=[("ps", 3 + g), ("y1", g)], w=[("y1", g)])
                P.op("pool", lambda e: e.tensor_tensor(out=t["t2"], in0=t["xs_tok"], in1=D_bc.unsqueeze(2).broadcast_to([128, 16, 64]), op=ALU.mult),
                     r=["xs_tok", KC], w=["t2"])
                P.op("dve", lambda e: e.tensor_tensor(out=y1, in0=y1, in1=t["t2"], op=ALU.add), r=[("y1", 0), ("y1", 1), "t2"], w=[("y1", 0), ("y1", 1)])
                y1f = y1.rearrange("p a b -> p (a b)")
                P.op("dve", lambda e: e.tensor_tensor(out=y1f, in0=y1f, in1=ztok_c, op=ALU.mult), r=[("y1", 0), ("y1", 1), "z_tok"], w=[("y1", 0), ("y1", 1)])
                if SSD_LEVEL < 5:
                    return
                gn = t["gn"]
                t2f = t["t2"].rearrange("p a b -> p (a b)")
                for g in range(2):
                    P.op("act", lambda e, g=g: e.activation(out=t2f[:, g * 512:(g + 1) * 512], in_=y1f[:, g * 512:(g + 1) * 512], func=AF.Square,
                                                            accum_out=gn[:, g:g + 1]), r=[("y1", g)], w=["t2", ("gn", g)])
                P.op("act", lambda e: e.activation(out=gn[:, 2:4], in_=gn[:, 0:2], func=AF.Ln, scale=1.0 / 512, bias=EPS), r=[("gn", 0), ("gn", 1)], w=["gn2"])
                P.op("act", lambda e: e.activation(out=gn[:, 4:6], in_=gn[:, 2:4], func=AF.Exp, scale=-0.5), r=["gn2"], w=["gn4"])
                for g in range(2):
                    P.op("dve", lambda e, g=g: e.scalar_tensor_tensor(out=t["yb"][:, g * 512:(g + 1) * 512], in0=y1f[:, g * 512:(g + 1) * 512],
                                                                      scalar=gn[:, 4 + g:5 + g], op0=ALU.mult, in1=ssdg_bc[:, g * 512:(g + 1) * 512], op1=ALU.mult),
                         r=[("y1", g), "gn4", "ssdg"], w=[("yb", g)])
                if SSD_LEVEL < 6:
                    return
                psy = pbb[2][:, 0:1024].rearrange("p (a b) -> p a b", a=8)

                def try_(e):
                    ins = None
                    for q in range(8):
                        ins = e.transpose(out=psy[:, q, :], in_=t["yb"][:, q * 128:(q + 1) * 128], identity=identB)
                    return ins
                P.op("pe", try_, r=[("yb", 0), ("yb", 1), "identB"], w=[("ps", 2)])
                P.op("act", lambda e: e.activation(out=YT[:, 8:16, cs], in_=psy, func=AF.Copy), r=[("ps", 2)], w=[("YTb", c)])
            sb = (7, 1)
            for g in range(2):
                P.op("pe", lambda e, g=g: e.matmul(pbf[sb[g]], lhsT=t["B_tok"][:, g * 128:(g + 1) * 128], rhs=t["xdtw"].rearrange("p a b -> p (a b)")[:, g * 512:(g + 1) * 512],
                                                   start=True, stop=True), r=["B_tok", "xdtw"], w=[("ps", sb[g])])
            hT3 = hT.rearrange("p (a b) -> p a b", a=16)
            P.op("dve", lambda e: e.tensor_tensor(out=hT3, in0=hT3, in1=dec.unsqueeze(2).broadcast_to([128, 16, 64]), op=ALU.mult), r=["hT", "ex"], w=["hT"])
            for g in range(2):
                P.op("dve", lambda e, g=g: e.tensor_tensor(out=hT[:, g * 512:(g + 1) * 512], in0=pbf[sb[g]], in1=hT[:, g * 512:(g + 1) * 512], op=ALU.add),
                     r=[("ps", sb[g]), "hT"], w=["hT"])
            P.op("act", lambda e: e.activation(out=hTb, in_=hT, func=AF.Copy), r=["hT"], w=["hTb"])

        dbg_ops = []

        def dbg_dump(name, ap, keys):
            if name in dbg_d:
                dbg_ops.append((name, ap, keys))

        def finish():
            P.barrier()
            outs = []
            for name, ap, keys in dbg_ops:
                k = ("dbgout", name)
                P.op("sp", lambda e, name=name, ap=ap: e.dma_start(out=dbg_d[name], in_=ap), r=keys, w=[k], dma=True)
                outs.append(k)
            P.op("sp", None, r=outs + OUTKEYS)
            P.resolve()
            P.emit(sems, block)

        OUTKEYS = []

        sc.reset()
        X = [sc.f32([DM]) for _ in range(2)]
        XN = [sc.bf16([DM]) for _ in range(2)]
        ST = [sc.f32([4]) for _ in range(2)]
        gbc = sc.f32([DM])
        m_common = sc.mark()
        nPT = YT[:, :, 0:TM]
        P.op("sp", lambda e, gbc=gbc: e.dma_start(out=gbc, in_=g_mix.partition_broadcast(128)), w=["gbc"], dma=True)
        for tt in range(8):
            i = tt % 2
            P.op("sp", lambda e, tt=tt, i=i: e.dma_start(out=X[i], in_=xprev[tt * 128:(tt + 1) * 128, :]), w=[("X", i)], dma=True)
            rms_transpose(X[i], ("X", i), 128, XN[i], ("XN", i), ST[i], ("ST", i), gbc, "gbc", nPT, tt * 128, ("nPT", tt), DM, tt)
        Wdt = sc.bf16([16, 16])
        xTp = sc.bf16([10, TM])
        dtraw_p = sc.f32([8, 16])
        m_after_dt = sc.mark()
        rawpad = [sc.f32([TM + 4]) for _ in range(2)]
        accb = [sc.f32([TM]) for _ in range(2)]
        blocks = [(COL_X, 512), (COL_X + 512, 512), (COL_B, 512)]
        load_w(Wdt, wview(w_in, COL_DT, 16), "Wdt")
        wcur = wload(wview(w_in, blocks[0][0], blocks[0][1]))
        for i in range(2):
            P.op("dve", lambda e, rp=rawpad[i]: e.memset(rp[:, 0:3], 0.0), w=[("rawpad", i)])
        jt = 0
        for bi, (c0, ncol) in enumerate(blocks):
            Wv, kW = wcur
            if bi + 1 < len(blocks):
                wcur = wload(wview(w_in, blocks[bi + 1][0], blocks[bi + 1][1]))
            for jj in range(4):
                j = (c0 - COL_X) // 128 + jj
                rp = rawpad[jt % 2]
                krp = ("rawpad", jt % 2)
                isC = j >= 10
                for nb in range(2):
                    if isC and nb == 0:
                        continue
                    bank = 3 + (jt * 2 + nb) % 4
                    mm_group(pbf[bank], [(Wv[:, k, jj * 128:(jj + 1) * 128], nPT[:, k, nb * 512:(nb + 1) * 512]) for k in range(16)],
                             r=[kW] + [("nPT", tt) for tt in range(nb * 4, nb * 4 + 4)], w=[("ps", bank)])
                    P.op("act", lambda e, rp=rp, nb=nb, bank=bank: e.activation(out=rp[:, 3 + nb * 512:3 + (nb + 1) * 512], in_=pbf[bank], func=AF.Copy),
                         r=[("ps", bank)], w=[krp])
                P.op("dve", lambda e, rp=rp, j=j: e.tensor_copy(out=prefix[:, j, :], in_=rp[:, TM:TM + 3]), r=[krp], w=["prefix"])
                if not isC:
                    conv_silu(rp, krp, accb[jt % 2], ("acc", jt % 2), j, TM, xTp[:, j, :], ("xT",))
                jt += 1
        for tt in range(8):
            mm_group(pbf[0][:, 0:16], [(nPT[:, k, tt * 128:(tt + 1) * 128], Wdt[:, k, :]) for k in range(16)], r=["Wdt", ("nPT", tt)], w=[("ps", 0)])
            P.op("act", lambda e, tt=tt: e.activation(out=dtraw_p[:, tt, :], in_=pbf[0][:, 0:16], func=AF.Copy), r=[("ps", 0)], w=["dtraw"])
        P.barrier()
        sc.reset(m_after_dt)
        tP = ssd_temps(sc)
        for c in range(8):
            ssd_chunk(tP, xTp, c, dtraw_p[:, c, :], "dtraw", main=False)
        P.op("dve", lambda e: e.tensor_scalar(out=hT, in0=hT, scalar1=flag, scalar2=None, op0=ALU.mult), r=["hT", KC], w=["hT"])
        P.op("act", lambda e: e.activation(out=hTb, in_=hT, func=AF.Copy), r=["hT"], w=["hTb"])
        P.barrier()
        if STOP_AFTER == "P":
            dbg_dump("nPT", YT.rearrange("p a b -> p (a b)"), [("nPT", tt) for tt in range(8)])
            dbg_dump("xTp", xTp.rearrange("p a b -> p (a b)"), [("xT",)])
            dbg_dump("dtraw", dtraw_p.rearrange("p a b -> p (a b)"), ["dtraw"])
            dbg_dump("gbc", gbc, ["gbc"])
            dbg_dump("hT", hT, ["hT"])
            dbg_dump("prefix", prefix.rearrange("p a b -> p (a b)"), ["prefix"])
            finish()
            return nc

        sc.reset(m_common)
        for tt in range(9):
            i = tt % 2
            rows = 128 if tt < 8 else NS
            src = xm[tt * 128:(tt + 1) * 128, :] if tt < 8 else xsmp[:, :]
            P.op("sp", lambda e, i=i, rows=rows, src=src: e.dma_start(out=X[i][:rows, :], in_=src), w=[("X", i)], dma=True)
            rms_transpose(X[i], ("X", i), rows, XN[i], ("XN", i), ST[i], ("ST", i), gbc, "gbc", nT, tt * 128, ("nT", tt), DM, tt)
        P.barrier()
        if STOP_AFTER == "A0":
            dbg_dump("nT", nT.rearrange("p a b -> p (a b)"), [("nT", tt) for tt in range(9)])
            finish()
            return nc
        sc.reset()
        xT = sc.bf16([12, T])
        z_tok = sc.bf16([8, 1024])
        zT_s = sc.f32([8, NS])
        dtraw = sc.f32([8, 16])
        dtT_s = sc.f32([NS])
        ssdg_bc = sc.f32([1024])
        Rsel = sc.f32([1024])
        raw_s = sc.f32([12, NS])
        m_A = sc.mark()
        Wdt = sc.bf16([16, 16])
        rawpad = [sc.f32([TM + 4]) for _ in range(2)]
        accb = [sc.f32([TM]) for _ in range(2)]
        scTok = sc.f32([3 * 1536])
        scT = sc.f32([36, NS])
        acc_s = sc.f32([12, NS])
        P.op("sp", lambda e: e.dma_start(out=ssdg_bc, in_=ssd_g.partition_broadcast(128)), w=["ssdg"], dma=True)
        P.op("sp", lambda e: e.dma_start(out=Rsel, in_=cst2_d[:, C2_RSEL:C2_RSEL + 1024]), w=["cst2"], dma=True)
        P.op("sp", lambda e: e.dma_start(out=scTok[:NS, :], in_=sconv.rearrange("b r c -> b (r c)")), w=["scTok"], dma=True)
        P.op("sp", lambda e: e.dma_start(out=nconv_s[:, 0:2, :], in_=sconv[:, 1:3, :]), w=["o_ncs01"], dma=True)
        OUTKEYS.append("o_ncs01")
        for half in range(2):
            def trs(e, half=half):
                ins = None
                for idx in range(half * 18, half * 18 + 18):
                    r_, j_ = idx // 12, idx % 12
                    ii = idx - half * 18
                    ins = e.transpose(out=pbf[half][:, ii * NS:(ii + 1) * NS], in_=scTok[:NS, r_ * 1536 + j_ * 128:r_ * 1536 + (j_ + 1) * 128],
                                      identity=identF[:NS, :NS])
                return ins
            P.op("pe", trs, r=["scTok", KC], w=[("ps", half)])
            P.op("dve", lambda e, half=half: e.tensor_copy(out=scT[:, half * 18:half * 18 + 18, :].rearrange("p a b -> p (a b)"), in_=pbf[half][:, 0:18 * NS]),
                 r=[("ps", half)], w=["scT"])
        blocksA = [(COL_Z, "z"), (COL_Z + 512, "z"), (COL_X, "x"), (COL_X + 512, "x"), (COL_B, "x")]
        load_w(Wdt, wview(w_in, COL_DT, 16), "Wdt")
        wcur = wload(wview(w_in, blocksA[0][0], 512))
        jt = 0
        grp = 0
        for bi, (c0, kind) in enumerate(blocksA):
            Wv, kW = wcur
            if bi + 1 < len(blocksA):
                wcur = wload(wview(w_in, blocksA[bi + 1][0], 512))
            if kind == "z":
                zb = (c0 - COL_Z) // 512
                for tt in range(8):
                    bank = 2 + grp % 4
                    grp += 1
                    mm_group(pbf[bank], [(nT[:, k, tt * 128:(tt + 1) * 128], Wv[:, k, :]) for k in range(16)], r=[kW, ("nT", tt)], w=[("ps", bank)])
                    P.op("act", lambda e, tt=tt, zb=zb, bank=bank: e.activation(out=z_tok[:, tt, zb * 512:(zb + 1) * 512], in_=pbf[bank], func=AF.Silu),
                         r=[("ps", bank)], w=["z_tok"])
                for jj in range(4):
                    bank = 2 + grp % 4
                    grp += 1
                    j = zb * 4 + jj
                    mm_group(pbf[bank][:, 0:NS], [(Wv[:, k, jj * 128:(jj + 1) * 128], nT[:, k, TM:T]) for k in range(16)], r=[kW, ("nT", 8)], w=[("ps", bank)])
                    P.op("act", lambda e, j=j, bank=bank: e.activation(out=zT_s[:, j, :], in_=pbf[bank][:, 0:NS], func=AF.Silu), r=[("ps", bank)], w=["zT_s"])
            else:
                for jj in range(4):
                    j = (c0 - COL_X) // 128 + jj
                    rp = rawpad[jt % 2]
                    krp = ("rawpad", jt % 2)
                    P.op("dve", lambda e, rp=rp, j=j: e.tensor_copy(out=rp[:, 0:3], in_=prefix[:, j, :]), r=["prefix"], w=[krp])
                    for nb in range(3):
                        bank = 2 + grp % 4
                        grp += 1
                        lo, hi = (nb * 512, (nb + 1) * 512) if nb < 2 else (TM, T)
                        n = hi - lo
                        rk = [("nT", tt) for tt in range(nb * 4, nb * 4 + 4)] if nb < 2 else [("nT", 8)]
                        mm_group(pbf[bank][:, 0:n], [(Wv[:, k, jj * 128:(jj + 1) * 128], nT[:, k, lo:hi]) for k in range(16)], r=[kW] + rk, w=[("ps", bank)])
                        if nb < 2:
                            P.op("act", lambda e, rp=rp, lo=lo, hi=hi, bank=bank: e.activation(out=rp[:, 3 + lo:3 + hi], in_=pbf[bank], func=AF.Copy),
                                 r=[("ps", bank)], w=[krp])
                        else:
                            P.op("act", lambda e, j=j, bank=bank: e.activation(out=raw_s[:, j, :], in_=pbf[bank][:, 0:NS], func=AF.Copy),
                                 r=[("ps", bank)], w=["raw_s"])
                    P.op("dve", lambda e, rp=rp, j=j: e.tensor_copy(out=ncv[:, j, :], in_=rp[:, TM:TM + 3]), r=[krp], w=["ncv"])
                    conv_silu(rp, krp, accb[jt % 2], ("acc", jt % 2), j, TM, xT[:, j, 0:TM], ("xT",))
                    P.op("dve", lambda e, j=j: e.tensor_scalar(out=acc_s[:, j, :], in0=raw_s[:, j, :], scalar1=cwb[:, j, 3:4], scalar2=cwb[:, j, 4:5],
                                                               op0=ALU.mult, op1=ALU.add), r=["raw_s", KC], w=["acc_s"])
                    for r_ in range(3):
                        P.op("dve", lambda e, j=j, r_=r_: e.scalar_tensor_tensor(out=acc_s[:, j, :], in0=scT[:, r_ * 12 + j, :], scalar=cwb[:, j, r_:r_ + 1],
                                                                                 op0=ALU.mult, in1=acc_s[:, j, :], op1=ALU.add), r=["scT", "acc_s", KC], w=["acc_s"])
                    jt += 1
        P.op("act", lambda e: e.activation(out=xT[:, :, TM:T], in_=acc_s, func=AF.Silu), r=["acc_s"], w=[("xT",)])
        for tt in range(8):
            mm_group(pbf[0][:, 0:16], [(nT[:, k, tt * 128:(tt + 1) * 128], Wdt[:, k, :]) for k in range(16)], r=["Wdt", ("nT", tt)], w=[("ps", 0)])
            P.op("act", lambda e, tt=tt: e.activation(out=dtraw[:, tt, :], in_=pbf[0][:, 0:16], func=AF.Copy), r=[("ps", 0)], w=["dtraw"])
        mm_group(pbf[1][:NS, 0:NS], [(Wdt[:, k, :], nT[:, k, TM:T]) for k in range(16)], r=["Wdt", ("nT", 8)], w=[("ps", 1)])
        P.op("act", lambda e: e.activation(out=dtT_s[:NS, :], in_=pbf[1][:NS, 0:NS], func=AF.Copy), r=[("ps", 1)], w=["dtT_s"])

        def rows_out(src3, ksrc, R, stage, kst, dram_ap, okey):
            for b3 in range(3):
                def trr(e, b3=b3):
                    ins = None
                    for jj in range(4):
                        j_ = b3 * 4 + jj
                        ins = e.transpose(out=pbf[5 + b3][:R, jj * 128:(jj + 1) * 128], in_=src3[:, j_, :], identity=identF)
                    return ins
                P.op("pe", trr, r=[ksrc, KC], w=[("ps", 5 + b3)])
                P.op("dve", lambda e, b3=b3: e.tensor_copy(out=stage[:R, b3 * 512:(b3 + 1) * 512], in_=pbf[5 + b3][:R, :]), r=[("ps", 5 + b3)], w=[kst, "scTok"])
            P.op("sp", lambda e: e.dma_start(out=dram_ap, in_=stage[:R, 0:1536]), r=[kst], w=[okey], dma=True)
            OUTKEYS.append(okey)
        rows_out(raw_s, "raw_s", NS, scTok[:, 0:1536], "stg0", nconv_s[:, 2, :], "o_ncs2")
        rows_out(ncv, "ncv", 3, scTok[:, 1536:3072], "stg1", nconv_p[:, :], "o_ncp")
        P.barrier()
        if STOP_AFTER == "A1":
            dbg_dump("xT", xT.rearrange("p a b -> p (a b)"), [("xT",)])
            dbg_dump("z_tok", z_tok.rearrange("p a b -> p (a b)"), ["z_tok"])
            dbg_dump("zT_s", zT_s.rearrange("p a b -> p (a b)"), ["zT_s"])
            dbg_dump("dtraw", dtraw.rearrange("p a b -> p (a b)"), ["dtraw"])
            dbg_dump("dtT_s", dtT_s, ["dtT_s"])
            finish()
            return nc

        sc.reset(m_A)
        tA = ssd_temps(sc)
        for c in range(8):
            ssd_chunk(tA, xT, c, dtraw[:, c, :], "dtraw", main=True, ztok_c=z_tok[:, c, :], ssdg_bc=ssdg_bc)
        if STOP_AFTER == "A2a":
            P.barrier()
            dbg_dump("YT", YT.rearrange("p a b -> p (a b)"), [("YTb", c) for c in range(8)])
            finish()
            return nc
        hout = sc.f32([8, 128])
        for half in range(2):
            def trh(e, half=half):
                ins = None
                for jj in range(4):
                    q = half * 4 + jj
                    ins = e.transpose(out=pbf[3 + half][:, jj * 128:(jj + 1) * 128], in_=hT[:, q * 128:(q + 1) * 128], identity=identF)
                return ins
            P.op("pe", trh, r=["hT", KC], w=[("ps", 3 + half)])
            P.op("dve", lambda e, half=half: e.tensor_copy(out=hout[:, half * 4:(half + 1) * 4, :].rearrange("p a b -> p (a b)"), in_=pbf[3 + half]),
                 r=[("ps", 3 + half)], w=["hout"])
        P.op("sp", lambda e: e.dma_start(out=nssm_p.rearrange("(q r) n -> r q n", r=128), in_=hout), r=["hout"], w=["o_nsp"], dma=True)
        OUTKEYS.append("o_nsp")

        if STOP_AFTER == "A2b":
            P.barrier()
            dbg_dump("YT", YT.rearrange("p a b -> p (a b)"), [("YTb", c) for c in range(8)])
            finish()
            return nc
        P.barrier()
        sc.reset(m_A)
        sm_s = sc.f32([8, NS])
        cat = sc.f32([32])
        rep = sc.f32([8, 32])
        dtx = sc.f32([8, NS])
        ys = sc.f32([8, NS])
        ysq = sc.f32([8, NS])
        rs = sc.f32([2, NS])
        BC_tok = sc.bf16([512])
        selB = sc.bf16([NS, 128])
        NHS = 4
        hs = [sc.f32([8, 128]) for _ in range(NHS)]
        junk = sc.f32([128])
        x0 = sm_s[:NS, 0, :]; e1 = sm_s[:NS, 1, :]; anp = sm_s[:NS, 2, 0:1]
        P.op("act", lambda e: e.activation(out=e1, in_=dtT_s[:NS, :], func=AF.Exp, bias=dtb_pp[:NS, :]), r=["dtT_s", KC], w=["s_e1"])
        P.op("act", lambda e: e.activation(out=cat[:NS, 16:32], in_=e1, func=AF.Ln, bias=1.0), r=["s_e1"], w=["s_dt"])
        P.op("act", lambda e: e.activation(out=anp, in_=al_pp[:NS, :], func=AF.Exp), r=[KC], w=["s_anp0"])
        P.op("dve", lambda e: e.tensor_scalar(out=anp, in0=anp, scalar1=-1.0, scalar2=None, op0=ALU.mult), r=["s_anp0"], w=["s_anp"])
        P.op("act", lambda e: e.activation(out=cat[:NS, 0:16], in_=cat[:NS, 16:32], func=AF.Exp, scale=anp), r=["s_dt", "s_anp"], w=["s_dA"])

        def repf(e):
            ins = None
            for q in range(8):
                ins = e.matmul(pbf[0][:, q * 32:(q + 1) * 32], lhsT=Rsel[:NS, q * 128:(q + 1) * 128], rhs=cat[:NS, :], start=True, stop=True)
            return ins
        P.op("pe", repf, r=["s_dA", "s_dt", "cst2"], w=[("ps", 0)])
        P.op("dve", lambda e: e.tensor_copy(out=rep.rearrange("p a b -> p (a b)"), in_=pbf[0][:, 0:256]), r=[("ps", 0)], w=["rep"])
        xs_s = xT[:, 0:8, TM:T]
        P.op("dve", lambda e: e.tensor_tensor(out=dtx, in0=rep[:, :, 16:32], in1=xs_s, op=ALU.mult), r=["rep", ("xT",)], w=["dtx"])
        psbc = pbb[1][:NS, 0:512].rearrange("p (a b) -> p a b", a=4)

        def trbc(e):
            ins = None
            for jj in range(4):
                ins = e.transpose(out=psbc[:, jj, :], in_=xT[:, 8 + jj, TM:T], identity=identB)
            return ins
        P.op("pe", trbc, r=[("xT",), "identB"], w=[("ps", 1)])
        P.op("act", lambda e: e.activation(out=BC_tok[:NS, :], in_=pbb[1][:NS, 0:512], func=AF.Copy), r=[("ps", 1)], w=["BC_tok"])
        P.op("dve", lambda e: e.tensor_copy(out=selB[:NS, :, :], in_=identB[:NS, 0:NS].unsqueeze(2).broadcast_to([NS, NS, 128])), r=["identB"], w=["selB"])
        P.op("dve", lambda e: e.memset(ys, 0.0), w=["ys"])

        def hs_load(b):
            P.op("sp", lambda e, b=b: e.dma_start(out=hs[b % NHS], in_=sssm[b].rearrange("(q r) n -> r q n", r=128)), w=[("hs", b % NHS, q) for q in range(8)], dma=True)
        for b in range(min(NHS - 1, NS)):
            hs_load(b)
        for b in range(NS):
            hb = hs[b % NHS]
            bank = 2 + b % 2
            if b + NHS - 1 < NS:
                hs_load(b + NHS - 1)
            P.op("pe", lambda e, b=b, bank=bank: e.matmul(pbf[bank], lhsT=selB[:NS, b, :], rhs=BC_tok[:NS, :], start=True, stop=True),
                 r=["selB", "BC_tok"], w=[("ps", bank)])
            for q in range(8):
                P.op("act", lambda e, b=b, q=q, hb=hb: e.activation(out=hb[:, q, :], in_=hb[:, q, :], func=AF.Copy, scale=rep[:, q, b:b + 1]),
                     r=[("hs", b % NHS, q), "rep"], w=[("hs", b % NHS, q)])

            def upd(q, b=b, hb=hb, bank=bank):
                g = q // 4
                P.op("dve", lambda e: e.scalar_tensor_tensor(out=hb[:, q, :], in0=pbf[bank][:, g * 128:(g + 1) * 128], scalar=dtx[:, q, b:b + 1], op0=ALU.mult,
                                                             in1=hb[:, q, :], op1=ALU.add), r=[("ps", bank), "dtx", ("hs", b % NHS, q)], w=[("hs", b % NHS, q)])

            def yacc(q, b=b, hb=hb, bank=bank):
                g = q // 4
                P.op("dve", lambda e: e.scalar_tensor_tensor(out=junk, in0=hb[:, q, :], scalar=1.0, op0=ALU.mult,
                                                             in1=pbf[bank][:, 256 + g * 128:256 + (g + 1) * 128], op1=ALU.mult,
                                                             accum_out=ys[:, q, b:b + 1]), r=[("hs", b % NHS, q), ("ps", bank)], w=["ys", "junk"])
            upd(0)
            upd(1)
            for q in range(8):
                yacc(q)
                if q + 2 < 8:
                    upd(q + 2)
            ok = ("o_nss", b)
            P.op("pool", lambda e, b=b, hb=hb: e.dma_start(out=nssm_s[b].rearrange("(q r) n -> r q n", r=128), in_=hb), r=[("hs", b % NHS, q) for q in range(8)], w=[ok], dma=True)
            OUTKEYS.append(ok)
        P.op("dve", lambda e: e.tensor_tensor(out=ysq, in0=xs_s, in1=D_pp.unsqueeze(2).broadcast_to([128, 8, NS]), op=ALU.mult), r=[("xT",), KC], w=["ysq"])
        P.op("dve", lambda e: e.tensor_tensor(out=ys, in0=ys, in1=ysq, op=ALU.add), r=["ys", "ysq"], w=["ys"])
        P.op("dve", lambda e: e.tensor_tensor(out=ys, in0=ys, in1=zT_s, op=ALU.mult), r=["ys", "zT_s"], w=["ys"])
        P.op("dve", lambda e: e.tensor_tensor(out=ysq, in0=ys, in1=ys, op=ALU.mult), r=["ys", "ysq"], w=["ysq"])

        def gsum(e):
            ins = None
            for q in range(8):
                g = q // 4
                ins = e.matmul(pbf[4][:, g * NS:(g + 1) * NS], lhsT=onesF, rhs=ysq[:, q, :], start=(q % 4 == 0), stop=(q % 4 == 3))
            return ins
        P.op("pe", gsum, r=["ysq", KC], w=[("ps", 4)])
        rsf = rs.rearrange("p a b -> p (a b)")
        P.op("act", lambda e: e.activation(out=rsf, in_=pbf[4][:, 0:2 * NS], func=AF.Ln, scale=1.0 / 512, bias=EPS), r=[("ps", 4)], w=["rs0"])
        P.op("act", lambda e: e.activation(out=rsf, in_=rsf, func=AF.Exp, scale=-0.5), r=["rs0"], w=["rs"])
        for g in range(2):
            P.op("dve", lambda e, g=g: e.tensor_tensor(out=ys[:, g * 4:(g + 1) * 4, :], in0=ys[:, g * 4:(g + 1) * 4, :],
                                                       in1=rs[:, g:g + 1, :].broadcast_to([128, 4, NS]), op=ALU.mult), r=["ys", "rs"], w=["ys"])
        P.op("dve", lambda e: e.tensor_tensor(out=YT[:, 8:16, TM:T], in0=ys, in1=sg_pp.unsqueeze(2).broadcast_to([128, 8, NS]), op=ALU.mult),
             r=["ys", KC], w=[("YTb", 8)])
        P.barrier()
        if STOP_AFTER == "A2":
            dbg_dump("YT", YT.rearrange("p a b -> p (a b)"), [("YTb", c) for c in range(9)])
            finish()
            return nc
        sc.reset()
        vng_bc = sc.f32([1024])
        bs_bc = sc.f32([8, 128])
        v_tok = sc.bf16([9, 1024])
        ya = sc.f32([8, T])
        WsT = sc.bf16([8, 128])
        wsraw = ya[:, 0, 0:1024].rearrange("p (a b) -> p a b", a=8)
        gv = [sc.f32([1024]) for _ in range(2)]
        vst = [sc.f32([4]) for _ in range(2)]
        ug = [sc.f32([T]) for _ in range(2)]
        _m_tm = sc.mark()
        tmpm = [sc.f32([512]) for _ in range(2)]
        _m_tm2 = sc.mark()
        sc.reset(_m_tm)
        vsf = sc.f32([1024])
        sc.reset(_m_tm2)
        sq = [sc.bf16([T]) for _ in range(2)]
        rstd_bc = sc.f32([T])
        w00I = sc.bf16([8, NS])
        P.op("sp", lambda e: e.dma_start(out=vng_bc, in_=vn_g.partition_broadcast(128)), w=["vng"], dma=True)
        P.op("sp", lambda e: e.dma_start(out=bs_bc.rearrange("p a b -> p (a b)"), in_=cst2_d[:, C2_BS:C2_BS + 1024]), w=["bs_bc"], dma=True)
        P.op("sp", lambda e: e.dma_start(out=wsraw, in_=ws_d.rearrange("h t s -> t h s")), w=["wsraw"], dma=True)
        wv2 = [wload(wview(w_in, COL_V, 512)), wload(wview(w_in, COL_V + 512, 512))]
        for half in range(2):
            def trw(e, half=half):
                ins = None
                for jj in range(4):
                    h_ = half * 4 + jj
                    ins = e.transpose(out=pbf[half][:, jj * 128:(jj + 1) * 128], in_=wsraw[:, h_, :], identity=identF)
                return ins
            P.op("pe", trw, r=["wsraw", KC], w=[("ps", half)])
            P.op("dve", lambda e, half=half: e.tensor_tensor(out=WsT[:, half * 4:(half + 1) * 4, :], in0=pbf[half].rearrange("p (a b) -> p a b", a=4),
                                                             in1=triF.unsqueeze(1).broadcast_to([128, 4, 128]), op=ALU.mult), r=[("ps", half), KC], w=["WsT"])
        for h_ in range(8):
            P.op("dve", lambda e, h_=h_: e.tensor_scalar(out=w00I[:NS, h_, :], in0=identF[:NS, 0:NS], scalar1=w00_bc[:NS, h_:h_ + 1], scalar2=None, op0=ALU.mult),
                 r=[KC], w=["w00I"])
        for tt in range(9):
            rows = 128 if tt < 8 else NS
            i = tt % 2
            tcols = slice(tt * 128, tt * 128 + rows)
            for zb in range(2):
                bank = 2 + (tt * 2 + zb) % 4
                mm_group(pbf[bank][:rows, :], [(nT[:, k, tcols], wv2[zb][0][:, k, :]) for k in range(16)], r=[wv2[zb][1], ("nT", tt)], w=[("ps", bank)])
                P.op("act", lambda e, i=i, zb=zb, bank=bank, rows=rows: e.activation(out=gv[i][:rows, zb * 512:(zb + 1) * 512], in_=pbf[bank][:rows, :],
                                                                                    func=AF.Gelu_apprx_tanh), r=[("ps", bank)], w=[("gv", i, zb)])
            P.op("act", lambda e, i=i, rows=rows: e.activation(out=ug[i][:rows, 0:1024], in_=gv[i][:rows, :], func=AF.Square, accum_out=vst[i][:rows, 0:1]),
                 r=[("gv", i, 0), ("gv", i, 1)], w=[("ug", i), ("vst", i)])
            P.op("act", lambda e, i=i, rows=rows: e.activation(out=vst[i][:rows, 1:2], in_=vst[i][:rows, 0:1], func=AF.Ln, scale=1.0 / 1024, bias=EPS),
                 r=[("vst", i)], w=[("vst", i)])
            P.op("act", lambda e, i=i, rows=rows: e.activation(out=vst[i][:rows, 2:3], in_=vst[i][:rows, 1:2], func=AF.Exp, scale=-0.5),
                 r=[("vst", i)], w=[("vst", i)])
            P.op("dve", lambda e, i=i, rows=rows, tt=tt: e.scalar_tensor_tensor(out=v_tok[:rows, tt, :], in0=gv[i][:rows, :], scalar=vst[i][:rows, 2:3], op0=ALU.mult,
                                                                                in1=vng_bc[:rows, :], op1=ALU.mult),
                 r=[("gv", i, 0), ("gv", i, 1), ("vst", i), "vng"], w=[("v_tok", tt)])
            if tt == 8:
                P.op("dve", lambda e, i=i: e.scalar_tensor_tensor(out=vsf[:NS, :], in0=gv[i][:NS, :], scalar=vst[i][:NS, 2:3], op0=ALU.mult,
                                                                  in1=vng_bc[:NS, :], op1=ALU.mult), r=[("gv", i, 0), ("gv", i, 1), ("vst", i), "vng"], w=["vsf", ("tmpm", 0), ("tmpm", 1)])
                P.op("sp", lambda e: e.dma_start(out=v_s[:, :], in_=vsf[:NS, :]), r=["vsf", ("tmpm", 0), ("tmpm", 1)], w=["o_vs"], dma=True)
                OUTKEYS.append("o_vs")
        wu2 = [wload(wview(w_in, COL_U, 512)), wload(wview(w_in, COL_U + 512, 512))]
        tok_blocks = [(0, 512), (512, 1024), (TM, T)]
        grp = 0
        for h_ in range(8):
            ub, jj = h_ // 4, h_ % 4
            Wv, kWu = wu2[ub]
            ui = h_ % 2
            for nb, (lo, hi) in enumerate(tok_blocks):
                bank = grp % 2
                grp += 1
                n = hi - lo
                rk = [("nT", tt) for tt in range(nb * 4, nb * 4 + 4)] if nb < 2 else [("nT", 8)]
                mm_group(pbf[bank][:, 0:n], [(Wv[:, k, jj * 128:(jj + 1) * 128], nT[:, k, lo:hi]) for k in range(16)], r=[kWu] + rk, w=[("ps", bank)])
                P.op("act", lambda e, ui=ui, lo=lo, hi=hi, n=n, bank=bank: e.activation(out=ug[ui][:, lo:hi], in_=pbf[bank][:, 0:n], func=AF.Gelu_apprx_tanh),
                     r=[("ps", bank)], w=[("ug", ui)])
            for half in range(2):
                def mix(e, half=half, h_=h_):
                    ins = None
                    for cc in range(4):
                        c = half * 4 + cc
                        ins = e.matmul(pbf[2 + half][:, cc * 128:(cc + 1) * 128], lhsT=v_tok[:, c, h_ * 128:(h_ + 1) * 128], rhs=WsT[:, h_, :], start=True, stop=True)
                    return ins
                P.op("pe", mix, r=[("v_tok", c) for c in range(half * 4, half * 4 + 4)] + ["WsT"], w=[("ps", 2 + half)])
                tm = tmpm[half]
                P.op("dve", lambda e, half=half, h_=h_, tm=tm: e.tensor_tensor(out=tm.rearrange("p (a b) -> p a b", a=4), in0=pbf[2 + half].rearrange("p (a b) -> p a b", a=4),
                                                                               in1=bs_bc[:, h_:h_ + 1, :].broadcast_to([128, 4, 128]), op=ALU.add),
                     r=[("ps", 2 + half), "bs_bc"], w=[("tmpm", half)])
                P.op("dve", lambda e, half=half, h_=h_, tm=tm, ui=ui: e.tensor_tensor(out=ya[:, h_, half * 512:(half + 1) * 512], in0=tm, in1=ug[ui][:, half * 512:(half + 1) * 512], op=ALU.mult),
                     r=[("tmpm", half), ("ug", ui)], w=[("ya", h_)])
            P.op("pe", lambda e, h_=h_: e.matmul(pbf[4][:, 0:NS], lhsT=v_tok[:NS, 8, h_ * 128:(h_ + 1) * 128], rhs=w00I[:NS, h_, :], start=True, stop=True),
                 r=[("v_tok", 8), "w00I"], w=[("ps", 4)])
            P.op("dve", lambda e, h_=h_, ui=ui: e.scalar_tensor_tensor(out=ya[:, h_, TM:T], in0=pbf[4][:, 0:NS], scalar=b0_bc[:, h_:h_ + 1], op0=ALU.add, in1=ug[ui][:, TM:T], op1=ALU.mult),
                 r=[("ps", 4), ("ug", ui), KC], w=[("ya", h_)])
            si = h_ % 2
            P.op("act", lambda e, h_=h_, si=si: e.activation(out=sq[si], in_=ya[:, h_, :], func=AF.Square), r=[("ya", h_)], w=[("sq", si)])
            for nb, (lo, hi) in enumerate(tok_blocks):
                n = hi - lo
                P.op("pe", lambda e, nb=nb, lo=lo, hi=hi, n=n, si=si, h_=h_: e.matmul(pbf[5 + nb][:, 0:n], lhsT=onesB, rhs=sq[si][:, lo:hi], start=(h_ == 0), stop=(h_ == 7)),
                     r=[("sq", si), "onesB"], w=[("ps", 5 + nb)])
        for nb, (lo, hi) in enumerate(tok_blocks):
            n = hi - lo
            P.op("act", lambda e, nb=nb, lo=lo, hi=hi, n=n: e.activation(out=rstd_bc[:, lo:hi], in_=pbf[5 + nb][:, 0:n], func=AF.Ln, scale=1.0 / 1024, bias=EPS),
                 r=[("ps", 5 + nb)], w=[("rstd", nb)])
            P.op("act", lambda e, lo=lo, hi=hi: e.activation(out=rstd_bc[:, lo:hi], in_=rstd_bc[:, lo:hi], func=AF.Exp, scale=-0.5), r=[("rstd", nb)], w=[("rstd", nb)])
        for h_ in range(8):
            P.op("dve", lambda e, h_=h_: e.scalar_tensor_tensor(out=YT[:, h_, :], in0=ya[:, h_, :], scalar=gon_pp[:, h_:h_ + 1], op0=ALU.mult, in1=rstd_bc, op1=ALU.mult),
                 r=[("ya", h_), ("rstd", 0), ("rstd", 1), ("rstd", 2), KC], w=[("YTa", h_)])
        P.barrier()
        if STOP_AFTER == "A3":
            dbg_dump("YT", YT.rearrange("p a b -> p (a b)"), [("YTa", h_) for h_ in range(8)])
            finish()
            return nc

        sc.reset()
        hres = sc.f32([9, DM])
        m_B = sc.mark()
        for tt in range(9):
            rows = 128 if tt < 8 else NS
            src = xm[tt * 128:(tt + 1) * 128, :] if tt < 8 else xsmp[:, :]
            P.op("sp", lambda e, tt=tt, rows=rows, src=src: e.dma_start(out=hres[:rows, tt, :], in_=src), w=[("h", tt)], dma=True)
        wcur = wload(wview(w_out, 0, 512))
        for cb in range(4):
            Wv, kW = wcur
            if cb + 1 < 4:
                wcur = wload(wview(w_out, (cb + 1) * 512, 512))
            for tt in range(9):
                rows = 128 if tt < 8 else NS
                tcols = slice(tt * 128, tt * 128 + rows)
                bank = (cb * 9 + tt) % 6
                mm_group(pbf[bank][:rows, :], [(YT[:, k, tcols], Wv[:, k, :]) for k in range(16)], r=[kW], w=[("ps", bank)])
                P.op("dve", lambda e, rows=rows, tt=tt, cb=cb, bank=bank: e.tensor_tensor(out=hres[:rows, tt, cb * 512:(cb + 1) * 512], in0=pbf[bank][:rows, :],
                                                                                         in1=hres[:rows, tt, cb * 512:(cb + 1) * 512], op=ALU.add),
                     r=[("ps", bank), ("h", tt)], w=[("h", tt)])
        P.barrier()
        if STOP_AFTER == "B":
            dbg_dump("h", hres.rearrange("p a b -> p (a b)"), [("h", tt) for tt in range(9)])
            finish()
            return nc

        sc.reset(m_B)
        gbc = sc.f32([DM])
        gbc2 = sc.f32([DM])
        stmp = [sc.f32([512]) for _ in range(2)]
        ST = [sc.f32([4]) for _ in range(2)]
        r2.reset()
        actb = [r2.bf16([2, T]) for _ in range(2)]
        Wd = [r2.bf16([2, DM]) for _ in range(2)]
        XN = [r2.bf16([DM]) for _ in range(2)]
        mT = nT
        P.op("sp", lambda e: e.dma_start(out=gbc, in_=g_ffn.partition_broadcast(128)), w=["gbc"], dma=True)
        P.op("sp", lambda e: e.dma_start(out=gbc2, in_=g_fin.partition_broadcast(128)), w=["gbc2"], dma=True)
        for tt in range(9):
            rows = 128 if tt < 8 else NS
            i = tt % 2
            rms_transpose(hres[:, tt, :], ("h", tt), rows, XN[i], ("XN", i), ST[i], ("ST", i), gbc, "gbc", mT, tt * 128, ("mT", tt), DM, tt)
        NFB = DFF // 256
        if STOP_AFTER == "C0":
            P.barrier()
            dbg_dump("mT", mT.rearrange("p a b -> p (a b)"), [("mT", tt) for tt in range(9)])
            dbg_dump("XN0", XN[0], [("XN", 0)])
            dbg_dump("ST0", ST[0], [("ST", 0)])
            dbg_dump("gbc", gbc, ["gbc"])
            dbg_dump("h", hres.rearrange("p a b -> p (a b)"), [("h", tt) for tt in range(9)])
            finish()
            return nc

        gu_slot = {}

        def load_gu(fb):
            slot = wq[0] % 2
            wq[0] += 1
            gu_slot[fb] = slot
            load_w(WG[slot], wview(w_gate, fb * 256, 256), ("W", slot), nobar=True)
            load_w(WU[slot], wview(w_up, fb * 256, 256), ("W", slot), nobar=True)

        def load_d(fb):
            load_w(Wd[fb % 2], w_down[fb * 256:(fb + 1) * 256, :].rearrange("(fl p) c -> p fl c", p=128), ("Wd", fb % 2))

        gctr = [0]

        def GU(fb):
            slot = gu_slot[fb]
            Wg_, Wu_, kWs = WG[slot], WU[slot], ("W", slot)
            for fl in range(2):
                for nb, (lo, hi) in enumerate(tok_blocks):
                    n = hi - lo
                    pr = gctr[0] % 2
                    gctr[0] += 1
                    bg, bu = pr * 2, pr * 2 + 1
                    rk = [("mT", tt) for tt in range(nb * 4, nb * 4 + 4)] if nb < 2 else [("mT", 8)]
                    mm_group(pbf[bg][:, 0:n], [(Wg_[:, k, fl * 128:(fl + 1) * 128], mT[:, k, lo:hi]) for k in range(16)], r=[kWs] + rk, w=[("ps", bg)])
                    mm_group(pbf[bu][:, 0:n], [(Wu_[:, k, fl * 128:(fl + 1) * 128], mT[:, k, lo:hi]) for k in range(16)], r=[kWs] + rk, w=[("ps", bu)])
                    P.op("act", lambda e, pr=pr, bg=bg, n=n: e.activation(out=stmp[pr][:, 0:n], in_=pbf[bg][:, 0:n], func=AF.Silu), r=[("ps", bg)], w=[("stmp", pr)])
                    P.op("dve", lambda e, pr=pr, bu=bu, n=n, fb=fb, fl=fl, lo=lo, hi=hi: e.tensor_tensor(out=actb[fb % 2][:, fl, lo:hi], in0=pbf[bu][:, 0:n], in1=stmp[pr][:, 0:n], op=ALU.mult),
                         r=[("ps", bu), ("stmp", pr)], w=[("act", fb % 2)])

        dctr = [0]

        def DN(fb):
            for tt in range(9):
                rows = 128 if tt < 8 else NS
                tcols = slice(tt * 128, tt * 128 + rows)
                for cb in range(4):
                    bank = 4 + dctr[0] % 4
                    dctr[0] += 1
                    mm_group(pbf[bank][:rows, :], [(actb[fb % 2][:, fl, tcols], Wd[fb % 2][:, fl, cb * 512:(cb + 1) * 512]) for fl in range(2)],
                             r=[("act", fb % 2), ("Wd", fb % 2)], w=[("ps", bank)])
                    P.op("dve", lambda e, rows=rows, tt=tt, cb=cb, bank=bank: e.tensor_tensor(out=hres[:rows, tt, cb * 512:(cb + 1) * 512], in0=pbf[bank][:rows, :],
                                                                                             in1=hres[:rows, tt, cb * 512:(cb + 1) * 512], op=ALU.add),
                         r=[("ps", bank), ("h", tt)], w=[("h", tt)])

        load_gu(0)
        load_d(0)
        for i in range(NFB + 1):
            if i + 1 < NFB:
                load_gu(i + 1)
            if i < NFB:
                GU(i)
            if i >= 1:
                DN(i - 1)
            if i + 1 < NFB and i >= 0:
                load_d(i + 1) if i >= 1 or True else None
        P.barrier()
        if STOP_AFTER == "C":
            dbg_dump("h", hres.rearrange("p a b -> p (a b)"), [("h", tt) for tt in range(9)])
            finish()
            return nc

        for tt in range(9):
            rows = 128 if tt < 8 else NS
            i = tt % 2
            ht = hres[:, tt, :]
            st = ST[i]
            P.op("act", lambda e, rows=rows, ht=ht, st=st, i=i: e.activation(out=XN[i][:rows, :], in_=ht[:rows, :], func=AF.Square, accum_out=st[:rows, 0:1]),
                 r=[("h", tt)], w=[("XN", i), ("ST", i)])
            P.op("act", lambda e, rows=rows, st=st: e.activation(out=st[:rows, 1:2], in_=st[:rows, 0:1], func=AF.Ln, scale=1.0 / DM, bias=EPS), r=[("ST", i)], w=[("ST", i)])
            P.op("act", lambda e, rows=rows, st=st: e.activation(out=st[:rows, 2:3], in_=st[:rows, 1:2], func=AF.Exp, scale=-0.5), r=[("ST", i)], w=[("ST", i)])
            P.op("dve", lambda e, rows=rows, ht=ht, st=st: e.scalar_tensor_tensor(out=ht[:rows, :], in0=ht[:rows, :], scalar=st[:rows, 2:3], op0=ALU.mult, in1=gbc2[:rows, :], op1=ALU.mult),
                 r=[("h", tt), ("ST", i), "gbc2"], w=[("h", tt)])
            dst = y_m[tt * 128:(tt + 1) * 128, :] if tt < 8 else y_s[:, :]
            ok = ("o_y", tt)
            P.op("sp", lambda e, rows=rows, ht=ht, dst=dst: e.dma_start(out=dst, in_=ht[:rows, :]), r=[("h", tt)], w=[ok], dma=True)
            OUTKEYS.append(ok)
        finish()
        return nc


def _host_consts(inputs, core):
    hf = core % 2
    cst = np.zeros((128, CSTW), np.float32)
    r = np.arange(128)
    cst[:, C_ID:C_ID + 128] = np.eye(128, dtype=np.float32)
    cst[:, C_TRI:C_TRI + 128] = (r[:, None] <= r[None, :]).astype(np.float32)
    cst[:, C_U:C_U + 128] = (r[:, None] > r[None, :]).astype(np.float32)
    cst[:, C_ONE:C_ONE + 128] = 1.0
    cw = np.asarray(inputs["ssd_conv_w"])[0]
    cb = np.asarray(inputs["ssd_conv_b"])[0]
    cwb = np.concatenate([cw, cb[None]], 0)
    cst[:, C_CWB:C_CWB + 60] = cwb.reshape(5, 12, 128).transpose(2, 1, 0).reshape(128, 60)
    cst[:, C_GON:C_GON + 8] = np.asarray(inputs["chunk_out_norm_g"])[0].reshape(8, 128).T
    cst[:, C_ALOG:C_ALOG + 16] = np.asarray(inputs["ssd_a_log"])[0][None, :]
    cst[:, C_DTB:C_DTB + 16] = np.asarray(inputs["ssd_dt_bias"])[0][None, :]
    cst[:, C_D:C_D + 16] = np.asarray(inputs["ssd_d"])[0][None, :]
    cst[:, C_W00:C_W00 + 8] = np.asarray(inputs["chunk_w_s"])[0][:, 0, 0][None, :]
    cst[:, C_B0:C_B0 + 8] = np.asarray(inputs["chunk_b_s"])[0][:, 0][None, :]
    cst[:, C_FLAG] = float(hf)
    cst[:, C_DPP:C_DPP + 8] = np.repeat(np.asarray(inputs["ssd_d"])[0], 64).reshape(8, 128).T
    cst[:, C_SGPP:C_SGPP + 8] = np.asarray(inputs["ssd_norm_g"])[0].reshape(8, 128).T
    cst[0:16, C_DTBPP] = np.asarray(inputs["ssd_dt_bias"])[0]
    cst[0:16, C_ALPP] = np.asarray(inputs["ssd_a_log"])[0]
    cst2 = np.zeros((128, CST2W), np.float32)
    cst2[:, C2_BS:C2_BS + 1024] = np.asarray(inputs["chunk_b_s"])[0].reshape(1, 1024)
    rs = np.zeros((16, 8, 128), np.float32)
    for q in range(8):
        for rr in range(128):
            rs[2 * q + rr // 64, q, rr] = 1.0
    cst2[0:16, C2_RSEL:C2_RSEL + 1024] = rs.reshape(16, 1024)
    return cst, cst2


def make_in_maps(inputs):
    xp = np.asarray(inputs["x_prompt"], np.float32)
    xs = np.asarray(inputs["x_sample"], np.float32)
    sconv = np.asarray(inputs["state_conv"], np.float32)[0]
    sssm = np.asarray(inputs["state_ssm"], np.float32)[0]
    shared = {
        "w_in": np.ascontiguousarray(np.asarray(inputs["w_in"], np.float32)[0]),
        "w_out": np.ascontiguousarray(np.asarray(inputs["w_out"], np.float32)[0]),
        "w_gate": np.ascontiguousarray(np.asarray(inputs["w_gate"], np.float32)[0]),
        "w_up": np.ascontiguousarray(np.asarray(inputs["w_up"], np.float32)[0]),
        "w_down": np.ascontiguousarray(np.asarray(inputs["w_down"], np.float32)[0]),
        "ws": np.ascontiguousarray(np.asarray(inputs["chunk_w_s"], np.float32)[0]),
        "g_mix": np.ascontiguousarray(np.asarray(inputs["norm_mix_g"], np.float32)[0]),
        "g_ffn": np.ascontiguousarray(np.asarray(inputs["norm_ffn_g"], np.float32)[0]),
        "g_fin": np.ascontiguousarray(np.asarray(inputs["norm_final_g"], np.float32)),
        "vn_g": np.ascontiguousarray(np.asarray(inputs["chunk_v_norm_g"], np.float32)[0]),
        "ssd_g": np.ascontiguousarray(np.asarray(inputs["ssd_norm_g"], np.float32)[0]),
    }
    zeros_prev = np.zeros((TM, DM), np.float32)
    maps = []
    for c in range(8):
        b, hf = c // 2, c % 2
        cst, cst2 = _host_consts(inputs, c)
        m = dict(shared)
        m["xm"] = np.ascontiguousarray(xp[b, hf * TM:(hf + 1) * TM])
        m["xprev"] = np.ascontiguousarray(xp[b, 0:TM]) if hf == 1 else zeros_prev
        m["xsmp"] = np.ascontiguousarray(xs[c * NS:(c + 1) * NS, 0])
        m["sconv"] = np.ascontiguousarray(sconv[c * NS:(c + 1) * NS])
        m["sssm"] = np.ascontiguousarray(sssm[c * NS:(c + 1) * NS].reshape(NS, 1024, 128))
        m["cst"] = cst
        m["cst2"] = cst2
        maps.append(m)
    return maps


def kernel(**inputs):
    nc = build_program()
    maps = make_in_maps(inputs)
    res = run_bass_kernel_spmd(nc, maps, core_ids=list(range(8)))
    R = res.results
    y_prompt = np.zeros((4, 2048, DM), np.float32)
    y_sample = np.zeros((128, 1, DM), np.float32)
    ncp = np.zeros((1, 4, 3, 1536), np.float32)
    nsp = np.zeros((1, 4, 16, 64, 128), np.float32)
    ncs = np.zeros((1, 128, 3, 1536), np.float32)
    nss = np.zeros((1, 128, 16, 64, 128), np.float32)
    vs = np.zeros((1, 128, 1, 1024), np.float32)
    for c in range(8):
        b, hf = c // 2, c % 2
        y_prompt[b, hf * TM:(hf + 1) * TM] = R[c]["y_m"]
        y_sample[c * NS:(c + 1) * NS, 0] = R[c]["y_s"]
        if hf == 1:
            ncp[0, b] = R[c]["nconv_p"]
            nsp[0, b] = R[c]["nssm_p"].reshape(16, 64, 128)
        ncs[0, c * NS:(c + 1) * NS] = R[c]["nconv_s"]
        nss[0, c * NS:(c + 1) * NS] = R[c]["nssm_s"].reshape(NS, 16, 64, 128)
        vs[0, c * NS:(c + 1) * NS, 0] = R[c]["v_s"]
    return (y_prompt, y_sample, ncp, nsp, ncs, nss, vs)
```

```python
import numpy as np
from contextlib import ExitStack
import concourse.bass as bass
import concourse.mybir as mybir
from concourse.bass_utils import run_bass_kernel_spmd

F32 = mybir.dt.float32
F32R = mybir.dt.float32r
BF16 = mybir.dt.bfloat16
AF = mybir.ActivationFunctionType
ALU = mybir.AluOpType
AX = mybir.AxisListType

ENGS = ["pe", "act", "dve", "pool", "sp"]
NDMASEM = 12

DEBUG = {}
STOP_AFTER = None
SSD_LEVEL = 9


class Op:
    __slots__ = ("eng", "fn", "r", "w", "dma", "deps", "sig", "sem", "prev_use", "waits", "need_sig", "xdeps")

    def __init__(self, eng, fn, r, w, dma):
        self.eng, self.fn, self.r, self.w, self.dma = eng, fn, tuple(r), tuple(w), dma
        self.deps = set()
        self.xdeps = set()
        self.sig = None
        self.sem = None
        self.prev_use = 0
        self.waits = []
        self.need_sig = False


class Prog:
    def __init__(self):
        self.ops = []
        self.pending_bar = {}

    def op(self, eng, fn, r=(), w=(), dma=False, nobar=False):
        w = list(w) + [k for k in r if isinstance(k, tuple) and k and k[0] == "ps" and k not in w]
        o = Op(eng, fn, r, w, dma)
        if eng in self.pending_bar and not nobar:
            o.xdeps |= self.pending_bar.pop(eng)
        self.ops.append(o)
        return o

    def barrier(self):
        last = {}
        dmas = []
        for i, o in enumerate(self.ops):
            if o.dma:
                dmas.append(i)
            elif o.fn is not None:
                last[o.eng] = i
        s = set(last.values()) | set(dmas)
        for e in ENGS:
            self.pending_bar[e] = set(s) | self.pending_bar.get(e, set())

    def resolve(self):
        ops = self.ops
        last_w = {}
        readers = {}
        for i, o in enumerate(ops):
            deps = set(o.xdeps)
            for k in o.r:
                if k in last_w:
                    deps.add(last_w[k])
            for k in o.w:
                if k in last_w:
                    deps.add(last_w[k])
                deps.update(readers.get(k, ()))
            deps.discard(i)
            keep = set()
            for d in deps:
                od = ops[d]
                if od.dma:
                    keep.add(d)
                elif od.eng == o.eng and not o.dma:
                    if o.eng == "pe":
                        continue
                    if d in o.xdeps:
                        continue
                    if set(od.w) & set(o.r):
                        keep.add(d)
                else:
                    keep.add(d)
            o.deps = keep
            for d in keep:
                ops[d].need_sig = True
            for k in o.r:
                readers.setdefault(k, []).append(i)
            for k in o.w:
                last_w[k] = i
                readers[k] = []
        cnt = {e: 0 for e in ENGS}
        dma_n = {e: 0 for e in ENGS}
        dma_use = {}
        for i, o in enumerate(ops):
            if o.dma:
                slot = (o.eng, dma_n[o.eng] % NDMASEM)
                dma_n[o.eng] += 1
                u = dma_use.get(slot, 0)
                o.prev_use = u
                dma_use[slot] = u + 1
                o.sem = slot
                o.sig = 16 * (u + 1)
            elif o.need_sig:
                cnt[o.eng] += 1
                o.sem = o.eng
                o.sig = cnt[o.eng]
        seen = {e: {} for e in ENGS}
        for i, o in enumerate(ops):
            sd = seen[o.eng]
            need = {}
            if o.dma and o.prev_use > 0:
                need[o.sem] = 16 * o.prev_use
            for d in o.deps:
                od = ops[d]
                need[od.sem] = max(need.get(od.sem, 0), od.sig)
            o.waits = []
            for s, v in need.items():
                if sd.get(s, 0) >= v:
                    continue
                sd[s] = v
                o.waits.append((s, v))

    def emit(self, sems, block):
        ops = self.ops

        def run(eng_name):
            def body(e):
                for o in ops:
                    if o.eng != eng_name:
                        continue
                    for s, v in o.waits:
                        e.wait_ge(sems[s], v)
                    if o.fn is None:
                        continue
                    ins = o.fn(e)
                    if o.dma:
                        ins.then_inc(sems[o.sem], 16)
                    elif o.sig is not None:
                        ins.then_inc(sems[o.sem], 1)
            return body

        block.sync(run("sp"))
        block.tensor(run("pe"))
        block.scalar(run("act"))
        block.vector(run("dve"))
        block.gpsimd(run("pool"))


DM = 2048
DIN = 4624
DFF = 5632
TM = 1024
NS = 16
T = TM + NS
EPS = 1e-6
COL_U, COL_V, COL_Z, COL_X, COL_B, COL_C, COL_DT = 0, 1024, 2048, 3072, 4096, 4352, 4608

C_ID, C_TRI, C_U, C_ONE = 0, 128, 256, 384
C_CWB, C_GON, C_ALOG, C_DTB, C_D, C_W00, C_B0, C_FLAG = 512, 572, 580, 596, 612, 628, 636, 644
C_DPP, C_SGPP, C_DTBPP, C_ALPP = 648, 656, 664, 665
CSTW = 672
C2_BS, C2_RSEL = 0, 1024
CST2W = 2048

AW = 51000


class Bump:
    def __init__(self, arena, start, end):
        self.arena, self.start, self.end, self.off = arena, start, end, start
        self.peak = start

    def reset(self, to=None):
        self.off = self.start if to is None else to

    def mark(self):
        return self.off

    def _take(self, words):
        o = self.off
        self.off += words
        assert self.off <= self.end, ("arena overflow", self.off, self.end)
        self.peak = max(self.peak, self.off)
        return o

    def f32(self, shape):
        n = int(np.prod(shape))
        o = self._take(n)
        v = self.arena[:, o:o + n]
        return _shape(v, shape)

    def bf16(self, shape):
        n = int(np.prod(shape))
        w = (n + 1) // 2
        o = self._take(w)
        v = self.arena[:, o:o + w].bitcast(BF16)[:, 0:n]
        return _shape(v, shape)


def _shape(v, shape):
    if len(shape) == 1:
        return v
    if len(shape) == 2:
        return v.rearrange("p (a b) -> p a b", a=shape[0])
    if len(shape) == 3:
        return v.rearrange("p (a b c) -> p a b c", a=shape[0], b=shape[1])
    raise ValueError(shape)


def build_program():
    nc = bass.Bass("TRN2", target_bir_lowering=False)

    def din(name, shape):
        return nc.dram_tensor(name, shape, F32, kind="ExternalInput").ap()

    def dout(name, shape):
        return nc.dram_tensor(name, shape, F32, kind="ExternalOutput").ap()

    xm = din("xm", [TM, DM])
    xprev = din("xprev", [TM, DM])
    xsmp = din("xsmp", [NS, DM])
    sconv = din("sconv", [NS, 3, 1536])
    sssm = din("sssm", [NS, 1024, 128])
    w_in = din("w_in", [DM, DIN])
    w_out = din("w_out", [DM, DM])
    w_gate = din("w_gate", [DM, DFF])
    w_up = din("w_up", [DM, DFF])
    w_down = din("w_down", [DFF, DM])
    cst_d = din("cst", [128, CSTW])
    cst2_d = din("cst2", [128, CST2W])
    ws_d = din("ws", [8, 128, 128])
    g_mix = din("g_mix", [DM])
    g_ffn = din("g_ffn", [DM])
    g_fin = din("g_fin", [DM])
    vn_g = din("vn_g", [1024])
    ssd_g = din("ssd_g", [1024])

    y_m = dout("y_m", [TM, DM])
    y_s = dout("y_s", [NS, DM])
    nconv_p = dout("nconv_p", [3, 1536])
    nssm_p = dout("nssm_p", [1024, 128])
    nconv_s = dout("nconv_s", [NS, 3, 1536])
    nssm_s = dout("nssm_s", [NS, 1024, 128])
    v_s = dout("v_s", [NS, 1024])
    dbg_d = {}
    for name, (shape, dty) in DEBUG.items():
        dbg_d[name] = nc.dram_tensor("dbg_" + name, list(shape), dty, kind="ExternalOutput").ap()

    P = Prog()
    with ExitStack() as es:
        arena = es.enter_context(nc.sbuf_tensor("arena", [128, AW], F32))
        rhsE = es.enter_context(nc.sbuf_tensor("rhsE", [128, 2048], F32))
        U32 = es.enter_context(nc.sbuf_tensor("U32", [128, 128], F32))
        pb = [es.enter_context(nc.psum_tensor(f"pb{i}", [128, 512], F32)) for i in range(8)]
        sems = {}
        for e in ENGS:
            sems[e] = es.enter_context(nc.semaphore("s_" + e))
        for e in ("sp", "pool"):
            for i in range(NDMASEM):
                sems[(e, i)] = es.enter_context(nc.semaphore(f"d_{e}_{i}"))
        block = es.enter_context(nc.Block())

        arena = arena[:, :]
        rhsEr = rhsE[:, :].bitcast(F32R)
        U32r = U32[:, :].bitcast(F32R)
        pbf = [p[:, :] for p in pb]
        pbb = [p[:, :].bitcast(BF16) for p in pb]

        fx = Bump(arena, 0, AW)
        CST = fx.f32([CSTW])
        identF = CST[:, C_ID:C_ID + 128]
        triF = CST[:, C_TRI:C_TRI + 128]
        UF = CST[:, C_U:C_U + 128]
        onesF = CST[:, C_ONE:C_ONE + 128]
        cwb = CST[:, C_CWB:C_CWB + 60].rearrange("p (j k) -> p j k", j=12)
        gon_pp = CST[:, C_GON:C_GON + 8]
        alog_bc = CST[:, C_ALOG:C_ALOG + 16]
        dtb_bc = CST[:, C_DTB:C_DTB + 16]
        D_bc = CST[:, C_D:C_D + 16]
        w00_bc = CST[:, C_W00:C_W00 + 8]
        b0_bc = CST[:, C_B0:C_B0 + 8]
        flag = CST[:, C_FLAG:C_FLAG + 1]
        D_pp = CST[:, C_DPP:C_DPP + 8]
        sg_pp = CST[:, C_SGPP:C_SGPP + 8]
        dtb_pp = CST[:, C_DTBPP:C_DTBPP + 1]
        al_pp = CST[:, C_ALPP:C_ALPP + 1]
        identB = fx.bf16([128])
        onesB = fx.bf16([128])
        aneg = fx.f32([16])
        hT = fx.f32([1024])
        hTb = fx.bf16([1024])
        prefix = fx.f32([12, 3])
        ncv = fx.f32([12, 3])
        nT = fx.bf16([16, T])
        YT = fx.bf16([16, T])
        r2_start = fx.off - (16 * T) // 2
        r2_end = fx.off
        Wfix = [fx.bf16([16, 512]) for _ in range(2)]
        _wflat = [w_.rearrange("p a b -> p (a b)") for w_ in Wfix]
        WG = [w_[:, 0:4096].rearrange("p (a b) -> p a b", a=16) for w_ in _wflat]
        WU = [w_[:, 4096:8192].rearrange("p (a b) -> p a b", a=16) for w_ in _wflat]
        wq = [0]
        S0 = fx.off
        sc = Bump(arena, S0, AW)
        r2 = Bump(arena, r2_start, r2_end)

        KC = ("cst",)

        P.op("sp", lambda e: e.dma_start(out=CST, in_=cst_d[:, :]), w=[KC], dma=True)
        P.op("dve", lambda e: e.tensor_copy(out=identB, in_=identF), r=[KC], w=["identB"])
        P.op("dve", lambda e: e.memset(onesB, 1.0), w=["onesB"])
        P.op("dve", lambda e: e.tensor_copy(out=U32r, in_=UF), r=[KC], w=["U32"])
        P.op("act", lambda e: e.activation(out=aneg, in_=alog_bc, func=AF.Exp), r=[KC], w=["aneg0"])
        P.op("dve", lambda e: e.tensor_scalar(out=aneg, in0=aneg, scalar1=-1.0, scalar2=None, op0=ALU.mult), r=["aneg0"], w=["aneg"])
        P.op("dve", lambda e: e.memset(hT, 0.0), w=["hT"])
        P.op("dve", lambda e: e.memset(hTb, 0.0), w=["hTb"])

        def rms_transpose(xt, kx, rows, xn, kxn, st, kst, gbc, kg, dstT, col0, kdst, width, tag):
            nk = width // 128
            P.op("act", lambda e: e.activation(out=xn[:rows, :], in_=xt[:rows, :], func=AF.Square, accum_out=st[:rows, 0:1]),
                 r=[kx], w=[kxn, kst])
            P.op("act", lambda e: e.activation(out=st[:rows, 1:2], in_=st[:rows, 0:1], func=AF.Ln, scale=1.0 / width, bias=EPS),
                 r=[kst], w=[kst])
            P.op("act", lambda e: e.activation(out=st[:rows, 2:3], in_=st[:rows, 1:2], func=AF.Exp, scale=-0.5),
                 r=[kst], w=[kst])
            P.op("dve", lambda e: e.scalar_tensor_tensor(out=xn[:rows, :], in0=xt[:rows, :], scalar=st[:rows, 2:3], op0=ALU.mult,
                                                         in1=gbc[:rows, :], op1=ALU.mult),
                 r=[kx, kst, kg, kxn], w=[kxn])
            for b8 in range(nk // 8):
                bank = tag % 2 if nk == 8 else b8
                psv = pbb[bank][:, 0:8 * rows].rearrange("p (a b) -> p a b", a=8)

                def tr(e, b8=b8, psv=psv):
                    ins = None
                    for j in range(8):
                        k = b8 * 8 + j
                        ins = e.transpose(out=psv[:, j, :], in_=xn[:rows, k * 128:(k + 1) * 128], identity=identB[:rows, :rows])
                    return ins
                P.op("pe", tr, r=[kxn, "identB"], w=[("ps", bank)])
                dst = dstT[:, b8 * 8:(b8 + 1) * 8, col0:col0 + rows]
                if b8 == 0:
                    P.op("act", lambda e, dst=dst, psv=psv: e.activation(out=dst, in_=psv, func=AF.Copy), r=[("ps", bank)], w=[kdst])
                else:
                    P.op("dve", lambda e, dst=dst, psv=psv: e.tensor_copy(out=dst, in_=psv), r=[("ps", bank)], w=[kdst])

        def mm_group(out, pairs, r, w):
            def fn(e):
                ins = None
                n = len(pairs)
                for i, (l, rh) in enumerate(pairs):
                    ins = e.matmul(out, lhsT=l, rhs=rh, start=(i == 0), stop=(i == n - 1))
                return ins
            P.op("pe", fn, r=r, w=w)

        def load_w(dst, src_ap, key, nobar=False):
            P.op("pool", lambda e: e.dma_start(out=dst, in_=src_ap), w=[key], dma=True, nobar=nobar)

        def wload(src_ap, ncols=512):
            slot = wq[0] % 2
            wq[0] += 1
            buf = Wfix[slot] if ncols == 512 else Wfix[slot][:, :, 0:ncols]
            load_w(buf, src_ap, ("W", slot), nobar=True)
            return buf, ("W", slot)

        def wview(wap, c0, ncols):
            return wap[:, c0:c0 + ncols].rearrange("(kt p) c -> p kt c", p=128)

        def conv_silu(rawpad, krp, acc, kacc, j, ntok, dst, kdst):
            P.op("dve", lambda e: e.tensor_scalar(out=acc[:, 0:ntok], in0=rawpad[:, 0:ntok], scalar1=cwb[:, j, 0:1], scalar2=cwb[:, j, 4:5],
                                                  op0=ALU.mult, op1=ALU.add), r=[krp, KC], w=[kacc])
            for k in (1, 2, 3):
                P.op("dve", lambda e, k=k: e.scalar_tensor_tensor(out=acc[:, 0:ntok], in0=rawpad[:, k:k + ntok], scalar=cwb[:, j, k:k + 1],
                                                                  op0=ALU.mult, in1=acc[:, 0:ntok], op1=ALU.add), r=[krp, kacc, KC], w=[kacc])
            P.op("act", lambda e: e.activation(out=dst, in_=acc[:, 0:ntok], func=AF.Silu), r=[kacc], w=[kdst])

        def ssd_temps(b):
            t = {}
            t["sm"] = b.f32([96])
            t["ex"] = b.f32([48])
            t["LT"] = b.f32([2048])
            t["MT"] = b.bf16([16, 128])
            t["xs_tok"] = b.bf16([16, 64])
            t["xdt"] = b.bf16([16, 64])
            t["xdtw"] = b.bf16([16, 64])
            t["B_tok"] = b.bf16([256])
            t["cbm"] = b.f32([2, 128])
            t["y1"] = b.f32([16, 64])
            t["t2"] = b.f32([16, 64])
            t["yb"] = b.bf16([1024])
            t["gn"] = b.f32([8])
            return t

        def ssd_chunk(t, xT, c, dtraw_c, kdt, main, ztok_c=None, ssdg_bc=None):
            sm, ex = t["sm"], t["ex"]
            cs = slice(c * 128, (c + 1) * 128)
            kx = ("xT",)
            P.op("dve", lambda e: e.tensor_tensor(out=sm[:, 0:16], in0=dtraw_c, in1=dtb_bc, op=ALU.add), r=[kdt, KC], w=["sm0"])
            P.op("act", lambda e: e.activation(out=sm[:, 16:32], in_=sm[:, 0:16], func=AF.Exp), r=["sm0"], w=["sm1"])
            P.op("act", lambda e: e.activation(out=sm[:, 32:48], in_=sm[:, 16:32], func=AF.Ln, bias=1.0), r=["sm1"], w=["dtc"])
            dtc = sm[:, 32:48]
            dta = sm[:, 48:64]
            P.op("dve", lambda e: e.tensor_tensor(out=dta, in0=dtc, in1=aneg, op=ALU.mult), r=["dtc", "aneg"], w=["dta"])

            def small(e):
                e.matmul(pbf[0][:, 0:16], lhsT=UF, rhs=dta, start=True, stop=True)
                ins = e.matmul(pbf[0][:, 16:32], lhsT=onesF, rhs=dta, start=True, stop=True)
                if main:
                    ins = e.matmul(pbf[0][:, 32:48], lhsT=triF, rhs=dta, start=True, stop=True)
                return ins
            P.op("pe", small, r=["dta", KC], w=[("ps", 0)])
            nex = 48 if main else 32
            P.op("act", lambda e: e.activation(out=ex[:, 0:nex], in_=pbf[0][:, 0:nex], func=AF.Exp), r=[("ps", 0)], w=["ex"])
            toend, dec, ea = ex[:, 0:16], ex[:, 16:32], ex[:, 32:48]

            psx = pbb[1][:, 0:1024].rearrange("p (a b) -> p a b", a=8)

            def trx(e):
                ins = None
                for q in range(8):
                    ins = e.transpose(out=psx[:, q, :], in_=xT[:, q, cs], identity=identB)
                return ins
            P.op("pe", trx, r=[kx, "identB"], w=[("ps", 1)])
            psB = pbb[2][:, 0:256].rearrange("p (a b) -> p a b", a=2)

            def trb(e):
                ins = None
                for g in range(2):
                    ins = e.transpose(out=psB[:, g, :], in_=xT[:, 8 + g, cs], identity=identB)
                return ins
            P.op("pe", trb, r=[kx, "identB"], w=[("ps", 2)])
            psx3 = pbb[1][:, 0:1024].rearrange("p (a b) -> p a b", a=16)
            w2 = sm[:, 64:80]
            P.op("dve", lambda e: e.tensor_tensor(out=w2, in0=dtc, in1=toend, op=ALU.mult), r=["dtc", "ex"], w=["w2"])
            P.op("dve", lambda e: e.tensor_tensor(out=t["xdtw"], in0=psx3, in1=w2.unsqueeze(2).broadcast_to([128, 16, 64]), op=ALU.mult),
                 r=[("ps", 1), "w2"], w=["xdtw"])
            P.op("act", lambda e: e.activation(out=t["B_tok"], in_=pbb[2][:, 0:256], func=AF.Copy), r=[("ps", 2)], w=["B_tok"])
            if main and SSD_LEVEL == 0:
                return
            if main:
                P.op("dve", lambda e: e.tensor_tensor(out=t["xdt"], in0=psx3, in1=dtc.unsqueeze(2).broadcast_to([128, 16, 64]), op=ALU.mult),
                     r=[("ps", 1), "dtc"], w=["xdt"])
                P.op("act", lambda e: e.activation(out=t["xs_tok"], in_=psx3, func=AF.Copy), r=[("ps", 1)], w=["xs_tok"])
                for e16 in range(16 if SSD_LEVEL >= 2 else 0):
                    P.op("dve", lambda e, e16=e16: e.tensor_scalar(out=rhsEr[:, e16 * 128:(e16 + 1) * 128], in0=triF, scalar1=dta[:, e16:e16 + 1],
                                                                   scalar2=None, op0=ALU.mult), r=["dta", KC], w=[("rhsE", e16 // 4)])
                for i in range(4 if SSD_LEVEL >= 2 else 0):
                    P.op("pe", lambda e, i=i: e.matmul(pbf[3 + i], lhsT=U32r, rhs=rhsEr[:, i * 512:(i + 1) * 512], start=True, stop=True),
                         r=[("rhsE", i), "U32"], w=[("ps", 3 + i)])
                    P.op("act", lambda e, i=i: e.activation(out=t["LT"][:, i * 512:(i + 1) * 512], in_=pbf[3 + i], func=AF.Exp),
                         r=[("ps", 3 + i)], w=[("LT", i)])
                psc = pbf[7][:, 0:256].rearrange("p (a b) -> p a b", a=2)
                if SSD_LEVEL < 3:
                    return

                def cbf(e):
                    ins = None
                    for g in range(2):
                        ins = e.matmul(psc[:, g, :], lhsT=xT[:, 8 + g, cs], rhs=xT[:, 10 + g, cs], start=True, stop=True)
                    return ins
                P.op("pe", cbf, r=[kx], w=[("ps", 7)])
                P.op("dve", lambda e: e.tensor_tensor(out=t["cbm"], in0=psc, in1=triF.unsqueeze(1).broadcast_to([128, 2, 128]), op=ALU.mult),
                     r=[("ps", 7), KC], w=["cbm"])
                LT3 = t["LT"].rearrange("p (a b) -> p a b", a=16)
                for g in range(2):
                    P.op("dve", lambda e, g=g: e.tensor_tensor(out=t["MT"][:, g * 8:(g + 1) * 8, :], in0=LT3[:, g * 8:(g + 1) * 8, :],
                                                               in1=t["cbm"][:, g:g + 1, :].broadcast_to([128, 8, 128]), op=ALU.mult),
                         r=[("LT", 2 * g), ("LT", 2 * g + 1), "cbm"], w=[("MT", g)])
                if SSD_LEVEL < 4:
                    return
                for g in range(2):
                    def yd(e, g=g):
                        ins = None
                        for j in range(8):
                            e16 = g * 8 + j
                            ins = e.matmul(pbf[3 + g][:, j * 64:(j + 1) * 64], lhsT=t["MT"][:, e16, :], rhs=t["xdt"][:, e16, :], start=True, stop=True)
                        return ins
                    P.op("pe", yd, r=[("MT", g), "xdt"], w=[("ps", 3 + g)])
                    P.op("pe", lambda e, g=g: e.matmul(pbf[5 + g], lhsT=xT[:, 10 + g, cs], rhs=hTb[:, g * 512:(g + 1) * 512], start=True, stop=True),
                         r=[kx, "hTb"], w=[("ps", 5 + g)])
                y1 = t["y1"]
                for g in range(2):
                    y1g = y1[:, g * 8:(g + 1) * 8, :]
                    P.op("dve", lambda e, g=g, y1g=y1g: e.tensor_tensor(out=y1g, in0=pbf[5 + g].rearrange("p (a b) -> p a b", a=8),
                                                                        in1=ea[:, g * 8:(g + 1) * 8].unsqueeze(2).broadcast_to([128, 8, 64]), op=ALU.mult),
                         r=[("ps", 5 + g), "ex"], w=[("y1", g)])
                    P.op("dve", lambda e, g=g, y1g=y1g: e.tensor_tensor(out=y1g, in0=pbf[3 + g].rearrange("p (a b) -> p a b", a=8), in1=y1g, op=ALU.add),
                         r=[("ps", 3 + g), ("y1", g)], w=[("y1", g)])
                P.op("pool", lambda e: e.tensor_tensor(out=t["t2"], in0=t["xs_tok"], in1=D_bc.unsqueeze(2).broadcast_to([128, 16, 64]), op=ALU.mult),
                     r=["xs_tok", KC], w=["t2"])
                P.op("dve", lambda e: e.tensor_tensor(out=y1, in0=y1, in1=t["t2"], op=ALU.add), r=[("y1", 0), ("y1", 1), "t2"], w=[("y1", 0), ("y1", 1)])
                y1f = y1.rearrange("p a b -> p (a b)")
                P.op("dve", lambda e: e.tensor_tensor(out=y1f, in0=y1f, in1=ztok_c, op=ALU.mult), r=[("y1", 0), ("y1", 1), "z_tok"], w=[("y1", 0), ("y1", 1)])
                if SSD_LEVEL < 5:
                    return
                gn = t["gn"]
                t2f = t["t2"].rearrange("p a b -> p (a b)")
                for g in range(2):
                    P.op("act", lambda e, g=g: e.activation(out=t2f[:, g * 512:(g + 1) * 512], in_=y1f[:, g * 512:(g + 1) * 512], func=AF.Square,
                                                            accum_out=gn[:, g:g + 1]), r=[("y1", g)], w=["t2", ("gn", g)])
                P.op("act", lambda e: e.activation(out=gn[:, 2:4], in_=gn[:, 0:2], func=AF.Ln, scale=1.0 / 512, bias=EPS), r=[("gn", 0), ("gn", 1)], w=["gn2"])
                P.op("act", lambda e: e.activation(out=gn[:, 4:6], in_=gn[:, 2:4], func=AF.Exp, scale=-0.5), r=["gn2"], w=["gn4"])
                for g in range(2):
                    P.op("dve", lambda e, g=g: e.scalar_tensor_tensor(out=t["yb"][:, g * 512:(g + 1) * 512], in0=y1f[:, g * 512:(g + 1) * 512],
                                                                      scalar=gn[:, 4 + g:5 + g], op0=ALU.mult, in1=ssdg_bc[:, g * 512:(g + 1) * 512], op1=ALU.mult),
                         r=[("y1", g), "gn4", "ssdg"], w=[("yb", g)])
                if SSD_LEVEL < 6:
                    return
                psy = pbb[2][:, 0:1024].rearrange("p (a b) -> p a b", a=8)

                def try_(e):
                    ins = None
                    for q in range(8):
                        ins = e.transpose(out=psy[:, q, :], in_=t["yb"][:, q * 128:(q + 1) * 128], identity=identB)
                    return ins
                P.op("pe", try_, r=[("yb", 0), ("yb", 1), "identB"], w=[("ps", 2)])
                P.op("act", lambda e: e.activation(out=YT[:, 8:16, cs], in_=psy, func=AF.Copy), r=[("ps", 2)], w=[("YTb", c)])
            sb = (7, 1)
            for g in range(2):
                P.op("pe", lambda e, g=g: e.matmul(pbf[sb[g]], lhsT=t["B_tok"][:, g * 128:(g + 1) * 128], rhs=t["xdtw"].rearrange("p a b -> p (a b)")[:, g * 512:(g + 1) * 512],
                                                   start=True, stop=True), r=["B_tok", "xdtw"], w=[("ps", sb[g])])
            hT3 = hT.rearrange("p (a b) -> p a b", a=16)
            P.op("dve", lambda e: e.tensor_tensor(out=hT3, in0=hT3, in1=dec.unsqueeze(2).broadcast_to([128, 16, 64]), op=ALU.mult), r=["hT", "ex"], w=["hT"])
            for g in range(2):
                P.op("dve", lambda e, g=g: e.tensor_tensor(out=hT[:, g * 512:(g + 1) * 512], in0=pbf[sb[g]], in1=hT[:, g * 512:(g + 1) * 512], op=ALU.add),
                     r=[("ps", sb[g]), "hT"], w=["hT"])
            P.op("act", lambda e: e.activation(out=hTb, in_=hT, func=AF.Copy), r=["hT"], w=["hTb"])

        dbg_ops = []

        def dbg_dump(name, ap, keys):
            if name in dbg_d:
                dbg_ops.append((name, ap, keys))

        def finish():
            P.barrier()
            outs = []
            for name, ap, keys in dbg_ops:
                k = ("dbgout", name)
                P.op("sp", lambda e, name=name, ap=ap: e.dma_start(out=dbg_d[name], in_=ap), r=keys, w=[k], dma=True)
                outs.append(k)
            P.op("sp", None, r=outs + OUTKEYS)
            P.resolve()
            P.emit(sems, block)

        OUTKEYS = []

        sc.reset()
        X = [sc.f32([DM]) for _ in range(2)]
        XN = [sc.bf16([DM]) for _ in range(2)]
        ST = [sc.f32([4]) for _ in range(2)]
        gbc = sc.f32([DM])
        m_common = sc.mark()
        nPT = YT[:, :, 0:TM]
        P.op("sp", lambda e, gbc=gbc: e.dma_start(out=gbc, in_=g_mix.partition_broadcast(128)), w=["gbc"], dma=True)
        for tt in range(8):
            i = tt % 2
            P.op("sp", lambda e, tt=tt, i=i: e.dma_start(out=X[i], in_=xprev[tt * 128:(tt + 1) * 128, :]), w=[("X", i)], dma=True)
            rms_transpose(X[i], ("X", i), 128, XN[i], ("XN", i), ST[i], ("ST", i), gbc, "gbc", nPT, tt * 128, ("nPT", tt), DM, tt)
        Wdt = sc.bf16([16, 16])
        xTp = sc.bf16([10, TM])
        dtraw_p = sc.f32([8, 16])
        m_after_dt = sc.mark()
        rawpad = [sc.f32([TM + 4]) for _ in range(2)]
        accb = [sc.f32([TM]) for _ in range(2)]
        blocks = [(COL_X, 512), (COL_X + 512, 512), (COL_B, 512)]
        load_w(Wdt, wview(w_in, COL_DT, 16), "Wdt")
        wcur = wload(wview(w_in, blocks[0][0], blocks[0][1]))
        for i in range(2):
            P.op("dve", lambda e, rp=rawpad[i]: e.memset(rp[:, 0:3], 0.0), w=[("rawpad", i)])
        jt = 0
        for bi, (c0, ncol) in enumerate(blocks):
            Wv, kW = wcur
            if bi + 1 < len(blocks):
                wcur = wload(wview(w_in, blocks[bi + 1][0], blocks[bi + 1][1]))
            for jj in range(4):
                j = (c0 - COL_X) // 128 + jj
                rp = rawpad[jt % 2]
                krp = ("rawpad", jt % 2)
                isC = j >= 10
                for nb in range(2):
                    if isC and nb == 0:
                        continue
                    bank = 3 + (jt * 2 + nb) % 4
                    mm_group(pbf[bank], [(Wv[:, k, jj * 128:(jj + 1) * 128], nPT[:, k, nb * 512:(nb + 1) * 512]) for k in range(16)],
                             r=[kW] + [("nPT", tt) for tt in range(nb * 4, nb * 4 + 4)], w=[("ps", bank)])
                    P.op("act", lambda e, rp=rp, nb=nb, bank=bank: e.activation(out=rp[:, 3 + nb * 512:3 + (nb + 1) * 512], in_=pbf[bank], func=AF.Copy),
                         r=[("ps", bank)], w=[krp])
                P.op("dve", lambda e, rp=rp, j=j: e.tensor_copy(out=prefix[:, j, :], in_=rp[:, TM:TM + 3]), r=[krp], w=["prefix"])
                if not isC:
                    conv_silu(rp, krp, accb[jt % 2], ("acc", jt % 2), j, TM, xTp[:, j, :], ("xT",))
                jt += 1
        for tt in range(8):
            mm_group(pbf[0][:, 0:16], [(nPT[:, k, tt * 128:(tt + 1) * 128], Wdt[:, k, :]) for k in range(16)], r=["Wdt", ("nPT", tt)], w=[("ps", 0)])
            P.op("act", lambda e, tt=tt: e.activation(out=dtraw_p[:, tt, :], in_=pbf[0][:, 0:16], func=AF.Copy), r=[("ps", 0)], w=["dtraw"])
        P.barrier()
        sc.reset(m_after_dt)
        tP = ssd_temps(sc)
        for c in range(8):
            ssd_chunk(tP, xTp, c, dtraw_p[:, c, :], "dtraw", main=False)
        P.op("dve", lambda e: e.tensor_scalar(out=hT, in0=hT, scalar1=flag, scalar2=None, op0=ALU.mult), r=["hT", KC], w=["hT"])
        P.op("act", lambda e: e.activation(out=hTb, in_=hT, func=AF.Copy), r=["hT"], w=["hTb"])
        P.barrier()
        if STOP_AFTER == "P":
            dbg_dump("nPT", YT.rearrange("p a b -> p (a b)"), [("nPT", tt) for tt in range(8)])
            dbg_dump("xTp", xTp.rearrange("p a b -> p (a b)"), [("xT",)])
            dbg_dump("dtraw", dtraw_p.rearrange("p a b -> p (a b)"), ["dtraw"])
            dbg_dump("gbc", gbc, ["gbc"])
            dbg_dump("hT", hT, ["hT"])
            dbg_dump("prefix", prefix.rearrange("p a b -> p (a b)"), ["prefix"])
            finish()
            return nc

        sc.reset(m_common)
        for tt in range(9):
            i = tt % 2
            rows = 128 if tt < 8 else NS
            src = xm[tt * 128:(tt + 1) * 128, :] if tt < 8 else xsmp[:, :]
            P.op("sp", lambda e, i=i, rows=rows, src=src: e.dma_start(out=X[i][:rows, :], in_=src), w=[("X", i)], dma=True)
            rms_transpose(X[i], ("X", i), rows, XN[i], ("XN", i), ST[i], ("ST", i), gbc, "gbc", nT, tt * 128, ("nT", tt), DM, tt)
        P.barrier()
        if STOP_AFTER == "A0":
            dbg_dump("nT", nT.rearrange("p a b -> p (a b)"), [("nT", tt) for tt in range(9)])
            finish()
            return nc
        sc.reset()
        xT = sc.bf16([12, T])
        z_tok = sc.bf16([8, 1024])
        zT_s = sc.f32([8, NS])
        dtraw = sc.f32([8, 16])
        dtT_s = sc.f32([NS])
        ssdg_bc = sc.f32([1024])
        Rsel = sc.f32([1024])
        raw_s = sc.f32([12, NS])
        m_A = sc.mark()
        Wdt = sc.bf16([16, 16])
        rawpad = [sc.f32([TM + 4]) for _ in range(2)]
        accb = [sc.f32([TM]) for _ in range(2)]
        scTok = sc.f32([3 * 1536])
        scT = sc.f32([36, NS])
        acc_s = sc.f32([12, NS])
        P.op("sp", lambda e: e.dma_start(out=ssdg_bc, in_=ssd_g.partition_broadcast(128)), w=["ssdg"], dma=True)
        P.op("sp", lambda e: e.dma_start(out=Rsel, in_=cst2_d[:, C2_RSEL:C2_RSEL + 1024]), w=["cst2"], dma=True)
        P.op("sp", lambda e: e.dma_start(out=scTok[:NS, :], in_=sconv.rearrange("b r c -> b (r c)")), w=["scTok"], dma=True)
        P.op("sp", lambda e: e.dma_start(out=nconv_s[:, 0:2, :], in_=sconv[:, 1:3, :]), w=["o_ncs01"], dma=True)
        OUTKEYS.append("o_ncs01")
        for half in range(2):
            def trs(e, half=half):
                ins = None
                for idx in range(half * 18, half * 18 + 18):
                    r_, j_ = idx // 12, idx % 12
                    ii = idx - half * 18
                    ins = e.transpose(out=pbf[half][:, ii * NS:(ii + 1) * NS], in_=scTok[:NS, r_ * 1536 + j_ * 128:r_ * 1536 + (j_ + 1) * 128],
                                      identity=identF[:NS, :NS])
                return ins
            P.op("pe", trs, r=["scTok", KC], w=[("ps", half)])
            P.op("dve", lambda e, half=half: e.tensor_copy(out=scT[:, half * 18:half * 18 + 18, :].rearrange("p a b -> p (a b)"), in_=pbf[half][:, 0:18 * NS]),
                 r=[("ps", half)], w=["scT"])
        blocksA = [(COL_Z, "z"), (COL_Z + 512, "z"), (COL_X, "x"), (COL_X + 512, "x"), (COL_B, "x")]
        load_w(Wdt, wview(w_in, COL_DT, 16), "Wdt")
        wcur = wload(wview(w_in, blocksA[0][0], 512))
        jt = 0
        grp = 0
        for bi, (c0, kind) in enumerate(blocksA):
            Wv, kW = wcur
            if bi + 1 < len(blocksA):
                wcur = wload(wview(w_in, blocksA[bi + 1][0], 512))
            if kind == "z":
                zb = (c0 - COL_Z) // 512
                for tt in range(8):
                    bank = 2 + grp % 4
                    grp += 1
                    mm_group(pbf[bank], [(nT[:, k, tt * 128:(tt + 1) * 128], Wv[:, k, :]) for k in range(16)], r=[kW, ("nT", tt)], w=[("ps", bank)])
                    P.op("act", lambda e, tt=tt, zb=zb, bank=bank: e.activation(out=z_tok[:, tt, zb * 512:(zb + 1) * 512], in_=pbf[bank], func=AF.Silu),
                         r=[("ps", bank)], w=["z_tok"])
                for jj in range(4):
                    bank = 2 + grp % 4
                    grp += 1
                    j = zb * 4 + jj
                    mm_group(pbf[bank][:, 0:NS], [(Wv[:, k, jj * 128:(jj + 1) * 128], nT[:, k, TM:T]) for k in range(16)], r=[kW, ("nT", 8)], w=[("ps", bank)])
                    P.op("act", lambda e, j=j, bank=bank: e.activation(out=zT_s[:, j, :], in_=pbf[bank][:, 0:NS], func=AF.Silu), r=[("ps", bank)], w=["zT_s"])
            else:
                for jj in range(4):
                    j = (c0 - COL_X) // 128 + jj
                    rp = rawpad[jt % 2]
                    krp = ("rawpad", jt % 2)
                    P.op("dve", lambda e, rp=rp, j=j: e.tensor_copy(out=rp[:, 0:3], in_=prefix[:, j, :]), r=["prefix"], w=[krp])
                    for nb in range(3):
                        bank = 2 + grp % 4
                        grp += 1
                        lo, hi = (nb * 512, (nb + 1) * 512) if nb < 2 else (TM, T)
                        n = hi - lo
                        rk = [("nT", tt) for tt in range(nb * 4, nb * 4 + 4)] if nb < 2 else [("nT", 8)]
                        mm_group(pbf[bank][:, 0:n], [(Wv[:, k, jj * 128:(jj + 1) * 128], nT[:, k, lo:hi]) for k in range(16)], r=[kW] + rk, w=[("ps", bank)])
                        if nb < 2:
                            P.op("act", lambda e, rp=rp, lo=lo, hi=hi, bank=bank: e.activation(out=rp[:, 3 + lo:3 + hi], in_=pbf[bank], func=AF.Copy),
                                 r=[("ps", bank)], w=[krp])
                        else:
                            P.op("act", lambda e, j=j, bank=bank: e.activation(out=raw_s[:, j, :], in_=pbf[bank][:, 0:NS], func=AF.Copy),
                                 r=[("ps", bank)], w=["raw_s"])
                    P.op("dve", lambda e, rp=rp, j=j: e.tensor_copy(out=ncv[:, j, :], in_=rp[:, TM:TM + 3]), r=[krp], w=["ncv"])
                    conv_silu(rp, krp, accb[jt % 2], ("acc", jt % 2), j, TM, xT[:, j, 0:TM], ("xT",))
                    P.op("dve", lambda e, j=j: e.tensor_scalar(out=acc_s[:, j, :], in0=raw_s[:, j, :], scalar1=cwb[:, j, 3:4], scalar2=cwb[:, j, 4:5],
                                                               op0=ALU.mult, op1=ALU.add), r=["raw_s", KC], w=["acc_s"])
                    for r_ in range(3):
                        P.op("dve", lambda e, j=j, r_=r_: e.scalar_tensor_tensor(out=acc_s[:, j, :], in0=scT[:, r_ * 12 + j, :], scalar=cwb[:, j, r_:r_ + 1],
                                                                                 op0=ALU.mult, in1=acc_s[:, j, :], op1=ALU.add), r=["scT", "acc_s", KC], w=["acc_s"])
                    jt += 1
        P.op("act", lambda e: e.activation(out=xT[:, :, TM:T], in_=acc_s, func=AF.Silu), r=["acc_s"], w=[("xT",)])
        for tt in range(8):
            mm_group(pbf[0][:, 0:16], [(nT[:, k, tt * 128:(tt + 1) * 128], Wdt[:, k, :]) for k in range(16)], r=["Wdt", ("nT", tt)], w=[("ps", 0)])
            P.op("act", lambda e, tt=tt: e.activation(out=dtraw[:, tt, :], in_=pbf[0][:, 0:16], func=AF.Copy), r=[("ps", 0)], w=["dtraw"])
        mm_group(pbf[1][:NS, 0:NS], [(Wdt[:, k, :], nT[:, k, TM:T]) for k in range(16)], r=["Wdt", ("nT", 8)], w=[("ps", 1)])
        P.op("act", lambda e: e.activation(out=dtT_s[:NS, :], in_=pbf[1][:NS, 0:NS], func=AF.Copy), r=[("ps", 1)], w=["dtT_s"])

        def rows_out(src3, ksrc, R, stage, kst, dram_ap, okey):
            for b3 in range(3):
                def trr(e, b3=b3):
                    ins = None
                    for jj in range(4):
                        j_ = b3 * 4 + jj
                        ins = e.transpose(out=pbf[5 + b3][:R, jj * 128:(jj + 1) * 128], in_=src3[:, j_, :], identity=identF)
                    return ins
                P.op("pe", trr, r=[ksrc, KC], w=[("ps", 5 + b3)])
                P.op("dve", lambda e, b3=b3: e.tensor_copy(out=stage[:R, b3 * 512:(b3 + 1) * 512], in_=pbf[5 + b3][:R, :]), r=[("ps", 5 + b3)], w=[kst, "scTok"])
            P.op("sp", lambda e: e.dma_start(out=dram_ap, in_=stage[:R, 0:1536]), r=[kst], w=[okey], dma=True)
            OUTKEYS.append(okey)
        rows_out(raw_s, "raw_s", NS, scTok[:, 0:1536], "stg0", nconv_s[:, 2, :], "o_ncs2")
        rows_out(ncv, "ncv", 3, scTok[:, 1536:3072], "stg1", nconv_p[:, :], "o_ncp")
        P.barrier()
        if STOP_AFTER == "A1":
            dbg_dump("xT", xT.rearrange("p a b -> p (a b)"), [("xT",)])
            dbg_dump("z_tok", z_tok.rearrange("p a b -> p (a b)"), ["z_tok"])
            dbg_dump("zT_s", zT_s.rearrange("p a b -> p (a b)"), ["zT_s"])
            dbg_dump("dtraw", dtraw.rearrange("p a b -> p (a b)"), ["dtraw"])
            dbg_dump("dtT_s", dtT_s, ["dtT_s"])
            finish()
            return nc

        sc.reset(m_A)
        tA = ssd_temps(sc)
        for c in range(8):
            ssd_chunk(tA, xT, c, dtraw[:, c, :], "dtraw", main=True, ztok_c=z_tok[:, c, :], ssdg_bc=ssdg_bc)
        if STOP_AFTER == "A2a":
            P.barrier()
            dbg_dump("YT", YT.rearrange("p a b -> p (a b)"), [("YTb", c) for c in range(8)])
            finish()
            return nc
        hout = sc.f32([8, 128])
        for half in range(2):
            def trh(e, half=half):
                ins = None
                for jj in range(4):
                    q = half * 4 + jj
                    ins = e.transpose(out=pbf[3 + half][:, jj * 128:(jj + 1) * 128], in_=hT[:, q * 128:(q + 1) * 128], identity=identF)
                return ins
            P.op("pe", trh, r=["hT", KC], w=[("ps", 3 + half)])
            P.op("dve", lambda e, half=half: e.tensor_copy(out=hout[:, half * 4:(half + 1) * 4, :].rearrange("p a b -> p (a b)"), in_=pbf[3 + half]),
                 r=[("ps", 3 + half)], w=["hout"])
        P.op("sp", lambda e: e.dma_start(out=nssm_p.rearrange("(q r) n -> r q n", r=128), in_=hout), r=["hout"], w=["o_nsp"], dma=True)
        OUTKEYS.append("o_nsp")

        if STOP_AFTER == "A2b":
            P.barrier()
            dbg_dump("YT", YT.rearrange("p a b -> p (a b)"), [("YTb", c) for c in range(8)])
            finish()
            return nc
        P.barrier()
        sc.reset(m_A)
        sm_s = sc.f32([8, NS])
        cat = sc.f32([32])
        rep = sc.f32([8, 32])
        dtx = sc.f32([8, NS])
        ys = sc.f32([8, NS])
        ysq = sc.f32([8, NS])
        rs = sc.f32([2, NS])
        BC_tok = sc.bf16([512])
        selB = sc.bf16([NS, 128])
        NHS = 4
        hs = [sc.f32([8, 128]) for _ in range(NHS)]
        junk = sc.f32([128])
        x0 = sm_s[:NS, 0, :]; e1 = sm_s[:NS, 1, :]; anp = sm_s[:NS, 2, 0:1]
        P.op("act", lambda e: e.activation(out=e1, in_=dtT_s[:NS, :], func=AF.Exp, bias=dtb_pp[:NS, :]), r=["dtT_s", KC], w=["s_e1"])
        P.op("act", lambda e: e.activation(out=cat[:NS, 16:32], in_=e1, func=AF.Ln, bias=1.0), r=["s_e1"], w=["s_dt"])
        P.op("act", lambda e: e.activation(out=anp, in_=al_pp[:NS, :], func=AF.Exp), r=[KC], w=["s_anp0"])
        P.op("dve", lambda e: e.tensor_scalar(out=anp, in0=anp, scalar1=-1.0, scalar2=None, op0=ALU.mult), r=["s_anp0"], w=["s_anp"])
        P.op("act", lambda e: e.activation(out=cat[:NS, 0:16], in_=cat[:NS, 16:32], func=AF.Exp, scale=anp), r=["s_dt", "s_anp"], w=["s_dA"])

        def repf(e):
            ins = None
            for q in range(8):
                ins = e.matmul(pbf[0][:, q * 32:(q + 1) * 32], lhsT=Rsel[:NS, q * 128:(q + 1) * 128], rhs=cat[:NS, :], start=True, stop=True)
            return ins
        P.op("pe", repf, r=["s_dA", "s_dt", "cst2"], w=[("ps", 0)])
        P.op("dve", lambda e: e.tensor_copy(out=rep.rearrange("p a b -> p (a b)"), in_=pbf[0][:, 0:256]), r=[("ps", 0)], w=["rep"])
        xs_s = xT[:, 0:8, TM:T]
        P.op("dve", lambda e: e.tensor_tensor(out=dtx, in0=rep[:, :, 16:32], in1=xs_s, op=ALU.mult), r=["rep", ("xT",)], w=["dtx"])
        psbc = pbb[1][:NS, 0:512].rearrange("p (a b) -> p a b", a=4)

        def trbc(e):
            ins = None
            for jj in range(4):
                ins = e.transpose(out=psbc[:, jj, :], in_=xT[:, 8 + jj, TM:T], identity=identB)
            return ins
        P.op("pe", trbc, r=[("xT",), "identB"], w=[("ps", 1)])
        P.op("act", lambda e: e.activation(out=BC_tok[:NS, :], in_=pbb[1][:NS, 0:512], func=AF.Copy), r=[("ps", 1)], w=["BC_tok"])
        P.op("dve", lambda e: e.tensor_copy(out=selB[:NS, :, :], in_=identB[:NS, 0:NS].unsqueeze(2).broadcast_to([NS, NS, 128])), r=["identB"], w=["selB"])
        P.op("dve", lambda e: e.memset(ys, 0.0), w=["ys"])

        def hs_load(b):
            P.op("sp", lambda e, b=b: e.dma_start(out=hs[b % NHS], in_=sssm[b].rearrange("(q r) n -> r q n", r=128)), w=[("hs", b % NHS, q) for q in range(8)], dma=True)
        for b in range(min(NHS - 1, NS)):
            hs_load(b)
        for b in range(NS):
            hb = hs[b % NHS]
            bank = 2 + b % 2
            if b + NHS - 1 < NS:
                hs_load(b + NHS - 1)
            P.op("pe", lambda e, b=b, bank=bank: e.matmul(pbf[bank], lhsT=selB[:NS, b, :], rhs=BC_tok[:NS, :], start=True, stop=True),
                 r=["selB", "BC_tok"], w=[("ps", bank)])
            for q in range(8):
                P.op("act", lambda e, b=b, q=q, hb=hb: e.activation(out=hb[:, q, :], in_=hb[:, q, :], func=AF.Copy, scale=rep[:, q, b:b + 1]),
                     r=[("hs", b % NHS, q), "rep"], w=[("hs", b % NHS, q)])

            def upd(q, b=b, hb=hb, bank=bank):
                g = q // 4
                P.op("dve", lambda e: e.scalar_tensor_tensor(out=hb[:, q, :], in0=pbf[bank][:, g * 128:(g + 1) * 128], scalar=dtx[:, q, b:b + 1], op0=ALU.mult,
                                                             in1=hb[:, q, :], op1=ALU.add), r=[("ps", bank), "dtx", ("hs", b % NHS, q)], w=[("hs", b % NHS, q)])

            def yacc(q, b=b, hb=hb, bank=bank):
                g = q // 4
                P.op("dve", lambda e: e.scalar_tensor_tensor(out=junk, in0=hb[:, q, :], scalar=1.0, op0=ALU.mult,
                                                             in1=pbf[bank][:, 256 + g * 128:256 + (g + 1) * 128], op1=ALU.mult,
                                                             accum_out=ys[:, q, b:b + 1]), r=[("hs", b % NHS, q), ("ps", bank)], w=["ys", "junk"])
            upd(0)
            upd(1)
            for q in range(8):
                yacc(q)
                if q + 2 < 8:
                    upd(q + 2)
            ok = ("o_nss", b)
            P.op("sp", lambda e, b=b, hb=hb: e.dma_start(out=nssm_s[b].rearrange("(q r) n -> r q n", r=128), in_=hb), r=[("hs", b % NHS, q) for q in range(8)], w=[ok], dma=True)
            OUTKEYS.append(ok)
        P.op("dve", lambda e: e.tensor_tensor(out=ysq, in0=xs_s, in1=D_pp.unsqueeze(2).broadcast_to([128, 8, NS]), op=ALU.mult), r=[("xT",), KC], w=["ysq"])
        P.op("dve", lambda e: e.tensor_tensor(out=ys, in0=ys, in1=ysq, op=ALU.add), r=["ys", "ysq"], w=["ys"])
        P.op("dve", lambda e: e.tensor_tensor(out=ys, in0=ys, in1=zT_s, op=ALU.mult), r=["ys", "zT_s"], w=["ys"])
        P.op("dve", lambda e: e.tensor_tensor(out=ysq, in0=ys, in1=ys, op=ALU.mult), r=["ys", "ysq"], w=["ysq"])

        def gsum(e):
            ins = None
            for q in range(8):
                g = q // 4
                ins = e.matmul(pbf[4][:, g * NS:(g + 1) * NS], lhsT=onesF, rhs=ysq[:, q, :], start=(q % 4 == 0), stop=(q % 4 == 3))
            return ins
        P.op("pe", gsum, r=["ysq", KC], w=[("ps", 4)])
        rsf = rs.rearrange("p a b -> p (a b)")
        P.op("act", lambda e: e.activation(out=rsf, in_=pbf[4][:, 0:2 * NS], func=AF.Ln, scale=1.0 / 512, bias=EPS), r=[("ps", 4)], w=["rs0"])
        P.op("act", lambda e: e.activation(out=rsf, in_=rsf, func=AF.Exp, scale=-0.5), r=["rs0"], w=["rs"])
        for g in range(2):
            P.op("dve", lambda e, g=g: e.tensor_tensor(out=ys[:, g * 4:(g + 1) * 4, :], in0=ys[:, g * 4:(g + 1) * 4, :],
                                                       in1=rs[:, g:g + 1, :].broadcast_to([128, 4, NS]), op=ALU.mult), r=["ys", "rs"], w=["ys"])
        P.op("dve", lambda e: e.tensor_tensor(out=YT[:, 8:16, TM:T], in0=ys, in1=sg_pp.unsqueeze(2).broadcast_to([128, 8, NS]), op=ALU.mult),
             r=["ys", KC], w=[("YTb", 8)])
        P.barrier()
        if STOP_AFTER == "A2":
            dbg_dump("YT", YT.rearrange("p a b -> p (a b)"), [("YTb", c) for c in range(9)])
            finish()
            return nc
        sc.reset()
        vng_bc = sc.f32([1024])
        bs_bc = sc.f32([8, 128])
        v_tok = sc.bf16([9, 1024])
        ya = sc.f32([8, T])
        WsT = sc.bf16([8, 128])
        wsraw = ya[:, 0, 0:1024].rearrange("p (a b) -> p a b", a=8)
        gv = [sc.f32([1024]) for _ in range(2)]
        vst = [sc.f32([4]) for _ in range(2)]
        ug = [sc.f32([T]) for _ in range(2)]
        _m_tm = sc.mark()
        tmpm = [sc.f32([512]) for _ in range(2)]
        _m_tm2 = sc.mark()
        sc.reset(_m_tm)
        vsf = sc.f32([1024])
        sc.reset(_m_tm2)
        sq = [sc.bf16([T]) for _ in range(2)]
        rstd_bc = sc.f32([T])
        w00I = sc.bf16([8, NS])
        P.op("sp", lambda e: e.dma_start(out=vng_bc, in_=vn_g.partition_broadcast(128)), w=["vng"], dma=True)
        P.op("sp", lambda e: e.dma_start(out=bs_bc.rearrange("p a b -> p (a b)"), in_=cst2_d[:, C2_BS:C2_BS + 1024]), w=["bs_bc"], dma=True)
        P.op("sp", lambda e: e.dma_start(out=wsraw, in_=ws_d.rearrange("h t s -> t h s")), w=["wsraw"], dma=True)
        wv2 = [wload(wview(w_in, COL_V, 512)), wload(wview(w_in, COL_V + 512, 512))]
        for half in range(2):
            def trw(e, half=half):
                ins = None
                for jj in range(4):
                    h_ = half * 4 + jj
                    ins = e.transpose(out=pbf[half][:, jj * 128:(jj + 1) * 128], in_=wsraw[:, h_, :], identity=identF)
                return ins
            P.op("pe", trw, r=["wsraw", KC], w=[("ps", half)])
            P.op("dve", lambda e, half=half: e.tensor_tensor(out=WsT[:, half * 4:(half + 1) * 4, :], in0=pbf[half].rearrange("p (a b) -> p a b", a=4),
                                                             in1=triF.unsqueeze(1).broadcast_to([128, 4, 128]), op=ALU.mult), r=[("ps", half), KC], w=["WsT"])
        for h_ in range(8):
            P.op("dve", lambda e, h_=h_: e.tensor_scalar(out=w00I[:NS, h_, :], in0=identF[:NS, 0:NS], scalar1=w00_bc[:NS, h_:h_ + 1], scalar2=None, op0=ALU.mult),
                 r=[KC], w=["w00I"])
        for tt in range(9):
            rows = 128 if tt < 8 else NS
            i = tt % 2
            tcols = slice(tt * 128, tt * 128 + rows)
            for zb in range(2):
                bank = 2 + (tt * 2 + zb) % 4
                mm_group(pbf[bank][:rows, :], [(nT[:, k, tcols], wv2[zb][0][:, k, :]) for k in range(16)], r=[wv2[zb][1], ("nT", tt)], w=[("ps", bank)])
                P.op("act", lambda e, i=i, zb=zb, bank=bank, rows=rows: e.activation(out=gv[i][:rows, zb * 512:(zb + 1) * 512], in_=pbf[bank][:rows, :],
                                                                                    func=AF.Gelu_apprx_tanh), r=[("ps", bank)], w=[("gv", i, zb)])
            P.op("act", lambda e, i=i, rows=rows: e.activation(out=ug[i][:rows, 0:1024], in_=gv[i][:rows, :], func=AF.Square, accum_out=vst[i][:rows, 0:1]),
                 r=[("gv", i, 0), ("gv", i, 1)], w=[("ug", i), ("vst", i)])
            P.op("act", lambda e, i=i, rows=rows: e.activation(out=vst[i][:rows, 1:2], in_=vst[i][:rows, 0:1], func=AF.Ln, scale=1.0 / 1024, bias=EPS),
                 r=[("vst", i)], w=[("vst", i)])
            P.op("act", lambda e, i=i, rows=rows: e.activation(out=vst[i][:rows, 2:3], in_=vst[i][:rows, 1:2], func=AF.Exp, scale=-0.5),
                 r=[("vst", i)], w=[("vst", i)])
            P.op("dve", lambda e, i=i, rows=rows, tt=tt: e.scalar_tensor_tensor(out=v_tok[:rows, tt, :], in0=gv[i][:rows, :], scalar=vst[i][:rows, 2:3], op0=ALU.mult,
                                                                                in1=vng_bc[:rows, :], op1=ALU.mult),
                 r=[("gv", i, 0), ("gv", i, 1), ("vst", i), "vng"], w=[("v_tok", tt)])
            if tt == 8:
                P.op("dve", lambda e, i=i: e.scalar_tensor_tensor(out=vsf[:NS, :], in0=gv[i][:NS, :], scalar=vst[i][:NS, 2:3], op0=ALU.mult,
                                                                  in1=vng_bc[:NS, :], op1=ALU.mult), r=[("gv", i, 0), ("gv", i, 1), ("vst", i), "vng"], w=["vsf", ("tmpm", 0), ("tmpm", 1)])
                P.op("sp", lambda e: e.dma_start(out=v_s[:, :], in_=vsf[:NS, :]), r=["vsf", ("tmpm", 0), ("tmpm", 1)], w=["o_vs"], dma=True)
                OUTKEYS.append("o_vs")
        wu2 = [wload(wview(w_in, COL_U, 512)), wload(wview(w_in, COL_U + 512, 512))]
        tok_blocks = [(0, 512), (512, 1024), (TM, T)]
        grp = 0
        for h_ in range(8):
            ub, jj = h_ // 4, h_ % 4
            Wv, kWu = wu2[ub]
            ui = h_ % 2
            for nb, (lo, hi) in enumerate(tok_blocks):
                bank = grp % 2
                grp += 1
                n = hi - lo
                rk = [("nT", tt) for tt in range(nb * 4, nb * 4 + 4)] if nb < 2 else [("nT", 8)]
                mm_group(pbf[bank][:, 0:n], [(Wv[:, k, jj * 128:(jj + 1) * 128], nT[:, k, lo:hi]) for k in range(16)], r=[kWu] + rk, w=[("ps", bank)])
                P.op("act", lambda e, ui=ui, lo=lo, hi=hi, n=n, bank=bank: e.activation(out=ug[ui][:, lo:hi], in_=pbf[bank][:, 0:n], func=AF.Gelu_apprx_tanh),
                     r=[("ps", bank)], w=[("ug", ui)])
            for half in range(2):
                def mix(e, half=half, h_=h_):
                    ins = None
                    for cc in range(4):
                        c = half * 4 + cc
                        ins = e.matmul(pbf[2 + half][:, cc * 128:(cc + 1) * 128], lhsT=v_tok[:, c, h_ * 128:(h_ + 1) * 128], rhs=WsT[:, h_, :], start=True, stop=True)
                    return ins
                P.op("pe", mix, r=[("v_tok", c) for c in range(half * 4, half * 4 + 4)] + ["WsT"], w=[("ps", 2 + half)])
                tm = tmpm[half]
                P.op("dve", lambda e, half=half, h_=h_, tm=tm: e.tensor_tensor(out=tm.rearrange("p (a b) -> p a b", a=4), in0=pbf[2 + half].rearrange("p (a b) -> p a b", a=4),
                                                                               in1=bs_bc[:, h_:h_ + 1, :].broadcast_to([128, 4, 128]), op=ALU.add),
                     r=[("ps", 2 + half), "bs_bc"], w=[("tmpm", half)])
                P.op("dve", lambda e, half=half, h_=h_, tm=tm, ui=ui: e.tensor_tensor(out=ya[:, h_, half * 512:(half + 1) * 512], in0=tm, in1=ug[ui][:, half * 512:(half + 1) * 512], op=ALU.mult),
                     r=[("tmpm", half), ("ug", ui)], w=[("ya", h_)])
            P.op("pe", lambda e, h_=h_: e.matmul(pbf[4][:, 0:NS], lhsT=v_tok[:NS, 8, h_ * 128:(h_ + 1) * 128], rhs=w00I[:NS, h_, :], start=True, stop=True),
                 r=[("v_tok", 8), "w00I"], w=[("ps", 4)])
            P.op("dve", lambda e, h_=h_, ui=ui: e.scalar_tensor_tensor(out=ya[:, h_, TM:T], in0=pbf[4][:, 0:NS], scalar=b0_bc[:, h_:h_ + 1], op0=ALU.add, in1=ug[ui][:, TM:T], op1=ALU.mult),
                 r=[("ps", 4), ("ug", ui), KC], w=[("ya", h_)])
            si = h_ % 2
            P.op("act", lambda e, h_=h_, si=si: e.activation(out=sq[si], in_=ya[:, h_, :], func=AF.Square), r=[("ya", h_)], w=[("sq", si)])
            for nb, (lo, hi) in enumerate(tok_blocks):
                n = hi - lo
                P.op("pe", lambda e, nb=nb, lo=lo, hi=hi, n=n, si=si, h_=h_: e.matmul(pbf[5 + nb][:, 0:n], lhsT=onesB, rhs=sq[si][:, lo:hi], start=(h_ == 0), stop=(h_ == 7)),
                     r=[("sq", si), "onesB"], w=[("ps", 5 + nb)])
        for nb, (lo, hi) in enumerate(tok_blocks):
            n = hi - lo
            P.op("act", lambda e, nb=nb, lo=lo, hi=hi, n=n: e.activation(out=rstd_bc[:, lo:hi], in_=pbf[5 + nb][:, 0:n], func=AF.Ln, scale=1.0 / 1024, bias=EPS),
                 r=[("ps", 5 + nb)], w=[("rstd", nb)])
            P.op("act", lambda e, lo=lo, hi=hi: e.activation(out=rstd_bc[:, lo:hi], in_=rstd_bc[:, lo:hi], func=AF.Exp, scale=-0.5), r=[("rstd", nb)], w=[("rstd", nb)])
        for h_ in range(8):
            P.op("dve", lambda e, h_=h_: e.scalar_tensor_tensor(out=YT[:, h_, :], in0=ya[:, h_, :], scalar=gon_pp[:, h_:h_ + 1], op0=ALU.mult, in1=rstd_bc, op1=ALU.mult),
                 r=[("ya", h_), ("rstd", 0), ("rstd", 1), ("rstd", 2), KC], w=[("YTa", h_)])
        P.barrier()
        if STOP_AFTER == "A3":
            dbg_dump("YT", YT.rearrange("p a b -> p (a b)"), [("YTa", h_) for h_ in range(8)])
            finish()
            return nc

        sc.reset()
        hres = sc.f32([9, DM])
        m_B = sc.mark()
        for tt in range(9):
            rows = 128 if tt < 8 else NS
            src = xm[tt * 128:(tt + 1) * 128, :] if tt < 8 else xsmp[:, :]
            P.op("sp", lambda e, tt=tt, rows=rows, src=src: e.dma_start(out=hres[:rows, tt, :], in_=src), w=[("h", tt)], dma=True)
        wcur = wload(wview(w_out, 0, 512))
        for cb in range(4):
            Wv, kW = wcur
            if cb + 1 < 4:
                wcur = wload(wview(w_out, (cb + 1) * 512, 512))
            for tt in range(9):
                rows = 128 if tt < 8 else NS
                tcols = slice(tt * 128, tt * 128 + rows)
                bank = (cb * 9 + tt) % 6
                mm_group(pbf[bank][:rows, :], [(YT[:, k, tcols], Wv[:, k, :]) for k in range(16)], r=[kW], w=[("ps", bank)])
                P.op("dve", lambda e, rows=rows, tt=tt, cb=cb, bank=bank: e.tensor_tensor(out=hres[:rows, tt, cb * 512:(cb + 1) * 512], in0=pbf[bank][:rows, :],
                                                                                         in1=hres[:rows, tt, cb * 512:(cb + 1) * 512], op=ALU.add),
                     r=[("ps", bank), ("h", tt)], w=[("h", tt)])
        P.barrier()
        if STOP_AFTER == "B":
            dbg_dump("h", hres.rearrange("p a b -> p (a b)"), [("h", tt) for tt in range(9)])
            finish()
            return nc

        sc.reset(m_B)
        gbc = sc.f32([DM])
        gbc2 = sc.f32([DM])
        stmp = [sc.f32([512]) for _ in range(2)]
        ST = [sc.f32([4]) for _ in range(2)]
        r2.reset()
        actb = [r2.bf16([2, T]) for _ in range(2)]
        Wd = [r2.bf16([2, DM]) for _ in range(2)]
        XN = [r2.bf16([DM]) for _ in range(2)]
        mT = nT
        P.op("sp", lambda e: e.dma_start(out=gbc, in_=g_ffn.partition_broadcast(128)), w=["gbc"], dma=True)
        P.op("sp", lambda e: e.dma_start(out=gbc2, in_=g_fin.partition_broadcast(128)), w=["gbc2"], dma=True)
        for tt in range(9):
            rows = 128 if tt < 8 else NS
            i = tt % 2
            rms_transpose(hres[:, tt, :], ("h", tt), rows, XN[i], ("XN", i), ST[i], ("ST", i), gbc, "gbc", mT, tt * 128, ("mT", tt), DM, tt)
        NFB = DFF // 256
        if STOP_AFTER == "C0":
            P.barrier()
            dbg_dump("mT", mT.rearrange("p a b -> p (a b)"), [("mT", tt) for tt in range(9)])
            dbg_dump("XN0", XN[0], [("XN", 0)])
            dbg_dump("ST0", ST[0], [("ST", 0)])
            dbg_dump("gbc", gbc, ["gbc"])
            dbg_dump("h", hres.rearrange("p a b -> p (a b)"), [("h", tt) for tt in range(9)])
            finish()
            return nc

        gu_slot = {}

        def load_gu(fb):
            slot = wq[0] % 2
            wq[0] += 1
            gu_slot[fb] = slot
            load_w(WG[slot], wview(w_gate, fb * 256, 256), ("W", slot), nobar=True)
            load_w(WU[slot], wview(w_up, fb * 256, 256), ("W", slot), nobar=True)

        def load_d(fb):
            load_w(Wd[fb % 2], w_down[fb * 256:(fb + 1) * 256, :].rearrange("(fl p) c -> p fl c", p=128), ("Wd", fb % 2))

        gctr = [0]

        def GU(fb):
            slot = gu_slot[fb]
            Wg_, Wu_, kWs = WG[slot], WU[slot], ("W", slot)
            for fl in range(2):
                for nb, (lo, hi) in enumerate(tok_blocks):
                    n = hi - lo
                    pr = gctr[0] % 2
                    gctr[0] += 1
                    bg, bu = pr * 2, pr * 2 + 1
                    rk = [("mT", tt) for tt in range(nb * 4, nb * 4 + 4)] if nb < 2 else [("mT", 8)]
                    mm_group(pbf[bg][:, 0:n], [(Wg_[:, k, fl * 128:(fl + 1) * 128], mT[:, k, lo:hi]) for k in range(16)], r=[kWs] + rk, w=[("ps", bg)])
                    mm_group(pbf[bu][:, 0:n], [(Wu_[:, k, fl * 128:(fl + 1) * 128], mT[:, k, lo:hi]) for k in range(16)], r=[kWs] + rk, w=[("ps", bu)])
                    P.op("act", lambda e, pr=pr, bg=bg, n=n: e.activation(out=stmp[pr][:, 0:n], in_=pbf[bg][:, 0:n], func=AF.Silu), r=[("ps", bg)], w=[("stmp", pr)])
                    P.op("dve", lambda e, pr=pr, bu=bu, n=n, fb=fb, fl=fl, lo=lo, hi=hi: e.tensor_tensor(out=actb[fb % 2][:, fl, lo:hi], in0=pbf[bu][:, 0:n], in1=stmp[pr][:, 0:n], op=ALU.mult),
                         r=[("ps", bu), ("stmp", pr)], w=[("act", fb % 2)])

        dctr = [0]

        def DN(fb):
            for tt in range(9):
                rows = 128 if tt < 8 else NS
                tcols = slice(tt * 128, tt * 128 + rows)
                for cb in range(4):
                    bank = 4 + dctr[0] % 4
                    dctr[0] += 1
                    mm_group(pbf[bank][:rows, :], [(actb[fb % 2][:, fl, tcols], Wd[fb % 2][:, fl, cb * 512:(cb + 1) * 512]) for fl in range(2)],
                             r=[("act", fb % 2), ("Wd", fb % 2)], w=[("ps", bank)])
                    P.op("dve", lambda e, rows=rows, tt=tt, cb=cb, bank=bank: e.tensor_tensor(out=hres[:rows, tt, cb * 512:(cb + 1) * 512], in0=pbf[bank][:rows, :],
                                                                                             in1=hres[:rows, tt, cb * 512:(cb + 1) * 512], op=ALU.add),
                         r=[("ps", bank), ("h", tt)], w=[("h", tt)])

        load_gu(0)
        load_d(0)
        for i in range(NFB + 1):
            if i + 1 < NFB:
                load_gu(i + 1)
            if i < NFB:
                GU(i)
            if i >= 1:
                DN(i - 1)
            if i + 1 < NFB and i >= 0:
                load_d(i + 1) if i >= 1 or True else None
        P.barrier()
        if STOP_AFTER == "C":
            dbg_dump("h", hres.rearrange("p a b -> p (a b)"), [("h", tt) for tt in range(9)])
            finish()
            return nc

        for tt in range(9):
            rows = 128 if tt < 8 else NS
            i = tt % 2
            ht = hres[:, tt, :]
            st = ST[i]
            P.op("act", lambda e, rows=rows, ht=ht, st=st, i=i: e.activation(out=XN[i][:rows, :], in_=ht[:rows, :], func=AF.Square, accum_out=st[:rows, 0:1]),
                 r=[("h", tt)], w=[("XN", i), ("ST", i)])
            P.op("act", lambda e, rows=rows, st=st: e.activation(out=st[:rows, 1:2], in_=st[:rows, 0:1], func=AF.Ln, scale=1.0 / DM, bias=EPS), r=[("ST", i)], w=[("ST", i)])
            P.op("act", lambda e, rows=rows, st=st: e.activation(out=st[:rows, 2:3], in_=st[:rows, 1:2], func=AF.Exp, scale=-0.5), r=[("ST", i)], w=[("ST", i)])
            P.op("dve", lambda e, rows=rows, ht=ht, st=st: e.scalar_tensor_tensor(out=ht[:rows, :], in0=ht[:rows, :], scalar=st[:rows, 2:3], op0=ALU.mult, in1=gbc2[:rows, :], op1=ALU.mult),
                 r=[("h", tt), ("ST", i), "gbc2"], w=[("h", tt)])
            dst = y_m[tt * 128:(tt + 1) * 128, :] if tt < 8 else y_s[:, :]
            ok = ("o_y", tt)
            P.op("sp", lambda e, rows=rows, ht=ht, dst=dst: e.dma_start(out=dst, in_=ht[:rows, :]), r=[("h", tt)], w=[ok], dma=True)
            OUTKEYS.append(ok)
        finish()
        return nc


def _host_consts(inputs, core):
    hf = core % 2
    cst = np.zeros((128, CSTW), np.float32)
    r = np.arange(128)
    cst[:, C_ID:C_ID + 128] = np.eye(128, dtype=np.float32)
    cst[:, C_TRI:C_TRI + 128] = (r[:, None] <= r[None, :]).astype(np.float32)
    cst[:, C_U:C_U + 128] = (r[:, None] > r[None, :]).astype(np.float32)
    cst[:, C_ONE:C_ONE + 128] = 1.0
    cw = np.asarray(inputs["ssd_conv_w"])[0]
    cb = np.asarray(inputs["ssd_conv_b"])[0]
    cwb = np.concatenate([cw, cb[None]], 0)
    cst[:, C_CWB:C_CWB + 60] = cwb.reshape(5, 12, 128).transpose(2, 1, 0).reshape(128, 60)
    cst[:, C_GON:C_GON + 8] = np.asarray(inputs["chunk_out_norm_g"])[0].reshape(8, 128).T
    cst[:, C_ALOG:C_ALOG + 16] = np.asarray(inputs["ssd_a_log"])[0][None, :]
    cst[:, C_DTB:C_DTB + 16] = np.asarray(inputs["ssd_dt_bias"])[0][None, :]
    cst[:, C_D:C_D + 16] = np.asarray(inputs["ssd_d"])[0][None, :]
    cst[:, C_W00:C_W00 + 8] = np.asarray(inputs["chunk_w_s"])[0][:, 0, 0][None, :]
    cst[:, C_B0:C_B0 + 8] = np.asarray(inputs["chunk_b_s"])[0][:, 0][None, :]
    cst[:, C_FLAG] = float(hf)
    cst[:, C_DPP:C_DPP + 8] = np.repeat(np.asarray(inputs["ssd_d"])[0], 64).reshape(8, 128).T
    cst[:, C_SGPP:C_SGPP + 8] = np.asarray(inputs["ssd_norm_g"])[0].reshape(8, 128).T
    cst[0:16, C_DTBPP] = np.asarray(inputs["ssd_dt_bias"])[0]
    cst[0:16, C_ALPP] = np.asarray(inputs["ssd_a_log"])[0]
    cst2 = np.zeros((128, CST2W), np.float32)
    cst2[:, C2_BS:C2_BS + 1024] = np.asarray(inputs["chunk_b_s"])[0].reshape(1, 1024)
    rs = np.zeros((16, 8, 128), np.float32)
    for q in range(8):
        for rr in range(128):
            rs[2 * q + rr // 64, q, rr] = 1.0
    cst2[0:16, C2_RSEL:C2_RSEL + 1024] = rs.reshape(16, 1024)
    return cst, cst2


def make_in_maps(inputs):
    xp = np.asarray(inputs["x_prompt"], np.float32)
    xs = np.asarray(inputs["x_sample"], np.float32)
    sconv = np.asarray(inputs["state_conv"], np.float32)[0]
    sssm = np.asarray(inputs["state_ssm"], np.float32)[0]
    shared = {
        "w_in": np.ascontiguousarray(np.asarray(inputs["w_in"], np.float32)[0]),
        "w_out": np.ascontiguousarray(np.asarray(inputs["w_out"], np.float32)[0]),
        "w_gate": np.ascontiguousarray(np.asarray(inputs["w_gate"], np.float32)[0]),
        "w_up": np.ascontiguousarray(np.asarray(inputs["w_up"], np.float32)[0]),
        "w_down": np.ascontiguousarray(np.asarray(inputs["w_down"], np.float32)[0]),
        "ws": np.ascontiguousarray(np.asarray(inputs["chunk_w_s"], np.float32)[0]),
        "g_mix": np.ascontiguousarray(np.asarray(inputs["norm_mix_g"], np.float32)[0]),
        "g_ffn": np.ascontiguousarray(np.asarray(inputs["norm_ffn_g"], np.float32)[0]),
        "g_fin": np.ascontiguousarray(np.asarray(inputs["norm_final_g"], np.float32)),
        "vn_g": np.ascontiguousarray(np.asarray(inputs["chunk_v_norm_g"], np.float32)[0]),
        "ssd_g": np.ascontiguousarray(np.asarray(inputs["ssd_norm_g"], np.float32)[0]),
    }
    zeros_prev = np.zeros((TM, DM), np.float32)
    maps = []
    for c in range(8):
        b, hf = c // 2, c % 2
        cst, cst2 = _host_consts(inputs, c)
        m = dict(shared)
        m["xm"] = np.ascontiguousarray(xp[b, hf * TM:(hf + 1) * TM])
        m["xprev"] = np.ascontiguousarray(xp[b, 0:TM]) if hf == 1 else zeros_prev
        m["xsmp"] = np.ascontiguousarray(xs[c * NS:(c + 1) * NS, 0])
        m["sconv"] = np.ascontiguousarray(sconv[c * NS:(c + 1) * NS])
        m["sssm"] = np.ascontiguousarray(sssm[c * NS:(c + 1) * NS].reshape(NS, 1024, 128))
        m["cst"] = cst
        m["cst2"] = cst2
        maps.append(m)
    return maps


def kernel(**inputs):
    nc = build_program()
    maps = make_in_maps(inputs)
    res = run_bass_kernel_spmd(nc, maps, core_ids=list(range(8)))
    R = res.results
    y_prompt = np.zeros((4, 2048, DM), np.float32)
    y_sample = np.zeros((128, 1, DM), np.float32)
    ncp = np.zeros((1, 4, 3, 1536), np.float32)
    nsp = np.zeros((1, 4, 16, 64, 128), np.float32)
    ncs = np.zeros((1, 128, 3, 1536), np.float32)
    nss = np.zeros((1, 128, 16, 64, 128), np.float32)
    vs = np.zeros((1, 128, 1, 1024), np.float32)
    for c in range(8):
        b, hf = c // 2, c % 2
        y_prompt[b, hf * TM:(hf + 1) * TM] = R[c]["y_m"]
        y_sample[c * NS:(c + 1) * NS, 0] = R[c]["y_s"]
        if hf == 1:
            ncp[0, b] = R[c]["nconv_p"]
            nsp[0, b] = R[c]["nssm_p"].reshape(16, 64, 128)
        ncs[0, c * NS:(c + 1) * NS] = R[c]["nconv_s"]
        nss[0, c * NS:(c + 1) * NS] = R[c]["nssm_s"].reshape(NS, 16, 64, 128)
        vs[0, c * NS:(c + 1) * NS, 0] = R[c]["v_s"]
    return (y_prompt, y_sample, ncp, nsp, ncs, nss, vs)
```

```python
import numpy as np
from contextlib import ExitStack
import concourse.bass as bass
import concourse.mybir as mybir
from concourse.bass_utils import run_bass_kernel_spmd

F32 = mybir.dt.float32
F32R = mybir.dt.float32r
BF16 = mybir.dt.bfloat16
AF = mybir.ActivationFunctionType
ALU = mybir.AluOpType
AX = mybir.AxisListType

ENGS = ["pe", "act", "dve", "pool", "sp"]
NDMASEM = 12

DEBUG = {}
STOP_AFTER = None
SSD_LEVEL = 9


class Op:
    __slots__ = ("eng", "fn", "r", "w", "dma", "deps", "sig", "sem", "prev_use", "waits", "need_sig", "xdeps", "nobar")

    def __init__(self, eng, fn, r, w, dma, nobar=False):
        self.eng, self.fn, self.r, self.w, self.dma = eng, fn, tuple(r), tuple(w), dma
        self.deps = set()
        self.xdeps = set()
        self.sig = None
        self.sem = None
        self.prev_use = 0
        self.waits = []
        self.need_sig = False
        self.nobar = nobar


BARRIER = Op(None, None, (), (), False)


class Prog:
    def __init__(self):
        self.ops = []

    def op(self, eng, fn, r=(), w=(), dma=False, nobar=False):
        w = list(w) + [k for k in r if isinstance(k, tuple) and k and k[0] == "ps" and k not in w]
        o = Op(eng, fn, r, w, dma, nobar)
        self.ops.append(o)
        return o

    def barrier(self):
        self.ops.append(BARRIER)

    def capture(self, fn):
        saved = self.ops
        self.ops = []
        try:
            fn()
            got = self.ops
        finally:
            self.ops = saved
        return got

    @staticmethod
    def merge(main, side):
        if not side:
            return list(main)
        out = []
        m, s_ = len(main), len(side)
        j = 0
        for i, o in enumerate(main):
            out.append(o)
            tgt = (i + 1) * s_ // m
            while j < tgt:
                out.append(side[j])
                j += 1
        out.extend(side[j:])
        return out

    def resolve(self):
        ops = self.ops
        last_w = {}
        readers = {}
        last_eng = {}
        dmas = []
        pending = {}
        for i, o in enumerate(ops):
            if o is BARRIER:
                s = set(last_eng.values()) | set(dmas)
                for e in ENGS:
                    pending[e] = set(s) | pending.get(e, set())
                continue
            if o.eng in pending and not o.nobar:
                o.xdeps = o.xdeps | pending.pop(o.eng)
            deps = set(o.xdeps)
            for k in o.r:
                if k in last_w:
                    deps.add(last_w[k])
            for k in o.w:
                if k in last_w:
                    deps.add(last_w[k])
                deps.update(readers.get(k, ()))
            deps.discard(i)
            keep = set()
            for d in deps:
                od = ops[d]
                if od.dma:
                    keep.add(d)
                elif od.eng == o.eng and not o.dma:
                    if o.eng == "pe":
                        continue
                    if d in o.xdeps:
                        continue
                    if set(od.w) & set(o.r):
                        keep.add(d)
                else:
                    keep.add(d)
            o.deps = keep
            for d in keep:
                ops[d].need_sig = True
            for k in o.r:
                readers.setdefault(k, []).append(i)
            for k in o.w:
                last_w[k] = i
                readers[k] = []
            if o.dma:
                dmas.append(i)
            elif o.fn is not None:
                last_eng[o.eng] = i
        cnt = {e: 0 for e in ENGS}
        dma_n = {e: 0 for e in ENGS}
        dma_use = {}
        for i, o in enumerate(ops):
            if o is BARRIER:
                continue
            if o.dma:
                slot = (o.eng, dma_n[o.eng] % NDMASEM)
                dma_n[o.eng] += 1
                u = dma_use.get(slot, 0)
                o.prev_use = u
                dma_use[slot] = u + 1
                o.sem = slot
                o.sig = 16 * (u + 1)
            elif o.need_sig:
                cnt[o.eng] += 1
                o.sem = o.eng
                o.sig = cnt[o.eng]
        seen = {e: {} for e in ENGS}
        for i, o in enumerate(ops):
            if o is BARRIER:
                continue
            sd = seen[o.eng]
            need = {}
            if o.dma and o.prev_use > 0:
                need[o.sem] = 16 * o.prev_use
            for d in o.deps:
                od = ops[d]
                need[od.sem] = max(need.get(od.sem, 0), od.sig)
            o.waits = []
            for s, v in need.items():
                if sd.get(s, 0) >= v:
                    continue
                sd[s] = v
                o.waits.append((s, v))

    def emit(self, sems, block):
        ops = self.ops

        def run(eng_name):
            def body(e):
                for o in ops:
                    if o is BARRIER or o.eng != eng_name:
                        continue
                    for s, v in o.waits:
                        e.wait_ge(sems[s], v)
                    if o.fn is None:
                        continue
                    ins = o.fn(e)
                    if o.dma:
                        ins.then_inc(sems[o.sem], 16)
                    elif o.sig is not None:
                        ins.then_inc(sems[o.sem], 1)
            return body

        block.sync(run("sp"))
        block.tensor(run("pe"))
        block.scalar(run("act"))
        block.vector(run("dve"))
        block.gpsimd(run("pool"))


DM = 2048
DIN = 4624
DFF = 5632
TM = 1024
NS = 16
T = TM + NS
EPS = 1e-6
COL_U, COL_V, COL_Z, COL_X, COL_B, COL_C, COL_DT = 0, 1024, 2048, 3072, 4096, 4352, 4608

C_ID, C_TRI, C_U, C_ONE = 0, 128, 256, 384
C_CWB, C_GON, C_ALOG, C_DTB, C_D, C_W00, C_B0, C_FLAG = 512, 572, 580, 596, 612, 628, 636, 644
C_DPP, C_SGPP, C_DTBPP, C_ALPP = 648, 656, 664, 665
CSTW = 672
C2_BS, C2_RSEL = 0, 1024
CST2W = 2048

AW = 51000


class Bump:
    def __init__(self, arena, start, end):
        self.arena, self.start, self.end, self.off = arena, start, end, start
        self.peak = start

    def reset(self, to=None):
        self.off = self.start if to is None else to

    def mark(self):
        return self.off

    def _take(self, words):
        o = self.off
        self.off += words
        assert self.off <= self.end, ("arena overflow", self.off, self.end)
        self.peak = max(self.peak, self.off)
        return o

    def f32(self, shape):
        n = int(np.prod(shape))
        o = self._take(n)
        v = self.arena[:, o:o + n]
        return _shape(v, shape)

    def bf16(self, shape):
        n = int(np.prod(shape))
        w = (n + 1) // 2
        o = self._take(w)
        v = self.arena[:, o:o + w].bitcast(BF16)[:, 0:n]
        return _shape(v, shape)


def _shape(v, shape):
    if len(shape) == 1:
        return v
    if len(shape) == 2:
        return v.rearrange("p (a b) -> p a b", a=shape[0])
    if len(shape) == 3:
        return v.rearrange("p (a b c) -> p a b c", a=shape[0], b=shape[1])
    raise ValueError(shape)


def build_program():
    nc = bass.Bass("TRN2", target_bir_lowering=False)

    def din(name, shape):
        return nc.dram_tensor(name, shape, F32, kind="ExternalInput").ap()

    def dout(name, shape):
        return nc.dram_tensor(name, shape, F32, kind="ExternalOutput").ap()

    xm = din("xm", [TM, DM])
    xprev = din("xprev", [TM, DM])
    xsmp = din("xsmp", [NS, DM])
    sconv = din("sconv", [NS, 3, 1536])
    sssm = din("sssm", [NS, 1024, 128])
    w_in = din("w_in", [DM, DIN])
    w_out = din("w_out", [DM, DM])
    w_gate = din("w_gate", [DM, DFF])
    w_up = din("w_up", [DM, DFF])
    w_down = din("w_down", [DFF, DM])
    cst_d = din("cst", [128, CSTW])
    cst2_d = din("cst2", [128, CST2W])
    ws_d = din("ws", [8, 128, 128])
    g_mix = din("g_mix", [DM])
    g_ffn = din("g_ffn", [DM])
    g_fin = din("g_fin", [DM])
    vn_g = din("vn_g", [1024])
    ssd_g = din("ssd_g", [1024])

    y_m = dout("y_m", [TM, DM])
    y_s = dout("y_s", [NS, DM])
    nconv_p = dout("nconv_p", [3, 1536])
    nssm_p = dout("nssm_p", [1024, 128])
    nconv_s = dout("nconv_s", [NS, 3, 1536])
    nssm_s = dout("nssm_s", [NS, 1024, 128])
    v_s = dout("v_s", [NS, 1024])
    dbg_d = {}
    for name, (shape, dty) in DEBUG.items():
        dbg_d[name] = nc.dram_tensor("dbg_" + name, list(shape), dty, kind="ExternalOutput").ap()

    P = Prog()
    with ExitStack() as es:
        arena = es.enter_context(nc.sbuf_tensor("arena", [128, AW], F32))
        rhsE = es.enter_context(nc.sbuf_tensor("rhsE", [128, 2048], F32))
        U32 = es.enter_context(nc.sbuf_tensor("U32", [128, 128], F32))
        pb = [es.enter_context(nc.psum_tensor(f"pb{i}", [128, 512], F32)) for i in range(8)]
        sems = {}
        for e in ENGS:
            sems[e] = es.enter_context(nc.semaphore("s_" + e))
        for e in ("sp", "pool"):
            for i in range(NDMASEM):
                sems[(e, i)] = es.enter_context(nc.semaphore(f"d_{e}_{i}"))
        block = es.enter_context(nc.Block())

        arena = arena[:, :]
        rhsEr = rhsE[:, :].bitcast(F32R)
        U32r = U32[:, :].bitcast(F32R)
        pbf = [p[:, :] for p in pb]
        pbb = [p[:, :].bitcast(BF16) for p in pb]

        fx = Bump(arena, 0, AW)
        CST = fx.f32([CSTW])
        identF = CST[:, C_ID:C_ID + 128]
        triF = CST[:, C_TRI:C_TRI + 128]
        UF = CST[:, C_U:C_U + 128]
        onesF = CST[:, C_ONE:C_ONE + 128]
        cwb = CST[:, C_CWB:C_CWB + 60].rearrange("p (j k) -> p j k", j=12)
        gon_pp = CST[:, C_GON:C_GON + 8]
        alog_bc = CST[:, C_ALOG:C_ALOG + 16]
        dtb_bc = CST[:, C_DTB:C_DTB + 16]
        D_bc = CST[:, C_D:C_D + 16]
        w00_bc = CST[:, C_W00:C_W00 + 8]
        b0_bc = CST[:, C_B0:C_B0 + 8]
        flag = CST[:, C_FLAG:C_FLAG + 1]
        D_pp = CST[:, C_DPP:C_DPP + 8]
        sg_pp = CST[:, C_SGPP:C_SGPP + 8]
        dtb_pp = CST[:, C_DTBPP:C_DTBPP + 1]
        al_pp = CST[:, C_ALPP:C_ALPP + 1]
        identB = fx.bf16([128])
        onesB = fx.bf16([128])
        aneg = fx.f32([16])
        hT = fx.f32([1024])
        hTb = fx.bf16([1024])
        prefix = fx.f32([12, 3])
        ncv = fx.f32([12, 3])
        nT = fx.bf16([16, T])
        YT = fx.bf16([16, T])
        r2_start = fx.off - (16 * T) // 2
        r2_end = fx.off
        Wfix = [fx.bf16([16, 512]) for _ in range(2)]
        _wflat = [w_.rearrange("p a b -> p (a b)") for w_ in Wfix]
        WG = [w_[:, 0:4096].rearrange("p (a b) -> p a b", a=16) for w_ in _wflat]
        WU = [w_[:, 4096:8192].rearrange("p (a b) -> p a b", a=16) for w_ in _wflat]
        wq = [0]
        S0 = fx.off
        sc = Bump(arena, S0, AW)
        r2 = Bump(arena, r2_start, r2_end)

        KC = ("cst",)

        P.op("sp", lambda e: e.dma_start(out=CST, in_=cst_d[:, :]), w=[KC], dma=True)
        P.op("dve", lambda e: e.tensor_copy(out=identB, in_=identF), r=[KC], w=["identB"])
        P.op("dve", lambda e: e.memset(onesB, 1.0), w=["onesB"])
        P.op("dve", lambda e: e.tensor_copy(out=U32r, in_=UF), r=[KC], w=["U32"])
        P.op("act", lambda e: e.activation(out=aneg, in_=alog_bc, func=AF.Exp), r=[KC], w=["aneg0"])
        P.op("dve", lambda e: e.tensor_scalar(out=aneg, in0=aneg, scalar1=-1.0, scalar2=None, op0=ALU.mult), r=["aneg0"], w=["aneg"])
        P.op("dve", lambda e: e.memset(hT, 0.0), w=["hT"])
        P.op("dve", lambda e: e.memset(hTb, 0.0), w=["hTb"])

        def rms_transpose(xt, kx, rows, xn, kxn, st, kst, gbc, kg, dstT, col0, kdst, width, tag):
            nk = width // 128
            P.op("act", lambda e: e.activation(out=xn[:rows, :], in_=xt[:rows, :], func=AF.Square, accum_out=st[:rows, 0:1]),
                 r=[kx], w=[kxn, kst])
            P.op("act", lambda e: e.activation(out=st[:rows, 1:2], in_=st[:rows, 0:1], func=AF.Ln, scale=1.0 / width, bias=EPS),
                 r=[kst], w=[kst])
            P.op("act", lambda e: e.activation(out=st[:rows, 2:3], in_=st[:rows, 1:2], func=AF.Exp, scale=-0.5),
                 r=[kst], w=[kst])
            P.op("dve", lambda e: e.scalar_tensor_tensor(out=xn[:rows, :], in0=xt[:rows, :], scalar=st[:rows, 2:3], op0=ALU.mult,
                                                         in1=gbc[:rows, :], op1=ALU.mult),
                 r=[kx, kst, kg, kxn], w=[kxn])
            for b8 in range(nk // 8):
                bank = tag % 2 if nk == 8 else b8
                psv = pbb[bank][:, 0:8 * rows].rearrange("p (a b) -> p a b", a=8)

                def tr(e, b8=b8, psv=psv):
                    ins = None
                    for j in range(8):
                        k = b8 * 8 + j
                        ins = e.transpose(out=psv[:, j, :], in_=xn[:rows, k * 128:(k + 1) * 128], identity=identB[:rows, :rows])
                    return ins
                P.op("pe", tr, r=[kxn, "identB"], w=[("ps", bank)])
                dst = dstT[:, b8 * 8:(b8 + 1) * 8, col0:col0 + rows]
                if b8 == 0:
                    P.op("act", lambda e, dst=dst, psv=psv: e.activation(out=dst, in_=psv, func=AF.Copy), r=[("ps", bank)], w=[kdst])
                else:
                    P.op("dve", lambda e, dst=dst, psv=psv: e.tensor_copy(out=dst, in_=psv), r=[("ps", bank)], w=[kdst])

        def mm_group(out, pairs, r, w):
            def fn(e):
                ins = None
                n = len(pairs)
                for i, (l, rh) in enumerate(pairs):
                    ins = e.matmul(out, lhsT=l, rhs=rh, start=(i == 0), stop=(i == n - 1))
                return ins
            P.op("pe", fn, r=r, w=w)

        def load_w(dst, src_ap, key, nobar=False):
            P.op("pool", lambda e: e.dma_start(out=dst, in_=src_ap), w=[key], dma=True, nobar=nobar)

        def wload(src_ap, ncols=512):
            slot = wq[0] % 2
            wq[0] += 1
            buf = Wfix[slot] if ncols == 512 else Wfix[slot][:, :, 0:ncols]
            load_w(buf, src_ap, ("W", slot), nobar=True)
            return buf, ("W", slot)

        def wview(wap, c0, ncols):
            return wap[:, c0:c0 + ncols].rearrange("(kt p) c -> p kt c", p=128)

        def conv_silu(rawpad, krp, acc, kacc, j, ntok, dst, kdst):
            P.op("dve", lambda e: e.tensor_scalar(out=acc[:, 0:ntok], in0=rawpad[:, 0:ntok], scalar1=cwb[:, j, 0:1], scalar2=cwb[:, j, 4:5],
                                                  op0=ALU.mult, op1=ALU.add), r=[krp, KC], w=[kacc])
            for k in (1, 2, 3):
                P.op("dve", lambda e, k=k: e.scalar_tensor_tensor(out=acc[:, 0:ntok], in0=rawpad[:, k:k + ntok], scalar=cwb[:, j, k:k + 1],
                                                                  op0=ALU.mult, in1=acc[:, 0:ntok], op1=ALU.add), r=[krp, kacc, KC], w=[kacc])
            P.op("act", lambda e: e.activation(out=dst, in_=acc[:, 0:ntok], func=AF.Silu), r=[kacc], w=[kdst])

        def ssd_temps(b, main=True):
            t = {}
            t["sm"] = b.f32([96])
            t["ex"] = b.f32([48])
            t["xdtw"] = b.bf16([16, 64])
            t["B_tok"] = b.bf16([256])
            if not main:
                return t
            t["LT"] = b.f32([2048])
            t["MT"] = b.bf16([16, 128])
            t["xs_tok"] = b.bf16([16, 64])
            t["xdt"] = b.bf16([16, 64])
            t["cbm"] = b.f32([2, 128])
            t["y1"] = b.f32([16, 64])
            t["t2"] = b.f32([16, 64])
            t["yb"] = b.bf16([1024])
            t["gn"] = b.f32([8])
            return t

        def ssd_chunk(t, xT, c, dtraw_c, kdt, main, ztok_c=None, ssdg_bc=None, banks=(0, 1, 2, 7, 1), ktag=""):
            sm, ex = t["sm"], t["ex"]
            cs = slice(c * 128, (c + 1) * 128)
            kx = ("xT" + ktag,)
            bs_, bx_, bb_ = banks[0], banks[1], banks[2]
            P.op("dve", lambda e: e.tensor_tensor(out=sm[:, 0:16], in0=dtraw_c, in1=dtb_bc, op=ALU.add), r=[kdt, KC], w=["sm0"])
            P.op("act", lambda e: e.activation(out=sm[:, 16:32], in_=sm[:, 0:16], func=AF.Exp), r=["sm0"], w=["sm1"])
            P.op("act", lambda e: e.activation(out=sm[:, 32:48], in_=sm[:, 16:32], func=AF.Ln, bias=1.0), r=["sm1"], w=["dtc"])
            dtc = sm[:, 32:48]
            dta = sm[:, 48:64]
            P.op("dve", lambda e: e.tensor_tensor(out=dta, in0=dtc, in1=aneg, op=ALU.mult), r=["dtc", "aneg"], w=["dta"])

            def small(e):
                e.matmul(pbf[bs_][:, 0:16], lhsT=UF, rhs=dta, start=True, stop=True)
                ins = e.matmul(pbf[bs_][:, 16:32], lhsT=onesF, rhs=dta, start=True, stop=True)
                if main:
                    ins = e.matmul(pbf[bs_][:, 32:48], lhsT=triF, rhs=dta, start=True, stop=True)
                return ins
            P.op("pe", small, r=["dta", KC], w=[("ps", bs_)])
            nex = 48 if main else 32
            P.op("act", lambda e: e.activation(out=ex[:, 0:nex], in_=pbf[bs_][:, 0:nex], func=AF.Exp), r=[("ps", bs_)], w=["ex"])
            toend, dec, ea = ex[:, 0:16], ex[:, 16:32], ex[:, 32:48]

            psx = pbb[bx_][:, 0:1024].rearrange("p (a b) -> p a b", a=8)

            def trx(e):
                ins = None
                for q in range(8):
                    ins = e.transpose(out=psx[:, q, :], in_=xT[:, q, cs], identity=identB)
                return ins
            P.op("pe", trx, r=[kx, "identB"], w=[("ps", bx_)])
            psB = pbb[bb_][:, 0:256].rearrange("p (a b) -> p a b", a=2)

            def trb(e):
                ins = None
                for g in range(2):
                    ins = e.transpose(out=psB[:, g, :], in_=xT[:, 8 + g, cs], identity=identB)
                return ins
            P.op("pe", trb, r=[kx, "identB"], w=[("ps", bb_)])
            psx3 = pbb[bx_][:, 0:1024].rearrange("p (a b) -> p a b", a=16)
            w2 = sm[:, 64:80]
            P.op("dve", lambda e: e.tensor_tensor(out=w2, in0=dtc, in1=toend, op=ALU.mult), r=["dtc", "ex"], w=["w2"])
            P.op("dve", lambda e: e.tensor_tensor(out=t["xdtw"], in0=psx3, in1=w2.unsqueeze(2).broadcast_to([128, 16, 64]), op=ALU.mult),
                 r=[("ps", bx_), "w2"], w=["xdtw"])
            P.op("act", lambda e: e.activation(out=t["B_tok"], in_=pbb[bb_][:, 0:256], func=AF.Copy), r=[("ps", bb_)], w=["B_tok"])
            if main and SSD_LEVEL == 0:
                return
            if main:
                P.op("dve", lambda e: e.tensor_tensor(out=t["xdt"], in0=psx3, in1=dtc.unsqueeze(2).broadcast_to([128, 16, 64]), op=ALU.mult),
                     r=[("ps", bx_), "dtc"], w=["xdt"])
                P.op("act", lambda e: e.activation(out=t["xs_tok"], in_=psx3, func=AF.Copy), r=[("ps", bx_)], w=["xs_tok"])
                for e16 in range(16 if SSD_LEVEL >= 2 else 0):
                    P.op("dve", lambda e, e16=e16: e.tensor_scalar(out=rhsEr[:, e16 * 128:(e16 + 1) * 128], in0=triF, scalar1=dta[:, e16:e16 + 1],
                                                                   scalar2=None, op0=ALU.mult), r=["dta", KC], w=[("rhsE", e16 // 4)])
                for i in range(4 if SSD_LEVEL >= 2 else 0):
                    P.op("pe", lambda e, i=i: e.matmul(pbf[3 + i], lhsT=U32r, rhs=rhsEr[:, i * 512:(i + 1) * 512], start=True, stop=True),
                         r=[("rhsE", i), "U32"], w=[("ps", 3 + i)])
                    P.op("act", lambda e, i=i: e.activation(out=t["LT"][:, i * 512:(i + 1) * 512], in_=pbf[3 + i], func=AF.Exp),
                         r=[("ps", 3 + i)], w=[("LT", i)])
                psc = pbf[7][:, 0:256].rearrange("p (a b) -> p a b", a=2)
                if SSD_LEVEL < 3:
                    return

                def cbf(e):
                    ins = None
                    for g in range(2):
                        ins = e.matmul(psc[:, g, :], lhsT=xT[:, 8 + g, cs], rhs=xT[:, 10 + g, cs], start=True, stop=True)
                    return ins
                P.op("pe", cbf, r=[kx], w=[("ps", 7)])
                P.op("dve", lambda e: e.tensor_tensor(out=t["cbm"], in0=psc, in1=triF.unsqueeze(1).broadcast_to([128, 2, 128]), op=ALU.mult),
                     r=[("ps", 7), KC], w=["cbm"])
                LT3 = t["LT"].rearrange("p (a b) -> p a b", a=16)
                for g in range(2):
                    P.op("dve", lambda e, g=g: e.tensor_tensor(out=t["MT"][:, g * 8:(g + 1) * 8, :], in0=LT3[:, g * 8:(g + 1) * 8, :],
                                                               in1=t["cbm"][:, g:g + 1, :].broadcast_to([128, 8, 128]), op=ALU.mult),
                         r=[("LT", 2 * g), ("LT", 2 * g + 1), "cbm"], w=[("MT", g)])
                if SSD_LEVEL < 4:
                    return
                for g in range(2):
                    def yd(e, g=g):
                        ins = None
                        for j in range(8):
                            e16 = g * 8 + j
                            ins = e.matmul(pbf[3 + g][:, j * 64:(j + 1) * 64], lhsT=t["MT"][:, e16, :], rhs=t["xdt"][:, e16, :], start=True, stop=True)
                        return ins
                    P.op("pe", yd, r=[("MT", g), "xdt"], w=[("ps", 3 + g)])
                    P.op("pe", lambda e, g=g: e.matmul(pbf[5 + g], lhsT=xT[:, 10 + g, cs], rhs=hTb[:, g * 512:(g + 1) * 512], start=True, stop=True),
                         r=[kx, "hTb"], w=[("ps", 5 + g)])
                y1 = t["y1"]
                for g in range(2):
                    y1g = y1[:, g * 8:(g + 1) * 8, :]
                    P.op("dve", lambda e, g=g, y1g=y1g: e.tensor_tensor(out=y1g, in0=pbf[5 + g].rearrange("p (a b) -> p a b", a=8),
                                                                        in1=ea[:, g * 8:(g + 1) * 8].unsqueeze(2).broadcast_to([128, 8, 64]), op=ALU.mult),
                         r=[("ps", 5 + g), "ex"], w=[("y1", g)])
                    P.op("dve", lambda e, g=g, y1g=y1g: e.tensor_tensor(out=y1g, in0=pbf[3 + g].rearrange("p (a b) -> p a b", a=8), in1=y1g, op=ALU.add),
                         r=[("ps", 3 + g), ("y1", g)], w=[("y1", g)])
                P.op("pool", lambda e: e.tensor_tensor(out=t["t2"], in0=t["xs_tok"], in1=D_bc.unsqueeze(2).broadcast_to([128, 16, 64]), op=ALU.mult),
                     r=["xs_tok", KC], w=["t2"])
                P.op("dve", lambda e: e.tensor_tensor(out=y1, in0=y1, in1=t["t2"], op=ALU.add), r=[("y1", 0), ("y1", 1), "t2"], w=[("y1", 0), ("y1", 1)])
                y1f = y1.rearrange("p a b -> p (a b)")
                P.op("dve", lambda e: e.tensor_tensor(out=y1f, in0=y1f, in1=ztok_c, op=ALU.mult), r=[("y1", 0), ("y1", 1), "z_tok"], w=[("y1", 0), ("y1", 1)])
                if SSD_LEVEL < 5:
                    return
                gn = t["gn"]
                t2f = t["t2"].rearrange("p a b -> p (a b)")
                for g in range(2):
                    P.op("act", lambda e, g=g: e.activation(out=t2f[:, g * 512:(g + 1) * 512], in_=y1f[:, g * 512:(g + 1) * 512], func=AF.Square,
                                                            accum_out=gn[:, g:g + 1]), r=[("y1", g)], w=["t2", ("gn", g)])
                P.op("act", lambda e: e.activation(out=gn[:, 2:4], in_=gn[:, 0:2], func=AF.Ln, scale=1.0 / 512, bias=EPS), r=[("gn", 0), ("gn", 1)], w=["gn2"])
                P.op("act", lambda e: e.activation(out=gn[:, 4:6], in_=gn[:, 2:4], func=AF.Exp, scale=-0.5), r=["gn2"], w=["gn4"])
                for g in range(2):
                    P.op("dve", lambda e, g=g: e.scalar_tensor_tensor(out=t["yb"][:, g * 512:(g + 1) * 512], in0=y1f[:, g * 512:(g + 1) * 512],
                                                                      scalar=gn[:, 4 + g:5 + g], op0=ALU.mult, in1=ssdg_bc[:, g * 512:(g + 1) * 512], op1=ALU.mult),
                         r=[("y1", g), "gn4", "ssdg"], w=[("yb", g)])
                if SSD_LEVEL < 6:
                    return
                psy = pbb[2][:, 0:1024].rearrange("p (a b) -> p a b", a=8)

                def try_(e):
                    ins = None
                    for q in range(8):
                        ins = e.transpose(out=psy[:, q, :], in_=t["yb"][:, q * 128:(q + 1) * 128], identity=identB)
                    return ins
                P.op("pe", try_, r=[("yb", 0), ("yb", 1), "identB"], w=[("ps", 2)])
                P.op("act", lambda e: e.activation(out=YT[:, 8:16, cs], in_=psy, func=AF.Copy), r=[("ps", 2)], w=[("YTb", c)])
            sb = (banks[3], banks[4])
            for g in range(2):
                P.op("pe", lambda e, g=g: e.matmul(pbf[sb[g]], lhsT=t["B_tok"][:, g * 128:(g + 1) * 128], rhs=t["xdtw"].rearrange("p a b -> p (a b)")[:, g * 512:(g + 1) * 512],
                                                   start=True, stop=True), r=["B_tok", "xdtw"], w=[("ps", sb[g])])
            hT3 = hT.rearrange("p (a b) -> p a b", a=16)
            P.op("dve", lambda e: e.tensor_tensor(out=hT3, in0=hT3, in1=dec.unsqueeze(2).broadcast_to([128, 16, 64]), op=ALU.mult), r=["hT", "ex"], w=["hT"])
            for g in range(2):
                P.op("dve", lambda e, g=g: e.tensor_tensor(out=hT[:, g * 512:(g + 1) * 512], in0=pbf[sb[g]], in1=hT[:, g * 512:(g + 1) * 512], op=ALU.add),
                     r=[("ps", sb[g]), "hT"], w=["hT"])
            P.op("act", lambda e: e.activation(out=hTb, in_=hT, func=AF.Copy), r=["hT"], w=["hTb"])

        dbg_ops = []

        def dbg_dump(name, ap, keys):
            if name in dbg_d:
                dbg_ops.append((name, ap, keys))

        def finish():
            P.barrier()
            outs = []
            for name, ap, keys in dbg_ops:
                k = ("dbgout", name)
                P.op("sp", lambda e, name=name, ap=ap: e.dma_start(out=dbg_d[name], in_=ap), r=keys, w=[k], dma=True)
                outs.append(k)
            P.op("sp", None, r=outs + OUTKEYS)
            P.resolve()
            P.emit(sems, block)

        OUTKEYS = []

        sc.reset()
        X = [sc.f32([DM]) for _ in range(2)]
        XN = [sc.bf16([DM]) for _ in range(2)]
        ST = [sc.f32([4]) for _ in range(2)]
        gbc = sc.f32([DM])
        m_common = sc.mark()
        nPT = nT[:, :, 0:TM]
        P.op("sp", lambda e, gbc=gbc: e.dma_start(out=gbc, in_=g_mix.partition_broadcast(128)), w=["gbc"], dma=True)
        for tt in range(8):
            i = tt % 2
            P.op("sp", lambda e, tt=tt, i=i: e.dma_start(out=X[i], in_=xprev[tt * 128:(tt + 1) * 128, :]), w=[("X", i)], dma=True)
            rms_transpose(X[i], ("X", i), 128, XN[i], ("XN", i), ST[i], ("ST", i), gbc, "gbc", nPT, tt * 128, ("nPT", tt), DM, tt)
        Wdt = sc.bf16([16, 16])
        r2.reset()
        xTp = r2.bf16([10, TM])
        dtraw_p = r2.f32([8, 16])
        tP = ssd_temps(r2, main=False)
        rawpad = [sc.f32([TM + 4]) for _ in range(2)]
        accb = [sc.f32([TM]) for _ in range(2)]
        blocks = [(COL_X, 512), (COL_X + 512, 512), (COL_B, 512)]
        load_w(Wdt, wview(w_in, COL_DT, 16), "Wdt")
        wcur = wload(wview(w_in, blocks[0][0], blocks[0][1]))
        for i in range(2):
            P.op("dve", lambda e, rp=rawpad[i]: e.memset(rp[:, 0:3], 0.0), w=[("rawpad", i)])
        jt = 0
        for bi, (c0, ncol) in enumerate(blocks):
            Wv, kW = wcur
            if bi + 1 < len(blocks):
                wcur = wload(wview(w_in, blocks[bi + 1][0], blocks[bi + 1][1]))
            for jj in range(4):
                j = (c0 - COL_X) // 128 + jj
                rp = rawpad[jt % 2]
                krp = ("rawpad", jt % 2)
                isC = j >= 10
                for nb in range(2):
                    if isC and nb == 0:
                        continue
                    bank = 3 + (jt * 2 + nb) % 4
                    mm_group(pbf[bank], [(Wv[:, k, jj * 128:(jj + 1) * 128], nPT[:, k, nb * 512:(nb + 1) * 512]) for k in range(16)],
                             r=[kW] + [("nPT", tt) for tt in range(nb * 4, nb * 4 + 4)], w=[("ps", bank)])
                    P.op("act", lambda e, rp=rp, nb=nb, bank=bank: e.activation(out=rp[:, 3 + nb * 512:3 + (nb + 1) * 512], in_=pbf[bank], func=AF.Copy),
                         r=[("ps", bank)], w=[krp])
                P.op("dve", lambda e, rp=rp, j=j: e.tensor_copy(out=prefix[:, j, :], in_=rp[:, TM:TM + 3]), r=[krp], w=["prefix"])
                if not isC:
                    conv_silu(rp, krp, accb[jt % 2], ("acc", jt % 2), j, TM, xTp[:, j, :], ("xTp",))
                jt += 1
        for tt in range(8):
            mm_group(pbf[0][:, 0:16], [(nPT[:, k, tt * 128:(tt + 1) * 128], Wdt[:, k, :]) for k in range(16)], r=["Wdt", ("nPT", tt)], w=[("ps", 0)])
            P.op("act", lambda e, tt=tt: e.activation(out=dtraw_p[:, tt, :], in_=pbf[0][:, 0:16], func=AF.Copy), r=[("ps", 0)], w=["dtraw_p"])
        P.barrier()

        def pchunks():
            for c in range(8):
                ssd_chunk(tP, xTp, c, dtraw_p[:, c, :], "dtraw_p", main=False, banks=(5, 6, 5, 7, 6), ktag="p")
            P.op("dve", lambda e: e.tensor_scalar(out=hT, in0=hT, scalar1=flag, scalar2=None, op0=ALU.mult), r=["hT", KC], w=["hT"])
            P.op("act", lambda e: e.activation(out=hTb, in_=hT, func=AF.Copy), r=["hT"], w=["hTb"])
        pch_ops = P.capture(pchunks)
        _saved_ops = P.ops
        P.ops = []
        sc.reset(m_common)
        for tt in range(9):
            i = tt % 2
            rows = 128 if tt < 8 else NS
            src = xm[tt * 128:(tt + 1) * 128, :] if tt < 8 else xsmp[:, :]
            P.op("sp", lambda e, i=i, rows=rows, src=src: e.dma_start(out=X[i][:rows, :], in_=src), w=[("X", i)], dma=True)
            rms_transpose(X[i], ("X", i), rows, XN[i], ("XN", i), ST[i], ("ST", i), gbc, "gbc", nT, tt * 128, ("nT", tt), DM, tt)
        P.barrier()
        sc.reset()
        xT = sc.bf16([12, T])
        z_tok = sc.bf16([8, 1024])
        zT_s = sc.f32([8, NS])
        dtraw = sc.f32([8, 16])
        dtT_s = sc.f32([NS])
        ssdg_bc = sc.f32([1024])
        Rsel = sc.f32([1024])
        raw_s = sc.f32([12, NS])
        m_A = sc.mark()
        Wdt = sc.bf16([16, 16])
        rawpad = [sc.f32([TM + 4]) for _ in range(2)]
        accb = [sc.f32([TM]) for _ in range(2)]
        scTok = sc.f32([3 * 1536])
        scT = sc.f32([36, NS])
        acc_s = sc.f32([12, NS])
        P.op("sp", lambda e: e.dma_start(out=ssdg_bc, in_=ssd_g.partition_broadcast(128)), w=["ssdg"], dma=True)
        P.op("sp", lambda e: e.dma_start(out=Rsel, in_=cst2_d[:, C2_RSEL:C2_RSEL + 1024]), w=["cst2"], dma=True)
        P.op("sp", lambda e: e.dma_start(out=scTok[:NS, :], in_=sconv.rearrange("b r c -> b (r c)")), w=["scTok"], dma=True)
        P.op("sp", lambda e: e.dma_start(out=nconv_s[:, 0:2, :], in_=sconv[:, 1:3, :]), w=["o_ncs01"], dma=True)
        OUTKEYS.append("o_ncs01")
        for half in range(2):
            def trs(e, half=half):
                ins = None
                for idx in range(half * 18, half * 18 + 18):
                    r_, j_ = idx // 12, idx % 12
                    ii = idx - half * 18
                    ins = e.transpose(out=pbf[half][:, ii * NS:(ii + 1) * NS], in_=scTok[:NS, r_ * 1536 + j_ * 128:r_ * 1536 + (j_ + 1) * 128],
                                      identity=identF[:NS, :NS])
                return ins
            P.op("pe", trs, r=["scTok", KC], w=[("ps", half)])
            P.op("dve", lambda e, half=half: e.tensor_copy(out=scT[:, half * 18:half * 18 + 18, :].rearrange("p a b -> p (a b)"), in_=pbf[half][:, 0:18 * NS]),
                 r=[("ps", half)], w=["scT"])
        blocksA = [(COL_Z, "z"), (COL_Z + 512, "z"), (COL_X, "x"), (COL_X + 512, "x"), (COL_B, "x")]
        load_w(Wdt, wview(w_in, COL_DT, 16), "Wdt")
        wcur = wload(wview(w_in, blocksA[0][0], 512))
        jt = 0
        grp = 0
        for bi, (c0, kind) in enumerate(blocksA):
            Wv, kW = wcur
            if bi + 1 < len(blocksA):
                wcur = wload(wview(w_in, blocksA[bi + 1][0], 512))
            if kind == "z":
                zb = (c0 - COL_Z) // 512
                for tt in range(8):
                    bank = 1 + grp % 4
                    grp += 1
                    mm_group(pbf[bank], [(nT[:, k, tt * 128:(tt + 1) * 128], Wv[:, k, :]) for k in range(16)], r=[kW, ("nT", tt)], w=[("ps", bank)])
                    P.op("act", lambda e, tt=tt, zb=zb, bank=bank: e.activation(out=z_tok[:, tt, zb * 512:(zb + 1) * 512], in_=pbf[bank], func=AF.Silu),
                         r=[("ps", bank)], w=["z_tok"])
                for jj in range(4):
                    bank = 1 + grp % 4
                    grp += 1
                    j = zb * 4 + jj
                    mm_group(pbf[bank][:, 0:NS], [(Wv[:, k, jj * 128:(jj + 1) * 128], nT[:, k, TM:T]) for k in range(16)], r=[kW, ("nT", 8)], w=[("ps", bank)])
                    P.op("act", lambda e, j=j, bank=bank: e.activation(out=zT_s[:, j, :], in_=pbf[bank][:, 0:NS], func=AF.Silu), r=[("ps", bank)], w=["zT_s"])
            else:
                for jj in range(4):
                    j = (c0 - COL_X) // 128 + jj
                    rp = rawpad[jt % 2]
                    krp = ("rawpad", jt % 2)
                    P.op("dve", lambda e, rp=rp, j=j: e.tensor_copy(out=rp[:, 0:3], in_=prefix[:, j, :]), r=["prefix"], w=[krp])
                    for nb in range(3):
                        bank = 1 + grp % 4
                        grp += 1
                        lo, hi = (nb * 512, (nb + 1) * 512) if nb < 2 else (TM, T)
                        n = hi - lo
                        rk = [("nT", tt) for tt in range(nb * 4, nb * 4 + 4)] if nb < 2 else [("nT", 8)]
                        mm_group(pbf[bank][:, 0:n], [(Wv[:, k, jj * 128:(jj + 1) * 128], nT[:, k, lo:hi]) for k in range(16)], r=[kW] + rk, w=[("ps", bank)])
                        if nb < 2:
                            P.op("act", lambda e, rp=rp, lo=lo, hi=hi, bank=bank: e.activation(out=rp[:, 3 + lo:3 + hi], in_=pbf[bank], func=AF.Copy),
                                 r=[("ps", bank)], w=[krp])
                        else:
                            P.op("act", lambda e, j=j, bank=bank: e.activation(out=raw_s[:, j, :], in_=pbf[bank][:, 0:NS], func=AF.Copy),
                                 r=[("ps", bank)], w=["raw_s"])
                    P.op("dve", lambda e, rp=rp, j=j: e.tensor_copy(out=ncv[:, j, :], in_=rp[:, TM:TM + 3]), r=[krp], w=["ncv"])
                    conv_silu(rp, krp, accb[jt % 2], ("acc", jt % 2), j, TM, xT[:, j, 0:TM], ("xT",))
                    P.op("dve", lambda e, j=j: e.tensor_scalar(out=acc_s[:, j, :], in0=raw_s[:, j, :], scalar1=cwb[:, j, 3:4], scalar2=cwb[:, j, 4:5],
                                                               op0=ALU.mult, op1=ALU.add), r=["raw_s", KC], w=["acc_s"])
                    for r_ in range(3):
                        P.op("dve", lambda e, j=j, r_=r_: e.scalar_tensor_tensor(out=acc_s[:, j, :], in0=scT[:, r_ * 12 + j, :], scalar=cwb[:, j, r_:r_ + 1],
                                                                                 op0=ALU.mult, in1=acc_s[:, j, :], op1=ALU.add), r=["scT", "acc_s", KC], w=["acc_s"])
                    jt += 1
        P.op("act", lambda e: e.activation(out=xT[:, :, TM:T], in_=acc_s, func=AF.Silu), r=["acc_s"], w=[("xT",)])
        for tt in range(8):
            mm_group(pbf[0][:, 0:16], [(nT[:, k, tt * 128:(tt + 1) * 128], Wdt[:, k, :]) for k in range(16)], r=["Wdt", ("nT", tt)], w=[("ps", 0)])
            P.op("act", lambda e, tt=tt: e.activation(out=dtraw[:, tt, :], in_=pbf[0][:, 0:16], func=AF.Copy), r=[("ps", 0)], w=["dtraw"])
        mm_group(pbf[1][:NS, 0:NS], [(Wdt[:, k, :], nT[:, k, TM:T]) for k in range(16)], r=["Wdt", ("nT", 8)], w=[("ps", 1)])
        P.op("act", lambda e: e.activation(out=dtT_s[:NS, :], in_=pbf[1][:NS, 0:NS], func=AF.Copy), r=[("ps", 1)], w=["dtT_s"])

        def rows_out(src3, ksrc, R, stage, kst, dram_ap, okey):
            for b3 in range(3):
                def trr(e, b3=b3):
                    ins = None
                    for jj in range(4):
                        j_ = b3 * 4 + jj
                        ins = e.transpose(out=pbf[2 + b3][:R, jj * 128:(jj + 1) * 128], in_=src3[:, j_, :], identity=identF)
                    return ins
                P.op("pe", trr, r=[ksrc, KC], w=[("ps", 2 + b3)])
                P.op("dve", lambda e, b3=b3: e.tensor_copy(out=stage[:R, b3 * 512:(b3 + 1) * 512], in_=pbf[2 + b3][:R, :]), r=[("ps", 2 + b3)], w=[kst, "scTok"])
            P.op("sp", lambda e: e.dma_start(out=dram_ap, in_=stage[:R, 0:1536]), r=[kst], w=[okey], dma=True)
            OUTKEYS.append(okey)
        rows_out(raw_s, "raw_s", NS, scTok[:, 0:1536], "stg0", nconv_s[:, 2, :], "o_ncs2")
        rows_out(ncv, "ncv", 3, scTok[:, 1536:3072], "stg1", nconv_p[:, :], "o_ncp")
        _main_ops = P.ops
        P.ops = _saved_ops
        P.ops.extend(Prog.merge(_main_ops, pch_ops))
        P.barrier()
        if STOP_AFTER == "A1":
            dbg_dump("xT", xT.rearrange("p a b -> p (a b)"), [("xT",)])
            dbg_dump("z_tok", z_tok.rearrange("p a b -> p (a b)"), ["z_tok"])
            dbg_dump("zT_s", zT_s.rearrange("p a b -> p (a b)"), ["zT_s"])
            dbg_dump("dtraw", dtraw.rearrange("p a b -> p (a b)"), ["dtraw"])
            dbg_dump("dtT_s", dtT_s, ["dtT_s"])
            finish()
            return nc

        sc.reset(m_A)
        tA = ssd_temps(sc)
        for c in range(8):
            ssd_chunk(tA, xT, c, dtraw[:, c, :], "dtraw", main=True, ztok_c=z_tok[:, c, :], ssdg_bc=ssdg_bc)
        if STOP_AFTER == "A2a":
            P.barrier()
            dbg_dump("YT", YT.rearrange("p a b -> p (a b)"), [("YTb", c) for c in range(8)])
            finish()
            return nc
        hout = sc.f32([8, 128])
        for half in range(2):
            def trh(e, half=half):
                ins = None
                for jj in range(4):
                    q = half * 4 + jj
                    ins = e.transpose(out=pbf[3 + half][:, jj * 128:(jj + 1) * 128], in_=hT[:, q * 128:(q + 1) * 128], identity=identF)
                return ins
            P.op("pe", trh, r=["hT", KC], w=[("ps", 3 + half)])
            P.op("dve", lambda e, half=half: e.tensor_copy(out=hout[:, half * 4:(half + 1) * 4, :].rearrange("p a b -> p (a b)"), in_=pbf[3 + half]),
                 r=[("ps", 3 + half)], w=["hout"])
        P.op("sp", lambda e: e.dma_start(out=nssm_p.rearrange("(q r) n -> r q n", r=128), in_=hout), r=["hout"], w=["o_nsp"], dma=True)
        OUTKEYS.append("o_nsp")

        if STOP_AFTER == "A2b":
            P.barrier()
            dbg_dump("YT", YT.rearrange("p a b -> p (a b)"), [("YTb", c) for c in range(8)])
            finish()
            return nc
        P.barrier()
        sc.reset(m_A)
        sm_s = sc.f32([8, NS])
        cat = sc.f32([32])
        rep = sc.f32([8, 32])
        dtx = sc.f32([8, NS])
        ys = sc.f32([8, NS])
        ysq = sc.f32([8, NS])
        rs = sc.f32([2, NS])
        BC_tok = sc.bf16([512])
        selB = sc.bf16([NS, 128])
        NHS = 4
        hs = [sc.f32([8, 128]) for _ in range(NHS)]
        junk = sc.f32([128])
        x0 = sm_s[:NS, 0, :]; e1 = sm_s[:NS, 1, :]; anp = sm_s[:NS, 2, 0:1]
        P.op("act", lambda e: e.activation(out=e1, in_=dtT_s[:NS, :], func=AF.Exp, bias=dtb_pp[:NS, :]), r=["dtT_s", KC], w=["s_e1"])
        P.op("act", lambda e: e.activation(out=cat[:NS, 16:32], in_=e1, func=AF.Ln, bias=1.0), r=["s_e1"], w=["s_dt"])
        P.op("act", lambda e: e.activation(out=anp, in_=al_pp[:NS, :], func=AF.Exp), r=[KC], w=["s_anp0"])
        P.op("dve", lambda e: e.tensor_scalar(out=anp, in0=anp, scalar1=-1.0, scalar2=None, op0=ALU.mult), r=["s_anp0"], w=["s_anp"])
        P.op("act", lambda e: e.activation(out=cat[:NS, 0:16], in_=cat[:NS, 16:32], func=AF.Exp, scale=anp), r=["s_dt", "s_anp"], w=["s_dA"])

        def repf(e):
            ins = None
            for q in range(8):
                ins = e.matmul(pbf[0][:, q * 32:(q + 1) * 32], lhsT=Rsel[:NS, q * 128:(q + 1) * 128], rhs=cat[:NS, :], start=True, stop=True)
            return ins
        P.op("pe", repf, r=["s_dA", "s_dt", "cst2"], w=[("ps", 0)])
        P.op("dve", lambda e: e.tensor_copy(out=rep.rearrange("p a b -> p (a b)"), in_=pbf[0][:, 0:256]), r=[("ps", 0)], w=["rep"])
        xs_s = xT[:, 0:8, TM:T]
        P.op("dve", lambda e: e.tensor_tensor(out=dtx, in0=rep[:, :, 16:32], in1=xs_s, op=ALU.mult), r=["rep", ("xT",)], w=["dtx"])
        psbc = pbb[1][:NS, 0:512].rearrange("p (a b) -> p a b", a=4)

        def trbc(e):
            ins = None
            for jj in range(4):
                ins = e.transpose(out=psbc[:, jj, :], in_=xT[:, 8 + jj, TM:T], identity=identB)
            return ins
        P.op("pe", trbc, r=[("xT",), "identB"], w=[("ps", 1)])
        P.op("act", lambda e: e.activation(out=BC_tok[:NS, :], in_=pbb[1][:NS, 0:512], func=AF.Copy), r=[("ps", 1)], w=["BC_tok"])
        P.op("dve", lambda e: e.tensor_copy(out=selB[:NS, :, :], in_=identB[:NS, 0:NS].unsqueeze(2).broadcast_to([NS, NS, 128])), r=["identB"], w=["selB"])
        P.op("dve", lambda e: e.memset(ys, 0.0), w=["ys"])

        def hs_load(b):
            P.op("sp", lambda e, b=b: e.dma_start(out=hs[b % NHS], in_=sssm[b].rearrange("(q r) n -> r q n", r=128)), w=[("hs", b % NHS, q) for q in range(8)], dma=True)
        for b in range(min(NHS - 1, NS)):
            hs_load(b)
        for b in range(NS):
            hb = hs[b % NHS]
            bank = 2 + b % 2
            if b + NHS - 1 < NS:
                hs_load(b + NHS - 1)
            P.op("pe", lambda e, b=b, bank=bank: e.matmul(pbf[bank], lhsT=selB[:NS, b, :], rhs=BC_tok[:NS, :], start=True, stop=True),
                 r=["selB", "BC_tok"], w=[("ps", bank)])
            for q in range(8):
                P.op("act", lambda e, b=b, q=q, hb=hb: e.activation(out=hb[:, q, :], in_=hb[:, q, :], func=AF.Copy, scale=rep[:, q, b:b + 1]),
                     r=[("hs", b % NHS, q), "rep"], w=[("hs", b % NHS, q)])

            def upd(q, b=b, hb=hb, bank=bank):
                g = q // 4
                P.op("dve", lambda e: e.scalar_tensor_tensor(out=hb[:, q, :], in0=pbf[bank][:, g * 128:(g + 1) * 128], scalar=dtx[:, q, b:b + 1], op0=ALU.mult,
                                                             in1=hb[:, q, :], op1=ALU.add), r=[("ps", bank), "dtx", ("hs", b % NHS, q)], w=[("hs", b % NHS, q)])

            def yacc(q, b=b, hb=hb, bank=bank):
                g = q // 4
                P.op("dve", lambda e: e.scalar_tensor_tensor(out=junk, in0=hb[:, q, :], scalar=1.0, op0=ALU.mult,
                                                             in1=pbf[bank][:, 256 + g * 128:256 + (g + 1) * 128], op1=ALU.mult,
                                                             accum_out=ys[:, q, b:b + 1]), r=[("hs", b % NHS, q), ("ps", bank)], w=["ys", "junk"])
            upd(0)
            upd(1)
            for q in range(8):
                yacc(q)
                if q + 2 < 8:
                    upd(q + 2)
            ok = ("o_nss", b)
            P.op("sp", lambda e, b=b, hb=hb: e.dma_start(out=nssm_s[b].rearrange("(q r) n -> r q n", r=128), in_=hb), r=[("hs", b % NHS, q) for q in range(8)], w=[ok], dma=True)
            OUTKEYS.append(ok)
        P.op("dve", lambda e: e.tensor_tensor(out=ysq, in0=xs_s, in1=D_pp.unsqueeze(2).broadcast_to([128, 8, NS]), op=ALU.mult), r=[("xT",), KC], w=["ysq"])
        P.op("dve", lambda e: e.tensor_tensor(out=ys, in0=ys, in1=ysq, op=ALU.add), r=["ys", "ysq"], w=["ys"])
        P.op("dve", lambda e: e.tensor_tensor(out=ys, in0=ys, in1=zT_s, op=ALU.mult), r=["ys", "zT_s"], w=["ys"])
        P.op("dve", lambda e: e.tensor_tensor(out=ysq, in0=ys, in1=ys, op=ALU.mult), r=["ys", "ysq"], w=["ysq"])

        def gsum(e):
            ins = None
            for q in range(8):
                g = q // 4
                ins = e.matmul(pbf[4][:, g * NS:(g + 1) * NS], lhsT=onesF, rhs=ysq[:, q, :], start=(q % 4 == 0), stop=(q % 4 == 3))
            return ins
        P.op("pe", gsum, r=["ysq", KC], w=[("ps", 4)])
        rsf = rs.rearrange("p a b -> p (a b)")
        P.op("act", lambda e: e.activation(out=rsf, in_=pbf[4][:, 0:2 * NS], func=AF.Ln, scale=1.0 / 512, bias=EPS), r=[("ps", 4)], w=["rs0"])
        P.op("act", lambda e: e.activation(out=rsf, in_=rsf, func=AF.Exp, scale=-0.5), r=["rs0"], w=["rs"])
        for g in range(2):
            P.op("dve", lambda e, g=g: e.tensor_tensor(out=ys[:, g * 4:(g + 1) * 4, :], in0=ys[:, g * 4:(g + 1) * 4, :],
                                                       in1=rs[:, g:g + 1, :].broadcast_to([128, 4, NS]), op=ALU.mult), r=["ys", "rs"], w=["ys"])
        P.op("dve", lambda e: e.tensor_tensor(out=YT[:, 8:16, TM:T], in0=ys, in1=sg_pp.unsqueeze(2).broadcast_to([128, 8, NS]), op=ALU.mult),
             r=["ys", KC], w=[("YTb", 8)])
        P.barrier()
        if STOP_AFTER == "A2":
            dbg_dump("YT", YT.rearrange("p a b -> p (a b)"), [("YTb", c) for c in range(9)])
            finish()
            return nc
        sc.reset()
        vng_bc = sc.f32([1024])
        bs_bc = sc.f32([8, 128])
        v_tok = sc.bf16([9, 1024])
        ya = sc.f32([8, T])
        WsT = sc.bf16([8, 128])
        wsraw = ya[:, 0, 0:1024].rearrange("p (a b) -> p a b", a=8)
        gv = [sc.f32([1024]) for _ in range(2)]
        vst = [sc.f32([4]) for _ in range(2)]
        ug = [sc.f32([T]) for _ in range(2)]
        _m_tm = sc.mark()
        tmpm = [sc.f32([512]) for _ in range(2)]
        _m_tm2 = sc.mark()
        sc.reset(_m_tm)
        vsf = sc.f32([1024])
        sc.reset(_m_tm2)
        sq = [sc.bf16([T]) for _ in range(2)]
        rstd_bc = sc.f32([T])
        w00I = sc.bf16([8, NS])
        P.op("sp", lambda e: e.dma_start(out=vng_bc, in_=vn_g.partition_broadcast(128)), w=["vng"], dma=True)
        P.op("sp", lambda e: e.dma_start(out=bs_bc.rearrange("p a b -> p (a b)"), in_=cst2_d[:, C2_BS:C2_BS + 1024]), w=["bs_bc"], dma=True)
        P.op("sp", lambda e: e.dma_start(out=wsraw, in_=ws_d.rearrange("h t s -> t h s")), w=["wsraw"], dma=True)
        wv2 = [wload(wview(w_in, COL_V, 512)), wload(wview(w_in, COL_V + 512, 512))]
        for half in range(2):
            def trw(e, half=half):
                ins = None
                for jj in range(4):
                    h_ = half * 4 + jj
                    ins = e.transpose(out=pbf[half][:, jj * 128:(jj + 1) * 128], in_=wsraw[:, h_, :], identity=identF)
                return ins
            P.op("pe", trw, r=["wsraw", KC], w=[("ps", half)])
            P.op("dve", lambda e, half=half: e.tensor_tensor(out=WsT[:, half * 4:(half + 1) * 4, :], in0=pbf[half].rearrange("p (a b) -> p a b", a=4),
                                                             in1=triF.unsqueeze(1).broadcast_to([128, 4, 128]), op=ALU.mult), r=[("ps", half), KC], w=["WsT"])
        for h_ in range(8):
            P.op("dve", lambda e, h_=h_: e.tensor_scalar(out=w00I[:NS, h_, :], in0=identF[:NS, 0:NS], scalar1=w00_bc[:NS, h_:h_ + 1], scalar2=None, op0=ALU.mult),
                 r=[KC], w=["w00I"])
        for tt in range(9):
            rows = 128 if tt < 8 else NS
            i = tt % 2
            tcols = slice(tt * 128, tt * 128 + rows)
            for zb in range(2):
                bank = 2 + (tt * 2 + zb) % 4
                mm_group(pbf[bank][:rows, :], [(nT[:, k, tcols], wv2[zb][0][:, k, :]) for k in range(16)], r=[wv2[zb][1], ("nT", tt)], w=[("ps", bank)])
                P.op("act", lambda e, i=i, zb=zb, bank=bank, rows=rows: e.activation(out=gv[i][:rows, zb * 512:(zb + 1) * 512], in_=pbf[bank][:rows, :],
                                                                                    func=AF.Gelu_apprx_tanh), r=[("ps", bank)], w=[("gv", i, zb)])
            P.op("act", lambda e, i=i, rows=rows: e.activation(out=ug[i][:rows, 0:1024], in_=gv[i][:rows, :], func=AF.Square, accum_out=vst[i][:rows, 0:1]),
                 r=[("gv", i, 0), ("gv", i, 1)], w=[("ug", i), ("vst", i)])
            P.op("act", lambda e, i=i, rows=rows: e.activation(out=vst[i][:rows, 1:2], in_=vst[i][:rows, 0:1], func=AF.Ln, scale=1.0 / 1024, bias=EPS),
                 r=[("vst", i)], w=[("vst", i)])
            P.op("act", lambda e, i=i, rows=rows: e.activation(out=vst[i][:rows, 2:3], in_=vst[i][:rows, 1:2], func=AF.Exp, scale=-0.5),
                 r=[("vst", i)], w=[("vst", i)])
            P.op("dve", lambda e, i=i, rows=rows, tt=tt: e.scalar_tensor_tensor(out=v_tok[:rows, tt, :], in0=gv[i][:rows, :], scalar=vst[i][:rows, 2:3], op0=ALU.mult,
                                                                                in1=vng_bc[:rows, :], op1=ALU.mult),
                 r=[("gv", i, 0), ("gv", i, 1), ("vst", i), "vng"], w=[("v_tok", tt)])
            if tt == 8:
                P.op("dve", lambda e, i=i: e.scalar_tensor_tensor(out=vsf[:NS, :], in0=gv[i][:NS, :], scalar=vst[i][:NS, 2:3], op0=ALU.mult,
                                                                  in1=vng_bc[:NS, :], op1=ALU.mult), r=[("gv", i, 0), ("gv", i, 1), ("vst", i), "vng"], w=["vsf", ("tmpm", 0), ("tmpm", 1)])
                P.op("sp", lambda e: e.dma_start(out=v_s[:, :], in_=vsf[:NS, :]), r=["vsf", ("tmpm", 0), ("tmpm", 1)], w=["o_vs"], dma=True)
                OUTKEYS.append("o_vs")
        wu2 = [wload(wview(w_in, COL_U, 512)), wload(wview(w_in, COL_U + 512, 512))]
        tok_blocks = [(0, 512), (512, 1024), (TM, T)]
        grp = 0
        for h_ in range(8):
            ub, jj = h_ // 4, h_ % 4
            Wv, kWu = wu2[ub]
            ui = h_ % 2
            for nb, (lo, hi) in enumerate(tok_blocks):
                bank = grp % 2
                grp += 1
                n = hi - lo
                rk = [("nT", tt) for tt in range(nb * 4, nb * 4 + 4)] if nb < 2 else [("nT", 8)]
                mm_group(pbf[bank][:, 0:n], [(Wv[:, k, jj * 128:(jj + 1) * 128], nT[:, k, lo:hi]) for k in range(16)], r=[kWu] + rk, w=[("ps", bank)])
                P.op("act", lambda e, ui=ui, lo=lo, hi=hi, n=n, bank=bank: e.activation(out=ug[ui][:, lo:hi], in_=pbf[bank][:, 0:n], func=AF.Gelu_apprx_tanh),
                     r=[("ps", bank)], w=[("ug", ui)])
            for half in range(2):
                def mix(e, half=half, h_=h_):
                    ins = None
                    for cc in range(4):
                        c = half * 4 + cc
                        ins = e.matmul(pbf[2 + half][:, cc * 128:(cc + 1) * 128], lhsT=v_tok[:, c, h_ * 128:(h_ + 1) * 128], rhs=WsT[:, h_, :], start=True, stop=True)
                    return ins
                P.op("pe", mix, r=[("v_tok", c) for c in range(half * 4, half * 4 + 4)] + ["WsT"], w=[("ps", 2 + half)])
                tm = tmpm[half]
                P.op("dve", lambda e, half=half, h_=h_, tm=tm: e.tensor_tensor(out=tm.rearrange("p (a b) -> p a b", a=4), in0=pbf[2 + half].rearrange("p (a b) -> p a b", a=4),
                                                                               in1=bs_bc[:, h_:h_ + 1, :].broadcast_to([128, 4, 128]), op=ALU.add),
                     r=[("ps", 2 + half), "bs_bc"], w=[("tmpm", half)])
                P.op("dve", lambda e, half=half, h_=h_, tm=tm, ui=ui: e.tensor_tensor(out=ya[:, h_, half * 512:(half + 1) * 512], in0=tm, in1=ug[ui][:, half * 512:(half + 1) * 512], op=ALU.mult),
                     r=[("tmpm", half), ("ug", ui)], w=[("ya", h_)])
            P.op("pe", lambda e, h_=h_: e.matmul(pbf[4][:, 0:NS], lhsT=v_tok[:NS, 8, h_ * 128:(h_ + 1) * 128], rhs=w00I[:NS, h_, :], start=True, stop=True),
                 r=[("v_tok", 8), "w00I"], w=[("ps", 4)])
            P.op("dve", lambda e, h_=h_, ui=ui: e.scalar_tensor_tensor(out=ya[:, h_, TM:T], in0=pbf[4][:, 0:NS], scalar=b0_bc[:, h_:h_ + 1], op0=ALU.add, in1=ug[ui][:, TM:T], op1=ALU.mult),
                 r=[("ps", 4), ("ug", ui), KC], w=[("ya", h_)])
            si = h_ % 2
            P.op("act", lambda e, h_=h_, si=si: e.activation(out=sq[si], in_=ya[:, h_, :], func=AF.Square), r=[("ya", h_)], w=[("sq", si)])
            for nb, (lo, hi) in enumerate(tok_blocks):
                n = hi - lo
                P.op("pe", lambda e, nb=nb, lo=lo, hi=hi, n=n, si=si, h_=h_: e.matmul(pbf[5 + nb][:, 0:n], lhsT=onesB, rhs=sq[si][:, lo:hi], start=(h_ == 0), stop=(h_ == 7)),
                     r=[("sq", si), "onesB"], w=[("ps", 5 + nb)])
        for nb, (lo, hi) in enumerate(tok_blocks):
            n = hi - lo
            P.op("act", lambda e, nb=nb, lo=lo, hi=hi, n=n: e.activation(out=rstd_bc[:, lo:hi], in_=pbf[5 + nb][:, 0:n], func=AF.Ln, scale=1.0 / 1024, bias=EPS),
                 r=[("ps", 5 + nb)], w=[("rstd", nb)])
            P.op("act", lambda e, lo=lo, hi=hi: e.activation(out=rstd_bc[:, lo:hi], in_=rstd_bc[:, lo:hi], func=AF.Exp, scale=-0.5), r=[("rstd", nb)], w=[("rstd", nb)])
        for h_ in range(8):
            P.op("dve", lambda e, h_=h_: e.scalar_tensor_tensor(out=YT[:, h_, :], in0=ya[:, h_, :], scalar=gon_pp[:, h_:h_ + 1], op0=ALU.mult, in1=rstd_bc, op1=ALU.mult),
                 r=[("ya", h_), ("rstd", 0), ("rstd", 1), ("rstd", 2), KC], w=[("YTa", h_)])
        P.barrier()
        if STOP_AFTER == "A3":
            dbg_dump("YT", YT.rearrange("p a b -> p (a b)"), [("YTa", h_) for h_ in range(8)])
            finish()
            return nc

        sc.reset()
        hres = sc.f32([9, DM])
        m_B = sc.mark()
        for tt in range(9):
            rows = 128 if tt < 8 else NS
            src = xm[tt * 128:(tt + 1) * 128, :] if tt < 8 else xsmp[:, :]
            P.op("sp", lambda e, tt=tt, rows=rows, src=src: e.dma_start(out=hres[:rows, tt, :], in_=src), w=[("h", tt, cb) for cb in range(4)], dma=True)
        wcur = wload(wview(w_out, 0, 512))
        for cb in range(4):
            Wv, kW = wcur
            if cb + 1 < 4:
                wcur = wload(wview(w_out, (cb + 1) * 512, 512))
            for tt in range(9):
                rows = 128 if tt < 8 else NS
                tcols = slice(tt * 128, tt * 128 + rows)
                bank = (cb * 9 + tt) % 6
                mm_group(pbf[bank][:rows, :], [(YT[:, k, tcols], Wv[:, k, :]) for k in range(16)], r=[kW], w=[("ps", bank)])
                P.op("dve", lambda e, rows=rows, tt=tt, cb=cb, bank=bank: e.tensor_tensor(out=hres[:rows, tt, cb * 512:(cb + 1) * 512], in0=pbf[bank][:rows, :],
                                                                                         in1=hres[:rows, tt, cb * 512:(cb + 1) * 512], op=ALU.add),
                     r=[("ps", bank), ("h", tt, cb)], w=[("h", tt, cb)])
        P.barrier()
        if STOP_AFTER == "B":
            dbg_dump("h", hres.rearrange("p a b -> p (a b)"), [("h", tt) for tt in range(9)])
            finish()
            return nc

        sc.reset(m_B)
        gbc = sc.f32([DM])
        gbc2 = sc.f32([DM])
        stmp = [sc.f32([512]) for _ in range(2)]
        ST = [sc.f32([4]) for _ in range(2)]
        r2.reset()
        actb = [r2.bf16([2, T]) for _ in range(2)]
        Wd = [r2.bf16([2, DM]) for _ in range(2)]
        XN = [r2.bf16([DM]) for _ in range(2)]
        mT = nT
        P.op("sp", lambda e: e.dma_start(out=gbc, in_=g_ffn.partition_broadcast(128)), w=["gbc"], dma=True)
        P.op("sp", lambda e: e.dma_start(out=gbc2, in_=g_fin.partition_broadcast(128)), w=["gbc2"], dma=True)
        for tt in range(9):
            rows = 128 if tt < 8 else NS
            i = tt % 2
            rms_transpose(hres[:, tt, :], ("h", tt), rows, XN[i], ("XN", i), ST[i], ("ST", i), gbc, "gbc", mT, tt * 128, ("mT", tt), DM, tt)
        NFB = DFF // 256
        if STOP_AFTER == "C0":
            P.barrier()
            dbg_dump("mT", mT.rearrange("p a b -> p (a b)"), [("mT", tt) for tt in range(9)])
            dbg_dump("XN0", XN[0], [("XN", 0)])
            dbg_dump("ST0", ST[0], [("ST", 0)])
            dbg_dump("gbc", gbc, ["gbc"])
            dbg_dump("h", hres.rearrange("p a b -> p (a b)"), [("h", tt) for tt in range(9)])
            finish()
            return nc

        gu_slot = {}

        def load_gu(fb):
            slot = wq[0] % 2
            wq[0] += 1
            gu_slot[fb] = slot
            load_w(WG[slot], wview(w_gate, fb * 256, 256), ("W", slot), nobar=True)
            load_w(WU[slot], wview(w_up, fb * 256, 256), ("W", slot), nobar=True)

        def load_d(fb):
            load_w(Wd[fb % 2], w_down[fb * 256:(fb + 1) * 256, :].rearrange("(fl p) c -> p fl c", p=128), ("Wd", fb % 2))

        gctr = [0]

        def GU_units(fb):
            slot = gu_slot[fb]
            Wg_, Wu_, kWs = WG[slot], WU[slot], ("W", slot)
            units = []
            for fl in range(2):
                for nb, (lo, hi) in enumerate(tok_blocks):
                    def unit(fl=fl, nb=nb, lo=lo, hi=hi):
                        n = hi - lo
                        pr = gctr[0] % 2
                        gctr[0] += 1
                        bg, bu = pr * 2, pr * 2 + 1
                        rk = [("mT", tt) for tt in range(nb * 4, nb * 4 + 4)] if nb < 2 else [("mT", 8)]
                        mm_group(pbf[bg][:, 0:n], [(Wg_[:, k, fl * 128:(fl + 1) * 128], mT[:, k, lo:hi]) for k in range(16)], r=[kWs] + rk, w=[("ps", bg)])
                        mm_group(pbf[bu][:, 0:n], [(Wu_[:, k, fl * 128:(fl + 1) * 128], mT[:, k, lo:hi]) for k in range(16)], r=[kWs] + rk, w=[("ps", bu)])
                        P.op("act", lambda e: e.activation(out=stmp[pr][:, 0:n], in_=pbf[bg][:, 0:n], func=AF.Silu), r=[("ps", bg)], w=[("stmp", pr)])
                        P.op("dve", lambda e: e.tensor_tensor(out=actb[fb % 2][:, fl, lo:hi], in0=pbf[bu][:, 0:n], in1=stmp[pr][:, 0:n], op=ALU.mult),
                             r=[("ps", bu), ("stmp", pr)], w=[("act", fb % 2)])
                    units.append(P.capture(unit))
            return units

        dctr = [0]

        def DN_groups(fb):
            groups = []
            for tt in range(9):
                rows = 128 if tt < 8 else NS
                tcols = slice(tt * 128, tt * 128 + rows)
                for cb in range(4):
                    def grp_(tt=tt, rows=rows, tcols=tcols, cb=cb):
                        bank = 4 + dctr[0] % 4
                        dctr[0] += 1
                        mm_group(pbf[bank][:rows, :], [(actb[fb % 2][:, fl, tcols], Wd[fb % 2][:, fl, cb * 512:(cb + 1) * 512]) for fl in range(2)],
                                 r=[("act", fb % 2), ("Wd", fb % 2)], w=[("ps", bank)])
                        P.op("dve", lambda e: e.tensor_tensor(out=hres[:rows, tt, cb * 512:(cb + 1) * 512], in0=pbf[bank][:rows, :],
                                                              in1=hres[:rows, tt, cb * 512:(cb + 1) * 512], op=ALU.add),
                             r=[("ps", bank), ("h", tt, cb)], w=[("h", tt, cb)])
                    groups.append(P.capture(grp_))
            return groups

        load_gu(0)
        load_d(0)
        for i in range(NFB + 1):
            if i + 1 < NFB:
                load_gu(i + 1)
            gu = GU_units(i) if i < NFB else []
            dn = DN_groups(i - 1) if i >= 1 else []
            nslot = max(len(gu), 1)
            per = -(-len(dn) // nslot)
            for u in range(nslot):
                if u < len(gu):
                    P.ops.extend(gu[u])
                for g_ in dn[u * per:(u + 1) * per]:
                    P.ops.extend(g_)
            if i + 1 < NFB:
                load_d(i + 1)
        P.barrier()
        if STOP_AFTER == "C":
            dbg_dump("h", hres.rearrange("p a b -> p (a b)"), [("h", tt) for tt in range(9)])
            finish()
            return nc

        for tt in range(9):
            rows = 128 if tt < 8 else NS
            i = tt % 2
            ht = hres[:, tt, :]
            st = ST[i]
            P.op("act", lambda e, rows=rows, ht=ht, st=st, i=i: e.activation(out=XN[i][:rows, :], in_=ht[:rows, :], func=AF.Square, accum_out=st[:rows, 0:1]),
                 r=[("h", tt, cb) for cb in range(4)], w=[("XN", i), ("ST", i)])
            P.op("act", lambda e, rows=rows, st=st: e.activation(out=st[:rows, 1:2], in_=st[:rows, 0:1], func=AF.Ln, scale=1.0 / DM, bias=EPS), r=[("ST", i)], w=[("ST", i)])
            P.op("act", lambda e, rows=rows, st=st: e.activation(out=st[:rows, 2:3], in_=st[:rows, 1:2], func=AF.Exp, scale=-0.5), r=[("ST", i)], w=[("ST", i)])
            P.op("dve", lambda e, rows=rows, ht=ht, st=st: e.scalar_tensor_tensor(out=ht[:rows, :], in0=ht[:rows, :], scalar=st[:rows, 2:3], op0=ALU.mult, in1=gbc2[:rows, :], op1=ALU.mult),
                 r=[("h", tt, cb) for cb in range(4)] + [("ST", i), "gbc2"], w=[("h", tt, cb) for cb in range(4)])
            dst = y_m[tt * 128:(tt + 1) * 128, :] if tt < 8 else y_s[:, :]
            ok = ("o_y", tt)
            P.op("sp", lambda e, rows=rows, ht=ht, dst=dst: e.dma_start(out=dst, in_=ht[:rows, :]), r=[("h", tt, cb) for cb in range(4)], w=[ok], dma=True)
            OUTKEYS.append(ok)
        finish()
        return nc


def _host_consts(inputs, core):
    hf = core % 2
    cst = np.zeros((128, CSTW), np.float32)
    r = np.arange(128)
    cst[:, C_ID:C_ID + 128] = np.eye(128, dtype=np.float32)
    cst[:, C_TRI:C_TRI + 128] = (r[:, None] <= r[None, :]).astype(np.float32)
    cst[:, C_U:C_U + 128] = (r[:, None] > r[None, :]).astype(np.float32)
    cst[:, C_ONE:C_ONE + 128] = 1.0
    cw = np.asarray(inputs["ssd_conv_w"])[0]
    cb = np.asarray(inputs["ssd_conv_b"])[0]
    cwb = np.concatenate([cw, cb[None]], 0)
    cst[:, C_CWB:C_CWB + 60] = cwb.reshape(5, 12, 128).transpose(2, 1, 0).reshape(128, 60)
    cst[:, C_GON:C_GON + 8] = np.asarray(inputs["chunk_out_norm_g"])[0].reshape(8, 128).T
    cst[:, C_ALOG:C_ALOG + 16] = np.asarray(inputs["ssd_a_log"])[0][None, :]
    cst[:, C_DTB:C_DTB + 16] = np.asarray(inputs["ssd_dt_bias"])[0][None, :]
    cst[:, C_D:C_D + 16] = np.asarray(inputs["ssd_d"])[0][None, :]
    cst[:, C_W00:C_W00 + 8] = np.asarray(inputs["chunk_w_s"])[0][:, 0, 0][None, :]
    cst[:, C_B0:C_B0 + 8] = np.asarray(inputs["chunk_b_s"])[0][:, 0][None, :]
    cst[:, C_FLAG] = float(hf)
    cst[:, C_DPP:C_DPP + 8] = np.repeat(np.asarray(inputs["ssd_d"])[0], 64).reshape(8, 128).T
    cst[:, C_SGPP:C_SGPP + 8] = np.asarray(inputs["ssd_norm_g"])[0].reshape(8, 128).T
    cst[0:16, C_DTBPP] = np.asarray(inputs["ssd_dt_bias"])[0]
    cst[0:16, C_ALPP] = np.asarray(inputs["ssd_a_log"])[0]
    cst2 = np.zeros((128, CST2W), np.float32)
    cst2[:, C2_BS:C2_BS + 1024] = np.asarray(inputs["chunk_b_s"])[0].reshape(1, 1024)
    rs = np.zeros((16, 8, 128), np.float32)
    for q in range(8):
        for rr in range(128):
            rs[2 * q + rr // 64, q, rr] = 1.0
    cst2[0:16, C2_RSEL:C2_RSEL + 1024] = rs.reshape(16, 1024)
    return cst, cst2


def make_in_maps(inputs):
    xp = np.asarray(inputs["x_prompt"], np.float32)
    xs = np.asarray(inputs["x_sample"], np.float32)
    sconv = np.asarray(inputs["state_conv"], np.float32)[0]
    sssm = np.asarray(inputs["state_ssm"], np.float32)[0]
    shared = {
        "w_in": np.ascontiguousarray(np.asarray(inputs["w_in"], np.float32)[0]),
        "w_out": np.ascontiguousarray(np.asarray(inputs["w_out"], np.float32)[0]),
        "w_gate": np.ascontiguousarray(np.asarray(inputs["w_gate"], np.float32)[0]),
        "w_up": np.ascontiguousarray(np.asarray(inputs["w_up"], np.float32)[0]),
        "w_down": np.ascontiguousarray(np.asarray(inputs["w_down"], np.float32)[0]),
        "ws": np.ascontiguousarray(np.asarray(inputs["chunk_w_s"], np.float32)[0]),
        "g_mix": np.ascontiguousarray(np.asarray(inputs["norm_mix_g"], np.float32)[0]),
        "g_ffn": np.ascontiguousarray(np.asarray(inputs["norm_ffn_g"], np.float32)[0]),
        "g_fin": np.ascontiguousarray(np.asarray(inputs["norm_final_g"], np.float32)),
        "vn_g": np.ascontiguousarray(np.asarray(inputs["chunk_v_norm_g"], np.float32)[0]),
        "ssd_g": np.ascontiguousarray(np.asarray(inputs["ssd_norm_g"], np.float32)[0]),
    }
    zeros_prev = np.zeros((TM, DM), np.float32)
    maps = []
    for c in range(8):
        b, hf = c // 2, c % 2
        cst, cst2 = _host_consts(inputs, c)
        m = dict(shared)
        m["xm"] = np.ascontiguousarray(xp[b, hf * TM:(hf + 1) * TM])
        m["xprev"] = np.ascontiguousarray(xp[b, 0:TM]) if hf == 1 else zeros_prev
        m["xsmp"] = np.ascontiguousarray(xs[c * NS:(c + 1) * NS, 0])
        m["sconv"] = np.ascontiguousarray(sconv[c * NS:(c + 1) * NS])
        m["sssm"] = np.ascontiguousarray(sssm[c * NS:(c + 1) * NS].reshape(NS, 1024, 128))
        m["cst"] = cst
        m["cst2"] = cst2
        maps.append(m)
    return maps


def kernel(**inputs):
    nc = build_program()
    maps = make_in_maps(inputs)
    res = run_bass_kernel_spmd(nc, maps, core_ids=list(range(8)))
    R = res.results
    y_prompt = np.zeros((4, 2048, DM), np.float32)
    y_sample = np.zeros((128, 1, DM), np.float32)
    ncp = np.zeros((1, 4, 3, 1536), np.float32)
    nsp = np.zeros((1, 4, 16, 64, 128), np.float32)
    ncs = np.zeros((1, 128, 3, 1536), np.float32)
    nss = np.zeros((1, 128, 16, 64, 128), np.float32)
    vs = np.zeros((1, 128, 1, 1024), np.float32)
    for c in range(8):
        b, hf = c // 2, c % 2
        y_prompt[b, hf * TM:(hf + 1) * TM] = R[c]["y_m"]
        y_sample[c * NS:(c + 1) * NS, 0] = R[c]["y_s"]
        if hf == 1:
            ncp[0, b] = R[c]["nconv_p"]
            nsp[0, b] = R[c]["nssm_p"].reshape(16, 64, 128)
        ncs[0, c * NS:(c + 1) * NS] = R[c]["nconv_s"]
        nss[0, c * NS:(c + 1) * NS] = R[c]["nssm_s"].reshape(NS, 16, 64, 128)
        vs[0, c * NS:(c + 1) * NS, 0] = R[c]["v_s"]
    return (y_prompt, y_sample, ncp, nsp, ncs, nss, vs)
```

```python
import numpy as np
from contextlib import ExitStack
import concourse.bass as bass
import concourse.mybir as mybir
from concourse.bass_utils import run_bass_kernel_spmd

F32 = mybir.dt.float32
F32R = mybir.dt.float32r
BF16 = mybir.dt.bfloat16
AF = mybir.ActivationFunctionType
ALU = mybir.AluOpType
AX = mybir.AxisListType

ENGS = ["pe", "act", "dve", "pool", "sp"]
NDMASEM = 12

DEBUG = {}
STOP_AFTER = None
SSD_LEVEL = 9


class Op:
    __slots__ = ("eng", "fn", "r", "w", "dma", "deps", "sig", "sem", "prev_use", "waits", "need_sig", "xdeps", "nobar")

    def __init__(self, eng, fn, r, w, dma, nobar=False):
        self.eng, self.fn, self.r, self.w, self.dma = eng, fn, tuple(r), tuple(w), dma
        self.deps = set()
        self.xdeps = set()
        self.sig = None
        self.sem = None
        self.prev_use = 0
        self.waits = []
        self.need_sig = False
        self.nobar = nobar


BARRIER = Op(None, None, (), (), False)


class Prog:
    def __init__(self):
        self.ops = []

    def op(self, eng, fn, r=(), w=(), dma=False, nobar=False):
        w = list(w) + [k for k in r if isinstance(k, tuple) and k and k[0] == "ps" and k not in w]
        o = Op(eng, fn, r, w, dma, nobar)
        self.ops.append(o)
        return o

    def barrier(self):
        self.ops.append(BARRIER)

    def capture(self, fn):
        saved = self.ops
        self.ops = []
        try:
            fn()
            got = self.ops
        finally:
            self.ops = saved
        return got

    @staticmethod
    def merge(main, side):
        if not side:
            return list(main)
        out = []
        m, s_ = len(main), len(side)
        j = 0
        for i, o in enumerate(main):
            out.append(o)
            tgt = (i + 1) * s_ // m
            while j < tgt:
                out.append(side[j])
                j += 1
        out.extend(side[j:])
        return out

    def resolve(self):
        ops = self.ops
        last_w = {}
        readers = {}
        last_eng = {}
        dmas = []
        pending = {}
        for i, o in enumerate(ops):
            if o is BARRIER:
                s = set(last_eng.values()) | set(dmas)
                for e in ENGS:
                    pending[e] = set(s) | pending.get(e, set())
                continue
            if o.eng in pending and not o.nobar:
                o.xdeps = o.xdeps | pending.pop(o.eng)
            deps = set(o.xdeps)
            for k in o.r:
                if k in last_w:
                    deps.add(last_w[k])
            for k in o.w:
                if k in last_w:
                    deps.add(last_w[k])
                deps.update(readers.get(k, ()))
            deps.discard(i)
            keep = set()
            for d in deps:
                od = ops[d]
                if od.dma:
                    keep.add(d)
                elif od.eng == o.eng and not o.dma:
                    if o.eng == "pe":
                        continue
                    if d in o.xdeps:
                        continue
                    if set(od.w) & set(o.r):
                        keep.add(d)
                else:
                    keep.add(d)
            o.deps = keep
            for d in keep:
                ops[d].need_sig = True
            for k in o.r:
                readers.setdefault(k, []).append(i)
            for k in o.w:
                last_w[k] = i
                readers[k] = []
            if o.dma:
                dmas.append(i)
            elif o.fn is not None:
                last_eng[o.eng] = i
        cnt = {e: 0 for e in ENGS}
        dma_n = {e: 0 for e in ENGS}
        dma_use = {}
        for i, o in enumerate(ops):
            if o is BARRIER:
                continue
            if o.dma:
                slot = (o.eng, dma_n[o.eng] % NDMASEM)
                dma_n[o.eng] += 1
                u = dma_use.get(slot, 0)
                o.prev_use = u
                dma_use[slot] = u + 1
                o.sem = slot
                o.sig = 16 * (u + 1)
            elif o.need_sig:
                cnt[o.eng] += 1
                o.sem = o.eng
                o.sig = cnt[o.eng]
        seen = {e: {} for e in ENGS}
        for i, o in enumerate(ops):
            if o is BARRIER:
                continue
            sd = seen[o.eng]
            need = {}
            if o.dma and o.prev_use > 0:
                need[o.sem] = 16 * o.prev_use
            for d in o.deps:
                od = ops[d]
                need[od.sem] = max(need.get(od.sem, 0), od.sig)
            o.waits = []
            for s, v in need.items():
                if sd.get(s, 0) >= v:
                    continue
                sd[s] = v
                o.waits.append((s, v))

    def emit(self, sems, block):
        ops = self.ops

        def run(eng_name):
            def body(e):
                for o in ops:
                    if o is BARRIER or o.eng != eng_name:
                        continue
                    for s, v in o.waits:
                        e.wait_ge(sems[s], v)
                    if o.fn is None:
                        continue
                    ins = o.fn(e)
                    if o.dma:
                        ins.then_inc(sems[o.sem], 16)
                    elif o.sig is not None:
                        ins.then_inc(sems[o.sem], 1)
            return body

        block.sync(run("sp"))
        block.tensor(run("pe"))
        block.scalar(run("act"))
        block.vector(run("dve"))
        block.gpsimd(run("pool"))


DM = 2048
DIN = 4624
DFF = 5632
TM = 1024
NS = 16
T = TM + NS
EPS = 1e-6
COL_U, COL_V, COL_Z, COL_X, COL_B, COL_C, COL_DT = 0, 1024, 2048, 3072, 4096, 4352, 4608

C_ID, C_TRI, C_U, C_ONE = 0, 128, 256, 384
C_CWB, C_GON, C_ALOG, C_DTB, C_D, C_W00, C_B0, C_FLAG = 512, 572, 580, 596, 612, 628, 636, 644
C_DPP, C_SGPP, C_DTBPP, C_ALPP = 648, 656, 664, 665
CSTW = 672
C2_BS, C2_RSEL = 0, 1024
CST2W = 2048

AW = 51000


class Bump:
    def __init__(self, arena, start, end):
        self.arena, self.start, self.end, self.off = arena, start, end, start
        self.peak = start

    def reset(self, to=None):
        self.off = self.start if to is None else to

    def mark(self):
        return self.off

    def _take(self, words):
        o = self.off
        self.off += words
        assert self.off <= self.end, ("arena overflow", self.off, self.end)
        self.peak = max(self.peak, self.off)
        return o

    def f32(self, shape):
        n = int(np.prod(shape))
        o = self._take(n)
        v = self.arena[:, o:o + n]
        return _shape(v, shape)

    def bf16(self, shape):
        n = int(np.prod(shape))
        w = (n + 1) // 2
        o = self._take(w)
        v = self.arena[:, o:o + w].bitcast(BF16)[:, 0:n]
        return _shape(v, shape)


def _shape(v, shape):
    if len(shape) == 1:
        return v
    if len(shape) == 2:
        return v.rearrange("p (a b) -> p a b", a=shape[0])
    if len(shape) == 3:
        return v.rearrange("p (a b c) -> p a b c", a=shape[0], b=shape[1])
    raise ValueError(shape)


def build_program():
    nc = bass.Bass("TRN2", target_bir_lowering=False)

    def din(name, shape):
        return nc.dram_tensor(name, shape, F32, kind="ExternalInput").ap()

    def dout(name, shape):
        return nc.dram_tensor(name, shape, F32, kind="ExternalOutput").ap()

    xm = din("xm", [TM, DM])
    xprev = din("xprev", [TM, DM])
    xsmp = din("xsmp", [NS, DM])
    sconv = din("sconv", [NS, 3, 1536])
    sssm = din("sssm", [NS, 1024, 128])
    w_in = din("w_in", [DM, DIN])
    w_out = din("w_out", [DM, DM])
    w_gate = din("w_gate", [DM, DFF])
    w_up = din("w_up", [DM, DFF])
    w_down = din("w_down", [DFF, DM])
    cst_d = din("cst", [128, CSTW])
    cst2_d = din("cst2", [128, CST2W])
    ws_d = din("ws", [8, 128, 128])
    g_mix = din("g_mix", [DM])
    g_ffn = din("g_ffn", [DM])
    g_fin = din("g_fin", [DM])
    vn_g = din("vn_g", [1024])
    ssd_g = din("ssd_g", [1024])

    y_m = dout("y_m", [TM, DM])
    y_s = dout("y_s", [NS, DM])
    nconv_p = dout("nconv_p", [3, 1536])
    nssm_p = dout("nssm_p", [1024, 128])
    nconv_s = dout("nconv_s", [NS, 3, 1536])
    nssm_s = dout("nssm_s", [NS, 1024, 128])
    v_s = dout("v_s", [NS, 1024])
    dbg_d = {}
    for name, (shape, dty) in DEBUG.items():
        dbg_d[name] = nc.dram_tensor("dbg_" + name, list(shape), dty, kind="ExternalOutput").ap()

    P = Prog()
    with ExitStack() as es:
        arena = es.enter_context(nc.sbuf_tensor("arena", [128, AW], F32))
        rhsE = es.enter_context(nc.sbuf_tensor("rhsE", [128, 2048], F32))
        U32 = es.enter_context(nc.sbuf_tensor("U32", [128, 128], F32))
        pb = [es.enter_context(nc.psum_tensor(f"pb{i}", [128, 512], F32)) for i in range(8)]
        sems = {}
        for e in ENGS:
            sems[e] = es.enter_context(nc.semaphore("s_" + e))
        for e in ("sp", "pool"):
            for i in range(NDMASEM):
                sems[(e, i)] = es.enter_context(nc.semaphore(f"d_{e}_{i}"))
        block = es.enter_context(nc.Block())

        arena = arena[:, :]
        rhsEr = rhsE[:, :].bitcast(F32R)
        U32r = U32[:, :].bitcast(F32R)
        pbf = [p[:, :] for p in pb]
        pbb = [p[:, :].bitcast(BF16) for p in pb]

        fx = Bump(arena, 0, AW)
        CST = fx.f32([CSTW])
        identF = CST[:, C_ID:C_ID + 128]
        triF = CST[:, C_TRI:C_TRI + 128]
        UF = CST[:, C_U:C_U + 128]
        onesF = CST[:, C_ONE:C_ONE + 128]
        cwb = CST[:, C_CWB:C_CWB + 60].rearrange("p (j k) -> p j k", j=12)
        gon_pp = CST[:, C_GON:C_GON + 8]
        alog_bc = CST[:, C_ALOG:C_ALOG + 16]
        dtb_bc = CST[:, C_DTB:C_DTB + 16]
        D_bc = CST[:, C_D:C_D + 16]
        w00_bc = CST[:, C_W00:C_W00 + 8]
        b0_bc = CST[:, C_B0:C_B0 + 8]
        flag = CST[:, C_FLAG:C_FLAG + 1]
        D_pp = CST[:, C_DPP:C_DPP + 8]
        sg_pp = CST[:, C_SGPP:C_SGPP + 8]
        dtb_pp = CST[:, C_DTBPP:C_DTBPP + 1]
        al_pp = CST[:, C_ALPP:C_ALPP + 1]
        identB = fx.bf16([128])
        onesB = fx.bf16([128])
        aneg = fx.f32([16])
        hT = fx.f32([1024])
        hTb = fx.bf16([1024])
        prefix = fx.f32([12, 3])
        ncv = fx.f32([12, 3])
        nT = fx.bf16([16, T])
        YT = fx.bf16([16, T])
        r2_start = fx.off - (16 * T) // 2
        r2_end = fx.off
        Wfix = [fx.bf16([16, 512]) for _ in range(2)]
        _wflat = [w_.rearrange("p a b -> p (a b)") for w_ in Wfix]
        WG = [w_[:, 0:4096].rearrange("p (a b) -> p a b", a=16) for w_ in _wflat]
        WU = [w_[:, 4096:8192].rearrange("p (a b) -> p a b", a=16) for w_ in _wflat]
        wq = [0]
        S0 = fx.off
        sc = Bump(arena, S0, AW)
        r2 = Bump(arena, r2_start, r2_end)

        KC = ("cst",)

        P.op("sp", lambda e: e.dma_start(out=CST, in_=cst_d[:, :]), w=[KC], dma=True)
        P.op("dve", lambda e: e.tensor_copy(out=identB, in_=identF), r=[KC], w=["identB"])
        P.op("dve", lambda e: e.memset(onesB, 1.0), w=["onesB"])
        P.op("dve", lambda e: e.tensor_copy(out=U32r, in_=UF), r=[KC], w=["U32"])
        P.op("act", lambda e: e.activation(out=aneg, in_=alog_bc, func=AF.Exp), r=[KC], w=["aneg0"])
        P.op("dve", lambda e: e.tensor_scalar(out=aneg, in0=aneg, scalar1=-1.0, scalar2=None, op0=ALU.mult), r=["aneg0"], w=["aneg"])
        P.op("dve", lambda e: e.memset(hT, 0.0), w=["hT"])
        P.op("dve", lambda e: e.memset(hTb, 0.0), w=["hTb"])

        def rms_transpose(xt, kx, rows, xn, kxn, st, kst, gbc, kg, dstT, col0, kdst, width, tag, pre=None):
            nk = width // 128

            def s1():
                if pre is not None:
                    pre()
                P.op("act", lambda e: e.activation(out=xn[:rows, :], in_=xt[:rows, :], func=AF.Square, accum_out=st[:rows, 0:1]),
                     r=[kx] if not isinstance(kx, list) else kx, w=[kxn, kst])
                P.op("act", lambda e: e.activation(out=st[:rows, 1:2], in_=st[:rows, 0:1], func=AF.Ln, scale=1.0 / width, bias=EPS),
                     r=[kst], w=[kst])
                P.op("act", lambda e: e.activation(out=st[:rows, 2:3], in_=st[:rows, 1:2], func=AF.Exp, scale=-0.5),
                     r=[kst], w=[kst])
                P.op("dve", lambda e: e.scalar_tensor_tensor(out=xn[:rows, :], in0=xt[:rows, :], scalar=st[:rows, 2:3], op0=ALU.mult,
                                                             in1=gbc[:rows, :], op1=ALU.mult),
                     r=([kx] if not isinstance(kx, list) else kx) + [kst, kg, kxn], w=[kxn])

            def s2():
                for b8 in range(nk // 8):
                    bank = (tag % 2) * 2 + b8
                    psv = pbb[bank][:, 0:8 * rows].rearrange("p (a b) -> p a b", a=8)

                    def tr(e, b8=b8, psv=psv):
                        ins = None
                        for j in range(8):
                            k = b8 * 8 + j
                            ins = e.transpose(out=psv[:, j, :], in_=xn[:rows, k * 128:(k + 1) * 128], identity=identB[:rows, :rows])
                        return ins
                    P.op("pe", tr, r=[kxn, "identB"], w=[("ps", bank)])
                    dst = dstT[:, b8 * 8:(b8 + 1) * 8, col0:col0 + rows]
                    if b8 == 0:
                        P.op("act", lambda e, dst=dst, psv=psv: e.activation(out=dst, in_=psv, func=AF.Copy), r=[("ps", bank)], w=[kdst])
                    else:
                        P.op("dve", lambda e, dst=dst, psv=psv: e.tensor_copy(out=dst, in_=psv), r=[("ps", bank)], w=[kdst])
            return P.capture(s1), P.capture(s2)

        def rms_pipeline(stages):
            n = len(stages)
            for i in range(n + 1):
                if i < n:
                    P.ops.extend(stages[i][0])
                if i >= 1:
                    P.ops.extend(stages[i - 1][1])

        def mm_group(out, pairs, r, w):
            def fn(e):
                ins = None
                n = len(pairs)
                for i, (l, rh) in enumerate(pairs):
                    ins = e.matmul(out, lhsT=l, rhs=rh, start=(i == 0), stop=(i == n - 1))
                return ins
            P.op("pe", fn, r=r, w=w)

        def load_w(dst, src_ap, key, nobar=False):
            P.op("pool", lambda e: e.dma_start(out=dst, in_=src_ap), w=[key], dma=True, nobar=nobar)

        def wload(src_ap, ncols=512):
            slot = wq[0] % 2
            wq[0] += 1
            buf = Wfix[slot] if ncols == 512 else Wfix[slot][:, :, 0:ncols]
            load_w(buf, src_ap, ("W", slot), nobar=True)
            return buf, ("W", slot)

        def wview(wap, c0, ncols):
            return wap[:, c0:c0 + ncols].rearrange("(kt p) c -> p kt c", p=128)

        def conv_silu(rawpad, krp, acc, kacc, j, ntok, dst, kdst, defer=None):
            P.op("dve", lambda e: e.tensor_scalar(out=acc[:, 0:ntok], in0=rawpad[:, 0:ntok], scalar1=cwb[:, j, 0:1], scalar2=cwb[:, j, 4:5],
                                                  op0=ALU.mult, op1=ALU.add), r=[krp, KC], w=[kacc])
            for k in (1, 2, 3):
                P.op("dve", lambda e, k=k: e.scalar_tensor_tensor(out=acc[:, 0:ntok], in0=rawpad[:, k:k + ntok], scalar=cwb[:, j, k:k + 1],
                                                                  op0=ALU.mult, in1=acc[:, 0:ntok], op1=ALU.add), r=[krp, kacc, KC], w=[kacc])
            silu_ops = P.capture(lambda: P.op("act", lambda e: e.activation(out=dst, in_=acc[:, 0:ntok], func=AF.Silu), r=[kacc], w=[kdst]))
            if defer is None:
                P.ops.extend(silu_ops)
            else:
                defer.append(silu_ops)

        def ssd_temps(b, main=True):
            t = {}
            t["sm"] = b.f32([96])
            t["ex"] = b.f32([48])
            t["xdtw"] = b.bf16([16, 64])
            t["B_tok"] = b.bf16([256])
            if not main:
                return t
            t["LT"] = b.f32([2048])
            t["MT"] = b.bf16([16, 128])
            t["xs_tok"] = b.bf16([16, 64])
            t["xdt"] = b.bf16([16, 64])
            t["cbm"] = b.f32([2, 128])
            t["y1"] = b.f32([16, 64])
            t["t2"] = b.f32([16, 64])
            t["yb"] = b.bf16([1024])
            t["gn"] = b.f32([8])
            return t

        def ssd_chunk(t, xT, c, dtraw_c, kdt, main, ztok_c=None, ssdg_bc=None, banks=(0, 1, 2, 7, 1), ktag=""):
            sm, ex = t["sm"], t["ex"]
            cs = slice(c * 128, (c + 1) * 128)
            kx = ("xT" + ktag,)
            bs_, bx_, bb_ = banks[0], banks[1], banks[2]
            P.op("dve", lambda e: e.tensor_tensor(out=sm[:, 0:16], in0=dtraw_c, in1=dtb_bc, op=ALU.add), r=[kdt, KC], w=["sm0"])
            P.op("act", lambda e: e.activation(out=sm[:, 16:32], in_=sm[:, 0:16], func=AF.Exp), r=["sm0"], w=["sm1"])
            P.op("act", lambda e: e.activation(out=sm[:, 32:48], in_=sm[:, 16:32], func=AF.Ln, bias=1.0), r=["sm1"], w=["dtc"])
            dtc = sm[:, 32:48]
            dta = sm[:, 48:64]
            P.op("dve", lambda e: e.tensor_tensor(out=dta, in0=dtc, in1=aneg, op=ALU.mult), r=["dtc", "aneg"], w=["dta"])

            def small(e):
                e.matmul(pbf[bs_][:, 0:16], lhsT=UF, rhs=dta, start=True, stop=True)
                ins = e.matmul(pbf[bs_][:, 16:32], lhsT=onesF, rhs=dta, start=True, stop=True)
                if main:
                    ins = e.matmul(pbf[bs_][:, 32:48], lhsT=triF, rhs=dta, start=True, stop=True)
                return ins
            P.op("pe", small, r=["dta", KC], w=[("ps", bs_)])
            nex = 48 if main else 32
            P.op("act", lambda e: e.activation(out=ex[:, 0:nex], in_=pbf[bs_][:, 0:nex], func=AF.Exp), r=[("ps", bs_)], w=["ex"])
            toend, dec, ea = ex[:, 0:16], ex[:, 16:32], ex[:, 32:48]

            psx = pbb[bx_][:, 0:1024].rearrange("p (a b) -> p a b", a=8)

            def trx(e):
                ins = None
                for q in range(8):
                    ins = e.transpose(out=psx[:, q, :], in_=xT[:, q, cs], identity=identB)
                return ins
            P.op("pe", trx, r=[kx, "identB"], w=[("ps", bx_)])
            psB = pbb[bb_][:, 0:256].rearrange("p (a b) -> p a b", a=2)

            def trb(e):
                ins = None
                for g in range(2):
                    ins = e.transpose(out=psB[:, g, :], in_=xT[:, 8 + g, cs], identity=identB)
                return ins
            P.op("pe", trb, r=[kx, "identB"], w=[("ps", bb_)])
            psx3 = pbb[bx_][:, 0:1024].rearrange("p (a b) -> p a b", a=16)
            w2 = sm[:, 64:80]
            P.op("dve", lambda e: e.tensor_tensor(out=w2, in0=dtc, in1=toend, op=ALU.mult), r=["dtc", "ex"], w=["w2"])
            P.op("dve", lambda e: e.tensor_tensor(out=t["xdtw"], in0=psx3, in1=w2.unsqueeze(2).broadcast_to([128, 16, 64]), op=ALU.mult),
                 r=[("ps", bx_), "w2"], w=["xdtw"])
            P.op("act", lambda e: e.activation(out=t["B_tok"], in_=pbb[bb_][:, 0:256], func=AF.Copy), r=[("ps", bb_)], w=["B_tok"])
            if main and SSD_LEVEL == 0:
                return
            if main:
                P.op("dve", lambda e: e.tensor_tensor(out=t["xdt"], in0=psx3, in1=dtc.unsqueeze(2).broadcast_to([128, 16, 64]), op=ALU.mult),
                     r=[("ps", bx_), "dtc"], w=["xdt"])
                P.op("act", lambda e: e.activation(out=t["xs_tok"], in_=psx3, func=AF.Copy), r=[("ps", bx_)], w=["xs_tok"])
                for e16 in range(16 if SSD_LEVEL >= 2 else 0):
                    P.op("dve", lambda e, e16=e16: e.tensor_scalar(out=rhsEr[:, e16 * 128:(e16 + 1) * 128], in0=triF, scalar1=dta[:, e16:e16 + 1],
                                                                   scalar2=None, op0=ALU.mult), r=["dta", KC], w=[("rhsE", e16 // 4)])
                for i in range(4 if SSD_LEVEL >= 2 else 0):
                    P.op("pe", lambda e, i=i: e.matmul(pbf[3 + i], lhsT=U32r, rhs=rhsEr[:, i * 512:(i + 1) * 512], start=True, stop=True),
                         r=[("rhsE", i), "U32"], w=[("ps", 3 + i)])
                    P.op("act", lambda e, i=i: e.activation(out=t["LT"][:, i * 512:(i + 1) * 512], in_=pbf[3 + i], func=AF.Exp),
                         r=[("ps", 3 + i)], w=[("LT", i)])
                psc = pbf[7][:, 0:256].rearrange("p (a b) -> p a b", a=2)
                if SSD_LEVEL < 3:
                    return

                def cbf(e):
                    ins = None
                    for g in range(2):
                        ins = e.matmul(psc[:, g, :], lhsT=xT[:, 8 + g, cs], rhs=xT[:, 10 + g, cs], start=True, stop=True)
                    return ins
                P.op("pe", cbf, r=[kx], w=[("ps", 7)])
                P.op("dve", lambda e: e.tensor_tensor(out=t["cbm"], in0=psc, in1=triF.unsqueeze(1).broadcast_to([128, 2, 128]), op=ALU.mult),
                     r=[("ps", 7), KC], w=["cbm"])
                LT3 = t["LT"].rearrange("p (a b) -> p a b", a=16)
                for g in range(2):
                    P.op("dve", lambda e, g=g: e.tensor_tensor(out=t["MT"][:, g * 8:(g + 1) * 8, :], in0=LT3[:, g * 8:(g + 1) * 8, :],
                                                               in1=t["cbm"][:, g:g + 1, :].broadcast_to([128, 8, 128]), op=ALU.mult),
                         r=[("LT", 2 * g), ("LT", 2 * g + 1), "cbm"], w=[("MT", g)])
                if SSD_LEVEL < 4:
                    return
                for g in range(2):
                    def yd(e, g=g):
                        ins = None
                        for j in range(8):
                            e16 = g * 8 + j
                            ins = e.matmul(pbf[3 + g][:, j * 64:(j + 1) * 64], lhsT=t["MT"][:, e16, :], rhs=t["xdt"][:, e16, :], start=True, stop=True)
                        return ins
                    P.op("pe", yd, r=[("MT", g), "xdt"], w=[("ps", 3 + g)])
                    P.op("pe", lambda e, g=g: e.matmul(pbf[5 + g], lhsT=xT[:, 10 + g, cs], rhs=hTb[:, g * 512:(g + 1) * 512], start=True, stop=True),
                         r=[kx, "hTb"], w=[("ps", 5 + g)])
                y1 = t["y1"]
                for g in range(2):
                    y1g = y1[:, g * 8:(g + 1) * 8, :]
                    P.op("dve", lambda e, g=g, y1g=y1g: e.tensor_tensor(out=y1g, in0=pbf[5 + g].rearrange("p (a b) -> p a b", a=8),
                                                                        in1=ea[:, g * 8:(g + 1) * 8].unsqueeze(2).broadcast_to([128, 8, 64]), op=ALU.mult),
                         r=[("ps", 5 + g), "ex"], w=[("y1", g)])
                    P.op("dve", lambda e, g=g, y1g=y1g: e.tensor_tensor(out=y1g, in0=pbf[3 + g].rearrange("p (a b) -> p a b", a=8), in1=y1g, op=ALU.add),
                         r=[("ps", 3 + g), ("y1", g)], w=[("y1", g)])
                P.op("pool", lambda e: e.tensor_tensor(out=t["t2"], in0=t["xs_tok"], in1=D_bc.unsqueeze(2).broadcast_to([128, 16, 64]), op=ALU.mult),
                     r=["xs_tok", KC], w=["t2"])
                P.op("dve", lambda e: e.tensor_tensor(out=y1, in0=y1, in1=t["t2"], op=ALU.add), r=[("y1", 0), ("y1", 1), "t2"], w=[("y1", 0), ("y1", 1)])
                y1f = y1.rearrange("p a b -> p (a b)")
                P.op("dve", lambda e: e.tensor_tensor(out=y1f, in0=y1f, in1=ztok_c, op=ALU.mult), r=[("y1", 0), ("y1", 1), "z_tok"], w=[("y1", 0), ("y1", 1)])
                if SSD_LEVEL < 5:
                    return
                gn = t["gn"]
                t2f = t["t2"].rearrange("p a b -> p (a b)")
                for g in range(2):
                    P.op("act", lambda e, g=g: e.activation(out=t2f[:, g * 512:(g + 1) * 512], in_=y1f[:, g * 512:(g + 1) * 512], func=AF.Square,
                                                            accum_out=gn[:, g:g + 1]), r=[("y1", g)], w=["t2", ("gn", g)])
                P.op("act", lambda e: e.activation(out=gn[:, 2:4], in_=gn[:, 0:2], func=AF.Ln, scale=1.0 / 512, bias=EPS), r=[("gn", 0), ("gn", 1)], w=["gn2"])
                P.op("act", lambda e: e.activation(out=gn[:, 4:6], in_=gn[:, 2:4], func=AF.Exp, scale=-0.5), r=["gn2"], w=["gn4"])
                for g in range(2):
                    P.op("dve", lambda e, g=g: e.scalar_tensor_tensor(out=t["yb"][:, g * 512:(g + 1) * 512], in0=y1f[:, g * 512:(g + 1) * 512],
                                                                      scalar=gn[:, 4 + g:5 + g], op0=ALU.mult, in1=ssdg_bc[:, g * 512:(g + 1) * 512], op1=ALU.mult),
                         r=[("y1", g), "gn4", "ssdg"], w=[("yb", g)])
                if SSD_LEVEL < 6:
                    return
                psy = pbb[2][:, 0:1024].rearrange("p (a b) -> p a b", a=8)

                def try_(e):
                    ins = None
                    for q in range(8):
                        ins = e.transpose(out=psy[:, q, :], in_=t["yb"][:, q * 128:(q + 1) * 128], identity=identB)
                    return ins
                P.op("pe", try_, r=[("yb", 0), ("yb", 1), "identB"], w=[("ps", 2)])
                P.op("act", lambda e: e.activation(out=YT[:, 8:16, cs], in_=psy, func=AF.Copy), r=[("ps", 2)], w=[("YTb", c)])
            sb = (banks[3], banks[4])
            for g in range(2):
                P.op("pe", lambda e, g=g: e.matmul(pbf[sb[g]], lhsT=t["B_tok"][:, g * 128:(g + 1) * 128], rhs=t["xdtw"].rearrange("p a b -> p (a b)")[:, g * 512:(g + 1) * 512],
                                                   start=True, stop=True), r=["B_tok", "xdtw"], w=[("ps", sb[g])])
            hT3 = hT.rearrange("p (a b) -> p a b", a=16)
            P.op("dve", lambda e: e.tensor_tensor(out=hT3, in0=hT3, in1=dec.unsqueeze(2).broadcast_to([128, 16, 64]), op=ALU.mult), r=["hT", "ex"], w=["hT"])
            for g in range(2):
                P.op("dve", lambda e, g=g: e.tensor_tensor(out=hT[:, g * 512:(g + 1) * 512], in0=pbf[sb[g]], in1=hT[:, g * 512:(g + 1) * 512], op=ALU.add),
                     r=[("ps", sb[g]), "hT"], w=["hT"])
            P.op("act", lambda e: e.activation(out=hTb, in_=hT, func=AF.Copy), r=["hT"], w=["hTb"])

        dbg_ops = []

        def dbg_dump(name, ap, keys):
            if name in dbg_d:
                dbg_ops.append((name, ap, keys))

        def finish():
            P.barrier()
            outs = []
            for name, ap, keys in dbg_ops:
                k = ("dbgout", name)
                P.op("sp", lambda e, name=name, ap=ap: e.dma_start(out=dbg_d[name], in_=ap), r=keys, w=[k], dma=True)
                outs.append(k)
            P.op("sp", None, r=outs + OUTKEYS)
            P.resolve()
            P.emit(sems, block)

        OUTKEYS = []

        sc.reset()
        X = [sc.f32([DM]) for _ in range(2)]
        XN = [sc.bf16([DM]) for _ in range(2)]
        ST = [sc.f32([4]) for _ in range(2)]
        gbc = sc.f32([DM])
        m_common = sc.mark()
        nPT = nT[:, :, 0:TM]
        P.op("sp", lambda e, gbc=gbc: e.dma_start(out=gbc, in_=g_mix.partition_broadcast(128)), w=["gbc"], dma=True)
        stg = []
        for tt in range(8):
            i = tt % 2
            pre = (lambda tt=tt, i=i: P.op("sp", lambda e: e.dma_start(out=X[i], in_=xprev[tt * 128:(tt + 1) * 128, :]), w=[("X", i)], dma=True))
            stg.append(rms_transpose(X[i], ("X", i), 128, XN[i], ("XN", i), ST[i], ("ST", i), gbc, "gbc", nPT, tt * 128, ("nPT", tt), DM, tt, pre=pre))
        rms_pipeline(stg)
        Wdt = sc.bf16([16, 16])
        r2.reset()
        xTp = r2.bf16([10, TM])
        dtraw_p = r2.f32([8, 16])
        tP = ssd_temps(r2, main=False)
        rawpad = [sc.f32([TM + 4]) for _ in range(2)]
        accb = [sc.f32([TM]) for _ in range(2)]
        blocks = [(COL_X, 512), (COL_X + 512, 512), (COL_B, 512)]
        load_w(Wdt, wview(w_in, COL_DT, 16), "Wdt")
        wcur = wload(wview(w_in, blocks[0][0], blocks[0][1]))
        for i in range(2):
            P.op("dve", lambda e, rp=rawpad[i]: e.memset(rp[:, 0:3], 0.0), w=[("rawpad", i)])
        jt = 0
        defer_p = []
        for bi, (c0, ncol) in enumerate(blocks):
            Wv, kW = wcur
            if bi + 1 < len(blocks):
                wcur = wload(wview(w_in, blocks[bi + 1][0], blocks[bi + 1][1]))
            for jj in range(4):
                j = (c0 - COL_X) // 128 + jj
                rp = rawpad[jt % 2]
                krp = ("rawpad", jt % 2)
                isC = j >= 10
                pend_p = list(defer_p)
                del defer_p[:]
                for nb in range(2):
                    if isC and nb == 0:
                        continue
                    bank = 3 + (jt * 2 + nb) % 4
                    mm_group(pbf[bank], [(Wv[:, k, jj * 128:(jj + 1) * 128], nPT[:, k, nb * 512:(nb + 1) * 512]) for k in range(16)],
                             r=[kW] + [("nPT", tt) for tt in range(nb * 4, nb * 4 + 4)], w=[("ps", bank)])
                    P.op("act", lambda e, rp=rp, nb=nb, bank=bank: e.activation(out=rp[:, 3 + nb * 512:3 + (nb + 1) * 512], in_=pbf[bank], func=AF.Copy),
                         r=[("ps", bank)], w=[krp])
                for so in pend_p:
                    P.ops.extend(so)
                P.op("dve", lambda e, rp=rp, j=j: e.tensor_copy(out=prefix[:, j, :], in_=rp[:, TM:TM + 3]), r=[krp], w=["prefix"])
                if not isC:
                    conv_silu(rp, krp, accb[jt % 2], ("acc", jt % 2), j, TM, xTp[:, j, :], ("xTp",), defer=defer_p)
                jt += 1
        for so in defer_p:
            P.ops.extend(so)
        for tt in range(8):
            mm_group(pbf[0][:, 0:16], [(nPT[:, k, tt * 128:(tt + 1) * 128], Wdt[:, k, :]) for k in range(16)], r=["Wdt", ("nPT", tt)], w=[("ps", 0)])
            P.op("act", lambda e, tt=tt: e.activation(out=dtraw_p[:, tt, :], in_=pbf[0][:, 0:16], func=AF.Copy), r=[("ps", 0)], w=["dtraw_p"])
        P.barrier()

        def pchunks():
            for c in range(8):
                ssd_chunk(tP, xTp, c, dtraw_p[:, c, :], "dtraw_p", main=False, banks=(5, 6, 5, 7, 6), ktag="p")
            P.op("dve", lambda e: e.tensor_scalar(out=hT, in0=hT, scalar1=flag, scalar2=None, op0=ALU.mult), r=["hT", KC], w=["hT"])
            P.op("act", lambda e: e.activation(out=hTb, in_=hT, func=AF.Copy), r=["hT"], w=["hTb"])
        pch_ops = P.capture(pchunks)
        _saved_ops = P.ops
        P.ops = []
        sc.reset(m_common)
        stgA = []
        for tt in range(9):
            i = tt % 2
            rows = 128 if tt < 8 else NS
            src = xm[tt * 128:(tt + 1) * 128, :] if tt < 8 else xsmp[:, :]
            pre = (lambda i=i, rows=rows, src=src: P.op("sp", lambda e: e.dma_start(out=X[i][:rows, :], in_=src), w=[("X", i)], dma=True))
            stgA.append(rms_transpose(X[i], ("X", i), rows, XN[i], ("XN", i), ST[i], ("ST", i), gbc, "gbc", nT, tt * 128, ("nT", tt), DM, tt, pre=pre))
        rms_pipeline(stgA)
        P.barrier()
        sc.reset()
        xT = sc.bf16([12, T])
        z_tok = sc.bf16([8, 1024])
        zT_s = sc.f32([8, NS])
        dtraw = sc.f32([8, 16])
        dtT_s = sc.f32([NS])
        ssdg_bc = sc.f32([1024])
        Rsel = sc.f32([1024])
        raw_s = sc.f32([12, NS])
        m_A = sc.mark()
        Wdt = sc.bf16([16, 16])
        rawpad = [sc.f32([TM + 4]) for _ in range(2)]
        accb = [sc.f32([TM]) for _ in range(2)]
        scTok = sc.f32([3 * 1536])
        scT = sc.f32([36, NS])
        acc_s = sc.f32([12, NS])
        P.op("sp", lambda e: e.dma_start(out=ssdg_bc, in_=ssd_g.partition_broadcast(128)), w=["ssdg"], dma=True)
        P.op("sp", lambda e: e.dma_start(out=Rsel, in_=cst2_d[:, C2_RSEL:C2_RSEL + 1024]), w=["cst2"], dma=True)
        P.op("sp", lambda e: e.dma_start(out=scTok[:NS, :], in_=sconv.rearrange("b r c -> b (r c)")), w=["scTok"], dma=True)
        P.op("sp", lambda e: e.dma_start(out=nconv_s[:, 0:2, :], in_=sconv[:, 1:3, :]), w=["o_ncs01"], dma=True)
        OUTKEYS.append("o_ncs01")
        for half in range(2):
            def trs(e, half=half):
                ins = None
                for idx in range(half * 18, half * 18 + 18):
                    r_, j_ = idx // 12, idx % 12
                    ii = idx - half * 18
                    ins = e.transpose(out=pbf[half][:, ii * NS:(ii + 1) * NS], in_=scTok[:NS, r_ * 1536 + j_ * 128:r_ * 1536 + (j_ + 1) * 128],
                                      identity=identF[:NS, :NS])
                return ins
            P.op("pe", trs, r=["scTok", KC], w=[("ps", half)])
            P.op("dve", lambda e, half=half: e.tensor_copy(out=scT[:, half * 18:half * 18 + 18, :].rearrange("p a b -> p (a b)"), in_=pbf[half][:, 0:18 * NS]),
                 r=[("ps", half)], w=["scT"])
        blocksA = [(COL_Z, "z"), (COL_Z + 512, "z"), (COL_X, "x"), (COL_X + 512, "x"), (COL_B, "x")]
        load_w(Wdt, wview(w_in, COL_DT, 16), "Wdt")
        wcur = wload(wview(w_in, blocksA[0][0], 512))
        jt = 0
        grp = 0
        defer_a = []
        for bi, (c0, kind) in enumerate(blocksA):
            Wv, kW = wcur
            if bi + 1 < len(blocksA):
                wcur = wload(wview(w_in, blocksA[bi + 1][0], 512))
            if kind == "z":
                zb = (c0 - COL_Z) // 512
                for tt in range(8):
                    bank = 1 + grp % 4
                    grp += 1
                    mm_group(pbf[bank], [(nT[:, k, tt * 128:(tt + 1) * 128], Wv[:, k, :]) for k in range(16)], r=[kW, ("nT", tt)], w=[("ps", bank)])
                    P.op("act", lambda e, tt=tt, zb=zb, bank=bank: e.activation(out=z_tok[:, tt, zb * 512:(zb + 1) * 512], in_=pbf[bank], func=AF.Silu),
                         r=[("ps", bank)], w=["z_tok"])
                for jj in range(4):
                    bank = 1 + grp % 4
                    grp += 1
                    j = zb * 4 + jj
                    mm_group(pbf[bank][:, 0:NS], [(Wv[:, k, jj * 128:(jj + 1) * 128], nT[:, k, TM:T]) for k in range(16)], r=[kW, ("nT", 8)], w=[("ps", bank)])
                    P.op("act", lambda e, j=j, bank=bank: e.activation(out=zT_s[:, j, :], in_=pbf[bank][:, 0:NS], func=AF.Silu), r=[("ps", bank)], w=["zT_s"])
            else:
                for jj in range(4):
                    j = (c0 - COL_X) // 128 + jj
                    rp = rawpad[jt % 2]
                    krp = ("rawpad", jt % 2)
                    pend_a = list(defer_a)
                    del defer_a[:]
                    P.op("dve", lambda e, rp=rp, j=j: e.tensor_copy(out=rp[:, 0:3], in_=prefix[:, j, :]), r=["prefix"], w=[krp])
                    for nb in range(3):
                        bank = 1 + grp % 4
                        grp += 1
                        lo, hi = (nb * 512, (nb + 1) * 512) if nb < 2 else (TM, T)
                        n = hi - lo
                        rk = [("nT", tt) for tt in range(nb * 4, nb * 4 + 4)] if nb < 2 else [("nT", 8)]
                        mm_group(pbf[bank][:, 0:n], [(Wv[:, k, jj * 128:(jj + 1) * 128], nT[:, k, lo:hi]) for k in range(16)], r=[kW] + rk, w=[("ps", bank)])
                        if nb < 2:
                            P.op("act", lambda e, rp=rp, lo=lo, hi=hi, bank=bank: e.activation(out=rp[:, 3 + lo:3 + hi], in_=pbf[bank], func=AF.Copy),
                                 r=[("ps", bank)], w=[krp])
                        else:
                            P.op("act", lambda e, j=j, bank=bank: e.activation(out=raw_s[:, j, :], in_=pbf[bank][:, 0:NS], func=AF.Copy),
                                 r=[("ps", bank)], w=["raw_s"])
                    for so in pend_a:
                        P.ops.extend(so)
                    P.op("dve", lambda e, rp=rp, j=j: e.tensor_copy(out=ncv[:, j, :], in_=rp[:, TM:TM + 3]), r=[krp], w=["ncv"])
                    conv_silu(rp, krp, accb[jt % 2], ("acc", jt % 2), j, TM, xT[:, j, 0:TM], ("xT",), defer=defer_a)
                    P.op("dve", lambda e, j=j: e.tensor_scalar(out=acc_s[:, j, :], in0=raw_s[:, j, :], scalar1=cwb[:, j, 3:4], scalar2=cwb[:, j, 4:5],
                                                               op0=ALU.mult, op1=ALU.add), r=["raw_s", KC], w=["acc_s"])
                    for r_ in range(3):
                        P.op("dve", lambda e, j=j, r_=r_: e.scalar_tensor_tensor(out=acc_s[:, j, :], in0=scT[:, r_ * 12 + j, :], scalar=cwb[:, j, r_:r_ + 1],
                                                                                 op0=ALU.mult, in1=acc_s[:, j, :], op1=ALU.add), r=["scT", "acc_s", KC], w=["acc_s"])
                    jt += 1
        for so in defer_a:
            P.ops.extend(so)
        P.op("act", lambda e: e.activation(out=xT[:, :, TM:T], in_=acc_s, func=AF.Silu), r=["acc_s"], w=[("xT",)])
        for tt in range(8):
            mm_group(pbf[0][:, 0:16], [(nT[:, k, tt * 128:(tt + 1) * 128], Wdt[:, k, :]) for k in range(16)], r=["Wdt", ("nT", tt)], w=[("ps", 0)])
            P.op("act", lambda e, tt=tt: e.activation(out=dtraw[:, tt, :], in_=pbf[0][:, 0:16], func=AF.Copy), r=[("ps", 0)], w=["dtraw"])
        mm_group(pbf[1][:NS, 0:NS], [(Wdt[:, k, :], nT[:, k, TM:T]) for k in range(16)], r=["Wdt", ("nT", 8)], w=[("ps", 1)])
        P.op("act", lambda e: e.activation(out=dtT_s[:NS, :], in_=pbf[1][:NS, 0:NS], func=AF.Copy), r=[("ps", 1)], w=["dtT_s"])

        def rows_out(src3, ksrc, R, stage, kst, dram_ap, okey):
            for b3 in range(3):
                def trr(e, b3=b3):
                    ins = None
                    for jj in range(4):
                        j_ = b3 * 4 + jj
                        ins = e.transpose(out=pbf[2 + b3][:R, jj * 128:(jj + 1) * 128], in_=src3[:, j_, :], identity=identF)
                    return ins
                P.op("pe", trr, r=[ksrc, KC], w=[("ps", 2 + b3)])
                P.op("dve", lambda e, b3=b3: e.tensor_copy(out=stage[:R, b3 * 512:(b3 + 1) * 512], in_=pbf[2 + b3][:R, :]), r=[("ps", 2 + b3)], w=[kst, "scTok"])
            P.op("sp", lambda e: e.dma_start(out=dram_ap, in_=stage[:R, 0:1536]), r=[kst], w=[okey], dma=True)
            OUTKEYS.append(okey)
        rows_out(raw_s, "raw_s", NS, scTok[:, 0:1536], "stg0", nconv_s[:, 2, :], "o_ncs2")
        rows_out(ncv, "ncv", 3, scTok[:, 1536:3072], "stg1", nconv_p[:, :], "o_ncp")
        _main_ops = P.ops
        P.ops = _saved_ops
        P.ops.extend(Prog.merge(_main_ops, pch_ops))
        P.barrier()
        if STOP_AFTER == "A1":
            dbg_dump("xT", xT.rearrange("p a b -> p (a b)"), [("xT",)])
            dbg_dump("z_tok", z_tok.rearrange("p a b -> p (a b)"), ["z_tok"])
            dbg_dump("zT_s", zT_s.rearrange("p a b -> p (a b)"), ["zT_s"])
            dbg_dump("dtraw", dtraw.rearrange("p a b -> p (a b)"), ["dtraw"])
            dbg_dump("dtT_s", dtT_s, ["dtT_s"])
            finish()
            return nc

        sc.reset(m_A)
        tA = ssd_temps(sc)
        for c in range(8):
            ssd_chunk(tA, xT, c, dtraw[:, c, :], "dtraw", main=True, ztok_c=z_tok[:, c, :], ssdg_bc=ssdg_bc)
        if STOP_AFTER == "A2a":
            P.barrier()
            dbg_dump("YT", YT.rearrange("p a b -> p (a b)"), [("YTb", c) for c in range(8)])
            finish()
            return nc
        hout = sc.f32([8, 128])
        for half in range(2):
            def trh(e, half=half):
                ins = None
                for jj in range(4):
                    q = half * 4 + jj
                    ins = e.transpose(out=pbf[3 + half][:, jj * 128:(jj + 1) * 128], in_=hT[:, q * 128:(q + 1) * 128], identity=identF)
                return ins
            P.op("pe", trh, r=["hT", KC], w=[("ps", 3 + half)])
            P.op("dve", lambda e, half=half: e.tensor_copy(out=hout[:, half * 4:(half + 1) * 4, :].rearrange("p a b -> p (a b)"), in_=pbf[3 + half]),
                 r=[("ps", 3 + half)], w=["hout"])
        P.op("sp", lambda e: e.dma_start(out=nssm_p.rearrange("(q r) n -> r q n", r=128), in_=hout), r=["hout"], w=["o_nsp"], dma=True)
        OUTKEYS.append("o_nsp")

        if STOP_AFTER == "A2b":
            P.barrier()
            dbg_dump("YT", YT.rearrange("p a b -> p (a b)"), [("YTb", c) for c in range(8)])
            finish()
            return nc
        P.barrier()
        sc.reset(m_A)
        sm_s = sc.f32([8, NS])
        cat = sc.f32([32])
        rep = sc.f32([8, 32])
        dtx = sc.f32([8, NS])
        ys = sc.f32([8, NS])
        ysq = sc.f32([8, NS])
        rs = sc.f32([2, NS])
        BC_tok = sc.bf16([512])
        selB = sc.bf16([NS, 128])
        NHS = 4
        hs = [sc.f32([8, 128]) for _ in range(NHS)]
        junk = sc.f32([128])
        x0 = sm_s[:NS, 0, :]; e1 = sm_s[:NS, 1, :]; anp = sm_s[:NS, 2, 0:1]
        P.op("act", lambda e: e.activation(out=e1, in_=dtT_s[:NS, :], func=AF.Exp, bias=dtb_pp[:NS, :]), r=["dtT_s", KC], w=["s_e1"])
        P.op("act", lambda e: e.activation(out=cat[:NS, 16:32], in_=e1, func=AF.Ln, bias=1.0), r=["s_e1"], w=["s_dt"])
        P.op("act", lambda e: e.activation(out=anp, in_=al_pp[:NS, :], func=AF.Exp), r=[KC], w=["s_anp0"])
        P.op("dve", lambda e: e.tensor_scalar(out=anp, in0=anp, scalar1=-1.0, scalar2=None, op0=ALU.mult), r=["s_anp0"], w=["s_anp"])
        P.op("act", lambda e: e.activation(out=cat[:NS, 0:16], in_=cat[:NS, 16:32], func=AF.Exp, scale=anp), r=["s_dt", "s_anp"], w=["s_dA"])

        def repf(e):
            ins = None
            for q in range(8):
                ins = e.matmul(pbf[0][:, q * 32:(q + 1) * 32], lhsT=Rsel[:NS, q * 128:(q + 1) * 128], rhs=cat[:NS, :], start=True, stop=True)
            return ins
        P.op("pe", repf, r=["s_dA", "s_dt", "cst2"], w=[("ps", 0)])
        P.op("dve", lambda e: e.tensor_copy(out=rep.rearrange("p a b -> p (a b)"), in_=pbf[0][:, 0:256]), r=[("ps", 0)], w=["rep"])
        xs_s = xT[:, 0:8, TM:T]
        P.op("dve", lambda e: e.tensor_tensor(out=dtx, in0=rep[:, :, 16:32], in1=xs_s, op=ALU.mult), r=["rep", ("xT",)], w=["dtx"])
        psbc = pbb[1][:NS, 0:512].rearrange("p (a b) -> p a b", a=4)

        def trbc(e):
            ins = None
            for jj in range(4):
                ins = e.transpose(out=psbc[:, jj, :], in_=xT[:, 8 + jj, TM:T], identity=identB)
            return ins
        P.op("pe", trbc, r=[("xT",), "identB"], w=[("ps", 1)])
        P.op("act", lambda e: e.activation(out=BC_tok[:NS, :], in_=pbb[1][:NS, 0:512], func=AF.Copy), r=[("ps", 1)], w=["BC_tok"])
        P.op("dve", lambda e: e.tensor_copy(out=selB[:NS, :, :], in_=identB[:NS, 0:NS].unsqueeze(2).broadcast_to([NS, NS, 128])), r=["identB"], w=["selB"])
        P.op("dve", lambda e: e.memset(ys, 0.0), w=["ys"])

        def hs_load(b):
            P.op("sp", lambda e, b=b: e.dma_start(out=hs[b % NHS], in_=sssm[b].rearrange("(q r) n -> r q n", r=128)), w=[("hs", b % NHS, q) for q in range(8)], dma=True)
        for b in range(min(NHS - 1, NS)):
            hs_load(b)
        for b in range(NS):
            hb = hs[b % NHS]
            bank = 2 + b % 2
            if b + NHS - 1 < NS:
                hs_load(b + NHS - 1)
            P.op("pe", lambda e, b=b, bank=bank: e.matmul(pbf[bank], lhsT=selB[:NS, b, :], rhs=BC_tok[:NS, :], start=True, stop=True),
                 r=["selB", "BC_tok"], w=[("ps", bank)])
            for q in range(8):
                P.op("act", lambda e, b=b, q=q, hb=hb: e.activation(out=hb[:, q, :], in_=hb[:, q, :], func=AF.Copy, scale=rep[:, q, b:b + 1]),
                     r=[("hs", b % NHS, q), "rep"], w=[("hs", b % NHS, q)])

            def upd(q, b=b, hb=hb, bank=bank):
                g = q // 4
                P.op("dve", lambda e: e.scalar_tensor_tensor(out=hb[:, q, :], in0=pbf[bank][:, g * 128:(g + 1) * 128], scalar=dtx[:, q, b:b + 1], op0=ALU.mult,
                                                             in1=hb[:, q, :], op1=ALU.add), r=[("ps", bank), "dtx", ("hs", b % NHS, q)], w=[("hs", b % NHS, q)])

            def yacc(q, b=b, hb=hb, bank=bank):
                g = q // 4
                P.op("dve", lambda e: e.scalar_tensor_tensor(out=junk, in0=hb[:, q, :], scalar=1.0, op0=ALU.mult,
                                                             in1=pbf[bank][:, 256 + g * 128:256 + (g + 1) * 128], op1=ALU.mult,
                                                             accum_out=ys[:, q, b:b + 1]), r=[("hs", b % NHS, q), ("ps", bank)], w=["ys", "junk"])
            upd(0)
            upd(1)
            for q in range(8):
                yacc(q)
                if q + 2 < 8:
                    upd(q + 2)
            ok = ("o_nss", b)
            P.op("sp", lambda e, b=b, hb=hb: e.dma_start(out=nssm_s[b].rearrange("(q r) n -> r q n", r=128), in_=hb), r=[("hs", b % NHS, q) for q in range(8)], w=[ok], dma=True)
            OUTKEYS.append(ok)
        P.op("dve", lambda e: e.tensor_tensor(out=ysq, in0=xs_s, in1=D_pp.unsqueeze(2).broadcast_to([128, 8, NS]), op=ALU.mult), r=[("xT",), KC], w=["ysq"])
        P.op("dve", lambda e: e.tensor_tensor(out=ys, in0=ys, in1=ysq, op=ALU.add), r=["ys", "ysq"], w=["ys"])
        P.op("dve", lambda e: e.tensor_tensor(out=ys, in0=ys, in1=zT_s, op=ALU.mult), r=["ys", "zT_s"], w=["ys"])
        P.op("dve", lambda e: e.tensor_tensor(out=ysq, in0=ys, in1=ys, op=ALU.mult), r=["ys", "ysq"], w=["ysq"])

        def gsum(e):
            ins = None
            for q in range(8):
                g = q // 4
                ins = e.matmul(pbf[4][:, g * NS:(g + 1) * NS], lhsT=onesF, rhs=ysq[:, q, :], start=(q % 4 == 0), stop=(q % 4 == 3))
            return ins
        P.op("pe", gsum, r=["ysq", KC], w=[("ps", 4)])
        rsf = rs.rearrange("p a b -> p (a b)")
        P.op("act", lambda e: e.activation(out=rsf, in_=pbf[4][:, 0:2 * NS], func=AF.Ln, scale=1.0 / 512, bias=EPS), r=[("ps", 4)], w=["rs0"])
        P.op("act", lambda e: e.activation(out=rsf, in_=rsf, func=AF.Exp, scale=-0.5), r=["rs0"], w=["rs"])
        for g in range(2):
            P.op("dve", lambda e, g=g: e.tensor_tensor(out=ys[:, g * 4:(g + 1) * 4, :], in0=ys[:, g * 4:(g + 1) * 4, :],
                                                       in1=rs[:, g:g + 1, :].broadcast_to([128, 4, NS]), op=ALU.mult), r=["ys", "rs"], w=["ys"])
        P.op("dve", lambda e: e.tensor_tensor(out=YT[:, 8:16, TM:T], in0=ys, in1=sg_pp.unsqueeze(2).broadcast_to([128, 8, NS]), op=ALU.mult),
             r=["ys", KC], w=[("YTb", 8)])
        P.barrier()
        if STOP_AFTER == "A2":
            dbg_dump("YT", YT.rearrange("p a b -> p (a b)"), [("YTb", c) for c in range(9)])
            finish()
            return nc
        sc.reset()
        vng_bc = sc.f32([1024])
        bs_bc = sc.f32([8, 128])
        v_tok = sc.bf16([9, 1024])
        ya = sc.f32([8, T])
        WsT = sc.bf16([8, 128])
        wsraw = ya[:, 0, 0:1024].rearrange("p (a b) -> p a b", a=8)
        gv = [sc.f32([1024]) for _ in range(2)]
        vst = [sc.f32([4]) for _ in range(2)]
        ug = [sc.f32([T]) for _ in range(2)]
        _m_tm = sc.mark()
        tmpm = [sc.f32([512]) for _ in range(2)]
        _m_tm2 = sc.mark()
        sc.reset(_m_tm)
        vsf = sc.f32([1024])
        sc.reset(_m_tm2)
        sq = [sc.bf16([T]) for _ in range(2)]
        rstd_bc = sc.f32([T])
        w00I = sc.bf16([8, NS])
        P.op("sp", lambda e: e.dma_start(out=vng_bc, in_=vn_g.partition_broadcast(128)), w=["vng"], dma=True)
        P.op("sp", lambda e: e.dma_start(out=bs_bc.rearrange("p a b -> p (a b)"), in_=cst2_d[:, C2_BS:C2_BS + 1024]), w=["bs_bc"], dma=True)
        P.op("sp", lambda e: e.dma_start(out=wsraw, in_=ws_d.rearrange("h t s -> t h s")), w=["wsraw"], dma=True)
        wv2 = [wload(wview(w_in, COL_V, 512)), wload(wview(w_in, COL_V + 512, 512))]
        for half in range(2):
            def trw(e, half=half):
                ins = None
                for jj in range(4):
                    h_ = half * 4 + jj
                    ins = e.transpose(out=pbf[half][:, jj * 128:(jj + 1) * 128], in_=wsraw[:, h_, :], identity=identF)
                return ins
            P.op("pe", trw, r=["wsraw", KC], w=[("ps", half)])
            P.op("dve", lambda e, half=half: e.tensor_tensor(out=WsT[:, half * 4:(half + 1) * 4, :], in0=pbf[half].rearrange("p (a b) -> p a b", a=4),
                                                             in1=triF.unsqueeze(1).broadcast_to([128, 4, 128]), op=ALU.mult), r=[("ps", half), KC], w=["WsT"])
        for h_ in range(8):
            P.op("dve", lambda e, h_=h_: e.tensor_scalar(out=w00I[:NS, h_, :], in0=identF[:NS, 0:NS], scalar1=w00_bc[:NS, h_:h_ + 1], scalar2=None, op0=ALU.mult),
                 r=[KC], w=["w00I"])
        for tt in range(9):
            rows = 128 if tt < 8 else NS
            i = tt % 2
            tcols = slice(tt * 128, tt * 128 + rows)
            for zb in range(2):
                bank = 2 + (tt * 2 + zb) % 4
                mm_group(pbf[bank][:rows, :], [(nT[:, k, tcols], wv2[zb][0][:, k, :]) for k in range(16)], r=[wv2[zb][1], ("nT", tt)], w=[("ps", bank)])
                P.op("act", lambda e, i=i, zb=zb, bank=bank, rows=rows: e.activation(out=gv[i][:rows, zb * 512:(zb + 1) * 512], in_=pbf[bank][:rows, :],
                                                                                    func=AF.Gelu_apprx_tanh), r=[("ps", bank)], w=[("gv", i, zb)])
            P.op("act", lambda e, i=i, rows=rows: e.activation(out=ug[i][:rows, 0:1024], in_=gv[i][:rows, :], func=AF.Square, accum_out=vst[i][:rows, 0:1]),
                 r=[("gv", i, 0), ("gv", i, 1)], w=[("ug", i), ("vst", i)])
            P.op("act", lambda e, i=i, rows=rows: e.activation(out=vst[i][:rows, 1:2], in_=vst[i][:rows, 0:1], func=AF.Ln, scale=1.0 / 1024, bias=EPS),
                 r=[("vst", i)], w=[("vst", i)])
            P.op("act", lambda e, i=i, rows=rows: e.activation(out=vst[i][:rows, 2:3], in_=vst[i][:rows, 1:2], func=AF.Exp, scale=-0.5),
                 r=[("vst", i)], w=[("vst", i)])
            P.op("dve", lambda e, i=i, rows=rows, tt=tt: e.scalar_tensor_tensor(out=v_tok[:rows, tt, :], in0=gv[i][:rows, :], scalar=vst[i][:rows, 2:3], op0=ALU.mult,
                                                                                in1=vng_bc[:rows, :], op1=ALU.mult),
                 r=[("gv", i, 0), ("gv", i, 1), ("vst", i), "vng"], w=[("v_tok", tt)])
            if tt == 8:
                P.op("dve", lambda e, i=i: e.scalar_tensor_tensor(out=vsf[:NS, :], in0=gv[i][:NS, :], scalar=vst[i][:NS, 2:3], op0=ALU.mult,
                                                                  in1=vng_bc[:NS, :], op1=ALU.mult), r=[("gv", i, 0), ("gv", i, 1), ("vst", i), "vng"], w=["vsf", ("tmpm", 0), ("tmpm", 1)])
                P.op("sp", lambda e: e.dma_start(out=v_s[:, :], in_=vsf[:NS, :]), r=["vsf", ("tmpm", 0), ("tmpm", 1)], w=["o_vs"], dma=True)
                OUTKEYS.append("o_vs")
        wu2 = [wload(wview(w_in, COL_U, 512)), wload(wview(w_in, COL_U + 512, 512))]
        tok_blocks = [(0, 512), (512, 1024), (TM, T)]
        grpc = [0]

        def stage_G(h_):
            ub, jj = h_ // 4, h_ % 4
            Wv, kWu = wu2[ub]
            ui = h_ % 2
            for nb, (lo, hi) in enumerate(tok_blocks):
                bank = grpc[0] % 2
                grpc[0] += 1
                n = hi - lo
                rk = [("nT", tt) for tt in range(nb * 4, nb * 4 + 4)] if nb < 2 else [("nT", 8)]
                mm_group(pbf[bank][:, 0:n], [(Wv[:, k, jj * 128:(jj + 1) * 128], nT[:, k, lo:hi]) for k in range(16)], r=[kWu] + rk, w=[("ps", bank)])
                P.op("act", lambda e, ui=ui, lo=lo, hi=hi, n=n, bank=bank: e.activation(out=ug[ui][:, lo:hi], in_=pbf[bank][:, 0:n], func=AF.Gelu_apprx_tanh),
                     r=[("ps", bank)], w=[("ug", ui)])

        def stage_M(h_):
            ui = h_ % 2
            for half in range(2):
                def mix(e, half=half, h_=h_):
                    ins = None
                    for cc in range(4):
                        c = half * 4 + cc
                        ins = e.matmul(pbf[2 + half][:, cc * 128:(cc + 1) * 128], lhsT=v_tok[:, c, h_ * 128:(h_ + 1) * 128], rhs=WsT[:, h_, :], start=True, stop=True)
                    return ins
                P.op("pe", mix, r=[("v_tok", c) for c in range(half * 4, half * 4 + 4)] + ["WsT"], w=[("ps", 2 + half)])
                tm = tmpm[half]
                P.op("dve", lambda e, half=half, h_=h_, tm=tm: e.tensor_tensor(out=tm.rearrange("p (a b) -> p a b", a=4), in0=pbf[2 + half].rearrange("p (a b) -> p a b", a=4),
                                                                               in1=bs_bc[:, h_:h_ + 1, :].broadcast_to([128, 4, 128]), op=ALU.add),
                     r=[("ps", 2 + half), "bs_bc"], w=[("tmpm", half)])
                P.op("dve", lambda e, half=half, h_=h_, tm=tm, ui=ui: e.tensor_tensor(out=ya[:, h_, half * 512:(half + 1) * 512], in0=tm, in1=ug[ui][:, half * 512:(half + 1) * 512], op=ALU.mult),
                     r=[("tmpm", half), ("ug", ui)], w=[("ya", h_)])
            P.op("pe", lambda e, h_=h_: e.matmul(pbf[4][:, 0:NS], lhsT=v_tok[:NS, 8, h_ * 128:(h_ + 1) * 128], rhs=w00I[:NS, h_, :], start=True, stop=True),
                 r=[("v_tok", 8), "w00I"], w=[("ps", 4)])
            P.op("dve", lambda e, h_=h_, ui=ui: e.scalar_tensor_tensor(out=ya[:, h_, TM:T], in0=pbf[4][:, 0:NS], scalar=b0_bc[:, h_:h_ + 1], op0=ALU.add, in1=ug[ui][:, TM:T], op1=ALU.mult),
                 r=[("ps", 4), ("ug", ui), KC], w=[("ya", h_)])
            si = h_ % 2
            P.op("act", lambda e, h_=h_, si=si: e.activation(out=sq[si], in_=ya[:, h_, :], func=AF.Square), r=[("ya", h_)], w=[("sq", si)])

        def stage_S(h_):
            si = h_ % 2
            for nb, (lo, hi) in enumerate(tok_blocks):
                n = hi - lo
                P.op("pe", lambda e, nb=nb, lo=lo, hi=hi, n=n, si=si, h_=h_: e.matmul(pbf[5 + nb][:, 0:n], lhsT=onesB, rhs=sq[si][:, lo:hi], start=(h_ == 0), stop=(h_ == 7)),
                     r=[("sq", si), "onesB"], w=[("ps", 5 + nb)])

        stage_G(0)
        for h_ in range(8):
            if h_ + 1 < 8:
                stage_G(h_ + 1)
            stage_M(h_)
            if h_ >= 1:
                stage_S(h_ - 1)
        stage_S(7)
        for nb, (lo, hi) in enumerate(tok_blocks):
            n = hi - lo
            P.op("act", lambda e, nb=nb, lo=lo, hi=hi, n=n: e.activation(out=rstd_bc[:, lo:hi], in_=pbf[5 + nb][:, 0:n], func=AF.Ln, scale=1.0 / 1024, bias=EPS),
                 r=[("ps", 5 + nb)], w=[("rstd", nb)])
            P.op("act", lambda e, lo=lo, hi=hi: e.activation(out=rstd_bc[:, lo:hi], in_=rstd_bc[:, lo:hi], func=AF.Exp, scale=-0.5), r=[("rstd", nb)], w=[("rstd", nb)])
        for h_ in range(8):
            P.op("dve", lambda e, h_=h_: e.scalar_tensor_tensor(out=YT[:, h_, :], in0=ya[:, h_, :], scalar=gon_pp[:, h_:h_ + 1], op0=ALU.mult, in1=rstd_bc, op1=ALU.mult),
                 r=[("ya", h_), ("rstd", 0), ("rstd", 1), ("rstd", 2), KC], w=[("YTa", h_)])
        P.barrier()
        if STOP_AFTER == "A3":
            dbg_dump("YT", YT.rearrange("p a b -> p (a b)"), [("YTa", h_) for h_ in range(8)])
            finish()
            return nc

        sc.reset()
        hres = sc.f32([9, DM])
        m_B = sc.mark()
        for tt in range(9):
            rows = 128 if tt < 8 else NS
            src = xm[tt * 128:(tt + 1) * 128, :] if tt < 8 else xsmp[:, :]
            P.op("sp", lambda e, tt=tt, rows=rows, src=src: e.dma_start(out=hres[:rows, tt, :], in_=src), w=[("h", tt, cb) for cb in range(4)], dma=True)
        wcur = wload(wview(w_out, 0, 512))
        for cb in range(4):
            Wv, kW = wcur
            if cb + 1 < 4:
                wcur = wload(wview(w_out, (cb + 1) * 512, 512))
            for tt in range(9):
                rows = 128 if tt < 8 else NS
                tcols = slice(tt * 128, tt * 128 + rows)
                bank = (cb * 9 + tt) % 6
                mm_group(pbf[bank][:rows, :], [(YT[:, k, tcols], Wv[:, k, :]) for k in range(16)], r=[kW], w=[("ps", bank)])
                P.op("dve", lambda e, rows=rows, tt=tt, cb=cb, bank=bank: e.tensor_tensor(out=hres[:rows, tt, cb * 512:(cb + 1) * 512], in0=pbf[bank][:rows, :],
                                                                                         in1=hres[:rows, tt, cb * 512:(cb + 1) * 512], op=ALU.add),
                     r=[("ps", bank), ("h", tt, cb)], w=[("h", tt, cb)])
        P.barrier()
        if STOP_AFTER == "B":
            dbg_dump("h", hres.rearrange("p a b -> p (a b)"), [("h", tt) for tt in range(9)])
            finish()
            return nc

        sc.reset(m_B)
        gbc = sc.f32([DM])
        gbc2 = sc.f32([DM])
        stmp = [sc.f32([512]) for _ in range(2)]
        ST = [sc.f32([4]) for _ in range(2)]
        r2.reset()
        actb = [r2.bf16([2, T]) for _ in range(2)]
        Wd = [r2.bf16([2, DM]) for _ in range(2)]
        XN = [r2.bf16([DM]) for _ in range(2)]
        mT = nT
        P.op("sp", lambda e: e.dma_start(out=gbc, in_=g_ffn.partition_broadcast(128)), w=["gbc"], dma=True)
        P.op("sp", lambda e: e.dma_start(out=gbc2, in_=g_fin.partition_broadcast(128)), w=["gbc2"], dma=True)
        stgC = []
        for tt in range(9):
            rows = 128 if tt < 8 else NS
            i = tt % 2
            stgC.append(rms_transpose(hres[:, tt, :], [("h", tt, cb) for cb in range(4)], rows, XN[i], ("XN", i), ST[i], ("ST", i), gbc, "gbc", mT, tt * 128, ("mT", tt), DM, tt))
        rms_pipeline(stgC)
        NFB = DFF // 256
        if STOP_AFTER == "C0":
            P.barrier()
            dbg_dump("mT", mT.rearrange("p a b -> p (a b)"), [("mT", tt) for tt in range(9)])
            dbg_dump("XN0", XN[0], [("XN", 0)])
            dbg_dump("ST0", ST[0], [("ST", 0)])
            dbg_dump("gbc", gbc, ["gbc"])
            dbg_dump("h", hres.rearrange("p a b -> p (a b)"), [("h", tt) for tt in range(9)])
            finish()
            return nc

        gu_slot = {}

        def load_gu(fb):
            slot = wq[0] % 2
            wq[0] += 1
            gu_slot[fb] = slot
            load_w(WG[slot], wview(w_gate, fb * 256, 256), ("W", slot), nobar=True)
            load_w(WU[slot], wview(w_up, fb * 256, 256), ("W", slot), nobar=True)

        def load_d(fb):
            load_w(Wd[fb % 2], w_down[fb * 256:(fb + 1) * 256, :].rearrange("(fl p) c -> p fl c", p=128), ("Wd", fb % 2))

        gctr = [0]

        def GU_units(fb):
            slot = gu_slot[fb]
            Wg_, Wu_, kWs = WG[slot], WU[slot], ("W", slot)
            units = []
            for fl in range(2):
                for nb, (lo, hi) in enumerate(tok_blocks):
                    def unit(fl=fl, nb=nb, lo=lo, hi=hi):
                        n = hi - lo
                        pr = gctr[0] % 2
                        gctr[0] += 1
                        bg, bu = pr * 2, pr * 2 + 1
                        rk = [("mT", tt) for tt in range(nb * 4, nb * 4 + 4)] if nb < 2 else [("mT", 8)]
                        mm_group(pbf[bg][:, 0:n], [(Wg_[:, k, fl * 128:(fl + 1) * 128], mT[:, k, lo:hi]) for k in range(16)], r=[kWs] + rk, w=[("ps", bg)])
                        mm_group(pbf[bu][:, 0:n], [(Wu_[:, k, fl * 128:(fl + 1) * 128], mT[:, k, lo:hi]) for k in range(16)], r=[kWs] + rk, w=[("ps", bu)])
                        P.op("act", lambda e: e.activation(out=stmp[pr][:, 0:n], in_=pbf[bg][:, 0:n], func=AF.Silu), r=[("ps", bg)], w=[("stmp", pr)])
                        P.op("dve", lambda e: e.tensor_tensor(out=actb[fb % 2][:, fl, lo:hi], in0=pbf[bu][:, 0:n], in1=stmp[pr][:, 0:n], op=ALU.mult),
                             r=[("ps", bu), ("stmp", pr)], w=[("act", fb % 2)])
                    units.append(P.capture(unit))
            return units

        dctr = [0]

        def DN_groups(fb):
            groups = []
            for tt in range(9):
                rows = 128 if tt < 8 else NS
                tcols = slice(tt * 128, tt * 128 + rows)
                for cb in range(4):
                    def grp_(tt=tt, rows=rows, tcols=tcols, cb=cb):
                        bank = 4 + dctr[0] % 4
                        dctr[0] += 1
                        mm_group(pbf[bank][:rows, :], [(actb[fb % 2][:, fl, tcols], Wd[fb % 2][:, fl, cb * 512:(cb + 1) * 512]) for fl in range(2)],
                                 r=[("act", fb % 2), ("Wd", fb % 2)], w=[("ps", bank)])
                        P.op("dve", lambda e: e.tensor_tensor(out=hres[:rows, tt, cb * 512:(cb + 1) * 512], in0=pbf[bank][:rows, :],
                                                              in1=hres[:rows, tt, cb * 512:(cb + 1) * 512], op=ALU.add),
                             r=[("ps", bank), ("h", tt, cb)], w=[("h", tt, cb)])
                    groups.append(P.capture(grp_))
            return groups

        load_gu(0)
        load_d(0)
        for i in range(NFB + 1):
            if i + 1 < NFB:
                load_gu(i + 1)
            gu = GU_units(i) if i < NFB else []
            dn = DN_groups(i - 1) if i >= 1 else []
            nslot = max(len(gu), 1)
            per = -(-len(dn) // nslot)
            for u in range(nslot):
                if u < len(gu):
                    P.ops.extend(gu[u])
                for g_ in dn[u * per:(u + 1) * per]:
                    P.ops.extend(g_)
            if i + 1 < NFB:
                load_d(i + 1)
        P.barrier()
        if STOP_AFTER == "C":
            dbg_dump("h", hres.rearrange("p a b -> p (a b)"), [("h", tt) for tt in range(9)])
            finish()
            return nc

        for tt in range(9):
            rows = 128 if tt < 8 else NS
            i = tt % 2
            ht = hres[:, tt, :]
            st = ST[i]
            P.op("act", lambda e, rows=rows, ht=ht, st=st, i=i: e.activation(out=XN[i][:rows, :], in_=ht[:rows, :], func=AF.Square, accum_out=st[:rows, 0:1]),
                 r=[("h", tt, cb) for cb in range(4)], w=[("XN", i), ("ST", i)])
            P.op("act", lambda e, rows=rows, st=st: e.activation(out=st[:rows, 1:2], in_=st[:rows, 0:1], func=AF.Ln, scale=1.0 / DM, bias=EPS), r=[("ST", i)], w=[("ST", i)])
            P.op("act", lambda e, rows=rows, st=st: e.activation(out=st[:rows, 2:3], in_=st[:rows, 1:2], func=AF.Exp, scale=-0.5), r=[("ST", i)], w=[("ST", i)])
            P.op("dve", lambda e, rows=rows, ht=ht, st=st: e.scalar_tensor_tensor(out=ht[:rows, :], in0=ht[:rows, :], scalar=st[:rows, 2:3], op0=ALU.mult, in1=gbc2[:rows, :], op1=ALU.mult),
                 r=[("h", tt, cb) for cb in range(4)] + [("ST", i), "gbc2"], w=[("h", tt, cb) for cb in range(4)])
            dst = y_m[tt * 128:(tt + 1) * 128, :] if tt < 8 else y_s[:, :]
            ok = ("o_y", tt)
            P.op("sp", lambda e, rows=rows, ht=ht, dst=dst: e.dma_start(out=dst, in_=ht[:rows, :]), r=[("h", tt, cb) for cb in range(4)], w=[ok], dma=True)
            OUTKEYS.append(ok)
        finish()
        return nc


def _host_consts(inputs, core):
    hf = core % 2
    cst = np.zeros((128, CSTW), np.float32)
    r = np.arange(128)
    cst[:, C_ID:C_ID + 128] = np.eye(128, dtype=np.float32)
    cst[:, C_TRI:C_TRI + 128] = (r[:, None] <= r[None, :]).astype(np.float32)
    cst[:, C_U:C_U + 128] = (r[:, None] > r[None, :]).astype(np.float32)
    cst[:, C_ONE:C_ONE + 128] = 1.0
    cw = np.asarray(inputs["ssd_conv_w"])[0]
    cb = np.asarray(inputs["ssd_conv_b"])[0]
    cwb = np.concatenate([cw, cb[None]], 0)
    cst[:, C_CWB:C_CWB + 60] = cwb.reshape(5, 12, 128).transpose(2, 1, 0).reshape(128, 60)
    cst[:, C_GON:C_GON + 8] = np.asarray(inputs["chunk_out_norm_g"])[0].reshape(8, 128).T
    cst[:, C_ALOG:C_ALOG + 16] = np.asarray(inputs["ssd_a_log"])[0][None, :]
    cst[:, C_DTB:C_DTB + 16] = np.asarray(inputs["ssd_dt_bias"])[0][None, :]
    cst[:, C_D:C_D + 16] = np.asarray(inputs["ssd_d"])[0][None, :]
    cst[:, C_W00:C_W00 + 8] = np.asarray(inputs["chunk_w_s"])[0][:, 0, 0][None, :]
    cst[:, C_B0:C_B0 + 8] = np.asarray(inputs["chunk_b_s"])[0][:, 0][None, :]
    cst[:, C_FLAG] = float(hf)
    cst[:, C_DPP:C_DPP + 8] = np.repeat(np.asarray(inputs["ssd_d"])[0], 64).reshape(8, 128).T
    cst[:, C_SGPP:C_SGPP + 8] = np.asarray(inputs["ssd_norm_g"])[0].reshape(8, 128).T
    cst[0:16, C_DTBPP] = np.asarray(inputs["ssd_dt_bias"])[0]
    cst[0:16, C_ALPP] = np.asarray(inputs["ssd_a_log"])[0]
    cst2 = np.zeros((128, CST2W), np.float32)
    cst2[:, C2_BS:C2_BS + 1024] = np.asarray(inputs["chunk_b_s"])[0].reshape(1, 1024)
    rs = np.zeros((16, 8, 128), np.float32)
    for q in range(8):
        for rr in range(128):
            rs[2 * q + rr // 64, q, rr] = 1.0
    cst2[0:16, C2_RSEL:C2_RSEL + 1024] = rs.reshape(16, 1024)
    return cst, cst2


def make_in_maps(inputs):
    xp = np.asarray(inputs["x_prompt"], np.float32)
    xs = np.asarray(inputs["x_sample"], np.float32)
    sconv = np.asarray(inputs["state_conv"], np.float32)[0]
    sssm = np.asarray(inputs["state_ssm"], np.float32)[0]
    shared = {
        "w_in": np.ascontiguousarray(np.asarray(inputs["w_in"], np.float32)[0]),
        "w_out": np.ascontiguousarray(np.asarray(inputs["w_out"], np.float32)[0]),
        "w_gate": np.ascontiguousarray(np.asarray(inputs["w_gate"], np.float32)[0]),
        "w_up": np.ascontiguousarray(np.asarray(inputs["w_up"], np.float32)[0]),
        "w_down": np.ascontiguousarray(np.asarray(inputs["w_down"], np.float32)[0]),
        "ws": np.ascontiguousarray(np.asarray(inputs["chunk_w_s"], np.float32)[0]),
        "g_mix": np.ascontiguousarray(np.asarray(inputs["norm_mix_g"], np.float32)[0]),
        "g_ffn": np.ascontiguousarray(np.asarray(inputs["norm_ffn_g"], np.float32)[0]),
        "g_fin": np.ascontiguousarray(np.asarray(inputs["norm_final_g"], np.float32)),
        "vn_g": np.ascontiguousarray(np.asarray(inputs["chunk_v_norm_g"], np.float32)[0]),
        "ssd_g": np.ascontiguousarray(np.asarray(inputs["ssd_norm_g"], np.float32)[0]),
    }
    zeros_prev = np.zeros((TM, DM), np.float32)
    maps = []
    for c in range(8):
        b, hf = c // 2, c % 2
        cst, cst2 = _host_consts(inputs, c)
        m = dict(shared)
        m["xm"] = np.ascontiguousarray(xp[b, hf * TM:(hf + 1) * TM])
        m["xprev"] = np.ascontiguousarray(xp[b, 0:TM]) if hf == 1 else zeros_prev
        m["xsmp"] = np.ascontiguousarray(xs[c * NS:(c + 1) * NS, 0])
        m["sconv"] = np.ascontiguousarray(sconv[c * NS:(c + 1) * NS])
        m["sssm"] = np.ascontiguousarray(sssm[c * NS:(c + 1) * NS].reshape(NS, 1024, 128))
        m["cst"] = cst
        m["cst2"] = cst2
        maps.append(m)
    return maps


def kernel(**inputs):
    nc = build_program()
    maps = make_in_maps(inputs)
    res = run_bass_kernel_spmd(nc, maps, core_ids=list(range(8)))
    R = res.results
    y_prompt = np.zeros((4, 2048, DM), np.float32)
    y_sample = np.zeros((128, 1, DM), np.float32)
    ncp = np.zeros((1, 4, 3, 1536), np.float32)
    nsp = np.zeros((1, 4, 16, 64, 128), np.float32)
    ncs = np.zeros((1, 128, 3, 1536), np.float32)
    nss = np.zeros((1, 128, 16, 64, 128), np.float32)
    vs = np.zeros((1, 128, 1, 1024), np.float32)
    for c in range(8):
        b, hf = c // 2, c % 2
        y_prompt[b, hf * TM:(hf + 1) * TM] = R[c]["y_m"]
        y_sample[c * NS:(c + 1) * NS, 0] = R[c]["y_s"]
        if hf == 1:
            ncp[0, b] = R[c]["nconv_p"]
            nsp[0, b] = R[c]["nssm_p"].reshape(16, 64, 128)
        ncs[0, c * NS:(c + 1) * NS] = R[c]["nconv_s"]
        nss[0, c * NS:(c + 1) * NS] = R[c]["nssm_s"].reshape(NS, 16, 64, 128)
        vs[0, c * NS:(c + 1) * NS, 0] = R[c]["v_s"]
    return (y_prompt, y_sample, ncp, nsp, ncs, nss, vs)
```

```python
import numpy as np
from contextlib import ExitStack
import concourse.bass as bass
import concourse.mybir as mybir
from concourse.bass_utils import run_bass_kernel_spmd

F32 = mybir.dt.float32
F32R = mybir.dt.float32r
BF16 = mybir.dt.bfloat16
AF = mybir.ActivationFunctionType
ALU = mybir.AluOpType
AX = mybir.AxisListType

ENGS = ["pe", "act", "dve", "pool", "sp"]
NDMASEM = 12

DEBUG = {}
STOP_AFTER = None
SSD_LEVEL = 9


class Op:
    __slots__ = ("eng", "fn", "r", "w", "dma", "deps", "sig", "sem", "prev_use", "waits", "need_sig", "xdeps", "nobar")

    def __init__(self, eng, fn, r, w, dma, nobar=False):
        self.eng, self.fn, self.r, self.w, self.dma = eng, fn, tuple(r), tuple(w), dma
        self.deps = set()
        self.xdeps = set()
        self.sig = None
        self.sem = None
        self.prev_use = 0
        self.waits = []
        self.need_sig = False
        self.nobar = nobar


BARRIER = Op(None, None, (), (), False)


class Prog:
    def __init__(self):
        self.ops = []

    def op(self, eng, fn, r=(), w=(), dma=False, nobar=False):
        w = list(w) + [k for k in r if isinstance(k, tuple) and k and k[0] == "ps" and k not in w]
        o = Op(eng, fn, r, w, dma, nobar)
        self.ops.append(o)
        return o

    def barrier(self):
        self.ops.append(BARRIER)

    def capture(self, fn):
        saved = self.ops
        self.ops = []
        try:
            fn()
            got = self.ops
        finally:
            self.ops = saved
        return got

    @staticmethod
    def merge(main, side):
        if not side:
            return list(main)
        out = []
        m, s_ = len(main), len(side)
        j = 0
        for i, o in enumerate(main):
            out.append(o)
            tgt = (i + 1) * s_ // m
            while j < tgt:
                out.append(side[j])
                j += 1
        out.extend(side[j:])
        return out

    def resolve(self):
        ops = self.ops
        last_w = {}
        readers = {}
        last_eng = {}
        dmas = []
        pending = {}
        for i, o in enumerate(ops):
            if o is BARRIER:
                s = set(last_eng.values()) | set(dmas)
                for e in ENGS:
                    pending[e] = set(s) | pending.get(e, set())
                continue
            if o.eng in pending and not o.nobar:
                o.xdeps = o.xdeps | pending.pop(o.eng)
            deps = set(o.xdeps)
            for k in o.r:
                if k in last_w:
                    deps.add(last_w[k])
            for k in o.w:
                if k in last_w:
                    deps.add(last_w[k])
                deps.update(readers.get(k, ()))
            deps.discard(i)
            keep = set()
            for d in deps:
                od = ops[d]
                if od.dma:
                    keep.add(d)
                elif od.eng == o.eng and not o.dma:
                    if o.eng == "pe":
                        continue
                    if d in o.xdeps:
                        continue
                    if set(od.w) & set(o.r):
                        keep.add(d)
                else:
                    keep.add(d)
            o.deps = keep
            for d in keep:
                ops[d].need_sig = True
            for k in o.r:
                readers.setdefault(k, []).append(i)
            for k in o.w:
                last_w[k] = i
                readers[k] = []
            if o.dma:
                dmas.append(i)
            elif o.fn is not None:
                last_eng[o.eng] = i
        cnt = {e: 0 for e in ENGS}
        dma_n = {e: 0 for e in ENGS}
        dma_use = {}
        for i, o in enumerate(ops):
            if o is BARRIER:
                continue
            if o.dma:
                slot = (o.eng, dma_n[o.eng] % NDMASEM)
                dma_n[o.eng] += 1
                u = dma_use.get(slot, 0)
                o.prev_use = u
                dma_use[slot] = u + 1
                o.sem = slot
                o.sig = 16 * (u + 1)
            elif o.need_sig:
                cnt[o.eng] += 1
                o.sem = o.eng
                o.sig = cnt[o.eng]
        seen = {e: {} for e in ENGS}
        for i, o in enumerate(ops):
            if o is BARRIER:
                continue
            sd = seen[o.eng]
            need = {}
            if o.dma and o.prev_use > 0:
                need[o.sem] = 16 * o.prev_use
            for d in o.deps:
                od = ops[d]
                need[od.sem] = max(need.get(od.sem, 0), od.sig)
            o.waits = []
            for s, v in need.items():
                if sd.get(s, 0) >= v:
                    continue
                sd[s] = v
                o.waits.append((s, v))

    def emit(self, sems, block):
        ops = self.ops

        def run(eng_name):
            def body(e):
                for o in ops:
                    if o is BARRIER or o.eng != eng_name:
                        continue
                    for s, v in o.waits:
                        e.wait_ge(sems[s], v)
                    if o.fn is None:
                        continue
                    ins = o.fn(e)
                    if o.dma:
                        ins.then_inc(sems[o.sem], 16)
                    elif o.sig is not None:
                        ins.then_inc(sems[o.sem], 1)
            return body

        block.sync(run("sp"))
        block.tensor(run("pe"))
        block.scalar(run("act"))
        block.vector(run("dve"))
        block.gpsimd(run("pool"))


DM = 2048
DIN = 4624
DFF = 5632
TM = 1024
NS = 16
T = TM + NS
EPS = 1e-6
COL_U, COL_V, COL_Z, COL_X, COL_B, COL_C, COL_DT = 0, 1024, 2048, 3072, 4096, 4352, 4608

C_ID, C_TRI, C_U, C_ONE = 0, 128, 256, 384
C_CWB, C_GON, C_ALOG, C_DTB, C_D, C_W00, C_B0, C_FLAG = 512, 572, 580, 596, 612, 628, 636, 644
C_DPP, C_SGPP, C_DTBPP, C_ALPP = 648, 656, 664, 665
CSTW = 672
C2_BS, C2_RSEL = 0, 1024
CST2W = 2048

AW = 51000


class Bump:
    def __init__(self, arena, start, end):
        self.arena, self.start, self.end, self.off = arena, start, end, start
        self.peak = start

    def reset(self, to=None):
        self.off = self.start if to is None else to

    def mark(self):
        return self.off

    def _take(self, words):
        o = self.off
        self.off += words
        assert self.off <= self.end, ("arena overflow", self.off, self.end)
        self.peak = max(self.peak, self.off)
        return o

    def f32(self, shape):
        n = int(np.prod(shape))
        o = self._take(n)
        v = self.arena[:, o:o + n]
        return _shape(v, shape)

    def bf16(self, shape):
        n = int(np.prod(shape))
        w = (n + 1) // 2
        o = self._take(w)
        v = self.arena[:, o:o + w].bitcast(BF16)[:, 0:n]
        return _shape(v, shape)


def _shape(v, shape):
    if len(shape) == 1:
        return v
    if len(shape) == 2:
        return v.rearrange("p (a b) -> p a b", a=shape[0])
    if len(shape) == 3:
        return v.rearrange("p (a b c) -> p a b c", a=shape[0], b=shape[1])
    raise ValueError(shape)


def build_program():
    nc = bass.Bass("TRN2", target_bir_lowering=False)

    def din(name, shape):
        return nc.dram_tensor(name, shape, F32, kind="ExternalInput").ap()

    def dout(name, shape):
        return nc.dram_tensor(name, shape, F32, kind="ExternalOutput").ap()

    xm = din("xm", [TM, DM])
    xprev = din("xprev", [TM, DM])
    xsmp = din("xsmp", [NS, DM])
    sconv = din("sconv", [NS, 3, 1536])
    sssm = din("sssm", [NS, 1024, 128])
    w_in = din("w_in", [DM, DIN])
    w_out = din("w_out", [DM, DM])
    w_gate = din("w_gate", [DM, DFF])
    w_up = din("w_up", [DM, DFF])
    w_down = din("w_down", [DFF, DM])
    cst_d = din("cst", [128, CSTW])
    cst2_d = din("cst2", [128, CST2W])
    ws_d = din("ws", [8, 128, 128])
    g_mix = din("g_mix", [DM])
    g_ffn = din("g_ffn", [DM])
    g_fin = din("g_fin", [DM])
    vn_g = din("vn_g", [1024])
    ssd_g = din("ssd_g", [1024])

    y_m = dout("y_m", [TM, DM])
    y_s = dout("y_s", [NS, DM])
    nconv_p = dout("nconv_p", [3, 1536])
    nssm_p = dout("nssm_p", [1024, 128])
    nconv_s = dout("nconv_s", [NS, 3, 1536])
    nssm_s = dout("nssm_s", [NS, 1024, 128])
    v_s = dout("v_s", [NS, 1024])
    dbg_d = {}
    for name, (shape, dty) in DEBUG.items():
        dbg_d[name] = nc.dram_tensor("dbg_" + name, list(shape), dty, kind="ExternalOutput").ap()

    P = Prog()
    with ExitStack() as es:
        arena = es.enter_context(nc.sbuf_tensor("arena", [128, AW], F32))
        rhsE = es.enter_context(nc.sbuf_tensor("rhsE", [128, 2048], F32))
        U32 = es.enter_context(nc.sbuf_tensor("U32", [128, 128], F32))
        pb = [es.enter_context(nc.psum_tensor(f"pb{i}", [128, 512], F32)) for i in range(8)]
        sems = {}
        for e in ENGS:
            sems[e] = es.enter_context(nc.semaphore("s_" + e))
        for e in ("sp", "pool"):
            for i in range(NDMASEM):
                sems[(e, i)] = es.enter_context(nc.semaphore(f"d_{e}_{i}"))
        block = es.enter_context(nc.Block())

        arena = arena[:, :]
        rhsEr = rhsE[:, :].bitcast(F32R)
        U32r = U32[:, :].bitcast(F32R)
        pbf = [p[:, :] for p in pb]
        pbb = [p[:, :].bitcast(BF16) for p in pb]

        fx = Bump(arena, 0, AW)
        CST = fx.f32([CSTW])
        identF = CST[:, C_ID:C_ID + 128]
        triF = CST[:, C_TRI:C_TRI + 128]
        UF = CST[:, C_U:C_U + 128]
        onesF = CST[:, C_ONE:C_ONE + 128]
        cwb = CST[:, C_CWB:C_CWB + 60].rearrange("p (j k) -> p j k", j=12)
        gon_pp = CST[:, C_GON:C_GON + 8]
        alog_bc = CST[:, C_ALOG:C_ALOG + 16]
        dtb_bc = CST[:, C_DTB:C_DTB + 16]
        D_bc = CST[:, C_D:C_D + 16]
        w00_bc = CST[:, C_W00:C_W00 + 8]
        b0_bc = CST[:, C_B0:C_B0 + 8]
        flag = CST[:, C_FLAG:C_FLAG + 1]
        D_pp = CST[:, C_DPP:C_DPP + 8]
        sg_pp = CST[:, C_SGPP:C_SGPP + 8]
        dtb_pp = CST[:, C_DTBPP:C_DTBPP + 1]
        al_pp = CST[:, C_ALPP:C_ALPP + 1]
        identB = fx.bf16([128])
        onesB = fx.bf16([128])
        aneg = fx.f32([16])
        hT = fx.f32([1024])
        hTb = fx.bf16([1024])
        prefix = fx.f32([12, 3])
        ncv = fx.f32([12, 3])
        nT = fx.bf16([16, T])
        YT = fx.bf16([16, T])
        r2_start = fx.off - (16 * T) // 2
        r2_end = fx.off
        Wfix = [fx.bf16([16, 512]) for _ in range(2)]
        _wflat = [w_.rearrange("p a b -> p (a b)") for w_ in Wfix]
        WG = [w_[:, 0:4096].rearrange("p (a b) -> p a b", a=16) for w_ in _wflat]
        WU = [w_[:, 4096:8192].rearrange("p (a b) -> p a b", a=16) for w_ in _wflat]
        wq = [0]
        S0 = fx.off
        sc = Bump(arena, S0, AW)
        r2 = Bump(arena, r2_start, r2_end)

        KC = ("cst",)

        P.op("sp", lambda e: e.dma_start(out=CST, in_=cst_d[:, :]), w=[KC], dma=True)
        P.op("dve", lambda e: e.tensor_copy(out=identB, in_=identF), r=[KC], w=["identB"])
        P.op("dve", lambda e: e.memset(onesB, 1.0), w=["onesB"])
        P.op("dve", lambda e: e.tensor_copy(out=U32r, in_=UF), r=[KC], w=["U32"])
        P.op("act", lambda e: e.activation(out=aneg, in_=alog_bc, func=AF.Exp), r=[KC], w=["aneg0"])
        P.op("dve", lambda e: e.tensor_scalar(out=aneg, in0=aneg, scalar1=-1.0, scalar2=None, op0=ALU.mult), r=["aneg0"], w=["aneg"])
        P.op("dve", lambda e: e.memset(hT, 0.0), w=["hT"])
        P.op("dve", lambda e: e.memset(hTb, 0.0), w=["hTb"])

        def rms_transpose(xt, kx, rows, xn, kxn, st, kst, gbc, kg, dstT, col0, kdst, width, tag, pre=None):
            nk = width // 128

            def s1():
                if pre is not None:
                    pre()
                P.op("act", lambda e: e.activation(out=xn[:rows, :], in_=xt[:rows, :], func=AF.Square, accum_out=st[:rows, 0:1]),
                     r=[kx] if not isinstance(kx, list) else kx, w=[kxn, kst])
                P.op("act", lambda e: e.activation(out=st[:rows, 1:2], in_=st[:rows, 0:1], func=AF.Ln, scale=1.0 / width, bias=EPS),
                     r=[kst], w=[kst])
                P.op("act", lambda e: e.activation(out=st[:rows, 2:3], in_=st[:rows, 1:2], func=AF.Exp, scale=-0.5),
                     r=[kst], w=[kst])
                P.op("dve", lambda e: e.scalar_tensor_tensor(out=xn[:rows, :], in0=xt[:rows, :], scalar=st[:rows, 2:3], op0=ALU.mult,
                                                             in1=gbc[:rows, :], op1=ALU.mult),
                     r=([kx] if not isinstance(kx, list) else kx) + [kst, kg, kxn], w=[kxn])

            def s2():
                for b8 in range(nk // 8):
                    bank = (tag % 2) * 2 + b8
                    psv = pbb[bank][:, 0:8 * rows].rearrange("p (a b) -> p a b", a=8)

                    def tr(e, b8=b8, psv=psv):
                        ins = None
                        for j in range(8):
                            k = b8 * 8 + j
                            ins = e.transpose(out=psv[:, j, :], in_=xn[:rows, k * 128:(k + 1) * 128], identity=identB[:rows, :rows])
                        return ins
                    P.op("pe", tr, r=[kxn, "identB"], w=[("ps", bank)])
                    dst = dstT[:, b8 * 8:(b8 + 1) * 8, col0:col0 + rows]
                    if b8 == 0:
                        P.op("act", lambda e, dst=dst, psv=psv: e.activation(out=dst, in_=psv, func=AF.Copy), r=[("ps", bank)], w=[kdst])
                    else:
                        P.op("dve", lambda e, dst=dst, psv=psv: e.tensor_copy(out=dst, in_=psv), r=[("ps", bank)], w=[kdst])
            return P.capture(s1), P.capture(s2)

        def rms_pipeline(stages):
            n = len(stages)
            for i in range(n + 1):
                if i < n:
                    P.ops.extend(stages[i][0])
                if i >= 1:
                    P.ops.extend(stages[i - 1][1])

        def mm_group(out, pairs, r, w):
            def fn(e):
                ins = None
                n = len(pairs)
                for i, (l, rh) in enumerate(pairs):
                    ins = e.matmul(out, lhsT=l, rhs=rh, start=(i == 0), stop=(i == n - 1))
                return ins
            P.op("pe", fn, r=r, w=w)

        def load_w(dst, src_ap, key, nobar=False):
            P.op("pool", lambda e: e.dma_start(out=dst, in_=src_ap), w=[key], dma=True, nobar=nobar)

        def wload(src_ap, ncols=512):
            slot = wq[0] % 2
            wq[0] += 1
            buf = Wfix[slot] if ncols == 512 else Wfix[slot][:, :, 0:ncols]
            load_w(buf, src_ap, ("W", slot), nobar=True)
            return buf, ("W", slot)

        def wview(wap, c0, ncols):
            return wap[:, c0:c0 + ncols].rearrange("(kt p) c -> p kt c", p=128)

        def conv_silu(rawpad, krp, acc, kacc, j, ntok, dst, kdst, defer=None):
            P.op("dve", lambda e: e.tensor_scalar(out=acc[:, 0:ntok], in0=rawpad[:, 0:ntok], scalar1=cwb[:, j, 0:1], scalar2=cwb[:, j, 4:5],
                                                  op0=ALU.mult, op1=ALU.add), r=[krp, KC], w=[kacc])
            for k in (1, 2, 3):
                P.op("dve", lambda e, k=k: e.scalar_tensor_tensor(out=acc[:, 0:ntok], in0=rawpad[:, k:k + ntok], scalar=cwb[:, j, k:k + 1],
                                                                  op0=ALU.mult, in1=acc[:, 0:ntok], op1=ALU.add), r=[krp, kacc, KC], w=[kacc])
            silu_ops = P.capture(lambda: P.op("act", lambda e: e.activation(out=dst, in_=acc[:, 0:ntok], func=AF.Silu), r=[kacc], w=[kdst]))
            if defer is None:
                P.ops.extend(silu_ops)
            else:
                defer.append(silu_ops)

        def ssd_temps(b, main=True):
            t = {}
            t["sm"] = [b.f32([96]) for _ in range(2)]
            t["ex"] = b.f32([48])
            t["xdtw"] = b.bf16([16, 64])
            t["B_tok"] = b.bf16([256])
            if not main:
                return t
            t["LT"] = b.f32([2048])
            t["MT"] = b.bf16([16, 128])
            t["xs_tok"] = b.bf16([16, 64])
            t["xdt"] = b.bf16([16, 64])
            t["cbm"] = b.f32([2, 128])
            t["y1"] = b.f32([16, 64])
            t["t2"] = b.f32([16, 64])
            t["yb"] = b.bf16([1024])
            t["gn"] = b.f32([8])
            return t

        def ssd_chunk(t, xT, c, dtraw_c, kdt, main, ztok_c=None, ssdg_bc=None, banks=(0, 1, 2, 7, 1), ktag=""):
            par = c % 2
            sm, ex = t["sm"][par], t["ex"]
            cs = slice(c * 128, (c + 1) * 128)
            kx = ("xT" + ktag,)
            bs_, bx_, bb_ = banks[0], banks[1], banks[2]
            k0, k1, kdtc, kdta, kw2 = ("sm0", par), ("sm1", par), ("dtc", par), ("dta", par), ("w2", par)
            dtc = sm[:, 32:48]
            dta = sm[:, 48:64]
            w2 = sm[:, 64:80]
            toend, dec, ea = ex[:, 0:16], ex[:, 16:32], ex[:, 32:48]
            psx = pbb[bx_][:, 0:1024].rearrange("p (a b) -> p a b", a=8)
            psx3 = pbb[bx_][:, 0:1024].rearrange("p (a b) -> p a b", a=16)
            psB = pbb[bb_][:, 0:256].rearrange("p (a b) -> p a b", a=2)

            def head():
                P.op("dve", lambda e: e.tensor_tensor(out=sm[:, 0:16], in0=dtraw_c, in1=dtb_bc, op=ALU.add), r=[kdt, KC], w=[k0])
                P.op("act", lambda e: e.activation(out=sm[:, 16:32], in_=sm[:, 0:16], func=AF.Exp), r=[k0], w=[k1])
                P.op("act", lambda e: e.activation(out=dtc, in_=sm[:, 16:32], func=AF.Ln, bias=1.0), r=[k1], w=[kdtc])
                P.op("dve", lambda e: e.tensor_tensor(out=dta, in0=dtc, in1=aneg, op=ALU.mult), r=[kdtc, "aneg"], w=[kdta])

            def part1():
                def trx(e):
                    ins = None
                    for q in range(8):
                        ins = e.transpose(out=psx[:, q, :], in_=xT[:, q, cs], identity=identB)
                    return ins
                P.op("pe", trx, r=[kx, "identB"], w=[("ps", bx_)])
                if main:
                    psc = pbf[7][:, 0:256].rearrange("p (a b) -> p a b", a=2)

                    def cbf(e):
                        ins = None
                        for g in range(2):
                            ins = e.matmul(psc[:, g, :], lhsT=xT[:, 8 + g, cs], rhs=xT[:, 10 + g, cs], start=True, stop=True)
                        return ins
                    P.op("pe", cbf, r=[kx], w=[("ps", 7)])
                    P.op("dve", lambda e: e.tensor_tensor(out=t["cbm"], in0=psc, in1=triF.unsqueeze(1).broadcast_to([128, 2, 128]), op=ALU.mult),
                         r=[("ps", 7), KC], w=["cbm"])
                    for e16 in range(16):
                        P.op("dve", lambda e, e16=e16: e.tensor_scalar(out=rhsEr[:, e16 * 128:(e16 + 1) * 128], in0=triF, scalar1=dta[:, e16:e16 + 1],
                                                                       scalar2=None, op0=ALU.mult), r=[kdta, KC], w=[("rhsE", e16 // 4)])

                def small(e):
                    e.matmul(pbf[bs_][:, 0:16], lhsT=UF, rhs=dta, start=True, stop=True)
                    ins = e.matmul(pbf[bs_][:, 16:32], lhsT=onesF, rhs=dta, start=True, stop=True)
                    if main:
                        ins = e.matmul(pbf[bs_][:, 32:48], lhsT=triF, rhs=dta, start=True, stop=True)
                    return ins
                P.op("pe", small, r=[kdta, KC], w=[("ps", bs_)])
                nex = 48 if main else 32
                P.op("act", lambda e: e.activation(out=ex[:, 0:nex], in_=pbf[bs_][:, 0:nex], func=AF.Exp), r=[("ps", bs_)], w=["ex"])
                if main:
                    for i in range(4):
                        P.op("pe", lambda e, i=i: e.matmul(pbf[3 + i], lhsT=U32r, rhs=rhsEr[:, i * 512:(i + 1) * 512], start=True, stop=True),
                             r=[("rhsE", i), "U32"], w=[("ps", 3 + i)])
                        P.op("act", lambda e, i=i: e.activation(out=t["LT"][:, i * 512:(i + 1) * 512], in_=pbf[3 + i], func=AF.Exp),
                             r=[("ps", 3 + i)], w=[("LT", i)])

                def trb(e):
                    ins = None
                    for g in range(2):
                        ins = e.transpose(out=psB[:, g, :], in_=xT[:, 8 + g, cs], identity=identB)
                    return ins
                P.op("pe", trb, r=[kx, "identB"], w=[("ps", bb_)])
                P.op("dve", lambda e: e.tensor_tensor(out=w2, in0=dtc, in1=toend, op=ALU.mult), r=[kdtc, "ex"], w=[kw2])
                P.op("dve", lambda e: e.tensor_tensor(out=t["xdtw"], in0=psx3, in1=w2.unsqueeze(2).broadcast_to([128, 16, 64]), op=ALU.mult),
                     r=[("ps", bx_), kw2], w=["xdtw"])
                P.op("act", lambda e: e.activation(out=t["B_tok"], in_=pbb[bb_][:, 0:256], func=AF.Copy), r=[("ps", bb_)], w=["B_tok"])
                if main:
                    P.op("dve", lambda e: e.tensor_tensor(out=t["xdt"], in0=psx3, in1=dtc.unsqueeze(2).broadcast_to([128, 16, 64]), op=ALU.mult),
                         r=[("ps", bx_), kdtc], w=["xdt"])
                    P.op("act", lambda e: e.activation(out=t["xs_tok"], in_=psx3, func=AF.Copy), r=[("ps", bx_)], w=["xs_tok"])
                    LT3 = t["LT"].rearrange("p (a b) -> p a b", a=16)
                    for g in range(2):
                        P.op("dve", lambda e, g=g: e.tensor_tensor(out=t["MT"][:, g * 8:(g + 1) * 8, :], in0=LT3[:, g * 8:(g + 1) * 8, :],
                                                                   in1=t["cbm"][:, g:g + 1, :].broadcast_to([128, 8, 128]), op=ALU.mult),
                             r=[("LT", 2 * g), ("LT", 2 * g + 1), "cbm"], w=[("MT", g)])

            def part2():
                if main:
                    for g in range(2):
                        def yd(e, g=g):
                            ins = None
                            for j in range(8):
                                e16 = g * 8 + j
                                ins = e.matmul(pbf[3 + g][:, j * 64:(j + 1) * 64], lhsT=t["MT"][:, e16, :], rhs=t["xdt"][:, e16, :], start=True, stop=True)
                            return ins
                        P.op("pe", yd, r=[("MT", g), "xdt"], w=[("ps", 3 + g)])
                        P.op("pe", lambda e, g=g: e.matmul(pbf[5 + g], lhsT=xT[:, 10 + g, cs], rhs=hTb[:, g * 512:(g + 1) * 512], start=True, stop=True),
                             r=[kx, "hTb"], w=[("ps", 5 + g)])
                    y1 = t["y1"]
                    for g in range(2):
                        y1g = y1[:, g * 8:(g + 1) * 8, :]
                        P.op("dve", lambda e, g=g, y1g=y1g: e.tensor_tensor(out=y1g, in0=pbf[5 + g].rearrange("p (a b) -> p a b", a=8),
                                                                            in1=ea[:, g * 8:(g + 1) * 8].unsqueeze(2).broadcast_to([128, 8, 64]), op=ALU.mult),
                             r=[("ps", 5 + g), "ex"], w=[("y1", g)])
                        P.op("dve", lambda e, g=g, y1g=y1g: e.tensor_tensor(out=y1g, in0=pbf[3 + g].rearrange("p (a b) -> p a b", a=8), in1=y1g, op=ALU.add),
                             r=[("ps", 3 + g), ("y1", g)], w=[("y1", g)])
                    P.op("pool", lambda e: e.tensor_tensor(out=t["t2"], in0=t["xs_tok"], in1=D_bc.unsqueeze(2).broadcast_to([128, 16, 64]), op=ALU.mult),
                         r=["xs_tok", KC], w=["t2"])
                    P.op("dve", lambda e: e.tensor_tensor(out=y1, in0=y1, in1=t["t2"], op=ALU.add), r=[("y1", 0), ("y1", 1), "t2"], w=[("y1", 0), ("y1", 1)])
                    y1f = y1.rearrange("p a b -> p (a b)")
                    P.op("dve", lambda e: e.tensor_tensor(out=y1f, in0=y1f, in1=ztok_c, op=ALU.mult), r=[("y1", 0), ("y1", 1), "z_tok"], w=[("y1", 0), ("y1", 1)])
                    gn = t["gn"]
                    t2f = t["t2"].rearrange("p a b -> p (a b)")
                    for g in range(2):
                        P.op("act", lambda e, g=g: e.activation(out=t2f[:, g * 512:(g + 1) * 512], in_=y1f[:, g * 512:(g + 1) * 512], func=AF.Square,
                                                                accum_out=gn[:, g:g + 1]), r=[("y1", g)], w=["t2", ("gn", g)])
                    P.op("act", lambda e: e.activation(out=gn[:, 2:4], in_=gn[:, 0:2], func=AF.Ln, scale=1.0 / 512, bias=EPS), r=[("gn", 0), ("gn", 1)], w=["gn2"])
                    P.op("act", lambda e: e.activation(out=gn[:, 4:6], in_=gn[:, 2:4], func=AF.Exp, scale=-0.5), r=["gn2"], w=["gn4"])
                    for g in range(2):
                        P.op("dve", lambda e, g=g: e.scalar_tensor_tensor(out=t["yb"][:, g * 512:(g + 1) * 512], in0=y1f[:, g * 512:(g + 1) * 512],
                                                                          scalar=gn[:, 4 + g:5 + g], op0=ALU.mult, in1=ssdg_bc[:, g * 512:(g + 1) * 512], op1=ALU.mult),
                             r=[("y1", g), "gn4", "ssdg"], w=[("yb", g)])
                    psy = pbb[2][:, 0:1024].rearrange("p (a b) -> p a b", a=8)

                    def try_(e):
                        ins = None
                        for q in range(8):
                            ins = e.transpose(out=psy[:, q, :], in_=t["yb"][:, q * 128:(q + 1) * 128], identity=identB)
                        return ins
                    P.op("pe", try_, r=[("yb", 0), ("yb", 1), "identB"], w=[("ps", 2)])
                    P.op("act", lambda e: e.activation(out=YT[:, 8:16, cs], in_=psy, func=AF.Copy), r=[("ps", 2)], w=[("YTb", c)])

            def part_state():
                sb = (banks[3], banks[4])
                for g in range(2):
                    P.op("pe", lambda e, g=g: e.matmul(pbf[sb[g]], lhsT=t["B_tok"][:, g * 128:(g + 1) * 128], rhs=t["xdtw"].rearrange("p a b -> p (a b)")[:, g * 512:(g + 1) * 512],
                                                       start=True, stop=True), r=["B_tok", "xdtw"], w=[("ps", sb[g])])
                hT3 = hT.rearrange("p (a b) -> p a b", a=16)
                P.op("dve", lambda e: e.tensor_tensor(out=hT3, in0=hT3, in1=dec.unsqueeze(2).broadcast_to([128, 16, 64]), op=ALU.mult), r=["hT", "ex"], w=["hT"])
                for g in range(2):
                    P.op("dve", lambda e, g=g: e.tensor_tensor(out=hT[:, g * 512:(g + 1) * 512], in0=pbf[sb[g]], in1=hT[:, g * 512:(g + 1) * 512], op=ALU.add),
                         r=[("ps", sb[g]), "hT"], w=["hT"])
                P.op("act", lambda e: e.activation(out=hTb, in_=hT, func=AF.Copy), r=["hT"], w=["hTb"])
            return P.capture(head), P.capture(part1), P.capture(part2) + P.capture(part_state)

        def ssd_emit(chunks):
            n = len(chunks)
            P.ops.extend(chunks[0][0])
            for c in range(n):
                P.ops.extend(chunks[c][1])
                if c + 1 < n:
                    P.ops.extend(chunks[c + 1][0])
                P.ops.extend(chunks[c][2])

        dbg_ops = []

        def dbg_dump(name, ap, keys):
            if name in dbg_d:
                dbg_ops.append((name, ap, keys))

        def finish():
            P.barrier()
            outs = []
            for name, ap, keys in dbg_ops:
                k = ("dbgout", name)
                P.op("sp", lambda e, name=name, ap=ap: e.dma_start(out=dbg_d[name], in_=ap), r=keys, w=[k], dma=True)
                outs.append(k)
            P.op("sp", None, r=outs + OUTKEYS)
            P.resolve()
            P.emit(sems, block)

        OUTKEYS = []

        sc.reset()
        X = [sc.f32([DM]) for _ in range(2)]
        XN = [sc.bf16([DM]) for _ in range(2)]
        ST = [sc.f32([4]) for _ in range(2)]
        gbc = sc.f32([DM])
        m_common = sc.mark()
        nPT = nT[:, :, 0:TM]
        P.op("sp", lambda e, gbc=gbc: e.dma_start(out=gbc, in_=g_mix.partition_broadcast(128)), w=["gbc"], dma=True)
        stg = []
        for tt in range(8):
            i = tt % 2
            pre = (lambda tt=tt, i=i: P.op("sp", lambda e: e.dma_start(out=X[i], in_=xprev[tt * 128:(tt + 1) * 128, :]), w=[("X", i)], dma=True))
            stg.append(rms_transpose(X[i], ("X", i), 128, XN[i], ("XN", i), ST[i], ("ST", i), gbc, "gbc", nPT, tt * 128, ("nPT", tt), DM, tt, pre=pre))
        rms_pipeline(stg)
        Wdt = sc.bf16([16, 16])
        r2.reset()
        xTp = r2.bf16([10, TM])
        dtraw_p = r2.f32([8, 16])
        tP = ssd_temps(r2, main=False)
        rawpad = [sc.f32([TM + 4]) for _ in range(2)]
        accb = [sc.f32([TM]) for _ in range(2)]
        blocks = [(COL_X, 512), (COL_X + 512, 512), (COL_B, 512)]
        load_w(Wdt, wview(w_in, COL_DT, 16), "Wdt")
        wcur = wload(wview(w_in, blocks[0][0], blocks[0][1]))
        for i in range(2):
            P.op("dve", lambda e, rp=rawpad[i]: e.memset(rp[:, 0:3], 0.0), w=[("rawpad", i)])
        jt = 0
        defer_p = []
        for bi, (c0, ncol) in enumerate(blocks):
            Wv, kW = wcur
            if bi + 1 < len(blocks):
                wcur = wload(wview(w_in, blocks[bi + 1][0], blocks[bi + 1][1]))
            for jj in range(4):
                j = (c0 - COL_X) // 128 + jj
                rp = rawpad[jt % 2]
                krp = ("rawpad", jt % 2)
                isC = j >= 10
                pend_p = list(defer_p)
                del defer_p[:]
                for nb in range(2):
                    if isC and nb == 0:
                        continue
                    bank = 3 + (jt * 2 + nb) % 4
                    mm_group(pbf[bank], [(Wv[:, k, jj * 128:(jj + 1) * 128], nPT[:, k, nb * 512:(nb + 1) * 512]) for k in range(16)],
                             r=[kW] + [("nPT", tt) for tt in range(nb * 4, nb * 4 + 4)], w=[("ps", bank)])
                    P.op("act", lambda e, rp=rp, nb=nb, bank=bank: e.activation(out=rp[:, 3 + nb * 512:3 + (nb + 1) * 512], in_=pbf[bank], func=AF.Copy),
                         r=[("ps", bank)], w=[krp])
                for so in pend_p:
                    P.ops.extend(so)
                P.op("dve", lambda e, rp=rp, j=j: e.tensor_copy(out=prefix[:, j, :], in_=rp[:, TM:TM + 3]), r=[krp], w=["prefix"])
                if not isC:
                    conv_silu(rp, krp, accb[jt % 2], ("acc", jt % 2), j, TM, xTp[:, j, :], ("xTp",), defer=defer_p)
                jt += 1
        for so in defer_p:
            P.ops.extend(so)
        for tt in range(8):
            mm_group(pbf[0][:, 0:16], [(nPT[:, k, tt * 128:(tt + 1) * 128], Wdt[:, k, :]) for k in range(16)], r=["Wdt", ("nPT", tt)], w=[("ps", 0)])
            P.op("act", lambda e, tt=tt: e.activation(out=dtraw_p[:, tt, :], in_=pbf[0][:, 0:16], func=AF.Copy), r=[("ps", 0)], w=["dtraw_p"])
        P.barrier()

        def pchunks():
            ssd_emit([ssd_chunk(tP, xTp, c, dtraw_p[:, c, :], "dtraw_p", main=False, banks=(5, 6, 5, 7, 6), ktag="p") for c in range(8)])
            P.op("dve", lambda e: e.tensor_scalar(out=hT, in0=hT, scalar1=flag, scalar2=None, op0=ALU.mult), r=["hT", KC], w=["hT"])
            P.op("act", lambda e: e.activation(out=hTb, in_=hT, func=AF.Copy), r=["hT"], w=["hTb"])
        pch_ops = P.capture(pchunks)
        _saved_ops = P.ops
        P.ops = []
        sc.reset(m_common)
        stgA = []
        for tt in range(9):
            i = tt % 2
            rows = 128 if tt < 8 else NS
            src = xm[tt * 128:(tt + 1) * 128, :] if tt < 8 else xsmp[:, :]
            pre = (lambda i=i, rows=rows, src=src: P.op("sp", lambda e: e.dma_start(out=X[i][:rows, :], in_=src), w=[("X", i)], dma=True))
            stgA.append(rms_transpose(X[i], ("X", i), rows, XN[i], ("XN", i), ST[i], ("ST", i), gbc, "gbc", nT, tt * 128, ("nT", tt), DM, tt, pre=pre))
        rms_pipeline(stgA)
        P.barrier()
        sc.reset()
        xT = sc.bf16([12, T])
        z_tok = sc.bf16([8, 1024])
        zT_s = sc.f32([8, NS])
        dtraw = sc.f32([8, 16])
        dtT_s = sc.f32([NS])
        ssdg_bc = sc.f32([1024])
        Rsel = sc.f32([1024])
        raw_s = sc.f32([12, NS])
        m_A = sc.mark()
        Wdt = sc.bf16([16, 16])
        rawpad = [sc.f32([TM + 4]) for _ in range(2)]
        accb = [sc.f32([TM]) for _ in range(2)]
        scTok = sc.f32([3 * 1536])
        scT = sc.f32([36, NS])
        acc_s = sc.f32([12, NS])
        P.op("sp", lambda e: e.dma_start(out=ssdg_bc, in_=ssd_g.partition_broadcast(128)), w=["ssdg"], dma=True)
        P.op("sp", lambda e: e.dma_start(out=Rsel, in_=cst2_d[:, C2_RSEL:C2_RSEL + 1024]), w=["cst2"], dma=True)
        P.op("sp", lambda e: e.dma_start(out=scTok[:NS, :], in_=sconv.rearrange("b r c -> b (r c)")), w=["scTok"], dma=True)
        P.op("sp", lambda e: e.dma_start(out=nconv_s[:, 0:2, :], in_=sconv[:, 1:3, :]), w=["o_ncs01"], dma=True)
        OUTKEYS.append("o_ncs01")
        for half in range(2):
            def trs(e, half=half):
                ins = None
                for idx in range(half * 18, half * 18 + 18):
                    r_, j_ = idx // 12, idx % 12
                    ii = idx - half * 18
                    ins = e.transpose(out=pbf[half][:, ii * NS:(ii + 1) * NS], in_=scTok[:NS, r_ * 1536 + j_ * 128:r_ * 1536 + (j_ + 1) * 128],
                                      identity=identF[:NS, :NS])
                return ins
            P.op("pe", trs, r=["scTok", KC], w=[("ps", half)])
            P.op("dve", lambda e, half=half: e.tensor_copy(out=scT[:, half * 18:half * 18 + 18, :].rearrange("p a b -> p (a b)"), in_=pbf[half][:, 0:18 * NS]),
                 r=[("ps", half)], w=["scT"])
        blocksA = [(COL_Z, "z"), (COL_Z + 512, "z"), (COL_X, "x"), (COL_X + 512, "x"), (COL_B, "x")]
        load_w(Wdt, wview(w_in, COL_DT, 16), "Wdt")
        wcur = wload(wview(w_in, blocksA[0][0], 512))
        jt = 0
        grp = 0
        defer_a = []
        for bi, (c0, kind) in enumerate(blocksA):
            Wv, kW = wcur
            if bi + 1 < len(blocksA):
                wcur = wload(wview(w_in, blocksA[bi + 1][0], 512))
            if kind == "z":
                zb = (c0 - COL_Z) // 512
                for tt in range(8):
                    bank = 1 + grp % 4
                    grp += 1
                    mm_group(pbf[bank], [(nT[:, k, tt * 128:(tt + 1) * 128], Wv[:, k, :]) for k in range(16)], r=[kW, ("nT", tt)], w=[("ps", bank)])
                    P.op("act", lambda e, tt=tt, zb=zb, bank=bank: e.activation(out=z_tok[:, tt, zb * 512:(zb + 1) * 512], in_=pbf[bank], func=AF.Silu),
                         r=[("ps", bank)], w=["z_tok"])
                for jj in range(4):
                    bank = 1 + grp % 4
                    grp += 1
                    j = zb * 4 + jj
                    mm_group(pbf[bank][:, 0:NS], [(Wv[:, k, jj * 128:(jj + 1) * 128], nT[:, k, TM:T]) for k in range(16)], r=[kW, ("nT", 8)], w=[("ps", bank)])
                    P.op("act", lambda e, j=j, bank=bank: e.activation(out=zT_s[:, j, :], in_=pbf[bank][:, 0:NS], func=AF.Silu), r=[("ps", bank)], w=["zT_s"])
            else:
                for jj in range(4):
                    j = (c0 - COL_X) // 128 + jj
                    rp = rawpad[jt % 2]
                    krp = ("rawpad", jt % 2)
                    pend_a = list(defer_a)
                    del defer_a[:]
                    P.op("dve", lambda e, rp=rp, j=j: e.tensor_copy(out=rp[:, 0:3], in_=prefix[:, j, :]), r=["prefix"], w=[krp])
                    for nb in range(3):
                        bank = 1 + grp % 4
                        grp += 1
                        lo, hi = (nb * 512, (nb + 1) * 512) if nb < 2 else (TM, T)
                        n = hi - lo
                        rk = [("nT", tt) for tt in range(nb * 4, nb * 4 + 4)] if nb < 2 else [("nT", 8)]
                        mm_group(pbf[bank][:, 0:n], [(Wv[:, k, jj * 128:(jj + 1) * 128], nT[:, k, lo:hi]) for k in range(16)], r=[kW] + rk, w=[("ps", bank)])
                        if nb < 2:
                            P.op("act", lambda e, rp=rp, lo=lo, hi=hi, bank=bank: e.activation(out=rp[:, 3 + lo:3 + hi], in_=pbf[bank], func=AF.Copy),
                                 r=[("ps", bank)], w=[krp])
                        else:
                            P.op("act", lambda e, j=j, bank=bank: e.activation(out=raw_s[:, j, :], in_=pbf[bank][:, 0:NS], func=AF.Copy),
                                 r=[("ps", bank)], w=["raw_s"])
                    for so in pend_a:
                        P.ops.extend(so)
                    P.op("dve", lambda e, rp=rp, j=j: e.tensor_copy(out=ncv[:, j, :], in_=rp[:, TM:TM + 3]), r=[krp], w=["ncv"])
                    conv_silu(rp, krp, accb[jt % 2], ("acc", jt % 2), j, TM, xT[:, j, 0:TM], ("xT",), defer=defer_a)
                    P.op("dve", lambda e, j=j: e.tensor_scalar(out=acc_s[:, j, :], in0=raw_s[:, j, :], scalar1=cwb[:, j, 3:4], scalar2=cwb[:, j, 4:5],
                                                               op0=ALU.mult, op1=ALU.add), r=["raw_s", KC], w=["acc_s"])
                    for r_ in range(3):
                        P.op("dve", lambda e, j=j, r_=r_: e.scalar_tensor_tensor(out=acc_s[:, j, :], in0=scT[:, r_ * 12 + j, :], scalar=cwb[:, j, r_:r_ + 1],
                                                                                 op0=ALU.mult, in1=acc_s[:, j, :], op1=ALU.add), r=["scT", "acc_s", KC], w=["acc_s"])
                    jt += 1
        for so in defer_a:
            P.ops.extend(so)
        P.op("act", lambda e: e.activation(out=xT[:, :, TM:T], in_=acc_s, func=AF.Silu), r=["acc_s"], w=[("xT",)])
        for tt in range(8):
            mm_group(pbf[0][:, 0:16], [(nT[:, k, tt * 128:(tt + 1) * 128], Wdt[:, k, :]) for k in range(16)], r=["Wdt", ("nT", tt)], w=[("ps", 0)])
            P.op("act", lambda e, tt=tt: e.activation(out=dtraw[:, tt, :], in_=pbf[0][:, 0:16], func=AF.Copy), r=[("ps", 0)], w=["dtraw"])
        mm_group(pbf[1][:NS, 0:NS], [(Wdt[:, k, :], nT[:, k, TM:T]) for k in range(16)], r=["Wdt", ("nT", 8)], w=[("ps", 1)])
        P.op("act", lambda e: e.activation(out=dtT_s[:NS, :], in_=pbf[1][:NS, 0:NS], func=AF.Copy), r=[("ps", 1)], w=["dtT_s"])

        def rows_out(src3, ksrc, R, stage, kst, dram_ap, okey):
            for b3 in range(3):
                def trr(e, b3=b3):
                    ins = None
                    for jj in range(4):
                        j_ = b3 * 4 + jj
                        ins = e.transpose(out=pbf[2 + b3][:R, jj * 128:(jj + 1) * 128], in_=src3[:, j_, :], identity=identF)
                    return ins
                P.op("pe", trr, r=[ksrc, KC], w=[("ps", 2 + b3)])
                P.op("dve", lambda e, b3=b3: e.tensor_copy(out=stage[:R, b3 * 512:(b3 + 1) * 512], in_=pbf[2 + b3][:R, :]), r=[("ps", 2 + b3)], w=[kst, "scTok"])
            P.op("sp", lambda e: e.dma_start(out=dram_ap, in_=stage[:R, 0:1536]), r=[kst], w=[okey], dma=True)
            OUTKEYS.append(okey)
        rows_out(raw_s, "raw_s", NS, scTok[:, 0:1536], "stg0", nconv_s[:, 2, :], "o_ncs2")
        rows_out(ncv, "ncv", 3, scTok[:, 1536:3072], "stg1", nconv_p[:, :], "o_ncp")
        _main_ops = P.ops
        P.ops = _saved_ops
        P.ops.extend(Prog.merge(_main_ops, pch_ops))
        P.barrier()
        if STOP_AFTER == "A1":
            dbg_dump("xT", xT.rearrange("p a b -> p (a b)"), [("xT",)])
            dbg_dump("z_tok", z_tok.rearrange("p a b -> p (a b)"), ["z_tok"])
            dbg_dump("zT_s", zT_s.rearrange("p a b -> p (a b)"), ["zT_s"])
            dbg_dump("dtraw", dtraw.rearrange("p a b -> p (a b)"), ["dtraw"])
            dbg_dump("dtT_s", dtT_s, ["dtT_s"])
            finish()
            return nc

        sc.reset(m_A)
        tA = ssd_temps(sc)
        ssd_emit([ssd_chunk(tA, xT, c, dtraw[:, c, :], "dtraw", main=True, ztok_c=z_tok[:, c, :], ssdg_bc=ssdg_bc) for c in range(8)])
        if STOP_AFTER == "A2a":
            P.barrier()
            dbg_dump("YT", YT.rearrange("p a b -> p (a b)"), [("YTb", c) for c in range(8)])
            finish()
            return nc
        hout = sc.f32([8, 128])
        for half in range(2):
            def trh(e, half=half):
                ins = None
                for jj in range(4):
                    q = half * 4 + jj
                    ins = e.transpose(out=pbf[3 + half][:, jj * 128:(jj + 1) * 128], in_=hT[:, q * 128:(q + 1) * 128], identity=identF)
                return ins
            P.op("pe", trh, r=["hT", KC], w=[("ps", 3 + half)])
            P.op("dve", lambda e, half=half: e.tensor_copy(out=hout[:, half * 4:(half + 1) * 4, :].rearrange("p a b -> p (a b)"), in_=pbf[3 + half]),
                 r=[("ps", 3 + half)], w=["hout"])
        P.op("sp", lambda e: e.dma_start(out=nssm_p.rearrange("(q r) n -> r q n", r=128), in_=hout), r=["hout"], w=["o_nsp"], dma=True)
        OUTKEYS.append("o_nsp")

        if STOP_AFTER == "A2b":
            P.barrier()
            dbg_dump("YT", YT.rearrange("p a b -> p (a b)"), [("YTb", c) for c in range(8)])
            finish()
            return nc
        P.barrier()
        sc.reset(m_A)
        sm_s = sc.f32([8, NS])
        cat = sc.f32([32])
        rep = sc.f32([8, 32])
        dtx = sc.f32([8, NS])
        ys = sc.f32([8, NS])
        ysq = sc.f32([8, NS])
        rs = sc.f32([2, NS])
        BC_tok = sc.bf16([512])
        selB = sc.bf16([NS, 128])
        NHS = 4
        hs = [sc.f32([8, 128]) for _ in range(NHS)]
        junk = sc.f32([128])
        tmpS = [sc.f32([1024]) for _ in range(2)]
        x0 = sm_s[:NS, 0, :]; e1 = sm_s[:NS, 1, :]; anp = sm_s[:NS, 2, 0:1]
        P.op("act", lambda e: e.activation(out=e1, in_=dtT_s[:NS, :], func=AF.Exp, bias=dtb_pp[:NS, :]), r=["dtT_s", KC], w=["s_e1"])
        P.op("act", lambda e: e.activation(out=cat[:NS, 16:32], in_=e1, func=AF.Ln, bias=1.0), r=["s_e1"], w=["s_dt"])
        P.op("act", lambda e: e.activation(out=anp, in_=al_pp[:NS, :], func=AF.Exp), r=[KC], w=["s_anp0"])
        P.op("dve", lambda e: e.tensor_scalar(out=anp, in0=anp, scalar1=-1.0, scalar2=None, op0=ALU.mult), r=["s_anp0"], w=["s_anp"])
        P.op("act", lambda e: e.activation(out=cat[:NS, 0:16], in_=cat[:NS, 16:32], func=AF.Exp, scale=anp), r=["s_dt", "s_anp"], w=["s_dA"])

        def repf(e):
            ins = None
            for q in range(8):
                ins = e.matmul(pbf[0][:, q * 32:(q + 1) * 32], lhsT=Rsel[:NS, q * 128:(q + 1) * 128], rhs=cat[:NS, :], start=True, stop=True)
            return ins
        P.op("pe", repf, r=["s_dA", "s_dt", "cst2"], w=[("ps", 0)])
        P.op("dve", lambda e: e.tensor_copy(out=rep.rearrange("p a b -> p (a b)"), in_=pbf[0][:, 0:256]), r=[("ps", 0)], w=["rep"])
        xs_s = xT[:, 0:8, TM:T]
        P.op("dve", lambda e: e.tensor_tensor(out=dtx, in0=rep[:, :, 16:32], in1=xs_s, op=ALU.mult), r=["rep", ("xT",)], w=["dtx"])
        psbc = pbb[1][:NS, 0:512].rearrange("p (a b) -> p a b", a=4)

        def trbc(e):
            ins = None
            for jj in range(4):
                ins = e.transpose(out=psbc[:, jj, :], in_=xT[:, 8 + jj, TM:T], identity=identB)
            return ins
        P.op("pe", trbc, r=[("xT",), "identB"], w=[("ps", 1)])
        P.op("act", lambda e: e.activation(out=BC_tok[:NS, :], in_=pbb[1][:NS, 0:512], func=AF.Copy), r=[("ps", 1)], w=["BC_tok"])
        P.op("dve", lambda e: e.tensor_copy(out=selB[:NS, :, :], in_=identB[:NS, 0:NS].unsqueeze(2).broadcast_to([NS, NS, 128])), r=["identB"], w=["selB"])
        P.op("dve", lambda e: e.memset(ys, 0.0), w=["ys"])

        def hs_load(b):
            P.op("sp", lambda e, b=b: e.dma_start(out=hs[b % NHS], in_=sssm[b].rearrange("(q r) n -> r q n", r=128)), w=[("hs", b % NHS, q) for q in range(8)], dma=True)
        for b in range(min(NHS - 1, NS)):
            hs_load(b)
        for b in range(NS):
            hb = hs[b % NHS]
            bank = 2 + b % 2
            if b + NHS - 1 < NS:
                hs_load(b + NHS - 1)
            P.op("pe", lambda e, b=b, bank=bank: e.matmul(pbf[bank], lhsT=selB[:NS, b, :], rhs=BC_tok[:NS, :], start=True, stop=True),
                 r=["selB", "BC_tok"], w=[("ps", bank)])
            for q in range(8):
                P.op("act", lambda e, b=b, q=q, hb=hb: e.activation(out=hb[:, q, :], in_=hb[:, q, :], func=AF.Copy, scale=rep[:, q, b:b + 1]),
                     r=[("hs", b % NHS, q), "rep"], w=[("hs", b % NHS, q)])

            bc4 = pbf[bank][:, 0:256].rearrange("p (g n) -> p g n", g=2).unsqueeze(2).broadcast_to([128, 2, 4, 128])
            cc4 = pbf[bank][:, 256:512].rearrange("p (g n) -> p g n", g=2).unsqueeze(2).broadcast_to([128, 2, 4, 128])
            hb4 = hb.rearrange("p (g j) n -> p g j n", g=2)
            tm4 = tmpS[b % 2].rearrange("p (g j n) -> p g j n", g=2, j=4)
            dx4 = dtx[:, :, b:b + 1].rearrange("p (g j) o -> p g j o", g=2).broadcast_to([128, 2, 4, 128])
            khs = [("hs", b % NHS, q) for q in range(8)]
            ktm = ("tmpS", b % 2)
            P.op("dve", lambda e, tm4=tm4, bc4=bc4, dx4=dx4: e.tensor_tensor(out=tm4, in0=bc4, in1=dx4, op=ALU.mult), r=[("ps", bank), "dtx"], w=[ktm])
            P.op("dve", lambda e, hb4=hb4, tm4=tm4: e.tensor_tensor(out=hb4, in0=hb4, in1=tm4, op=ALU.add), r=khs + [ktm], w=khs)
            P.op("dve", lambda e, hb4=hb4, tm4=tm4, cc4=cc4: e.tensor_tensor(out=tm4, in0=cc4, in1=hb4, op=ALU.mult), r=khs + [("ps", bank), ktm], w=[ktm])
            P.op("dve", lambda e, b=b, tm=tmpS[b % 2]: e.tensor_reduce(out=ys[:, :, b:b + 1], in_=tm.rearrange("p (q n) -> p q n", q=8), axis=AX.X, op=ALU.add),
                 r=[ktm], w=["ys"])
            ok = ("o_nss", b)
            P.op("sp", lambda e, b=b, hb=hb: e.dma_start(out=nssm_s[b].rearrange("(q r) n -> r q n", r=128), in_=hb), r=[("hs", b % NHS, q) for q in range(8)], w=[ok], dma=True)
            OUTKEYS.append(ok)
        P.op("dve", lambda e: e.tensor_tensor(out=ysq, in0=xs_s, in1=D_pp.unsqueeze(2).broadcast_to([128, 8, NS]), op=ALU.mult), r=[("xT",), KC], w=["ysq"])
        P.op("dve", lambda e: e.tensor_tensor(out=ys, in0=ys, in1=ysq, op=ALU.add), r=["ys", "ysq"], w=["ys"])
        P.op("dve", lambda e: e.tensor_tensor(out=ys, in0=ys, in1=zT_s, op=ALU.mult), r=["ys", "zT_s"], w=["ys"])
        P.op("dve", lambda e: e.tensor_tensor(out=ysq, in0=ys, in1=ys, op=ALU.mult), r=["ys", "ysq"], w=["ysq"])

        def gsum(e):
            ins = None
            for q in range(8):
                g = q // 4
                ins = e.matmul(pbf[4][:, g * NS:(g + 1) * NS], lhsT=onesF, rhs=ysq[:, q, :], start=(q % 4 == 0), stop=(q % 4 == 3))
            return ins
        P.op("pe", gsum, r=["ysq", KC], w=[("ps", 4)])
        rsf = rs.rearrange("p a b -> p (a b)")
        P.op("act", lambda e: e.activation(out=rsf, in_=pbf[4][:, 0:2 * NS], func=AF.Ln, scale=1.0 / 512, bias=EPS), r=[("ps", 4)], w=["rs0"])
        P.op("act", lambda e: e.activation(out=rsf, in_=rsf, func=AF.Exp, scale=-0.5), r=["rs0"], w=["rs"])
        for g in range(2):
            P.op("dve", lambda e, g=g: e.tensor_tensor(out=ys[:, g * 4:(g + 1) * 4, :], in0=ys[:, g * 4:(g + 1) * 4, :],
                                                       in1=rs[:, g:g + 1, :].broadcast_to([128, 4, NS]), op=ALU.mult), r=["ys", "rs"], w=["ys"])
        P.op("dve", lambda e: e.tensor_tensor(out=YT[:, 8:16, TM:T], in0=ys, in1=sg_pp.unsqueeze(2).broadcast_to([128, 8, NS]), op=ALU.mult),
             r=["ys", KC], w=[("YTb", 8)])
        P.barrier()
        if STOP_AFTER == "A2":
            dbg_dump("YT", YT.rearrange("p a b -> p (a b)"), [("YTb", c) for c in range(9)])
            finish()
            return nc
        sc.reset()
        vng_bc = sc.f32([1024])
        bs_bc = sc.f32([8, 128])
        v_tok = sc.bf16([9, 1024])
        ya = sc.f32([8, T])
        WsT = sc.bf16([8, 128])
        wsraw = ya[:, 0, 0:1024].rearrange("p (a b) -> p a b", a=8)
        gv = [sc.f32([1024]) for _ in range(2)]
        vst = [sc.f32([4]) for _ in range(2)]
        ug = [sc.f32([T]) for _ in range(2)]
        _m_tm = sc.mark()
        tmpm = [sc.f32([512]) for _ in range(2)]
        _m_tm2 = sc.mark()
        sc.reset(_m_tm)
        vsf = sc.f32([1024])
        sc.reset(_m_tm2)
        sq = [sc.bf16([T]) for _ in range(2)]
        rstd_bc = sc.f32([T])
        w00I = sc.bf16([8, NS])
        P.op("sp", lambda e: e.dma_start(out=vng_bc, in_=vn_g.partition_broadcast(128)), w=["vng"], dma=True)
        P.op("sp", lambda e: e.dma_start(out=bs_bc.rearrange("p a b -> p (a b)"), in_=cst2_d[:, C2_BS:C2_BS + 1024]), w=["bs_bc"], dma=True)
        P.op("sp", lambda e: e.dma_start(out=wsraw, in_=ws_d.rearrange("h t s -> t h s")), w=["wsraw"], dma=True)
        wv2 = [wload(wview(w_in, COL_V, 512)), wload(wview(w_in, COL_V + 512, 512))]
        for half in range(2):
            def trw(e, half=half):
                ins = None
                for jj in range(4):
                    h_ = half * 4 + jj
                    ins = e.transpose(out=pbf[half][:, jj * 128:(jj + 1) * 128], in_=wsraw[:, h_, :], identity=identF)
                return ins
            P.op("pe", trw, r=["wsraw", KC], w=[("ps", half)])
            P.op("dve", lambda e, half=half: e.tensor_tensor(out=WsT[:, half * 4:(half + 1) * 4, :], in0=pbf[half].rearrange("p (a b) -> p a b", a=4),
                                                             in1=triF.unsqueeze(1).broadcast_to([128, 4, 128]), op=ALU.mult), r=[("ps", half), KC], w=["WsT"])
        for h_ in range(8):
            P.op("dve", lambda e, h_=h_: e.tensor_scalar(out=w00I[:NS, h_, :], in0=identF[:NS, 0:NS], scalar1=w00_bc[:NS, h_:h_ + 1], scalar2=None, op0=ALU.mult),
                 r=[KC], w=["w00I"])
        for tt in range(9):
            rows = 128 if tt < 8 else NS
            i = tt % 2
            tcols = slice(tt * 128, tt * 128 + rows)
            for zb in range(2):
                bank = 2 + (tt * 2 + zb) % 4
                mm_group(pbf[bank][:rows, :], [(nT[:, k, tcols], wv2[zb][0][:, k, :]) for k in range(16)], r=[wv2[zb][1], ("nT", tt)], w=[("ps", bank)])
                P.op("act", lambda e, i=i, zb=zb, bank=bank, rows=rows: e.activation(out=gv[i][:rows, zb * 512:(zb + 1) * 512], in_=pbf[bank][:rows, :],
                                                                                    func=AF.Gelu_apprx_tanh), r=[("ps", bank)], w=[("gv", i, zb)])
            P.op("act", lambda e, i=i, rows=rows: e.activation(out=ug[i][:rows, 0:1024], in_=gv[i][:rows, :], func=AF.Square, accum_out=vst[i][:rows, 0:1]),
                 r=[("gv", i, 0), ("gv", i, 1)], w=[("ug", i), ("vst", i)])
            P.op("act", lambda e, i=i, rows=rows: e.activation(out=vst[i][:rows, 1:2], in_=vst[i][:rows, 0:1], func=AF.Ln, scale=1.0 / 1024, bias=EPS),
                 r=[("vst", i)], w=[("vst", i)])
            P.op("act", lambda e, i=i, rows=rows: e.activation(out=vst[i][:rows, 2:3], in_=vst[i][:rows, 1:2], func=AF.Exp, scale=-0.5),
                 r=[("vst", i)], w=[("vst", i)])
            P.op("dve", lambda e, i=i, rows=rows, tt=tt: e.scalar_tensor_tensor(out=v_tok[:rows, tt, :], in0=gv[i][:rows, :], scalar=vst[i][:rows, 2:3], op0=ALU.mult,
                                                                                in1=vng_bc[:rows, :], op1=ALU.mult),
                 r=[("gv", i, 0), ("gv", i, 1), ("vst", i), "vng"], w=[("v_tok", tt)])
            if tt == 8:
                P.op("dve", lambda e, i=i: e.scalar_tensor_tensor(out=vsf[:NS, :], in0=gv[i][:NS, :], scalar=vst[i][:NS, 2:3], op0=ALU.mult,
                                                                  in1=vng_bc[:NS, :], op1=ALU.mult), r=[("gv", i, 0), ("gv", i, 1), ("vst", i), "vng"], w=["vsf", ("tmpm", 0), ("tmpm", 1)])
                P.op("sp", lambda e: e.dma_start(out=v_s[:, :], in_=vsf[:NS, :]), r=["vsf", ("tmpm", 0), ("tmpm", 1)], w=["o_vs"], dma=True)
                OUTKEYS.append("o_vs")
        wu2 = [wload(wview(w_in, COL_U, 512)), wload(wview(w_in, COL_U + 512, 512))]
        tok_blocks = [(0, 512), (512, 1024), (TM, T)]
        grpc = [0]

        def stage_G(h_):
            ub, jj = h_ // 4, h_ % 4
            Wv, kWu = wu2[ub]
            ui = h_ % 2
            for nb, (lo, hi) in enumerate(tok_blocks):
                bank = grpc[0] % 2
                grpc[0] += 1
                n = hi - lo
                rk = [("nT", tt) for tt in range(nb * 4, nb * 4 + 4)] if nb < 2 else [("nT", 8)]
                mm_group(pbf[bank][:, 0:n], [(Wv[:, k, jj * 128:(jj + 1) * 128], nT[:, k, lo:hi]) for k in range(16)], r=[kWu] + rk, w=[("ps", bank)])
                P.op("act", lambda e, ui=ui, lo=lo, hi=hi, n=n, bank=bank: e.activation(out=ug[ui][:, lo:hi], in_=pbf[bank][:, 0:n], func=AF.Gelu_apprx_tanh),
                     r=[("ps", bank)], w=[("ug", ui)])

        def stage_M(h_):
            ui = h_ % 2
            for half in range(2):
                def mix(e, half=half, h_=h_):
                    ins = None
                    for cc in range(4):
                        c = half * 4 + cc
                        ins = e.matmul(pbf[2 + half][:, cc * 128:(cc + 1) * 128], lhsT=v_tok[:, c, h_ * 128:(h_ + 1) * 128], rhs=WsT[:, h_, :], start=True, stop=True)
                    return ins
                P.op("pe", mix, r=[("v_tok", c) for c in range(half * 4, half * 4 + 4)] + ["WsT"], w=[("ps", 2 + half)])
                tm = tmpm[half]
                P.op("dve", lambda e, half=half, h_=h_, tm=tm: e.tensor_tensor(out=tm.rearrange("p (a b) -> p a b", a=4), in0=pbf[2 + half].rearrange("p (a b) -> p a b", a=4),
                                                                               in1=bs_bc[:, h_:h_ + 1, :].broadcast_to([128, 4, 128]), op=ALU.add),
                     r=[("ps", 2 + half), "bs_bc"], w=[("tmpm", half)])
                P.op("dve", lambda e, half=half, h_=h_, tm=tm, ui=ui: e.tensor_tensor(out=ya[:, h_, half * 512:(half + 1) * 512], in0=tm, in1=ug[ui][:, half * 512:(half + 1) * 512], op=ALU.mult),
                     r=[("tmpm", half), ("ug", ui)], w=[("ya", h_)])
            P.op("pe", lambda e, h_=h_: e.matmul(pbf[4][:, 0:NS], lhsT=v_tok[:NS, 8, h_ * 128:(h_ + 1) * 128], rhs=w00I[:NS, h_, :], start=True, stop=True),
                 r=[("v_tok", 8), "w00I"], w=[("ps", 4)])
            P.op("dve", lambda e, h_=h_, ui=ui: e.scalar_tensor_tensor(out=ya[:, h_, TM:T], in0=pbf[4][:, 0:NS], scalar=b0_bc[:, h_:h_ + 1], op0=ALU.add, in1=ug[ui][:, TM:T], op1=ALU.mult),
                 r=[("ps", 4), ("ug", ui), KC], w=[("ya", h_)])
            si = h_ % 2
            P.op("act", lambda e, h_=h_, si=si: e.activation(out=sq[si], in_=ya[:, h_, :], func=AF.Square), r=[("ya", h_)], w=[("sq", si)])

        def stage_S(h_):
            si = h_ % 2
            for nb, (lo, hi) in enumerate(tok_blocks):
                n = hi - lo
                P.op("pe", lambda e, nb=nb, lo=lo, hi=hi, n=n, si=si, h_=h_: e.matmul(pbf[5 + nb][:, 0:n], lhsT=onesB, rhs=sq[si][:, lo:hi], start=(h_ == 0), stop=(h_ == 7)),
                     r=[("sq", si), "onesB"], w=[("ps", 5 + nb)])

        stage_G(0)
        for h_ in range(8):
            if h_ + 1 < 8:
                stage_G(h_ + 1)
            stage_M(h_)
            if h_ >= 1:
                stage_S(h_ - 1)
        stage_S(7)
        for nb, (lo, hi) in enumerate(tok_blocks):
            n = hi - lo
            P.op("act", lambda e, nb=nb, lo=lo, hi=hi, n=n: e.activation(out=rstd_bc[:, lo:hi], in_=pbf[5 + nb][:, 0:n], func=AF.Ln, scale=1.0 / 1024, bias=EPS),
                 r=[("ps", 5 + nb)], w=[("rstd", nb)])
            P.op("act", lambda e, lo=lo, hi=hi: e.activation(out=rstd_bc[:, lo:hi], in_=rstd_bc[:, lo:hi], func=AF.Exp, scale=-0.5), r=[("rstd", nb)], w=[("rstd", nb)])
        for h_ in range(8):
            P.op("dve", lambda e, h_=h_: e.scalar_tensor_tensor(out=YT[:, h_, :], in0=ya[:, h_, :], scalar=gon_pp[:, h_:h_ + 1], op0=ALU.mult, in1=rstd_bc, op1=ALU.mult),
                 r=[("ya", h_), ("rstd", 0), ("rstd", 1), ("rstd", 2), KC], w=[("YTa", h_)])
        P.barrier()
        if STOP_AFTER == "A3":
            dbg_dump("YT", YT.rearrange("p a b -> p (a b)"), [("YTa", h_) for h_ in range(8)])
            finish()
            return nc

        sc.reset()
        hres = sc.f32([9, DM])
        m_B = sc.mark()
        for tt in range(9):
            rows = 128 if tt < 8 else NS
            src = xm[tt * 128:(tt + 1) * 128, :] if tt < 8 else xsmp[:, :]
            P.op("sp", lambda e, tt=tt, rows=rows, src=src: e.dma_start(out=hres[:rows, tt, :], in_=src), w=[("h", tt, cb) for cb in range(4)], dma=True)
        wcur = wload(wview(w_out, 0, 512))
        for cb in range(4):
            Wv, kW = wcur
            if cb + 1 < 4:
                wcur = wload(wview(w_out, (cb + 1) * 512, 512))
            for tt in range(9):
                rows = 128 if tt < 8 else NS
                tcols = slice(tt * 128, tt * 128 + rows)
                bank = (cb * 9 + tt) % 6
                mm_group(pbf[bank][:rows, :], [(YT[:, k, tcols], Wv[:, k, :]) for k in range(16)], r=[kW], w=[("ps", bank)])
                P.op("dve", lambda e, rows=rows, tt=tt, cb=cb, bank=bank: e.tensor_tensor(out=hres[:rows, tt, cb * 512:(cb + 1) * 512], in0=pbf[bank][:rows, :],
                                                                                         in1=hres[:rows, tt, cb * 512:(cb + 1) * 512], op=ALU.add),
                     r=[("ps", bank), ("h", tt, cb)], w=[("h", tt, cb)])
        P.barrier()
        if STOP_AFTER == "B":
            dbg_dump("h", hres.rearrange("p a b -> p (a b)"), [("h", tt) for tt in range(9)])
            finish()
            return nc

        sc.reset(m_B)
        gbc = sc.f32([DM])
        gbc2 = sc.f32([DM])
        stmp = [sc.f32([512]) for _ in range(2)]
        ST = [sc.f32([4]) for _ in range(2)]
        r2.reset()
        actb = [r2.bf16([2, T]) for _ in range(2)]
        Wd = [r2.bf16([2, DM]) for _ in range(2)]
        XN = [r2.bf16([DM]) for _ in range(2)]
        mT = nT
        P.op("sp", lambda e: e.dma_start(out=gbc, in_=g_ffn.partition_broadcast(128)), w=["gbc"], dma=True)
        P.op("sp", lambda e: e.dma_start(out=gbc2, in_=g_fin.partition_broadcast(128)), w=["gbc2"], dma=True)
        stgC = []
        for tt in range(9):
            rows = 128 if tt < 8 else NS
            i = tt % 2
            stgC.append(rms_transpose(hres[:, tt, :], [("h", tt, cb) for cb in range(4)], rows, XN[i], ("XN", i), ST[i], ("ST", i), gbc, "gbc", mT, tt * 128, ("mT", tt), DM, tt))
        rms_pipeline(stgC)
        NFB = DFF // 256
        if STOP_AFTER == "C0":
            P.barrier()
            dbg_dump("mT", mT.rearrange("p a b -> p (a b)"), [("mT", tt) for tt in range(9)])
            dbg_dump("XN0", XN[0], [("XN", 0)])
            dbg_dump("ST0", ST[0], [("ST", 0)])
            dbg_dump("gbc", gbc, ["gbc"])
            dbg_dump("h", hres.rearrange("p a b -> p (a b)"), [("h", tt) for tt in range(9)])
            finish()
            return nc

        gu_slot = {}

        def load_gu(fb):
            slot = wq[0] % 2
            wq[0] += 1
            gu_slot[fb] = slot
            load_w(WG[slot], wview(w_gate, fb * 256, 256), ("W", slot), nobar=True)
            load_w(WU[slot], wview(w_up, fb * 256, 256), ("W", slot), nobar=True)

        def load_d(fb):
            load_w(Wd[fb % 2], w_down[fb * 256:(fb + 1) * 256, :].rearrange("(fl p) c -> p fl c", p=128), ("Wd", fb % 2))

        gctr = [0]

        def GU_units(fb):
            slot = gu_slot[fb]
            Wg_, Wu_, kWs = WG[slot], WU[slot], ("W", slot)
            units = []
            for fl in range(2):
                for nb, (lo, hi) in enumerate(tok_blocks):
                    def unit(fl=fl, nb=nb, lo=lo, hi=hi):
                        n = hi - lo
                        pr = gctr[0] % 2
                        gctr[0] += 1
                        bg, bu = pr * 2, pr * 2 + 1
                        rk = [("mT", tt) for tt in range(nb * 4, nb * 4 + 4)] if nb < 2 else [("mT", 8)]
                        mm_group(pbf[bg][:, 0:n], [(Wg_[:, k, fl * 128:(fl + 1) * 128], mT[:, k, lo:hi]) for k in range(16)], r=[kWs] + rk, w=[("ps", bg)])
                        mm_group(pbf[bu][:, 0:n], [(Wu_[:, k, fl * 128:(fl + 1) * 128], mT[:, k, lo:hi]) for k in range(16)], r=[kWs] + rk, w=[("ps", bu)])
                        P.op("act", lambda e: e.activation(out=stmp[pr][:, 0:n], in_=pbf[bg][:, 0:n], func=AF.Silu), r=[("ps", bg)], w=[("stmp", pr)])
                        P.op("dve", lambda e: e.tensor_tensor(out=actb[fb % 2][:, fl, lo:hi], in0=pbf[bu][:, 0:n], in1=stmp[pr][:, 0:n], op=ALU.mult),
                             r=[("ps", bu), ("stmp", pr)], w=[("act", fb % 2)])
                    units.append(P.capture(unit))
            return units

        dctr = [0]

        def DN_groups(fb):
            groups = []
            for tt in range(9):
                rows = 128 if tt < 8 else NS
                tcols = slice(tt * 128, tt * 128 + rows)
                for cb in range(4):
                    def grp_(tt=tt, rows=rows, tcols=tcols, cb=cb):
                        bank = 4 + dctr[0] % 4
                        dctr[0] += 1
                        mm_group(pbf[bank][:rows, :], [(actb[fb % 2][:, fl, tcols], Wd[fb % 2][:, fl, cb * 512:(cb + 1) * 512]) for fl in range(2)],
                                 r=[("act", fb % 2), ("Wd", fb % 2)], w=[("ps", bank)])
                        P.op("dve", lambda e: e.tensor_tensor(out=hres[:rows, tt, cb * 512:(cb + 1) * 512], in0=pbf[bank][:rows, :],
                                                              in1=hres[:rows, tt, cb * 512:(cb + 1) * 512], op=ALU.add),
                             r=[("ps", bank), ("h", tt, cb)], w=[("h", tt, cb)])
                    groups.append(P.capture(grp_))
            return groups

        load_gu(0)
        load_d(0)
        for i in range(NFB + 1):
            if i + 1 < NFB:
                load_gu(i + 1)
            gu = GU_units(i) if i < NFB else []
            dn = DN_groups(i - 1) if i >= 1 else []
            nslot = max(len(gu), 1)
            per = -(-len(dn) // nslot)
            for u in range(nslot):
                if u < len(gu):
                    P.ops.extend(gu[u])
                for g_ in dn[u * per:(u + 1) * per]:
                    P.ops.extend(g_)
            if i + 1 < NFB:
                load_d(i + 1)
        P.barrier()
        if STOP_AFTER == "C":
            dbg_dump("h", hres.rearrange("p a b -> p (a b)"), [("h", tt) for tt in range(9)])
            finish()
            return nc

        for tt in range(9):
            rows = 128 if tt < 8 else NS
            i = tt % 2
            ht = hres[:, tt, :]
            st = ST[i]
            P.op("act", lambda e, rows=rows, ht=ht, st=st, i=i: e.activation(out=XN[i][:rows, :], in_=ht[:rows, :], func=AF.Square, accum_out=st[:rows, 0:1]),
                 r=[("h", tt, cb) for cb in range(4)], w=[("XN", i), ("ST", i)])
            P.op("act", lambda e, rows=rows, st=st: e.activation(out=st[:rows, 1:2], in_=st[:rows, 0:1], func=AF.Ln, scale=1.0 / DM, bias=EPS), r=[("ST", i)], w=[("ST", i)])
            P.op("act", lambda e, rows=rows, st=st: e.activation(out=st[:rows, 2:3], in_=st[:rows, 1:2], func=AF.Exp, scale=-0.5), r=[("ST", i)], w=[("ST", i)])
            P.op("dve", lambda e, rows=rows, ht=ht, st=st: e.scalar_tensor_tensor(out=ht[:rows, :], in0=ht[:rows, :], scalar=st[:rows, 2:3], op0=ALU.mult, in1=gbc2[:rows, :], op1=ALU.mult),
                 r=[("h", tt, cb) for cb in range(4)] + [("ST", i), "gbc2"], w=[("h", tt, cb) for cb in range(4)])
            dst = y_m[tt * 128:(tt + 1) * 128, :] if tt < 8 else y_s[:, :]
            ok = ("o_y", tt)
            P.op("sp", lambda e, rows=rows, ht=ht, dst=dst: e.dma_start(out=dst, in_=ht[:rows, :]), r=[("h", tt, cb) for cb in range(4)], w=[ok], dma=True)
            OUTKEYS.append(ok)
        finish()
        return nc


def _host_consts(inputs, core):
    hf = core % 2
    cst = np.zeros((128, CSTW), np.float32)
    r = np.arange(128)
    cst[:, C_ID:C_ID + 128] = np.eye(128, dtype=np.float32)
    cst[:, C_TRI:C_TRI + 128] = (r[:, None] <= r[None, :]).astype(np.float32)
    cst[:, C_U:C_U + 128] = (r[:, None] > r[None, :]).astype(np.float32)
    cst[:, C_ONE:C_ONE + 128] = 1.0
    cw = np.asarray(inputs["ssd_conv_w"])[0]
    cb = np.asarray(inputs["ssd_conv_b"])[0]
    cwb = np.concatenate([cw, cb[None]], 0)
    cst[:, C_CWB:C_CWB + 60] = cwb.reshape(5, 12, 128).transpose(2, 1, 0).reshape(128, 60)
    cst[:, C_GON:C_GON + 8] = np.asarray(inputs["chunk_out_norm_g"])[0].reshape(8, 128).T
    cst[:, C_ALOG:C_ALOG + 16] = np.asarray(inputs["ssd_a_log"])[0][None, :]
    cst[:, C_DTB:C_DTB + 16] = np.asarray(inputs["ssd_dt_bias"])[0][None, :]
    cst[:, C_D:C_D + 16] = np.asarray(inputs["ssd_d"])[0][None, :]
    cst[:, C_W00:C_W00 + 8] = np.asarray(inputs["chunk_w_s"])[0][:, 0, 0][None, :]
    cst[:, C_B0:C_B0 + 8] = np.asarray(inputs["chunk_b_s"])[0][:, 0][None, :]
    cst[:, C_FLAG] = float(hf)
    cst[:, C_DPP:C_DPP + 8] = np.repeat(np.asarray(inputs["ssd_d"])[0], 64).reshape(8, 128).T
    cst[:, C_SGPP:C_SGPP + 8] = np.asarray(inputs["ssd_norm_g"])[0].reshape(8, 128).T
    cst[0:16, C_DTBPP] = np.asarray(inputs["ssd_dt_bias"])[0]
    cst[0:16, C_ALPP] = np.asarray(inputs["ssd_a_log"])[0]
    cst2 = np.zeros((128, CST2W), np.float32)
    cst2[:, C2_BS:C2_BS + 1024] = np.asarray(inputs["chunk_b_s"])[0].reshape(1, 1024)
    rs = np.zeros((16, 8, 128), np.float32)
    for q in range(8):
        for rr in range(128):
            rs[2 * q + rr // 64, q, rr] = 1.0
    cst2[0:16, C2_RSEL:C2_RSEL + 1024] = rs.reshape(16, 1024)
    return cst, cst2


def make_in_maps(inputs):
    xp = np.asarray(inputs["x_prompt"], np.float32)
    xs = np.asarray(inputs["x_sample"], np.float32)
    sconv = np.asarray(inputs["state_conv"], np.float32)[0]
    sssm = np.asarray(inputs["state_ssm"], np.float32)[0]
    shared = {
        "w_in": np.ascontiguousarray(np.asarray(inputs["w_in"], np.float32)[0]),
        "w_out": np.ascontiguousarray(np.asarray(inputs["w_out"], np.float32)[0]),
        "w_gate": np.ascontiguousarray(np.asarray(inputs["w_gate"], np.float32)[0]),
        "w_up": np.ascontiguousarray(np.asarray(inputs["w_up"], np.float32)[0]),
        "w_down": np.ascontiguousarray(np.asarray(inputs["w_down"], np.float32)[0]),
        "ws": np.ascontiguousarray(np.asarray(inputs["chunk_w_s"], np.float32)[0]),
        "g_mix": np.ascontiguousarray(np.asarray(inputs["norm_mix_g"], np.float32)[0]),
        "g_ffn": np.ascontiguousarray(np.asarray(inputs["norm_ffn_g"], np.float32)[0]),
        "g_fin": np.ascontiguousarray(np.asarray(inputs["norm_final_g"], np.float32)),
        "vn_g": np.ascontiguousarray(np.asarray(inputs["chunk_v_norm_g"], np.float32)[0]),
        "ssd_g": np.ascontiguousarray(np.asarray(inputs["ssd_norm_g"], np.float32)[0]),
    }
    zeros_prev = np.zeros((TM, DM), np.float32)
    maps = []
    for c in range(8):
        b, hf = c // 2, c % 2
        cst, cst2 = _host_consts(inputs, c)
        m = dict(shared)
        m["xm"] = np.ascontiguousarray(xp[b, hf * TM:(hf + 1) * TM])
        m["xprev"] = np.ascontiguousarray(xp[b, 0:TM]) if hf == 1 else zeros_prev
        m["xsmp"] = np.ascontiguousarray(xs[c * NS:(c + 1) * NS, 0])
        m["sconv"] = np.ascontiguousarray(sconv[c * NS:(c + 1) * NS])
        m["sssm"] = np.ascontiguousarray(sssm[c * NS:(c + 1) * NS].reshape(NS, 1024, 128))
        m["cst"] = cst
        m["cst2"] = cst2
        maps.append(m)
    return maps


def kernel(**inputs):
    nc = build_program()
    maps = make_in_maps(inputs)
    res = run_bass_kernel_spmd(nc, maps, core_ids=list(range(8)))
    R = res.results
    y_prompt = np.zeros((4, 2048, DM), np.float32)
    y_sample = np.zeros((128, 1, DM), np.float32)
    ncp = np.zeros((1, 4, 3, 1536), np.float32)
    nsp = np.zeros((1, 4, 16, 64, 128), np.float32)
    ncs = np.zeros((1, 128, 3, 1536), np.float32)
    nss = np.zeros((1, 128, 16, 64, 128), np.float32)
    vs = np.zeros((1, 128, 1, 1024), np.float32)
    for c in range(8):
        b, hf = c // 2, c % 2
        y_prompt[b, hf * TM:(hf + 1) * TM] = R[c]["y_m"]
        y_sample[c * NS:(c + 1) * NS, 0] = R[c]["y_s"]
        if hf == 1:
            ncp[0, b] = R[c]["nconv_p"]
            nsp[0, b] = R[c]["nssm_p"].reshape(16, 64, 128)
        ncs[0, c * NS:(c + 1) * NS] = R[c]["nconv_s"]
        nss[0, c * NS:(c + 1) * NS] = R[c]["nssm_s"].reshape(NS, 16, 64, 128)
        vs[0, c * NS:(c + 1) * NS, 0] = R[c]["v_s"]
    return (y_prompt, y_sample, ncp, nsp, ncs, nss, vs)
```

```python
import numpy as np
from contextlib import ExitStack
import concourse.bass as bass
import concourse.mybir as mybir
from concourse.bass_utils import run_bass_kernel_spmd

F32 = mybir.dt.float32
F32R = mybir.dt.float32r
BF16 = mybir.dt.bfloat16
AF = mybir.ActivationFunctionType
ALU = mybir.AluOpType
AX = mybir.AxisListType

ENGS = ["pe", "act", "dve", "pool", "sp"]
NDMASEM = 12

DEBUG = {}
STOP_AFTER = None
SSD_LEVEL = 9


class Op:
    __slots__ = ("eng", "fn", "r", "w", "dma", "deps", "sig", "sem", "prev_use", "waits", "need_sig", "xdeps", "nobar")

    def __init__(self, eng, fn, r, w, dma, nobar=False):
        self.eng, self.fn, self.r, self.w, self.dma = eng, fn, tuple(r), tuple(w), dma
        self.deps = set()
        self.xdeps = set()
        self.sig = None
        self.sem = None
        self.prev_use = 0
        self.waits = []
        self.need_sig = False
        self.nobar = nobar


BARRIER = Op(None, None, (), (), False)


class Prog:
    def __init__(self):
        self.ops = []

    def op(self, eng, fn, r=(), w=(), dma=False, nobar=False):
        w = list(w) + [k for k in r if isinstance(k, tuple) and k and k[0] == "ps" and k not in w]
        o = Op(eng, fn, r, w, dma, nobar)
        self.ops.append(o)
        return o

    def barrier(self):
        self.ops.append(BARRIER)

    def capture(self, fn):
        saved = self.ops
        self.ops = []
        try:
            fn()
            got = self.ops
        finally:
            self.ops = saved
        return got

    @staticmethod
    def merge(main, side):
        if not side:
            return list(main)
        out = []
        m, s_ = len(main), len(side)
        j = 0
        for i, o in enumerate(main):
            out.append(o)
            tgt = (i + 1) * s_ // m
            while j < tgt:
                out.append(side[j])
                j += 1
        out.extend(side[j:])
        return out

    def resolve(self):
        ops = self.ops
        last_w = {}
        readers = {}
        last_eng = {}
        dmas = []
        pending = {}
        for i, o in enumerate(ops):
            if o is BARRIER:
                s = set(last_eng.values()) | set(dmas)
                for e in ENGS:
                    pending[e] = set(s) | pending.get(e, set())
                continue
            if o.eng in pending and not o.nobar:
                o.xdeps = o.xdeps | pending.pop(o.eng)
            deps = set(o.xdeps)
            for k in o.r:
                if k in last_w:
                    deps.add(last_w[k])
            for k in o.w:
                if k in last_w:
                    deps.add(last_w[k])
                deps.update(readers.get(k, ()))
            deps.discard(i)
            keep = set()
            for d in deps:
                od = ops[d]
                if od.dma:
                    keep.add(d)
                elif od.eng == o.eng and not o.dma:
                    if o.eng == "pe":
                        continue
                    if d in o.xdeps:
                        continue
                    if set(od.w) & set(o.r):
                        keep.add(d)
                else:
                    keep.add(d)
            o.deps = keep
            for d in keep:
                ops[d].need_sig = True
            for k in o.r:
                readers.setdefault(k, []).append(i)
            for k in o.w:
                last_w[k] = i
                readers[k] = []
            if o.dma:
                dmas.append(i)
            elif o.fn is not None:
                last_eng[o.eng] = i
        cnt = {e: 0 for e in ENGS}
        dma_n = {e: 0 for e in ENGS}
        dma_use = {}
        for i, o in enumerate(ops):
            if o is BARRIER:
                continue
            if o.dma:
                slot = (o.eng, dma_n[o.eng] % NDMASEM)
                dma_n[o.eng] += 1
                u = dma_use.get(slot, 0)
                o.prev_use = u
                dma_use[slot] = u + 1
                o.sem = slot
                o.sig = 16 * (u + 1)
            elif o.need_sig:
                cnt[o.eng] += 1
                o.sem = o.eng
                o.sig = cnt[o.eng]
        seen = {e: {} for e in ENGS}
        for i, o in enumerate(ops):
            if o is BARRIER:
                continue
            sd = seen[o.eng]
            need = {}
            if o.dma and o.prev_use > 0:
                need[o.sem] = 16 * o.prev_use
            for d in o.deps:
                od = ops[d]
                need[od.sem] = max(need.get(od.sem, 0), od.sig)
            o.waits = []
            for s, v in need.items():
                if sd.get(s, 0) >= v:
                    continue
                sd[s] = v
                o.waits.append((s, v))

    def emit(self, sems, block):
        ops = self.ops

        def run(eng_name):
            def body(e):
                for o in ops:
                    if o is BARRIER or o.eng != eng_name:
                        continue
                    for s, v in o.waits:
                        e.wait_ge(sems[s], v)
                    if o.fn is None:
                        continue
                    ins = o.fn(e)
                    if o.dma:
                        ins.then_inc(sems[o.sem], 16)
                    elif o.sig is not None:
                        ins.then_inc(sems[o.sem], 1)
            return body

        block.sync(run("sp"))
        block.tensor(run("pe"))
        block.scalar(run("act"))
        block.vector(run("dve"))
        block.gpsimd(run("pool"))


DM = 2048
DIN = 4624
DFF = 5632
TM = 1024
NS = 16
T = TM + NS
EPS = 1e-6
COL_U, COL_V, COL_Z, COL_X, COL_B, COL_C, COL_DT = 0, 1024, 2048, 3072, 4096, 4352, 4608

C_ID, C_TRI, C_U, C_ONE = 0, 128, 256, 384
C_CWB, C_GON, C_ALOG, C_DTB, C_D, C_W00, C_B0, C_FLAG = 512, 572, 580, 596, 612, 628, 636, 644
C_DPP, C_SGPP, C_DTBPP, C_ALPP = 648, 656, 664, 665
CSTW = 672
C2_BS, C2_RSEL = 0, 1024
CST2W = 2048

AW = 51000


class Bump:
    def __init__(self, arena, start, end):
        self.arena, self.start, self.end, self.off = arena, start, end, start
        self.peak = start

    def reset(self, to=None):
        self.off = self.start if to is None else to

    def mark(self):
        return self.off

    def _take(self, words):
        o = self.off
        self.off += words
        assert self.off <= self.end, ("arena overflow", self.off, self.end)
        self.peak = max(self.peak, self.off)
        return o

    def f32(self, shape):
        n = int(np.prod(shape))
        o = self._take(n)
        v = self.arena[:, o:o + n]
        return _shape(v, shape)

    def bf16(self, shape):
        n = int(np.prod(shape))
        w = (n + 1) // 2
        o = self._take(w)
        v = self.arena[:, o:o + w].bitcast(BF16)[:, 0:n]
        return _shape(v, shape)


def _shape(v, shape):
    if len(shape) == 1:
        return v
    if len(shape) == 2:
        return v.rearrange("p (a b) -> p a b", a=shape[0])
    if len(shape) == 3:
        return v.rearrange("p (a b c) -> p a b c", a=shape[0], b=shape[1])
    raise ValueError(shape)


def build_program():
    nc = bass.Bass("TRN2", target_bir_lowering=False)

    def din(name, shape):
        return nc.dram_tensor(name, shape, F32, kind="ExternalInput").ap()

    def dout(name, shape):
        return nc.dram_tensor(name, shape, F32, kind="ExternalOutput").ap()

    xm = din("xm", [TM, DM])
    xprev = din("xprev", [TM, DM])
    xsmp = din("xsmp", [NS, DM])
    sconv = din("sconv", [NS, 3, 1536])
    sssm = din("sssm", [NS, 1024, 128])
    w_in = din("w_in", [DM, DIN])
    w_out = din("w_out", [DM, DM])
    w_gate = din("w_gate", [DM, DFF])
    w_up = din("w_up", [DM, DFF])
    w_down = din("w_down", [DFF, DM])
    cst_d = din("cst", [128, CSTW])
    cst2_d = din("cst2", [128, CST2W])
    ws_d = din("ws", [8, 128, 128])
    g_mix = din("g_mix", [DM])
    g_ffn = din("g_ffn", [DM])
    g_fin = din("g_fin", [DM])
    vn_g = din("vn_g", [1024])
    ssd_g = din("ssd_g", [1024])

    y_m = dout("y_m", [TM, DM])
    y_s = dout("y_s", [NS, DM])
    nconv_p = dout("nconv_p", [3, 1536])
    nssm_p = dout("nssm_p", [1024, 128])
    nconv_s = dout("nconv_s", [NS, 3, 1536])
    nssm_s = dout("nssm_s", [NS, 1024, 128])
    v_s = dout("v_s", [NS, 1024])
    dbg_d = {}
    for name, (shape, dty) in DEBUG.items():
        dbg_d[name] = nc.dram_tensor("dbg_" + name, list(shape), dty, kind="ExternalOutput").ap()

    P = Prog()
    with ExitStack() as es:
        arena = es.enter_context(nc.sbuf_tensor("arena", [128, AW], F32))
        rhsE = es.enter_context(nc.sbuf_tensor("rhsE", [128, 2048], F32))
        U32 = es.enter_context(nc.sbuf_tensor("U32", [128, 128], F32))
        pb = [es.enter_context(nc.psum_tensor(f"pb{i}", [128, 512], F32)) for i in range(8)]
        sems = {}
        for e in ENGS:
            sems[e] = es.enter_context(nc.semaphore("s_" + e))
        for e in ("sp", "pool"):
            for i in range(NDMASEM):
                sems[(e, i)] = es.enter_context(nc.semaphore(f"d_{e}_{i}"))
        block = es.enter_context(nc.Block())

        arena = arena[:, :]
        rhsEr = rhsE[:, :].bitcast(F32R)
        U32r = U32[:, :].bitcast(F32R)
        pbf = [p[:, :] for p in pb]
        pbb = [p[:, :].bitcast(BF16) for p in pb]

        fx = Bump(arena, 0, AW)
        CST = fx.f32([CSTW])
        identF = CST[:, C_ID:C_ID + 128]
        triF = CST[:, C_TRI:C_TRI + 128]
        UF = CST[:, C_U:C_U + 128]
        onesF = CST[:, C_ONE:C_ONE + 128]
        cwb = CST[:, C_CWB:C_CWB + 60].rearrange("p (j k) -> p j k", j=12)
        gon_pp = CST[:, C_GON:C_GON + 8]
        alog_bc = CST[:, C_ALOG:C_ALOG + 16]
        dtb_bc = CST[:, C_DTB:C_DTB + 16]
        D_bc = CST[:, C_D:C_D + 16]
        w00_bc = CST[:, C_W00:C_W00 + 8]
        b0_bc = CST[:, C_B0:C_B0 + 8]
        flag = CST[:, C_FLAG:C_FLAG + 1]
        D_pp = CST[:, C_DPP:C_DPP + 8]
        sg_pp = CST[:, C_SGPP:C_SGPP + 8]
        dtb_pp = CST[:, C_DTBPP:C_DTBPP + 1]
        al_pp = CST[:, C_ALPP:C_ALPP + 1]
        identB = fx.bf16([128])
        onesB = fx.bf16([128])
        aneg = fx.f32([16])
        hT = fx.f32([1024])
        hTb = fx.bf16([1024])
        prefix = fx.f32([12, 3])
        ncv = fx.f32([12, 3])
        nT = fx.bf16([16, T])
        YT = fx.bf16([16, T])
        r2_start = fx.off - (16 * T) // 2
        r2_end = fx.off
        xT_s = fx.bf16([12, NS])
        zT_s = fx.f32([8, NS])
        dtT_s = fx.f32([NS])
        Rsel = fx.f32([1024])
        Wfix = [fx.bf16([16, 512]) for _ in range(2)]
        _wflat = [w_.rearrange("p a b -> p (a b)") for w_ in Wfix]
        WG = [w_[:, 0:4096].rearrange("p (a b) -> p a b", a=16) for w_ in _wflat]
        WU = [w_[:, 4096:8192].rearrange("p (a b) -> p a b", a=16) for w_ in _wflat]
        wq = [0]
        S0 = fx.off
        sc = Bump(arena, S0, AW)
        r2 = Bump(arena, r2_start, r2_end)

        KC = ("cst",)

        P.op("sp", lambda e: e.dma_start(out=CST, in_=cst_d[:, :]), w=[KC], dma=True)
        P.op("dve", lambda e: e.tensor_copy(out=identB, in_=identF), r=[KC], w=["identB"])
        P.op("dve", lambda e: e.memset(onesB, 1.0), w=["onesB"])
        P.op("dve", lambda e: e.tensor_copy(out=U32r, in_=UF), r=[KC], w=["U32"])
        P.op("act", lambda e: e.activation(out=aneg, in_=alog_bc, func=AF.Exp), r=[KC], w=["aneg0"])
        P.op("dve", lambda e: e.tensor_scalar(out=aneg, in0=aneg, scalar1=-1.0, scalar2=None, op0=ALU.mult), r=["aneg0"], w=["aneg"])
        P.op("dve", lambda e: e.memset(hT, 0.0), w=["hT"])
        P.op("dve", lambda e: e.memset(hTb, 0.0), w=["hTb"])

        def rms_transpose(xt, kx, rows, xn, kxn, st, kst, gbc, kg, dstT, col0, kdst, width, tag, pre=None):
            nk = width // 128

            def s1():
                if pre is not None:
                    pre()
                P.op("act", lambda e: e.activation(out=xn[:rows, :], in_=xt[:rows, :], func=AF.Square, accum_out=st[:rows, 0:1]),
                     r=[kx] if not isinstance(kx, list) else kx, w=[kxn, kst])
                P.op("act", lambda e: e.activation(out=st[:rows, 1:2], in_=st[:rows, 0:1], func=AF.Ln, scale=1.0 / width, bias=EPS),
                     r=[kst], w=[kst])
                P.op("act", lambda e: e.activation(out=st[:rows, 2:3], in_=st[:rows, 1:2], func=AF.Exp, scale=-0.5),
                     r=[kst], w=[kst])
                P.op("dve", lambda e: e.scalar_tensor_tensor(out=xn[:rows, :], in0=xt[:rows, :], scalar=st[:rows, 2:3], op0=ALU.mult,
                                                             in1=gbc[:rows, :], op1=ALU.mult),
                     r=([kx] if not isinstance(kx, list) else kx) + [kst, kg, kxn], w=[kxn])

            def s2():
                for b8 in range(nk // 8):
                    bank = (tag % 2) * 2 + b8
                    psv = pbb[bank][:, 0:8 * rows].rearrange("p (a b) -> p a b", a=8)

                    def tr(e, b8=b8, psv=psv):
                        ins = None
                        for j in range(8):
                            k = b8 * 8 + j
                            ins = e.transpose(out=psv[:, j, :], in_=xn[:rows, k * 128:(k + 1) * 128], identity=identB[:rows, :rows])
                        return ins
                    P.op("pe", tr, r=[kxn, "identB"], w=[("ps", bank)])
                    dst = dstT[:, b8 * 8:(b8 + 1) * 8, col0:col0 + rows]
                    if b8 == 0:
                        P.op("act", lambda e, dst=dst, psv=psv: e.activation(out=dst, in_=psv, func=AF.Copy), r=[("ps", bank)], w=[kdst])
                    else:
                        P.op("dve", lambda e, dst=dst, psv=psv: e.tensor_copy(out=dst, in_=psv), r=[("ps", bank)], w=[kdst])
            return P.capture(s1), P.capture(s2)

        def rms_pipeline(stages):
            n = len(stages)
            for i in range(n + 1):
                if i < n:
                    P.ops.extend(stages[i][0])
                if i >= 1:
                    P.ops.extend(stages[i - 1][1])

        def mm_group(out, pairs, r, w):
            def fn(e):
                ins = None
                n = len(pairs)
                for i, (l, rh) in enumerate(pairs):
                    ins = e.matmul(out, lhsT=l, rhs=rh, start=(i == 0), stop=(i == n - 1))
                return ins
            P.op("pe", fn, r=r, w=w)

        def load_w(dst, src_ap, key, nobar=False):
            P.op("pool", lambda e: e.dma_start(out=dst, in_=src_ap), w=[key], dma=True, nobar=nobar)

        def wload(src_ap, ncols=512):
            slot = wq[0] % 2
            wq[0] += 1
            buf = Wfix[slot] if ncols == 512 else Wfix[slot][:, :, 0:ncols]
            load_w(buf, src_ap, ("W", slot), nobar=True)
            return buf, ("W", slot)

        def wview(wap, c0, ncols):
            return wap[:, c0:c0 + ncols].rearrange("(kt p) c -> p kt c", p=128)

        def conv_silu(rawpad, krp, acc, kacc, j, ntok, dst, kdst, defer=None):
            for k in (0, 1, 2):
                P.op("dve", lambda e, k=k: e.scalar_tensor_tensor(out=acc[:, 0:ntok], in0=rawpad[:, k:k + ntok], scalar=cwb[:, j, k:k + 1],
                                                                  op0=ALU.mult, in1=acc[:, 0:ntok], op1=ALU.add), r=[krp, kacc, KC], w=[kacc])
            silu_ops = P.capture(lambda: P.op("act", lambda e: e.activation(out=dst, in_=acc[:, 0:ntok], func=AF.Silu), r=[kacc], w=[kdst]))
            if defer is None:
                P.ops.extend(silu_ops)
            else:
                defer.append(silu_ops)

        def ssd_temps(b, main=True):
            t = {}
            t["sm"] = [b.f32([96]) for _ in range(2)]
            t["ex"] = b.f32([48])
            t["xdtw"] = b.bf16([16, 64])
            t["B_tok"] = b.bf16([256])
            if not main:
                return t
            t["LT"] = b.f32([2048])
            t["MT"] = b.bf16([16, 128])
            t["xs_tok"] = b.bf16([16, 64])
            t["xdt"] = b.bf16([16, 64])
            t["cbm"] = b.f32([2, 128])
            t["y1"] = b.f32([16, 64])
            t["t2"] = b.f32([16, 64])
            t["yb"] = b.bf16([1024])
            t["gn"] = b.f32([8])
            return t

        def ssd_chunk(t, xT, c, dtraw_c, kdt, main, ztok_c=None, ssdg_bc=None, banks=(0, 1, 2, 7, 1), ktag=""):
            par = c % 2
            sm, ex = t["sm"][par], t["ex"]
            cs = slice(c * 128, (c + 1) * 128)
            kx = ("xT" + ktag,)
            bs_, bx_, bb_ = banks[0], banks[1], banks[2]
            k0, k1, kdtc, kdta, kw2 = ("sm0", par), ("sm1", par), ("dtc", par), ("dta", par), ("w2", par)
            dtc = sm[:, 32:48]
            dta = sm[:, 48:64]
            w2 = sm[:, 64:80]
            toend, dec, ea = ex[:, 0:16], ex[:, 16:32], ex[:, 32:48]
            psx = pbb[bx_][:, 0:1024].rearrange("p (a b) -> p a b", a=8)
            psx3 = pbb[bx_][:, 0:1024].rearrange("p (a b) -> p a b", a=16)
            psB = pbb[bb_][:, 0:256].rearrange("p (a b) -> p a b", a=2)

            def head():
                P.op("dve", lambda e: e.tensor_tensor(out=sm[:, 0:16], in0=dtraw_c, in1=dtb_bc, op=ALU.add), r=[kdt, KC], w=[k0])
                P.op("act", lambda e: e.activation(out=sm[:, 16:32], in_=sm[:, 0:16], func=AF.Exp), r=[k0], w=[k1])
                P.op("act", lambda e: e.activation(out=dtc, in_=sm[:, 16:32], func=AF.Ln, bias=1.0), r=[k1], w=[kdtc])
                P.op("dve", lambda e: e.tensor_tensor(out=dta, in0=dtc, in1=aneg, op=ALU.mult), r=[kdtc, "aneg"], w=[kdta])

            def part1():
                def trx(e):
                    ins = None
                    for q in range(8):
                        ins = e.transpose(out=psx[:, q, :], in_=xT[:, q, cs], identity=identB)
                    return ins
                P.op("pe", trx, r=[kx, "identB"], w=[("ps", bx_)])
                if main:
                    psc = pbf[7][:, 0:256].rearrange("p (a b) -> p a b", a=2)

                    def cbf(e):
                        ins = None
                        for g in range(2):
                            ins = e.matmul(psc[:, g, :], lhsT=xT[:, 8 + g, cs], rhs=xT[:, 10 + g, cs], start=True, stop=True)
                        return ins
                    P.op("pe", cbf, r=[kx], w=[("ps", 7)])
                    P.op("dve", lambda e: e.tensor_tensor(out=t["cbm"], in0=psc, in1=triF.unsqueeze(1).broadcast_to([128, 2, 128]), op=ALU.mult),
                         r=[("ps", 7), KC], w=["cbm"])
                    for e16 in range(16):
                        P.op("dve", lambda e, e16=e16: e.tensor_scalar(out=rhsEr[:, e16 * 128:(e16 + 1) * 128], in0=triF, scalar1=dta[:, e16:e16 + 1],
                                                                       scalar2=None, op0=ALU.mult), r=[kdta, KC], w=[("rhsE", e16 // 4)])

                def small(e):
                    e.matmul(pbf[bs_][:, 0:16], lhsT=UF, rhs=dta, start=True, stop=True)
                    ins = e.matmul(pbf[bs_][:, 16:32], lhsT=onesF, rhs=dta, start=True, stop=True)
                    if main:
                        ins = e.matmul(pbf[bs_][:, 32:48], lhsT=triF, rhs=dta, start=True, stop=True)
                    return ins
                P.op("pe", small, r=[kdta, KC], w=[("ps", bs_)])
                nex = 48 if main else 32
                P.op("act", lambda e: e.activation(out=ex[:, 0:nex], in_=pbf[bs_][:, 0:nex], func=AF.Exp), r=[("ps", bs_)], w=["ex"])
                if main:
                    for i in range(4):
                        P.op("pe", lambda e, i=i: e.matmul(pbf[3 + i], lhsT=U32r, rhs=rhsEr[:, i * 512:(i + 1) * 512], start=True, stop=True),
                             r=[("rhsE", i), "U32"], w=[("ps", 3 + i)])
                        P.op("act", lambda e, i=i: e.activation(out=t["LT"][:, i * 512:(i + 1) * 512], in_=pbf[3 + i], func=AF.Exp),
                             r=[("ps", 3 + i)], w=[("LT", i)])

                def trb(e):
                    ins = None
                    for g in range(2):
                        ins = e.transpose(out=psB[:, g, :], in_=xT[:, 8 + g, cs], identity=identB)
                    return ins
                P.op("pe", trb, r=[kx, "identB"], w=[("ps", bb_)])
                P.op("dve", lambda e: e.tensor_tensor(out=w2, in0=dtc, in1=toend, op=ALU.mult), r=[kdtc, "ex"], w=[kw2])
                P.op("dve", lambda e: e.tensor_tensor(out=t["xdtw"], in0=psx3, in1=w2.unsqueeze(2).broadcast_to([128, 16, 64]), op=ALU.mult),
                     r=[("ps", bx_), kw2], w=["xdtw"])
                P.op("act", lambda e: e.activation(out=t["B_tok"], in_=pbb[bb_][:, 0:256], func=AF.Copy), r=[("ps", bb_)], w=["B_tok"])
                if main:
                    P.op("dve", lambda e: e.tensor_tensor(out=t["xdt"], in0=psx3, in1=dtc.unsqueeze(2).broadcast_to([128, 16, 64]), op=ALU.mult),
                         r=[("ps", bx_), kdtc], w=["xdt"])
                    P.op("act", lambda e: e.activation(out=t["xs_tok"], in_=psx3, func=AF.Copy), r=[("ps", bx_)], w=["xs_tok"])
                    LT3 = t["LT"].rearrange("p (a b) -> p a b", a=16)
                    for g in range(2):
                        P.op("dve", lambda e, g=g: e.tensor_tensor(out=t["MT"][:, g * 8:(g + 1) * 8, :], in0=LT3[:, g * 8:(g + 1) * 8, :],
                                                                   in1=t["cbm"][:, g:g + 1, :].broadcast_to([128, 8, 128]), op=ALU.mult),
                             r=[("LT", 2 * g), ("LT", 2 * g + 1), "cbm"], w=[("MT", g)])

            def part2():
                if main:
                    for g in range(2):
                        def yd(e, g=g):
                            ins = None
                            for j in range(8):
                                e16 = g * 8 + j
                                ins = e.matmul(pbf[3 + g][:, j * 64:(j + 1) * 64], lhsT=t["MT"][:, e16, :], rhs=t["xdt"][:, e16, :], start=True, stop=True)
                            return ins
                        P.op("pe", yd, r=[("MT", g), "xdt"], w=[("ps", 3 + g)])
                        P.op("pe", lambda e, g=g: e.matmul(pbf[5 + g], lhsT=xT[:, 10 + g, cs], rhs=hTb[:, g * 512:(g + 1) * 512], start=True, stop=True),
                             r=[kx, "hTb"], w=[("ps", 5 + g)])
                    y1 = t["y1"]
                    for g in range(2):
                        y1g = y1[:, g * 8:(g + 1) * 8, :]
                        P.op("dve", lambda e, g=g, y1g=y1g: e.tensor_tensor(out=y1g, in0=pbf[5 + g].rearrange("p (a b) -> p a b", a=8),
                                                                            in1=ea[:, g * 8:(g + 1) * 8].unsqueeze(2).broadcast_to([128, 8, 64]), op=ALU.mult),
                             r=[("ps", 5 + g), "ex"], w=[("y1", g)])
                        P.op("dve", lambda e, g=g, y1g=y1g: e.tensor_tensor(out=y1g, in0=pbf[3 + g].rearrange("p (a b) -> p a b", a=8), in1=y1g, op=ALU.add),
                             r=[("ps", 3 + g), ("y1", g)], w=[("y1", g)])
                    P.op("pool", lambda e: e.tensor_tensor(out=t["t2"], in0=t["xs_tok"], in1=D_bc.unsqueeze(2).broadcast_to([128, 16, 64]), op=ALU.mult),
                         r=["xs_tok", KC], w=["t2"])
                    P.op("dve", lambda e: e.tensor_tensor(out=y1, in0=y1, in1=t["t2"], op=ALU.add), r=[("y1", 0), ("y1", 1), "t2"], w=[("y1", 0), ("y1", 1)])
                    y1f = y1.rearrange("p a b -> p (a b)")
                    P.op("dve", lambda e: e.tensor_tensor(out=y1f, in0=y1f, in1=ztok_c, op=ALU.mult), r=[("y1", 0), ("y1", 1), "z_tok"], w=[("y1", 0), ("y1", 1)])
                    gn = t["gn"]
                    t2f = t["t2"].rearrange("p a b -> p (a b)")
                    for g in range(2):
                        P.op("act", lambda e, g=g: e.activation(out=t2f[:, g * 512:(g + 1) * 512], in_=y1f[:, g * 512:(g + 1) * 512], func=AF.Square,
                                                                accum_out=gn[:, g:g + 1]), r=[("y1", g)], w=["t2", ("gn", g)])
                    P.op("act", lambda e: e.activation(out=gn[:, 2:4], in_=gn[:, 0:2], func=AF.Ln, scale=1.0 / 512, bias=EPS), r=[("gn", 0), ("gn", 1)], w=["gn2"])
                    P.op("act", lambda e: e.activation(out=gn[:, 4:6], in_=gn[:, 2:4], func=AF.Exp, scale=-0.5), r=["gn2"], w=["gn4"])
                    for g in range(2):
                        P.op("dve", lambda e, g=g: e.scalar_tensor_tensor(out=t["yb"][:, g * 512:(g + 1) * 512], in0=y1f[:, g * 512:(g + 1) * 512],
                                                                          scalar=gn[:, 4 + g:5 + g], op0=ALU.mult, in1=ssdg_bc[:, g * 512:(g + 1) * 512], op1=ALU.mult),
                             r=[("y1", g), "gn4", "ssdg"], w=[("yb", g)])
                    psy = pbb[2][:, 0:1024].rearrange("p (a b) -> p a b", a=8)

                    def try_(e):
                        ins = None
                        for q in range(8):
                            ins = e.transpose(out=psy[:, q, :], in_=t["yb"][:, q * 128:(q + 1) * 128], identity=identB)
                        return ins
                    P.op("pe", try_, r=[("yb", 0), ("yb", 1), "identB"], w=[("ps", 2)])
                    P.op("act", lambda e: e.activation(out=YT[:, 8:16, cs], in_=psy, func=AF.Copy), r=[("ps", 2)], w=[("YTb", c)])

            def part_state():
                sb = (banks[3], banks[4])
                for g in range(2):
                    P.op("pe", lambda e, g=g: e.matmul(pbf[sb[g]], lhsT=t["B_tok"][:, g * 128:(g + 1) * 128], rhs=t["xdtw"].rearrange("p a b -> p (a b)")[:, g * 512:(g + 1) * 512],
                                                       start=True, stop=True), r=["B_tok", "xdtw"], w=[("ps", sb[g])])
                hT3 = hT.rearrange("p (a b) -> p a b", a=16)
                P.op("dve", lambda e: e.tensor_tensor(out=hT3, in0=hT3, in1=dec.unsqueeze(2).broadcast_to([128, 16, 64]), op=ALU.mult), r=["hT", "ex"], w=["hT"])
                for g in range(2):
                    P.op("dve", lambda e, g=g: e.tensor_tensor(out=hT[:, g * 512:(g + 1) * 512], in0=pbf[sb[g]], in1=hT[:, g * 512:(g + 1) * 512], op=ALU.add),
                         r=[("ps", sb[g]), "hT"], w=["hT"])
                P.op("act", lambda e: e.activation(out=hTb, in_=hT, func=AF.Copy), r=["hT"], w=["hTb"])
            return P.capture(head), P.capture(part1), P.capture(part2) + P.capture(part_state)

        def ssd_emit(chunks):
            n = len(chunks)
            P.ops.extend(chunks[0][0])
            for c in range(n):
                P.ops.extend(chunks[c][1])
                if c + 1 < n:
                    P.ops.extend(chunks[c + 1][0])
                P.ops.extend(chunks[c][2])

        dbg_ops = []

        def dbg_dump(name, ap, keys):
            if name in dbg_d:
                dbg_ops.append((name, ap, keys))

        def finish():
            P.barrier()
            outs = []
            for name, ap, keys in dbg_ops:
                k = ("dbgout", name)
                P.op("sp", lambda e, name=name, ap=ap: e.dma_start(out=dbg_d[name], in_=ap), r=keys, w=[k], dma=True)
                outs.append(k)
            P.op("sp", None, r=outs + OUTKEYS)
            P.resolve()
            P.emit(sems, block)

        OUTKEYS = []

        sc.reset()
        NX = 4
        X = [sc.f32([DM]) for _ in range(NX)]
        XN = [sc.bf16([DM]) for _ in range(2)]
        ST = [sc.f32([4]) for _ in range(2)]
        gbc = sc.f32([DM])
        m_common = sc.mark()
        nPT = nT[:, :, 0:TM]
        P.op("sp", lambda e, gbc=gbc: e.dma_start(out=gbc, in_=g_mix.partition_broadcast(128)), w=["gbc"], dma=True)
        stg = []
        for tt in range(8):
            i = tt % 2
            ix = tt % NX
            pre = (lambda tt=tt, ix=ix: P.op("sp", lambda e: e.dma_start(out=X[ix], in_=xprev[tt * 128:(tt + 1) * 128, :]), w=[("X", ix)], dma=True))
            stg.append(rms_transpose(X[ix], ("X", ix), 128, XN[i], ("XN", i), ST[i], ("ST", i), gbc, "gbc", nPT, tt * 128, ("nPT", tt), DM, tt, pre=pre))
        rms_pipeline(stg)
        Wdt = sc.bf16([16, 16])
        r2.reset()
        xTp = r2.bf16([10, TM])
        dtraw_p = r2.f32([8, 16])
        tP = ssd_temps(r2, main=False)
        rawpad = [sc.f32([TM + 4]) for _ in range(2)]
        accb = [sc.f32([TM]) for _ in range(2)]
        blocks = [(COL_X, 512), (COL_X + 512, 512), (COL_B, 512)]
        load_w(Wdt, wview(w_in, COL_DT, 16), "Wdt")
        wcur = wload(wview(w_in, blocks[0][0], blocks[0][1]))
        for i in range(2):
            P.op("dve", lambda e, rp=rawpad[i]: e.memset(rp[:, 0:3], 0.0), w=[("rawpad", i)])
        jt = 0
        defer_p = []
        for bi, (c0, ncol) in enumerate(blocks):
            Wv, kW = wcur
            if bi + 1 < len(blocks):
                wcur = wload(wview(w_in, blocks[bi + 1][0], blocks[bi + 1][1]))
            for jj in range(4):
                j = (c0 - COL_X) // 128 + jj
                rp = rawpad[jt % 2]
                krp = ("rawpad", jt % 2)
                isC = j >= 10
                pend_p = list(defer_p)
                del defer_p[:]
                for nb in range(2):
                    if isC and nb == 0:
                        continue
                    bank = 3 + (jt * 2 + nb) % 4
                    mm_group(pbf[bank], [(Wv[:, k, jj * 128:(jj + 1) * 128], nPT[:, k, nb * 512:(nb + 1) * 512]) for k in range(16)],
                             r=[kW] + [("nPT", tt) for tt in range(nb * 4, nb * 4 + 4)], w=[("ps", bank)])
                    P.op("act", lambda e, rp=rp, nb=nb, bank=bank: e.activation(out=rp[:, 3 + nb * 512:3 + (nb + 1) * 512], in_=pbf[bank], func=AF.Copy),
                         r=[("ps", bank)], w=[krp])
                    if not isC:
                        P.op("act", lambda e, ac=accb[jt % 2], nb=nb, bank=bank, j=j: e.activation(out=ac[:, nb * 512:(nb + 1) * 512], in_=pbf[bank], func=AF.Identity,
                                                                                                scale=cwb[:, j, 3:4], bias=cwb[:, j, 4:5]),
                             r=[("ps", bank), KC], w=[("acc", jt % 2)])
                for so in pend_p:
                    P.ops.extend(so)
                P.op("dve", lambda e, rp=rp, j=j: e.tensor_copy(out=prefix[:, j, :], in_=rp[:, TM:TM + 3]), r=[krp], w=["prefix"])
                if not isC:
                    conv_silu(rp, krp, accb[jt % 2], ("acc", jt % 2), j, TM, xTp[:, j, :], ("xTp",), defer=defer_p)
                jt += 1
        for so in defer_p:
            P.ops.extend(so)
        for tt in range(8):
            mm_group(pbf[0][:, 0:16], [(nPT[:, k, tt * 128:(tt + 1) * 128], Wdt[:, k, :]) for k in range(16)], r=["Wdt", ("nPT", tt)], w=[("ps", 0)])
            P.op("act", lambda e, tt=tt: e.activation(out=dtraw_p[:, tt, :], in_=pbf[0][:, 0:16], func=AF.Copy), r=[("ps", 0)], w=["dtraw_p"])
        P.barrier()

        def pchunks():
            ssd_emit([ssd_chunk(tP, xTp, c, dtraw_p[:, c, :], "dtraw_p", main=False, banks=(5, 6, 5, 7, 6), ktag="p") for c in range(8)])
            P.op("dve", lambda e: e.tensor_scalar(out=hT, in0=hT, scalar1=flag, scalar2=None, op0=ALU.mult), r=["hT", KC], w=["hT"])
            P.op("act", lambda e: e.activation(out=hTb, in_=hT, func=AF.Copy), r=["hT"], w=["hTb"])
        pch_ops = P.capture(pchunks)
        _saved_ops = P.ops
        P.ops = []
        sc.reset(m_common)
        stgA = []
        for tt in range(9):
            i = tt % 2
            rows = 128 if tt < 8 else NS
            src = xm[tt * 128:(tt + 1) * 128, :] if tt < 8 else xsmp[:, :]
            ix = tt % NX
            pre = (lambda ix=ix, rows=rows, src=src: P.op("sp", lambda e: e.dma_start(out=X[ix][:rows, :], in_=src), w=[("X", ix)], dma=True))
            stgA.append(rms_transpose(X[ix], ("X", ix), rows, XN[i], ("XN", i), ST[i], ("ST", i), gbc, "gbc", nT, tt * 128, ("nT", tt), DM, tt, pre=pre))
        rms_pipeline(stgA)
        P.barrier()
        sc.reset()
        xT = sc.bf16([12, T])
        z_tok = sc.bf16([8, 1024])
        dtraw = sc.f32([8, 16])
        ssdg_bc = sc.f32([1024])
        raw_s = sc.f32([12, NS])
        m_A = sc.mark()
        Wdt = sc.bf16([16, 16])
        rawpad = [sc.f32([TM + 4]) for _ in range(2)]
        accb = [sc.f32([TM]) for _ in range(2)]
        scTok = sc.f32([3 * 1536])
        scT = sc.f32([36, NS])
        acc_s = sc.f32([12, NS])
        P.op("sp", lambda e: e.dma_start(out=ssdg_bc, in_=ssd_g.partition_broadcast(128)), w=["ssdg"], dma=True)
        P.op("sp", lambda e: e.dma_start(out=Rsel, in_=cst2_d[:, C2_RSEL:C2_RSEL + 1024]), w=["cst2"], dma=True)
        P.op("sp", lambda e: e.dma_start(out=scTok[:NS, :], in_=sconv.rearrange("b r c -> b (r c)")), w=["scTok"], dma=True)
        P.op("sp", lambda e: e.dma_start(out=nconv_s[:, 0:2, :], in_=sconv[:, 1:3, :]), w=["o_ncs01"], dma=True)
        OUTKEYS.append("o_ncs01")
        for half in range(2):
            def trs(e, half=half):
                ins = None
                for idx in range(half * 18, half * 18 + 18):
                    r_, j_ = idx // 12, idx % 12
                    ii = idx - half * 18
                    ins = e.transpose(out=pbf[half][:, ii * NS:(ii + 1) * NS], in_=scTok[:NS, r_ * 1536 + j_ * 128:r_ * 1536 + (j_ + 1) * 128],
                                      identity=identF[:NS, :NS])
                return ins
            P.op("pe", trs, r=["scTok", KC], w=[("ps", half)])
            P.op("dve", lambda e, half=half: e.tensor_copy(out=scT[:, half * 18:half * 18 + 18, :].rearrange("p a b -> p (a b)"), in_=pbf[half][:, 0:18 * NS]),
                 r=[("ps", half)], w=["scT"])
        blocksA = [(COL_Z, "z"), (COL_Z + 512, "z"), (COL_X, "x"), (COL_X + 512, "x"), (COL_B, "x")]
        load_w(Wdt, wview(w_in, COL_DT, 16), "Wdt")
        wcur = wload(wview(w_in, blocksA[0][0], 512))
        jt = 0
        grp = 0
        defer_a = []
        for bi, (c0, kind) in enumerate(blocksA):
            Wv, kW = wcur
            if bi + 1 < len(blocksA):
                wcur = wload(wview(w_in, blocksA[bi + 1][0], 512))
            if kind == "z":
                zb = (c0 - COL_Z) // 512
                for tt in range(8):
                    bank = 1 + grp % 4
                    grp += 1
                    mm_group(pbf[bank], [(nT[:, k, tt * 128:(tt + 1) * 128], Wv[:, k, :]) for k in range(16)], r=[kW, ("nT", tt)], w=[("ps", bank)])
                    P.op("act", lambda e, tt=tt, zb=zb, bank=bank: e.activation(out=z_tok[:, tt, zb * 512:(zb + 1) * 512], in_=pbf[bank], func=AF.Silu),
                         r=[("ps", bank)], w=["z_tok"])
                for jj in range(4):
                    bank = 1 + grp % 4
                    grp += 1
                    j = zb * 4 + jj
                    mm_group(pbf[bank][:, 0:NS], [(Wv[:, k, jj * 128:(jj + 1) * 128], nT[:, k, TM:T]) for k in range(16)], r=[kW, ("nT", 8)], w=[("ps", bank)])
                    P.op("act", lambda e, j=j, bank=bank: e.activation(out=zT_s[:, j, :], in_=pbf[bank][:, 0:NS], func=AF.Silu), r=[("ps", bank)], w=["zT_s"])
            else:
                for jj in range(4):
                    j = (c0 - COL_X) // 128 + jj
                    rp = rawpad[jt % 2]
                    krp = ("rawpad", jt % 2)
                    pend_a = list(defer_a)
                    del defer_a[:]
                    P.op("dve", lambda e, rp=rp, j=j: e.tensor_copy(out=rp[:, 0:3], in_=prefix[:, j, :]), r=["prefix"], w=[krp])
                    for nb in range(3):
                        bank = 1 + grp % 4
                        grp += 1
                        lo, hi = (nb * 512, (nb + 1) * 512) if nb < 2 else (TM, T)
                        n = hi - lo
                        rk = [("nT", tt) for tt in range(nb * 4, nb * 4 + 4)] if nb < 2 else [("nT", 8)]
                        mm_group(pbf[bank][:, 0:n], [(Wv[:, k, jj * 128:(jj + 1) * 128], nT[:, k, lo:hi]) for k in range(16)], r=[kW] + rk, w=[("ps", bank)])
                        if nb < 2:
                            P.op("act", lambda e, rp=rp, lo=lo, hi=hi, bank=bank: e.activation(out=rp[:, 3 + lo:3 + hi], in_=pbf[bank], func=AF.Copy),
                                 r=[("ps", bank)], w=[krp])
                            P.op("act", lambda e, ac=accb[jt % 2], lo=lo, hi=hi, bank=bank, j=j: e.activation(out=ac[:, lo:hi], in_=pbf[bank], func=AF.Identity,
                                                                                                    scale=cwb[:, j, 3:4], bias=cwb[:, j, 4:5]),
                                 r=[("ps", bank), KC], w=[("acc", jt % 2)])
                        else:
                            P.op("act", lambda e, j=j, bank=bank: e.activation(out=raw_s[:, j, :], in_=pbf[bank][:, 0:NS], func=AF.Copy),
                                 r=[("ps", bank)], w=["raw_s"])
                    for so in pend_a:
                        P.ops.extend(so)
                    P.op("dve", lambda e, rp=rp, j=j: e.tensor_copy(out=ncv[:, j, :], in_=rp[:, TM:TM + 3]), r=[krp], w=["ncv"])
                    conv_silu(rp, krp, accb[jt % 2], ("acc", jt % 2), j, TM, xT[:, j, 0:TM], ("xT",), defer=defer_a)
                    P.op("dve", lambda e, j=j: e.tensor_scalar(out=acc_s[:, j, :], in0=raw_s[:, j, :], scalar1=cwb[:, j, 3:4], scalar2=cwb[:, j, 4:5],
                                                               op0=ALU.mult, op1=ALU.add), r=["raw_s", KC], w=["acc_s"])
                    for r_ in range(3):
                        P.op("dve", lambda e, j=j, r_=r_: e.scalar_tensor_tensor(out=acc_s[:, j, :], in0=scT[:, r_ * 12 + j, :], scalar=cwb[:, j, r_:r_ + 1],
                                                                                 op0=ALU.mult, in1=acc_s[:, j, :], op1=ALU.add), r=["scT", "acc_s", KC], w=["acc_s"])
                    jt += 1
        for so in defer_a:
            P.ops.extend(so)
        P.op("act", lambda e: e.activation(out=xT_s, in_=acc_s, func=AF.Silu), r=["acc_s"], w=["xT_s"])
        for tt in range(8):
            mm_group(pbf[0][:, 0:16], [(nT[:, k, tt * 128:(tt + 1) * 128], Wdt[:, k, :]) for k in range(16)], r=["Wdt", ("nT", tt)], w=[("ps", 0)])
            P.op("act", lambda e, tt=tt: e.activation(out=dtraw[:, tt, :], in_=pbf[0][:, 0:16], func=AF.Copy), r=[("ps", 0)], w=["dtraw"])
        mm_group(pbf[1][:NS, 0:NS], [(Wdt[:, k, :], nT[:, k, TM:T]) for k in range(16)], r=["Wdt", ("nT", 8)], w=[("ps", 1)])
        P.op("act", lambda e: e.activation(out=dtT_s[:NS, :], in_=pbf[1][:NS, 0:NS], func=AF.Copy), r=[("ps", 1)], w=["dtT_s"])

        def rows_out(src3, ksrc, R, stage, kst, dram_ap, okey):
            for b3 in range(3):
                def trr(e, b3=b3):
                    ins = None
                    for jj in range(4):
                        j_ = b3 * 4 + jj
                        ins = e.transpose(out=pbf[2 + b3][:R, jj * 128:(jj + 1) * 128], in_=src3[:, j_, :], identity=identF)
                    return ins
                P.op("pe", trr, r=[ksrc, KC], w=[("ps", 2 + b3)])
                P.op("dve", lambda e, b3=b3: e.tensor_copy(out=stage[:R, b3 * 512:(b3 + 1) * 512], in_=pbf[2 + b3][:R, :]), r=[("ps", 2 + b3)], w=[kst, "scTok"])
            P.op("sp", lambda e: e.dma_start(out=dram_ap, in_=stage[:R, 0:1536]), r=[kst], w=[okey], dma=True)
            OUTKEYS.append(okey)
        rows_out(raw_s, "raw_s", NS, scTok[:, 0:1536], "stg0", nconv_s[:, 2, :], "o_ncs2")
        rows_out(ncv, "ncv", 3, scTok[:, 1536:3072], "stg1", nconv_p[:, :], "o_ncp")
        _main_ops = P.ops
        P.ops = _saved_ops
        P.ops.extend(Prog.merge(_main_ops, pch_ops))
        P.barrier()
        if STOP_AFTER == "A1":
            dbg_dump("xT", xT.rearrange("p a b -> p (a b)"), [("xT",)])
            dbg_dump("z_tok", z_tok.rearrange("p a b -> p (a b)"), ["z_tok"])
            dbg_dump("zT_s", zT_s.rearrange("p a b -> p (a b)"), ["zT_s"])
            dbg_dump("dtraw", dtraw.rearrange("p a b -> p (a b)"), ["dtraw"])
            dbg_dump("dtT_s", dtT_s, ["dtT_s"])
            finish()
            return nc

        sc.reset(m_A)
        tA = ssd_temps(sc)
        ssd_emit([ssd_chunk(tA, xT, c, dtraw[:, c, :], "dtraw", main=True, ztok_c=z_tok[:, c, :], ssdg_bc=ssdg_bc) for c in range(8)])
        if STOP_AFTER == "A2a":
            P.barrier()
            dbg_dump("YT", YT.rearrange("p a b -> p (a b)"), [("YTb", c) for c in range(8)])
            finish()
            return nc
        hout = sc.f32([8, 128])
        for half in range(2):
            def trh(e, half=half):
                ins = None
                for jj in range(4):
                    q = half * 4 + jj
                    ins = e.transpose(out=pbf[3 + half][:, jj * 128:(jj + 1) * 128], in_=hT[:, q * 128:(q + 1) * 128], identity=identF)
                return ins
            P.op("pe", trh, r=["hT", KC], w=[("ps", 3 + half)])
            P.op("dve", lambda e, half=half: e.tensor_copy(out=hout[:, half * 4:(half + 1) * 4, :].rearrange("p a b -> p (a b)"), in_=pbf[3 + half]),
                 r=[("ps", 3 + half)], w=["hout"])
        P.op("sp", lambda e: e.dma_start(out=nssm_p.rearrange("(q r) n -> r q n", r=128), in_=hout), r=["hout"], w=["o_nsp"], dma=True)
        OUTKEYS.append("o_nsp")

        if STOP_AFTER == "A2b":
            P.barrier()
            dbg_dump("YT", YT.rearrange("p a b -> p (a b)"), [("YTb", c) for c in range(8)])
            finish()
            return nc
        P.barrier()
        SMPW = 7300
        scS = Bump(arena, AW - SMPW, AW)
        _saved_main = P.ops
        P.ops = []
        _sc_main = sc
        sc = scS
        sm_s = sc.f32([8, NS])
        cat = sc.f32([32])
        rep = sc.f32([8, 32])
        dtx = sc.f32([8, NS])
        ys = sc.f32([8, NS])
        ysq = sc.f32([8, NS])
        rs = sc.f32([2, NS])
        BC_tok = sc.bf16([512])
        selB = sc.bf16([NS, 128])
        NHS = 3
        hs = [sc.f32([8, 128]) for _ in range(NHS)]
        tmpS = [sc.f32([1024]) for _ in range(2)]
        x0 = sm_s[:NS, 0, :]; e1 = sm_s[:NS, 1, :]; anp = sm_s[:NS, 2, 0:1]
        P.op("act", lambda e: e.activation(out=e1, in_=dtT_s[:NS, :], func=AF.Exp, bias=dtb_pp[:NS, :]), r=["dtT_s", KC], w=["s_e1"])
        P.op("act", lambda e: e.activation(out=cat[:NS, 16:32], in_=e1, func=AF.Ln, bias=1.0), r=["s_e1"], w=["s_dt"])
        P.op("act", lambda e: e.activation(out=anp, in_=al_pp[:NS, :], func=AF.Exp), r=[KC], w=["s_anp0"])
        P.op("dve", lambda e: e.tensor_scalar(out=anp, in0=anp, scalar1=-1.0, scalar2=None, op0=ALU.mult), r=["s_anp0"], w=["s_anp"])
        P.op("act", lambda e: e.activation(out=cat[:NS, 0:16], in_=cat[:NS, 16:32], func=AF.Exp, scale=anp), r=["s_dt", "s_anp"], w=["s_dA"])

        def repf(e):
            ins = None
            for q in range(8):
                ins = e.matmul(pbf[7][:, q * 32:(q + 1) * 32], lhsT=Rsel[:NS, q * 128:(q + 1) * 128], rhs=cat[:NS, :], start=True, stop=True)
            return ins
        P.op("pe", repf, r=["s_dA", "s_dt", "cst2"], w=[("ps", 7)])
        P.op("dve", lambda e: e.tensor_copy(out=rep.rearrange("p a b -> p (a b)"), in_=pbf[7][:, 0:256]), r=[("ps", 7)], w=["rep"])
        xs_s = xT_s[:, 0:8, :]
        P.op("dve", lambda e: e.tensor_tensor(out=dtx, in0=rep[:, :, 16:32], in1=xs_s, op=ALU.mult), r=["rep", "xT_s"], w=["dtx"])
        psbc = pbb[7][:NS, 0:512].rearrange("p (a b) -> p a b", a=4)

        def trbc(e):
            ins = None
            for jj in range(4):
                ins = e.transpose(out=psbc[:, jj, :], in_=xT_s[:, 8 + jj, :], identity=identB)
            return ins
        P.op("pe", trbc, r=["xT_s", "identB"], w=[("ps", 7)])
        P.op("act", lambda e: e.activation(out=BC_tok[:NS, :], in_=pbb[7][:NS, 0:512], func=AF.Copy), r=[("ps", 7)], w=["BC_tok"])
        P.op("dve", lambda e: e.tensor_copy(out=selB[:NS, :, :], in_=identB[:NS, 0:NS].unsqueeze(2).broadcast_to([NS, NS, 128])), r=["identB"], w=["selB"])
        P.op("dve", lambda e: e.memset(ys, 0.0), w=["ys"])

        def hs_load(b):
            P.op("sp", lambda e, b=b: e.dma_start(out=hs[b % NHS], in_=sssm[b].rearrange("(q r) n -> r q n", r=128)), w=[("hs", b % NHS, q) for q in range(8)], dma=True)
        for b in range(min(NHS - 1, NS)):
            hs_load(b)
        for b in range(NS):
            hb = hs[b % NHS]
            bank = 5 + b % 2
            if b + NHS - 1 < NS:
                hs_load(b + NHS - 1)
            P.op("pe", lambda e, b=b, bank=bank: e.matmul(pbf[bank], lhsT=selB[:NS, b, :], rhs=BC_tok[:NS, :], start=True, stop=True),
                 r=["selB", "BC_tok"], w=[("ps", bank)])
            for q in range(8):
                P.op("act", lambda e, b=b, q=q, hb=hb: e.activation(out=hb[:, q, :], in_=hb[:, q, :], func=AF.Copy, scale=rep[:, q, b:b + 1]),
                     r=[("hs", b % NHS, q), "rep"], w=[("hs", b % NHS, q)])

            bc4 = pbf[bank][:, 0:256].rearrange("p (g n) -> p g n", g=2).unsqueeze(2).broadcast_to([128, 2, 4, 128])
            cc4 = pbf[bank][:, 256:512].rearrange("p (g n) -> p g n", g=2).unsqueeze(2).broadcast_to([128, 2, 4, 128])
            hb4 = hb.rearrange("p (g j) n -> p g j n", g=2)
            tm4 = tmpS[b % 2].rearrange("p (g j n) -> p g j n", g=2, j=4)
            dx4 = dtx[:, :, b:b + 1].rearrange("p (g j) o -> p g j o", g=2).broadcast_to([128, 2, 4, 128])
            khs = [("hs", b % NHS, q) for q in range(8)]
            ktm = ("tmpS", b % 2)
            P.op("dve", lambda e, tm4=tm4, bc4=bc4, dx4=dx4: e.tensor_tensor(out=tm4, in0=bc4, in1=dx4, op=ALU.mult), r=[("ps", bank), "dtx"], w=[ktm])
            P.op("dve", lambda e, hb4=hb4, tm4=tm4: e.tensor_tensor(out=hb4, in0=hb4, in1=tm4, op=ALU.add), r=khs + [ktm], w=khs)
            P.op("dve", lambda e, hb4=hb4, tm4=tm4, cc4=cc4: e.tensor_tensor(out=tm4, in0=cc4, in1=hb4, op=ALU.mult), r=khs + [("ps", bank), ktm], w=[ktm])
            P.op("dve", lambda e, b=b, tm=tmpS[b % 2]: e.tensor_reduce(out=ys[:, :, b:b + 1], in_=tm.rearrange("p (q n) -> p q n", q=8), axis=AX.X, op=ALU.add),
                 r=[ktm], w=["ys"])
            ok = ("o_nss", b)
            P.op("sp", lambda e, b=b, hb=hb: e.dma_start(out=nssm_s[b].rearrange("(q r) n -> r q n", r=128), in_=hb), r=[("hs", b % NHS, q) for q in range(8)], w=[ok], dma=True)
            OUTKEYS.append(ok)
        P.op("dve", lambda e: e.tensor_tensor(out=ysq, in0=xs_s, in1=D_pp.unsqueeze(2).broadcast_to([128, 8, NS]), op=ALU.mult), r=["xT_s", KC], w=["ysq"])
        P.op("dve", lambda e: e.tensor_tensor(out=ys, in0=ys, in1=ysq, op=ALU.add), r=["ys", "ysq"], w=["ys"])
        P.op("dve", lambda e: e.tensor_tensor(out=ys, in0=ys, in1=zT_s, op=ALU.mult), r=["ys", "zT_s"], w=["ys"])
        P.op("dve", lambda e: e.tensor_tensor(out=ysq, in0=ys, in1=ys, op=ALU.mult), r=["ys", "ysq"], w=["ysq"])

        def gsum(e):
            ins = None
            for q in range(8):
                g = q // 4
                ins = e.matmul(pbf[7][:, g * NS:(g + 1) * NS], lhsT=onesF, rhs=ysq[:, q, :], start=(q % 4 == 0), stop=(q % 4 == 3))
            return ins
        P.op("pe", gsum, r=["ysq", KC], w=[("ps", 7)])
        rsf = rs.rearrange("p a b -> p (a b)")
        P.op("act", lambda e: e.activation(out=rsf, in_=pbf[7][:, 0:2 * NS], func=AF.Ln, scale=1.0 / 512, bias=EPS), r=[("ps", 7)], w=["rs0"])
        P.op("act", lambda e: e.activation(out=rsf, in_=rsf, func=AF.Exp, scale=-0.5), r=["rs0"], w=["rs"])
        for g in range(2):
            P.op("dve", lambda e, g=g: e.tensor_tensor(out=ys[:, g * 4:(g + 1) * 4, :], in0=ys[:, g * 4:(g + 1) * 4, :],
                                                       in1=rs[:, g:g + 1, :].broadcast_to([128, 4, NS]), op=ALU.mult), r=["ys", "rs"], w=["ys"])
        P.op("dve", lambda e: e.tensor_tensor(out=YT[:, 8:16, TM:T], in0=ys, in1=sg_pp.unsqueeze(2).broadcast_to([128, 8, NS]), op=ALU.mult),
             r=["ys", KC], w=[("YTb", 8)])
        smp_ops = P.ops
        P.ops = _saved_main
        sc = _sc_main

        sc.reset()
        sc.end = AW - SMPW
        _saved_a3 = P.ops
        P.ops = []
        vng_bc = sc.f32([1024])
        bs_bc = sc.f32([8, 128])
        v_tok = sc.bf16([9, 1024])
        WsT = sc.bf16([8, 128])
        sqacc = sc.f32([T])
        gv = [sc.f32([1024]) for _ in range(2)]
        wsraw = gv[0].rearrange("p (a b) -> p a b", a=8)
        vst = [sc.f32([4]) for _ in range(2)]
        ug = [sc.f32([T]) for _ in range(2)]
        _m_tm = sc.mark()
        tmpm = [sc.f32([512]) for _ in range(2)]
        _m_tm2 = sc.mark()
        sc.reset(_m_tm)
        vsf = sc.f32([1024])
        sc.reset(_m_tm2)
        sq = [sc.bf16([T]) for _ in range(2)]
        rstd_bc = sqacc
        w00I = sc.bf16([8, NS])
        P.op("sp", lambda e: e.dma_start(out=vng_bc, in_=vn_g.partition_broadcast(128)), w=["vng"], dma=True)
        P.op("sp", lambda e: e.dma_start(out=bs_bc.rearrange("p a b -> p (a b)"), in_=cst2_d[:, C2_BS:C2_BS + 1024]), w=["bs_bc"], dma=True)
        P.op("sp", lambda e: e.dma_start(out=wsraw, in_=ws_d.rearrange("h t s -> t h s")), w=["wsraw"], dma=True)
        wv2 = [wload(wview(w_in, COL_V, 512)), wload(wview(w_in, COL_V + 512, 512))]
        for half in range(2):
            def trw(e, half=half):
                ins = None
                for jj in range(4):
                    h_ = half * 4 + jj
                    ins = e.transpose(out=pbf[half][:, jj * 128:(jj + 1) * 128], in_=wsraw[:, h_, :], identity=identF)
                return ins
            P.op("pe", trw, r=["wsraw", KC], w=[("ps", half)])
            P.op("dve", lambda e, half=half: e.tensor_tensor(out=WsT[:, half * 4:(half + 1) * 4, :], in0=pbf[half].rearrange("p (a b) -> p a b", a=4),
                                                             in1=triF.unsqueeze(1).broadcast_to([128, 4, 128]), op=ALU.mult), r=[("ps", half), KC], w=["WsT"])
        for h_ in range(8):
            P.op("dve", lambda e, h_=h_: e.tensor_scalar(out=w00I[:NS, h_, :], in0=identF[:NS, 0:NS], scalar1=w00_bc[:NS, h_:h_ + 1], scalar2=None, op0=ALU.mult),
                 r=[KC], w=["w00I"])
        for tt in range(9):
            rows = 128 if tt < 8 else NS
            i = tt % 2
            tcols = slice(tt * 128, tt * 128 + rows)
            for zb in range(2):
                bank = 1 + (tt * 2 + zb) % 4
                mm_group(pbf[bank][:rows, :], [(nT[:, k, tcols], wv2[zb][0][:, k, :]) for k in range(16)], r=[wv2[zb][1], ("nT", tt)], w=[("ps", bank)])
                P.op("act", lambda e, i=i, zb=zb, bank=bank, rows=rows: e.activation(out=gv[i][:rows, zb * 512:(zb + 1) * 512], in_=pbf[bank][:rows, :],
                                                                                    func=AF.Gelu_apprx_tanh), r=[("ps", bank)], w=[("gv", i, zb)])
            P.op("act", lambda e, i=i, rows=rows: e.activation(out=ug[i][:rows, 0:1024], in_=gv[i][:rows, :], func=AF.Square, accum_out=vst[i][:rows, 0:1]),
                 r=[("gv", i, 0), ("gv", i, 1)], w=[("ug", i), ("vst", i)])
            P.op("act", lambda e, i=i, rows=rows: e.activation(out=vst[i][:rows, 1:2], in_=vst[i][:rows, 0:1], func=AF.Ln, scale=1.0 / 1024, bias=EPS),
                 r=[("vst", i)], w=[("vst", i)])
            P.op("act", lambda e, i=i, rows=rows: e.activation(out=vst[i][:rows, 2:3], in_=vst[i][:rows, 1:2], func=AF.Exp, scale=-0.5),
                 r=[("vst", i)], w=[("vst", i)])
            P.op("dve", lambda e, i=i, rows=rows, tt=tt: e.scalar_tensor_tensor(out=v_tok[:rows, tt, :], in0=gv[i][:rows, :], scalar=vst[i][:rows, 2:3], op0=ALU.mult,
                                                                                in1=vng_bc[:rows, :], op1=ALU.mult),
                 r=[("gv", i, 0), ("gv", i, 1), ("vst", i), "vng"], w=[("v_tok", tt)])
            if tt == 8:
                P.op("dve", lambda e, i=i: e.scalar_tensor_tensor(out=vsf[:NS, :], in0=gv[i][:NS, :], scalar=vst[i][:NS, 2:3], op0=ALU.mult,
                                                                  in1=vng_bc[:NS, :], op1=ALU.mult), r=[("gv", i, 0), ("gv", i, 1), ("vst", i), "vng"], w=["vsf", ("tmpm", 0), ("tmpm", 1)])
                P.op("sp", lambda e: e.dma_start(out=v_s[:, :], in_=vsf[:NS, :]), r=["vsf", ("tmpm", 0), ("tmpm", 1)], w=["o_vs"], dma=True)
                OUTKEYS.append("o_vs")
        wu2 = [wload(wview(w_in, COL_U, 512)), wload(wview(w_in, COL_U + 512, 512))]
        tok_blocks = [(0, 512), (512, 1024), (TM, T)]
        grpc = [0]

        def stage_G(h_):
            ub, jj = h_ // 4, h_ % 4
            Wv, kWu = wu2[ub]
            ui = h_ % 2
            for nb, (lo, hi) in enumerate(tok_blocks):
                bank = grpc[0] % 2
                grpc[0] += 1
                n = hi - lo
                rk = [("nT", tt) for tt in range(nb * 4, nb * 4 + 4)] if nb < 2 else [("nT", 8)]
                mm_group(pbf[bank][:, 0:n], [(Wv[:, k, jj * 128:(jj + 1) * 128], nT[:, k, lo:hi]) for k in range(16)], r=[kWu] + rk, w=[("ps", bank)])
                P.op("act", lambda e, ui=ui, lo=lo, hi=hi, n=n, bank=bank: e.activation(out=ug[ui][:, lo:hi], in_=pbf[bank][:, 0:n], func=AF.Gelu_apprx_tanh),
                     r=[("ps", bank)], w=[("ug", ui)])

        def stage_M(h_):
            ui = h_ % 2
            for half in range(2):
                def mix(e, half=half, h_=h_):
                    ins = None
                    for cc in range(4):
                        c = half * 4 + cc
                        ins = e.matmul(pbf[2 + half][:, cc * 128:(cc + 1) * 128], lhsT=v_tok[:, c, h_ * 128:(h_ + 1) * 128], rhs=WsT[:, h_, :], start=True, stop=True)
                    return ins
                P.op("pe", mix, r=[("v_tok", c) for c in range(half * 4, half * 4 + 4)] + ["WsT"], w=[("ps", 2 + half)])
                tm = tmpm[half]
                P.op("dve", lambda e, half=half, h_=h_, tm=tm: e.tensor_tensor(out=tm.rearrange("p (a b) -> p a b", a=4), in0=pbf[2 + half].rearrange("p (a b) -> p a b", a=4),
                                                                               in1=bs_bc[:, h_:h_ + 1, :].broadcast_to([128, 4, 128]), op=ALU.add),
                     r=[("ps", 2 + half), "bs_bc"], w=[("tmpm", half)])
                P.op("dve", lambda e, half=half, h_=h_, tm=tm, ui=ui: e.tensor_tensor(out=YT[:, h_, half * 512:(half + 1) * 512], in0=tm, in1=ug[ui][:, half * 512:(half + 1) * 512], op=ALU.mult),
                     r=[("tmpm", half), ("ug", ui)], w=[("ya", h_)])
            P.op("pe", lambda e, h_=h_: e.matmul(pbf[4][:, 0:NS], lhsT=v_tok[:NS, 8, h_ * 128:(h_ + 1) * 128], rhs=w00I[:NS, h_, :], start=True, stop=True),
                 r=[("v_tok", 8), "w00I"], w=[("ps", 4)])
            P.op("dve", lambda e, h_=h_, ui=ui: e.scalar_tensor_tensor(out=YT[:, h_, TM:T], in0=pbf[4][:, 0:NS], scalar=b0_bc[:, h_:h_ + 1], op0=ALU.add, in1=ug[ui][:, TM:T], op1=ALU.mult),
                 r=[("ps", 4), ("ug", ui), KC], w=[("ya", h_)])
            si = h_ % 2
            if h_ == 0:
                P.op("act", lambda e, h_=h_: e.activation(out=sqacc, in_=YT[:, h_, :], func=AF.Square), r=[("ya", h_)], w=["sqacc"])
            else:
                P.op("act", lambda e, h_=h_, si=si: e.activation(out=sq[si], in_=YT[:, h_, :], func=AF.Square), r=[("ya", h_)], w=[("sq", si)])
                P.op("dve", lambda e, si=si: e.tensor_tensor(out=sqacc, in0=sqacc, in1=sq[si], op=ALU.add), r=["sqacc", ("sq", si)], w=["sqacc"])

        stage_G(0)
        for h_ in range(8):
            if h_ + 1 < 8:
                stage_G(h_ + 1)
            stage_M(h_)
        P.op("dve", lambda e: e.tensor_copy(out=sq[0], in_=sqacc), r=["sqacc", ("sq", 0)], w=[("sq", 0)])
        sbank = (0, 1, 4)
        for nb, (lo, hi) in enumerate(tok_blocks):
            n = hi - lo
            P.op("pe", lambda e, nb=nb, lo=lo, hi=hi, n=n: e.matmul(pbf[sbank[nb]][:, 0:n], lhsT=onesB, rhs=sq[0][:, lo:hi], start=True, stop=True),
                 r=[("sq", 0), "onesB"], w=[("ps", sbank[nb])])
        for nb, (lo, hi) in enumerate(tok_blocks):
            n = hi - lo
            P.op("act", lambda e, nb=nb, lo=lo, hi=hi, n=n: e.activation(out=rstd_bc[:, lo:hi], in_=pbf[sbank[nb]][:, 0:n], func=AF.Ln, scale=1.0 / 1024, bias=EPS),
                 r=[("ps", sbank[nb])], w=[("rstd", nb)])
            P.op("act", lambda e, lo=lo, hi=hi: e.activation(out=rstd_bc[:, lo:hi], in_=rstd_bc[:, lo:hi], func=AF.Exp, scale=-0.5), r=[("rstd", nb)], w=[("rstd", nb)])
        for h_ in range(8):
            P.op("dve", lambda e, h_=h_: e.scalar_tensor_tensor(out=YT[:, h_, :], in0=YT[:, h_, :], scalar=gon_pp[:, h_:h_ + 1], op0=ALU.mult, in1=rstd_bc, op1=ALU.mult),
                 r=[("ya", h_), ("rstd", 0), ("rstd", 1), ("rstd", 2), KC], w=[("YTa", h_)])
        _a3_ops = P.ops
        P.ops = _saved_a3
        P.ops.extend(Prog.merge(_a3_ops, smp_ops))
        sc.end = AW
        P.barrier()
        if STOP_AFTER == "A3":
            dbg_dump("YT", YT.rearrange("p a b -> p (a b)"), [("YTa", h_) for h_ in range(8)])
            finish()
            return nc

        sc.reset()
        hres = sc.f32([9, DM])
        m_B = sc.mark()
        for tt in range(9):
            rows = 128 if tt < 8 else NS
            src = xm[tt * 128:(tt + 1) * 128, :] if tt < 8 else xsmp[:, :]
            P.op("sp", lambda e, tt=tt, rows=rows, src=src: e.dma_start(out=hres[:rows, tt, :], in_=src), w=[("h", tt, cb) for cb in range(4)], dma=True)
        wcur = wload(wview(w_out, 0, 512))
        for cb in range(4):
            Wv, kW = wcur
            if cb + 1 < 4:
                wcur = wload(wview(w_out, (cb + 1) * 512, 512))
            for tt in range(9):
                rows = 128 if tt < 8 else NS
                tcols = slice(tt * 128, tt * 128 + rows)
                bank = (cb * 9 + tt) % 6
                mm_group(pbf[bank][:rows, :], [(YT[:, k, tcols], Wv[:, k, :]) for k in range(16)], r=[kW], w=[("ps", bank)])
                P.op("dve", lambda e, rows=rows, tt=tt, cb=cb, bank=bank: e.tensor_tensor(out=hres[:rows, tt, cb * 512:(cb + 1) * 512], in0=pbf[bank][:rows, :],
                                                                                         in1=hres[:rows, tt, cb * 512:(cb + 1) * 512], op=ALU.add),
                     r=[("ps", bank), ("h", tt, cb)], w=[("h", tt, cb)])
        P.barrier()
        if STOP_AFTER == "B":
            dbg_dump("h", hres.rearrange("p a b -> p (a b)"), [("h", tt) for tt in range(9)])
            finish()
            return nc

        sc.reset(m_B)
        gbc = sc.f32([DM])
        stmp = [sc.f32([512]) for _ in range(2)]
        ST = [sc.f32([4]) for _ in range(2)]
        r2.reset()
        actb = [r2.bf16([2, T]) for _ in range(2)]
        Wd = [r2.bf16([2, DM]) for _ in range(2)]
        XN = [r2.bf16([DM]) for _ in range(2)]
        mT = nT
        P.op("sp", lambda e: e.dma_start(out=gbc, in_=g_ffn.partition_broadcast(128)), w=["gbc"], dma=True)
        stgC = []
        for tt in range(9):
            rows = 128 if tt < 8 else NS
            i = tt % 2
            stgC.append(rms_transpose(hres[:, tt, :], [("h", tt, cb) for cb in range(4)], rows, XN[i], ("XN", i), ST[i], ("ST", i), gbc, "gbc", mT, tt * 128, ("mT", tt), DM, tt))
        rms_pipeline(stgC)
        NFB = DFF // 256
        if STOP_AFTER == "C0":
            P.barrier()
            dbg_dump("mT", mT.rearrange("p a b -> p (a b)"), [("mT", tt) for tt in range(9)])
            dbg_dump("XN0", XN[0], [("XN", 0)])
            dbg_dump("ST0", ST[0], [("ST", 0)])
            dbg_dump("gbc", gbc, ["gbc"])
            dbg_dump("h", hres.rearrange("p a b -> p (a b)"), [("h", tt) for tt in range(9)])
            finish()
            return nc

        gu_slot = {}

        def load_gu(fb):
            slot = wq[0] % 2
            wq[0] += 1
            gu_slot[fb] = slot
            load_w(WG[slot], wview(w_gate, fb * 256, 256), ("W", slot), nobar=True)
            load_w(WU[slot], wview(w_up, fb * 256, 256), ("W", slot), nobar=True)

        def load_d(fb):
            load_w(Wd[fb % 2], w_down[fb * 256:(fb + 1) * 256, :].rearrange("(fl p) c -> p fl c", p=128), ("Wd", fb % 2))

        gctr = [0]

        def GU_units(fb):
            slot = gu_slot[fb]
            Wg_, Wu_, kWs = WG[slot], WU[slot], ("W", slot)
            units = []
            for fl in range(2):
                for nb, (lo, hi) in enumerate(tok_blocks):
                    def unit(fl=fl, nb=nb, lo=lo, hi=hi):
                        n = hi - lo
                        pr = gctr[0] % 2
                        gctr[0] += 1
                        bg, bu = pr * 2, pr * 2 + 1
                        rk = [("mT", tt) for tt in range(nb * 4, nb * 4 + 4)] if nb < 2 else [("mT", 8)]
                        mm_group(pbf[bg][:, 0:n], [(Wg_[:, k, fl * 128:(fl + 1) * 128], mT[:, k, lo:hi]) for k in range(16)], r=[kWs] + rk, w=[("ps", bg)])
                        mm_group(pbf[bu][:, 0:n], [(Wu_[:, k, fl * 128:(fl + 1) * 128], mT[:, k, lo:hi]) for k in range(16)], r=[kWs] + rk, w=[("ps", bu)])
                        P.op("act", lambda e: e.activation(out=stmp[pr][:, 0:n], in_=pbf[bg][:, 0:n], func=AF.Silu), r=[("ps", bg)], w=[("stmp", pr)])
                        P.op("dve", lambda e: e.tensor_tensor(out=actb[fb % 2][:, fl, lo:hi], in0=pbf[bu][:, 0:n], in1=stmp[pr][:, 0:n], op=ALU.mult),
                             r=[("ps", bu), ("stmp", pr)], w=[("act", fb % 2)])
                    units.append(P.capture(unit))
            return units

        dctr = [0]

        def DN_groups(fb):
            groups = []
            for tt in range(9):
                rows = 128 if tt < 8 else NS
                tcols = slice(tt * 128, tt * 128 + rows)
                for cb in range(4):
                    def grp_(tt=tt, rows=rows, tcols=tcols, cb=cb):
                        bank = 4 + dctr[0] % 4
                        dctr[0] += 1
                        mm_group(pbf[bank][:rows, :], [(actb[fb % 2][:, fl, tcols], Wd[fb % 2][:, fl, cb * 512:(cb + 1) * 512]) for fl in range(2)],
                                 r=[("act", fb % 2), ("Wd", fb % 2)], w=[("ps", bank)])
                        P.op("dve", lambda e: e.tensor_tensor(out=hres[:rows, tt, cb * 512:(cb + 1) * 512], in0=pbf[bank][:rows, :],
                                                              in1=hres[:rows, tt, cb * 512:(cb + 1) * 512], op=ALU.add),
                             r=[("ps", bank), ("h", tt, cb)], w=[("h", tt, cb)])
                    groups.append(P.capture(grp_))
            return groups

        load_gu(0)
        load_d(0)
        for i in range(NFB + 1):
            if i + 1 < NFB:
                load_gu(i + 1)
            gu = GU_units(i) if i < NFB else []
            dn = DN_groups(i - 1) if i >= 1 else []
            nslot = max(len(gu), 1)
            per = -(-len(dn) // nslot)
            for u in range(nslot):
                if u < len(gu):
                    P.ops.extend(gu[u])
                for g_ in dn[u * per:(u + 1) * per]:
                    P.ops.extend(g_)
            if i + 1 < NFB:
                load_d(i + 1)
        P.barrier()
        if STOP_AFTER == "C":
            dbg_dump("h", hres.rearrange("p a b -> p (a b)"), [("h", tt) for tt in range(9)])
            finish()
            return nc

        P.op("sp", lambda e: e.dma_start(out=gbc, in_=g_fin.partition_broadcast(128)), w=["gbc"], dma=True)
        for tt in range(9):
            rows = 128 if tt < 8 else NS
            i = tt % 2
            ht = hres[:, tt, :]
            st = ST[i]
            P.op("act", lambda e, rows=rows, ht=ht, st=st, i=i: e.activation(out=XN[i][:rows, :], in_=ht[:rows, :], func=AF.Square, accum_out=st[:rows, 0:1]),
                 r=[("h", tt, cb) for cb in range(4)], w=[("XN", i), ("ST", i)])
            P.op("act", lambda e, rows=rows, st=st: e.activation(out=st[:rows, 1:2], in_=st[:rows, 0:1], func=AF.Ln, scale=1.0 / DM, bias=EPS), r=[("ST", i)], w=[("ST", i)])
            P.op("act", lambda e, rows=rows, st=st: e.activation(out=st[:rows, 2:3], in_=st[:rows, 1:2], func=AF.Exp, scale=-0.5), r=[("ST", i)], w=[("ST", i)])
            P.op("dve", lambda e, rows=rows, ht=ht, st=st: e.scalar_tensor_tensor(out=ht[:rows, :], in0=ht[:rows, :], scalar=st[:rows, 2:3], op0=ALU.mult, in1=gbc[:rows, :], op1=ALU.mult),
                 r=[("h", tt, cb) for cb in range(4)] + [("ST", i), "gbc"], w=[("h", tt, cb) for cb in range(4)])
            dst = y_m[tt * 128:(tt + 1) * 128, :] if tt < 8 else y_s[:, :]
            ok = ("o_y", tt)
            P.op("sp", lambda e, rows=rows, ht=ht, dst=dst: e.dma_start(out=dst, in_=ht[:rows, :]), r=[("h", tt, cb) for cb in range(4)], w=[ok], dma=True)
            OUTKEYS.append(ok)
        finish()
        return nc


def _host_consts(inputs, core):
    hf = core % 2
    cst = np.zeros((128, CSTW), np.float32)
    r = np.arange(128)
    cst[:, C_ID:C_ID + 128] = np.eye(128, dtype=np.float32)
    cst[:, C_TRI:C_TRI + 128] = (r[:, None] <= r[None, :]).astype(np.float32)
    cst[:, C_U:C_U + 128] = (r[:, None] > r[None, :]).astype(np.float32)
    cst[:, C_ONE:C_ONE + 128] = 1.0
    cw = np.asarray(inputs["ssd_conv_w"])[0]
    cb = np.asarray(inputs["ssd_conv_b"])[0]
    cwb = np.concatenate([cw, cb[None]], 0)
    cst[:, C_CWB:C_CWB + 60] = cwb.reshape(5, 12, 128).transpose(2, 1, 0).reshape(128, 60)
    cst[:, C_GON:C_GON + 8] = np.asarray(inputs["chunk_out_norm_g"])[0].reshape(8, 128).T
    cst[:, C_ALOG:C_ALOG + 16] = np.asarray(inputs["ssd_a_log"])[0][None, :]
    cst[:, C_DTB:C_DTB + 16] = np.asarray(inputs["ssd_dt_bias"])[0][None, :]
    cst[:, C_D:C_D + 16] = np.asarray(inputs["ssd_d"])[0][None, :]
    cst[:, C_W00:C_W00 + 8] = np.asarray(inputs["chunk_w_s"])[0][:, 0, 0][None, :]
    cst[:, C_B0:C_B0 + 8] = np.asarray(inputs["chunk_b_s"])[0][:, 0][None, :]
    cst[:, C_FLAG] = float(hf)
    cst[:, C_DPP:C_DPP + 8] = np.repeat(np.asarray(inputs["ssd_d"])[0], 64).reshape(8, 128).T
    cst[:, C_SGPP:C_SGPP + 8] = np.asarray(inputs["ssd_norm_g"])[0].reshape(8, 128).T
    cst[0:16, C_DTBPP] = np.asarray(inputs["ssd_dt_bias"])[0]
    cst[0:16, C_ALPP] = np.asarray(inputs["ssd_a_log"])[0]
    cst2 = np.zeros((128, CST2W), np.float32)
    cst2[:, C2_BS:C2_BS + 1024] = np.asarray(inputs["chunk_b_s"])[0].reshape(1, 1024)
    rs = np.zeros((16, 8, 128), np.float32)
    for q in range(8):
        for rr in range(128):
            rs[2 * q + rr // 64, q, rr] = 1.0
    cst2[0:16, C2_RSEL:C2_RSEL + 1024] = rs.reshape(16, 1024)
    return cst, cst2


def make_in_maps(inputs):
    xp = np.asarray(inputs["x_prompt"], np.float32)
    xs = np.asarray(inputs["x_sample"], np.float32)
    sconv = np.asarray(inputs["state_conv"], np.float32)[0]
    sssm = np.asarray(inputs["state_ssm"], np.float32)[0]
    shared = {
        "w_in": np.ascontiguousarray(np.asarray(inputs["w_in"], np.float32)[0]),
        "w_out": np.ascontiguousarray(np.asarray(inputs["w_out"], np.float32)[0]),
        "w_gate": np.ascontiguousarray(np.asarray(inputs["w_gate"], np.float32)[0]),
        "w_up": np.ascontiguousarray(np.asarray(inputs["w_up"], np.float32)[0]),
        "w_down": np.ascontiguousarray(np.asarray(inputs["w_down"], np.float32)[0]),
        "ws": np.ascontiguousarray(np.asarray(inputs["chunk_w_s"], np.float32)[0]),
        "g_mix": np.ascontiguousarray(np.asarray(inputs["norm_mix_g"], np.float32)[0]),
        "g_ffn": np.ascontiguousarray(np.asarray(inputs["norm_ffn_g"], np.float32)[0]),
        "g_fin": np.ascontiguousarray(np.asarray(inputs["norm_final_g"], np.float32)),
        "vn_g": np.ascontiguousarray(np.asarray(inputs["chunk_v_norm_g"], np.float32)[0]),
        "ssd_g": np.ascontiguousarray(np.asarray(inputs["ssd_norm_g"], np.float32)[0]),
    }
    zeros_prev = np.zeros((TM, DM), np.float32)
    maps = []
    for c in range(8):
        b, hf = c // 2, c % 2
        cst, cst2 = _host_consts(inputs, c)
        m = dict(shared)
        m["xm"] = np.ascontiguousarray(xp[b, hf * TM:(hf + 1) * TM])
        m["xprev"] = np.ascontiguousarray(xp[b, 0:TM]) if hf == 1 else zeros_prev
        m["xsmp"] = np.ascontiguousarray(xs[c * NS:(c + 1) * NS, 0])
        m["sconv"] = np.ascontiguousarray(sconv[c * NS:(c + 1) * NS])
        m["sssm"] = np.ascontiguousarray(sssm[c * NS:(c + 1) * NS].reshape(NS, 1024, 128))
        m["cst"] = cst
        m["cst2"] = cst2
        maps.append(m)
    return maps


def kernel(**inputs):
    nc = build_program()
    maps = make_in_maps(inputs)
    res = run_bass_kernel_spmd(nc, maps, core_ids=list(range(8)))
    R = res.results
    y_prompt = np.zeros((4, 2048, DM), np.float32)
    y_sample = np.zeros((128, 1, DM), np.float32)
    ncp = np.zeros((1, 4, 3, 1536), np.float32)
    nsp = np.zeros((1, 4, 16, 64, 128), np.float32)
    ncs = np.zeros((1, 128, 3, 1536), np.float32)
    nss = np.zeros((1, 128, 16, 64, 128), np.float32)
    vs = np.zeros((1, 128, 1, 1024), np.float32)
    for c in range(8):
        b, hf = c // 2, c % 2
        y_prompt[b, hf * TM:(hf + 1) * TM] = R[c]["y_m"]
        y_sample[c * NS:(c + 1) * NS, 0] = R[c]["y_s"]
        if hf == 1:
            ncp[0, b] = R[c]["nconv_p"]
            nsp[0, b] = R[c]["nssm_p"].reshape(16, 64, 128)
        ncs[0, c * NS:(c + 1) * NS] = R[c]["nconv_s"]
        nss[0, c * NS:(c + 1) * NS] = R[c]["nssm_s"].reshape(NS, 16, 64, 128)
        vs[0, c * NS:(c + 1) * NS, 0] = R[c]["v_s"]
    return (y_prompt, y_sample, ncp, nsp, ncs, nss, vs)
```

```python
import numpy as np
from contextlib import ExitStack
import concourse.bass as bass
import concourse.mybir as mybir
from concourse.bass_utils import run_bass_kernel_spmd

F32 = mybir.dt.float32
F32R = mybir.dt.float32r
BF16 = mybir.dt.bfloat16
AF = mybir.ActivationFunctionType
ALU = mybir.AluOpType
AX = mybir.AxisListType

ENGS = ["pe", "act", "dve", "pool", "sp"]
NDMASEM = 12

DEBUG = {}
STOP_AFTER = None
SSD_LEVEL = 9


class Op:
    __slots__ = ("eng", "fn", "r", "w", "dma", "deps", "sig", "sem", "prev_use", "waits", "need_sig", "xdeps", "nobar")

    def __init__(self, eng, fn, r, w, dma, nobar=False):
        self.eng, self.fn, self.r, self.w, self.dma = eng, fn, tuple(r), tuple(w), dma
        self.deps = set()
        self.xdeps = set()
        self.sig = None
        self.sem = None
        self.prev_use = 0
        self.waits = []
        self.need_sig = False
        self.nobar = nobar


BARRIER = Op(None, None, (), (), False)


class Prog:
    def __init__(self):
        self.ops = []

    def op(self, eng, fn, r=(), w=(), dma=False, nobar=False):
        w = list(w) + [k for k in r if isinstance(k, tuple) and k and k[0] == "ps" and k not in w]
        o = Op(eng, fn, r, w, dma, nobar)
        self.ops.append(o)
        return o

    def barrier(self):
        self.ops.append(BARRIER)

    def capture(self, fn):
        saved = self.ops
        self.ops = []
        try:
            fn()
            got = self.ops
        finally:
            self.ops = saved
        return got

    @staticmethod
    def merge(main, side):
        if not side:
            return list(main)
        out = []
        m, s_ = len(main), len(side)
        j = 0
        for i, o in enumerate(main):
            out.append(o)
            tgt = (i + 1) * s_ // m
            while j < tgt:
                out.append(side[j])
                j += 1
        out.extend(side[j:])
        return out

    def resolve(self):
        ops = self.ops
        last_w = {}
        readers = {}
        last_eng = {}
        dmas = []
        pending = {}
        for i, o in enumerate(ops):
            if o is BARRIER:
                s = set(last_eng.values()) | set(dmas)
                for e in ENGS:
                    pending[e] = set(s) | pending.get(e, set())
                continue
            if o.eng in pending and not o.nobar:
                o.xdeps = o.xdeps | pending.pop(o.eng)
            deps = set(o.xdeps)
            for k in o.r:
                if k in last_w:
                    deps.add(last_w[k])
            for k in o.w:
                if k in last_w:
                    deps.add(last_w[k])
                deps.update(readers.get(k, ()))
            deps.discard(i)
            keep = set()
            for d in deps:
                od = ops[d]
                if od.dma:
                    keep.add(d)
                elif od.eng == o.eng and not o.dma:
                    if o.eng == "pe":
                        continue
                    if d in o.xdeps:
                        continue
                    if set(od.w) & set(o.r):
                        keep.add(d)
                else:
                    keep.add(d)
            o.deps = keep
            for d in keep:
                ops[d].need_sig = True
            for k in o.r:
                readers.setdefault(k, []).append(i)
            for k in o.w:
                last_w[k] = i
                readers[k] = []
            if o.dma:
                dmas.append(i)
            elif o.fn is not None:
                last_eng[o.eng] = i
        cnt = {e: 0 for e in ENGS}
        dma_n = {e: 0 for e in ENGS}
        dma_use = {}
        for i, o in enumerate(ops):
            if o is BARRIER:
                continue
            if o.dma:
                slot = (o.eng, dma_n[o.eng] % NDMASEM)
                dma_n[o.eng] += 1
                u = dma_use.get(slot, 0)
                o.prev_use = u
                dma_use[slot] = u + 1
                o.sem = slot
                o.sig = 16 * (u + 1)
            elif o.need_sig:
                cnt[o.eng] += 1
                o.sem = o.eng
                o.sig = cnt[o.eng]
        seen = {e: {} for e in ENGS}
        for i, o in enumerate(ops):
            if o is BARRIER:
                continue
            sd = seen[o.eng]
            need = {}
            if o.dma and o.prev_use > 0:
                need[o.sem] = 16 * o.prev_use
            for d in o.deps:
                od = ops[d]
                need[od.sem] = max(need.get(od.sem, 0), od.sig)
            o.waits = []
            for s, v in need.items():
                if sd.get(s, 0) >= v:
                    continue
                sd[s] = v
                o.waits.append((s, v))

    def emit(self, sems, block):
        ops = self.ops

        def run(eng_name):
            def body(e):
                for o in ops:
                    if o is BARRIER or o.eng != eng_name:
                        continue
                    for s, v in o.waits:
                        e.wait_ge(sems[s], v)
                    if o.fn is None:
                        continue
                    ins = o.fn(e)
                    if o.dma:
                        ins.then_inc(sems[o.sem], 16)
                    elif o.sig is not None:
                        ins.then_inc(sems[o.sem], 1)
            return body

        block.sync(run("sp"))
        block.tensor(run("pe"))
        block.scalar(run("act"))
        block.vector(run("dve"))
        block.gpsimd(run("pool"))


DM = 2048
DIN = 4624
DFF = 5632
TM = 1024
NS = 16
T = TM + NS
EPS = 1e-6
COL_U, COL_V, COL_Z, COL_X, COL_B, COL_C, COL_DT = 0, 1024, 2048, 3072, 4096, 4352, 4608

C_ID, C_TRI, C_U, C_ONE = 0, 128, 256, 384
C_CWB, C_GON, C_ALOG, C_DTB, C_D, C_W00, C_B0, C_FLAG = 512, 572, 580, 596, 612, 628, 636, 644
C_DPP, C_SGPP, C_DTBPP, C_ALPP = 648, 656, 664, 665
CSTW = 672
C2_BS, C2_RSEL = 0, 1024
CST2W = 2048

AW = 51000


class Bump:
    def __init__(self, arena, start, end):
        self.arena, self.start, self.end, self.off = arena, start, end, start
        self.peak = start

    def reset(self, to=None):
        self.off = self.start if to is None else to

    def mark(self):
        return self.off

    def _take(self, words):
        o = self.off
        self.off += words
        assert self.off <= self.end, ("arena overflow", self.off, self.end)
        self.peak = max(self.peak, self.off)
        return o

    def f32(self, shape):
        n = int(np.prod(shape))
        o = self._take(n)
        v = self.arena[:, o:o + n]
        return _shape(v, shape)

    def bf16(self, shape):
        n = int(np.prod(shape))
        w = (n + 1) // 2
        o = self._take(w)
        v = self.arena[:, o:o + w].bitcast(BF16)[:, 0:n]
        return _shape(v, shape)


def _shape(v, shape):
    if len(shape) == 1:
        return v
    if len(shape) == 2:
        return v.rearrange("p (a b) -> p a b", a=shape[0])
    if len(shape) == 3:
        return v.rearrange("p (a b c) -> p a b c", a=shape[0], b=shape[1])
    raise ValueError(shape)


def build_program():
    nc = bass.Bass("TRN2", target_bir_lowering=False)

    def din(name, shape):
        return nc.dram_tensor(name, shape, F32, kind="ExternalInput").ap()

    def dout(name, shape):
        return nc.dram_tensor(name, shape, F32, kind="ExternalOutput").ap()

    xm = din("xm", [TM, DM])
    xprev = din("xprev", [TM, DM])
    xsmp = din("xsmp", [NS, DM])
    sconv = din("sconv", [NS, 3, 1536])
    sssm = din("sssm", [NS, 1024, 128])
    w_in = din("w_in", [DM, DIN])
    w_out = din("w_out", [DM, DM])
    w_gate = din("w_gate", [DM, DFF])
    w_up = din("w_up", [DM, DFF])
    w_down = din("w_down", [DFF, DM])
    cst_d = din("cst", [128, CSTW])
    cst2_d = din("cst2", [128, CST2W])
    ws_d = din("ws", [8, 128, 128])
    g_mix = din("g_mix", [DM])
    g_ffn = din("g_ffn", [DM])
    g_fin = din("g_fin", [DM])
    vn_g = din("vn_g", [1024])
    ssd_g = din("ssd_g", [1024])

    y_m = dout("y_m", [TM, DM])
    y_s = dout("y_s", [NS, DM])
    nconv_p = dout("nconv_p", [3, 1536])
    nssm_p = dout("nssm_p", [1024, 128])
    nconv_s = dout("nconv_s", [NS, 3, 1536])
    nssm_s = dout("nssm_s", [NS, 1024, 128])
    v_s = dout("v_s", [NS, 1024])
    dbg_d = {}
    for name, (shape, dty) in DEBUG.items():
        dbg_d[name] = nc.dram_tensor("dbg_" + name, list(shape), dty, kind="ExternalOutput").ap()

    P = Prog()
    with ExitStack() as es:
        arena = es.enter_context(nc.sbuf_tensor("arena", [128, AW], F32))
        rhsE = es.enter_context(nc.sbuf_tensor("rhsE", [128, 2048], F32))
        U32 = es.enter_context(nc.sbuf_tensor("U32", [128, 128], F32))
        pb = [es.enter_context(nc.psum_tensor(f"pb{i}", [128, 512], F32)) for i in range(8)]
        sems = {}
        for e in ENGS:
            sems[e] = es.enter_context(nc.semaphore("s_" + e))
        for e in ("sp", "pool"):
            for i in range(NDMASEM):
                sems[(e, i)] = es.enter_context(nc.semaphore(f"d_{e}_{i}"))
        block = es.enter_context(nc.Block())

        arena = arena[:, :]
        rhsEr = rhsE[:, :].bitcast(F32R)
        U32r = U32[:, :].bitcast(F32R)
        pbf = [p[:, :] for p in pb]
        pbb = [p[:, :].bitcast(BF16) for p in pb]

        fx = Bump(arena, 0, AW)
        CST = fx.f32([CSTW])
        identF = CST[:, C_ID:C_ID + 128]
        triF = CST[:, C_TRI:C_TRI + 128]
        UF = CST[:, C_U:C_U + 128]
        onesF = CST[:, C_ONE:C_ONE + 128]
        cwb = CST[:, C_CWB:C_CWB + 60].rearrange("p (j k) -> p j k", j=12)
        gon_pp = CST[:, C_GON:C_GON + 8]
        alog_bc = CST[:, C_ALOG:C_ALOG + 16]
        dtb_bc = CST[:, C_DTB:C_DTB + 16]
        D_bc = CST[:, C_D:C_D + 16]
        w00_bc = CST[:, C_W00:C_W00 + 8]
        b0_bc = CST[:, C_B0:C_B0 + 8]
        flag = CST[:, C_FLAG:C_FLAG + 1]
        D_pp = CST[:, C_DPP:C_DPP + 8]
        sg_pp = CST[:, C_SGPP:C_SGPP + 8]
        dtb_pp = CST[:, C_DTBPP:C_DTBPP + 1]
        al_pp = CST[:, C_ALPP:C_ALPP + 1]
        identB = fx.bf16([128])
        onesB = fx.bf16([128])
        aneg = fx.f32([16])
        hT = fx.f32([1024])
        hTb = fx.bf16([1024])
        prefix = fx.f32([12, 3])
        ncv = fx.f32([12, 3])
        nT = fx.bf16([16, T])
        YT = fx.bf16([16, T])
        r2_start = fx.off - (16 * T) // 2
        r2_end = fx.off
        xT_s = fx.bf16([12, NS])
        zT_s = fx.f32([8, NS])
        dtT_s = fx.f32([NS])
        Rsel = fx.f32([1024])
        Wfix = [fx.bf16([16, 512]) for _ in range(2)]
        _wflat = [w_.rearrange("p a b -> p (a b)") for w_ in Wfix]
        WG = [w_[:, 0:4096].rearrange("p (a b) -> p a b", a=16) for w_ in _wflat]
        WU = [w_[:, 4096:8192].rearrange("p (a b) -> p a b", a=16) for w_ in _wflat]
        wq = [0]
        S0 = fx.off
        sc = Bump(arena, S0, AW)
        r2 = Bump(arena, r2_start, r2_end)

        KC = ("cst",)

        P.op("sp", lambda e: e.dma_start(out=CST, in_=cst_d[:, :]), w=[KC], dma=True)
        P.op("dve", lambda e: e.tensor_copy(out=identB, in_=identF), r=[KC], w=["identB"])
        P.op("dve", lambda e: e.memset(onesB, 1.0), w=["onesB"])
        P.op("dve", lambda e: e.tensor_copy(out=U32r, in_=UF), r=[KC], w=["U32"])
        P.op("act", lambda e: e.activation(out=aneg, in_=alog_bc, func=AF.Exp), r=[KC], w=["aneg0"])
        P.op("dve", lambda e: e.tensor_scalar(out=aneg, in0=aneg, scalar1=-1.0, scalar2=None, op0=ALU.mult), r=["aneg0"], w=["aneg"])
        P.op("dve", lambda e: e.memset(hT, 0.0), w=["hT"])
        P.op("dve", lambda e: e.memset(hTb, 0.0), w=["hTb"])

        def rms_transpose(xt, kx, rows, xn, kxn, st, kst, gbc, kg, dstT, col0, kdst, width, tag, pre=None):
            nk = width // 128

            def s1():
                if pre is not None:
                    pre()
                P.op("act", lambda e: e.activation(out=xn[:rows, :], in_=xt[:rows, :], func=AF.Square, accum_out=st[:rows, 0:1]),
                     r=[kx] if not isinstance(kx, list) else kx, w=[kxn, kst])
                P.op("act", lambda e: e.activation(out=st[:rows, 1:2], in_=st[:rows, 0:1], func=AF.Ln, scale=1.0 / width, bias=EPS),
                     r=[kst], w=[kst])
                P.op("act", lambda e: e.activation(out=st[:rows, 2:3], in_=st[:rows, 1:2], func=AF.Exp, scale=-0.5),
                     r=[kst], w=[kst])
                P.op("dve", lambda e: e.scalar_tensor_tensor(out=xn[:rows, :], in0=xt[:rows, :], scalar=st[:rows, 2:3], op0=ALU.mult,
                                                             in1=gbc[:rows, :], op1=ALU.mult),
                     r=([kx] if not isinstance(kx, list) else kx) + [kst, kg, kxn], w=[kxn])

            def s2():
                for b8 in range(nk // 8):
                    bank = (tag % 2) * 2 + b8
                    psv = pbb[bank][:, 0:8 * rows].rearrange("p (a b) -> p a b", a=8)

                    def tr(e, b8=b8, psv=psv):
                        ins = None
                        for j in range(8):
                            k = b8 * 8 + j
                            ins = e.transpose(out=psv[:, j, :], in_=xn[:rows, k * 128:(k + 1) * 128], identity=identB[:rows, :rows])
                        return ins
                    P.op("pe", tr, r=[kxn, "identB"], w=[("ps", bank)])
                    dst = dstT[:, b8 * 8:(b8 + 1) * 8, col0:col0 + rows]
                    if b8 == 0:
                        P.op("act", lambda e, dst=dst, psv=psv: e.activation(out=dst, in_=psv, func=AF.Copy), r=[("ps", bank)], w=[kdst])
                    else:
                        P.op("dve", lambda e, dst=dst, psv=psv: e.tensor_copy(out=dst, in_=psv), r=[("ps", bank)], w=[kdst])
            return P.capture(s1), P.capture(s2)

        def rms_pipeline(stages):
            n = len(stages)
            for i in range(n + 1):
                if i < n:
                    P.ops.extend(stages[i][0])
                if i >= 1:
                    P.ops.extend(stages[i - 1][1])

        def mm_group(out, pairs, r, w):
            def fn(e):
                ins = None
                n = len(pairs)
                for i, (l, rh) in enumerate(pairs):
                    ins = e.matmul(out, lhsT=l, rhs=rh, start=(i == 0), stop=(i == n - 1))
                return ins
            P.op("pe", fn, r=r, w=w)

        def load_w(dst, src_ap, key, nobar=False):
            P.op("pool", lambda e: e.dma_start(out=dst, in_=src_ap), w=[key], dma=True, nobar=nobar)

        def wload(src_ap, ncols=512):
            slot = wq[0] % 2
            wq[0] += 1
            buf = Wfix[slot] if ncols == 512 else Wfix[slot][:, :, 0:ncols]
            load_w(buf, src_ap, ("W", slot), nobar=True)
            return buf, ("W", slot)

        def wview(wap, c0, ncols):
            return wap[:, c0:c0 + ncols].rearrange("(kt p) c -> p kt c", p=128)

        def conv_silu(rawpad, krp, acc, kacc, j, ntok, dst, kdst, defer=None):
            for k in (0, 1, 2):
                P.op("dve", lambda e, k=k: e.scalar_tensor_tensor(out=acc[:, 0:ntok], in0=rawpad[:, k:k + ntok], scalar=cwb[:, j, k:k + 1],
                                                                  op0=ALU.mult, in1=acc[:, 0:ntok], op1=ALU.add), r=[krp, kacc, KC], w=[kacc])
            silu_ops = P.capture(lambda: P.op("act", lambda e: e.activation(out=dst, in_=acc[:, 0:ntok], func=AF.Silu), r=[kacc], w=[kdst]))
            if defer is None:
                P.ops.extend(silu_ops)
            else:
                defer.append(silu_ops)

        def ssd_temps(b, main=True):
            t = {}
            t["sm"] = [b.f32([96]) for _ in range(2)]
            t["ex"] = b.f32([48])
            t["xdtw"] = b.bf16([16, 64])
            t["B_tok"] = b.bf16([256])
            if not main:
                return t
            t["LT"] = b.f32([2048])
            t["MT"] = b.bf16([16, 128])
            t["xs_tok"] = b.bf16([16, 64])
            t["xdt"] = b.bf16([16, 64])
            t["cbm"] = b.f32([2, 128])
            t["y1"] = b.f32([16, 64])
            t["t2"] = b.f32([16, 64])
            t["yb"] = b.bf16([1024])
            t["gn"] = b.f32([8])
            return t

        def ssd_chunk(t, xT, c, dtraw_c, kdt, main, ztok_c=None, ssdg_bc=None, banks=(0, 1, 2, 7, 1), ktag=""):
            par = c % 2
            sm, ex = t["sm"][par], t["ex"]
            cs = slice(c * 128, (c + 1) * 128)
            kx = ("xT" + ktag,)
            bs_, bx_, bb_ = banks[0], banks[1], banks[2]
            k0, k1, kdtc, kdta, kw2 = ("sm0", par), ("sm1", par), ("dtc", par), ("dta", par), ("w2", par)
            dtc = sm[:, 32:48]
            dta = sm[:, 48:64]
            w2 = sm[:, 64:80]
            toend, dec, ea = ex[:, 0:16], ex[:, 16:32], ex[:, 32:48]
            psx = pbb[bx_][:, 0:1024].rearrange("p (a b) -> p a b", a=8)
            psx3 = pbb[bx_][:, 0:1024].rearrange("p (a b) -> p a b", a=16)
            psB = pbb[bb_][:, 0:256].rearrange("p (a b) -> p a b", a=2)

            def head():
                P.op("dve", lambda e: e.tensor_tensor(out=sm[:, 0:16], in0=dtraw_c, in1=dtb_bc, op=ALU.add), r=[kdt, KC], w=[k0])
                P.op("act", lambda e: e.activation(out=sm[:, 16:32], in_=sm[:, 0:16], func=AF.Exp), r=[k0], w=[k1])
                P.op("act", lambda e: e.activation(out=dtc, in_=sm[:, 16:32], func=AF.Ln, bias=1.0), r=[k1], w=[kdtc])
                P.op("dve", lambda e: e.tensor_tensor(out=dta, in0=dtc, in1=aneg, op=ALU.mult), r=[kdtc, "aneg"], w=[kdta])

            def part1():
                def trx(e):
                    ins = None
                    for q in range(8):
                        ins = e.transpose(out=psx[:, q, :], in_=xT[:, q, cs], identity=identB)
                    return ins
                P.op("pe", trx, r=[kx, "identB"], w=[("ps", bx_)])
                if main:
                    psc = pbf[7][:, 0:256].rearrange("p (a b) -> p a b", a=2)

                    def cbf(e):
                        ins = None
                        for g in range(2):
                            ins = e.matmul(psc[:, g, :], lhsT=xT[:, 8 + g, cs], rhs=xT[:, 10 + g, cs], start=True, stop=True)
                        return ins
                    P.op("pe", cbf, r=[kx], w=[("ps", 7)])
                    P.op("dve", lambda e: e.tensor_tensor(out=t["cbm"], in0=psc, in1=triF.unsqueeze(1).broadcast_to([128, 2, 128]), op=ALU.mult),
                         r=[("ps", 7), KC], w=["cbm"])
                    for e16 in range(16):
                        P.op("dve", lambda e, e16=e16: e.tensor_scalar(out=rhsEr[:, e16 * 128:(e16 + 1) * 128], in0=triF, scalar1=dta[:, e16:e16 + 1],
                                                                       scalar2=None, op0=ALU.mult), r=[kdta, KC], w=[("rhsE", e16 // 4)])

                def small(e):
                    e.matmul(pbf[bs_][:, 0:16], lhsT=UF, rhs=dta, start=True, stop=True)
                    ins = e.matmul(pbf[bs_][:, 16:32], lhsT=onesF, rhs=dta, start=True, stop=True)
                    if main:
                        ins = e.matmul(pbf[bs_][:, 32:48], lhsT=triF, rhs=dta, start=True, stop=True)
                    return ins
                P.op("pe", small, r=[kdta, KC], w=[("ps", bs_)])
                nex = 48 if main else 32
                P.op("act", lambda e: e.activation(out=ex[:, 0:nex], in_=pbf[bs_][:, 0:nex], func=AF.Exp), r=[("ps", bs_)], w=["ex"])
                if main:
                    for i in range(4):
                        P.op("pe", lambda e, i=i: e.matmul(pbf[3 + i], lhsT=U32r, rhs=rhsEr[:, i * 512:(i + 1) * 512], start=True, stop=True),
                             r=[("rhsE", i), "U32"], w=[("ps", 3 + i)])
                        P.op("act", lambda e, i=i: e.activation(out=t["LT"][:, i * 512:(i + 1) * 512], in_=pbf[3 + i], func=AF.Exp),
                             r=[("ps", 3 + i)], w=[("LT", i)])

                def trb(e):
                    ins = None
                    for g in range(2):
                        ins = e.transpose(out=psB[:, g, :], in_=xT[:, 8 + g, cs], identity=identB)
                    return ins
                P.op("pe", trb, r=[kx, "identB"], w=[("ps", bb_)])
                P.op("dve", lambda e: e.tensor_tensor(out=w2, in0=dtc, in1=toend, op=ALU.mult), r=[kdtc, "ex"], w=[kw2])
                P.op("dve", lambda e: e.tensor_tensor(out=t["xdtw"], in0=psx3, in1=w2.unsqueeze(2).broadcast_to([128, 16, 64]), op=ALU.mult),
                     r=[("ps", bx_), kw2], w=["xdtw"])
                P.op("act", lambda e: e.activation(out=t["B_tok"], in_=pbb[bb_][:, 0:256], func=AF.Copy), r=[("ps", bb_)], w=["B_tok"])
                if main:
                    P.op("dve", lambda e: e.tensor_tensor(out=t["xdt"], in0=psx3, in1=dtc.unsqueeze(2).broadcast_to([128, 16, 64]), op=ALU.mult),
                         r=[("ps", bx_), kdtc], w=["xdt"])
                    P.op("act", lambda e: e.activation(out=t["xs_tok"], in_=psx3, func=AF.Copy), r=[("ps", bx_)], w=["xs_tok"])
                    LT3 = t["LT"].rearrange("p (a b) -> p a b", a=16)
                    for g in range(2):
                        P.op("dve", lambda e, g=g: e.tensor_tensor(out=t["MT"][:, g * 8:(g + 1) * 8, :], in0=LT3[:, g * 8:(g + 1) * 8, :],
                                                                   in1=t["cbm"][:, g:g + 1, :].broadcast_to([128, 8, 128]), op=ALU.mult),
                             r=[("LT", 2 * g), ("LT", 2 * g + 1), "cbm"], w=[("MT", g)])

            def part2():
                if main:
                    for g in range(2):
                        def yd(e, g=g):
                            ins = None
                            for j in range(8):
                                e16 = g * 8 + j
                                ins = e.matmul(pbf[3 + g][:, j * 64:(j + 1) * 64], lhsT=t["MT"][:, e16, :], rhs=t["xdt"][:, e16, :], start=True, stop=True)
                            return ins
                        P.op("pe", yd, r=[("MT", g), "xdt"], w=[("ps", 3 + g)])
                        P.op("pe", lambda e, g=g: e.matmul(pbf[5 + g], lhsT=xT[:, 10 + g, cs], rhs=hTb[:, g * 512:(g + 1) * 512], start=True, stop=True),
                             r=[kx, "hTb"], w=[("ps", 5 + g)])
                    y1 = t["y1"]
                    for g in range(2):
                        y1g = y1[:, g * 8:(g + 1) * 8, :]
                        P.op("dve", lambda e, g=g, y1g=y1g: e.tensor_tensor(out=y1g, in0=pbf[5 + g].rearrange("p (a b) -> p a b", a=8),
                                                                            in1=ea[:, g * 8:(g + 1) * 8].unsqueeze(2).broadcast_to([128, 8, 64]), op=ALU.mult),
                             r=[("ps", 5 + g), "ex"], w=[("y1", g)])
                        P.op("dve", lambda e, g=g, y1g=y1g: e.tensor_tensor(out=y1g, in0=pbf[3 + g].rearrange("p (a b) -> p a b", a=8), in1=y1g, op=ALU.add),
                             r=[("ps", 3 + g), ("y1", g)], w=[("y1", g)])
                    P.op("pool", lambda e: e.tensor_tensor(out=t["t2"], in0=t["xs_tok"], in1=D_bc.unsqueeze(2).broadcast_to([128, 16, 64]), op=ALU.mult),
                         r=["xs_tok", KC], w=["t2"])
                    P.op("dve", lambda e: e.tensor_tensor(out=y1, in0=y1, in1=t["t2"], op=ALU.add), r=[("y1", 0), ("y1", 1), "t2"], w=[("y1", 0), ("y1", 1)])
                    y1f = y1.rearrange("p a b -> p (a b)")
                    P.op("dve", lambda e: e.tensor_tensor(out=y1f, in0=y1f, in1=ztok_c, op=ALU.mult), r=[("y1", 0), ("y1", 1), "z_tok"], w=[("y1", 0), ("y1", 1)])
                    gn = t["gn"]
                    t2f = t["t2"].rearrange("p a b -> p (a b)")
                    for g in range(2):
                        P.op("act", lambda e, g=g: e.activation(out=t2f[:, g * 512:(g + 1) * 512], in_=y1f[:, g * 512:(g + 1) * 512], func=AF.Square,
                                                                accum_out=gn[:, g:g + 1]), r=[("y1", g)], w=["t2", ("gn", g)])
                    P.op("act", lambda e: e.activation(out=gn[:, 2:4], in_=gn[:, 0:2], func=AF.Ln, scale=1.0 / 512, bias=EPS), r=[("gn", 0), ("gn", 1)], w=["gn2"])
                    P.op("act", lambda e: e.activation(out=gn[:, 4:6], in_=gn[:, 2:4], func=AF.Exp, scale=-0.5), r=["gn2"], w=["gn4"])
                    for g in range(2):
                        P.op("dve", lambda e, g=g: e.scalar_tensor_tensor(out=t["yb"][:, g * 512:(g + 1) * 512], in0=y1f[:, g * 512:(g + 1) * 512],
                                                                          scalar=gn[:, 4 + g:5 + g], op0=ALU.mult, in1=ssdg_bc[:, g * 512:(g + 1) * 512], op1=ALU.mult),
                             r=[("y1", g), "gn4", "ssdg"], w=[("yb", g)])
                    psy = pbb[2][:, 0:1024].rearrange("p (a b) -> p a b", a=8)

                    def try_(e):
                        ins = None
                        for q in range(8):
                            ins = e.transpose(out=psy[:, q, :], in_=t["yb"][:, q * 128:(q + 1) * 128], identity=identB)
                        return ins
                    P.op("pe", try_, r=[("yb", 0), ("yb", 1), "identB"], w=[("ps", 2)])
                    P.op("act", lambda e: e.activation(out=YT[:, 8:16, cs], in_=psy, func=AF.Copy), r=[("ps", 2)], w=[("YTb", c)])

            def part_state():
                sb = (banks[3], banks[4])
                for g in range(2):
                    P.op("pe", lambda e, g=g: e.matmul(pbf[sb[g]], lhsT=t["B_tok"][:, g * 128:(g + 1) * 128], rhs=t["xdtw"].rearrange("p a b -> p (a b)")[:, g * 512:(g + 1) * 512],
                                                       start=True, stop=True), r=["B_tok", "xdtw"], w=[("ps", sb[g])])
                hT3 = hT.rearrange("p (a b) -> p a b", a=16)
                P.op("dve", lambda e: e.tensor_tensor(out=hT3, in0=hT3, in1=dec.unsqueeze(2).broadcast_to([128, 16, 64]), op=ALU.mult), r=["hT", "ex"], w=["hT"])
                for g in range(2):
                    P.op("dve", lambda e, g=g: e.tensor_tensor(out=hT[:, g * 512:(g + 1) * 512], in0=pbf[sb[g]], in1=hT[:, g * 512:(g + 1) * 512], op=ALU.add),
                         r=[("ps", sb[g]), "hT"], w=["hT"])
                P.op("act", lambda e: e.activation(out=hTb, in_=hT, func=AF.Copy), r=["hT"], w=["hTb"])
            return P.capture(head), P.capture(part1), P.capture(part2) + P.capture(part_state)

        def ssd_emit(chunks):
            n = len(chunks)
            P.ops.extend(chunks[0][0])
            for c in range(n):
                P.ops.extend(chunks[c][1])
                if c + 1 < n:
                    P.ops.extend(chunks[c + 1][0])
                P.ops.extend(chunks[c][2])

        dbg_ops = []

        def dbg_dump(name, ap, keys):
            if name in dbg_d:
                dbg_ops.append((name, ap, keys))

        def finish():
            P.barrier()
            outs = []
            for name, ap, keys in dbg_ops:
                k = ("dbgout", name)
                P.op("sp", lambda e, name=name, ap=ap: e.dma_start(out=dbg_d[name], in_=ap), r=keys, w=[k], dma=True)
                outs.append(k)
            P.op("sp", None, r=outs + OUTKEYS)
            P.resolve()
            P.emit(sems, block)

        OUTKEYS = []

        sc.reset()
        NX = 4
        X = [sc.f32([DM]) for _ in range(NX)]
        XN = [sc.bf16([DM]) for _ in range(2)]
        ST = [sc.f32([4]) for _ in range(2)]
        gbc = sc.f32([DM])
        m_common = sc.mark()
        nPT = nT[:, :, 0:TM]
        P.op("sp", lambda e, gbc=gbc: e.dma_start(out=gbc, in_=g_mix.partition_broadcast(128)), w=["gbc"], dma=True)
        stg = []
        for tt in range(8):
            i = tt % 2
            ix = tt % NX
            pre = (lambda tt=tt, ix=ix: P.op("sp", lambda e: e.dma_start(out=X[ix], in_=xprev[tt * 128:(tt + 1) * 128, :]), w=[("X", ix)], dma=True))
            stg.append(rms_transpose(X[ix], ("X", ix), 128, XN[i], ("XN", i), ST[i], ("ST", i), gbc, "gbc", nPT, tt * 128, ("nPT", tt), DM, tt, pre=pre))
        rms_pipeline(stg)
        Wdt = sc.bf16([16, 16])
        r2.reset()
        xTp = r2.bf16([10, TM])
        dtraw_p = r2.f32([8, 16])
        tP = ssd_temps(r2, main=False)
        rawpad = [sc.f32([TM + 4]) for _ in range(2)]
        accb = [sc.f32([TM]) for _ in range(2)]
        blocks = [(COL_X, 512), (COL_X + 512, 512), (COL_B, 512)]
        load_w(Wdt, wview(w_in, COL_DT, 16), "Wdt")
        wcur = wload(wview(w_in, blocks[0][0], blocks[0][1]))
        for i in range(2):
            P.op("dve", lambda e, rp=rawpad[i]: e.memset(rp[:, 0:3], 0.0), w=[("rawpad", i)])
        jt = 0
        defer_p = []
        for bi, (c0, ncol) in enumerate(blocks):
            Wv, kW = wcur
            if bi + 1 < len(blocks):
                wcur = wload(wview(w_in, blocks[bi + 1][0], blocks[bi + 1][1]))
            for jj in range(4):
                j = (c0 - COL_X) // 128 + jj
                rp = rawpad[jt % 2]
                krp = ("rawpad", jt % 2)
                isC = j >= 10
                pend_p = list(defer_p)
                del defer_p[:]
                for nb in range(2):
                    if isC and nb == 0:
                        continue
                    bank = 3 + (jt * 2 + nb) % 4
                    mm_group(pbf[bank], [(Wv[:, k, jj * 128:(jj + 1) * 128], nPT[:, k, nb * 512:(nb + 1) * 512]) for k in range(16)],
                             r=[kW] + [("nPT", tt) for tt in range(nb * 4, nb * 4 + 4)], w=[("ps", bank)])
                    P.op("act", lambda e, rp=rp, nb=nb, bank=bank: e.activation(out=rp[:, 3 + nb * 512:3 + (nb + 1) * 512], in_=pbf[bank], func=AF.Copy),
                         r=[("ps", bank)], w=[krp])
                    if not isC:
                        P.op("act", lambda e, ac=accb[jt % 2], nb=nb, bank=bank, j=j: e.activation(out=ac[:, nb * 512:(nb + 1) * 512], in_=pbf[bank], func=AF.Identity,
                                                                                                scale=cwb[:, j, 3:4], bias=cwb[:, j, 4:5]),
                             r=[("ps", bank), KC], w=[("acc", jt % 2)])
                for so in pend_p:
                    P.ops.extend(so)
                P.op("dve", lambda e, rp=rp, j=j: e.tensor_copy(out=prefix[:, j, :], in_=rp[:, TM:TM + 3]), r=[krp], w=["prefix"])
                if not isC:
                    conv_silu(rp, krp, accb[jt % 2], ("acc", jt % 2), j, TM, xTp[:, j, :], ("xTp",), defer=defer_p)
                jt += 1
        for so in defer_p:
            P.ops.extend(so)
        for tt in range(8):
            mm_group(pbf[0][:, 0:16], [(nPT[:, k, tt * 128:(tt + 1) * 128], Wdt[:, k, :]) for k in range(16)], r=["Wdt", ("nPT", tt)], w=[("ps", 0)])
            P.op("act", lambda e, tt=tt: e.activation(out=dtraw_p[:, tt, :], in_=pbf[0][:, 0:16], func=AF.Copy), r=[("ps", 0)], w=["dtraw_p"])
        P.barrier()

        def pchunks():
            ssd_emit([ssd_chunk(tP, xTp, c, dtraw_p[:, c, :], "dtraw_p", main=False, banks=(5, 6, 5, 7, 6), ktag="p") for c in range(8)])
            P.op("dve", lambda e: e.tensor_scalar(out=hT, in0=hT, scalar1=flag, scalar2=None, op0=ALU.mult), r=["hT", KC], w=["hT"])
            P.op("act", lambda e: e.activation(out=hTb, in_=hT, func=AF.Copy), r=["hT"], w=["hTb"])
        pch_ops = P.capture(pchunks)
        _saved_ops = P.ops
        P.ops = []
        sc.reset(m_common)
        stgA = []
        for tt in range(9):
            i = tt % 2
            rows = 128 if tt < 8 else NS
            src = xm[tt * 128:(tt + 1) * 128, :] if tt < 8 else xsmp[:, :]
            ix = tt % NX
            pre = (lambda ix=ix, rows=rows, src=src: P.op("sp", lambda e: e.dma_start(out=X[ix][:rows, :], in_=src), w=[("X", ix)], dma=True))
            stgA.append(rms_transpose(X[ix], ("X", ix), rows, XN[i], ("XN", i), ST[i], ("ST", i), gbc, "gbc", nT, tt * 128, ("nT", tt), DM, tt, pre=pre))
        rms_pipeline(stgA)
        P.barrier()
        sc.reset()
        xT = sc.bf16([12, T])
        z_tok = sc.bf16([8, 1024])
        dtraw = sc.f32([8, 16])
        ssdg_bc = sc.f32([1024])
        raw_s = sc.f32([12, NS])
        m_A = sc.mark()
        Wdt = sc.bf16([16, 16])
        rawpad = [sc.f32([TM + 4]) for _ in range(2)]
        accb = [sc.f32([TM]) for _ in range(2)]
        scTok = sc.f32([3 * 1536])
        scT = sc.f32([36, NS])
        acc_s = sc.f32([12, NS])
        P.op("sp", lambda e: e.dma_start(out=ssdg_bc, in_=ssd_g.partition_broadcast(128)), w=["ssdg"], dma=True)
        P.op("sp", lambda e: e.dma_start(out=Rsel, in_=cst2_d[:, C2_RSEL:C2_RSEL + 1024]), w=["cst2"], dma=True)
        P.op("sp", lambda e: e.dma_start(out=scTok[:NS, :], in_=sconv.rearrange("b r c -> b (r c)")), w=["scTok"], dma=True)
        P.op("sp", lambda e: e.dma_start(out=nconv_s[:, 0:2, :], in_=sconv[:, 1:3, :]), w=["o_ncs01"], dma=True)
        OUTKEYS.append("o_ncs01")
        for half in range(2):
            def trs(e, half=half):
                ins = None
                for idx in range(half * 18, half * 18 + 18):
                    r_, j_ = idx // 12, idx % 12
                    ii = idx - half * 18
                    ins = e.transpose(out=pbf[half][:, ii * NS:(ii + 1) * NS], in_=scTok[:NS, r_ * 1536 + j_ * 128:r_ * 1536 + (j_ + 1) * 128],
                                      identity=identF[:NS, :NS])
                return ins
            P.op("pe", trs, r=["scTok", KC], w=[("ps", half)])
            P.op("dve", lambda e, half=half: e.tensor_copy(out=scT[:, half * 18:half * 18 + 18, :].rearrange("p a b -> p (a b)"), in_=pbf[half][:, 0:18 * NS]),
                 r=[("ps", half)], w=["scT"])
        blocksA = [(COL_Z, "z"), (COL_Z + 512, "z"), (COL_X, "x"), (COL_X + 512, "x"), (COL_B, "x")]
        load_w(Wdt, wview(w_in, COL_DT, 16), "Wdt")
        wcur = wload(wview(w_in, blocksA[0][0], 512))
        jt = 0
        grp = 0
        defer_a = []
        for bi, (c0, kind) in enumerate(blocksA):
            Wv, kW = wcur
            if bi + 1 < len(blocksA):
                wcur = wload(wview(w_in, blocksA[bi + 1][0], 512))
            if kind == "z":
                zb = (c0 - COL_Z) // 512
                for tt in range(8):
                    bank = 1 + grp % 4
                    grp += 1
                    mm_group(pbf[bank], [(nT[:, k, tt * 128:(tt + 1) * 128], Wv[:, k, :]) for k in range(16)], r=[kW, ("nT", tt)], w=[("ps", bank)])
                    P.op("act", lambda e, tt=tt, zb=zb, bank=bank: e.activation(out=z_tok[:, tt, zb * 512:(zb + 1) * 512], in_=pbf[bank], func=AF.Silu),
                         r=[("ps", bank)], w=["z_tok"])
                for jj in range(4):
                    bank = 1 + grp % 4
                    grp += 1
                    j = zb * 4 + jj
                    mm_group(pbf[bank][:, 0:NS], [(Wv[:, k, jj * 128:(jj + 1) * 128], nT[:, k, TM:T]) for k in range(16)], r=[kW, ("nT", 8)], w=[("ps", bank)])
                    P.op("act", lambda e, j=j, bank=bank: e.activation(out=zT_s[:, j, :], in_=pbf[bank][:, 0:NS], func=AF.Silu), r=[("ps", bank)], w=["zT_s"])
            else:
                for jj in range(4):
                    j = (c0 - COL_X) // 128 + jj
                    rp = rawpad[jt % 2]
                    krp = ("rawpad", jt % 2)
                    pend_a = list(defer_a)
                    del defer_a[:]
                    P.op("dve", lambda e, rp=rp, j=j: e.tensor_copy(out=rp[:, 0:3], in_=prefix[:, j, :]), r=["prefix"], w=[krp])
                    for nb in range(3):
                        bank = 1 + grp % 4
                        grp += 1
                        lo, hi = (nb * 512, (nb + 1) * 512) if nb < 2 else (TM, T)
                        n = hi - lo
                        rk = [("nT", tt) for tt in range(nb * 4, nb * 4 + 4)] if nb < 2 else [("nT", 8)]
                        mm_group(pbf[bank][:, 0:n], [(Wv[:, k, jj * 128:(jj + 1) * 128], nT[:, k, lo:hi]) for k in range(16)], r=[kW] + rk, w=[("ps", bank)])
                        if nb < 2:
                            P.op("act", lambda e, rp=rp, lo=lo, hi=hi, bank=bank: e.activation(out=rp[:, 3 + lo:3 + hi], in_=pbf[bank], func=AF.Copy),
                                 r=[("ps", bank)], w=[krp])
                            P.op("act", lambda e, ac=accb[jt % 2], lo=lo, hi=hi, bank=bank, j=j: e.activation(out=ac[:, lo:hi], in_=pbf[bank], func=AF.Identity,
                                                                                                    scale=cwb[:, j, 3:4], bias=cwb[:, j, 4:5]),
                                 r=[("ps", bank), KC], w=[("acc", jt % 2)])
                        else:
                            P.op("act", lambda e, j=j, bank=bank: e.activation(out=raw_s[:, j, :], in_=pbf[bank][:, 0:NS], func=AF.Copy),
                                 r=[("ps", bank)], w=["raw_s"])
                    for so in pend_a:
                        P.ops.extend(so)
                    P.op("dve", lambda e, rp=rp, j=j: e.tensor_copy(out=ncv[:, j, :], in_=rp[:, TM:TM + 3]), r=[krp], w=["ncv"])
                    conv_silu(rp, krp, accb[jt % 2], ("acc", jt % 2), j, TM, xT[:, j, 0:TM], ("xT",), defer=defer_a)
                    P.op("dve", lambda e, j=j: e.tensor_scalar(out=acc_s[:, j, :], in0=raw_s[:, j, :], scalar1=cwb[:, j, 3:4], scalar2=cwb[:, j, 4:5],
                                                               op0=ALU.mult, op1=ALU.add), r=["raw_s", KC], w=["acc_s"])
                    for r_ in range(3):
                        P.op("dve", lambda e, j=j, r_=r_: e.scalar_tensor_tensor(out=acc_s[:, j, :], in0=scT[:, r_ * 12 + j, :], scalar=cwb[:, j, r_:r_ + 1],
                                                                                 op0=ALU.mult, in1=acc_s[:, j, :], op1=ALU.add), r=["scT", "acc_s", KC], w=["acc_s"])
                    jt += 1
        for so in defer_a:
            P.ops.extend(so)
        P.op("act", lambda e: e.activation(out=xT_s, in_=acc_s, func=AF.Silu), r=["acc_s"], w=["xT_s"])
        for tt in range(8):
            mm_group(pbf[0][:, 0:16], [(nT[:, k, tt * 128:(tt + 1) * 128], Wdt[:, k, :]) for k in range(16)], r=["Wdt", ("nT", tt)], w=[("ps", 0)])
            P.op("act", lambda e, tt=tt: e.activation(out=dtraw[:, tt, :], in_=pbf[0][:, 0:16], func=AF.Copy), r=[("ps", 0)], w=["dtraw"])
        mm_group(pbf[1][:NS, 0:NS], [(Wdt[:, k, :], nT[:, k, TM:T]) for k in range(16)], r=["Wdt", ("nT", 8)], w=[("ps", 1)])
        P.op("act", lambda e: e.activation(out=dtT_s[:NS, :], in_=pbf[1][:NS, 0:NS], func=AF.Copy), r=[("ps", 1)], w=["dtT_s"])

        def rows_out(src3, ksrc, R, stage, kst, dram_ap, okey):
            for b3 in range(3):
                def trr(e, b3=b3):
                    ins = None
                    for jj in range(4):
                        j_ = b3 * 4 + jj
                        ins = e.transpose(out=pbf[2 + b3][:R, jj * 128:(jj + 1) * 128], in_=src3[:, j_, :], identity=identF)
                    return ins
                P.op("pe", trr, r=[ksrc, KC], w=[("ps", 2 + b3)])
                P.op("dve", lambda e, b3=b3: e.tensor_copy(out=stage[:R, b3 * 512:(b3 + 1) * 512], in_=pbf[2 + b3][:R, :]), r=[("ps", 2 + b3)], w=[kst, "scTok"])
            P.op("sp", lambda e: e.dma_start(out=dram_ap, in_=stage[:R, 0:1536]), r=[kst], w=[okey], dma=True)
            OUTKEYS.append(okey)
        rows_out(raw_s, "raw_s", NS, scTok[:, 0:1536], "stg0", nconv_s[:, 2, :], "o_ncs2")
        rows_out(ncv, "ncv", 3, scTok[:, 1536:3072], "stg1", nconv_p[:, :], "o_ncp")
        _main_ops = P.ops
        P.ops = _saved_ops
        P.ops.extend(Prog.merge(_main_ops, pch_ops))
        P.barrier()
        if STOP_AFTER == "A1":
            dbg_dump("xT", xT.rearrange("p a b -> p (a b)"), [("xT",)])
            dbg_dump("z_tok", z_tok.rearrange("p a b -> p (a b)"), ["z_tok"])
            dbg_dump("zT_s", zT_s.rearrange("p a b -> p (a b)"), ["zT_s"])
            dbg_dump("dtraw", dtraw.rearrange("p a b -> p (a b)"), ["dtraw"])
            dbg_dump("dtT_s", dtT_s, ["dtT_s"])
            finish()
            return nc

        sc.reset(m_A)
        tA = ssd_temps(sc)
        ssd_emit([ssd_chunk(tA, xT, c, dtraw[:, c, :], "dtraw", main=True, ztok_c=z_tok[:, c, :], ssdg_bc=ssdg_bc) for c in range(8)])
        if STOP_AFTER == "A2a":
            P.barrier()
            dbg_dump("YT", YT.rearrange("p a b -> p (a b)"), [("YTb", c) for c in range(8)])
            finish()
            return nc
        hout = sc.f32([8, 128])
        for half in range(2):
            def trh(e, half=half):
                ins = None
                for jj in range(4):
                    q = half * 4 + jj
                    ins = e.transpose(out=pbf[3 + half][:, jj * 128:(jj + 1) * 128], in_=hT[:, q * 128:(q + 1) * 128], identity=identF)
                return ins
            P.op("pe", trh, r=["hT", KC], w=[("ps", 3 + half)])
            P.op("dve", lambda e, half=half: e.tensor_copy(out=hout[:, half * 4:(half + 1) * 4, :].rearrange("p a b -> p (a b)"), in_=pbf[3 + half]),
                 r=[("ps", 3 + half)], w=["hout"])
        P.op("sp", lambda e: e.dma_start(out=nssm_p.rearrange("(q r) n -> r q n", r=128), in_=hout), r=["hout"], w=["o_nsp"], dma=True)
        OUTKEYS.append("o_nsp")

        if STOP_AFTER == "A2b":
            P.barrier()
            dbg_dump("YT", YT.rearrange("p a b -> p (a b)"), [("YTb", c) for c in range(8)])
            finish()
            return nc
        P.barrier()
        SMPW = 7300
        scS = Bump(arena, AW - SMPW, AW)
        _saved_main = P.ops
        P.ops = []
        _sc_main = sc
        sc = scS
        sm_s = sc.f32([8, NS])
        cat = sc.f32([32])
        rep = sc.f32([8, 32])
        dtx = sc.f32([8, NS])
        ys = sc.f32([8, NS])
        ysq = sc.f32([8, NS])
        rs = sc.f32([2, NS])
        BC_tok = sc.bf16([512])
        selB = sc.bf16([NS, 128])
        NHS = 3
        hs = [sc.f32([8, 128]) for _ in range(NHS)]
        tmpS = [sc.f32([1024]) for _ in range(2)]
        x0 = sm_s[:NS, 0, :]; e1 = sm_s[:NS, 1, :]; anp = sm_s[:NS, 2, 0:1]
        P.op("act", lambda e: e.activation(out=e1, in_=dtT_s[:NS, :], func=AF.Exp, bias=dtb_pp[:NS, :]), r=["dtT_s", KC], w=["s_e1"])
        P.op("act", lambda e: e.activation(out=cat[:NS, 16:32], in_=e1, func=AF.Ln, bias=1.0), r=["s_e1"], w=["s_dt"])
        P.op("act", lambda e: e.activation(out=anp, in_=al_pp[:NS, :], func=AF.Exp), r=[KC], w=["s_anp0"])
        P.op("dve", lambda e: e.tensor_scalar(out=anp, in0=anp, scalar1=-1.0, scalar2=None, op0=ALU.mult), r=["s_anp0"], w=["s_anp"])
        P.op("act", lambda e: e.activation(out=cat[:NS, 0:16], in_=cat[:NS, 16:32], func=AF.Exp, scale=anp), r=["s_dt", "s_anp"], w=["s_dA"])

        def repf(e):
            ins = None
            for q in range(8):
                ins = e.matmul(pbf[7][:, q * 32:(q + 1) * 32], lhsT=Rsel[:NS, q * 128:(q + 1) * 128], rhs=cat[:NS, :], start=True, stop=True)
            return ins
        P.op("pe", repf, r=["s_dA", "s_dt", "cst2"], w=[("ps", 7)])
        P.op("dve", lambda e: e.tensor_copy(out=rep.rearrange("p a b -> p (a b)"), in_=pbf[7][:, 0:256]), r=[("ps", 7)], w=["rep"])
        xs_s = xT_s[:, 0:8, :]
        P.op("dve", lambda e: e.tensor_tensor(out=dtx, in0=rep[:, :, 16:32], in1=xs_s, op=ALU.mult), r=["rep", "xT_s"], w=["dtx"])
        psbc = pbb[7][:NS, 0:512].rearrange("p (a b) -> p a b", a=4)

        def trbc(e):
            ins = None
            for jj in range(4):
                ins = e.transpose(out=psbc[:, jj, :], in_=xT_s[:, 8 + jj, :], identity=identB)
            return ins
        P.op("pe", trbc, r=["xT_s", "identB"], w=[("ps", 7)])
        P.op("act", lambda e: e.activation(out=BC_tok[:NS, :], in_=pbb[7][:NS, 0:512], func=AF.Copy), r=[("ps", 7)], w=["BC_tok"])
        P.op("dve", lambda e: e.tensor_copy(out=selB[:NS, :, :], in_=identB[:NS, 0:NS].unsqueeze(2).broadcast_to([NS, NS, 128])), r=["identB"], w=["selB"])
        P.op("dve", lambda e: e.memset(ys, 0.0), w=["ys"])

        def hs_load(b):
            P.op("sp", lambda e, b=b: e.dma_start(out=hs[b % NHS], in_=sssm[b].rearrange("(q r) n -> r q n", r=128)), w=[("hs", b % NHS, q) for q in range(8)], dma=True)
        for b in range(min(NHS - 1, NS)):
            hs_load(b)
        for b in range(NS):
            hb = hs[b % NHS]
            bank = 5 + b % 2
            if b + NHS - 1 < NS:
                hs_load(b + NHS - 1)
            P.op("pe", lambda e, b=b, bank=bank: e.matmul(pbf[bank], lhsT=selB[:NS, b, :], rhs=BC_tok[:NS, :], start=True, stop=True),
                 r=["selB", "BC_tok"], w=[("ps", bank)])
            for q in range(8):
                P.op("act", lambda e, b=b, q=q, hb=hb: e.activation(out=hb[:, q, :], in_=hb[:, q, :], func=AF.Copy, scale=rep[:, q, b:b + 1]),
                     r=[("hs", b % NHS, q), "rep"], w=[("hs", b % NHS, q)])

            bc4 = pbf[bank][:, 0:256].rearrange("p (g n) -> p g n", g=2).unsqueeze(2).broadcast_to([128, 2, 4, 128])
            cc4 = pbf[bank][:, 256:512].rearrange("p (g n) -> p g n", g=2).unsqueeze(2).broadcast_to([128, 2, 4, 128])
            hb4 = hb.rearrange("p (g j) n -> p g j n", g=2)
            tm4 = tmpS[b % 2].rearrange("p (g j n) -> p g j n", g=2, j=4)
            dx4 = dtx[:, :, b:b + 1].rearrange("p (g j) o -> p g j o", g=2).broadcast_to([128, 2, 4, 128])
            khs = [("hs", b % NHS, q) for q in range(8)]
            ktm = ("tmpS", b % 2)
            P.op("dve", lambda e, tm4=tm4, bc4=bc4, dx4=dx4: e.tensor_tensor(out=tm4, in0=bc4, in1=dx4, op=ALU.mult), r=[("ps", bank), "dtx"], w=[ktm])
            P.op("dve", lambda e, hb4=hb4, tm4=tm4: e.tensor_tensor(out=hb4, in0=hb4, in1=tm4, op=ALU.add), r=khs + [ktm], w=khs)
            P.op("dve", lambda e, hb4=hb4, tm4=tm4, cc4=cc4: e.tensor_tensor(out=tm4, in0=cc4, in1=hb4, op=ALU.mult), r=khs + [("ps", bank), ktm], w=[ktm])
            P.op("dve", lambda e, b=b, tm=tmpS[b % 2]: e.tensor_reduce(out=ys[:, :, b:b + 1], in_=tm.rearrange("p (q n) -> p q n", q=8), axis=AX.X, op=ALU.add),
                 r=[ktm], w=["ys"])
            ok = ("o_nss", b)
            P.op("sp", lambda e, b=b, hb=hb: e.dma_start(out=nssm_s[b].rearrange("(q r) n -> r q n", r=128), in_=hb), r=[("hs", b % NHS, q) for q in range(8)], w=[ok], dma=True)
            OUTKEYS.append(ok)
        P.op("dve", lambda e: e.tensor_tensor(out=ysq, in0=xs_s, in1=D_pp.unsqueeze(2).broadcast_to([128, 8, NS]), op=ALU.mult), r=["xT_s", KC], w=["ysq"])
        P.op("dve", lambda e: e.tensor_tensor(out=ys, in0=ys, in1=ysq, op=ALU.add), r=["ys", "ysq"], w=["ys"])
        P.op("dve", lambda e: e.tensor_tensor(out=ys, in0=ys, in1=zT_s, op=ALU.mult), r=["ys", "zT_s"], w=["ys"])
        P.op("dve", lambda e: e.tensor_tensor(out=ysq, in0=ys, in1=ys, op=ALU.mult), r=["ys", "ysq"], w=["ysq"])

        def gsum(e):
            ins = None
            for q in range(8):
                g = q // 4
                ins = e.matmul(pbf[7][:, g * NS:(g + 1) * NS], lhsT=onesF, rhs=ysq[:, q, :], start=(q % 4 == 0), stop=(q % 4 == 3))
            return ins
        P.op("pe", gsum, r=["ysq", KC], w=[("ps", 7)])
        rsf = rs.rearrange("p a b -> p (a b)")
        P.op("act", lambda e: e.activation(out=rsf, in_=pbf[7][:, 0:2 * NS], func=AF.Ln, scale=1.0 / 512, bias=EPS), r=[("ps", 7)], w=["rs0"])
        P.op("act", lambda e: e.activation(out=rsf, in_=rsf, func=AF.Exp, scale=-0.5), r=["rs0"], w=["rs"])
        for g in range(2):
            P.op("dve", lambda e, g=g: e.tensor_tensor(out=ys[:, g * 4:(g + 1) * 4, :], in0=ys[:, g * 4:(g + 1) * 4, :],
                                                       in1=rs[:, g:g + 1, :].broadcast_to([128, 4, NS]), op=ALU.mult), r=["ys", "rs"], w=["ys"])
        P.op("dve", lambda e: e.tensor_tensor(out=YT[:, 8:16, TM:T], in0=ys, in1=sg_pp.unsqueeze(2).broadcast_to([128, 8, NS]), op=ALU.mult),
             r=["ys", KC], w=[("YTb", 8)])
        smp_ops = P.ops
        P.ops = _saved_main
        sc = _sc_main

        sc.reset()
        sc.end = AW - SMPW
        _saved_a3 = P.ops
        P.ops = []
        vng_bc = sc.f32([1024])
        bs_bc = sc.f32([8, 128])
        v_tok = sc.bf16([9, 1024])
        WsT = sc.bf16([8, 128])
        sqacc = sc.f32([T])
        gv = [sc.f32([1024]) for _ in range(2)]
        wsraw = gv[0].rearrange("p (a b) -> p a b", a=8)
        vst = [sc.f32([4]) for _ in range(2)]
        ug = [sc.f32([T]) for _ in range(2)]
        _m_tm = sc.mark()
        tmpm = [sc.f32([512]) for _ in range(2)]
        _m_tm2 = sc.mark()
        sc.reset(_m_tm)
        vsf = sc.f32([1024])
        sc.reset(_m_tm2)
        sq = [sc.bf16([T]) for _ in range(2)]
        rstd_bc = sqacc
        w00I = sc.bf16([8, NS])
        P.op("sp", lambda e: e.dma_start(out=vng_bc, in_=vn_g.partition_broadcast(128)), w=["vng"], dma=True)
        P.op("sp", lambda e: e.dma_start(out=bs_bc.rearrange("p a b -> p (a b)"), in_=cst2_d[:, C2_BS:C2_BS + 1024]), w=["bs_bc"], dma=True)
        P.op("sp", lambda e: e.dma_start(out=wsraw, in_=ws_d.rearrange("h t s -> t h s")), w=["wsraw"], dma=True)
        wv2 = [wload(wview(w_in, COL_V, 512)), wload(wview(w_in, COL_V + 512, 512))]
        for half in range(2):
            def trw(e, half=half):
                ins = None
                for jj in range(4):
                    h_ = half * 4 + jj
                    ins = e.transpose(out=pbf[half][:, jj * 128:(jj + 1) * 128], in_=wsraw[:, h_, :], identity=identF)
                return ins
            P.op("pe", trw, r=["wsraw", KC], w=[("ps", half)])
            P.op("dve", lambda e, half=half: e.tensor_tensor(out=WsT[:, half * 4:(half + 1) * 4, :], in0=pbf[half].rearrange("p (a b) -> p a b", a=4),
                                                             in1=triF.unsqueeze(1).broadcast_to([128, 4, 128]), op=ALU.mult), r=[("ps", half), KC], w=["WsT"])
        for h_ in range(8):
            P.op("dve", lambda e, h_=h_: e.tensor_scalar(out=w00I[:NS, h_, :], in0=identF[:NS, 0:NS], scalar1=w00_bc[:NS, h_:h_ + 1], scalar2=None, op0=ALU.mult),
                 r=[KC], w=["w00I"])
        for tt in range(9):
            rows = 128 if tt < 8 else NS
            i = tt % 2
            tcols = slice(tt * 128, tt * 128 + rows)
            for zb in range(2):
                bank = 1 + (tt * 2 + zb) % 4
                mm_group(pbf[bank][:rows, :], [(nT[:, k, tcols], wv2[zb][0][:, k, :]) for k in range(16)], r=[wv2[zb][1], ("nT", tt)], w=[("ps", bank)])
                P.op("act", lambda e, i=i, zb=zb, bank=bank, rows=rows: e.activation(out=gv[i][:rows, zb * 512:(zb + 1) * 512], in_=pbf[bank][:rows, :],
                                                                                    func=AF.Gelu_apprx_tanh), r=[("ps", bank)], w=[("gv", i, zb)])
            P.op("act", lambda e, i=i, rows=rows: e.activation(out=ug[i][:rows, 0:1024], in_=gv[i][:rows, :], func=AF.Square, accum_out=vst[i][:rows, 0:1]),
                 r=[("gv", i, 0), ("gv", i, 1)], w=[("ug", i), ("vst", i)])
            P.op("act", lambda e, i=i, rows=rows: e.activation(out=vst[i][:rows, 1:2], in_=vst[i][:rows, 0:1], func=AF.Ln, scale=1.0 / 1024, bias=EPS),
                 r=[("vst", i)], w=[("vst", i)])
            P.op("act", lambda e, i=i, rows=rows: e.activation(out=vst[i][:rows, 2:3], in_=vst[i][:rows, 1:2], func=AF.Exp, scale=-0.5),
                 r=[("vst", i)], w=[("vst", i)])
            P.op("dve", lambda e, i=i, rows=rows, tt=tt: e.scalar_tensor_tensor(out=v_tok[:rows, tt, :], in0=gv[i][:rows, :], scalar=vst[i][:rows, 2:3], op0=ALU.mult,
                                                                                in1=vng_bc[:rows, :], op1=ALU.mult),
                 r=[("gv", i, 0), ("gv", i, 1), ("vst", i), "vng"], w=[("v_tok", tt)])
            if tt == 8:
                P.op("dve", lambda e, i=i: e.scalar_tensor_tensor(out=vsf[:NS, :], in0=gv[i][:NS, :], scalar=vst[i][:NS, 2:3], op0=ALU.mult,
                                                                  in1=vng_bc[:NS, :], op1=ALU.mult), r=[("gv", i, 0), ("gv", i, 1), ("vst", i), "vng"], w=["vsf", ("tmpm", 0), ("tmpm", 1)])
                P.op("sp", lambda e: e.dma_start(out=v_s[:, :], in_=vsf[:NS, :]), r=["vsf", ("tmpm", 0), ("tmpm", 1)], w=["o_vs"], dma=True)
                OUTKEYS.append("o_vs")
        wu2 = [wload(wview(w_in, COL_U, 512)), wload(wview(w_in, COL_U + 512, 512))]
        tok_blocks = [(0, 512), (512, 1024), (TM, T)]
        grpc = [0]

        def stage_G(h_):
            ub, jj = h_ // 4, h_ % 4
            Wv, kWu = wu2[ub]
            ui = h_ % 2
            for nb, (lo, hi) in enumerate(tok_blocks):
                bank = grpc[0] % 2
                grpc[0] += 1
                n = hi - lo
                rk = [("nT", tt) for tt in range(nb * 4, nb * 4 + 4)] if nb < 2 else [("nT", 8)]
                mm_group(pbf[bank][:, 0:n], [(Wv[:, k, jj * 128:(jj + 1) * 128], nT[:, k, lo:hi]) for k in range(16)], r=[kWu] + rk, w=[("ps", bank)])
                P.op("act", lambda e, ui=ui, lo=lo, hi=hi, n=n, bank=bank: e.activation(out=ug[ui][:, lo:hi], in_=pbf[bank][:, 0:n], func=AF.Gelu_apprx_tanh),
                     r=[("ps", bank)], w=[("ug", ui)])

        def stage_M(h_):
            ui = h_ % 2
            for half in range(2):
                def mix(e, half=half, h_=h_):
                    ins = None
                    for cc in range(4):
                        c = half * 4 + cc
                        ins = e.matmul(pbf[2 + half][:, cc * 128:(cc + 1) * 128], lhsT=v_tok[:, c, h_ * 128:(h_ + 1) * 128], rhs=WsT[:, h_, :], start=True, stop=True)
                    return ins
                P.op("pe", mix, r=[("v_tok", c) for c in range(half * 4, half * 4 + 4)] + ["WsT"], w=[("ps", 2 + half)])
                tm = tmpm[half]
                P.op("dve", lambda e, half=half, h_=h_, tm=tm: e.tensor_tensor(out=tm.rearrange("p (a b) -> p a b", a=4), in0=pbf[2 + half].rearrange("p (a b) -> p a b", a=4),
                                                                               in1=bs_bc[:, h_:h_ + 1, :].broadcast_to([128, 4, 128]), op=ALU.add),
                     r=[("ps", 2 + half), "bs_bc"], w=[("tmpm", half)])
                P.op("dve", lambda e, half=half, h_=h_, tm=tm, ui=ui: e.tensor_tensor(out=YT[:, h_, half * 512:(half + 1) * 512], in0=tm, in1=ug[ui][:, half * 512:(half + 1) * 512], op=ALU.mult),
                     r=[("tmpm", half), ("ug", ui)], w=[("ya", h_)])
            P.op("pe", lambda e, h_=h_: e.matmul(pbf[4][:, 0:NS], lhsT=v_tok[:NS, 8, h_ * 128:(h_ + 1) * 128], rhs=w00I[:NS, h_, :], start=True, stop=True),
                 r=[("v_tok", 8), "w00I"], w=[("ps", 4)])
            P.op("dve", lambda e, h_=h_, ui=ui: e.scalar_tensor_tensor(out=YT[:, h_, TM:T], in0=pbf[4][:, 0:NS], scalar=b0_bc[:, h_:h_ + 1], op0=ALU.add, in1=ug[ui][:, TM:T], op1=ALU.mult),
                 r=[("ps", 4), ("ug", ui), KC], w=[("ya", h_)])
            si = h_ % 2
            if h_ == 0:
                P.op("act", lambda e, h_=h_: e.activation(out=sqacc, in_=YT[:, h_, :], func=AF.Square), r=[("ya", h_)], w=["sqacc"])
            else:
                P.op("act", lambda e, h_=h_, si=si: e.activation(out=sq[si], in_=YT[:, h_, :], func=AF.Square), r=[("ya", h_)], w=[("sq", si)])
                P.op("dve", lambda e, si=si: e.tensor_tensor(out=sqacc, in0=sqacc, in1=sq[si], op=ALU.add), r=["sqacc", ("sq", si)], w=["sqacc"])

        stage_G(0)
        for h_ in range(8):
            if h_ + 1 < 8:
                stage_G(h_ + 1)
            stage_M(h_)
        P.op("dve", lambda e: e.tensor_copy(out=sq[0], in_=sqacc), r=["sqacc", ("sq", 0)], w=[("sq", 0)])
        sbank = (0, 1, 4)
        for nb, (lo, hi) in enumerate(tok_blocks):
            n = hi - lo
            P.op("pe", lambda e, nb=nb, lo=lo, hi=hi, n=n: e.matmul(pbf[sbank[nb]][:, 0:n], lhsT=onesB, rhs=sq[0][:, lo:hi], start=True, stop=True),
                 r=[("sq", 0), "onesB"], w=[("ps", sbank[nb])])
        for nb, (lo, hi) in enumerate(tok_blocks):
            n = hi - lo
            P.op("act", lambda e, nb=nb, lo=lo, hi=hi, n=n: e.activation(out=rstd_bc[:, lo:hi], in_=pbf[sbank[nb]][:, 0:n], func=AF.Ln, scale=1.0 / 1024, bias=EPS),
                 r=[("ps", sbank[nb])], w=[("rstd", nb)])
            P.op("act", lambda e, lo=lo, hi=hi: e.activation(out=rstd_bc[:, lo:hi], in_=rstd_bc[:, lo:hi], func=AF.Exp, scale=-0.5), r=[("rstd", nb)], w=[("rstd", nb)])
        for h_ in range(8):
            P.op("dve", lambda e, h_=h_: e.scalar_tensor_tensor(out=YT[:, h_, :], in0=YT[:, h_, :], scalar=gon_pp[:, h_:h_ + 1], op0=ALU.mult, in1=rstd_bc, op1=ALU.mult),
                 r=[("ya", h_), ("rstd", 0), ("rstd", 1), ("rstd", 2), KC], w=[("YTa", h_)])
        _a3_ops = P.ops
        P.ops = _saved_a3
        P.ops.extend(Prog.merge(_a3_ops, smp_ops))
        sc.end = AW
        P.barrier()
        if STOP_AFTER == "A3":
            dbg_dump("YT", YT.rearrange("p a b -> p (a b)"), [("YTa", h_) for h_ in range(8)])
            finish()
            return nc

        sc.reset()
        hres = sc.f32([9, DM])
        m_B = sc.mark()
        for tt in range(9):
            rows = 128 if tt < 8 else NS
            src = xm[tt * 128:(tt + 1) * 128, :] if tt < 8 else xsmp[:, :]
            P.op("sp", lambda e, tt=tt, rows=rows, src=src: e.dma_start(out=hres[:rows, tt, :], in_=src), w=[("h", tt, cb) for cb in range(4)], dma=True)
        wcur = wload(wview(w_out, 0, 512))
        for cb in range(4):
            Wv, kW = wcur
            if cb + 1 < 4:
                wcur = wload(wview(w_out, (cb + 1) * 512, 512))
            for tt in range(9):
                rows = 128 if tt < 8 else NS
                tcols = slice(tt * 128, tt * 128 + rows)
                bank = (cb * 9 + tt) % 6
                mm_group(pbf[bank][:rows, :], [(YT[:, k, tcols], Wv[:, k, :]) for k in range(16)], r=[kW], w=[("ps", bank)])
                P.op("dve", lambda e, rows=rows, tt=tt, cb=cb, bank=bank: e.tensor_tensor(out=hres[:rows, tt, cb * 512:(cb + 1) * 512], in0=pbf[bank][:rows, :],
                                                                                         in1=hres[:rows, tt, cb * 512:(cb + 1) * 512], op=ALU.add),
                     r=[("ps", bank), ("h", tt, cb)], w=[("h", tt, cb)])
        P.barrier()
        if STOP_AFTER == "B":
            dbg_dump("h", hres.rearrange("p a b -> p (a b)"), [("h", tt) for tt in range(9)])
            finish()
            return nc

        sc.reset(m_B)
        gbc = sc.f32([DM])
        stmp = [sc.f32([512]) for _ in range(2)]
        ST = [sc.f32([4]) for _ in range(2)]
        r2.reset()
        actb = [r2.bf16([2, T]) for _ in range(2)]
        Wd = [r2.bf16([2, DM]) for _ in range(2)]
        XN = [r2.bf16([DM]) for _ in range(2)]
        mT = nT
        P.op("sp", lambda e: e.dma_start(out=gbc, in_=g_ffn.partition_broadcast(128)), w=["gbc"], dma=True)
        stgC = []
        for tt in range(9):
            rows = 128 if tt < 8 else NS
            i = tt % 2
            stgC.append(rms_transpose(hres[:, tt, :], [("h", tt, cb) for cb in range(4)], rows, XN[i], ("XN", i), ST[i], ("ST", i), gbc, "gbc", mT, tt * 128, ("mT", tt), DM, tt))
        rms_pipeline(stgC)
        NFB = DFF // 256
        if STOP_AFTER == "C0":
            P.barrier()
            dbg_dump("mT", mT.rearrange("p a b -> p (a b)"), [("mT", tt) for tt in range(9)])
            dbg_dump("XN0", XN[0], [("XN", 0)])
            dbg_dump("ST0", ST[0], [("ST", 0)])
            dbg_dump("gbc", gbc, ["gbc"])
            dbg_dump("h", hres.rearrange("p a b -> p (a b)"), [("h", tt) for tt in range(9)])
            finish()
            return nc

        gu_slot = {}

        def load_gu(fb):
            slot = wq[0] % 2
            wq[0] += 1
            gu_slot[fb] = slot
            load_w(WG[slot], wview(w_gate, fb * 256, 256), ("W", slot), nobar=True)
            load_w(WU[slot], wview(w_up, fb * 256, 256), ("W", slot), nobar=True)

        def load_d(fb):
            load_w(Wd[fb % 2], w_down[fb * 256:(fb + 1) * 256, :].rearrange("(fl p) c -> p fl c", p=128), ("Wd", fb % 2))

        gctr = [0]

        ffn_blocks = [(0, 348), (348, 696), (696, T)]

        def GU_units(fb):
            slot = gu_slot[fb]
            Wg_, Wu_, kWs = WG[slot], WU[slot], ("W", slot)
            units = []
            for fl in range(2):
                for nb, (lo, hi) in enumerate(ffn_blocks):
                    def unit(fl=fl, nb=nb, lo=lo, hi=hi):
                        n = hi - lo
                        pr = gctr[0] % 2
                        gctr[0] += 1
                        bg, bu = pr * 2, pr * 2 + 1
                        rk = [("mT", tt) for tt in range(lo // 128, min(8, (hi - 1) // 128) + 1)]
                        mm_group(pbf[bg][:, 0:n], [(Wg_[:, k, fl * 128:(fl + 1) * 128], mT[:, k, lo:hi]) for k in range(16)], r=[kWs] + rk, w=[("ps", bg)])
                        mm_group(pbf[bu][:, 0:n], [(Wu_[:, k, fl * 128:(fl + 1) * 128], mT[:, k, lo:hi]) for k in range(16)], r=[kWs] + rk, w=[("ps", bu)])
                        P.op("act", lambda e: e.activation(out=stmp[pr][:, 0:n], in_=pbf[bg][:, 0:n], func=AF.Silu), r=[("ps", bg)], w=[("stmp", pr)])
                        P.op("dve", lambda e: e.tensor_tensor(out=actb[fb % 2][:, fl, lo:hi], in0=pbf[bu][:, 0:n], in1=stmp[pr][:, 0:n], op=ALU.mult),
                             r=[("ps", bu), ("stmp", pr)], w=[("act", fb % 2)])
                    units.append(P.capture(unit))
            return units

        dctr = [0]

        def DN_groups(fb):
            groups = []
            for tt in range(9):
                rows = 128 if tt < 8 else NS
                tcols = slice(tt * 128, tt * 128 + rows)
                for cb in range(4):
                    def grp_(tt=tt, rows=rows, tcols=tcols, cb=cb):
                        bank = 4 + dctr[0] % 4
                        dctr[0] += 1
                        mm_group(pbf[bank][:rows, :], [(actb[fb % 2][:, fl, tcols], Wd[fb % 2][:, fl, cb * 512:(cb + 1) * 512]) for fl in range(2)],
                                 r=[("act", fb % 2), ("Wd", fb % 2)], w=[("ps", bank)])
                        P.op("dve", lambda e: e.tensor_tensor(out=hres[:rows, tt, cb * 512:(cb + 1) * 512], in0=pbf[bank][:rows, :],
                                                              in1=hres[:rows, tt, cb * 512:(cb + 1) * 512], op=ALU.add),
                             r=[("ps", bank), ("h", tt, cb)], w=[("h", tt, cb)])
                    groups.append(P.capture(grp_))
            return groups

        load_gu(0)
        load_d(0)
        for i in range(NFB + 1):
            if i + 1 < NFB:
                load_gu(i + 1)
            gu = GU_units(i) if i < NFB else []
            dn = DN_groups(i - 1) if i >= 1 else []
            nslot = max(len(gu), 1)
            per = -(-len(dn) // nslot)
            for u in range(nslot):
                if u < len(gu):
                    P.ops.extend(gu[u])
                for g_ in dn[u * per:(u + 1) * per]:
                    P.ops.extend(g_)
            if i + 1 < NFB:
                load_d(i + 1)
        P.barrier()
        if STOP_AFTER == "C":
            dbg_dump("h", hres.rearrange("p a b -> p (a b)"), [("h", tt) for tt in range(9)])
            finish()
            return nc

        P.op("sp", lambda e: e.dma_start(out=gbc, in_=g_fin.partition_broadcast(128)), w=["gbc"], dma=True)
        for tt in range(9):
            rows = 128 if tt < 8 else NS
            i = tt % 2
            ht = hres[:, tt, :]
            st = ST[i]
            P.op("act", lambda e, rows=rows, ht=ht, st=st, i=i: e.activation(out=XN[i][:rows, :], in_=ht[:rows, :], func=AF.Square, accum_out=st[:rows, 0:1]),
                 r=[("h", tt, cb) for cb in range(4)], w=[("XN", i), ("ST", i)])
            P.op("act", lambda e, rows=rows, st=st: e.activation(out=st[:rows, 1:2], in_=st[:rows, 0:1], func=AF.Ln, scale=1.0 / DM, bias=EPS), r=[("ST", i)], w=[("ST", i)])
            P.op("act", lambda e, rows=rows, st=st: e.activation(out=st[:rows, 2:3], in_=st[:rows, 1:2], func=AF.Exp, scale=-0.5), r=[("ST", i)], w=[("ST", i)])
            P.op("dve", lambda e, rows=rows, ht=ht, st=st: e.scalar_tensor_tensor(out=ht[:rows, :], in0=ht[:rows, :], scalar=st[:rows, 2:3], op0=ALU.mult, in1=gbc[:rows, :], op1=ALU.mult),
                 r=[("h", tt, cb) for cb in range(4)] + [("ST", i), "gbc"], w=[("h", tt, cb) for cb in range(4)])
            dst = y_m[tt * 128:(tt + 1) * 128, :] if tt < 8 else y_s[:, :]
            ok = ("o_y", tt)
            P.op("sp", lambda e, rows=rows, ht=ht, dst=dst: e.dma_start(out=dst, in_=ht[:rows, :]), r=[("h", tt, cb) for cb in range(4)], w=[ok], dma=True)
            OUTKEYS.append(ok)
        finish()
        return nc


def _host_consts(inputs, core):
    hf = core % 2
    cst = np.zeros((128, CSTW), np.float32)
    r = np.arange(128)
    cst[:, C_ID:C_ID + 128] = np.eye(128, dtype=np.float32)
    cst[:, C_TRI:C_TRI + 128] = (r[:, None] <= r[None, :]).astype(np.float32)
    cst[:, C_U:C_U + 128] = (r[:, None] > r[None, :]).astype(np.float32)
    cst[:, C_ONE:C_ONE + 128] = 1.0
    cw = np.asarray(inputs["ssd_conv_w"])[0]
    cb = np.asarray(inputs["ssd_conv_b"])[0]
    cwb = np.concatenate([cw, cb[None]], 0)
    cst[:, C_CWB:C_CWB + 60] = cwb.reshape(5, 12, 128).transpose(2, 1, 0).reshape(128, 60)
    cst[:, C_GON:C_GON + 8] = np.asarray(inputs["chunk_out_norm_g"])[0].reshape(8, 128).T
    cst[:, C_ALOG:C_ALOG + 16] = np.asarray(inputs["ssd_a_log"])[0][None, :]
    cst[:, C_DTB:C_DTB + 16] = np.asarray(inputs["ssd_dt_bias"])[0][None, :]
    cst[:, C_D:C_D + 16] = np.asarray(inputs["ssd_d"])[0][None, :]
    cst[:, C_W00:C_W00 + 8] = np.asarray(inputs["chunk_w_s"])[0][:, 0, 0][None, :]
    cst[:, C_B0:C_B0 + 8] = np.asarray(inputs["chunk_b_s"])[0][:, 0][None, :]
    cst[:, C_FLAG] = float(hf)
    cst[:, C_DPP:C_DPP + 8] = np.repeat(np.asarray(inputs["ssd_d"])[0], 64).reshape(8, 128).T
    cst[:, C_SGPP:C_SGPP + 8] = np.asarray(inputs["ssd_norm_g"])[0].reshape(8, 128).T
    cst[0:16, C_DTBPP] = np.asarray(inputs["ssd_dt_bias"])[0]
    cst[0:16, C_ALPP] = np.asarray(inputs["ssd_a_log"])[0]
    cst2 = np.zeros((128, CST2W), np.float32)
    cst2[:, C2_BS:C2_BS + 1024] = np.asarray(inputs["chunk_b_s"])[0].reshape(1, 1024)
    rs = np.zeros((16, 8, 128), np.float32)
    for q in range(8):
        for rr in range(128):
            rs[2 * q + rr // 64, q, rr] = 1.0
    cst2[0:16, C2_RSEL:C2_RSEL + 1024] = rs.reshape(16, 1024)
    return cst, cst2


def make_in_maps(inputs):
    xp = np.asarray(inputs["x_prompt"], np.float32)
    xs = np.asarray(inputs["x_sample"], np.float32)
    sconv = np.asarray(inputs["state_conv"], np.float32)[0]
    sssm = np.asarray(inputs["state_ssm"], np.float32)[0]
    shared = {
        "w_in": np.ascontiguousarray(np.asarray(inputs["w_in"], np.float32)[0]),
        "w_out": np.ascontiguousarray(np.asarray(inputs["w_out"], np.float32)[0]),
        "w_gate": np.ascontiguousarray(np.asarray(inputs["w_gate"], np.float32)[0]),
        "w_up": np.ascontiguousarray(np.asarray(inputs["w_up"], np.float32)[0]),
        "w_down": np.ascontiguousarray(np.asarray(inputs["w_down"], np.float32)[0]),
        "ws": np.ascontiguousarray(np.asarray(inputs["chunk_w_s"], np.float32)[0]),
        "g_mix": np.ascontiguousarray(np.asarray(inputs["norm_mix_g"], np.float32)[0]),
        "g_ffn": np.ascontiguousarray(np.asarray(inputs["norm_ffn_g"], np.float32)[0]),
        "g_fin": np.ascontiguousarray(np.asarray(inputs["norm_final_g"], np.float32)),
        "vn_g": np.ascontiguousarray(np.asarray(inputs["chunk_v_norm_g"], np.float32)[0]),
        "ssd_g": np.ascontiguousarray(np.asarray(inputs["ssd_norm_g"], np.float32)[0]),
    }
    zeros_prev = np.zeros((TM, DM), np.float32)
    maps = []
    for c in range(8):
        b, hf = c // 2, c % 2
        cst, cst2 = _host_consts(inputs, c)
        m = dict(shared)
        m["xm"] = np.ascontiguousarray(xp[b, hf * TM:(hf + 1) * TM])
        m["xprev"] = np.ascontiguousarray(xp[b, 0:TM]) if hf == 1 else zeros_prev
        m["xsmp"] = np.ascontiguousarray(xs[c * NS:(c + 1) * NS, 0])
        m["sconv"] = np.ascontiguousarray(sconv[c * NS:(c + 1) * NS])
        m["sssm"] = np.ascontiguousarray(sssm[c * NS:(c + 1) * NS].reshape(NS, 1024, 128))
        m["cst"] = cst
        m["cst2"] = cst2
        maps.append(m)
    return maps


def kernel(**inputs):
    nc = build_program()
    maps = make_in_maps(inputs)
    res = run_bass_kernel_spmd(nc, maps, core_ids=list(range(8)))
    R = res.results
    y_prompt = np.zeros((4, 2048, DM), np.float32)
    y_sample = np.zeros((128, 1, DM), np.float32)
    ncp = np.zeros((1, 4, 3, 1536), np.float32)
    nsp = np.zeros((1, 4, 16, 64, 128), np.float32)
    ncs = np.zeros((1, 128, 3, 1536), np.float32)
    nss = np.zeros((1, 128, 16, 64, 128), np.float32)
    vs = np.zeros((1, 128, 1, 1024), np.float32)
    for c in range(8):
        b, hf = c // 2, c % 2
        y_prompt[b, hf * TM:(hf + 1) * TM] = R[c]["y_m"]
        y_sample[c * NS:(c + 1) * NS, 0] = R[c]["y_s"]
        if hf == 1:
            ncp[0, b] = R[c]["nconv_p"]
            nsp[0, b] = R[c]["nssm_p"].reshape(16, 64, 128)
        ncs[0, c * NS:(c + 1) * NS] = R[c]["nconv_s"]
        nss[0, c * NS:(c + 1) * NS] = R[c]["nssm_s"].reshape(NS, 16, 64, 128)
        vs[0, c * NS:(c + 1) * NS, 0] = R[c]["v_s"]
    return (y_prompt, y_sample, ncp, nsp, ncs, nss, vs)
```

```python
import numpy as np
from contextlib import ExitStack
import concourse.bass as bass
import concourse.mybir as mybir
from concourse.bass_utils import run_bass_kernel_spmd

F32 = mybir.dt.float32
F32R = mybir.dt.float32r
BF16 = mybir.dt.bfloat16
AF = mybir.ActivationFunctionType
ALU = mybir.AluOpType
AX = mybir.AxisListType

ENGS = ["pe", "act", "dve", "pool", "sp"]
NDMASEM = 12

DEBUG = {}
STOP_AFTER = None
SSD_LEVEL = 9


class Op:
    __slots__ = ("eng", "fn", "r", "w", "dma", "deps", "sig", "sem", "prev_use", "waits", "need_sig", "xdeps", "nobar")

    def __init__(self, eng, fn, r, w, dma, nobar=False):
        self.eng, self.fn, self.r, self.w, self.dma = eng, fn, tuple(r), tuple(w), dma
        self.deps = set()
        self.xdeps = set()
        self.sig = None
        self.sem = None
        self.prev_use = 0
        self.waits = []
        self.need_sig = False
        self.nobar = nobar


BARRIER = Op(None, None, (), (), False)


class Prog:
    def __init__(self):
        self.ops = []

    def op(self, eng, fn, r=(), w=(), dma=False, nobar=False):
        w = list(w) + [k for k in r if isinstance(k, tuple) and k and k[0] == "ps" and k not in w]
        o = Op(eng, fn, r, w, dma, nobar)
        self.ops.append(o)
        return o

    def barrier(self):
        self.ops.append(BARRIER)

    def capture(self, fn):
        saved = self.ops
        self.ops = []
        try:
            fn()
            got = self.ops
        finally:
            self.ops = saved
        return got

    @staticmethod
    def merge(main, side, lo=0, hi=None):
        if not side:
            return list(main)
        hi = len(main) if hi is None else hi
        out = []
        s_ = len(side)
        span = max(hi - lo, 1)
        j = 0
        for i, o in enumerate(main):
            out.append(o)
            if i >= lo:
                tgt = min(s_, (i + 1 - lo) * s_ // span)
                while j < tgt:
                    out.append(side[j])
                    j += 1
        out.extend(side[j:])
        return out

    def resolve(self):
        ops = self.ops
        last_w = {}
        readers = {}
        last_eng = {}
        dmas = []
        pending = {}
        for i, o in enumerate(ops):
            if o is BARRIER:
                s = set(last_eng.values()) | set(dmas)
                for e in ENGS:
                    pending[e] = set(s) | pending.get(e, set())
                continue
            if o.eng in pending and not o.nobar:
                o.xdeps = o.xdeps | pending.pop(o.eng)
            deps = set(o.xdeps)
            for k in o.r:
                if k in last_w:
                    deps.add(last_w[k])
            for k in o.w:
                if k in last_w:
                    deps.add(last_w[k])
                deps.update(readers.get(k, ()))
            deps.discard(i)
            keep = set()
            for d in deps:
                od = ops[d]
                if od.dma:
                    keep.add(d)
                elif od.eng == o.eng and not o.dma:
                    if o.eng == "pe":
                        continue
                    if d in o.xdeps:
                        continue
                    if set(od.w) & set(o.r):
                        keep.add(d)
                else:
                    keep.add(d)
            o.deps = keep
            for d in keep:
                ops[d].need_sig = True
            for k in o.r:
                readers.setdefault(k, []).append(i)
            for k in o.w:
                last_w[k] = i
                readers[k] = []
            if o.dma:
                dmas.append(i)
            elif o.fn is not None:
                last_eng[o.eng] = i
        cnt = {e: 0 for e in ENGS}
        dma_n = {e: 0 for e in ENGS}
        dma_use = {}
        for i, o in enumerate(ops):
            if o is BARRIER:
                continue
            if o.dma:
                slot = (o.eng, dma_n[o.eng] % NDMASEM)
                dma_n[o.eng] += 1
                u = dma_use.get(slot, 0)
                o.prev_use = u
                dma_use[slot] = u + 1
                o.sem = slot
                o.sig = 16 * (u + 1)
            elif o.need_sig:
                cnt[o.eng] += 1
                o.sem = o.eng
                o.sig = cnt[o.eng]
        seen = {e: {} for e in ENGS}
        for i, o in enumerate(ops):
            if o is BARRIER:
                continue
            sd = seen[o.eng]
            need = {}
            if o.dma and o.prev_use > 0:
                need[o.sem] = 16 * o.prev_use
            for d in o.deps:
                od = ops[d]
                need[od.sem] = max(need.get(od.sem, 0), od.sig)
            o.waits = []
            for s, v in need.items():
                if sd.get(s, 0) >= v:
                    continue
                sd[s] = v
                o.waits.append((s, v))

    def emit(self, sems, block):
        ops = self.ops

        def run(eng_name):
            def body(e):
                for o in ops:
                    if o is BARRIER or o.eng != eng_name:
                        continue
                    for s, v in o.waits:
                        e.wait_ge(sems[s], v)
                    if o.fn is None:
                        continue
                    ins = o.fn(e)
                    if o.dma:
                        ins.then_inc(sems[o.sem], 16)
                    elif o.sig is not None:
                        ins.then_inc(sems[o.sem], 1)
            return body

        block.sync(run("sp"))
        block.tensor(run("pe"))
        block.scalar(run("act"))
        block.vector(run("dve"))
        block.gpsimd(run("pool"))


DM = 2048
DIN = 4624
DFF = 5632
TM = 1024
NS = 16
T = TM + NS
EPS = 1e-6
COL_U, COL_V, COL_Z, COL_X, COL_B, COL_C, COL_DT = 0, 1024, 2048, 3072, 4096, 4352, 4608

C_ID, C_TRI, C_U, C_ONE = 0, 128, 256, 384
C_CWB, C_GON, C_ALOG, C_DTB, C_D, C_W00, C_B0, C_FLAG = 512, 572, 580, 596, 612, 628, 636, 644
C_DPP, C_SGPP, C_DTBPP, C_ALPP = 648, 656, 664, 665
CSTW = 672
C2_BS, C2_RSEL = 0, 1024
CST2W = 2048

AW = 51000


class Bump:
    def __init__(self, arena, start, end):
        self.arena, self.start, self.end, self.off = arena, start, end, start
        self.peak = start

    def reset(self, to=None):
        self.off = self.start if to is None else to

    def mark(self):
        return self.off

    def _take(self, words):
        o = self.off
        self.off += words
        assert self.off <= self.end, ("arena overflow", self.off, self.end)
        self.peak = max(self.peak, self.off)
        return o

    def f32(self, shape):
        n = int(np.prod(shape))
        o = self._take(n)
        v = self.arena[:, o:o + n]
        return _shape(v, shape)

    def bf16(self, shape):
        n = int(np.prod(shape))
        w = (n + 1) // 2
        o = self._take(w)
        v = self.arena[:, o:o + w].bitcast(BF16)[:, 0:n]
        return _shape(v, shape)


def _shape(v, shape):
    if len(shape) == 1:
        return v
    if len(shape) == 2:
        return v.rearrange("p (a b) -> p a b", a=shape[0])
    if len(shape) == 3:
        return v.rearrange("p (a b c) -> p a b c", a=shape[0], b=shape[1])
    raise ValueError(shape)


def build_program():
    nc = bass.Bass("TRN2", target_bir_lowering=False)

    def din(name, shape):
        return nc.dram_tensor(name, shape, F32, kind="ExternalInput").ap()

    def dout(name, shape):
        return nc.dram_tensor(name, shape, F32, kind="ExternalOutput").ap()

    xm = din("xm", [TM, DM])
    xprev = din("xprev", [TM, DM])
    xsmp = din("xsmp", [NS, DM])
    sconv = din("sconv", [NS, 3, 1536])
    sssm = din("sssm", [NS, 1024, 128])
    w_in = din("w_in", [DM, DIN])
    w_out = din("w_out", [DM, DM])
    w_gate = din("w_gate", [DM, DFF])
    w_up = din("w_up", [DM, DFF])
    w_down = din("w_down", [DFF, DM])
    cst_d = din("cst", [128, CSTW])
    cst2_d = din("cst2", [128, CST2W])
    ws_d = din("ws", [8, 128, 128])
    g_mix = din("g_mix", [DM])
    g_ffn = din("g_ffn", [DM])
    g_fin = din("g_fin", [DM])
    vn_g = din("vn_g", [1024])
    ssd_g = din("ssd_g", [1024])

    y_m = dout("y_m", [TM, DM])
    y_s = dout("y_s", [NS, DM])
    nconv_p = dout("nconv_p", [3, 1536])
    nssm_p = dout("nssm_p", [1024, 128])
    nconv_s = dout("nconv_s", [NS, 3, 1536])
    nssm_s = dout("nssm_s", [NS, 1024, 128])
    v_s = dout("v_s", [NS, 1024])
    dbg_d = {}
    for name, (shape, dty) in DEBUG.items():
        dbg_d[name] = nc.dram_tensor("dbg_" + name, list(shape), dty, kind="ExternalOutput").ap()

    P = Prog()
    with ExitStack() as es:
        arena = es.enter_context(nc.sbuf_tensor("arena", [128, AW], F32))
        rhsE = es.enter_context(nc.sbuf_tensor("rhsE", [128, 2048], F32))
        U32 = es.enter_context(nc.sbuf_tensor("U32", [128, 128], F32))
        pb = [es.enter_context(nc.psum_tensor(f"pb{i}", [128, 512], F32)) for i in range(8)]
        sems = {}
        for e in ENGS:
            sems[e] = es.enter_context(nc.semaphore("s_" + e))
        for e in ("sp", "pool"):
            for i in range(NDMASEM):
                sems[(e, i)] = es.enter_context(nc.semaphore(f"d_{e}_{i}"))
        block = es.enter_context(nc.Block())

        arena = arena[:, :]
        rhsEr = rhsE[:, :].bitcast(F32R)
        U32r = U32[:, :].bitcast(F32R)
        pbf = [p[:, :] for p in pb]
        pbb = [p[:, :].bitcast(BF16) for p in pb]

        fx = Bump(arena, 0, AW)
        CST = fx.f32([CSTW])
        identF = CST[:, C_ID:C_ID + 128]
        triF = CST[:, C_TRI:C_TRI + 128]
        UF = CST[:, C_U:C_U + 128]
        onesF = CST[:, C_ONE:C_ONE + 128]
        cwb = CST[:, C_CWB:C_CWB + 60].rearrange("p (j k) -> p j k", j=12)
        gon_pp = CST[:, C_GON:C_GON + 8]
        alog_bc = CST[:, C_ALOG:C_ALOG + 16]
        dtb_bc = CST[:, C_DTB:C_DTB + 16]
        D_bc = CST[:, C_D:C_D + 16]
        w00_bc = CST[:, C_W00:C_W00 + 8]
        b0_bc = CST[:, C_B0:C_B0 + 8]
        flag = CST[:, C_FLAG:C_FLAG + 1]
        D_pp = CST[:, C_DPP:C_DPP + 8]
        sg_pp = CST[:, C_SGPP:C_SGPP + 8]
        dtb_pp = CST[:, C_DTBPP:C_DTBPP + 1]
        al_pp = CST[:, C_ALPP:C_ALPP + 1]
        identB = fx.bf16([128])
        onesB = fx.bf16([128])
        aneg = fx.f32([16])
        hT = fx.f32([1024])
        hTb = fx.bf16([1024])
        prefix = fx.f32([12, 3])
        ncv = fx.f32([12, 3])
        nT = fx.bf16([16, T])
        YT = fx.bf16([16, T])
        r2_start = fx.off - (16 * T) // 2
        r2_end = fx.off
        xT_s = fx.bf16([12, NS])
        zT_s = fx.f32([8, NS])
        dtT_s = fx.f32([NS])
        Rsel = fx.f32([1024])
        Wfix = [fx.bf16([16, 512]) for _ in range(2)]
        _wflat = [w_.rearrange("p a b -> p (a b)") for w_ in Wfix]
        WG = [w_[:, 0:4096].rearrange("p (a b) -> p a b", a=16) for w_ in _wflat]
        WU = [w_[:, 4096:8192].rearrange("p (a b) -> p a b", a=16) for w_ in _wflat]
        wq = [0]
        S0 = fx.off
        sc = Bump(arena, S0, AW)
        r2 = Bump(arena, r2_start, r2_end)

        KC = ("cst",)

        P.op("sp", lambda e: e.dma_start(out=CST, in_=cst_d[:, :]), w=[KC], dma=True)
        P.op("dve", lambda e: e.tensor_copy(out=identB, in_=identF), r=[KC], w=["identB"])
        P.op("dve", lambda e: e.memset(onesB, 1.0), w=["onesB"])
        P.op("dve", lambda e: e.tensor_copy(out=U32r, in_=UF), r=[KC], w=["U32"])
        P.op("act", lambda e: e.activation(out=aneg, in_=alog_bc, func=AF.Exp), r=[KC], w=["aneg0"])
        P.op("dve", lambda e: e.tensor_scalar(out=aneg, in0=aneg, scalar1=-1.0, scalar2=None, op0=ALU.mult), r=["aneg0"], w=["aneg"])
        P.op("dve", lambda e: e.memset(hT, 0.0), w=["hT"])
        P.op("dve", lambda e: e.memset(hTb, 0.0), w=["hTb"])

        def rms_transpose(xt, kx, rows, xn, kxn, st, kst, gbc, kg, dstT, col0, kdst, width, tag, pre=None):
            nk = width // 128

            def s1():
                if pre is not None:
                    pre()
                P.op("act", lambda e: e.activation(out=xn[:rows, :], in_=xt[:rows, :], func=AF.Square, accum_out=st[:rows, 0:1]),
                     r=[kx] if not isinstance(kx, list) else kx, w=[kxn, kst])
                P.op("act", lambda e: e.activation(out=st[:rows, 1:2], in_=st[:rows, 0:1], func=AF.Ln, scale=1.0 / width, bias=EPS),
                     r=[kst], w=[kst])
                P.op("act", lambda e: e.activation(out=st[:rows, 2:3], in_=st[:rows, 1:2], func=AF.Exp, scale=-0.5),
                     r=[kst], w=[kst])
                P.op("dve", lambda e: e.scalar_tensor_tensor(out=xn[:rows, :], in0=xt[:rows, :], scalar=st[:rows, 2:3], op0=ALU.mult,
                                                             in1=gbc[:rows, :], op1=ALU.mult),
                     r=([kx] if not isinstance(kx, list) else kx) + [kst, kg, kxn], w=[kxn])

            def s2():
                for b8 in range(nk // 8):
                    bank = (tag % 2) * 2 + b8
                    psv = pbb[bank][:, 0:8 * rows].rearrange("p (a b) -> p a b", a=8)

                    def tr(e, b8=b8, psv=psv):
                        ins = None
                        for j in range(8):
                            k = b8 * 8 + j
                            ins = e.transpose(out=psv[:, j, :], in_=xn[:rows, k * 128:(k + 1) * 128], identity=identB[:rows, :rows])
                        return ins
                    P.op("pe", tr, r=[kxn, "identB"], w=[("ps", bank)])
                    dst = dstT[:, b8 * 8:(b8 + 1) * 8, col0:col0 + rows]
                    if b8 == 0:
                        P.op("act", lambda e, dst=dst, psv=psv: e.activation(out=dst, in_=psv, func=AF.Copy), r=[("ps", bank)], w=[kdst])
                    else:
                        P.op("dve", lambda e, dst=dst, psv=psv: e.tensor_copy(out=dst, in_=psv), r=[("ps", bank)], w=[kdst])
            return P.capture(s1), P.capture(s2)

        def rms_pipeline(stages):
            n = len(stages)
            for i in range(n + 1):
                if i < n:
                    P.ops.extend(stages[i][0])
                if i >= 1:
                    P.ops.extend(stages[i - 1][1])

        def mm_group(out, pairs, r, w):
            def fn(e):
                ins = None
                n = len(pairs)
                for i, (l, rh) in enumerate(pairs):
                    ins = e.matmul(out, lhsT=l, rhs=rh, start=(i == 0), stop=(i == n - 1))
                return ins
            P.op("pe", fn, r=r, w=w)

        def load_w(dst, src_ap, key, nobar=False):
            P.op("pool", lambda e: e.dma_start(out=dst, in_=src_ap), w=[key], dma=True, nobar=nobar)

        def wload(src_ap, ncols=512):
            slot = wq[0] % 2
            wq[0] += 1
            buf = Wfix[slot] if ncols == 512 else Wfix[slot][:, :, 0:ncols]
            load_w(buf, src_ap, ("W", slot), nobar=True)
            return buf, ("W", slot)

        def wview(wap, c0, ncols):
            return wap[:, c0:c0 + ncols].rearrange("(kt p) c -> p kt c", p=128)

        def conv_silu(rawpad, krp, acc, kacc, j, ntok, dst, kdst, defer=None):
            for k in (0, 1, 2):
                P.op("dve", lambda e, k=k: e.scalar_tensor_tensor(out=acc[:, 0:ntok], in0=rawpad[:, k:k + ntok], scalar=cwb[:, j, k:k + 1],
                                                                  op0=ALU.mult, in1=acc[:, 0:ntok], op1=ALU.add), r=[krp, kacc, KC], w=[kacc])
            silu_ops = P.capture(lambda: P.op("act", lambda e: e.activation(out=dst, in_=acc[:, 0:ntok], func=AF.Silu), r=[kacc], w=[kdst]))
            if defer is None:
                P.ops.extend(silu_ops)
            else:
                defer.append(silu_ops)

        def ssd_temps(b, main=True):
            t = {}
            t["sm"] = [b.f32([96]) for _ in range(2)]
            t["ex"] = b.f32([48])
            t["xdtw"] = b.bf16([16, 64])
            t["B_tok"] = b.bf16([256])
            if not main:
                return t
            t["LT"] = b.f32([2048])
            t["MT"] = b.bf16([16, 128])
            t["xs_tok"] = b.bf16([16, 64])
            t["xdt"] = b.bf16([16, 64])
            t["cbm"] = b.f32([2, 128])
            t["y1"] = b.f32([16, 64])
            t["t2"] = b.f32([16, 64])
            t["yb"] = b.bf16([1024])
            t["gn"] = b.f32([8])
            return t

        def ssd_chunk(t, xT, c, dtraw_c, kdt, main, ztok_c=None, ssdg_bc=None, banks=(0, 1, 2, 7, 1), ktag=""):
            par = c % 2
            sm, ex = t["sm"][par], t["ex"]
            cs = slice(c * 128, (c + 1) * 128)
            kx = ("xT" + ktag,)
            bs_, bx_, bb_ = banks[0], banks[1], banks[2]
            k0, k1, kdtc, kdta, kw2 = ("sm0", par), ("sm1", par), ("dtc", par), ("dta", par), ("w2", par)
            dtc = sm[:, 32:48]
            dta = sm[:, 48:64]
            w2 = sm[:, 64:80]
            toend, dec, ea = ex[:, 0:16], ex[:, 16:32], ex[:, 32:48]
            psx = pbb[bx_][:, 0:1024].rearrange("p (a b) -> p a b", a=8)
            psx3 = pbb[bx_][:, 0:1024].rearrange("p (a b) -> p a b", a=16)
            psB = pbb[bb_][:, 0:256].rearrange("p (a b) -> p a b", a=2)

            def head():
                P.op("dve", lambda e: e.tensor_tensor(out=sm[:, 0:16], in0=dtraw_c, in1=dtb_bc, op=ALU.add), r=[kdt, KC], w=[k0])
                P.op("act", lambda e: e.activation(out=sm[:, 16:32], in_=sm[:, 0:16], func=AF.Exp), r=[k0], w=[k1])
                P.op("act", lambda e: e.activation(out=dtc, in_=sm[:, 16:32], func=AF.Ln, bias=1.0), r=[k1], w=[kdtc])
                P.op("dve", lambda e: e.tensor_tensor(out=dta, in0=dtc, in1=aneg, op=ALU.mult), r=[kdtc, "aneg"], w=[kdta])

            def part1():
                def trx(e):
                    ins = None
                    for q in range(8):
                        ins = e.transpose(out=psx[:, q, :], in_=xT[:, q, cs], identity=identB)
                    return ins
                P.op("pe", trx, r=[kx, "identB"], w=[("ps", bx_)])
                if main:
                    psc = pbf[7][:, 0:256].rearrange("p (a b) -> p a b", a=2)

                    def cbf(e):
                        ins = None
                        for g in range(2):
                            ins = e.matmul(psc[:, g, :], lhsT=xT[:, 8 + g, cs], rhs=xT[:, 10 + g, cs], start=True, stop=True)
                        return ins
                    P.op("pe", cbf, r=[kx], w=[("ps", 7)])
                    P.op("dve", lambda e: e.tensor_tensor(out=t["cbm"], in0=psc, in1=triF.unsqueeze(1).broadcast_to([128, 2, 128]), op=ALU.mult),
                         r=[("ps", 7), KC], w=["cbm"])
                    for e16 in range(16):
                        P.op("dve", lambda e, e16=e16: e.tensor_scalar(out=rhsEr[:, e16 * 128:(e16 + 1) * 128], in0=triF, scalar1=dta[:, e16:e16 + 1],
                                                                       scalar2=None, op0=ALU.mult), r=[kdta, KC], w=[("rhsE", e16 // 4)])

                def small(e):
                    e.matmul(pbf[bs_][:, 0:16], lhsT=UF, rhs=dta, start=True, stop=True)
                    ins = e.matmul(pbf[bs_][:, 16:32], lhsT=onesF, rhs=dta, start=True, stop=True)
                    if main:
                        ins = e.matmul(pbf[bs_][:, 32:48], lhsT=triF, rhs=dta, start=True, stop=True)
                    return ins
                P.op("pe", small, r=[kdta, KC], w=[("ps", bs_)])
                nex = 48 if main else 32
                P.op("act", lambda e: e.activation(out=ex[:, 0:nex], in_=pbf[bs_][:, 0:nex], func=AF.Exp), r=[("ps", bs_)], w=["ex"])
                if main:
                    for i in range(4):
                        P.op("pe", lambda e, i=i: e.matmul(pbf[3 + i], lhsT=U32r, rhs=rhsEr[:, i * 512:(i + 1) * 512], start=True, stop=True),
                             r=[("rhsE", i), "U32"], w=[("ps", 3 + i)])
                        P.op("act", lambda e, i=i: e.activation(out=t["LT"][:, i * 512:(i + 1) * 512], in_=pbf[3 + i], func=AF.Exp),
                             r=[("ps", 3 + i)], w=[("LT", i)])

                def trb(e):
                    ins = None
                    for g in range(2):
                        ins = e.transpose(out=psB[:, g, :], in_=xT[:, 8 + g, cs], identity=identB)
                    return ins
                P.op("pe", trb, r=[kx, "identB"], w=[("ps", bb_)])
                P.op("dve", lambda e: e.tensor_tensor(out=w2, in0=dtc, in1=toend, op=ALU.mult), r=[kdtc, "ex"], w=[kw2])
                P.op("dve", lambda e: e.tensor_tensor(out=t["xdtw"], in0=psx3, in1=w2.unsqueeze(2).broadcast_to([128, 16, 64]), op=ALU.mult),
                     r=[("ps", bx_), kw2], w=["xdtw"])
                P.op("act", lambda e: e.activation(out=t["B_tok"], in_=pbb[bb_][:, 0:256], func=AF.Copy), r=[("ps", bb_)], w=["B_tok"])
                if main:
                    P.op("dve", lambda e: e.tensor_tensor(out=t["xdt"], in0=psx3, in1=dtc.unsqueeze(2).broadcast_to([128, 16, 64]), op=ALU.mult),
                         r=[("ps", bx_), kdtc], w=["xdt"])
                    P.op("act", lambda e: e.activation(out=t["xs_tok"], in_=psx3, func=AF.Copy), r=[("ps", bx_)], w=["xs_tok"])
                    LT3 = t["LT"].rearrange("p (a b) -> p a b", a=16)
                    for g in range(2):
                        P.op("dve", lambda e, g=g: e.tensor_tensor(out=t["MT"][:, g * 8:(g + 1) * 8, :], in0=LT3[:, g * 8:(g + 1) * 8, :],
                                                                   in1=t["cbm"][:, g:g + 1, :].broadcast_to([128, 8, 128]), op=ALU.mult),
                             r=[("LT", 2 * g), ("LT", 2 * g + 1), "cbm"], w=[("MT", g)])

            def part2():
                if main:
                    for g in range(2):
                        def yd(e, g=g):
                            ins = None
                            for j in range(8):
                                e16 = g * 8 + j
                                ins = e.matmul(pbf[3 + g][:, j * 64:(j + 1) * 64], lhsT=t["MT"][:, e16, :], rhs=t["xdt"][:, e16, :], start=True, stop=True)
                            return ins
                        P.op("pe", yd, r=[("MT", g), "xdt"], w=[("ps", 3 + g)])
                        P.op("pe", lambda e, g=g: e.matmul(pbf[5 + g], lhsT=xT[:, 10 + g, cs], rhs=hTb[:, g * 512:(g + 1) * 512], start=True, stop=True),
                             r=[kx, "hTb"], w=[("ps", 5 + g)])
                    y1 = t["y1"]
                    for g in range(2):
                        y1g = y1[:, g * 8:(g + 1) * 8, :]
                        P.op("dve", lambda e, g=g, y1g=y1g: e.tensor_tensor(out=y1g, in0=pbf[5 + g].rearrange("p (a b) -> p a b", a=8),
                                                                            in1=ea[:, g * 8:(g + 1) * 8].unsqueeze(2).broadcast_to([128, 8, 64]), op=ALU.mult),
                             r=[("ps", 5 + g), "ex"], w=[("y1", g)])
                        P.op("dve", lambda e, g=g, y1g=y1g: e.tensor_tensor(out=y1g, in0=pbf[3 + g].rearrange("p (a b) -> p a b", a=8), in1=y1g, op=ALU.add),
                             r=[("ps", 3 + g), ("y1", g)], w=[("y1", g)])
                    P.op("pool", lambda e: e.tensor_tensor(out=t["t2"], in0=t["xs_tok"], in1=D_bc.unsqueeze(2).broadcast_to([128, 16, 64]), op=ALU.mult),
                         r=["xs_tok", KC], w=["t2"])
                    P.op("dve", lambda e: e.tensor_tensor(out=y1, in0=y1, in1=t["t2"], op=ALU.add), r=[("y1", 0), ("y1", 1), "t2"], w=[("y1", 0), ("y1", 1)])
                    y1f = y1.rearrange("p a b -> p (a b)")
                    P.op("dve", lambda e: e.tensor_tensor(out=y1f, in0=y1f, in1=ztok_c, op=ALU.mult), r=[("y1", 0), ("y1", 1), "z_tok"], w=[("y1", 0), ("y1", 1)])
                    gn = t["gn"]
                    t2f = t["t2"].rearrange("p a b -> p (a b)")
                    for g in range(2):
                        P.op("act", lambda e, g=g: e.activation(out=t2f[:, g * 512:(g + 1) * 512], in_=y1f[:, g * 512:(g + 1) * 512], func=AF.Square,
                                                                accum_out=gn[:, g:g + 1]), r=[("y1", g)], w=["t2", ("gn", g)])
                    P.op("act", lambda e: e.activation(out=gn[:, 2:4], in_=gn[:, 0:2], func=AF.Ln, scale=1.0 / 512, bias=EPS), r=[("gn", 0), ("gn", 1)], w=["gn2"])
                    P.op("act", lambda e: e.activation(out=gn[:, 4:6], in_=gn[:, 2:4], func=AF.Exp, scale=-0.5), r=["gn2"], w=["gn4"])
                    for g in range(2):
                        P.op("dve", lambda e, g=g: e.scalar_tensor_tensor(out=t["yb"][:, g * 512:(g + 1) * 512], in0=y1f[:, g * 512:(g + 1) * 512],
                                                                          scalar=gn[:, 4 + g:5 + g], op0=ALU.mult, in1=ssdg_bc[:, g * 512:(g + 1) * 512], op1=ALU.mult),
                             r=[("y1", g), "gn4", "ssdg"], w=[("yb", g)])
                    psy = pbb[2][:, 0:1024].rearrange("p (a b) -> p a b", a=8)

                    def try_(e):
                        ins = None
                        for q in range(8):
                            ins = e.transpose(out=psy[:, q, :], in_=t["yb"][:, q * 128:(q + 1) * 128], identity=identB)
                        return ins
                    P.op("pe", try_, r=[("yb", 0), ("yb", 1), "identB"], w=[("ps", 2)])
                    P.op("act", lambda e: e.activation(out=YT[:, 8:16, cs], in_=psy, func=AF.Copy), r=[("ps", 2)], w=[("YTb", c)])

            def part_state():
                sb = (banks[3], banks[4])
                for g in range(2):
                    P.op("pe", lambda e, g=g: e.matmul(pbf[sb[g]], lhsT=t["B_tok"][:, g * 128:(g + 1) * 128], rhs=t["xdtw"].rearrange("p a b -> p (a b)")[:, g * 512:(g + 1) * 512],
                                                       start=True, stop=True), r=["B_tok", "xdtw"], w=[("ps", sb[g])])
                hT3 = hT.rearrange("p (a b) -> p a b", a=16)
                P.op("dve", lambda e: e.tensor_tensor(out=hT3, in0=hT3, in1=dec.unsqueeze(2).broadcast_to([128, 16, 64]), op=ALU.mult), r=["hT", "ex"], w=["hT"])
                for g in range(2):
                    P.op("dve", lambda e, g=g: e.tensor_tensor(out=hT[:, g * 512:(g + 1) * 512], in0=pbf[sb[g]], in1=hT[:, g * 512:(g + 1) * 512], op=ALU.add),
                         r=[("ps", sb[g]), "hT"], w=["hT"])
                P.op("act", lambda e: e.activation(out=hTb, in_=hT, func=AF.Copy), r=["hT"], w=["hTb"])
            return P.capture(head), P.capture(part1), P.capture(part2) + P.capture(part_state)

        def ssd_emit(chunks):
            n = len(chunks)
            P.ops.extend(chunks[0][0])
            for c in range(n):
                P.ops.extend(chunks[c][1])
                if c + 1 < n:
                    P.ops.extend(chunks[c + 1][0])
                P.ops.extend(chunks[c][2])

        dbg_ops = []

        def dbg_dump(name, ap, keys):
            if name in dbg_d:
                dbg_ops.append((name, ap, keys))

        def finish():
            P.barrier()
            outs = []
            for name, ap, keys in dbg_ops:
                k = ("dbgout", name)
                P.op("sp", lambda e, name=name, ap=ap: e.dma_start(out=dbg_d[name], in_=ap), r=keys, w=[k], dma=True)
                outs.append(k)
            P.op("sp", None, r=outs + OUTKEYS)
            P.resolve()
            P.emit(sems, block)

        OUTKEYS = []

        sc.reset()
        NX = 4
        X = [sc.f32([DM]) for _ in range(NX)]
        XN = [sc.bf16([DM]) for _ in range(2)]
        ST = [sc.f32([4]) for _ in range(2)]
        gbc = sc.f32([DM])
        m_common = sc.mark()
        nPT = nT[:, :, 0:TM]
        P.op("sp", lambda e, gbc=gbc: e.dma_start(out=gbc, in_=g_mix.partition_broadcast(128)), w=["gbc"], dma=True)
        stg = []
        for tt in range(8):
            i = tt % 2
            ix = tt % NX
            pre = (lambda tt=tt, ix=ix: P.op("sp", lambda e: e.dma_start(out=X[ix], in_=xprev[tt * 128:(tt + 1) * 128, :]), w=[("X", ix)], dma=True))
            stg.append(rms_transpose(X[ix], ("X", ix), 128, XN[i], ("XN", i), ST[i], ("ST", i), gbc, "gbc", nPT, tt * 128, ("nPT", tt), DM, tt, pre=pre))
        rms_pipeline(stg)
        Wdt = sc.bf16([16, 16])
        r2.reset()
        xTp = r2.bf16([10, TM])
        dtraw_p = r2.f32([8, 16])
        tP = ssd_temps(r2, main=False)
        rawpad = [sc.f32([TM + 4]) for _ in range(2)]
        accb = [sc.f32([TM]) for _ in range(2)]
        blocks = [(COL_X, 512), (COL_X + 512, 512), (COL_B, 512)]
        load_w(Wdt, wview(w_in, COL_DT, 16), "Wdt")
        wcur = wload(wview(w_in, blocks[0][0], blocks[0][1]))
        for i in range(2):
            P.op("dve", lambda e, rp=rawpad[i]: e.memset(rp[:, 0:3], 0.0), w=[("rawpad", i)])
        jt = 0
        defer_p = []
        for bi, (c0, ncol) in enumerate(blocks):
            Wv, kW = wcur
            if bi + 1 < len(blocks):
                wcur = wload(wview(w_in, blocks[bi + 1][0], blocks[bi + 1][1]))
            for jj in range(4):
                j = (c0 - COL_X) // 128 + jj
                rp = rawpad[jt % 2]
                krp = ("rawpad", jt % 2)
                isC = j >= 10
                pend_p = list(defer_p)
                del defer_p[:]
                for nb in range(2):
                    if isC and nb == 0:
                        continue
                    bank = 3 + (jt * 2 + nb) % 4
                    mm_group(pbf[bank], [(Wv[:, k, jj * 128:(jj + 1) * 128], nPT[:, k, nb * 512:(nb + 1) * 512]) for k in range(16)],
                             r=[kW] + [("nPT", tt) for tt in range(nb * 4, nb * 4 + 4)], w=[("ps", bank)])
                    P.op("act", lambda e, rp=rp, nb=nb, bank=bank: e.activation(out=rp[:, 3 + nb * 512:3 + (nb + 1) * 512], in_=pbf[bank], func=AF.Copy),
                         r=[("ps", bank)], w=[krp])
                    if not isC:
                        P.op("act", lambda e, ac=accb[jt % 2], nb=nb, bank=bank, j=j: e.activation(out=ac[:, nb * 512:(nb + 1) * 512], in_=pbf[bank], func=AF.Identity,
                                                                                                scale=cwb[:, j, 3:4], bias=cwb[:, j, 4:5]),
                             r=[("ps", bank), KC], w=[("acc", jt % 2)])
                for so in pend_p:
                    P.ops.extend(so)
                P.op("dve", lambda e, rp=rp, j=j: e.tensor_copy(out=prefix[:, j, :], in_=rp[:, TM:TM + 3]), r=[krp], w=["prefix"])
                if not isC:
                    conv_silu(rp, krp, accb[jt % 2], ("acc", jt % 2), j, TM, xTp[:, j, :], ("xTp",), defer=defer_p)
                jt += 1
        for so in defer_p:
            P.ops.extend(so)
        for tt in range(8):
            mm_group(pbf[0][:, 0:16], [(nPT[:, k, tt * 128:(tt + 1) * 128], Wdt[:, k, :]) for k in range(16)], r=["Wdt", ("nPT", tt)], w=[("ps", 0)])
            P.op("act", lambda e, tt=tt: e.activation(out=dtraw_p[:, tt, :], in_=pbf[0][:, 0:16], func=AF.Copy), r=[("ps", 0)], w=["dtraw_p"])
        P.barrier()

        def pchunks():
            ssd_emit([ssd_chunk(tP, xTp, c, dtraw_p[:, c, :], "dtraw_p", main=False, banks=(5, 6, 5, 7, 6), ktag="p") for c in range(8)])
            P.op("dve", lambda e: e.tensor_scalar(out=hT, in0=hT, scalar1=flag, scalar2=None, op0=ALU.mult), r=["hT", KC], w=["hT"])
            P.op("act", lambda e: e.activation(out=hTb, in_=hT, func=AF.Copy), r=["hT"], w=["hTb"])
        pch_ops = P.capture(pchunks)
        _saved_ops = P.ops
        P.ops = []
        sc.reset(m_common)
        stgA = []
        for tt in range(9):
            i = tt % 2
            rows = 128 if tt < 8 else NS
            src = xm[tt * 128:(tt + 1) * 128, :] if tt < 8 else xsmp[:, :]
            ix = tt % NX
            pre = (lambda ix=ix, rows=rows, src=src: P.op("sp", lambda e: e.dma_start(out=X[ix][:rows, :], in_=src), w=[("X", ix)], dma=True))
            stgA.append(rms_transpose(X[ix], ("X", ix), rows, XN[i], ("XN", i), ST[i], ("ST", i), gbc, "gbc", nT, tt * 128, ("nT", tt), DM, tt, pre=pre))
        rms_pipeline(stgA)
        P.barrier()
        sc.reset()
        xT = sc.bf16([12, T])
        z_tok = sc.bf16([8, 1024])
        dtraw = sc.f32([8, 16])
        ssdg_bc = sc.f32([1024])
        raw_s = sc.f32([12, NS])
        m_A = sc.mark()
        Wdt = sc.bf16([16, 16])
        rawpad = [sc.f32([TM + 4]) for _ in range(2)]
        accb = [sc.f32([TM]) for _ in range(2)]
        scTok = sc.f32([3 * 1536])
        scT = sc.f32([36, NS])
        acc_s = sc.f32([12, NS])
        P.op("sp", lambda e: e.dma_start(out=ssdg_bc, in_=ssd_g.partition_broadcast(128)), w=["ssdg"], dma=True)
        P.op("sp", lambda e: e.dma_start(out=Rsel, in_=cst2_d[:, C2_RSEL:C2_RSEL + 1024]), w=["cst2"], dma=True)
        P.op("sp", lambda e: e.dma_start(out=scTok[:NS, :], in_=sconv.rearrange("b r c -> b (r c)")), w=["scTok"], dma=True)
        P.op("sp", lambda e: e.dma_start(out=nconv_s[:, 0:2, :], in_=sconv[:, 1:3, :]), w=["o_ncs01"], dma=True)
        OUTKEYS.append("o_ncs01")
        for half in range(2):
            def trs(e, half=half):
                ins = None
                for idx in range(half * 18, half * 18 + 18):
                    r_, j_ = idx // 12, idx % 12
                    ii = idx - half * 18
                    ins = e.transpose(out=pbf[half][:, ii * NS:(ii + 1) * NS], in_=scTok[:NS, r_ * 1536 + j_ * 128:r_ * 1536 + (j_ + 1) * 128],
                                      identity=identF[:NS, :NS])
                return ins
            P.op("pe", trs, r=["scTok", KC], w=[("ps", half)])
            P.op("dve", lambda e, half=half: e.tensor_copy(out=scT[:, half * 18:half * 18 + 18, :].rearrange("p a b -> p (a b)"), in_=pbf[half][:, 0:18 * NS]),
                 r=[("ps", half)], w=["scT"])
        blocksA = [(COL_Z, "z"), (COL_Z + 512, "z"), (COL_X, "x"), (COL_X + 512, "x"), (COL_B, "x")]
        load_w(Wdt, wview(w_in, COL_DT, 16), "Wdt")
        wcur = wload(wview(w_in, blocksA[0][0], 512))
        jt = 0
        grp = 0
        defer_a = []
        _zwin = [len(P.ops), None]
        for bi, (c0, kind) in enumerate(blocksA):
            Wv, kW = wcur
            if kind == "x" and _zwin[1] is None:
                _zwin[1] = len(P.ops)
            if bi + 1 < len(blocksA):
                wcur = wload(wview(w_in, blocksA[bi + 1][0], 512))
            if kind == "z":
                zb = (c0 - COL_Z) // 512
                for tt in range(8):
                    bank = 1 + grp % 4
                    grp += 1
                    mm_group(pbf[bank], [(nT[:, k, tt * 128:(tt + 1) * 128], Wv[:, k, :]) for k in range(16)], r=[kW, ("nT", tt)], w=[("ps", bank)])
                    P.op("act", lambda e, tt=tt, zb=zb, bank=bank: e.activation(out=z_tok[:, tt, zb * 512:(zb + 1) * 512], in_=pbf[bank], func=AF.Silu),
                         r=[("ps", bank)], w=["z_tok"])
                for jj in range(4):
                    bank = 1 + grp % 4
                    grp += 1
                    j = zb * 4 + jj
                    mm_group(pbf[bank][:, 0:NS], [(Wv[:, k, jj * 128:(jj + 1) * 128], nT[:, k, TM:T]) for k in range(16)], r=[kW, ("nT", 8)], w=[("ps", bank)])
                    P.op("act", lambda e, j=j, bank=bank: e.activation(out=zT_s[:, j, :], in_=pbf[bank][:, 0:NS], func=AF.Silu), r=[("ps", bank)], w=["zT_s"])
            else:
                for jj in range(4):
                    j = (c0 - COL_X) // 128 + jj
                    rp = rawpad[jt % 2]
                    krp = ("rawpad", jt % 2)
                    pend_a = list(defer_a)
                    del defer_a[:]
                    P.op("dve", lambda e, rp=rp, j=j: e.tensor_copy(out=rp[:, 0:3], in_=prefix[:, j, :]), r=["prefix"], w=[krp])
                    for nb in range(3):
                        bank = 1 + grp % 4
                        grp += 1
                        lo, hi = (nb * 512, (nb + 1) * 512) if nb < 2 else (TM, T)
                        n = hi - lo
                        rk = [("nT", tt) for tt in range(nb * 4, nb * 4 + 4)] if nb < 2 else [("nT", 8)]
                        mm_group(pbf[bank][:, 0:n], [(Wv[:, k, jj * 128:(jj + 1) * 128], nT[:, k, lo:hi]) for k in range(16)], r=[kW] + rk, w=[("ps", bank)])
                        if nb < 2:
                            P.op("act", lambda e, rp=rp, lo=lo, hi=hi, bank=bank: e.activation(out=rp[:, 3 + lo:3 + hi], in_=pbf[bank], func=AF.Copy),
                                 r=[("ps", bank)], w=[krp])
                            P.op("act", lambda e, ac=accb[jt % 2], lo=lo, hi=hi, bank=bank, j=j: e.activation(out=ac[:, lo:hi], in_=pbf[bank], func=AF.Identity,
                                                                                                    scale=cwb[:, j, 3:4], bias=cwb[:, j, 4:5]),
                                 r=[("ps", bank), KC], w=[("acc", jt % 2)])
                        else:
                            P.op("act", lambda e, j=j, bank=bank: e.activation(out=raw_s[:, j, :], in_=pbf[bank][:, 0:NS], func=AF.Copy),
                                 r=[("ps", bank)], w=["raw_s"])
                    for so in pend_a:
                        P.ops.extend(so)
                    P.op("dve", lambda e, rp=rp, j=j: e.tensor_copy(out=ncv[:, j, :], in_=rp[:, TM:TM + 3]), r=[krp], w=["ncv"])
                    conv_silu(rp, krp, accb[jt % 2], ("acc", jt % 2), j, TM, xT[:, j, 0:TM], ("xT",), defer=defer_a)
                    P.op("dve", lambda e, j=j: e.tensor_scalar(out=acc_s[:, j, :], in0=raw_s[:, j, :], scalar1=cwb[:, j, 3:4], scalar2=cwb[:, j, 4:5],
                                                               op0=ALU.mult, op1=ALU.add), r=["raw_s", KC], w=["acc_s"])
                    for r_ in range(3):
                        P.op("dve", lambda e, j=j, r_=r_: e.scalar_tensor_tensor(out=acc_s[:, j, :], in0=scT[:, r_ * 12 + j, :], scalar=cwb[:, j, r_:r_ + 1],
                                                                                 op0=ALU.mult, in1=acc_s[:, j, :], op1=ALU.add), r=["scT", "acc_s", KC], w=["acc_s"])
                    jt += 1
        for so in defer_a:
            P.ops.extend(so)
        P.op("act", lambda e: e.activation(out=xT_s, in_=acc_s, func=AF.Silu), r=["acc_s"], w=["xT_s"])
        for tt in range(8):
            mm_group(pbf[0][:, 0:16], [(nT[:, k, tt * 128:(tt + 1) * 128], Wdt[:, k, :]) for k in range(16)], r=["Wdt", ("nT", tt)], w=[("ps", 0)])
            P.op("act", lambda e, tt=tt: e.activation(out=dtraw[:, tt, :], in_=pbf[0][:, 0:16], func=AF.Copy), r=[("ps", 0)], w=["dtraw"])
        mm_group(pbf[1][:NS, 0:NS], [(Wdt[:, k, :], nT[:, k, TM:T]) for k in range(16)], r=["Wdt", ("nT", 8)], w=[("ps", 1)])
        P.op("act", lambda e: e.activation(out=dtT_s[:NS, :], in_=pbf[1][:NS, 0:NS], func=AF.Copy), r=[("ps", 1)], w=["dtT_s"])

        def rows_out(src3, ksrc, R, stage, kst, dram_ap, okey):
            for b3 in range(3):
                def trr(e, b3=b3):
                    ins = None
                    for jj in range(4):
                        j_ = b3 * 4 + jj
                        ins = e.transpose(out=pbf[2 + b3][:R, jj * 128:(jj + 1) * 128], in_=src3[:, j_, :], identity=identF)
                    return ins
                P.op("pe", trr, r=[ksrc, KC], w=[("ps", 2 + b3)])
                P.op("dve", lambda e, b3=b3: e.tensor_copy(out=stage[:R, b3 * 512:(b3 + 1) * 512], in_=pbf[2 + b3][:R, :]), r=[("ps", 2 + b3)], w=[kst, "scTok"])
            P.op("sp", lambda e: e.dma_start(out=dram_ap, in_=stage[:R, 0:1536]), r=[kst], w=[okey], dma=True)
            OUTKEYS.append(okey)
        rows_out(raw_s, "raw_s", NS, scTok[:, 0:1536], "stg0", nconv_s[:, 2, :], "o_ncs2")
        rows_out(ncv, "ncv", 3, scTok[:, 1536:3072], "stg1", nconv_p[:, :], "o_ncp")
        _main_ops = P.ops
        P.ops = _saved_ops
        P.ops.extend(Prog.merge(_main_ops, pch_ops, _zwin[0], _zwin[1]))
        P.barrier()
        if STOP_AFTER == "A1":
            dbg_dump("xT", xT.rearrange("p a b -> p (a b)"), [("xT",)])
            dbg_dump("z_tok", z_tok.rearrange("p a b -> p (a b)"), ["z_tok"])
            dbg_dump("zT_s", zT_s.rearrange("p a b -> p (a b)"), ["zT_s"])
            dbg_dump("dtraw", dtraw.rearrange("p a b -> p (a b)"), ["dtraw"])
            dbg_dump("dtT_s", dtT_s, ["dtT_s"])
            finish()
            return nc

        sc.reset(m_A)
        tA = ssd_temps(sc)
        ssd_emit([ssd_chunk(tA, xT, c, dtraw[:, c, :], "dtraw", main=True, ztok_c=z_tok[:, c, :], ssdg_bc=ssdg_bc) for c in range(8)])
        if STOP_AFTER == "A2a":
            P.barrier()
            dbg_dump("YT", YT.rearrange("p a b -> p (a b)"), [("YTb", c) for c in range(8)])
            finish()
            return nc
        hout = sc.f32([8, 128])
        for half in range(2):
            def trh(e, half=half):
                ins = None
                for jj in range(4):
                    q = half * 4 + jj
                    ins = e.transpose(out=pbf[3 + half][:, jj * 128:(jj + 1) * 128], in_=hT[:, q * 128:(q + 1) * 128], identity=identF)
                return ins
            P.op("pe", trh, r=["hT", KC], w=[("ps", 3 + half)])
            P.op("dve", lambda e, half=half: e.tensor_copy(out=hout[:, half * 4:(half + 1) * 4, :].rearrange("p a b -> p (a b)"), in_=pbf[3 + half]),
                 r=[("ps", 3 + half)], w=["hout"])
        P.op("sp", lambda e: e.dma_start(out=nssm_p.rearrange("(q r) n -> r q n", r=128), in_=hout), r=["hout"], w=["o_nsp"], dma=True)
        OUTKEYS.append("o_nsp")

        if STOP_AFTER == "A2b":
            P.barrier()
            dbg_dump("YT", YT.rearrange("p a b -> p (a b)"), [("YTb", c) for c in range(8)])
            finish()
            return nc
        P.barrier()
        SMPW = 7300
        scS = Bump(arena, AW - SMPW, AW)
        _saved_main = P.ops
        P.ops = []
        _sc_main = sc
        sc = scS
        sm_s = sc.f32([8, NS])
        cat = sc.f32([32])
        rep = sc.f32([8, 32])
        dtx = sc.f32([8, NS])
        ys = sc.f32([8, NS])
        ysq = sc.f32([8, NS])
        rs = sc.f32([2, NS])
        BC_tok = sc.bf16([512])
        selB = sc.bf16([NS, 128])
        NHS = 3
        hs = [sc.f32([8, 128]) for _ in range(NHS)]
        tmpS = [sc.f32([1024]) for _ in range(2)]
        x0 = sm_s[:NS, 0, :]; e1 = sm_s[:NS, 1, :]; anp = sm_s[:NS, 2, 0:1]
        P.op("act", lambda e: e.activation(out=e1, in_=dtT_s[:NS, :], func=AF.Exp, bias=dtb_pp[:NS, :]), r=["dtT_s", KC], w=["s_e1"])
        P.op("act", lambda e: e.activation(out=cat[:NS, 16:32], in_=e1, func=AF.Ln, bias=1.0), r=["s_e1"], w=["s_dt"])
        P.op("act", lambda e: e.activation(out=anp, in_=al_pp[:NS, :], func=AF.Exp), r=[KC], w=["s_anp0"])
        P.op("dve", lambda e: e.tensor_scalar(out=anp, in0=anp, scalar1=-1.0, scalar2=None, op0=ALU.mult), r=["s_anp0"], w=["s_anp"])
        P.op("act", lambda e: e.activation(out=cat[:NS, 0:16], in_=cat[:NS, 16:32], func=AF.Exp, scale=anp), r=["s_dt", "s_anp"], w=["s_dA"])

        def repf(e):
            ins = None
            for q in range(8):
                ins = e.matmul(pbf[7][:, q * 32:(q + 1) * 32], lhsT=Rsel[:NS, q * 128:(q + 1) * 128], rhs=cat[:NS, :], start=True, stop=True)
            return ins
        P.op("pe", repf, r=["s_dA", "s_dt", "cst2"], w=[("ps", 7)])
        P.op("dve", lambda e: e.tensor_copy(out=rep.rearrange("p a b -> p (a b)"), in_=pbf[7][:, 0:256]), r=[("ps", 7)], w=["rep"])
        xs_s = xT_s[:, 0:8, :]
        P.op("dve", lambda e: e.tensor_tensor(out=dtx, in0=rep[:, :, 16:32], in1=xs_s, op=ALU.mult), r=["rep", "xT_s"], w=["dtx"])
        psbc = pbb[7][:NS, 0:512].rearrange("p (a b) -> p a b", a=4)

        def trbc(e):
            ins = None
            for jj in range(4):
                ins = e.transpose(out=psbc[:, jj, :], in_=xT_s[:, 8 + jj, :], identity=identB)
            return ins
        P.op("pe", trbc, r=["xT_s", "identB"], w=[("ps", 7)])
        P.op("act", lambda e: e.activation(out=BC_tok[:NS, :], in_=pbb[7][:NS, 0:512], func=AF.Copy), r=[("ps", 7)], w=["BC_tok"])
        P.op("dve", lambda e: e.tensor_copy(out=selB[:NS, :, :], in_=identB[:NS, 0:NS].unsqueeze(2).broadcast_to([NS, NS, 128])), r=["identB"], w=["selB"])
        P.op("dve", lambda e: e.memset(ys, 0.0), w=["ys"])

        def hs_load(b):
            P.op("sp", lambda e, b=b: e.dma_start(out=hs[b % NHS], in_=sssm[b].rearrange("(q r) n -> r q n", r=128)), w=[("hs", b % NHS, q) for q in range(8)], dma=True)
        for b in range(min(NHS - 1, NS)):
            hs_load(b)
        for b in range(NS):
            hb = hs[b % NHS]
            bank = 5 + b % 2
            if b + NHS - 1 < NS:
                hs_load(b + NHS - 1)
            P.op("pe", lambda e, b=b, bank=bank: e.matmul(pbf[bank], lhsT=selB[:NS, b, :], rhs=BC_tok[:NS, :], start=True, stop=True),
                 r=["selB", "BC_tok"], w=[("ps", bank)])
            for q in range(8):
                P.op("act", lambda e, b=b, q=q, hb=hb: e.activation(out=hb[:, q, :], in_=hb[:, q, :], func=AF.Copy, scale=rep[:, q, b:b + 1]),
                     r=[("hs", b % NHS, q), "rep"], w=[("hs", b % NHS, q)])

            bc4 = pbf[bank][:, 0:256].rearrange("p (g n) -> p g n", g=2).unsqueeze(2).broadcast_to([128, 2, 4, 128])
            cc4 = pbf[bank][:, 256:512].rearrange("p (g n) -> p g n", g=2).unsqueeze(2).broadcast_to([128, 2, 4, 128])
            hb4 = hb.rearrange("p (g j) n -> p g j n", g=2)
            tm4 = tmpS[b % 2].rearrange("p (g j n) -> p g j n", g=2, j=4)
            dx4 = dtx[:, :, b:b + 1].rearrange("p (g j) o -> p g j o", g=2).broadcast_to([128, 2, 4, 128])
            khs = [("hs", b % NHS, q) for q in range(8)]
            ktm = ("tmpS", b % 2)
            P.op("dve", lambda e, tm4=tm4, bc4=bc4, dx4=dx4: e.tensor_tensor(out=tm4, in0=bc4, in1=dx4, op=ALU.mult), r=[("ps", bank), "dtx"], w=[ktm])
            P.op("dve", lambda e, hb4=hb4, tm4=tm4: e.tensor_tensor(out=hb4, in0=hb4, in1=tm4, op=ALU.add), r=khs + [ktm], w=khs)
            P.op("dve", lambda e, hb4=hb4, tm4=tm4, cc4=cc4: e.tensor_tensor(out=tm4, in0=cc4, in1=hb4, op=ALU.mult), r=khs + [("ps", bank), ktm], w=[ktm])
            P.op("dve", lambda e, b=b, tm=tmpS[b % 2]: e.tensor_reduce(out=ys[:, :, b:b + 1], in_=tm.rearrange("p (q n) -> p q n", q=8), axis=AX.X, op=ALU.add),
                 r=[ktm], w=["ys"])
            ok = ("o_nss", b)
            P.op("sp", lambda e, b=b, hb=hb: e.dma_start(out=nssm_s[b].rearrange("(q r) n -> r q n", r=128), in_=hb), r=[("hs", b % NHS, q) for q in range(8)], w=[ok], dma=True)
            OUTKEYS.append(ok)
        P.op("dve", lambda e: e.tensor_tensor(out=ysq, in0=xs_s, in1=D_pp.unsqueeze(2).broadcast_to([128, 8, NS]), op=ALU.mult), r=["xT_s", KC], w=["ysq"])
        P.op("dve", lambda e: e.tensor_tensor(out=ys, in0=ys, in1=ysq, op=ALU.add), r=["ys", "ysq"], w=["ys"])
        P.op("dve", lambda e: e.tensor_tensor(out=ys, in0=ys, in1=zT_s, op=ALU.mult), r=["ys", "zT_s"], w=["ys"])
        P.op("dve", lambda e: e.tensor_tensor(out=ysq, in0=ys, in1=ys, op=ALU.mult), r=["ys", "ysq"], w=["ysq"])

        def gsum(e):
            ins = None
            for q in range(8):
                g = q // 4
                ins = e.matmul(pbf[7][:, g * NS:(g + 1) * NS], lhsT=onesF, rhs=ysq[:, q, :], start=(q % 4 == 0), stop=(q % 4 == 3))
            return ins
        P.op("pe", gsum, r=["ysq", KC], w=[("ps", 7)])
        rsf = rs.rearrange("p a b -> p (a b)")
        P.op("act", lambda e: e.activation(out=rsf, in_=pbf[7][:, 0:2 * NS], func=AF.Ln, scale=1.0 / 512, bias=EPS), r=[("ps", 7)], w=["rs0"])
        P.op("act", lambda e: e.activation(out=rsf, in_=rsf, func=AF.Exp, scale=-0.5), r=["rs0"], w=["rs"])
        for g in range(2):
            P.op("dve", lambda e, g=g: e.tensor_tensor(out=ys[:, g * 4:(g + 1) * 4, :], in0=ys[:, g * 4:(g + 1) * 4, :],
                                                       in1=rs[:, g:g + 1, :].broadcast_to([128, 4, NS]), op=ALU.mult), r=["ys", "rs"], w=["ys"])
        P.op("dve", lambda e: e.tensor_tensor(out=YT[:, 8:16, TM:T], in0=ys, in1=sg_pp.unsqueeze(2).broadcast_to([128, 8, NS]), op=ALU.mult),
             r=["ys", KC], w=[("YTb", 8)])
        smp_ops = P.ops
        P.ops = _saved_main
        sc = _sc_main

        sc.reset()
        sc.end = AW - SMPW
        _saved_a3 = P.ops
        P.ops = []
        vng_bc = sc.f32([1024])
        bs_bc = sc.f32([8, 128])
        v_tok = sc.bf16([9, 1024])
        WsT = sc.bf16([8, 128])
        sqacc = sc.f32([T])
        gv = [sc.f32([1024]) for _ in range(2)]
        wsraw = gv[0].rearrange("p (a b) -> p a b", a=8)
        vst = [sc.f32([4]) for _ in range(2)]
        ug = [sc.f32([T]) for _ in range(2)]
        _m_tm = sc.mark()
        tmpm = [sc.f32([512]) for _ in range(2)]
        _m_tm2 = sc.mark()
        sc.reset(_m_tm)
        vsf = sc.f32([1024])
        sc.reset(_m_tm2)
        sq = [sc.bf16([T]) for _ in range(2)]
        rstd_bc = sqacc
        w00I = sc.bf16([8, NS])
        P.op("sp", lambda e: e.dma_start(out=vng_bc, in_=vn_g.partition_broadcast(128)), w=["vng"], dma=True)
        P.op("sp", lambda e: e.dma_start(out=bs_bc.rearrange("p a b -> p (a b)"), in_=cst2_d[:, C2_BS:C2_BS + 1024]), w=["bs_bc"], dma=True)
        P.op("sp", lambda e: e.dma_start(out=wsraw, in_=ws_d.rearrange("h t s -> t h s")), w=["wsraw"], dma=True)
        wv2 = [wload(wview(w_in, COL_V, 512)), wload(wview(w_in, COL_V + 512, 512))]
        for half in range(2):
            def trw(e, half=half):
                ins = None
                for jj in range(4):
                    h_ = half * 4 + jj
                    ins = e.transpose(out=pbf[half][:, jj * 128:(jj + 1) * 128], in_=wsraw[:, h_, :], identity=identF)
                return ins
            P.op("pe", trw, r=["wsraw", KC], w=[("ps", half)])
            P.op("dve", lambda e, half=half: e.tensor_tensor(out=WsT[:, half * 4:(half + 1) * 4, :], in0=pbf[half].rearrange("p (a b) -> p a b", a=4),
                                                             in1=triF.unsqueeze(1).broadcast_to([128, 4, 128]), op=ALU.mult), r=[("ps", half), KC], w=["WsT"])
        for h_ in range(8):
            P.op("dve", lambda e, h_=h_: e.tensor_scalar(out=w00I[:NS, h_, :], in0=identF[:NS, 0:NS], scalar1=w00_bc[:NS, h_:h_ + 1], scalar2=None, op0=ALU.mult),
                 r=[KC], w=["w00I"])
        for tt in range(9):
            rows = 128 if tt < 8 else NS
            i = tt % 2
            tcols = slice(tt * 128, tt * 128 + rows)
            for zb in range(2):
                bank = 1 + (tt * 2 + zb) % 4
                mm_group(pbf[bank][:rows, :], [(nT[:, k, tcols], wv2[zb][0][:, k, :]) for k in range(16)], r=[wv2[zb][1], ("nT", tt)], w=[("ps", bank)])
                P.op("act", lambda e, i=i, zb=zb, bank=bank, rows=rows: e.activation(out=gv[i][:rows, zb * 512:(zb + 1) * 512], in_=pbf[bank][:rows, :],
                                                                                    func=AF.Gelu_apprx_tanh), r=[("ps", bank)], w=[("gv", i, zb)])
            P.op("act", lambda e, i=i, rows=rows: e.activation(out=ug[i][:rows, 0:1024], in_=gv[i][:rows, :], func=AF.Square, accum_out=vst[i][:rows, 0:1]),
                 r=[("gv", i, 0), ("gv", i, 1)], w=[("ug", i), ("vst", i)])
            P.op("act", lambda e, i=i, rows=rows: e.activation(out=vst[i][:rows, 1:2], in_=vst[i][:rows, 0:1], func=AF.Ln, scale=1.0 / 1024, bias=EPS),
                 r=[("vst", i)], w=[("vst", i)])
            P.op("act", lambda e, i=i, rows=rows: e.activation(out=vst[i][:rows, 2:3], in_=vst[i][:rows, 1:2], func=AF.Exp, scale=-0.5),
                 r=[("vst", i)], w=[("vst", i)])
            P.op("dve", lambda e, i=i, rows=rows, tt=tt: e.scalar_tensor_tensor(out=v_tok[:rows, tt, :], in0=gv[i][:rows, :], scalar=vst[i][:rows, 2:3], op0=ALU.mult,
                                                                                in1=vng_bc[:rows, :], op1=ALU.mult),
                 r=[("gv", i, 0), ("gv", i, 1), ("vst", i), "vng"], w=[("v_tok", tt)])
            if tt == 8:
                P.op("dve", lambda e, i=i: e.scalar_tensor_tensor(out=vsf[:NS, :], in0=gv[i][:NS, :], scalar=vst[i][:NS, 2:3], op0=ALU.mult,
                                                                  in1=vng_bc[:NS, :], op1=ALU.mult), r=[("gv", i, 0), ("gv", i, 1), ("vst", i), "vng"], w=["vsf", ("tmpm", 0), ("tmpm", 1)])
                P.op("sp", lambda e: e.dma_start(out=v_s[:, :], in_=vsf[:NS, :]), r=["vsf", ("tmpm", 0), ("tmpm", 1)], w=["o_vs"], dma=True)
                OUTKEYS.append("o_vs")
        wu2 = [wload(wview(w_in, COL_U, 512)), wload(wview(w_in, COL_U + 512, 512))]
        tok_blocks = [(0, 512), (512, 1024), (TM, T)]
        grpc = [0]

        def stage_G(h_):
            ub, jj = h_ // 4, h_ % 4
            Wv, kWu = wu2[ub]
            ui = h_ % 2
            for nb, (lo, hi) in enumerate(tok_blocks):
                bank = grpc[0] % 2
                grpc[0] += 1
                n = hi - lo
                rk = [("nT", tt) for tt in range(nb * 4, nb * 4 + 4)] if nb < 2 else [("nT", 8)]
                mm_group(pbf[bank][:, 0:n], [(Wv[:, k, jj * 128:(jj + 1) * 128], nT[:, k, lo:hi]) for k in range(16)], r=[kWu] + rk, w=[("ps", bank)])
                P.op("act", lambda e, ui=ui, lo=lo, hi=hi, n=n, bank=bank: e.activation(out=ug[ui][:, lo:hi], in_=pbf[bank][:, 0:n], func=AF.Gelu_apprx_tanh),
                     r=[("ps", bank)], w=[("ug", ui)])

        def stage_M(h_):
            ui = h_ % 2
            for half in range(2):
                def mix(e, half=half, h_=h_):
                    ins = None
                    for cc in range(4):
                        c = half * 4 + cc
                        ins = e.matmul(pbf[2 + half][:, cc * 128:(cc + 1) * 128], lhsT=v_tok[:, c, h_ * 128:(h_ + 1) * 128], rhs=WsT[:, h_, :], start=True, stop=True)
                    return ins
                P.op("pe", mix, r=[("v_tok", c) for c in range(half * 4, half * 4 + 4)] + ["WsT"], w=[("ps", 2 + half)])
                tm = tmpm[half]
                P.op("dve", lambda e, half=half, h_=h_, tm=tm: e.tensor_tensor(out=tm.rearrange("p (a b) -> p a b", a=4), in0=pbf[2 + half].rearrange("p (a b) -> p a b", a=4),
                                                                               in1=bs_bc[:, h_:h_ + 1, :].broadcast_to([128, 4, 128]), op=ALU.add),
                     r=[("ps", 2 + half), "bs_bc"], w=[("tmpm", half)])
                P.op("dve", lambda e, half=half, h_=h_, tm=tm, ui=ui: e.tensor_tensor(out=YT[:, h_, half * 512:(half + 1) * 512], in0=tm, in1=ug[ui][:, half * 512:(half + 1) * 512], op=ALU.mult),
                     r=[("tmpm", half), ("ug", ui)], w=[("ya", h_)])
            P.op("pe", lambda e, h_=h_: e.matmul(pbf[4][:, 0:NS], lhsT=v_tok[:NS, 8, h_ * 128:(h_ + 1) * 128], rhs=w00I[:NS, h_, :], start=True, stop=True),
                 r=[("v_tok", 8), "w00I"], w=[("ps", 4)])
            P.op("dve", lambda e, h_=h_, ui=ui: e.scalar_tensor_tensor(out=YT[:, h_, TM:T], in0=pbf[4][:, 0:NS], scalar=b0_bc[:, h_:h_ + 1], op0=ALU.add, in1=ug[ui][:, TM:T], op1=ALU.mult),
                 r=[("ps", 4), ("ug", ui), KC], w=[("ya", h_)])
            si = h_ % 2
            if h_ == 0:
                P.op("act", lambda e, h_=h_: e.activation(out=sqacc, in_=YT[:, h_, :], func=AF.Square), r=[("ya", h_)], w=["sqacc"])
            else:
                P.op("act", lambda e, h_=h_, si=si: e.activation(out=sq[si], in_=YT[:, h_, :], func=AF.Square), r=[("ya", h_)], w=[("sq", si)])
                P.op("dve", lambda e, si=si: e.tensor_tensor(out=sqacc, in0=sqacc, in1=sq[si], op=ALU.add), r=["sqacc", ("sq", si)], w=["sqacc"])

        stage_G(0)
        for h_ in range(8):
            if h_ + 1 < 8:
                stage_G(h_ + 1)
            stage_M(h_)
        P.op("dve", lambda e: e.tensor_copy(out=sq[0], in_=sqacc), r=["sqacc", ("sq", 0)], w=[("sq", 0)])
        sbank = (0, 1, 4)
        for nb, (lo, hi) in enumerate(tok_blocks):
            n = hi - lo
            P.op("pe", lambda e, nb=nb, lo=lo, hi=hi, n=n: e.matmul(pbf[sbank[nb]][:, 0:n], lhsT=onesB, rhs=sq[0][:, lo:hi], start=True, stop=True),
                 r=[("sq", 0), "onesB"], w=[("ps", sbank[nb])])
        for nb, (lo, hi) in enumerate(tok_blocks):
            n = hi - lo
            P.op("act", lambda e, nb=nb, lo=lo, hi=hi, n=n: e.activation(out=rstd_bc[:, lo:hi], in_=pbf[sbank[nb]][:, 0:n], func=AF.Ln, scale=1.0 / 1024, bias=EPS),
                 r=[("ps", sbank[nb])], w=[("rstd", nb)])
            P.op("act", lambda e, lo=lo, hi=hi: e.activation(out=rstd_bc[:, lo:hi], in_=rstd_bc[:, lo:hi], func=AF.Exp, scale=-0.5), r=[("rstd", nb)], w=[("rstd", nb)])
        for h_ in range(8):
            P.op("dve", lambda e, h_=h_: e.scalar_tensor_tensor(out=YT[:, h_, :], in0=YT[:, h_, :], scalar=gon_pp[:, h_:h_ + 1], op0=ALU.mult, in1=rstd_bc, op1=ALU.mult),
                 r=[("ya", h_), ("rstd", 0), ("rstd", 1), ("rstd", 2), KC], w=[("YTa", h_)])
        _a3_ops = P.ops
        P.ops = _saved_a3
        P.ops.extend(Prog.merge(_a3_ops, smp_ops, 0, int(len(_a3_ops) * 0.8)))
        sc.end = AW
        P.barrier()
        if STOP_AFTER == "A3":
            dbg_dump("YT", YT.rearrange("p a b -> p (a b)"), [("YTa", h_) for h_ in range(8)])
            finish()
            return nc

        sc.reset()
        hres = sc.f32([9, DM])
        m_B = sc.mark()
        for tt in range(9):
            rows = 128 if tt < 8 else NS
            src = xm[tt * 128:(tt + 1) * 128, :] if tt < 8 else xsmp[:, :]
            P.op("sp", lambda e, tt=tt, rows=rows, src=src: e.dma_start(out=hres[:rows, tt, :], in_=src), w=[("h", tt, cb) for cb in range(4)], dma=True)
        wcur = wload(wview(w_out, 0, 512))
        for cb in range(4):
            Wv, kW = wcur
            if cb + 1 < 4:
                wcur = wload(wview(w_out, (cb + 1) * 512, 512))
            for tt in range(9):
                rows = 128 if tt < 8 else NS
                tcols = slice(tt * 128, tt * 128 + rows)
                bank = (cb * 9 + tt) % 6
                mm_group(pbf[bank][:rows, :], [(YT[:, k, tcols], Wv[:, k, :]) for k in range(16)], r=[kW], w=[("ps", bank)])
                P.op("dve", lambda e, rows=rows, tt=tt, cb=cb, bank=bank: e.tensor_tensor(out=hres[:rows, tt, cb * 512:(cb + 1) * 512], in0=pbf[bank][:rows, :],
                                                                                         in1=hres[:rows, tt, cb * 512:(cb + 1) * 512], op=ALU.add),
                     r=[("ps", bank), ("h", tt, cb)], w=[("h", tt, cb)])
        P.barrier()
        if STOP_AFTER == "B":
            dbg_dump("h", hres.rearrange("p a b -> p (a b)"), [("h", tt) for tt in range(9)])
            finish()
            return nc

        sc.reset(m_B)
        gbc = sc.f32([DM])
        stmp = [sc.f32([512]) for _ in range(2)]
        ST = [sc.f32([4]) for _ in range(2)]
        r2.reset()
        actb = [r2.bf16([2, T]) for _ in range(2)]
        Wd = [r2.bf16([2, DM]) for _ in range(2)]
        XN = [r2.bf16([DM]) for _ in range(2)]
        mT = nT
        P.op("sp", lambda e: e.dma_start(out=gbc, in_=g_ffn.partition_broadcast(128)), w=["gbc"], dma=True)
        stgC = []
        for tt in range(9):
            rows = 128 if tt < 8 else NS
            i = tt % 2
            stgC.append(rms_transpose(hres[:, tt, :], [("h", tt, cb) for cb in range(4)], rows, XN[i], ("XN", i), ST[i], ("ST", i), gbc, "gbc", mT, tt * 128, ("mT", tt), DM, tt))
        rms_pipeline(stgC)
        NFB = DFF // 256
        if STOP_AFTER == "C0":
            P.barrier()
            dbg_dump("mT", mT.rearrange("p a b -> p (a b)"), [("mT", tt) for tt in range(9)])
            dbg_dump("XN0", XN[0], [("XN", 0)])
            dbg_dump("ST0", ST[0], [("ST", 0)])
            dbg_dump("gbc", gbc, ["gbc"])
            dbg_dump("h", hres.rearrange("p a b -> p (a b)"), [("h", tt) for tt in range(9)])
            finish()
            return nc

        gu_slot = {}

        def load_gu(fb):
            slot = wq[0] % 2
            wq[0] += 1
            gu_slot[fb] = slot
            load_w(WG[slot], wview(w_gate, fb * 256, 256), ("W", slot), nobar=True)
            load_w(WU[slot], wview(w_up, fb * 256, 256), ("W", slot), nobar=True)

        def load_d(fb):
            load_w(Wd[fb % 2], w_down[fb * 256:(fb + 1) * 256, :].rearrange("(fl p) c -> p fl c", p=128), ("Wd", fb % 2))

        gctr = [0]

        ffn_blocks = [(0, 348), (348, 696), (696, T)]

        def GU_units(fb):
            slot = gu_slot[fb]
            Wg_, Wu_, kWs = WG[slot], WU[slot], ("W", slot)
            units = []
            for fl in range(2):
                for nb, (lo, hi) in enumerate(ffn_blocks):
                    def unit(fl=fl, nb=nb, lo=lo, hi=hi):
                        n = hi - lo
                        pr = gctr[0] % 2
                        gctr[0] += 1
                        bg, bu = pr * 2, pr * 2 + 1
                        rk = [("mT", tt) for tt in range(lo // 128, min(8, (hi - 1) // 128) + 1)]
                        mm_group(pbf[bg][:, 0:n], [(Wg_[:, k, fl * 128:(fl + 1) * 128], mT[:, k, lo:hi]) for k in range(16)], r=[kWs] + rk, w=[("ps", bg)])
                        mm_group(pbf[bu][:, 0:n], [(Wu_[:, k, fl * 128:(fl + 1) * 128], mT[:, k, lo:hi]) for k in range(16)], r=[kWs] + rk, w=[("ps", bu)])
                        P.op("act", lambda e: e.activation(out=stmp[pr][:, 0:n], in_=pbf[bg][:, 0:n], func=AF.Silu), r=[("ps", bg)], w=[("stmp", pr)])
                        P.op("dve", lambda e: e.tensor_tensor(out=actb[fb % 2][:, fl, lo:hi], in0=pbf[bu][:, 0:n], in1=stmp[pr][:, 0:n], op=ALU.mult),
                             r=[("ps", bu), ("stmp", pr)], w=[("act", fb % 2)])
                    units.append(P.capture(unit))
            return units

        dctr = [0]

        def DN_groups(fb):
            groups = []
            for tt in range(9):
                rows = 128 if tt < 8 else NS
                tcols = slice(tt * 128, tt * 128 + rows)
                for cb in range(4):
                    def grp_(tt=tt, rows=rows, tcols=tcols, cb=cb):
                        bank = 4 + dctr[0] % 4
                        dctr[0] += 1
                        mm_group(pbf[bank][:rows, :], [(actb[fb % 2][:, fl, tcols], Wd[fb % 2][:, fl, cb * 512:(cb + 1) * 512]) for fl in range(2)],
                                 r=[("act", fb % 2), ("Wd", fb % 2)], w=[("ps", bank)])
                        P.op("dve", lambda e: e.tensor_tensor(out=hres[:rows, tt, cb * 512:(cb + 1) * 512], in0=pbf[bank][:rows, :],
                                                              in1=hres[:rows, tt, cb * 512:(cb + 1) * 512], op=ALU.add),
                             r=[("ps", bank), ("h", tt, cb)], w=[("h", tt, cb)])
                    groups.append(P.capture(grp_))
            return groups

        load_gu(0)
        load_d(0)
        for i in range(NFB + 1):
            if i + 1 < NFB:
                load_gu(i + 1)
            gu = GU_units(i) if i < NFB else []
            dn = DN_groups(i - 1) if i >= 1 else []
            nslot = max(len(gu), 1)
            per = -(-len(dn) // nslot)
            for u in range(nslot):
                if u < len(gu):
                    P.ops.extend(gu[u])
                for g_ in dn[u * per:(u + 1) * per]:
                    P.ops.extend(g_)
            if i + 1 < NFB:
                load_d(i + 1)
        P.barrier()
        if STOP_AFTER == "C":
            dbg_dump("h", hres.rearrange("p a b -> p (a b)"), [("h", tt) for tt in range(9)])
            finish()
            return nc

        P.op("sp", lambda e: e.dma_start(out=gbc, in_=g_fin.partition_broadcast(128)), w=["gbc"], dma=True)
        for tt in range(9):
            rows = 128 if tt < 8 else NS
            i = tt % 2
            ht = hres[:, tt, :]
            st = ST[i]
            P.op("act", lambda e, rows=rows, ht=ht, st=st, i=i: e.activation(out=XN[i][:rows, :], in_=ht[:rows, :], func=AF.Square, accum_out=st[:rows, 0:1]),
                 r=[("h", tt, cb) for cb in range(4)], w=[("XN", i), ("ST", i)])
            P.op("act", lambda e, rows=rows, st=st: e.activation(out=st[:rows, 1:2], in_=st[:rows, 0:1], func=AF.Ln, scale=1.0 / DM, bias=EPS), r=[("ST", i)], w=[("ST", i)])
            P.op("act", lambda e, rows=rows, st=st: e.activation(out=st[:rows, 2:3], in_=st[:rows, 1:2], func=AF.Exp, scale=-0.5), r=[("ST", i)], w=[("ST", i)])
            P.op("dve", lambda e, rows=rows, ht=ht, st=st: e.scalar_tensor_tensor(out=ht[:rows, :], in0=ht[:rows, :], scalar=st[:rows, 2:3], op0=ALU.mult, in1=gbc[:rows, :], op1=ALU.mult),
                 r=[("h", tt, cb) for cb in range(4)] + [("ST", i), "gbc"], w=[("h", tt, cb) for cb in range(4)])
            dst = y_m[tt * 128:(tt + 1) * 128, :] if tt < 8 else y_s[:, :]
            ok = ("o_y", tt)
            P.op("sp", lambda e, rows=rows, ht=ht, dst=dst: e.dma_start(out=dst, in_=ht[:rows, :]), r=[("h", tt, cb) for cb in range(4)], w=[ok], dma=True)
            OUTKEYS.append(ok)
        finish()
        return nc


def _host_consts(inputs, core):
    hf = core % 2
    cst = np.zeros((128, CSTW), np.float32)
    r = np.arange(128)
    cst[:, C_ID:C_ID + 128] = np.eye(128, dtype=np.float32)
    cst[:, C_TRI:C_TRI + 128] = (r[:, None] <= r[None, :]).astype(np.float32)
    cst[:, C_U:C_U + 128] = (r[:, None] > r[None, :]).astype(np.float32)
    cst[:, C_ONE:C_ONE + 128] = 1.0
    cw = np.asarray(inputs["ssd_conv_w"])[0]
    cb = np.asarray(inputs["ssd_conv_b"])[0]
    cwb = np.concatenate([cw, cb[None]], 0)
    cst[:, C_CWB:C_CWB + 60] = cwb.reshape(5, 12, 128).transpose(2, 1, 0).reshape(128, 60)
    cst[:, C_GON:C_GON + 8] = np.asarray(inputs["chunk_out_norm_g"])[0].reshape(8, 128).T
    cst[:, C_ALOG:C_ALOG + 16] = np.asarray(inputs["ssd_a_log"])[0][None, :]
    cst[:, C_DTB:C_DTB + 16] = np.asarray(inputs["ssd_dt_bias"])[0][None, :]
    cst[:, C_D:C_D + 16] = np.asarray(inputs["ssd_d"])[0][None, :]
    cst[:, C_W00:C_W00 + 8] = np.asarray(inputs["chunk_w_s"])[0][:, 0, 0][None, :]
    cst[:, C_B0:C_B0 + 8] = np.asarray(inputs["chunk_b_s"])[0][:, 0][None, :]
    cst[:, C_FLAG] = float(hf)
    cst[:, C_DPP:C_DPP + 8] = np.repeat(np.asarray(inputs["ssd_d"])[0], 64).reshape(8, 128).T
    cst[:, C_SGPP:C_SGPP + 8] = np.asarray(inputs["ssd_norm_g"])[0].reshape(8, 128).T
    cst[0:16, C_DTBPP] = np.asarray(inputs["ssd_dt_bias"])[0]
    cst[0:16, C_ALPP] = np.asarray(inputs["ssd_a_log"])[0]
    cst2 = np.zeros((128, CST2W), np.float32)
    cst2[:, C2_BS:C2_BS + 1024] = np.asarray(inputs["chunk_b_s"])[0].reshape(1, 1024)
    rs = np.zeros((16, 8, 128), np.float32)
    for q in range(8):
        for rr in range(128):
            rs[2 * q + rr // 64, q, rr] = 1.0
    cst2[0:16, C2_RSEL:C2_RSEL + 1024] = rs.reshape(16, 1024)
    return cst, cst2


def make_in_maps(inputs):
    xp = np.asarray(inputs["x_prompt"], np.float32)
    xs = np.asarray(inputs["x_sample"], np.float32)
    sconv = np.asarray(inputs["state_conv"], np.float32)[0]
    sssm = np.asarray(inputs["state_ssm"], np.float32)[0]
    shared = {
        "w_in": np.ascontiguousarray(np.asarray(inputs["w_in"], np.float32)[0]),
        "w_out": np.ascontiguousarray(np.asarray(inputs["w_out"], np.float32)[0]),
        "w_gate": np.ascontiguousarray(np.asarray(inputs["w_gate"], np.float32)[0]),
        "w_up": np.ascontiguousarray(np.asarray(inputs["w_up"], np.float32)[0]),
        "w_down": np.ascontiguousarray(np.asarray(inputs["w_down"], np.float32)[0]),
        "ws": np.ascontiguousarray(np.asarray(inputs["chunk_w_s"], np.float32)[0]),
        "g_mix": np.ascontiguousarray(np.asarray(inputs["norm_mix_g"], np.float32)[0]),
        "g_ffn": np.ascontiguousarray(np.asarray(inputs["norm_ffn_g"], np.float32)[0]),
        "g_fin": np.ascontiguousarray(np.asarray(inputs["norm_final_g"], np.float32)),
        "vn_g": np.ascontiguousarray(np.asarray(inputs["chunk_v_norm_g"], np.float32)[0]),
        "ssd_g": np.ascontiguousarray(np.asarray(inputs["ssd_norm_g"], np.float32)[0]),
    }
    zeros_prev = np.zeros((TM, DM), np.float32)
    maps = []
    for c in range(8):
        b, hf = c // 2, c % 2
        cst, cst2 = _host_consts(inputs, c)
        m = dict(shared)
        m["xm"] = np.ascontiguousarray(xp[b, hf * TM:(hf + 1) * TM])
        m["xprev"] = np.ascontiguousarray(xp[b, 0:TM]) if hf == 1 else zeros_prev
        m["xsmp"] = np.ascontiguousarray(xs[c * NS:(c + 1) * NS, 0])
        m["sconv"] = np.ascontiguousarray(sconv[c * NS:(c + 1) * NS])
        m["sssm"] = np.ascontiguousarray(sssm[c * NS:(c + 1) * NS].reshape(NS, 1024, 128))
        m["cst"] = cst
        m["cst2"] = cst2
        maps.append(m)
    return maps


def kernel(**inputs):
    nc = build_program()
    maps = make_in_maps(inputs)
    res = run_bass_kernel_spmd(nc, maps, core_ids=list(range(8)))
    R = res.results
    y_prompt = np.zeros((4, 2048, DM), np.float32)
    y_sample = np.zeros((128, 1, DM), np.float32)
    ncp = np.zeros((1, 4, 3, 1536), np.float32)
    nsp = np.zeros((1, 4, 16, 64, 128), np.float32)
    ncs = np.zeros((1, 128, 3, 1536), np.float32)
    nss = np.zeros((1, 128, 16, 64, 128), np.float32)
    vs = np.zeros((1, 128, 1, 1024), np.float32)
    for c in range(8):
        b, hf = c // 2, c % 2
        y_prompt[b, hf * TM:(hf + 1) * TM] = R[c]["y_m"]
        y_sample[c * NS:(c + 1) * NS, 0] = R[c]["y_s"]
        if hf == 1:
            ncp[0, b] = R[c]["nconv_p"]
            nsp[0, b] = R[c]["nssm_p"].reshape(16, 64, 128)
        ncs[0, c * NS:(c + 1) * NS] = R[c]["nconv_s"]
        nss[0, c * NS:(c + 1) * NS] = R[c]["nssm_s"].reshape(NS, 16, 64, 128)
        vs[0, c * NS:(c + 1) * NS, 0] = R[c]["v_s"]
    return (y_prompt, y_sample, ncp, nsp, ncs, nss, vs)
```

```python
import numpy as np
from contextlib import ExitStack
import concourse.bass as bass
import concourse.mybir as mybir
from concourse.bass_utils import run_bass_kernel_spmd

F32 = mybir.dt.float32
F32R = mybir.dt.float32r
BF16 = mybir.dt.bfloat16
AF = mybir.ActivationFunctionType
ALU = mybir.AluOpType
AX = mybir.AxisListType

ENGS = ["pe", "act", "dve", "pool", "sp"]
NDMASEM = 12

DEBUG = {}
STOP_AFTER = None
SSD_LEVEL = 9


class Op:
    __slots__ = ("eng", "fn", "r", "w", "dma", "deps", "sig", "sem", "prev_use", "waits", "need_sig", "xdeps", "nobar")

    def __init__(self, eng, fn, r, w, dma, nobar=False):
        self.eng, self.fn, self.r, self.w, self.dma = eng, fn, tuple(r), tuple(w), dma
        self.deps = set()
        self.xdeps = set()
        self.sig = None
        self.sem = None
        self.prev_use = 0
        self.waits = []
        self.need_sig = False
        self.nobar = nobar


BARRIER = Op(None, None, (), (), False)


class Prog:
    def __init__(self):
        self.ops = []

    def op(self, eng, fn, r=(), w=(), dma=False, nobar=False):
        w = list(w) + [k for k in r if isinstance(k, tuple) and k and k[0] == "ps" and k not in w]
        o = Op(eng, fn, r, w, dma, nobar)
        self.ops.append(o)
        return o

    def barrier(self):
        self.ops.append(BARRIER)

    def capture(self, fn):
        saved = self.ops
        self.ops = []
        try:
            fn()
            got = self.ops
        finally:
            self.ops = saved
        return got

    @staticmethod
    def merge(main, side, lo=0, hi=None):
        if not side:
            return list(main)
        hi = len(main) if hi is None else hi
        out = []
        s_ = len(side)
        span = max(hi - lo, 1)
        j = 0
        for i, o in enumerate(main):
            out.append(o)
            if i >= lo:
                tgt = min(s_, (i + 1 - lo) * s_ // span)
                while j < tgt:
                    out.append(side[j])
                    j += 1
        out.extend(side[j:])
        return out

    def resolve(self):
        ops = self.ops
        last_w = {}
        readers = {}
        last_eng = {}
        dmas = []
        pending = {}
        for i, o in enumerate(ops):
            if o is BARRIER:
                s = set(last_eng.values()) | set(dmas)
                for e in ENGS:
                    pending[e] = set(s) | pending.get(e, set())
                continue
            if o.eng in pending and not o.nobar:
                o.xdeps = o.xdeps | pending.pop(o.eng)
            deps = set(o.xdeps)
            for k in o.r:
                if k in last_w:
                    deps.add(last_w[k])
            for k in o.w:
                if k in last_w:
                    deps.add(last_w[k])
                deps.update(readers.get(k, ()))
            deps.discard(i)
            keep = set()
            for d in deps:
                od = ops[d]
                if od.dma:
                    keep.add(d)
                elif od.eng == o.eng and not o.dma:
                    if o.eng == "pe":
                        continue
                    if d in o.xdeps:
                        continue
                    if set(od.w) & set(o.r):
                        keep.add(d)
                else:
                    keep.add(d)
            o.deps = keep
            for d in keep:
                ops[d].need_sig = True
            for k in o.r:
                readers.setdefault(k, []).append(i)
            for k in o.w:
                last_w[k] = i
                readers[k] = []
            if o.dma:
                dmas.append(i)
            elif o.fn is not None:
                last_eng[o.eng] = i
        cnt = {e: 0 for e in ENGS}
        dma_n = {e: 0 for e in ENGS}
        dma_use = {}
        for i, o in enumerate(ops):
            if o is BARRIER:
                continue
            if o.dma:
                slot = (o.eng, dma_n[o.eng] % NDMASEM)
                dma_n[o.eng] += 1
                u = dma_use.get(slot, 0)
                o.prev_use = u
                dma_use[slot] = u + 1
                o.sem = slot
                o.sig = 16 * (u + 1)
            elif o.need_sig:
                cnt[o.eng] += 1
                o.sem = o.eng
                o.sig = cnt[o.eng]
        seen = {e: {} for e in ENGS}
        for i, o in enumerate(ops):
            if o is BARRIER:
                continue
            sd = seen[o.eng]
            need = {}
            if o.dma and o.prev_use > 0:
                need[o.sem] = 16 * o.prev_use
            for d in o.deps:
                od = ops[d]
                need[od.sem] = max(need.get(od.sem, 0), od.sig)
            o.waits = []
            for s, v in need.items():
                if sd.get(s, 0) >= v:
                    continue
                sd[s] = v
                o.waits.append((s, v))

    def emit(self, sems, block):
        ops = self.ops

        def run(eng_name):
            def body(e):
                for o in ops:
                    if o is BARRIER or o.eng != eng_name:
                        continue
                    for s, v in o.waits:
                        e.wait_ge(sems[s], v)
                    if o.fn is None:
                        continue
                    ins = o.fn(e)
                    if o.dma:
                        ins.then_inc(sems[o.sem], 16)
                    elif o.sig is not None:
                        ins.then_inc(sems[o.sem], 1)
            return body

        block.sync(run("sp"))
        block.tensor(run("pe"))
        block.scalar(run("act"))
        block.vector(run("dve"))
        block.gpsimd(run("pool"))


DM = 2048
DIN = 4624
DFF = 5632
TM = 1024
NS = 16
T = TM + NS
EPS = 1e-6
COL_U, COL_V, COL_Z, COL_X, COL_B, COL_C, COL_DT = 0, 1024, 2048, 3072, 4096, 4352, 4608

C_ID, C_TRI, C_U, C_ONE = 0, 128, 256, 384
C_CWB, C_GON, C_ALOG, C_DTB, C_D, C_W00, C_B0, C_FLAG = 512, 572, 580, 596, 612, 628, 636, 644
C_DPP, C_SGPP, C_DTBPP, C_ALPP = 648, 656, 664, 665
CSTW = 672
C2_BS, C2_RSEL = 0, 1024
CST2W = 2048

AW = 51000


class Bump:
    def __init__(self, arena, start, end):
        self.arena, self.start, self.end, self.off = arena, start, end, start
        self.peak = start

    def reset(self, to=None):
        self.off = self.start if to is None else to

    def mark(self):
        return self.off

    def _take(self, words):
        o = self.off
        self.off += words
        assert self.off <= self.end, ("arena overflow", self.off, self.end)
        self.peak = max(self.peak, self.off)
        return o

    def f32(self, shape):
        n = int(np.prod(shape))
        o = self._take(n)
        v = self.arena[:, o:o + n]
        return _shape(v, shape)

    def bf16(self, shape):
        n = int(np.prod(shape))
        w = (n + 1) // 2
        o = self._take(w)
        v = self.arena[:, o:o + w].bitcast(BF16)[:, 0:n]
        return _shape(v, shape)


def _shape(v, shape):
    if len(shape) == 1:
        return v
    if len(shape) == 2:
        return v.rearrange("p (a b) -> p a b", a=shape[0])
    if len(shape) == 3:
        return v.rearrange("p (a b c) -> p a b c", a=shape[0], b=shape[1])
    raise ValueError(shape)


def build_program():
    nc = bass.Bass("TRN2", target_bir_lowering=False)

    def din(name, shape):
        return nc.dram_tensor(name, shape, F32, kind="ExternalInput").ap()

    def dout(name, shape):
        return nc.dram_tensor(name, shape, F32, kind="ExternalOutput").ap()

    xm = din("xm", [TM, DM])
    xprev = din("xprev", [TM, DM])
    xsmp = din("xsmp", [NS, DM])
    sconv = din("sconv", [NS, 3, 1536])
    sssm = din("sssm", [NS, 1024, 128])
    w_in = din("w_in", [DM, DIN])
    w_out = din("w_out", [DM, DM])
    w_gate = din("w_gate", [DM, DFF])
    w_up = din("w_up", [DM, DFF])
    w_down = din("w_down", [DFF, DM])
    cst_d = din("cst", [128, CSTW])
    cst2_d = din("cst2", [128, CST2W])
    ws_d = din("ws", [8, 128, 128])
    g_mix = din("g_mix", [DM])
    g_ffn = din("g_ffn", [DM])
    g_fin = din("g_fin", [DM])
    vn_g = din("vn_g", [1024])
    ssd_g = din("ssd_g", [1024])

    y_m = dout("y_m", [TM, DM])
    y_s = dout("y_s", [NS, DM])
    nconv_p = dout("nconv_p", [3, 1536])
    nssm_p = dout("nssm_p", [1024, 128])
    nconv_s = dout("nconv_s", [NS, 3, 1536])
    nssm_s = dout("nssm_s", [NS, 1024, 128])
    v_s = dout("v_s", [NS, 1024])
    dbg_d = {}
    for name, (shape, dty) in DEBUG.items():
        dbg_d[name] = nc.dram_tensor("dbg_" + name, list(shape), dty, kind="ExternalOutput").ap()

    P = Prog()
    with ExitStack() as es:
        arena = es.enter_context(nc.sbuf_tensor("arena", [128, AW], F32))
        rhsE = es.enter_context(nc.sbuf_tensor("rhsE", [128, 2048], F32))
        U32 = es.enter_context(nc.sbuf_tensor("U32", [128, 128], F32))
        pb = [es.enter_context(nc.psum_tensor(f"pb{i}", [128, 512], F32)) for i in range(8)]
        sems = {}
        for e in ENGS:
            sems[e] = es.enter_context(nc.semaphore("s_" + e))
        for e in ("sp", "pool"):
            for i in range(NDMASEM):
                sems[(e, i)] = es.enter_context(nc.semaphore(f"d_{e}_{i}"))
        block = es.enter_context(nc.Block())

        arena = arena[:, :]
        rhsEr = rhsE[:, :].bitcast(F32R)
        U32r = U32[:, :].bitcast(F32R)
        pbf = [p[:, :] for p in pb]
        pbb = [p[:, :].bitcast(BF16) for p in pb]

        fx = Bump(arena, 0, AW)
        CST = fx.f32([CSTW])
        identF = CST[:, C_ID:C_ID + 128]
        triF = CST[:, C_TRI:C_TRI + 128]
        UF = CST[:, C_U:C_U + 128]
        onesF = CST[:, C_ONE:C_ONE + 128]
        cwb = CST[:, C_CWB:C_CWB + 60].rearrange("p (j k) -> p j k", j=12)
        gon_pp = CST[:, C_GON:C_GON + 8]
        alog_bc = CST[:, C_ALOG:C_ALOG + 16]
        dtb_bc = CST[:, C_DTB:C_DTB + 16]
        D_bc = CST[:, C_D:C_D + 16]
        w00_bc = CST[:, C_W00:C_W00 + 8]
        b0_bc = CST[:, C_B0:C_B0 + 8]
        flag = CST[:, C_FLAG:C_FLAG + 1]
        D_pp = CST[:, C_DPP:C_DPP + 8]
        sg_pp = CST[:, C_SGPP:C_SGPP + 8]
        dtb_pp = CST[:, C_DTBPP:C_DTBPP + 1]
        al_pp = CST[:, C_ALPP:C_ALPP + 1]
        identB = fx.bf16([128])
        onesB = fx.bf16([128])
        aneg = fx.f32([16])
        hT = fx.f32([1024])
        hTb = fx.bf16([1024])
        prefix = fx.f32([12, 3])
        ncv = fx.f32([12, 3])
        nT = fx.bf16([16, T])
        YT = fx.bf16([16, T])
        r2_start = fx.off - (16 * T) // 2
        r2_end = fx.off
        xT_s = fx.bf16([12, NS])
        zT_s = fx.f32([8, NS])
        dtT_s = fx.f32([NS])
        Rsel = fx.f32([1024])
        Wfix = [fx.bf16([16, 512]) for _ in range(2)]
        _wflat = [w_.rearrange("p a b -> p (a b)") for w_ in Wfix]
        WG = [w_[:, 0:4096].rearrange("p (a b) -> p a b", a=16) for w_ in _wflat]
        WU = [w_[:, 4096:8192].rearrange("p (a b) -> p a b", a=16) for w_ in _wflat]
        wq = [0]
        S0 = fx.off
        sc = Bump(arena, S0, AW)
        r2 = Bump(arena, r2_start, r2_end)

        KC = ("cst",)

        P.op("sp", lambda e: e.dma_start(out=CST, in_=cst_d[:, :]), w=[KC], dma=True)
        P.op("dve", lambda e: e.tensor_copy(out=identB, in_=identF), r=[KC], w=["identB"])
        P.op("dve", lambda e: e.memset(onesB, 1.0), w=["onesB"])
        P.op("dve", lambda e: e.tensor_copy(out=U32r, in_=UF), r=[KC], w=["U32"])
        P.op("act", lambda e: e.activation(out=aneg, in_=alog_bc, func=AF.Exp), r=[KC], w=["aneg0"])
        P.op("dve", lambda e: e.tensor_scalar(out=aneg, in0=aneg, scalar1=-1.0, scalar2=None, op0=ALU.mult), r=["aneg0"], w=["aneg"])
        P.op("dve", lambda e: e.memset(hT, 0.0), w=["hT"])
        P.op("dve", lambda e: e.memset(hTb, 0.0), w=["hTb"])

        def rms_transpose(xt, kx, rows, xn, kxn, st, kst, gbc, kg, dstT, col0, kdst, width, tag, pre=None):
            nk = width // 128

            def s1():
                if pre is not None:
                    pre()
                P.op("act", lambda e: e.activation(out=xn[:rows, :], in_=xt[:rows, :], func=AF.Square, accum_out=st[:rows, 0:1]),
                     r=[kx] if not isinstance(kx, list) else kx, w=[kxn, kst])
                P.op("act", lambda e: e.activation(out=st[:rows, 1:2], in_=st[:rows, 0:1], func=AF.Ln, scale=1.0 / width, bias=EPS),
                     r=[kst], w=[kst])
                P.op("act", lambda e: e.activation(out=st[:rows, 2:3], in_=st[:rows, 1:2], func=AF.Exp, scale=-0.5),
                     r=[kst], w=[kst])
                P.op("dve", lambda e: e.scalar_tensor_tensor(out=xn[:rows, :], in0=xt[:rows, :], scalar=st[:rows, 2:3], op0=ALU.mult,
                                                             in1=gbc[:rows, :], op1=ALU.mult),
                     r=([kx] if not isinstance(kx, list) else kx) + [kst, kg, kxn], w=[kxn])

            def s2():
                for b8 in range(nk // 8):
                    bank = (tag % 2) * 2 + b8
                    psv = pbb[bank][:, 0:8 * rows].rearrange("p (a b) -> p a b", a=8)

                    def tr(e, b8=b8, psv=psv):
                        ins = None
                        for j in range(8):
                            k = b8 * 8 + j
                            ins = e.transpose(out=psv[:, j, :], in_=xn[:rows, k * 128:(k + 1) * 128], identity=identB[:rows, :rows])
                        return ins
                    P.op("pe", tr, r=[kxn, "identB"], w=[("ps", bank)])
                    dst = dstT[:, b8 * 8:(b8 + 1) * 8, col0:col0 + rows]
                    if b8 == 0:
                        P.op("act", lambda e, dst=dst, psv=psv: e.activation(out=dst, in_=psv, func=AF.Copy), r=[("ps", bank)], w=[kdst])
                    else:
                        P.op("dve", lambda e, dst=dst, psv=psv: e.tensor_copy(out=dst, in_=psv), r=[("ps", bank)], w=[kdst])
            return P.capture(s1), P.capture(s2)

        def rms_pipeline(stages):
            n = len(stages)
            for i in range(n + 1):
                if i < n:
                    P.ops.extend(stages[i][0])
                if i >= 1:
                    P.ops.extend(stages[i - 1][1])

        def mm_group(out, pairs, r, w):
            def fn(e):
                ins = None
                n = len(pairs)
                for i, (l, rh) in enumerate(pairs):
                    ins = e.matmul(out, lhsT=l, rhs=rh, start=(i == 0), stop=(i == n - 1))
                return ins
            P.op("pe", fn, r=r, w=w)

        def load_w(dst, src_ap, key, nobar=False):
            P.op("pool", lambda e: e.dma_start(out=dst, in_=src_ap), w=[key], dma=True, nobar=nobar)

        def wload(src_ap, ncols=512):
            slot = wq[0] % 2
            wq[0] += 1
            buf = Wfix[slot] if ncols == 512 else Wfix[slot][:, :, 0:ncols]
            load_w(buf, src_ap, ("W", slot), nobar=True)
            return buf, ("W", slot)

        def wview(wap, c0, ncols):
            return wap[:, c0:c0 + ncols].rearrange("(kt p) c -> p kt c", p=128)

        def conv_silu(rawpad, krp, acc, kacc, j, ntok, dst, kdst, defer=None):
            for k in (0, 1, 2):
                P.op("dve", lambda e, k=k: e.scalar_tensor_tensor(out=acc[:, 0:ntok], in0=rawpad[:, k:k + ntok], scalar=cwb[:, j, k:k + 1],
                                                                  op0=ALU.mult, in1=acc[:, 0:ntok], op1=ALU.add), r=[krp, kacc, KC], w=[kacc])
            silu_ops = P.capture(lambda: P.op("act", lambda e: e.activation(out=dst, in_=acc[:, 0:ntok], func=AF.Silu), r=[kacc], w=[kdst]))
            if defer is None:
                P.ops.extend(silu_ops)
            else:
                defer.append(silu_ops)

        def ssd_temps(b, main=True):
            t = {}
            t["sm"] = [b.f32([96]) for _ in range(2)]
            t["ex"] = b.f32([48])
            t["xdtw"] = b.bf16([16, 64])
            t["B_tok"] = b.bf16([256])
            if not main:
                return t
            t["LT"] = b.f32([2048])
            t["MT"] = b.bf16([16, 128])
            t["xs_tok"] = b.bf16([16, 64])
            t["xdt"] = b.bf16([16, 64])
            t["cbm"] = b.f32([2, 128])
            t["y1"] = b.f32([16, 64])
            t["t2"] = b.f32([16, 64])
            t["yb"] = b.bf16([1024])
            t["gn"] = b.f32([8])
            return t

        def ssd_chunk(t, xT, c, dtraw_c, kdt, main, ztok_c=None, ssdg_bc=None, banks=(0, 1, 2, 7, 1), ktag=""):
            par = c % 2
            sm, ex = t["sm"][par], t["ex"]
            cs = slice(c * 128, (c + 1) * 128)
            kx = ("xT" + ktag,)
            bs_, bx_, bb_ = banks[0], banks[1], banks[2]
            k0, k1, kdtc, kdta, kw2 = ("sm0", par), ("sm1", par), ("dtc", par), ("dta", par), ("w2", par)
            dtc = sm[:, 32:48]
            dta = sm[:, 48:64]
            w2 = sm[:, 64:80]
            toend, dec, ea = ex[:, 0:16], ex[:, 16:32], ex[:, 32:48]
            psx = pbb[bx_][:, 0:1024].rearrange("p (a b) -> p a b", a=8)
            psx3 = pbb[bx_][:, 0:1024].rearrange("p (a b) -> p a b", a=16)
            psB = pbb[bb_][:, 0:256].rearrange("p (a b) -> p a b", a=2)

            def head():
                P.op("dve", lambda e: e.tensor_tensor(out=sm[:, 0:16], in0=dtraw_c, in1=dtb_bc, op=ALU.add), r=[kdt, KC], w=[k0])
                P.op("act", lambda e: e.activation(out=sm[:, 16:32], in_=sm[:, 0:16], func=AF.Exp), r=[k0], w=[k1])
                P.op("act", lambda e: e.activation(out=dtc, in_=sm[:, 16:32], func=AF.Ln, bias=1.0), r=[k1], w=[kdtc])
                P.op("dve", lambda e: e.tensor_tensor(out=dta, in0=dtc, in1=aneg, op=ALU.mult), r=[kdtc, "aneg"], w=[kdta])

            def part1():
                def trx(e):
                    ins = None
                    for q in range(8):
                        ins = e.transpose(out=psx[:, q, :], in_=xT[:, q, cs], identity=identB)
                    return ins
                P.op("pe", trx, r=[kx, "identB"], w=[("ps", bx_)])
                if main:
                    psc = pbf[7][:, 0:256].rearrange("p (a b) -> p a b", a=2)

                    def cbf(e):
                        ins = None
                        for g in range(2):
                            ins = e.matmul(psc[:, g, :], lhsT=xT[:, 8 + g, cs], rhs=xT[:, 10 + g, cs], start=True, stop=True)
                        return ins
                    P.op("pe", cbf, r=[kx], w=[("ps", 7)])
                    P.op("dve", lambda e: e.tensor_tensor(out=t["cbm"], in0=psc, in1=triF.unsqueeze(1).broadcast_to([128, 2, 128]), op=ALU.mult),
                         r=[("ps", 7), KC], w=["cbm"])
                    for e16 in range(16):
                        P.op("dve", lambda e, e16=e16: e.tensor_scalar(out=rhsEr[:, e16 * 128:(e16 + 1) * 128], in0=triF, scalar1=dta[:, e16:e16 + 1],
                                                                       scalar2=None, op0=ALU.mult), r=[kdta, KC], w=[("rhsE", e16 // 4)])

                def small(e):
                    e.matmul(pbf[bs_][:, 0:16], lhsT=UF, rhs=dta, start=True, stop=True)
                    ins = e.matmul(pbf[bs_][:, 16:32], lhsT=onesF, rhs=dta, start=True, stop=True)
                    if main:
                        ins = e.matmul(pbf[bs_][:, 32:48], lhsT=triF, rhs=dta, start=True, stop=True)
                    return ins
                P.op("pe", small, r=[kdta, KC], w=[("ps", bs_)])
                nex = 48 if main else 32
                P.op("act", lambda e: e.activation(out=ex[:, 0:nex], in_=pbf[bs_][:, 0:nex], func=AF.Exp), r=[("ps", bs_)], w=["ex"])
                if main:
                    for i in range(4):
                        P.op("pe", lambda e, i=i: e.matmul(pbf[3 + i], lhsT=U32r, rhs=rhsEr[:, i * 512:(i + 1) * 512], start=True, stop=True),
                             r=[("rhsE", i), "U32"], w=[("ps", 3 + i)])
                        P.op("act", lambda e, i=i: e.activation(out=t["LT"][:, i * 512:(i + 1) * 512], in_=pbf[3 + i], func=AF.Exp),
                             r=[("ps", 3 + i)], w=[("LT", i)])

                def trb(e):
                    ins = None
                    for g in range(2):
                        ins = e.transpose(out=psB[:, g, :], in_=xT[:, 8 + g, cs], identity=identB)
                    return ins
                P.op("pe", trb, r=[kx, "identB"], w=[("ps", bb_)])
                P.op("dve", lambda e: e.tensor_tensor(out=w2, in0=dtc, in1=toend, op=ALU.mult), r=[kdtc, "ex"], w=[kw2])
                P.op("dve", lambda e: e.tensor_tensor(out=t["xdtw"], in0=psx3, in1=w2.unsqueeze(2).broadcast_to([128, 16, 64]), op=ALU.mult),
                     r=[("ps", bx_), kw2], w=["xdtw"])
                P.op("act", lambda e: e.activation(out=t["B_tok"], in_=pbb[bb_][:, 0:256], func=AF.Copy), r=[("ps", bb_)], w=["B_tok"])
                if main:
                    P.op("dve", lambda e: e.tensor_tensor(out=t["xdt"], in0=psx3, in1=dtc.unsqueeze(2).broadcast_to([128, 16, 64]), op=ALU.mult),
                         r=[("ps", bx_), kdtc], w=["xdt"])
                    P.op("act", lambda e: e.activation(out=t["xs_tok"], in_=psx3, func=AF.Copy), r=[("ps", bx_)], w=["xs_tok"])
                    LT3 = t["LT"].rearrange("p (a b) -> p a b", a=16)
                    for g in range(2):
                        P.op("dve", lambda e, g=g: e.tensor_tensor(out=t["MT"][:, g * 8:(g + 1) * 8, :], in0=LT3[:, g * 8:(g + 1) * 8, :],
                                                                   in1=t["cbm"][:, g:g + 1, :].broadcast_to([128, 8, 128]), op=ALU.mult),
                             r=[("LT", 2 * g), ("LT", 2 * g + 1), "cbm"], w=[("MT", g)])

            def part2():
                if main:
                    for g in range(2):
                        def yd(e, g=g):
                            ins = None
                            for j in range(8):
                                e16 = g * 8 + j
                                ins = e.matmul(pbf[3 + g][:, j * 64:(j + 1) * 64], lhsT=t["MT"][:, e16, :], rhs=t["xdt"][:, e16, :], start=True, stop=True)
                            return ins
                        P.op("pe", yd, r=[("MT", g), "xdt"], w=[("ps", 3 + g)])
                        P.op("pe", lambda e, g=g: e.matmul(pbf[5 + g], lhsT=xT[:, 10 + g, cs], rhs=hTb[:, g * 512:(g + 1) * 512], start=True, stop=True),
                             r=[kx, "hTb"], w=[("ps", 5 + g)])
                    y1 = t["y1"]
                    for g in range(2):
                        y1g = y1[:, g * 8:(g + 1) * 8, :]
                        P.op("dve", lambda e, g=g, y1g=y1g: e.tensor_tensor(out=y1g, in0=pbf[5 + g].rearrange("p (a b) -> p a b", a=8),
                                                                            in1=ea[:, g * 8:(g + 1) * 8].unsqueeze(2).broadcast_to([128, 8, 64]), op=ALU.mult),
                             r=[("ps", 5 + g), "ex"], w=[("y1", g)])
                        P.op("dve", lambda e, g=g, y1g=y1g: e.tensor_tensor(out=y1g, in0=pbf[3 + g].rearrange("p (a b) -> p a b", a=8), in1=y1g, op=ALU.add),
                             r=[("ps", 3 + g), ("y1", g)], w=[("y1", g)])
                    P.op("pool", lambda e: e.tensor_tensor(out=t["t2"], in0=t["xs_tok"], in1=D_bc.unsqueeze(2).broadcast_to([128, 16, 64]), op=ALU.mult),
                         r=["xs_tok", KC], w=["t2"])
                    P.op("dve", lambda e: e.tensor_tensor(out=y1, in0=y1, in1=t["t2"], op=ALU.add), r=[("y1", 0), ("y1", 1), "t2"], w=[("y1", 0), ("y1", 1)])
                    y1f = y1.rearrange("p a b -> p (a b)")
                    P.op("dve", lambda e: e.tensor_tensor(out=y1f, in0=y1f, in1=ztok_c, op=ALU.mult), r=[("y1", 0), ("y1", 1), "z_tok"], w=[("y1", 0), ("y1", 1)])
                    gn = t["gn"]
                    t2f = t["t2"].rearrange("p a b -> p (a b)")
                    for g in range(2):
                        P.op("act", lambda e, g=g: e.activation(out=t2f[:, g * 512:(g + 1) * 512], in_=y1f[:, g * 512:(g + 1) * 512], func=AF.Square,
                                                                accum_out=gn[:, g:g + 1]), r=[("y1", g)], w=["t2", ("gn", g)])
                    P.op("act", lambda e: e.activation(out=gn[:, 2:4], in_=gn[:, 0:2], func=AF.Ln, scale=1.0 / 512, bias=EPS), r=[("gn", 0), ("gn", 1)], w=["gn2"])
                    P.op("act", lambda e: e.activation(out=gn[:, 4:6], in_=gn[:, 2:4], func=AF.Exp, scale=-0.5), r=["gn2"], w=["gn4"])
                    for g in range(2):
                        P.op("dve", lambda e, g=g: e.scalar_tensor_tensor(out=t["yb"][:, g * 512:(g + 1) * 512], in0=y1f[:, g * 512:(g + 1) * 512],
                                                                          scalar=gn[:, 4 + g:5 + g], op0=ALU.mult, in1=ssdg_bc[:, g * 512:(g + 1) * 512], op1=ALU.mult),
                             r=[("y1", g), "gn4", "ssdg"], w=[("yb", g)])
                    psy = pbb[2][:, 0:1024].rearrange("p (a b) -> p a b", a=8)

                    def try_(e):
                        ins = None
                        for q in range(8):
                            ins = e.transpose(out=psy[:, q, :], in_=t["yb"][:, q * 128:(q + 1) * 128], identity=identB)
                        return ins
                    P.op("pe", try_, r=[("yb", 0), ("yb", 1), "identB"], w=[("ps", 2)])
                    P.op("act", lambda e: e.activation(out=YT[:, 8:16, cs], in_=psy, func=AF.Copy), r=[("ps", 2)], w=[("YTb", c)])

            def part_state():
                sb = (banks[3], banks[4])
                for g in range(2):
                    P.op("pe", lambda e, g=g: e.matmul(pbf[sb[g]], lhsT=t["B_tok"][:, g * 128:(g + 1) * 128], rhs=t["xdtw"].rearrange("p a b -> p (a b)")[:, g * 512:(g + 1) * 512],
                                                       start=True, stop=True), r=["B_tok", "xdtw"], w=[("ps", sb[g])])
                hT3 = hT.rearrange("p (a b) -> p a b", a=16)
                P.op("dve", lambda e: e.tensor_tensor(out=hT3, in0=hT3, in1=dec.unsqueeze(2).broadcast_to([128, 16, 64]), op=ALU.mult), r=["hT", "ex"], w=["hT"])
                for g in range(2):
                    P.op("dve", lambda e, g=g: e.tensor_tensor(out=hT[:, g * 512:(g + 1) * 512], in0=pbf[sb[g]], in1=hT[:, g * 512:(g + 1) * 512], op=ALU.add),
                         r=[("ps", sb[g]), "hT"], w=["hT"])
                P.op("act", lambda e: e.activation(out=hTb, in_=hT, func=AF.Copy), r=["hT"], w=["hTb"])
            return P.capture(head), P.capture(part1), P.capture(part2) + P.capture(part_state)

        def ssd_emit(chunks):
            n = len(chunks)
            P.ops.extend(chunks[0][0])
            for c in range(n):
                P.ops.extend(chunks[c][1])
                if c + 1 < n:
                    P.ops.extend(chunks[c + 1][0])
                P.ops.extend(chunks[c][2])

        dbg_ops = []

        def dbg_dump(name, ap, keys):
            if name in dbg_d:
                dbg_ops.append((name, ap, keys))

        def finish():
            P.barrier()
            outs = []
            for name, ap, keys in dbg_ops:
                k = ("dbgout", name)
                P.op("sp", lambda e, name=name, ap=ap: e.dma_start(out=dbg_d[name], in_=ap), r=keys, w=[k], dma=True)
                outs.append(k)
            P.op("sp", None, r=outs + OUTKEYS)
            P.resolve()
            P.emit(sems, block)

        OUTKEYS = []

        sc.reset()
        NX = 4
        X = [sc.f32([DM]) for _ in range(NX)]
        XN = [sc.bf16([DM]) for _ in range(2)]
        ST = [sc.f32([4]) for _ in range(2)]
        gbc = sc.f32([DM])
        m_common = sc.mark()
        nPT = nT[:, :, 0:TM]
        P.op("sp", lambda e, gbc=gbc: e.dma_start(out=gbc, in_=g_mix.partition_broadcast(128)), w=["gbc"], dma=True)
        stg = []
        for tt in range(8):
            i = tt % 2
            ix = tt % NX
            pre = (lambda tt=tt, ix=ix: P.op("sp", lambda e: e.dma_start(out=X[ix], in_=xprev[tt * 128:(tt + 1) * 128, :]), w=[("X", ix)], dma=True))
            stg.append(rms_transpose(X[ix], ("X", ix), 128, XN[i], ("XN", i), ST[i], ("ST", i), gbc, "gbc", nPT, tt * 128, ("nPT", tt), DM, tt, pre=pre))
        rms_pipeline(stg)
        Wdt = sc.bf16([16, 16])
        r2.reset()
        xTp = r2.bf16([10, TM])
        dtraw_p = r2.f32([8, 16])
        tP = ssd_temps(r2, main=False)
        rawpad = [sc.f32([TM + 4]) for _ in range(2)]
        accb = [sc.f32([TM]) for _ in range(2)]
        blocks = [(COL_X, 512), (COL_X + 512, 512), (COL_B, 512)]
        load_w(Wdt, wview(w_in, COL_DT, 16), "Wdt")
        wcur = wload(wview(w_in, blocks[0][0], blocks[0][1]))
        for i in range(2):
            P.op("dve", lambda e, rp=rawpad[i]: e.memset(rp[:, 0:3], 0.0), w=[("rawpad", i)])
        jt = 0
        defer_p = []
        for bi, (c0, ncol) in enumerate(blocks):
            Wv, kW = wcur
            if bi + 1 < len(blocks):
                wcur = wload(wview(w_in, blocks[bi + 1][0], blocks[bi + 1][1]))
            for jj in range(4):
                j = (c0 - COL_X) // 128 + jj
                rp = rawpad[jt % 2]
                krp = ("rawpad", jt % 2)
                isC = j >= 10
                pend_p = list(defer_p)
                del defer_p[:]
                for nb in range(2):
                    if isC and nb == 0:
                        continue
                    bank = 3 + (jt * 2 + nb) % 4
                    mm_group(pbf[bank], [(Wv[:, k, jj * 128:(jj + 1) * 128], nPT[:, k, nb * 512:(nb + 1) * 512]) for k in range(16)],
                             r=[kW] + [("nPT", tt) for tt in range(nb * 4, nb * 4 + 4)], w=[("ps", bank)])
                    P.op("act", lambda e, rp=rp, nb=nb, bank=bank: e.activation(out=rp[:, 3 + nb * 512:3 + (nb + 1) * 512], in_=pbf[bank], func=AF.Copy),
                         r=[("ps", bank)], w=[krp])
                    if not isC:
                        P.op("act", lambda e, ac=accb[jt % 2], nb=nb, bank=bank, j=j: e.activation(out=ac[:, nb * 512:(nb + 1) * 512], in_=pbf[bank], func=AF.Identity,
                                                                                                scale=cwb[:, j, 3:4], bias=cwb[:, j, 4:5]),
                             r=[("ps", bank), KC], w=[("acc", jt % 2)])
                for so in pend_p:
                    P.ops.extend(so)
                P.op("dve", lambda e, rp=rp, j=j: e.tensor_copy(out=prefix[:, j, :], in_=rp[:, TM:TM + 3]), r=[krp], w=["prefix"])
                if not isC:
                    conv_silu(rp, krp, accb[jt % 2], ("acc", jt % 2), j, TM, xTp[:, j, :], ("xTp",), defer=defer_p)
                jt += 1
        for so in defer_p:
            P.ops.extend(so)
        for tt in range(8):
            mm_group(pbf[0][:, 0:16], [(nPT[:, k, tt * 128:(tt + 1) * 128], Wdt[:, k, :]) for k in range(16)], r=["Wdt", ("nPT", tt)], w=[("ps", 0)])
            P.op("act", lambda e, tt=tt: e.activation(out=dtraw_p[:, tt, :], in_=pbf[0][:, 0:16], func=AF.Copy), r=[("ps", 0)], w=["dtraw_p"])
        P.barrier()

        def pchunks():
            ssd_emit([ssd_chunk(tP, xTp, c, dtraw_p[:, c, :], "dtraw_p", main=False, banks=(5, 6, 5, 7, 6), ktag="p") for c in range(8)])
            P.op("dve", lambda e: e.tensor_scalar(out=hT, in0=hT, scalar1=flag, scalar2=None, op0=ALU.mult), r=["hT", KC], w=["hT"])
            P.op("act", lambda e: e.activation(out=hTb, in_=hT, func=AF.Copy), r=["hT"], w=["hTb"])
        pch_ops = P.capture(pchunks)
        _saved_ops = P.ops
        P.ops = []
        sc.reset(m_common)
        stgA = []
        for tt in range(9):
            i = tt % 2
            rows = 128 if tt < 8 else NS
            src = xm[tt * 128:(tt + 1) * 128, :] if tt < 8 else xsmp[:, :]
            ix = tt % NX
            pre = (lambda ix=ix, rows=rows, src=src: P.op("sp", lambda e: e.dma_start(out=X[ix][:rows, :], in_=src), w=[("X", ix)], dma=True))
            stgA.append(rms_transpose(X[ix], ("X", ix), rows, XN[i], ("XN", i), ST[i], ("ST", i), gbc, "gbc", nT, tt * 128, ("nT", tt), DM, tt, pre=pre))
        rms_pipeline(stgA)
        P.barrier()
        sc.reset()
        xT = sc.bf16([12, T])
        z_tok = sc.bf16([8, 1024])
        dtraw = sc.f32([8, 16])
        ssdg_bc = sc.f32([1024])
        raw_s = sc.f32([12, NS])
        m_A = sc.mark()
        Wdt = sc.bf16([16, 16])
        rawpad = [sc.f32([TM + 4]) for _ in range(2)]
        accb = [sc.f32([TM]) for _ in range(2)]
        scTok = sc.f32([3 * 1536])
        scT = sc.f32([36, NS])
        acc_s = sc.f32([12, NS])
        P.op("sp", lambda e: e.dma_start(out=ssdg_bc, in_=ssd_g.partition_broadcast(128)), w=["ssdg"], dma=True)
        P.op("sp", lambda e: e.dma_start(out=Rsel, in_=cst2_d[:, C2_RSEL:C2_RSEL + 1024]), w=["cst2"], dma=True)
        P.op("sp", lambda e: e.dma_start(out=scTok[:NS, :], in_=sconv.rearrange("b r c -> b (r c)")), w=["scTok"], dma=True)
        P.op("sp", lambda e: e.dma_start(out=nconv_s[:, 0:2, :], in_=sconv[:, 1:3, :]), w=["o_ncs01"], dma=True)
        OUTKEYS.append("o_ncs01")
        for half in range(2):
            def trs(e, half=half):
                ins = None
                for idx in range(half * 18, half * 18 + 18):
                    r_, j_ = idx // 12, idx % 12
                    ii = idx - half * 18
                    ins = e.transpose(out=pbf[half][:, ii * NS:(ii + 1) * NS], in_=scTok[:NS, r_ * 1536 + j_ * 128:r_ * 1536 + (j_ + 1) * 128],
                                      identity=identF[:NS, :NS])
                return ins
            P.op("pe", trs, r=["scTok", KC], w=[("ps", half)])
            P.op("dve", lambda e, half=half: e.tensor_copy(out=scT[:, half * 18:half * 18 + 18, :].rearrange("p a b -> p (a b)"), in_=pbf[half][:, 0:18 * NS]),
                 r=[("ps", half)], w=["scT"])
        blocksA = [(COL_Z, "z"), (COL_Z + 512, "z"), (COL_X, "x"), (COL_X + 512, "x"), (COL_B, "x")]
        load_w(Wdt, wview(w_in, COL_DT, 16), "Wdt")
        wcur = wload(wview(w_in, blocksA[0][0], 512))
        jt = 0
        grp = 0
        defer_a = []
        _zwin = [len(P.ops), None]
        for bi, (c0, kind) in enumerate(blocksA):
            Wv, kW = wcur
            if kind == "x" and _zwin[1] is None:
                _zwin[1] = len(P.ops)
            if bi + 1 < len(blocksA):
                wcur = wload(wview(w_in, blocksA[bi + 1][0], 512))
            if kind == "z":
                zb = (c0 - COL_Z) // 512
                for tt in range(8):
                    bank = 1 + grp % 4
                    grp += 1
                    mm_group(pbf[bank], [(nT[:, k, tt * 128:(tt + 1) * 128], Wv[:, k, :]) for k in range(16)], r=[kW, ("nT", tt)], w=[("ps", bank)])
                    P.op("act", lambda e, tt=tt, zb=zb, bank=bank: e.activation(out=z_tok[:, tt, zb * 512:(zb + 1) * 512], in_=pbf[bank], func=AF.Silu),
                         r=[("ps", bank)], w=["z_tok"])
                for jj in range(4):
                    bank = 1 + grp % 4
                    grp += 1
                    j = zb * 4 + jj
                    mm_group(pbf[bank][:, 0:NS], [(Wv[:, k, jj * 128:(jj + 1) * 128], nT[:, k, TM:T]) for k in range(16)], r=[kW, ("nT", 8)], w=[("ps", bank)])
                    P.op("act", lambda e, j=j, bank=bank: e.activation(out=zT_s[:, j, :], in_=pbf[bank][:, 0:NS], func=AF.Silu), r=[("ps", bank)], w=["zT_s"])
            else:
                for jj in range(4):
                    j = (c0 - COL_X) // 128 + jj
                    rp = rawpad[jt % 2]
                    krp = ("rawpad", jt % 2)
                    pend_a = list(defer_a)
                    del defer_a[:]
                    P.op("dve", lambda e, rp=rp, j=j: e.tensor_copy(out=rp[:, 0:3], in_=prefix[:, j, :]), r=["prefix"], w=[krp])
                    for nb in range(3):
                        bank = 1 + grp % 4
                        grp += 1
                        lo, hi = (nb * 512, (nb + 1) * 512) if nb < 2 else (TM, T)
                        n = hi - lo
                        rk = [("nT", tt) for tt in range(nb * 4, nb * 4 + 4)] if nb < 2 else [("nT", 8)]
                        mm_group(pbf[bank][:, 0:n], [(Wv[:, k, jj * 128:(jj + 1) * 128], nT[:, k, lo:hi]) for k in range(16)], r=[kW] + rk, w=[("ps", bank)])
                        if nb < 2:
                            P.op("act", lambda e, rp=rp, lo=lo, hi=hi, bank=bank: e.activation(out=rp[:, 3 + lo:3 + hi], in_=pbf[bank], func=AF.Copy),
                                 r=[("ps", bank)], w=[krp])
                            P.op("act", lambda e, ac=accb[jt % 2], lo=lo, hi=hi, bank=bank, j=j: e.activation(out=ac[:, lo:hi], in_=pbf[bank], func=AF.Identity,
                                                                                                    scale=cwb[:, j, 3:4], bias=cwb[:, j, 4:5]),
                                 r=[("ps", bank), KC], w=[("acc", jt % 2)])
                        else:
                            P.op("act", lambda e, j=j, bank=bank: e.activation(out=raw_s[:, j, :], in_=pbf[bank][:, 0:NS], func=AF.Copy),
                                 r=[("ps", bank)], w=["raw_s"])
                    for so in pend_a:
                        P.ops.extend(so)
                    P.op("dve", lambda e, rp=rp, j=j: e.tensor_copy(out=ncv[:, j, :], in_=rp[:, TM:TM + 3]), r=[krp], w=["ncv"])
                    conv_silu(rp, krp, accb[jt % 2], ("acc", jt % 2), j, TM, xT[:, j, 0:TM], ("xT",), defer=defer_a)
                    P.op("dve", lambda e, j=j: e.tensor_scalar(out=acc_s[:, j, :], in0=raw_s[:, j, :], scalar1=cwb[:, j, 3:4], scalar2=cwb[:, j, 4:5],
                                                               op0=ALU.mult, op1=ALU.add), r=["raw_s", KC], w=["acc_s"])
                    for r_ in range(3):
                        P.op("dve", lambda e, j=j, r_=r_: e.scalar_tensor_tensor(out=acc_s[:, j, :], in0=scT[:, r_ * 12 + j, :], scalar=cwb[:, j, r_:r_ + 1],
                                                                                 op0=ALU.mult, in1=acc_s[:, j, :], op1=ALU.add), r=["scT", "acc_s", KC], w=["acc_s"])
                    jt += 1
        for so in defer_a:
            P.ops.extend(so)
        P.op("act", lambda e: e.activation(out=xT_s, in_=acc_s, func=AF.Silu), r=["acc_s"], w=["xT_s"])
        for tt in range(8):
            mm_group(pbf[0][:, 0:16], [(nT[:, k, tt * 128:(tt + 1) * 128], Wdt[:, k, :]) for k in range(16)], r=["Wdt", ("nT", tt)], w=[("ps", 0)])
            P.op("act", lambda e, tt=tt: e.activation(out=dtraw[:, tt, :], in_=pbf[0][:, 0:16], func=AF.Copy), r=[("ps", 0)], w=["dtraw"])
        mm_group(pbf[1][:NS, 0:NS], [(Wdt[:, k, :], nT[:, k, TM:T]) for k in range(16)], r=["Wdt", ("nT", 8)], w=[("ps", 1)])
        P.op("act", lambda e: e.activation(out=dtT_s[:NS, :], in_=pbf[1][:NS, 0:NS], func=AF.Copy), r=[("ps", 1)], w=["dtT_s"])

        def rows_out(src3, ksrc, R, stage, kst, dram_ap, okey):
            for b3 in range(3):
                def trr(e, b3=b3):
                    ins = None
                    for jj in range(4):
                        j_ = b3 * 4 + jj
                        ins = e.transpose(out=pbf[2 + b3][:R, jj * 128:(jj + 1) * 128], in_=src3[:, j_, :], identity=identF)
                    return ins
                P.op("pe", trr, r=[ksrc, KC], w=[("ps", 2 + b3)])
                P.op("dve", lambda e, b3=b3: e.tensor_copy(out=stage[:R, b3 * 512:(b3 + 1) * 512], in_=pbf[2 + b3][:R, :]), r=[("ps", 2 + b3)], w=[kst, "scTok"])
            P.op("sp", lambda e: e.dma_start(out=dram_ap, in_=stage[:R, 0:1536]), r=[kst], w=[okey], dma=True)
            OUTKEYS.append(okey)
        rows_out(raw_s, "raw_s", NS, scTok[:, 0:1536], "stg0", nconv_s[:, 2, :], "o_ncs2")
        rows_out(ncv, "ncv", 3, scTok[:, 1536:3072], "stg1", nconv_p[:, :], "o_ncp")
        _main_ops = P.ops
        P.ops = _saved_ops
        P.ops.extend(Prog.merge(_main_ops, pch_ops, _zwin[0], _zwin[1]))
        P.barrier()
        if STOP_AFTER == "A1":
            dbg_dump("xT", xT.rearrange("p a b -> p (a b)"), [("xT",)])
            dbg_dump("z_tok", z_tok.rearrange("p a b -> p (a b)"), ["z_tok"])
            dbg_dump("zT_s", zT_s.rearrange("p a b -> p (a b)"), ["zT_s"])
            dbg_dump("dtraw", dtraw.rearrange("p a b -> p (a b)"), ["dtraw"])
            dbg_dump("dtT_s", dtT_s, ["dtT_s"])
            finish()
            return nc

        sc.reset(m_A)
        tA = ssd_temps(sc)
        ssd_emit([ssd_chunk(tA, xT, c, dtraw[:, c, :], "dtraw", main=True, ztok_c=z_tok[:, c, :], ssdg_bc=ssdg_bc) for c in range(8)])
        if STOP_AFTER == "A2a":
            P.barrier()
            dbg_dump("YT", YT.rearrange("p a b -> p (a b)"), [("YTb", c) for c in range(8)])
            finish()
            return nc
        hout = sc.f32([8, 128])
        for half in range(2):
            def trh(e, half=half):
                ins = None
                for jj in range(4):
                    q = half * 4 + jj
                    ins = e.transpose(out=pbf[3 + half][:, jj * 128:(jj + 1) * 128], in_=hT[:, q * 128:(q + 1) * 128], identity=identF)
                return ins
            P.op("pe", trh, r=["hT", KC], w=[("ps", 3 + half)])
            P.op("dve", lambda e, half=half: e.tensor_copy(out=hout[:, half * 4:(half + 1) * 4, :].rearrange("p a b -> p (a b)"), in_=pbf[3 + half]),
                 r=[("ps", 3 + half)], w=["hout"])
        P.op("sp", lambda e: e.dma_start(out=nssm_p.rearrange("(q r) n -> r q n", r=128), in_=hout), r=["hout"], w=["o_nsp"], dma=True)
        OUTKEYS.append("o_nsp")

        if STOP_AFTER == "A2b":
            P.barrier()
            dbg_dump("YT", YT.rearrange("p a b -> p (a b)"), [("YTb", c) for c in range(8)])
            finish()
            return nc
        P.barrier()
        SMPW = 7300
        scS = Bump(arena, AW - SMPW, AW)
        _saved_main = P.ops
        P.ops = []
        _sc_main = sc
        sc = scS
        sm_s = sc.f32([8, NS])
        cat = sc.f32([32])
        rep = sc.f32([8, 32])
        dtx = sc.f32([8, NS])
        ys = sc.f32([8, NS])
        ysq = sc.f32([8, NS])
        rs = sc.f32([2, NS])
        BC_tok = sc.bf16([512])
        selB = sc.bf16([NS, 128])
        NHS = 3
        hs = [sc.f32([8, 128]) for _ in range(NHS)]
        tmpS = [sc.f32([1024]) for _ in range(2)]
        x0 = sm_s[:NS, 0, :]; e1 = sm_s[:NS, 1, :]; anp = sm_s[:NS, 2, 0:1]
        P.op("act", lambda e: e.activation(out=e1, in_=dtT_s[:NS, :], func=AF.Exp, bias=dtb_pp[:NS, :]), r=["dtT_s", KC], w=["s_e1"])
        P.op("act", lambda e: e.activation(out=cat[:NS, 16:32], in_=e1, func=AF.Ln, bias=1.0), r=["s_e1"], w=["s_dt"])
        P.op("act", lambda e: e.activation(out=anp, in_=al_pp[:NS, :], func=AF.Exp), r=[KC], w=["s_anp0"])
        P.op("dve", lambda e: e.tensor_scalar(out=anp, in0=anp, scalar1=-1.0, scalar2=None, op0=ALU.mult), r=["s_anp0"], w=["s_anp"])
        P.op("act", lambda e: e.activation(out=cat[:NS, 0:16], in_=cat[:NS, 16:32], func=AF.Exp, scale=anp), r=["s_dt", "s_anp"], w=["s_dA"])

        def repf(e):
            ins = None
            for q in range(8):
                ins = e.matmul(pbf[7][:, q * 32:(q + 1) * 32], lhsT=Rsel[:NS, q * 128:(q + 1) * 128], rhs=cat[:NS, :], start=True, stop=True)
            return ins
        P.op("pe", repf, r=["s_dA", "s_dt", "cst2"], w=[("ps", 7)])
        P.op("dve", lambda e: e.tensor_copy(out=rep.rearrange("p a b -> p (a b)"), in_=pbf[7][:, 0:256]), r=[("ps", 7)], w=["rep"])
        xs_s = xT_s[:, 0:8, :]
        P.op("dve", lambda e: e.tensor_tensor(out=dtx, in0=rep[:, :, 16:32], in1=xs_s, op=ALU.mult), r=["rep", "xT_s"], w=["dtx"])
        psbc = pbb[7][:NS, 0:512].rearrange("p (a b) -> p a b", a=4)

        def trbc(e):
            ins = None
            for jj in range(4):
                ins = e.transpose(out=psbc[:, jj, :], in_=xT_s[:, 8 + jj, :], identity=identB)
            return ins
        P.op("pe", trbc, r=["xT_s", "identB"], w=[("ps", 7)])
        P.op("act", lambda e: e.activation(out=BC_tok[:NS, :], in_=pbb[7][:NS, 0:512], func=AF.Copy), r=[("ps", 7)], w=["BC_tok"])
        P.op("dve", lambda e: e.tensor_copy(out=selB[:NS, :, :], in_=identB[:NS, 0:NS].unsqueeze(2).broadcast_to([NS, NS, 128])), r=["identB"], w=["selB"])
        P.op("dve", lambda e: e.memset(ys, 0.0), w=["ys"])

        def hs_load(b):
            P.op("sp", lambda e, b=b: e.dma_start(out=hs[b % NHS], in_=sssm[b].rearrange("(q r) n -> r q n", r=128)), w=[("hs", b % NHS, q) for q in range(8)], dma=True)
        for b in range(min(NHS - 1, NS)):
            hs_load(b)
        for b in range(NS):
            hb = hs[b % NHS]
            bank = 5 + b % 2
            if b + NHS - 1 < NS:
                hs_load(b + NHS - 1)
            P.op("pe", lambda e, b=b, bank=bank: e.matmul(pbf[bank], lhsT=selB[:NS, b, :], rhs=BC_tok[:NS, :], start=True, stop=True),
                 r=["selB", "BC_tok"], w=[("ps", bank)])
            for q in range(8):
                P.op("act", lambda e, b=b, q=q, hb=hb: e.activation(out=hb[:, q, :], in_=hb[:, q, :], func=AF.Copy, scale=rep[:, q, b:b + 1]),
                     r=[("hs", b % NHS, q), "rep"], w=[("hs", b % NHS, q)])

            bc4 = pbf[bank][:, 0:256].rearrange("p (g n) -> p g n", g=2).unsqueeze(2).broadcast_to([128, 2, 4, 128])
            cc4 = pbf[bank][:, 256:512].rearrange("p (g n) -> p g n", g=2).unsqueeze(2).broadcast_to([128, 2, 4, 128])
            hb4 = hb.rearrange("p (g j) n -> p g j n", g=2)
            tm4 = tmpS[b % 2].rearrange("p (g j n) -> p g j n", g=2, j=4)
            dx4 = dtx[:, :, b:b + 1].rearrange("p (g j) o -> p g j o", g=2).broadcast_to([128, 2, 4, 128])
            khs = [("hs", b % NHS, q) for q in range(8)]
            ktm = ("tmpS", b % 2)
            P.op("dve", lambda e, tm4=tm4, bc4=bc4, dx4=dx4: e.tensor_tensor(out=tm4, in0=bc4, in1=dx4, op=ALU.mult), r=[("ps", bank), "dtx"], w=[ktm])
            P.op("dve", lambda e, hb4=hb4, tm4=tm4: e.tensor_tensor(out=hb4, in0=hb4, in1=tm4, op=ALU.add), r=khs + [ktm], w=khs)
            P.op("dve", lambda e, hb4=hb4, tm4=tm4, cc4=cc4: e.tensor_tensor(out=tm4, in0=cc4, in1=hb4, op=ALU.mult), r=khs + [("ps", bank), ktm], w=[ktm])
            P.op("dve", lambda e, b=b, tm=tmpS[b % 2]: e.tensor_reduce(out=ys[:, :, b:b + 1], in_=tm.rearrange("p (q n) -> p q n", q=8), axis=AX.X, op=ALU.add),
                 r=[ktm], w=["ys"])
            ok = ("o_nss", b)
            P.op("sp", lambda e, b=b, hb=hb: e.dma_start(out=nssm_s[b].rearrange("(q r) n -> r q n", r=128), in_=hb), r=[("hs", b % NHS, q) for q in range(8)], w=[ok], dma=True)
            OUTKEYS.append(ok)
        P.op("dve", lambda e: e.tensor_tensor(out=ysq, in0=xs_s, in1=D_pp.unsqueeze(2).broadcast_to([128, 8, NS]), op=ALU.mult), r=["xT_s", KC], w=["ysq"])
        P.op("dve", lambda e: e.tensor_tensor(out=ys, in0=ys, in1=ysq, op=ALU.add), r=["ys", "ysq"], w=["ys"])
        P.op("dve", lambda e: e.tensor_tensor(out=ys, in0=ys, in1=zT_s, op=ALU.mult), r=["ys", "zT_s"], w=["ys"])
        P.op("dve", lambda e: e.tensor_tensor(out=ysq, in0=ys, in1=ys, op=ALU.mult), r=["ys", "ysq"], w=["ysq"])

        def gsum(e):
            ins = None
            for q in range(8):
                g = q // 4
                ins = e.matmul(pbf[7][:, g * NS:(g + 1) * NS], lhsT=onesF, rhs=ysq[:, q, :], start=(q % 4 == 0), stop=(q % 4 == 3))
            return ins
        P.op("pe", gsum, r=["ysq", KC], w=[("ps", 7)])
        rsf = rs.rearrange("p a b -> p (a b)")
        P.op("act", lambda e: e.activation(out=rsf, in_=pbf[7][:, 0:2 * NS], func=AF.Ln, scale=1.0 / 512, bias=EPS), r=[("ps", 7)], w=["rs0"])
        P.op("act", lambda e: e.activation(out=rsf, in_=rsf, func=AF.Exp, scale=-0.5), r=["rs0"], w=["rs"])
        for g in range(2):
            P.op("dve", lambda e, g=g: e.tensor_tensor(out=ys[:, g * 4:(g + 1) * 4, :], in0=ys[:, g * 4:(g + 1) * 4, :],
                                                       in1=rs[:, g:g + 1, :].broadcast_to([128, 4, NS]), op=ALU.mult), r=["ys", "rs"], w=["ys"])
        P.op("dve", lambda e: e.tensor_tensor(out=YT[:, 8:16, TM:T], in0=ys, in1=sg_pp.unsqueeze(2).broadcast_to([128, 8, NS]), op=ALU.mult),
             r=["ys", KC], w=[("YTb", 8)])
        smp_ops = P.ops
        P.ops = _saved_main
        sc = _sc_main

        sc.reset()
        sc.end = AW - SMPW
        _saved_a3 = P.ops
        P.ops = []
        vng_bc = sc.f32([1024])
        bs_bc = sc.f32([8, 128])
        v_tok = sc.bf16([9, 1024])
        WsT = sc.bf16([8, 128])
        sqacc = sc.f32([T])
        gv = [sc.f32([1024]) for _ in range(2)]
        wsraw = gv[0].rearrange("p (a b) -> p a b", a=8)
        vst = [sc.f32([4]) for _ in range(2)]
        ug = [sc.f32([T]) for _ in range(2)]
        _m_tm = sc.mark()
        tmpm = [sc.f32([512]) for _ in range(2)]
        _m_tm2 = sc.mark()
        sc.reset(_m_tm)
        vsf = sc.f32([1024])
        sc.reset(_m_tm2)
        sq = [sc.bf16([T]) for _ in range(2)]
        rstd_bc = sqacc
        w00I = sc.bf16([8, NS])
        P.op("sp", lambda e: e.dma_start(out=vng_bc, in_=vn_g.partition_broadcast(128)), w=["vng"], dma=True)
        P.op("sp", lambda e: e.dma_start(out=bs_bc.rearrange("p a b -> p (a b)"), in_=cst2_d[:, C2_BS:C2_BS + 1024]), w=["bs_bc"], dma=True)
        P.op("sp", lambda e: e.dma_start(out=wsraw, in_=ws_d.rearrange("h t s -> t h s")), w=["wsraw"], dma=True)
        wv2 = [wload(wview(w_in, COL_V, 512)), wload(wview(w_in, COL_V + 512, 512))]
        for half in range(2):
            def trw(e, half=half):
                ins = None
                for jj in range(4):
                    h_ = half * 4 + jj
                    ins = e.transpose(out=pbf[half][:, jj * 128:(jj + 1) * 128], in_=wsraw[:, h_, :], identity=identF)
                return ins
            P.op("pe", trw, r=["wsraw", KC], w=[("ps", half)])
            P.op("dve", lambda e, half=half: e.tensor_tensor(out=WsT[:, half * 4:(half + 1) * 4, :], in0=pbf[half].rearrange("p (a b) -> p a b", a=4),
                                                             in1=triF.unsqueeze(1).broadcast_to([128, 4, 128]), op=ALU.mult), r=[("ps", half), KC], w=["WsT"])
        for h_ in range(8):
            P.op("dve", lambda e, h_=h_: e.tensor_scalar(out=w00I[:NS, h_, :], in0=identF[:NS, 0:NS], scalar1=w00_bc[:NS, h_:h_ + 1], scalar2=None, op0=ALU.mult),
                 r=[KC], w=["w00I"])
        for tt in range(9):
            rows = 128 if tt < 8 else NS
            i = tt % 2
            tcols = slice(tt * 128, tt * 128 + rows)
            for zb in range(2):
                bank = 1 + (tt * 2 + zb) % 4
                mm_group(pbf[bank][:rows, :], [(nT[:, k, tcols], wv2[zb][0][:, k, :]) for k in range(16)], r=[wv2[zb][1], ("nT", tt)], w=[("ps", bank)])
                P.op("act", lambda e, i=i, zb=zb, bank=bank, rows=rows: e.activation(out=gv[i][:rows, zb * 512:(zb + 1) * 512], in_=pbf[bank][:rows, :],
                                                                                    func=AF.Gelu_apprx_tanh), r=[("ps", bank)], w=[("gv", i, zb)])
            P.op("act", lambda e, i=i, rows=rows: e.activation(out=ug[i][:rows, 0:1024], in_=gv[i][:rows, :], func=AF.Square, accum_out=vst[i][:rows, 0:1]),
                 r=[("gv", i, 0), ("gv", i, 1)], w=[("ug", i), ("vst", i)])
            P.op("act", lambda e, i=i, rows=rows: e.activation(out=vst[i][:rows, 1:2], in_=vst[i][:rows, 0:1], func=AF.Ln, scale=1.0 / 1024, bias=EPS),
                 r=[("vst", i)], w=[("vst", i)])
            P.op("act", lambda e, i=i, rows=rows: e.activation(out=vst[i][:rows, 2:3], in_=vst[i][:rows, 1:2], func=AF.Exp, scale=-0.5),
                 r=[("vst", i)], w=[("vst", i)])
            P.op("dve", lambda e, i=i, rows=rows, tt=tt: e.scalar_tensor_tensor(out=v_tok[:rows, tt, :], in0=gv[i][:rows, :], scalar=vst[i][:rows, 2:3], op0=ALU.mult,
                                                                                in1=vng_bc[:rows, :], op1=ALU.mult),
                 r=[("gv", i, 0), ("gv", i, 1), ("vst", i), "vng"], w=[("v_tok", tt)])
            if tt == 8:
                P.op("dve", lambda e, i=i: e.scalar_tensor_tensor(out=vsf[:NS, :], in0=gv[i][:NS, :], scalar=vst[i][:NS, 2:3], op0=ALU.mult,
                                                                  in1=vng_bc[:NS, :], op1=ALU.mult), r=[("gv", i, 0), ("gv", i, 1), ("vst", i), "vng"], w=["vsf", ("tmpm", 0), ("tmpm", 1)])
                P.op("sp", lambda e: e.dma_start(out=v_s[:, :], in_=vsf[:NS, :]), r=["vsf", ("tmpm", 0), ("tmpm", 1)], w=["o_vs"], dma=True)
                OUTKEYS.append("o_vs")
        wu2 = [wload(wview(w_in, COL_U, 512)), wload(wview(w_in, COL_U + 512, 512))]
        tok_blocks = [(0, 348), (348, 696), (696, T)]
        grpc = [0]

        def stage_G(h_):
            ub, jj = h_ // 4, h_ % 4
            Wv, kWu = wu2[ub]
            ui = h_ % 2
            for nb, (lo, hi) in enumerate(tok_blocks):
                bank = grpc[0] % 2
                grpc[0] += 1
                n = hi - lo
                rk = [("nT", tt) for tt in range(lo // 128, min(8, (hi - 1) // 128) + 1)]
                mm_group(pbf[bank][:, 0:n], [(Wv[:, k, jj * 128:(jj + 1) * 128], nT[:, k, lo:hi]) for k in range(16)], r=[kWu] + rk, w=[("ps", bank)])
                P.op("act", lambda e, ui=ui, lo=lo, hi=hi, n=n, bank=bank: e.activation(out=ug[ui][:, lo:hi], in_=pbf[bank][:, 0:n], func=AF.Gelu_apprx_tanh),
                     r=[("ps", bank)], w=[("ug", ui)])

        def stage_M(h_):
            ui = h_ % 2
            for half in range(2):
                def mix(e, half=half, h_=h_):
                    ins = None
                    for cc in range(4):
                        c = half * 4 + cc
                        ins = e.matmul(pbf[2 + half][:, cc * 128:(cc + 1) * 128], lhsT=v_tok[:, c, h_ * 128:(h_ + 1) * 128], rhs=WsT[:, h_, :], start=True, stop=True)
                    return ins
                P.op("pe", mix, r=[("v_tok", c) for c in range(half * 4, half * 4 + 4)] + ["WsT"], w=[("ps", 2 + half)])
                tm = tmpm[half]
                P.op("dve", lambda e, half=half, h_=h_, tm=tm: e.tensor_tensor(out=tm.rearrange("p (a b) -> p a b", a=4), in0=pbf[2 + half].rearrange("p (a b) -> p a b", a=4),
                                                                               in1=bs_bc[:, h_:h_ + 1, :].broadcast_to([128, 4, 128]), op=ALU.add),
                     r=[("ps", 2 + half), "bs_bc"], w=[("tmpm", half)])
                P.op("dve", lambda e, half=half, h_=h_, tm=tm, ui=ui: e.tensor_tensor(out=YT[:, h_, half * 512:(half + 1) * 512], in0=tm, in1=ug[ui][:, half * 512:(half + 1) * 512], op=ALU.mult),
                     r=[("tmpm", half), ("ug", ui)], w=[("ya", h_)])
            P.op("pe", lambda e, h_=h_: e.matmul(pbf[4][:, 0:NS], lhsT=v_tok[:NS, 8, h_ * 128:(h_ + 1) * 128], rhs=w00I[:NS, h_, :], start=True, stop=True),
                 r=[("v_tok", 8), "w00I"], w=[("ps", 4)])
            P.op("dve", lambda e, h_=h_, ui=ui: e.scalar_tensor_tensor(out=YT[:, h_, TM:T], in0=pbf[4][:, 0:NS], scalar=b0_bc[:, h_:h_ + 1], op0=ALU.add, in1=ug[ui][:, TM:T], op1=ALU.mult),
                 r=[("ps", 4), ("ug", ui), KC], w=[("ya", h_)])
            si = h_ % 2
            if h_ == 0:
                P.op("act", lambda e, h_=h_: e.activation(out=sqacc, in_=YT[:, h_, :], func=AF.Square), r=[("ya", h_)], w=["sqacc"])
            else:
                P.op("act", lambda e, h_=h_, si=si: e.activation(out=sq[si], in_=YT[:, h_, :], func=AF.Square), r=[("ya", h_)], w=[("sq", si)])
                P.op("dve", lambda e, si=si: e.tensor_tensor(out=sqacc, in0=sqacc, in1=sq[si], op=ALU.add), r=["sqacc", ("sq", si)], w=["sqacc"])

        stage_G(0)
        for h_ in range(8):
            if h_ + 1 < 8:
                stage_G(h_ + 1)
            stage_M(h_)
        P.op("dve", lambda e: e.tensor_copy(out=sq[0], in_=sqacc), r=["sqacc", ("sq", 0)], w=[("sq", 0)])
        sbank = (0, 1, 4)
        for nb, (lo, hi) in enumerate(tok_blocks):
            n = hi - lo
            P.op("pe", lambda e, nb=nb, lo=lo, hi=hi, n=n: e.matmul(pbf[sbank[nb]][:, 0:n], lhsT=onesB, rhs=sq[0][:, lo:hi], start=True, stop=True),
                 r=[("sq", 0), "onesB"], w=[("ps", sbank[nb])])
        for nb, (lo, hi) in enumerate(tok_blocks):
            n = hi - lo
            P.op("act", lambda e, nb=nb, lo=lo, hi=hi, n=n: e.activation(out=rstd_bc[:, lo:hi], in_=pbf[sbank[nb]][:, 0:n], func=AF.Ln, scale=1.0 / 1024, bias=EPS),
                 r=[("ps", sbank[nb])], w=[("rstd", nb)])
            P.op("act", lambda e, lo=lo, hi=hi: e.activation(out=rstd_bc[:, lo:hi], in_=rstd_bc[:, lo:hi], func=AF.Exp, scale=-0.5), r=[("rstd", nb)], w=[("rstd", nb)])
        for h_ in range(8):
            P.op("dve", lambda e, h_=h_: e.scalar_tensor_tensor(out=YT[:, h_, :], in0=YT[:, h_, :], scalar=gon_pp[:, h_:h_ + 1], op0=ALU.mult, in1=rstd_bc, op1=ALU.mult),
                 r=[("ya", h_), ("rstd", 0), ("rstd", 1), ("rstd", 2), KC], w=[("YTa", h_)])
        _a3_ops = P.ops
        P.ops = _saved_a3
        P.ops.extend(Prog.merge(_a3_ops, smp_ops, 0, int(len(_a3_ops) * 0.8)))
        sc.end = AW
        P.barrier()
        if STOP_AFTER == "A3":
            dbg_dump("YT", YT.rearrange("p a b -> p (a b)"), [("YTa", h_) for h_ in range(8)])
            finish()
            return nc

        sc.reset()
        hres = sc.f32([9, DM])
        m_B = sc.mark()
        for tt in range(9):
            rows = 128 if tt < 8 else NS
            src = xm[tt * 128:(tt + 1) * 128, :] if tt < 8 else xsmp[:, :]
            P.op("sp", lambda e, tt=tt, rows=rows, src=src: e.dma_start(out=hres[:rows, tt, :], in_=src), w=[("h", tt, cb) for cb in range(4)], dma=True)
        wcur = wload(wview(w_out, 0, 512))
        for cb in range(4):
            Wv, kW = wcur
            if cb + 1 < 4:
                wcur = wload(wview(w_out, (cb + 1) * 512, 512))
            for tt in range(9):
                rows = 128 if tt < 8 else NS
                tcols = slice(tt * 128, tt * 128 + rows)
                bank = (cb * 9 + tt) % 6
                mm_group(pbf[bank][:rows, :], [(YT[:, k, tcols], Wv[:, k, :]) for k in range(16)], r=[kW], w=[("ps", bank)])
                P.op("dve", lambda e, rows=rows, tt=tt, cb=cb, bank=bank: e.tensor_tensor(out=hres[:rows, tt, cb * 512:(cb + 1) * 512], in0=pbf[bank][:rows, :],
                                                                                         in1=hres[:rows, tt, cb * 512:(cb + 1) * 512], op=ALU.add),
                     r=[("ps", bank), ("h", tt, cb)], w=[("h", tt, cb)])
        P.barrier()
        if STOP_AFTER == "B":
            dbg_dump("h", hres.rearrange("p a b -> p (a b)"), [("h", tt) for tt in range(9)])
            finish()
            return nc

        sc.reset(m_B)
        gbc = sc.f32([DM])
        stmp = [sc.f32([512]) for _ in range(2)]
        ST = [sc.f32([4]) for _ in range(2)]
        r2.reset()
        actb = [r2.bf16([2, T]) for _ in range(2)]
        Wd = [r2.bf16([2, DM]) for _ in range(2)]
        XN = [r2.bf16([DM]) for _ in range(2)]
        mT = nT
        P.op("sp", lambda e: e.dma_start(out=gbc, in_=g_ffn.partition_broadcast(128)), w=["gbc"], dma=True)
        stgC = []
        for tt in range(9):
            rows = 128 if tt < 8 else NS
            i = tt % 2
            stgC.append(rms_transpose(hres[:, tt, :], [("h", tt, cb) for cb in range(4)], rows, XN[i], ("XN", i), ST[i], ("ST", i), gbc, "gbc", mT, tt * 128, ("mT", tt), DM, tt))
        rms_pipeline(stgC)
        NFB = DFF // 256
        if STOP_AFTER == "C0":
            P.barrier()
            dbg_dump("mT", mT.rearrange("p a b -> p (a b)"), [("mT", tt) for tt in range(9)])
            dbg_dump("XN0", XN[0], [("XN", 0)])
            dbg_dump("ST0", ST[0], [("ST", 0)])
            dbg_dump("gbc", gbc, ["gbc"])
            dbg_dump("h", hres.rearrange("p a b -> p (a b)"), [("h", tt) for tt in range(9)])
            finish()
            return nc

        gu_slot = {}

        def load_gu(fb):
            slot = wq[0] % 2
            wq[0] += 1
            gu_slot[fb] = slot
            load_w(WG[slot], wview(w_gate, fb * 256, 256), ("W", slot), nobar=True)
            load_w(WU[slot], wview(w_up, fb * 256, 256), ("W", slot), nobar=True)

        def load_d(fb):
            load_w(Wd[fb % 2], w_down[fb * 256:(fb + 1) * 256, :].rearrange("(fl p) c -> p fl c", p=128), ("Wd", fb % 2))

        gctr = [0]

        ffn_blocks = [(0, 348), (348, 696), (696, T)]

        def GU_units(fb):
            slot = gu_slot[fb]
            Wg_, Wu_, kWs = WG[slot], WU[slot], ("W", slot)
            units = []
            for fl in range(2):
                for nb, (lo, hi) in enumerate(ffn_blocks):
                    def unit(fl=fl, nb=nb, lo=lo, hi=hi):
                        n = hi - lo
                        pr = gctr[0] % 2
                        gctr[0] += 1
                        bg, bu = pr * 2, pr * 2 + 1
                        rk = [("mT", tt) for tt in range(lo // 128, min(8, (hi - 1) // 128) + 1)]
                        mm_group(pbf[bg][:, 0:n], [(Wg_[:, k, fl * 128:(fl + 1) * 128], mT[:, k, lo:hi]) for k in range(16)], r=[kWs] + rk, w=[("ps", bg)])
                        mm_group(pbf[bu][:, 0:n], [(Wu_[:, k, fl * 128:(fl + 1) * 128], mT[:, k, lo:hi]) for k in range(16)], r=[kWs] + rk, w=[("ps", bu)])
                        P.op("act", lambda e: e.activation(out=stmp[pr][:, 0:n], in_=pbf[bg][:, 0:n], func=AF.Silu), r=[("ps", bg)], w=[("stmp", pr)])
                        P.op("dve", lambda e: e.tensor_tensor(out=actb[fb % 2][:, fl, lo:hi], in0=pbf[bu][:, 0:n], in1=stmp[pr][:, 0:n], op=ALU.mult),
                             r=[("ps", bu), ("stmp", pr)], w=[("act", fb % 2)])
                    units.append(P.capture(unit))
            return units

        dctr = [0]

        def DN_groups(fb):
            groups = []
            for tt in range(9):
                rows = 128 if tt < 8 else NS
                tcols = slice(tt * 128, tt * 128 + rows)
                for cb in range(4):
                    def grp_(tt=tt, rows=rows, tcols=tcols, cb=cb):
                        bank = 4 + dctr[0] % 4
                        dctr[0] += 1
                        mm_group(pbf[bank][:rows, :], [(actb[fb % 2][:, fl, tcols], Wd[fb % 2][:, fl, cb * 512:(cb + 1) * 512]) for fl in range(2)],
                                 r=[("act", fb % 2), ("Wd", fb % 2)], w=[("ps", bank)])
                        P.op("dve", lambda e: e.tensor_tensor(out=hres[:rows, tt, cb * 512:(cb + 1) * 512], in0=pbf[bank][:rows, :],
                                                              in1=hres[:rows, tt, cb * 512:(cb + 1) * 512], op=ALU.add),
                             r=[("ps", bank), ("h", tt, cb)], w=[("h", tt, cb)])
                    groups.append(P.capture(grp_))
            return groups

        load_gu(0)
        load_d(0)
        for i in range(NFB + 1):
            if i + 1 < NFB:
                load_gu(i + 1)
            gu = GU_units(i) if i < NFB else []
            dn = DN_groups(i - 1) if i >= 1 else []
            nslot = max(len(gu), 1)
            per = -(-len(dn) // nslot)
            for u in range(nslot):
                if u < len(gu):
                    P.ops.extend(gu[u])
                for g_ in dn[u * per:(u + 1) * per]:
                    P.ops.extend(g_)
            if i + 1 < NFB:
                load_d(i + 1)
        P.barrier()
        if STOP_AFTER == "C":
            dbg_dump("h", hres.rearrange("p a b -> p (a b)"), [("h", tt) for tt in range(9)])
            finish()
            return nc

        P.op("sp", lambda e: e.dma_start(out=gbc, in_=g_fin.partition_broadcast(128)), w=["gbc"], dma=True)
        for tt in range(9):
            rows = 128 if tt < 8 else NS
            i = tt % 2
            ht = hres[:, tt, :]
            st = ST[i]
            P.op("act", lambda e, rows=rows, ht=ht, st=st, i=i: e.activation(out=XN[i][:rows, :], in_=ht[:rows, :], func=AF.Square, accum_out=st[:rows, 0:1]),
                 r=[("h", tt, cb) for cb in range(4)], w=[("XN", i), ("ST", i)])
            P.op("act", lambda e, rows=rows, st=st: e.activation(out=st[:rows, 1:2], in_=st[:rows, 0:1], func=AF.Ln, scale=1.0 / DM, bias=EPS), r=[("ST", i)], w=[("ST", i)])
            P.op("act", lambda e, rows=rows, st=st: e.activation(out=st[:rows, 2:3], in_=st[:rows, 1:2], func=AF.Exp, scale=-0.5), r=[("ST", i)], w=[("ST", i)])
            P.op("dve", lambda e, rows=rows, ht=ht, st=st: e.scalar_tensor_tensor(out=ht[:rows, :], in0=ht[:rows, :], scalar=st[:rows, 2:3], op0=ALU.mult, in1=gbc[:rows, :], op1=ALU.mult),
                 r=[("h", tt, cb) for cb in range(4)] + [("ST", i), "gbc"], w=[("h", tt, cb) for cb in range(4)])
            dst = y_m[tt * 128:(tt + 1) * 128, :] if tt < 8 else y_s[:, :]
            ok = ("o_y", tt)
            P.op("sp", lambda e, rows=rows, ht=ht, dst=dst: e.dma_start(out=dst, in_=ht[:rows, :]), r=[("h", tt, cb) for cb in range(4)], w=[ok], dma=True)
            OUTKEYS.append(ok)
        finish()
        return nc


def _host_consts(inputs, core):
    hf = core % 2
    cst = np.zeros((128, CSTW), np.float32)
    r = np.arange(128)
    cst[:, C_ID:C_ID + 128] = np.eye(128, dtype=np.float32)
    cst[:, C_TRI:C_TRI + 128] = (r[:, None] <= r[None, :]).astype(np.float32)
    cst[:, C_U:C_U + 128] = (r[:, None] > r[None, :]).astype(np.float32)
    cst[:, C_ONE:C_ONE + 128] = 1.0
    cw = np.asarray(inputs["ssd_conv_w"])[0]
    cb = np.asarray(inputs["ssd_conv_b"])[0]
    cwb = np.concatenate([cw, cb[None]], 0)
    cst[:, C_CWB:C_CWB + 60] = cwb.reshape(5, 12, 128).transpose(2, 1, 0).reshape(128, 60)
    cst[:, C_GON:C_GON + 8] = np.asarray(inputs["chunk_out_norm_g"])[0].reshape(8, 128).T
    cst[:, C_ALOG:C_ALOG + 16] = np.asarray(inputs["ssd_a_log"])[0][None, :]
    cst[:, C_DTB:C_DTB + 16] = np.asarray(inputs["ssd_dt_bias"])[0][None, :]
    cst[:, C_D:C_D + 16] = np.asarray(inputs["ssd_d"])[0][None, :]
    cst[:, C_W00:C_W00 + 8] = np.asarray(inputs["chunk_w_s"])[0][:, 0, 0][None, :]
    cst[:, C_B0:C_B0 + 8] = np.asarray(inputs["chunk_b_s"])[0][:, 0][None, :]
    cst[:, C_FLAG] = float(hf)
    cst[:, C_DPP:C_DPP + 8] = np.repeat(np.asarray(inputs["ssd_d"])[0], 64).reshape(8, 128).T
    cst[:, C_SGPP:C_SGPP + 8] = np.asarray(inputs["ssd_norm_g"])[0].reshape(8, 128).T
    cst[0:16, C_DTBPP] = np.asarray(inputs["ssd_dt_bias"])[0]
    cst[0:16, C_ALPP] = np.asarray(inputs["ssd_a_log"])[0]
    cst2 = np.zeros((128, CST2W), np.float32)
    cst2[:, C2_BS:C2_BS + 1024] = np.asarray(inputs["chunk_b_s"])[0].reshape(1, 1024)
    rs = np.zeros((16, 8, 128), np.float32)
    for q in range(8):
        for rr in range(128):
            rs[2 * q + rr // 64, q, rr] = 1.0
    cst2[0:16, C2_RSEL:C2_RSEL + 1024] = rs.reshape(16, 1024)
    return cst, cst2


def make_in_maps(inputs):
    xp = np.asarray(inputs["x_prompt"], np.float32)
    xs = np.asarray(inputs["x_sample"], np.float32)
    sconv = np.asarray(inputs["state_conv"], np.float32)[0]
    sssm = np.asarray(inputs["state_ssm"], np.float32)[0]
    shared = {
        "w_in": np.ascontiguousarray(np.asarray(inputs["w_in"], np.float32)[0]),
        "w_out": np.ascontiguousarray(np.asarray(inputs["w_out"], np.float32)[0]),
        "w_gate": np.ascontiguousarray(np.asarray(inputs["w_gate"], np.float32)[0]),
        "w_up": np.ascontiguousarray(np.asarray(inputs["w_up"], np.float32)[0]),
        "w_down": np.ascontiguousarray(np.asarray(inputs["w_down"], np.float32)[0]),
        "ws": np.ascontiguousarray(np.asarray(inputs["chunk_w_s"], np.float32)[0]),
        "g_mix": np.ascontiguousarray(np.asarray(inputs["norm_mix_g"], np.float32)[0]),
        "g_ffn": np.ascontiguousarray(np.asarray(inputs["norm_ffn_g"], np.float32)[0]),
        "g_fin": np.ascontiguousarray(np.asarray(inputs["norm_final_g"], np.float32)),
        "vn_g": np.ascontiguousarray(np.asarray(inputs["chunk_v_norm_g"], np.float32)[0]),
        "ssd_g": np.ascontiguousarray(np.asarray(inputs["ssd_norm_g"], np.float32)[0]),
    }
    zeros_prev = np.zeros((TM, DM), np.float32)
    maps = []
    for c in range(8):
        b, hf = c // 2, c % 2
        cst, cst2 = _host_consts(inputs, c)
        m = dict(shared)
        m["xm"] = np.ascontiguousarray(xp[b, hf * TM:(hf + 1) * TM])
        m["xprev"] = np.ascontiguousarray(xp[b, 0:TM]) if hf == 1 else zeros_prev
        m["xsmp"] = np.ascontiguousarray(xs[c * NS:(c + 1) * NS, 0])
        m["sconv"] = np.ascontiguousarray(sconv[c * NS:(c + 1) * NS])
        m["sssm"] = np.ascontiguousarray(sssm[c * NS:(c + 1) * NS].reshape(NS, 1024, 128))
        m["cst"] = cst
        m["cst2"] = cst2
        maps.append(m)
    return maps


def kernel(**inputs):
    nc = build_program()
    maps = make_in_maps(inputs)
    res = run_bass_kernel_spmd(nc, maps, core_ids=list(range(8)))
    R = res.results
    y_prompt = np.zeros((4, 2048, DM), np.float32)
    y_sample = np.zeros((128, 1, DM), np.float32)
    ncp = np.zeros((1, 4, 3, 1536), np.float32)
    nsp = np.zeros((1, 4, 16, 64, 128), np.float32)
    ncs = np.zeros((1, 128, 3, 1536), np.float32)
    nss = np.zeros((1, 128, 16, 64, 128), np.float32)
    vs = np.zeros((1, 128, 1, 1024), np.float32)
    for c in range(8):
        b, hf = c // 2, c % 2
        y_prompt[b, hf * TM:(hf + 1) * TM] = R[c]["y_m"]
        y_sample[c * NS:(c + 1) * NS, 0] = R[c]["y_s"]
        if hf == 1:
            ncp[0, b] = R[c]["nconv_p"]
            nsp[0, b] = R[c]["nssm_p"].reshape(16, 64, 128)
        ncs[0, c * NS:(c + 1) * NS] = R[c]["nconv_s"]
        nss[0, c * NS:(c + 1) * NS] = R[c]["nssm_s"].reshape(NS, 16, 64, 128)
        vs[0, c * NS:(c + 1) * NS, 0] = R[c]["v_s"]
    return (y_prompt, y_sample, ncp, nsp, ncs, nss, vs)
```

```python
import numpy as np
from contextlib import ExitStack
import concourse.bass as bass
import concourse.mybir as mybir
from concourse.bass_utils import run_bass_kernel_spmd

F32 = mybir.dt.float32
F32R = mybir.dt.float32r
BF16 = mybir.dt.bfloat16
AF = mybir.ActivationFunctionType
ALU = mybir.AluOpType
AX = mybir.AxisListType

ENGS = ["pe", "act", "dve", "pool", "sp"]
NDMASEM = 12

DEBUG = {}
STOP_AFTER = None
SSD_LEVEL = 9


class Op:
    __slots__ = ("eng", "fn", "r", "w", "dma", "deps", "sig", "sem", "prev_use", "waits", "need_sig", "xdeps", "nobar")

    def __init__(self, eng, fn, r, w, dma, nobar=False):
        self.eng, self.fn, self.r, self.w, self.dma = eng, fn, tuple(r), tuple(w), dma
        self.deps = set()
        self.xdeps = set()
        self.sig = None
        self.sem = None
        self.prev_use = 0
        self.waits = []
        self.need_sig = False
        self.nobar = nobar


BARRIER = Op(None, None, (), (), False)


class Prog:
    def __init__(self):
        self.ops = []

    def op(self, eng, fn, r=(), w=(), dma=False, nobar=False):
        w = list(w) + [k for k in r if isinstance(k, tuple) and k and k[0] == "ps" and k not in w]
        o = Op(eng, fn, r, w, dma, nobar)
        self.ops.append(o)
        return o

    def barrier(self):
        self.ops.append(BARRIER)

    def capture(self, fn):
        saved = self.ops
        self.ops = []
        try:
            fn()
            got = self.ops
        finally:
            self.ops = saved
        return got

    @staticmethod
    def merge(main, side, lo=0, hi=None):
        if not side:
            return list(main)
        hi = len(main) if hi is None else hi
        out = []
        s_ = len(side)
        span = max(hi - lo, 1)
        j = 0
        for i, o in enumerate(main):
            out.append(o)
            if i >= lo:
                tgt = min(s_, (i + 1 - lo) * s_ // span)
                while j < tgt:
                    out.append(side[j])
                    j += 1
        out.extend(side[j:])
        return out

    def resolve(self):
        ops = self.ops
        last_w = {}
        readers = {}
        last_eng = {}
        dmas = []
        pending = {}
        for i, o in enumerate(ops):
            if o is BARRIER:
                s = set(last_eng.values()) | set(dmas)
                for e in ENGS:
                    pending[e] = set(s) | pending.get(e, set())
                continue
            if o.eng in pending and not o.nobar:
                o.xdeps = o.xdeps | pending.pop(o.eng)
            deps = set(o.xdeps)
            for k in o.r:
                if k in last_w:
                    deps.add(last_w[k])
            for k in o.w:
                if k in last_w:
                    deps.add(last_w[k])
                deps.update(readers.get(k, ()))
            deps.discard(i)
            keep = set()
            for d in deps:
                od = ops[d]
                if od.dma:
                    keep.add(d)
                elif od.eng == o.eng and not o.dma:
                    if o.eng == "pe":
                        continue
                    if d in o.xdeps:
                        continue
                    if set(od.w) & set(o.r):
                        keep.add(d)
                else:
                    keep.add(d)
            o.deps = keep
            for d in keep:
                ops[d].need_sig = True
            for k in o.r:
                readers.setdefault(k, []).append(i)
            for k in o.w:
                last_w[k] = i
                readers[k] = []
            if o.dma:
                dmas.append(i)
            elif o.fn is not None:
                last_eng[o.eng] = i
        cnt = {e: 0 for e in ENGS}
        dma_n = {e: 0 for e in ENGS}
        dma_use = {}
        for i, o in enumerate(ops):
            if o is BARRIER:
                continue
            if o.dma:
                slot = (o.eng, dma_n[o.eng] % NDMASEM)
                dma_n[o.eng] += 1
                u = dma_use.get(slot, 0)
                o.prev_use = u
                dma_use[slot] = u + 1
                o.sem = slot
                o.sig = 16 * (u + 1)
            elif o.need_sig:
                cnt[o.eng] += 1
                o.sem = o.eng
                o.sig = cnt[o.eng]
        seen = {e: {} for e in ENGS}
        for i, o in enumerate(ops):
            if o is BARRIER:
                continue
            sd = seen[o.eng]
            need = {}
            if o.dma and o.prev_use > 0:
                need[o.sem] = 16 * o.prev_use
            for d in o.deps:
                od = ops[d]
                need[od.sem] = max(need.get(od.sem, 0), od.sig)
            o.waits = []
            for s, v in need.items():
                if sd.get(s, 0) >= v:
                    continue
                sd[s] = v
                o.waits.append((s, v))

    def emit(self, sems, block):
        ops = self.ops

        def run(eng_name):
            def body(e):
                for o in ops:
                    if o is BARRIER or o.eng != eng_name:
                        continue
                    for s, v in o.waits:
                        e.wait_ge(sems[s], v)
                    if o.fn is None:
                        continue
                    ins = o.fn(e)
                    if o.dma:
                        ins.then_inc(sems[o.sem], 16)
                    elif o.sig is not None:
                        ins.then_inc(sems[o.sem], 1)
            return body

        block.sync(run("sp"))
        block.tensor(run("pe"))
        block.scalar(run("act"))
        block.vector(run("dve"))
        block.gpsimd(run("pool"))


DM = 2048
DIN = 4624
DFF = 5632
TM = 1024
NS = 16
T = TM + NS
EPS = 1e-6
COL_U, COL_V, COL_Z, COL_X, COL_B, COL_C, COL_DT = 0, 1024, 2048, 3072, 4096, 4352, 4608

C_ID, C_TRI, C_U, C_ONE = 0, 128, 256, 384
C_CWB, C_GON, C_ALOG, C_DTB, C_D, C_W00, C_B0, C_FLAG = 512, 572, 580, 596, 612, 628, 636, 644
C_DPP, C_SGPP, C_DTBPP, C_ALPP = 648, 656, 664, 665
CSTW = 672
C2_BS, C2_RSEL = 0, 1024
CST2W = 2048

AW = 51000


class Bump:
    def __init__(self, arena, start, end):
        self.arena, self.start, self.end, self.off = arena, start, end, start
        self.peak = start

    def reset(self, to=None):
        self.off = self.start if to is None else to

    def mark(self):
        return self.off

    def _take(self, words):
        o = self.off
        self.off += words
        assert self.off <= self.end, ("arena overflow", self.off, self.end)
        self.peak = max(self.peak, self.off)
        return o

    def f32(self, shape):
        n = int(np.prod(shape))
        o = self._take(n)
        v = self.arena[:, o:o + n]
        return _shape(v, shape)

    def bf16(self, shape):
        n = int(np.prod(shape))
        w = (n + 1) // 2
        o = self._take(w)
        v = self.arena[:, o:o + w].bitcast(BF16)[:, 0:n]
        return _shape(v, shape)


def _shape(v, shape):
    if len(shape) == 1:
        return v
    if len(shape) == 2:
        return v.rearrange("p (a b) -> p a b", a=shape[0])
    if len(shape) == 3:
        return v.rearrange("p (a b c) -> p a b c", a=shape[0], b=shape[1])
    raise ValueError(shape)


def build_program():
    nc = bass.Bass("TRN2", target_bir_lowering=False)

    def din(name, shape):
        return nc.dram_tensor(name, shape, F32, kind="ExternalInput").ap()

    def dout(name, shape):
        return nc.dram_tensor(name, shape, F32, kind="ExternalOutput").ap()

    xm = din("xm", [TM, DM])
    xprev = din("xprev", [TM, DM])
    xsmp = din("xsmp", [NS, DM])
    sconv = din("sconv", [NS, 3, 1536])
    sssm = din("sssm", [NS, 1024, 128])
    w_in = din("w_in", [DM, DIN])
    w_out = din("w_out", [DM, DM])
    w_gate = din("w_gate", [DM, DFF])
    w_up = din("w_up", [DM, DFF])
    w_down = din("w_down", [DFF, DM])
    cst_d = din("cst", [128, CSTW])
    cst2_d = din("cst2", [128, CST2W])
    ws_d = din("ws", [8, 128, 128])
    g_mix = din("g_mix", [DM])
    g_ffn = din("g_ffn", [DM])
    g_fin = din("g_fin", [DM])
    vn_g = din("vn_g", [1024])
    ssd_g = din("ssd_g", [1024])

    y_m = dout("y_m", [TM, DM])
    y_s = dout("y_s", [NS, DM])
    nconv_p = dout("nconv_p", [3, 1536])
    nssm_p = dout("nssm_p", [1024, 128])
    nconv_s = dout("nconv_s", [NS, 3, 1536])
    nssm_s = dout("nssm_s", [NS, 1024, 128])
    v_s = dout("v_s", [NS, 1024])
    dbg_d = {}
    for name, (shape, dty) in DEBUG.items():
        dbg_d[name] = nc.dram_tensor("dbg_" + name, list(shape), dty, kind="ExternalOutput").ap()

    P = Prog()
    with ExitStack() as es:
        arena = es.enter_context(nc.sbuf_tensor("arena", [128, AW], F32))
        rhsE = es.enter_context(nc.sbuf_tensor("rhsE", [128, 2048], F32))
        U32 = es.enter_context(nc.sbuf_tensor("U32", [128, 128], F32))
        pb = [es.enter_context(nc.psum_tensor(f"pb{i}", [128, 512], F32)) for i in range(8)]
        sems = {}
        for e in ENGS:
            sems[e] = es.enter_context(nc.semaphore("s_" + e))
        for e in ("sp", "pool"):
            for i in range(NDMASEM):
                sems[(e, i)] = es.enter_context(nc.semaphore(f"d_{e}_{i}"))
        block = es.enter_context(nc.Block())

        arena = arena[:, :]
        rhsEr = rhsE[:, :].bitcast(F32R)
        U32r = U32[:, :].bitcast(F32R)
        pbf = [p[:, :] for p in pb]
        pbb = [p[:, :].bitcast(BF16) for p in pb]

        fx = Bump(arena, 0, AW)
        CST = fx.f32([CSTW])
        identF = CST[:, C_ID:C_ID + 128]
        triF = CST[:, C_TRI:C_TRI + 128]
        UF = CST[:, C_U:C_U + 128]
        onesF = CST[:, C_ONE:C_ONE + 128]
        cwb = CST[:, C_CWB:C_CWB + 60].rearrange("p (j k) -> p j k", j=12)
        gon_pp = CST[:, C_GON:C_GON + 8]
        alog_bc = CST[:, C_ALOG:C_ALOG + 16]
        dtb_bc = CST[:, C_DTB:C_DTB + 16]
        D_bc = CST[:, C_D:C_D + 16]
        w00_bc = CST[:, C_W00:C_W00 + 8]
        b0_bc = CST[:, C_B0:C_B0 + 8]
        flag = CST[:, C_FLAG:C_FLAG + 1]
        D_pp = CST[:, C_DPP:C_DPP + 8]
        sg_pp = CST[:, C_SGPP:C_SGPP + 8]
        dtb_pp = CST[:, C_DTBPP:C_DTBPP + 1]
        al_pp = CST[:, C_ALPP:C_ALPP + 1]
        identB = fx.bf16([128])
        onesB = fx.bf16([128])
        aneg = fx.f32([16])
        hT = fx.f32([1024])
        hTb = fx.bf16([1024])
        prefix = fx.f32([12, 3])
        ncv = fx.f32([12, 3])
        nT = fx.bf16([16, T])
        YT = fx.bf16([16, T])
        r2_start = fx.off - (16 * T) // 2
        r2_end = fx.off
        xT_s = fx.bf16([12, NS])
        zT_s = fx.f32([8, NS])
        dtT_s = fx.f32([NS])
        Rsel = fx.f32([1024])
        Wfix = [fx.bf16([16, 512]) for _ in range(2)]
        _wflat = [w_.rearrange("p a b -> p (a b)") for w_ in Wfix]
        WG = [w_[:, 0:4096].rearrange("p (a b) -> p a b", a=16) for w_ in _wflat]
        WU = [w_[:, 4096:8192].rearrange("p (a b) -> p a b", a=16) for w_ in _wflat]
        wq = [0]
        S0 = fx.off
        sc = Bump(arena, S0, AW)
        r2 = Bump(arena, r2_start, r2_end)

        KC = ("cst",)

        P.op("sp", lambda e: e.dma_start(out=CST, in_=cst_d[:, :]), w=[KC], dma=True)
        P.op("dve", lambda e: e.tensor_copy(out=identB, in_=identF), r=[KC], w=["identB"])
        P.op("dve", lambda e: e.memset(onesB, 1.0), w=["onesB"])
        P.op("dve", lambda e: e.tensor_copy(out=U32r, in_=UF), r=[KC], w=["U32"])
        P.op("act", lambda e: e.activation(out=aneg, in_=alog_bc, func=AF.Exp), r=[KC], w=["aneg0"])
        P.op("dve", lambda e: e.tensor_scalar(out=aneg, in0=aneg, scalar1=-1.0, scalar2=None, op0=ALU.mult), r=["aneg0"], w=["aneg"])
        P.op("dve", lambda e: e.memset(hT, 0.0), w=["hT"])
        P.op("dve", lambda e: e.memset(hTb, 0.0), w=["hTb"])

        def rms_transpose(xt, kx, rows, xn, kxn, st, kst, gbc, kg, dstT, col0, kdst, width, tag, pre=None):
            nk = width // 128

            def s1():
                if pre is not None:
                    pre()
                P.op("act", lambda e: e.activation(out=xn[:rows, :], in_=xt[:rows, :], func=AF.Square, accum_out=st[:rows, 0:1]),
                     r=[kx] if not isinstance(kx, list) else kx, w=[kxn, kst])
                P.op("act", lambda e: e.activation(out=st[:rows, 1:2], in_=st[:rows, 0:1], func=AF.Ln, scale=1.0 / width, bias=EPS),
                     r=[kst], w=[kst])
                P.op("act", lambda e: e.activation(out=st[:rows, 2:3], in_=st[:rows, 1:2], func=AF.Exp, scale=-0.5),
                     r=[kst], w=[kst])
                P.op("dve", lambda e: e.scalar_tensor_tensor(out=xn[:rows, :], in0=xt[:rows, :], scalar=st[:rows, 2:3], op0=ALU.mult,
                                                             in1=gbc[:rows, :], op1=ALU.mult),
                     r=([kx] if not isinstance(kx, list) else kx) + [kst, kg, kxn], w=[kxn])

            def s2():
                for b8 in range(nk // 8):
                    bank = (tag % 2) * 2 + b8
                    psv = pbb[bank][:, 0:8 * rows].rearrange("p (a b) -> p a b", a=8)

                    def tr(e, b8=b8, psv=psv):
                        ins = None
                        for j in range(8):
                            k = b8 * 8 + j
                            ins = e.transpose(out=psv[:, j, :], in_=xn[:rows, k * 128:(k + 1) * 128], identity=identB[:rows, :rows])
                        return ins
                    P.op("pe", tr, r=[kxn, "identB"], w=[("ps", bank)])
                    dst = dstT[:, b8 * 8:(b8 + 1) * 8, col0:col0 + rows]
                    if b8 == 0:
                        P.op("act", lambda e, dst=dst, psv=psv: e.activation(out=dst, in_=psv, func=AF.Copy), r=[("ps", bank)], w=[kdst])
                    else:
                        P.op("dve", lambda e, dst=dst, psv=psv: e.tensor_copy(out=dst, in_=psv), r=[("ps", bank)], w=[kdst])
            return P.capture(s1), P.capture(s2)

        def rms_pipeline(stages):
            n = len(stages)
            for i in range(n + 1):
                if i < n:
                    P.ops.extend(stages[i][0])
                if i >= 1:
                    P.ops.extend(stages[i - 1][1])

        def mm_group(out, pairs, r, w):
            def fn(e):
                ins = None
                n = len(pairs)
                for i, (l, rh) in enumerate(pairs):
                    ins = e.matmul(out, lhsT=l, rhs=rh, start=(i == 0), stop=(i == n - 1))
                return ins
            P.op("pe", fn, r=r, w=w)

        def load_w(dst, src_ap, key, nobar=False):
            P.op("pool", lambda e: e.dma_start(out=dst, in_=src_ap), w=[key], dma=True, nobar=nobar)

        def wload(src_ap, ncols=512):
            slot = wq[0] % 2
            wq[0] += 1
            buf = Wfix[slot] if ncols == 512 else Wfix[slot][:, :, 0:ncols]
            load_w(buf, src_ap, ("W", slot), nobar=True)
            return buf, ("W", slot)

        def wview(wap, c0, ncols):
            return wap[:, c0:c0 + ncols].rearrange("(kt p) c -> p kt c", p=128)

        def conv_silu(rawpad, krp, acc, kacc, j, ntok, dst, kdst, defer=None):
            for k in (0, 1, 2):
                P.op("dve", lambda e, k=k: e.scalar_tensor_tensor(out=acc[:, 0:ntok], in0=rawpad[:, k:k + ntok], scalar=cwb[:, j, k:k + 1],
                                                                  op0=ALU.mult, in1=acc[:, 0:ntok], op1=ALU.add), r=[krp, kacc, KC], w=[kacc])
            silu_ops = P.capture(lambda: P.op("act", lambda e: e.activation(out=dst, in_=acc[:, 0:ntok], func=AF.Silu), r=[kacc], w=[kdst]))
            if defer is None:
                P.ops.extend(silu_ops)
            else:
                defer.append(silu_ops)

        def ssd_temps(b, main=True):
            t = {}
            t["sm"] = [b.f32([96]) for _ in range(2)]
            t["ex"] = b.f32([48])
            t["xdtw"] = b.bf16([16, 64])
            t["B_tok"] = b.bf16([256])
            if not main:
                return t
            t["LT"] = b.f32([2048])
            t["MT"] = b.bf16([16, 128])
            t["xs_tok"] = b.bf16([16, 64])
            t["xdt"] = b.bf16([16, 64])
            t["cbm"] = b.f32([2, 128])
            t["y1"] = b.f32([16, 64])
            t["t2"] = b.f32([16, 64])
            t["yb"] = b.bf16([1024])
            t["gn"] = b.f32([8])
            return t

        def ssd_chunk(t, xT, c, dtraw_c, kdt, main, ztok_c=None, ssdg_bc=None, banks=(0, 1, 2, 7, 1), ktag=""):
            par = c % 2
            sm, ex = t["sm"][par], t["ex"]
            cs = slice(c * 128, (c + 1) * 128)
            kx = ("xT" + ktag,)
            bs_, bx_, bb_ = banks[0], banks[1], banks[2]
            k0, k1, kdtc, kdta, kw2 = ("sm0", par), ("sm1", par), ("dtc", par), ("dta", par), ("w2", par)
            dtc = sm[:, 32:48]
            dta = sm[:, 48:64]
            w2 = sm[:, 64:80]
            toend, dec, ea = ex[:, 0:16], ex[:, 16:32], ex[:, 32:48]
            psx = pbb[bx_][:, 0:1024].rearrange("p (a b) -> p a b", a=8)
            psx3 = pbb[bx_][:, 0:1024].rearrange("p (a b) -> p a b", a=16)
            psB = pbb[bb_][:, 0:256].rearrange("p (a b) -> p a b", a=2)

            def head():
                P.op("dve", lambda e: e.tensor_tensor(out=sm[:, 0:16], in0=dtraw_c, in1=dtb_bc, op=ALU.add), r=[kdt, KC], w=[k0])
                P.op("act", lambda e: e.activation(out=sm[:, 16:32], in_=sm[:, 0:16], func=AF.Exp), r=[k0], w=[k1])
                P.op("act", lambda e: e.activation(out=dtc, in_=sm[:, 16:32], func=AF.Ln, bias=1.0), r=[k1], w=[kdtc])
                P.op("dve", lambda e: e.tensor_tensor(out=dta, in0=dtc, in1=aneg, op=ALU.mult), r=[kdtc, "aneg"], w=[kdta])

            def part1():
                def trx(e):
                    ins = None
                    for q in range(8):
                        ins = e.transpose(out=psx[:, q, :], in_=xT[:, q, cs], identity=identB)
                    return ins
                P.op("pe", trx, r=[kx, "identB"], w=[("ps", bx_)])
                if main:
                    psc = pbf[7][:, 0:256].rearrange("p (a b) -> p a b", a=2)

                    def cbf(e):
                        ins = None
                        for g in range(2):
                            ins = e.matmul(psc[:, g, :], lhsT=xT[:, 8 + g, cs], rhs=xT[:, 10 + g, cs], start=True, stop=True)
                        return ins
                    P.op("pe", cbf, r=[kx], w=[("ps", 7)])
                    P.op("dve", lambda e: e.tensor_tensor(out=t["cbm"], in0=psc, in1=triF.unsqueeze(1).broadcast_to([128, 2, 128]), op=ALU.mult),
                         r=[("ps", 7), KC], w=["cbm"])
                    for i4 in range(4):
                        P.op("dve", lambda e, i4=i4: e.tensor_tensor(out=rhsEr[:, i4 * 512:(i4 + 1) * 512].rearrange("p (a b) -> p a b", a=4),
                                                                     in0=triF.unsqueeze(1).broadcast_to([128, 4, 128]),
                                                                     in1=dta[:, i4 * 4:(i4 + 1) * 4].unsqueeze(2).broadcast_to([128, 4, 128]), op=ALU.mult),
                             r=[kdta, KC], w=[("rhsE", i4)])

                def small(e):
                    e.matmul(pbf[bs_][:, 0:16], lhsT=UF, rhs=dta, start=True, stop=True)
                    ins = e.matmul(pbf[bs_][:, 16:32], lhsT=onesF, rhs=dta, start=True, stop=True)
                    if main:
                        ins = e.matmul(pbf[bs_][:, 32:48], lhsT=triF, rhs=dta, start=True, stop=True)
                    return ins
                P.op("pe", small, r=[kdta, KC], w=[("ps", bs_)])
                nex = 48 if main else 32
                P.op("act", lambda e: e.activation(out=ex[:, 0:nex], in_=pbf[bs_][:, 0:nex], func=AF.Exp), r=[("ps", bs_)], w=["ex"])
                if main:
                    for i in range(4):
                        P.op("pe", lambda e, i=i: e.matmul(pbf[3 + i], lhsT=U32r, rhs=rhsEr[:, i * 512:(i + 1) * 512], start=True, stop=True),
                             r=[("rhsE", i), "U32"], w=[("ps", 3 + i)])
                        P.op("act", lambda e, i=i: e.activation(out=t["LT"][:, i * 512:(i + 1) * 512], in_=pbf[3 + i], func=AF.Exp),
                             r=[("ps", 3 + i)], w=[("LT", i)])

                def trb(e):
                    ins = None
                    for g in range(2):
                        ins = e.transpose(out=psB[:, g, :], in_=xT[:, 8 + g, cs], identity=identB)
                    return ins
                P.op("pe", trb, r=[kx, "identB"], w=[("ps", bb_)])
                P.op("dve", lambda e: e.tensor_tensor(out=w2, in0=dtc, in1=toend, op=ALU.mult), r=[kdtc, "ex"], w=[kw2])
                P.op("dve", lambda e: e.tensor_tensor(out=t["xdtw"], in0=psx3, in1=w2.unsqueeze(2).broadcast_to([128, 16, 64]), op=ALU.mult),
                     r=[("ps", bx_), kw2], w=["xdtw"])
                P.op("act", lambda e: e.activation(out=t["B_tok"], in_=pbb[bb_][:, 0:256], func=AF.Copy), r=[("ps", bb_)], w=["B_tok"])
                if main:
                    P.op("dve", lambda e: e.tensor_tensor(out=t["xdt"], in0=psx3, in1=dtc.unsqueeze(2).broadcast_to([128, 16, 64]), op=ALU.mult),
                         r=[("ps", bx_), kdtc], w=["xdt"])
                    P.op("act", lambda e: e.activation(out=t["xs_tok"], in_=psx3, func=AF.Copy), r=[("ps", bx_)], w=["xs_tok"])
                    LT3 = t["LT"].rearrange("p (a b) -> p a b", a=16)
                    for g in range(2):
                        P.op("dve", lambda e, g=g: e.tensor_tensor(out=t["MT"][:, g * 8:(g + 1) * 8, :], in0=LT3[:, g * 8:(g + 1) * 8, :],
                                                                   in1=t["cbm"][:, g:g + 1, :].broadcast_to([128, 8, 128]), op=ALU.mult),
                             r=[("LT", 2 * g), ("LT", 2 * g + 1), "cbm"], w=[("MT", g)])

            def part2():
                if main:
                    for g in range(2):
                        def yd(e, g=g):
                            ins = None
                            for j in range(8):
                                e16 = g * 8 + j
                                ins = e.matmul(pbf[3 + g][:, j * 64:(j + 1) * 64], lhsT=t["MT"][:, e16, :], rhs=t["xdt"][:, e16, :], start=True, stop=True)
                            return ins
                        P.op("pe", yd, r=[("MT", g), "xdt"], w=[("ps", 3 + g)])
                        P.op("pe", lambda e, g=g: e.matmul(pbf[5 + g], lhsT=xT[:, 10 + g, cs], rhs=hTb[:, g * 512:(g + 1) * 512], start=True, stop=True),
                             r=[kx, "hTb"], w=[("ps", 5 + g)])
                    y1 = t["y1"]
                    for g in range(2):
                        y1g = y1[:, g * 8:(g + 1) * 8, :]
                        P.op("dve", lambda e, g=g, y1g=y1g: e.tensor_tensor(out=y1g, in0=pbf[5 + g].rearrange("p (a b) -> p a b", a=8),
                                                                            in1=ea[:, g * 8:(g + 1) * 8].unsqueeze(2).broadcast_to([128, 8, 64]), op=ALU.mult),
                             r=[("ps", 5 + g), "ex"], w=[("y1", g)])
                        P.op("dve", lambda e, g=g, y1g=y1g: e.tensor_tensor(out=y1g, in0=pbf[3 + g].rearrange("p (a b) -> p a b", a=8), in1=y1g, op=ALU.add),
                             r=[("ps", 3 + g), ("y1", g)], w=[("y1", g)])
                    P.op("pool", lambda e: e.tensor_tensor(out=t["t2"], in0=t["xs_tok"], in1=D_bc.unsqueeze(2).broadcast_to([128, 16, 64]), op=ALU.mult),
                         r=["xs_tok", KC], w=["t2"])
                    P.op("dve", lambda e: e.tensor_tensor(out=y1, in0=y1, in1=t["t2"], op=ALU.add), r=[("y1", 0), ("y1", 1), "t2"], w=[("y1", 0), ("y1", 1)])
                    y1f = y1.rearrange("p a b -> p (a b)")
                    P.op("dve", lambda e: e.tensor_tensor(out=y1f, in0=y1f, in1=ztok_c, op=ALU.mult), r=[("y1", 0), ("y1", 1), "z_tok"], w=[("y1", 0), ("y1", 1)])
                    gn = t["gn"]
                    t2f = t["t2"].rearrange("p a b -> p (a b)")
                    for g in range(2):
                        P.op("act", lambda e, g=g: e.activation(out=t2f[:, g * 512:(g + 1) * 512], in_=y1f[:, g * 512:(g + 1) * 512], func=AF.Square,
                                                                accum_out=gn[:, g:g + 1]), r=[("y1", g)], w=["t2", ("gn", g)])
                    P.op("act", lambda e: e.activation(out=gn[:, 2:4], in_=gn[:, 0:2], func=AF.Ln, scale=1.0 / 512, bias=EPS), r=[("gn", 0), ("gn", 1)], w=["gn2"])
                    P.op("act", lambda e: e.activation(out=gn[:, 4:6], in_=gn[:, 2:4], func=AF.Exp, scale=-0.5), r=["gn2"], w=["gn4"])
                    for g in range(2):
                        P.op("dve", lambda e, g=g: e.scalar_tensor_tensor(out=t["yb"][:, g * 512:(g + 1) * 512], in0=y1f[:, g * 512:(g + 1) * 512],
                                                                          scalar=gn[:, 4 + g:5 + g], op0=ALU.mult, in1=ssdg_bc[:, g * 512:(g + 1) * 512], op1=ALU.mult),
                             r=[("y1", g), "gn4", "ssdg"], w=[("yb", g)])
                    psy = pbb[2][:, 0:1024].rearrange("p (a b) -> p a b", a=8)

                    def try_(e):
                        ins = None
                        for q in range(8):
                            ins = e.transpose(out=psy[:, q, :], in_=t["yb"][:, q * 128:(q + 1) * 128], identity=identB)
                        return ins
                    P.op("pe", try_, r=[("yb", 0), ("yb", 1), "identB"], w=[("ps", 2)])
                    P.op("act", lambda e: e.activation(out=YT[:, 8:16, cs], in_=psy, func=AF.Copy), r=[("ps", 2)], w=[("YTb", c)])

            def part_state():
                sb = (banks[3], banks[4])
                for g in range(2):
                    P.op("pe", lambda e, g=g: e.matmul(pbf[sb[g]], lhsT=t["B_tok"][:, g * 128:(g + 1) * 128], rhs=t["xdtw"].rearrange("p a b -> p (a b)")[:, g * 512:(g + 1) * 512],
                                                       start=True, stop=True), r=["B_tok", "xdtw"], w=[("ps", sb[g])])
                hT3 = hT.rearrange("p (a b) -> p a b", a=16)
                P.op("dve", lambda e: e.tensor_tensor(out=hT3, in0=hT3, in1=dec.unsqueeze(2).broadcast_to([128, 16, 64]), op=ALU.mult), r=["hT", "ex"], w=["hT"])
                for g in range(2):
                    P.op("dve", lambda e, g=g: e.tensor_tensor(out=hT[:, g * 512:(g + 1) * 512], in0=pbf[sb[g]], in1=hT[:, g * 512:(g + 1) * 512], op=ALU.add),
                         r=[("ps", sb[g]), "hT"], w=["hT"])
                P.op("act", lambda e: e.activation(out=hTb, in_=hT, func=AF.Copy), r=["hT"], w=["hTb"])
            return P.capture(head), P.capture(part1), P.capture(part2) + P.capture(part_state)

        def ssd_emit(chunks):
            n = len(chunks)
            P.ops.extend(chunks[0][0])
            for c in range(n):
                P.ops.extend(chunks[c][1])
                if c + 1 < n:
                    P.ops.extend(chunks[c + 1][0])
                P.ops.extend(chunks[c][2])

        dbg_ops = []

        def dbg_dump(name, ap, keys):
            if name in dbg_d:
                dbg_ops.append((name, ap, keys))

        def finish():
            P.barrier()
            outs = []
            for name, ap, keys in dbg_ops:
                k = ("dbgout", name)
                P.op("sp", lambda e, name=name, ap=ap: e.dma_start(out=dbg_d[name], in_=ap), r=keys, w=[k], dma=True)
                outs.append(k)
            P.op("sp", None, r=outs + OUTKEYS)
            P.resolve()
            P.emit(sems, block)

        OUTKEYS = []

        sc.reset()
        NX = 4
        X = [sc.f32([DM]) for _ in range(NX)]
        XN = [sc.bf16([DM]) for _ in range(2)]
        ST = [sc.f32([4]) for _ in range(2)]
        gbc = sc.f32([DM])
        m_common = sc.mark()
        nPT = nT[:, :, 0:TM]
        P.op("sp", lambda e, gbc=gbc: e.dma_start(out=gbc, in_=g_mix.partition_broadcast(128)), w=["gbc"], dma=True)
        stg = []
        for tt in range(8):
            i = tt % 2
            ix = tt % NX
            pre = (lambda tt=tt, ix=ix: P.op("sp", lambda e: e.dma_start(out=X[ix], in_=xprev[tt * 128:(tt + 1) * 128, :]), w=[("X", ix)], dma=True))
            stg.append(rms_transpose(X[ix], ("X", ix), 128, XN[i], ("XN", i), ST[i], ("ST", i), gbc, "gbc", nPT, tt * 128, ("nPT", tt), DM, tt, pre=pre))
        rms_pipeline(stg)
        Wdt = sc.bf16([16, 16])
        r2.reset()
        xTp = r2.bf16([10, TM])
        dtraw_p = r2.f32([8, 16])
        tP = ssd_temps(r2, main=False)
        rawpad = [sc.f32([TM + 4]) for _ in range(2)]
        accb = [sc.f32([TM]) for _ in range(2)]
        blocks = [(COL_X, 512), (COL_X + 512, 512), (COL_B, 512)]
        load_w(Wdt, wview(w_in, COL_DT, 16), "Wdt")
        wcur = wload(wview(w_in, blocks[0][0], blocks[0][1]))
        for i in range(2):
            P.op("dve", lambda e, rp=rawpad[i]: e.memset(rp[:, 0:3], 0.0), w=[("rawpad", i)])
        jt = 0
        defer_p = []
        for bi, (c0, ncol) in enumerate(blocks):
            Wv, kW = wcur
            if bi + 1 < len(blocks):
                wcur = wload(wview(w_in, blocks[bi + 1][0], blocks[bi + 1][1]))
            for jj in range(4):
                j = (c0 - COL_X) // 128 + jj
                rp = rawpad[jt % 2]
                krp = ("rawpad", jt % 2)
                isC = j >= 10
                pend_p = list(defer_p)
                del defer_p[:]
                for nb in range(2):
                    if isC and nb == 0:
                        continue
                    bank = 3 + (jt * 2 + nb) % 4
                    mm_group(pbf[bank], [(Wv[:, k, jj * 128:(jj + 1) * 128], nPT[:, k, nb * 512:(nb + 1) * 512]) for k in range(16)],
                             r=[kW] + [("nPT", tt) for tt in range(nb * 4, nb * 4 + 4)], w=[("ps", bank)])
                    P.op("act", lambda e, rp=rp, nb=nb, bank=bank: e.activation(out=rp[:, 3 + nb * 512:3 + (nb + 1) * 512], in_=pbf[bank], func=AF.Copy),
                         r=[("ps", bank)], w=[krp])
                    if not isC:
                        P.op("act", lambda e, ac=accb[jt % 2], nb=nb, bank=bank, j=j: e.activation(out=ac[:, nb * 512:(nb + 1) * 512], in_=pbf[bank], func=AF.Identity,
                                                                                                scale=cwb[:, j, 3:4], bias=cwb[:, j, 4:5]),
                             r=[("ps", bank), KC], w=[("acc", jt % 2)])
                for so in pend_p:
                    P.ops.extend(so)
                P.op("dve", lambda e, rp=rp, j=j: e.tensor_copy(out=prefix[:, j, :], in_=rp[:, TM:TM + 3]), r=[krp], w=["prefix"])
                if not isC:
                    conv_silu(rp, krp, accb[jt % 2], ("acc", jt % 2), j, TM, xTp[:, j, :], ("xTp",), defer=defer_p)
                jt += 1
        for so in defer_p:
            P.ops.extend(so)
        for tt in range(8):
            mm_group(pbf[0][:, 0:16], [(nPT[:, k, tt * 128:(tt + 1) * 128], Wdt[:, k, :]) for k in range(16)], r=["Wdt", ("nPT", tt)], w=[("ps", 0)])
            P.op("act", lambda e, tt=tt: e.activation(out=dtraw_p[:, tt, :], in_=pbf[0][:, 0:16], func=AF.Copy), r=[("ps", 0)], w=["dtraw_p"])
        P.barrier()

        def pchunks():
            ssd_emit([ssd_chunk(tP, xTp, c, dtraw_p[:, c, :], "dtraw_p", main=False, banks=(5, 6, 5, 7, 6), ktag="p") for c in range(8)])
            P.op("dve", lambda e: e.tensor_scalar(out=hT, in0=hT, scalar1=flag, scalar2=None, op0=ALU.mult), r=["hT", KC], w=["hT"])
            P.op("act", lambda e: e.activation(out=hTb, in_=hT, func=AF.Copy), r=["hT"], w=["hTb"])
        pch_ops = P.capture(pchunks)
        _saved_ops = P.ops
        P.ops = []
        sc.reset(m_common)
        stgA = []
        for tt in range(9):
            i = tt % 2
            rows = 128 if tt < 8 else NS
            src = xm[tt * 128:(tt + 1) * 128, :] if tt < 8 else xsmp[:, :]
            ix = tt % NX
            pre = (lambda ix=ix, rows=rows, src=src: P.op("sp", lambda e: e.dma_start(out=X[ix][:rows, :], in_=src), w=[("X", ix)], dma=True))
            stgA.append(rms_transpose(X[ix], ("X", ix), rows, XN[i], ("XN", i), ST[i], ("ST", i), gbc, "gbc", nT, tt * 128, ("nT", tt), DM, tt, pre=pre))
        rms_pipeline(stgA)
        P.barrier()
        sc.reset()
        xT = sc.bf16([12, T])
        z_tok = sc.bf16([8, 1024])
        dtraw = sc.f32([8, 16])
        ssdg_bc = sc.f32([1024])
        raw_s = sc.f32([12, NS])
        m_A = sc.mark()
        Wdt = sc.bf16([16, 16])
        rawpad = [sc.f32([TM + 4]) for _ in range(2)]
        accb = [sc.f32([TM]) for _ in range(2)]
        scTok = sc.f32([3 * 1536])
        scT = sc.f32([36, NS])
        acc_s = sc.f32([12, NS])
        P.op("sp", lambda e: e.dma_start(out=ssdg_bc, in_=ssd_g.partition_broadcast(128)), w=["ssdg"], dma=True)
        P.op("sp", lambda e: e.dma_start(out=Rsel, in_=cst2_d[:, C2_RSEL:C2_RSEL + 1024]), w=["cst2"], dma=True)
        P.op("sp", lambda e: e.dma_start(out=scTok[:NS, :], in_=sconv.rearrange("b r c -> b (r c)")), w=["scTok"], dma=True)
        P.op("sp", lambda e: e.dma_start(out=nconv_s[:, 0:2, :], in_=sconv[:, 1:3, :]), w=["o_ncs01"], dma=True)
        OUTKEYS.append("o_ncs01")
        for half in range(2):
            def trs(e, half=half):
                ins = None
                for idx in range(half * 18, half * 18 + 18):
                    r_, j_ = idx // 12, idx % 12
                    ii = idx - half * 18
                    ins = e.transpose(out=pbf[half][:, ii * NS:(ii + 1) * NS], in_=scTok[:NS, r_ * 1536 + j_ * 128:r_ * 1536 + (j_ + 1) * 128],
                                      identity=identF[:NS, :NS])
                return ins
            P.op("pe", trs, r=["scTok", KC], w=[("ps", half)])
            P.op("dve", lambda e, half=half: e.tensor_copy(out=scT[:, half * 18:half * 18 + 18, :].rearrange("p a b -> p (a b)"), in_=pbf[half][:, 0:18 * NS]),
                 r=[("ps", half)], w=["scT"])
        blocksA = [(COL_Z, "z"), (COL_Z + 512, "z"), (COL_X, "x"), (COL_X + 512, "x"), (COL_B, "x")]
        load_w(Wdt, wview(w_in, COL_DT, 16), "Wdt")
        wcur = wload(wview(w_in, blocksA[0][0], 512))
        jt = 0
        grp = 0
        defer_a = []
        _zwin = [len(P.ops), None]
        for bi, (c0, kind) in enumerate(blocksA):
            Wv, kW = wcur
            if kind == "x" and _zwin[1] is None:
                _zwin[1] = len(P.ops)
            if bi + 1 < len(blocksA):
                wcur = wload(wview(w_in, blocksA[bi + 1][0], 512))
            if kind == "z":
                zb = (c0 - COL_Z) // 512
                for tt in range(8):
                    bank = 1 + grp % 4
                    grp += 1
                    mm_group(pbf[bank], [(nT[:, k, tt * 128:(tt + 1) * 128], Wv[:, k, :]) for k in range(16)], r=[kW, ("nT", tt)], w=[("ps", bank)])
                    P.op("act", lambda e, tt=tt, zb=zb, bank=bank: e.activation(out=z_tok[:, tt, zb * 512:(zb + 1) * 512], in_=pbf[bank], func=AF.Silu),
                         r=[("ps", bank)], w=["z_tok"])
                for jj in range(4):
                    bank = 1 + grp % 4
                    grp += 1
                    j = zb * 4 + jj
                    mm_group(pbf[bank][:, 0:NS], [(Wv[:, k, jj * 128:(jj + 1) * 128], nT[:, k, TM:T]) for k in range(16)], r=[kW, ("nT", 8)], w=[("ps", bank)])
                    P.op("act", lambda e, j=j, bank=bank: e.activation(out=zT_s[:, j, :], in_=pbf[bank][:, 0:NS], func=AF.Silu), r=[("ps", bank)], w=["zT_s"])
            else:
                for jj in range(4):
                    j = (c0 - COL_X) // 128 + jj
                    rp = rawpad[jt % 2]
                    krp = ("rawpad", jt % 2)
                    pend_a = list(defer_a)
                    del defer_a[:]
                    P.op("dve", lambda e, rp=rp, j=j: e.tensor_copy(out=rp[:, 0:3], in_=prefix[:, j, :]), r=["prefix"], w=[krp])
                    for nb in range(3):
                        bank = 1 + grp % 4
                        grp += 1
                        lo, hi = (nb * 512, (nb + 1) * 512) if nb < 2 else (TM, T)
                        n = hi - lo
                        rk = [("nT", tt) for tt in range(nb * 4, nb * 4 + 4)] if nb < 2 else [("nT", 8)]
                        mm_group(pbf[bank][:, 0:n], [(Wv[:, k, jj * 128:(jj + 1) * 128], nT[:, k, lo:hi]) for k in range(16)], r=[kW] + rk, w=[("ps", bank)])
                        if nb < 2:
                            P.op("act", lambda e, rp=rp, lo=lo, hi=hi, bank=bank: e.activation(out=rp[:, 3 + lo:3 + hi], in_=pbf[bank], func=AF.Copy),
                                 r=[("ps", bank)], w=[krp])
                            P.op("act", lambda e, ac=accb[jt % 2], lo=lo, hi=hi, bank=bank, j=j: e.activation(out=ac[:, lo:hi], in_=pbf[bank], func=AF.Identity,
                                                                                                    scale=cwb[:, j, 3:4], bias=cwb[:, j, 4:5]),
                                 r=[("ps", bank), KC], w=[("acc", jt % 2)])
                        else:
                            P.op("act", lambda e, j=j, bank=bank: e.activation(out=raw_s[:, j, :], in_=pbf[bank][:, 0:NS], func=AF.Copy),
                                 r=[("ps", bank)], w=["raw_s"])
                    for so in pend_a:
                        P.ops.extend(so)
                    P.op("dve", lambda e, rp=rp, j=j: e.tensor_copy(out=ncv[:, j, :], in_=rp[:, TM:TM + 3]), r=[krp], w=["ncv"])
                    conv_silu(rp, krp, accb[jt % 2], ("acc", jt % 2), j, TM, xT[:, j, 0:TM], ("xT",), defer=defer_a)
                    P.op("dve", lambda e, j=j: e.tensor_scalar(out=acc_s[:, j, :], in0=raw_s[:, j, :], scalar1=cwb[:, j, 3:4], scalar2=cwb[:, j, 4:5],
                                                               op0=ALU.mult, op1=ALU.add), r=["raw_s", KC], w=["acc_s"])
                    for r_ in range(3):
                        P.op("dve", lambda e, j=j, r_=r_: e.scalar_tensor_tensor(out=acc_s[:, j, :], in0=scT[:, r_ * 12 + j, :], scalar=cwb[:, j, r_:r_ + 1],
                                                                                 op0=ALU.mult, in1=acc_s[:, j, :], op1=ALU.add), r=["scT", "acc_s", KC], w=["acc_s"])
                    jt += 1
        for so in defer_a:
            P.ops.extend(so)
        P.op("act", lambda e: e.activation(out=xT_s, in_=acc_s, func=AF.Silu), r=["acc_s"], w=["xT_s"])
        for tt in range(8):
            mm_group(pbf[0][:, 0:16], [(nT[:, k, tt * 128:(tt + 1) * 128], Wdt[:, k, :]) for k in range(16)], r=["Wdt", ("nT", tt)], w=[("ps", 0)])
            P.op("act", lambda e, tt=tt: e.activation(out=dtraw[:, tt, :], in_=pbf[0][:, 0:16], func=AF.Copy), r=[("ps", 0)], w=["dtraw"])
        mm_group(pbf[1][:NS, 0:NS], [(Wdt[:, k, :], nT[:, k, TM:T]) for k in range(16)], r=["Wdt", ("nT", 8)], w=[("ps", 1)])
        P.op("act", lambda e: e.activation(out=dtT_s[:NS, :], in_=pbf[1][:NS, 0:NS], func=AF.Copy), r=[("ps", 1)], w=["dtT_s"])

        def rows_out(src3, ksrc, R, stage, kst, dram_ap, okey):
            for b3 in range(3):
                def trr(e, b3=b3):
                    ins = None
                    for jj in range(4):
                        j_ = b3 * 4 + jj
                        ins = e.transpose(out=pbf[2 + b3][:R, jj * 128:(jj + 1) * 128], in_=src3[:, j_, :], identity=identF)
                    return ins
                P.op("pe", trr, r=[ksrc, KC], w=[("ps", 2 + b3)])
                P.op("dve", lambda e, b3=b3: e.tensor_copy(out=stage[:R, b3 * 512:(b3 + 1) * 512], in_=pbf[2 + b3][:R, :]), r=[("ps", 2 + b3)], w=[kst, "scTok"])
            P.op("sp", lambda e: e.dma_start(out=dram_ap, in_=stage[:R, 0:1536]), r=[kst], w=[okey], dma=True)
            OUTKEYS.append(okey)
        rows_out(raw_s, "raw_s", NS, scTok[:, 0:1536], "stg0", nconv_s[:, 2, :], "o_ncs2")
        rows_out(ncv, "ncv", 3, scTok[:, 1536:3072], "stg1", nconv_p[:, :], "o_ncp")
        _main_ops = P.ops
        P.ops = _saved_ops
        P.ops.extend(Prog.merge(_main_ops, pch_ops, _zwin[0], _zwin[1]))
        P.barrier()
        if STOP_AFTER == "A1":
            dbg_dump("xT", xT.rearrange("p a b -> p (a b)"), [("xT",)])
            dbg_dump("z_tok", z_tok.rearrange("p a b -> p (a b)"), ["z_tok"])
            dbg_dump("zT_s", zT_s.rearrange("p a b -> p (a b)"), ["zT_s"])
            dbg_dump("dtraw", dtraw.rearrange("p a b -> p (a b)"), ["dtraw"])
            dbg_dump("dtT_s", dtT_s, ["dtT_s"])
            finish()
            return nc

        sc.reset(m_A)
        tA = ssd_temps(sc)
        ssd_emit([ssd_chunk(tA, xT, c, dtraw[:, c, :], "dtraw", main=True, ztok_c=z_tok[:, c, :], ssdg_bc=ssdg_bc) for c in range(8)])
        if STOP_AFTER == "A2a":
            P.barrier()
            dbg_dump("YT", YT.rearrange("p a b -> p (a b)"), [("YTb", c) for c in range(8)])
            finish()
            return nc
        hout = sc.f32([8, 128])
        for half in range(2):
            def trh(e, half=half):
                ins = None
                for jj in range(4):
                    q = half * 4 + jj
                    ins = e.transpose(out=pbf[3 + half][:, jj * 128:(jj + 1) * 128], in_=hT[:, q * 128:(q + 1) * 128], identity=identF)
                return ins
            P.op("pe", trh, r=["hT", KC], w=[("ps", 3 + half)])
            P.op("dve", lambda e, half=half: e.tensor_copy(out=hout[:, half * 4:(half + 1) * 4, :].rearrange("p a b -> p (a b)"), in_=pbf[3 + half]),
                 r=[("ps", 3 + half)], w=["hout"])
        P.op("sp", lambda e: e.dma_start(out=nssm_p.rearrange("(q r) n -> r q n", r=128), in_=hout), r=["hout"], w=["o_nsp"], dma=True)
        OUTKEYS.append("o_nsp")

        if STOP_AFTER == "A2b":
            P.barrier()
            dbg_dump("YT", YT.rearrange("p a b -> p (a b)"), [("YTb", c) for c in range(8)])
            finish()
            return nc
        P.barrier()
        SMPW = 7300
        scS = Bump(arena, AW - SMPW, AW)
        _saved_main = P.ops
        P.ops = []
        _sc_main = sc
        sc = scS
        sm_s = sc.f32([8, NS])
        cat = sc.f32([32])
        rep = sc.f32([8, 32])
        dtx = sc.f32([8, NS])
        ys = sc.f32([8, NS])
        ysq = sc.f32([8, NS])
        rs = sc.f32([2, NS])
        BC_tok = sc.bf16([512])
        selB = sc.bf16([NS, 128])
        NHS = 3
        hs = [sc.f32([8, 128]) for _ in range(NHS)]
        tmpS = [sc.f32([1024]) for _ in range(2)]
        x0 = sm_s[:NS, 0, :]; e1 = sm_s[:NS, 1, :]; anp = sm_s[:NS, 2, 0:1]
        P.op("act", lambda e: e.activation(out=e1, in_=dtT_s[:NS, :], func=AF.Exp, bias=dtb_pp[:NS, :]), r=["dtT_s", KC], w=["s_e1"])
        P.op("act", lambda e: e.activation(out=cat[:NS, 16:32], in_=e1, func=AF.Ln, bias=1.0), r=["s_e1"], w=["s_dt"])
        P.op("act", lambda e: e.activation(out=anp, in_=al_pp[:NS, :], func=AF.Exp), r=[KC], w=["s_anp0"])
        P.op("dve", lambda e: e.tensor_scalar(out=anp, in0=anp, scalar1=-1.0, scalar2=None, op0=ALU.mult), r=["s_anp0"], w=["s_anp"])
        P.op("act", lambda e: e.activation(out=cat[:NS, 0:16], in_=cat[:NS, 16:32], func=AF.Exp, scale=anp), r=["s_dt", "s_anp"], w=["s_dA"])

        def repf(e):
            ins = None
            for q in range(8):
                ins = e.matmul(pbf[7][:, q * 32:(q + 1) * 32], lhsT=Rsel[:NS, q * 128:(q + 1) * 128], rhs=cat[:NS, :], start=True, stop=True)
            return ins
        P.op("pe", repf, r=["s_dA", "s_dt", "cst2"], w=[("ps", 7)])
        P.op("dve", lambda e: e.tensor_copy(out=rep.rearrange("p a b -> p (a b)"), in_=pbf[7][:, 0:256]), r=[("ps", 7)], w=["rep"])
        xs_s = xT_s[:, 0:8, :]
        P.op("dve", lambda e: e.tensor_tensor(out=dtx, in0=rep[:, :, 16:32], in1=xs_s, op=ALU.mult), r=["rep", "xT_s"], w=["dtx"])
        psbc = pbb[7][:NS, 0:512].rearrange("p (a b) -> p a b", a=4)

        def trbc(e):
            ins = None
            for jj in range(4):
                ins = e.transpose(out=psbc[:, jj, :], in_=xT_s[:, 8 + jj, :], identity=identB)
            return ins
        P.op("pe", trbc, r=["xT_s", "identB"], w=[("ps", 7)])
        P.op("act", lambda e: e.activation(out=BC_tok[:NS, :], in_=pbb[7][:NS, 0:512], func=AF.Copy), r=[("ps", 7)], w=["BC_tok"])
        P.op("dve", lambda e: e.tensor_copy(out=selB[:NS, :, :], in_=identB[:NS, 0:NS].unsqueeze(2).broadcast_to([NS, NS, 128])), r=["identB"], w=["selB"])
        P.op("dve", lambda e: e.memset(ys, 0.0), w=["ys"])

        def hs_load(b):
            P.op("sp", lambda e, b=b: e.dma_start(out=hs[b % NHS], in_=sssm[b].rearrange("(q r) n -> r q n", r=128)), w=[("hs", b % NHS, q) for q in range(8)], dma=True)
        for b in range(min(NHS - 1, NS)):
            hs_load(b)
        for b in range(NS):
            hb = hs[b % NHS]
            bank = 5 + b % 2
            if b + NHS - 1 < NS:
                hs_load(b + NHS - 1)
            P.op("pe", lambda e, b=b, bank=bank: e.matmul(pbf[bank], lhsT=selB[:NS, b, :], rhs=BC_tok[:NS, :], start=True, stop=True),
                 r=["selB", "BC_tok"], w=[("ps", bank)])
            for q in range(8):
                P.op("act", lambda e, b=b, q=q, hb=hb: e.activation(out=hb[:, q, :], in_=hb[:, q, :], func=AF.Copy, scale=rep[:, q, b:b + 1]),
                     r=[("hs", b % NHS, q), "rep"], w=[("hs", b % NHS, q)])

            bc4 = pbf[bank][:, 0:256].rearrange("p (g n) -> p g n", g=2).unsqueeze(2).broadcast_to([128, 2, 4, 128])
            cc4 = pbf[bank][:, 256:512].rearrange("p (g n) -> p g n", g=2).unsqueeze(2).broadcast_to([128, 2, 4, 128])
            hb4 = hb.rearrange("p (g j) n -> p g j n", g=2)
            tm4 = tmpS[b % 2].rearrange("p (g j n) -> p g j n", g=2, j=4)
            dx4 = dtx[:, :, b:b + 1].rearrange("p (g j) o -> p g j o", g=2).broadcast_to([128, 2, 4, 128])
            khs = [("hs", b % NHS, q) for q in range(8)]
            ktm = ("tmpS", b % 2)
            P.op("dve", lambda e, tm4=tm4, bc4=bc4, dx4=dx4: e.tensor_tensor(out=tm4, in0=bc4, in1=dx4, op=ALU.mult), r=[("ps", bank), "dtx"], w=[ktm])
            P.op("dve", lambda e, hb4=hb4, tm4=tm4: e.tensor_tensor(out=hb4, in0=hb4, in1=tm4, op=ALU.add), r=khs + [ktm], w=khs)
            P.op("dve", lambda e, hb4=hb4, tm4=tm4, cc4=cc4: e.tensor_tensor(out=tm4, in0=cc4, in1=hb4, op=ALU.mult), r=khs + [("ps", bank), ktm], w=[ktm])
            P.op("dve", lambda e, b=b, tm=tmpS[b % 2]: e.tensor_reduce(out=ys[:, :, b:b + 1], in_=tm.rearrange("p (q n) -> p q n", q=8), axis=AX.X, op=ALU.add),
                 r=[ktm], w=["ys"])
            ok = ("o_nss", b)
            P.op("sp", lambda e, b=b, hb=hb: e.dma_start(out=nssm_s[b].rearrange("(q r) n -> r q n", r=128), in_=hb), r=[("hs", b % NHS, q) for q in range(8)], w=[ok], dma=True)
            OUTKEYS.append(ok)
        P.op("dve", lambda e: e.tensor_tensor(out=ysq, in0=xs_s, in1=D_pp.unsqueeze(2).broadcast_to([128, 8, NS]), op=ALU.mult), r=["xT_s", KC], w=["ysq"])
        P.op("dve", lambda e: e.tensor_tensor(out=ys, in0=ys, in1=ysq, op=ALU.add), r=["ys", "ysq"], w=["ys"])
        P.op("dve", lambda e: e.tensor_tensor(out=ys, in0=ys, in1=zT_s, op=ALU.mult), r=["ys", "zT_s"], w=["ys"])
        P.op("dve", lambda e: e.tensor_tensor(out=ysq, in0=ys, in1=ys, op=ALU.mult), r=["ys", "ysq"], w=["ysq"])

        def gsum(e):
            ins = None
            for q in range(8):
                g = q // 4
                ins = e.matmul(pbf[7][:, g * NS:(g + 1) * NS], lhsT=onesF, rhs=ysq[:, q, :], start=(q % 4 == 0), stop=(q % 4 == 3))
            return ins
        P.op("pe", gsum, r=["ysq", KC], w=[("ps", 7)])
        rsf = rs.rearrange("p a b -> p (a b)")
        P.op("act", lambda e: e.activation(out=rsf, in_=pbf[7][:, 0:2 * NS], func=AF.Ln, scale=1.0 / 512, bias=EPS), r=[("ps", 7)], w=["rs0"])
        P.op("act", lambda e: e.activation(out=rsf, in_=rsf, func=AF.Exp, scale=-0.5), r=["rs0"], w=["rs"])
        for g in range(2):
            P.op("dve", lambda e, g=g: e.tensor_tensor(out=ys[:, g * 4:(g + 1) * 4, :], in0=ys[:, g * 4:(g + 1) * 4, :],
                                                       in1=rs[:, g:g + 1, :].broadcast_to([128, 4, NS]), op=ALU.mult), r=["ys", "rs"], w=["ys"])
        P.op("dve", lambda e: e.tensor_tensor(out=YT[:, 8:16, TM:T], in0=ys, in1=sg_pp.unsqueeze(2).broadcast_to([128, 8, NS]), op=ALU.mult),
             r=["ys", KC], w=[("YTb", 8)])
        smp_ops = P.ops
        P.ops = _saved_main
        sc = _sc_main

        sc.reset()
        sc.end = AW - SMPW
        _saved_a3 = P.ops
        P.ops = []
        vng_bc = sc.f32([1024])
        bs_bc = sc.f32([8, 128])
        v_tok = sc.bf16([9, 1024])
        WsT = sc.bf16([8, 128])
        sqacc = sc.f32([T])
        gv = [sc.f32([1024]) for _ in range(2)]
        wsraw = gv[0].rearrange("p (a b) -> p a b", a=8)
        vst = [sc.f32([4]) for _ in range(2)]
        ug = [sc.f32([T]) for _ in range(2)]
        _m_tm = sc.mark()
        tmpm = [sc.f32([512]) for _ in range(2)]
        _m_tm2 = sc.mark()
        sc.reset(_m_tm)
        vsf = sc.f32([1024])
        sc.reset(_m_tm2)
        sq = [sc.bf16([T]) for _ in range(2)]
        rstd_bc = sqacc
        w00I = sc.bf16([8, NS])
        P.op("sp", lambda e: e.dma_start(out=vng_bc, in_=vn_g.partition_broadcast(128)), w=["vng"], dma=True)
        P.op("sp", lambda e: e.dma_start(out=bs_bc.rearrange("p a b -> p (a b)"), in_=cst2_d[:, C2_BS:C2_BS + 1024]), w=["bs_bc"], dma=True)
        P.op("sp", lambda e: e.dma_start(out=wsraw, in_=ws_d.rearrange("h t s -> t h s")), w=["wsraw"], dma=True)
        wv2 = [wload(wview(w_in, COL_V, 512)), wload(wview(w_in, COL_V + 512, 512))]
        for half in range(2):
            def trw(e, half=half):
                ins = None
                for jj in range(4):
                    h_ = half * 4 + jj
                    ins = e.transpose(out=pbf[half][:, jj * 128:(jj + 1) * 128], in_=wsraw[:, h_, :], identity=identF)
                return ins
            P.op("pe", trw, r=["wsraw", KC], w=[("ps", half)])
            P.op("dve", lambda e, half=half: e.tensor_tensor(out=WsT[:, half * 4:(half + 1) * 4, :], in0=pbf[half].rearrange("p (a b) -> p a b", a=4),
                                                             in1=triF.unsqueeze(1).broadcast_to([128, 4, 128]), op=ALU.mult), r=[("ps", half), KC], w=["WsT"])
        for h_ in range(8):
            P.op("dve", lambda e, h_=h_: e.tensor_scalar(out=w00I[:NS, h_, :], in0=identF[:NS, 0:NS], scalar1=w00_bc[:NS, h_:h_ + 1], scalar2=None, op0=ALU.mult),
                 r=[KC], w=["w00I"])
        for tt in range(9):
            rows = 128 if tt < 8 else NS
            i = tt % 2
            tcols = slice(tt * 128, tt * 128 + rows)
            for zb in range(2):
                bank = 1 + (tt * 2 + zb) % 4
                mm_group(pbf[bank][:rows, :], [(nT[:, k, tcols], wv2[zb][0][:, k, :]) for k in range(16)], r=[wv2[zb][1], ("nT", tt)], w=[("ps", bank)])
                P.op("act", lambda e, i=i, zb=zb, bank=bank, rows=rows: e.activation(out=gv[i][:rows, zb * 512:(zb + 1) * 512], in_=pbf[bank][:rows, :],
                                                                                    func=AF.Gelu_apprx_tanh), r=[("ps", bank)], w=[("gv", i, zb)])
            P.op("act", lambda e, i=i, rows=rows: e.activation(out=ug[i][:rows, 0:1024], in_=gv[i][:rows, :], func=AF.Square, accum_out=vst[i][:rows, 0:1]),
                 r=[("gv", i, 0), ("gv", i, 1)], w=[("ug", i), ("vst", i)])
            P.op("act", lambda e, i=i, rows=rows: e.activation(out=vst[i][:rows, 1:2], in_=vst[i][:rows, 0:1], func=AF.Ln, scale=1.0 / 1024, bias=EPS),
                 r=[("vst", i)], w=[("vst", i)])
            P.op("act", lambda e, i=i, rows=rows: e.activation(out=vst[i][:rows, 2:3], in_=vst[i][:rows, 1:2], func=AF.Exp, scale=-0.5),
                 r=[("vst", i)], w=[("vst", i)])
            P.op("dve", lambda e, i=i, rows=rows, tt=tt: e.scalar_tensor_tensor(out=v_tok[:rows, tt, :], in0=gv[i][:rows, :], scalar=vst[i][:rows, 2:3], op0=ALU.mult,
                                                                                in1=vng_bc[:rows, :], op1=ALU.mult),
                 r=[("gv", i, 0), ("gv", i, 1), ("vst", i), "vng"], w=[("v_tok", tt)])
            if tt == 8:
                P.op("dve", lambda e, i=i: e.scalar_tensor_tensor(out=vsf[:NS, :], in0=gv[i][:NS, :], scalar=vst[i][:NS, 2:3], op0=ALU.mult,
                                                                  in1=vng_bc[:NS, :], op1=ALU.mult), r=[("gv", i, 0), ("gv", i, 1), ("vst", i), "vng"], w=["vsf", ("tmpm", 0), ("tmpm", 1)])
                P.op("sp", lambda e: e.dma_start(out=v_s[:, :], in_=vsf[:NS, :]), r=["vsf", ("tmpm", 0), ("tmpm", 1)], w=["o_vs"], dma=True)
                OUTKEYS.append("o_vs")
        wu2 = [wload(wview(w_in, COL_U, 512)), wload(wview(w_in, COL_U + 512, 512))]
        tok_blocks = [(0, 348), (348, 696), (696, T)]
        grpc = [0]

        def stage_G(h_):
            ub, jj = h_ // 4, h_ % 4
            Wv, kWu = wu2[ub]
            ui = h_ % 2
            for nb, (lo, hi) in enumerate(tok_blocks):
                bank = grpc[0] % 2
                grpc[0] += 1
                n = hi - lo
                rk = [("nT", tt) for tt in range(lo // 128, min(8, (hi - 1) // 128) + 1)]
                mm_group(pbf[bank][:, 0:n], [(Wv[:, k, jj * 128:(jj + 1) * 128], nT[:, k, lo:hi]) for k in range(16)], r=[kWu] + rk, w=[("ps", bank)])
                P.op("act", lambda e, ui=ui, lo=lo, hi=hi, n=n, bank=bank: e.activation(out=ug[ui][:, lo:hi], in_=pbf[bank][:, 0:n], func=AF.Gelu_apprx_tanh),
                     r=[("ps", bank)], w=[("ug", ui)])

        def stage_M(h_):
            ui = h_ % 2
            for half in range(2):
                def mix(e, half=half, h_=h_):
                    ins = None
                    for cc in range(4):
                        c = half * 4 + cc
                        ins = e.matmul(pbf[2 + half][:, cc * 128:(cc + 1) * 128], lhsT=v_tok[:, c, h_ * 128:(h_ + 1) * 128], rhs=WsT[:, h_, :], start=True, stop=True)
                    return ins
                P.op("pe", mix, r=[("v_tok", c) for c in range(half * 4, half * 4 + 4)] + ["WsT"], w=[("ps", 2 + half)])
                tm = tmpm[half]
                P.op("dve", lambda e, half=half, h_=h_, tm=tm: e.tensor_tensor(out=tm.rearrange("p (a b) -> p a b", a=4), in0=pbf[2 + half].rearrange("p (a b) -> p a b", a=4),
                                                                               in1=bs_bc[:, h_:h_ + 1, :].broadcast_to([128, 4, 128]), op=ALU.add),
                     r=[("ps", 2 + half), "bs_bc"], w=[("tmpm", half)])
                P.op("dve", lambda e, half=half, h_=h_, tm=tm, ui=ui: e.tensor_tensor(out=YT[:, h_, half * 512:(half + 1) * 512], in0=tm, in1=ug[ui][:, half * 512:(half + 1) * 512], op=ALU.mult),
                     r=[("tmpm", half), ("ug", ui)], w=[("ya", h_)])
            P.op("pe", lambda e, h_=h_: e.matmul(pbf[4][:, 0:NS], lhsT=v_tok[:NS, 8, h_ * 128:(h_ + 1) * 128], rhs=w00I[:NS, h_, :], start=True, stop=True),
                 r=[("v_tok", 8), "w00I"], w=[("ps", 4)])
            P.op("dve", lambda e, h_=h_, ui=ui: e.scalar_tensor_tensor(out=YT[:, h_, TM:T], in0=pbf[4][:, 0:NS], scalar=b0_bc[:, h_:h_ + 1], op0=ALU.add, in1=ug[ui][:, TM:T], op1=ALU.mult),
                 r=[("ps", 4), ("ug", ui), KC], w=[("ya", h_)])
            si = h_ % 2
            if h_ == 0:
                P.op("act", lambda e, h_=h_: e.activation(out=sqacc, in_=YT[:, h_, :], func=AF.Square), r=[("ya", h_)], w=["sqacc"])
            else:
                P.op("act", lambda e, h_=h_, si=si: e.activation(out=sq[si], in_=YT[:, h_, :], func=AF.Square), r=[("ya", h_)], w=[("sq", si)])
                P.op("dve", lambda e, si=si: e.tensor_tensor(out=sqacc, in0=sqacc, in1=sq[si], op=ALU.add), r=["sqacc", ("sq", si)], w=["sqacc"])

        stage_G(0)
        for h_ in range(8):
            if h_ + 1 < 8:
                stage_G(h_ + 1)
            stage_M(h_)
        P.op("dve", lambda e: e.tensor_copy(out=sq[0], in_=sqacc), r=["sqacc", ("sq", 0)], w=[("sq", 0)])
        sbank = (0, 1, 4)
        for nb, (lo, hi) in enumerate(tok_blocks):
            n = hi - lo
            P.op("pe", lambda e, nb=nb, lo=lo, hi=hi, n=n: e.matmul(pbf[sbank[nb]][:, 0:n], lhsT=onesB, rhs=sq[0][:, lo:hi], start=True, stop=True),
                 r=[("sq", 0), "onesB"], w=[("ps", sbank[nb])])
        for nb, (lo, hi) in enumerate(tok_blocks):
            n = hi - lo
            P.op("act", lambda e, nb=nb, lo=lo, hi=hi, n=n: e.activation(out=rstd_bc[:, lo:hi], in_=pbf[sbank[nb]][:, 0:n], func=AF.Ln, scale=1.0 / 1024, bias=EPS),
                 r=[("ps", sbank[nb])], w=[("rstd", nb)])
            P.op("act", lambda e, lo=lo, hi=hi: e.activation(out=rstd_bc[:, lo:hi], in_=rstd_bc[:, lo:hi], func=AF.Exp, scale=-0.5), r=[("rstd", nb)], w=[("rstd", nb)])
        for h_ in range(8):
            P.op("dve", lambda e, h_=h_: e.scalar_tensor_tensor(out=YT[:, h_, :], in0=YT[:, h_, :], scalar=gon_pp[:, h_:h_ + 1], op0=ALU.mult, in1=rstd_bc, op1=ALU.mult),
                 r=[("ya", h_), ("rstd", 0), ("rstd", 1), ("rstd", 2), KC], w=[("YTa", h_)])
        _a3_ops = P.ops
        P.ops = _saved_a3
        P.ops.extend(Prog.merge(_a3_ops, smp_ops, 0, int(len(_a3_ops) * 0.8)))
        sc.end = AW
        P.barrier()
        if STOP_AFTER == "A3":
            dbg_dump("YT", YT.rearrange("p a b -> p (a b)"), [("YTa", h_) for h_ in range(8)])
            finish()
            return nc

        sc.reset()
        hres = sc.f32([9, DM])
        m_B = sc.mark()
        for tt in range(9):
            rows = 128 if tt < 8 else NS
            src = xm[tt * 128:(tt + 1) * 128, :] if tt < 8 else xsmp[:, :]
            P.op("sp", lambda e, tt=tt, rows=rows, src=src: e.dma_start(out=hres[:rows, tt, :], in_=src), w=[("h", tt, cb) for cb in range(4)], dma=True)
        wcur = wload(wview(w_out, 0, 512))
        for cb in range(4):
            Wv, kW = wcur
            if cb + 1 < 4:
                wcur = wload(wview(w_out, (cb + 1) * 512, 512))
            for tt in range(9):
                rows = 128 if tt < 8 else NS
                tcols = slice(tt * 128, tt * 128 + rows)
                bank = (cb * 9 + tt) % 6
                mm_group(pbf[bank][:rows, :], [(YT[:, k, tcols], Wv[:, k, :]) for k in range(16)], r=[kW], w=[("ps", bank)])
                P.op("dve", lambda e, rows=rows, tt=tt, cb=cb, bank=bank: e.tensor_tensor(out=hres[:rows, tt, cb * 512:(cb + 1) * 512], in0=pbf[bank][:rows, :],
                                                                                         in1=hres[:rows, tt, cb * 512:(cb + 1) * 512], op=ALU.add),
                     r=[("ps", bank), ("h", tt, cb)], w=[("h", tt, cb)])
        P.barrier()
        if STOP_AFTER == "B":
            dbg_dump("h", hres.rearrange("p a b -> p (a b)"), [("h", tt) for tt in range(9)])
            finish()
            return nc

        sc.reset(m_B)
        gbc = sc.f32([DM])
        stmp = [sc.f32([512]) for _ in range(2)]
        ST = [sc.f32([4]) for _ in range(2)]
        r2.reset()
        actb = [r2.bf16([2, T]) for _ in range(2)]
        Wd = [r2.bf16([2, DM]) for _ in range(2)]
        XN = [r2.bf16([DM]) for _ in range(2)]
        mT = nT
        P.op("sp", lambda e: e.dma_start(out=gbc, in_=g_ffn.partition_broadcast(128)), w=["gbc"], dma=True)
        stgC = []
        for tt in range(9):
            rows = 128 if tt < 8 else NS
            i = tt % 2
            stgC.append(rms_transpose(hres[:, tt, :], [("h", tt, cb) for cb in range(4)], rows, XN[i], ("XN", i), ST[i], ("ST", i), gbc, "gbc", mT, tt * 128, ("mT", tt), DM, tt))
        rms_pipeline(stgC)
        NFB = DFF // 256
        if STOP_AFTER == "C0":
            P.barrier()
            dbg_dump("mT", mT.rearrange("p a b -> p (a b)"), [("mT", tt) for tt in range(9)])
            dbg_dump("XN0", XN[0], [("XN", 0)])
            dbg_dump("ST0", ST[0], [("ST", 0)])
            dbg_dump("gbc", gbc, ["gbc"])
            dbg_dump("h", hres.rearrange("p a b -> p (a b)"), [("h", tt) for tt in range(9)])
            finish()
            return nc

        gu_slot = {}

        def load_gu(fb):
            slot = wq[0] % 2
            wq[0] += 1
            gu_slot[fb] = slot
            load_w(WG[slot], wview(w_gate, fb * 256, 256), ("W", slot), nobar=True)
            load_w(WU[slot], wview(w_up, fb * 256, 256), ("W", slot), nobar=True)

        def load_d(fb):
            load_w(Wd[fb % 2], w_down[fb * 256:(fb + 1) * 256, :].rearrange("(fl p) c -> p fl c", p=128), ("Wd", fb % 2))

        gctr = [0]

        ffn_blocks = [(0, 348), (348, 696), (696, T)]

        def GU_units(fb):
            slot = gu_slot[fb]
            Wg_, Wu_, kWs = WG[slot], WU[slot], ("W", slot)
            units = []
            for fl in range(2):
                for nb, (lo, hi) in enumerate(ffn_blocks):
                    def unit(fl=fl, nb=nb, lo=lo, hi=hi):
                        n = hi - lo
                        pr = gctr[0] % 2
                        gctr[0] += 1
                        bg, bu = pr * 2, pr * 2 + 1
                        rk = [("mT", tt) for tt in range(lo // 128, min(8, (hi - 1) // 128) + 1)]
                        mm_group(pbf[bg][:, 0:n], [(Wg_[:, k, fl * 128:(fl + 1) * 128], mT[:, k, lo:hi]) for k in range(16)], r=[kWs] + rk, w=[("ps", bg)])
                        mm_group(pbf[bu][:, 0:n], [(Wu_[:, k, fl * 128:(fl + 1) * 128], mT[:, k, lo:hi]) for k in range(16)], r=[kWs] + rk, w=[("ps", bu)])
                        P.op("act", lambda e: e.activation(out=stmp[pr][:, 0:n], in_=pbf[bg][:, 0:n], func=AF.Silu), r=[("ps", bg)], w=[("stmp", pr)])
                        P.op("dve", lambda e: e.tensor_tensor(out=actb[fb % 2][:, fl, lo:hi], in0=pbf[bu][:, 0:n], in1=stmp[pr][:, 0:n], op=ALU.mult),
                             r=[("ps", bu), ("stmp", pr)], w=[("act", fb % 2)])
                    units.append(P.capture(unit))
            return units

        dctr = [0]

        def DN_groups(fb):
            groups = []
            for tt in range(9):
                rows = 128 if tt < 8 else NS
                tcols = slice(tt * 128, tt * 128 + rows)
                for cb in range(4):
                    def grp_(tt=tt, rows=rows, tcols=tcols, cb=cb):
                        bank = 4 + dctr[0] % 4
                        dctr[0] += 1
                        mm_group(pbf[bank][:rows, :], [(actb[fb % 2][:, fl, tcols], Wd[fb % 2][:, fl, cb * 512:(cb + 1) * 512]) for fl in range(2)],
                                 r=[("act", fb % 2), ("Wd", fb % 2)], w=[("ps", bank)])
                        P.op("dve", lambda e: e.tensor_tensor(out=hres[:rows, tt, cb * 512:(cb + 1) * 512], in0=pbf[bank][:rows, :],
                                                              in1=hres[:rows, tt, cb * 512:(cb + 1) * 512], op=ALU.add),
                             r=[("ps", bank), ("h", tt, cb)], w=[("h", tt, cb)])
                    groups.append(P.capture(grp_))
            return groups

        load_gu(0)
        load_d(0)
        for i in range(NFB + 1):
            if i + 1 < NFB:
                load_gu(i + 1)
            gu = GU_units(i) if i < NFB else []
            dn = DN_groups(i - 1) if i >= 1 else []
            nslot = max(len(gu), 1)
            per = -(-len(dn) // nslot)
            for u in range(nslot):
                if u < len(gu):
                    P.ops.extend(gu[u])
                for g_ in dn[u * per:(u + 1) * per]:
                    P.ops.extend(g_)
            if i + 1 < NFB:
                load_d(i + 1)
        P.barrier()
        if STOP_AFTER == "C":
            dbg_dump("h", hres.rearrange("p a b -> p (a b)"), [("h", tt) for tt in range(9)])
            finish()
            return nc

        P.op("sp", lambda e: e.dma_start(out=gbc, in_=g_fin.partition_broadcast(128)), w=["gbc"], dma=True)
        for tt in range(9):
            rows = 128 if tt < 8 else NS
            i = tt % 2
            ht = hres[:, tt, :]
            st = ST[i]
            P.op("act", lambda e, rows=rows, ht=ht, st=st, i=i: e.activation(out=XN[i][:rows, :], in_=ht[:rows, :], func=AF.Square, accum_out=st[:rows, 0:1]),
                 r=[("h", tt, cb) for cb in range(4)], w=[("XN", i), ("ST", i)])
            P.op("act", lambda e, rows=rows, st=st: e.activation(out=st[:rows, 1:2], in_=st[:rows, 0:1], func=AF.Ln, scale=1.0 / DM, bias=EPS), r=[("ST", i)], w=[("ST", i)])
            P.op("act", lambda e, rows=rows, st=st: e.activation(out=st[:rows, 2:3], in_=st[:rows, 1:2], func=AF.Exp, scale=-0.5), r=[("ST", i)], w=[("ST", i)])
            P.op("dve", lambda e, rows=rows, ht=ht, st=st: e.scalar_tensor_tensor(out=ht[:rows, :], in0=ht[:rows, :], scalar=st[:rows, 2:3], op0=ALU.mult, in1=gbc[:rows, :], op1=ALU.mult),
                 r=[("h", tt, cb) for cb in range(4)] + [("ST", i), "gbc"], w=[("h", tt, cb) for cb in range(4)])
            dst = y_m[tt * 128:(tt + 1) * 128, :] if tt < 8 else y_s[:, :]
            ok = ("o_y", tt)
            P.op("sp", lambda e, rows=rows, ht=ht, dst=dst: e.dma_start(out=dst, in_=ht[:rows, :]), r=[("h", tt, cb) for cb in range(4)], w=[ok], dma=True)
            OUTKEYS.append(ok)
        finish()
        return nc


def _host_consts(inputs, core):
    hf = core % 2
    cst = np.zeros((128, CSTW), np.float32)
    r = np.arange(128)
    cst[:, C_ID:C_ID + 128] = np.eye(128, dtype=np.float32)
    cst[:, C_TRI:C_TRI + 128] = (r[:, None] <= r[None, :]).astype(np.float32)
    cst[:, C_U:C_U + 128] = (r[:, None] > r[None, :]).astype(np.float32)
    cst[:, C_ONE:C_ONE + 128] = 1.0
    cw = np.asarray(inputs["ssd_conv_w"])[0]
    cb = np.asarray(inputs["ssd_conv_b"])[0]
    cwb = np.concatenate([cw, cb[None]], 0)
    cst[:, C_CWB:C_CWB + 60] = cwb.reshape(5, 12, 128).transpose(2, 1, 0).reshape(128, 60)
    cst[:, C_GON:C_GON + 8] = np.asarray(inputs["chunk_out_norm_g"])[0].reshape(8, 128).T
    cst[:, C_ALOG:C_ALOG + 16] = np.asarray(inputs["ssd_a_log"])[0][None, :]
    cst[:, C_DTB:C_DTB + 16] = np.asarray(inputs["ssd_dt_bias"])[0][None, :]
    cst[:, C_D:C_D + 16] = np.asarray(inputs["ssd_d"])[0][None, :]
    cst[:, C_W00:C_W00 + 8] = np.asarray(inputs["chunk_w_s"])[0][:, 0, 0][None, :]
    cst[:, C_B0:C_B0 + 8] = np.asarray(inputs["chunk_b_s"])[0][:, 0][None, :]
    cst[:, C_FLAG] = float(hf)
    cst[:, C_DPP:C_DPP + 8] = np.repeat(np.asarray(inputs["ssd_d"])[0], 64).reshape(8, 128).T
    cst[:, C_SGPP:C_SGPP + 8] = np.asarray(inputs["ssd_norm_g"])[0].reshape(8, 128).T
    cst[0:16, C_DTBPP] = np.asarray(inputs["ssd_dt_bias"])[0]
    cst[0:16, C_ALPP] = np.asarray(inputs["ssd_a_log"])[0]
    cst2 = np.zeros((128, CST2W), np.float32)
    cst2[:, C2_BS:C2_BS + 1024] = np.asarray(inputs["chunk_b_s"])[0].reshape(1, 1024)
    rs = np.zeros((16, 8, 128), np.float32)
    for q in range(8):
        for rr in range(128):
            rs[2 * q + rr // 64, q, rr] = 1.0
    cst2[0:16, C2_RSEL:C2_RSEL + 1024] = rs.reshape(16, 1024)
    return cst, cst2


def make_in_maps(inputs):
    xp = np.asarray(inputs["x_prompt"], np.float32)
    xs = np.asarray(inputs["x_sample"], np.float32)
    sconv = np.asarray(inputs["state_conv"], np.float32)[0]
    sssm = np.asarray(inputs["state_ssm"], np.float32)[0]
    shared = {
        "w_in": np.ascontiguousarray(np.asarray(inputs["w_in"], np.float32)[0]),
        "w_out": np.ascontiguousarray(np.asarray(inputs["w_out"], np.float32)[0]),
        "w_gate": np.ascontiguousarray(np.asarray(inputs["w_gate"], np.float32)[0]),
        "w_up": np.ascontiguousarray(np.asarray(inputs["w_up"], np.float32)[0]),
        "w_down": np.ascontiguousarray(np.asarray(inputs["w_down"], np.float32)[0]),
        "ws": np.ascontiguousarray(np.asarray(inputs["chunk_w_s"], np.float32)[0]),
        "g_mix": np.ascontiguousarray(np.asarray(inputs["norm_mix_g"], np.float32)[0]),
        "g_ffn": np.ascontiguousarray(np.asarray(inputs["norm_ffn_g"], np.float32)[0]),
        "g_fin": np.ascontiguousarray(np.asarray(inputs["norm_final_g"], np.float32)),
        "vn_g": np.ascontiguousarray(np.asarray(inputs["chunk_v_norm_g"], np.float32)[0]),
        "ssd_g": np.ascontiguousarray(np.asarray(inputs["ssd_norm_g"], np.float32)[0]),
    }
    zeros_prev = np.zeros((TM, DM), np.float32)
    maps = []
    for c in range(8):
        b, hf = c // 2, c % 2
        cst, cst2 = _host_consts(inputs, c)
        m = dict(shared)
        m["xm"] = np.ascontiguousarray(xp[b, hf * TM:(hf + 1) * TM])
        m["xprev"] = np.ascontiguousarray(xp[b, 0:TM]) if hf == 1 else zeros_prev
        m["xsmp"] = np.ascontiguousarray(xs[c * NS:(c + 1) * NS, 0])
        m["sconv"] = np.ascontiguousarray(sconv[c * NS:(c + 1) * NS])
        m["sssm"] = np.ascontiguousarray(sssm[c * NS:(c + 1) * NS].reshape(NS, 1024, 128))
        m["cst"] = cst
        m["cst2"] = cst2
        maps.append(m)
    return maps


def kernel(**inputs):
    nc = build_program()
    maps = make_in_maps(inputs)
    res = run_bass_kernel_spmd(nc, maps, core_ids=list(range(8)))
    R = res.results
    y_prompt = np.zeros((4, 2048, DM), np.float32)
    y_sample = np.zeros((128, 1, DM), np.float32)
    ncp = np.zeros((1, 4, 3, 1536), np.float32)
    nsp = np.zeros((1, 4, 16, 64, 128), np.float32)
    ncs = np.zeros((1, 128, 3, 1536), np.float32)
    nss = np.zeros((1, 128, 16, 64, 128), np.float32)
    vs = np.zeros((1, 128, 1, 1024), np.float32)
    for c in range(8):
        b, hf = c // 2, c % 2
        y_prompt[b, hf * TM:(hf + 1) * TM] = R[c]["y_m"]
        y_sample[c * NS:(c + 1) * NS, 0] = R[c]["y_s"]
        if hf == 1:
            ncp[0, b] = R[c]["nconv_p"]
            nsp[0, b] = R[c]["nssm_p"].reshape(16, 64, 128)
        ncs[0, c * NS:(c + 1) * NS] = R[c]["nconv_s"]
        nss[0, c * NS:(c + 1) * NS] = R[c]["nssm_s"].reshape(NS, 16, 64, 128)
        vs[0, c * NS:(c + 1) * NS, 0] = R[c]["v_s"]
    return (y_prompt, y_sample, ncp, nsp, ncs, nss, vs)
```

```python
import numpy as np
from contextlib import ExitStack
import concourse.bass as bass
import concourse.mybir as mybir
from concourse.bass_utils import run_bass_kernel_spmd

F32 = mybir.dt.float32
F32R = mybir.dt.float32r
BF16 = mybir.dt.bfloat16
AF = mybir.ActivationFunctionType
ALU = mybir.AluOpType
AX = mybir.AxisListType

ENGS = ["pe", "act", "dve", "pool", "sp"]
NDMASEM = 12

DEBUG = {}
STOP_AFTER = None
SSD_LEVEL = 9


class Op:
    __slots__ = ("eng", "fn", "r", "w", "dma", "deps", "sig", "sem", "prev_use", "waits", "need_sig", "xdeps", "nobar")

    def __init__(self, eng, fn, r, w, dma, nobar=False):
        self.eng, self.fn, self.r, self.w, self.dma = eng, fn, tuple(r), tuple(w), dma
        self.deps = set()
        self.xdeps = set()
        self.sig = None
        self.sem = None
        self.prev_use = 0
        self.waits = []
        self.need_sig = False
        self.nobar = nobar


BARRIER = Op(None, None, (), (), False)


class Prog:
    def __init__(self):
        self.ops = []

    def op(self, eng, fn, r=(), w=(), dma=False, nobar=False):
        w = list(w) + [k for k in r if isinstance(k, tuple) and k and k[0] == "ps" and k not in w]
        o = Op(eng, fn, r, w, dma, nobar)
        self.ops.append(o)
        return o

    def barrier(self):
        self.ops.append(BARRIER)

    def capture(self, fn):
        saved = self.ops
        self.ops = []
        try:
            fn()
            got = self.ops
        finally:
            self.ops = saved
        return got

    @staticmethod
    def merge(main, side, lo=0, hi=None):
        if not side:
            return list(main)
        hi = len(main) if hi is None else hi
        out = []
        s_ = len(side)
        span = max(hi - lo, 1)
        j = 0
        for i, o in enumerate(main):
            out.append(o)
            if i >= lo:
                tgt = min(s_, (i + 1 - lo) * s_ // span)
                while j < tgt:
                    out.append(side[j])
                    j += 1
        out.extend(side[j:])
        return out

    def resolve(self):
        ops = self.ops
        last_w = {}
        readers = {}
        last_eng = {}
        dmas = []
        pending = {}
        for i, o in enumerate(ops):
            if o is BARRIER:
                s = set(last_eng.values()) | set(dmas)
                for e in ENGS:
                    pending[e] = set(s) | pending.get(e, set())
                continue
            if o.eng in pending and not o.nobar:
                o.xdeps = o.xdeps | pending.pop(o.eng)
            deps = set(o.xdeps)
            for k in o.r:
                if k in last_w:
                    deps.add(last_w[k])
            for k in o.w:
                if k in last_w:
                    deps.add(last_w[k])
                deps.update(readers.get(k, ()))
            deps.discard(i)
            keep = set()
            for d in deps:
                od = ops[d]
                if od.dma:
                    keep.add(d)
                elif od.eng == o.eng and not o.dma:
                    if o.eng == "pe":
                        continue
                    if d in o.xdeps:
                        continue
                    if set(od.w) & set(o.r):
                        keep.add(d)
                else:
                    keep.add(d)
            o.deps = keep
            for d in keep:
                ops[d].need_sig = True
            for k in o.r:
                readers.setdefault(k, []).append(i)
            for k in o.w:
                last_w[k] = i
                readers[k] = []
            if o.dma:
                dmas.append(i)
            elif o.fn is not None:
                last_eng[o.eng] = i
        cnt = {e: 0 for e in ENGS}
        dma_n = {e: 0 for e in ENGS}
        dma_use = {}
        for i, o in enumerate(ops):
            if o is BARRIER:
                continue
            if o.dma:
                slot = (o.eng, dma_n[o.eng] % NDMASEM)
                dma_n[o.eng] += 1
                u = dma_use.get(slot, 0)
                o.prev_use = u
                dma_use[slot] = u + 1
                o.sem = slot
                o.sig = 16 * (u + 1)
            elif o.need_sig:
                cnt[o.eng] += 1
                o.sem = o.eng
                o.sig = cnt[o.eng]
        seen = {e: {} for e in ENGS}
        for i, o in enumerate(ops):
            if o is BARRIER:
                continue
            sd = seen[o.eng]
            need = {}
            if o.dma and o.prev_use > 0:
                need[o.sem] = 16 * o.prev_use
            for d in o.deps:
                od = ops[d]
                need[od.sem] = max(need.get(od.sem, 0), od.sig)
            o.waits = []
            for s, v in need.items():
                if sd.get(s, 0) >= v:
                    continue
                sd[s] = v
                o.waits.append((s, v))

    def emit(self, sems, block):
        ops = self.ops

        def run(eng_name):
            def body(e):
                for o in ops:
                    if o is BARRIER or o.eng != eng_name:
                        continue
                    for s, v in o.waits:
                        e.wait_ge(sems[s], v)
                    if o.fn is None:
                        continue
                    ins = o.fn(e)
                    if o.dma:
                        ins.then_inc(sems[o.sem], 16)
                    elif o.sig is not None:
                        ins.then_inc(sems[o.sem], 1)
            return body

        block.sync(run("sp"))
        block.tensor(run("pe"))
        block.scalar(run("act"))
        block.vector(run("dve"))
        block.gpsimd(run("pool"))


DM = 2048
DIN = 4624
DFF = 5632
TM = 1024
NS = 16
T = TM + NS
EPS = 1e-6
COL_U, COL_V, COL_Z, COL_X, COL_B, COL_C, COL_DT = 0, 1024, 2048, 3072, 4096, 4352, 4608

C_ID, C_TRI, C_U, C_ONE = 0, 128, 256, 384
C_CWB, C_GON, C_ALOG, C_DTB, C_D, C_W00, C_B0, C_FLAG = 512, 572, 580, 596, 612, 628, 636, 644
C_DPP, C_SGPP, C_DTBPP, C_ALPP = 648, 656, 664, 665
CSTW = 672
C2_BS, C2_RSEL = 0, 1024
CST2W = 2048

AW = 51000


class Bump:
    def __init__(self, arena, start, end):
        self.arena, self.start, self.end, self.off = arena, start, end, start
        self.peak = start

    def reset(self, to=None):
        self.off = self.start if to is None else to

    def mark(self):
        return self.off

    def _take(self, words):
        o = self.off
        self.off += words
        assert self.off <= self.end, ("arena overflow", self.off, self.end)
        self.peak = max(self.peak, self.off)
        return o

    def f32(self, shape):
        n = int(np.prod(shape))
        o = self._take(n)
        v = self.arena[:, o:o + n]
        return _shape(v, shape)

    def bf16(self, shape):
        n = int(np.prod(shape))
        w = (n + 1) // 2
        o = self._take(w)
        v = self.arena[:, o:o + w].bitcast(BF16)[:, 0:n]
        return _shape(v, shape)


def _shape(v, shape):
    if len(shape) == 1:
        return v
    if len(shape) == 2:
        return v.rearrange("p (a b) -> p a b", a=shape[0])
    if len(shape) == 3:
        return v.rearrange("p (a b c) -> p a b c", a=shape[0], b=shape[1])
    raise ValueError(shape)


def build_program():
    nc = bass.Bass("TRN2", target_bir_lowering=False)

    def din(name, shape):
        return nc.dram_tensor(name, shape, F32, kind="ExternalInput").ap()

    def dout(name, shape):
        return nc.dram_tensor(name, shape, F32, kind="ExternalOutput").ap()

    xm = din("xm", [TM, DM])
    xprev = din("xprev", [TM, DM])
    xsmp = din("xsmp", [NS, DM])
    sconv = din("sconv", [NS, 3, 1536])
    sssm = din("sssm", [NS, 1024, 128])
    w_in = din("w_in", [DM, DIN])
    w_out = din("w_out", [DM, DM])
    w_gate = din("w_gate", [DM, DFF])
    w_up = din("w_up", [DM, DFF])
    w_down = din("w_down", [DFF, DM])
    cst_d = din("cst", [128, CSTW])
    cst2_d = din("cst2", [128, CST2W])
    ws_d = din("ws", [8, 128, 128])
    g_mix = din("g_mix", [DM])
    g_ffn = din("g_ffn", [DM])
    g_fin = din("g_fin", [DM])
    vn_g = din("vn_g", [1024])
    ssd_g = din("ssd_g", [1024])

    y_m = dout("y_m", [TM, DM])
    y_s = dout("y_s", [NS, DM])
    nconv_p = dout("nconv_p", [3, 1536])
    nssm_p = dout("nssm_p", [1024, 128])
    nconv_s = dout("nconv_s", [NS, 3, 1536])
    nssm_s = dout("nssm_s", [NS, 1024, 128])
    v_s = dout("v_s", [NS, 1024])
    dbg_d = {}
    for name, (shape, dty) in DEBUG.items():
        dbg_d[name] = nc.dram_tensor("dbg_" + name, list(shape), dty, kind="ExternalOutput").ap()

    P = Prog()
    with ExitStack() as es:
        arena = es.enter_context(nc.sbuf_tensor("arena", [128, AW], F32))
        rhsE = es.enter_context(nc.sbuf_tensor("rhsE", [128, 2048], F32))
        U32 = es.enter_context(nc.sbuf_tensor("U32", [128, 128], F32))
        pb = [es.enter_context(nc.psum_tensor(f"pb{i}", [128, 512], F32)) for i in range(8)]
        sems = {}
        for e in ENGS:
            sems[e] = es.enter_context(nc.semaphore("s_" + e))
        for e in ("sp", "pool"):
            for i in range(NDMASEM):
                sems[(e, i)] = es.enter_context(nc.semaphore(f"d_{e}_{i}"))
        block = es.enter_context(nc.Block())

        arena = arena[:, :]
        rhsEr = rhsE[:, :].bitcast(F32R)
        U32r = U32[:, :].bitcast(F32R)
        pbf = [p[:, :] for p in pb]
        pbb = [p[:, :].bitcast(BF16) for p in pb]

        fx = Bump(arena, 0, AW)
        CST = fx.f32([CSTW])
        identF = CST[:, C_ID:C_ID + 128]
        triF = CST[:, C_TRI:C_TRI + 128]
        UF = CST[:, C_U:C_U + 128]
        onesF = CST[:, C_ONE:C_ONE + 128]
        cwb = CST[:, C_CWB:C_CWB + 60].rearrange("p (j k) -> p j k", j=12)
        gon_pp = CST[:, C_GON:C_GON + 8]
        alog_bc = CST[:, C_ALOG:C_ALOG + 16]
        dtb_bc = CST[:, C_DTB:C_DTB + 16]
        D_bc = CST[:, C_D:C_D + 16]
        w00_bc = CST[:, C_W00:C_W00 + 8]
        b0_bc = CST[:, C_B0:C_B0 + 8]
        flag = CST[:, C_FLAG:C_FLAG + 1]
        D_pp = CST[:, C_DPP:C_DPP + 8]
        sg_pp = CST[:, C_SGPP:C_SGPP + 8]
        dtb_pp = CST[:, C_DTBPP:C_DTBPP + 1]
        al_pp = CST[:, C_ALPP:C_ALPP + 1]
        identB = fx.bf16([128])
        onesB = fx.bf16([128])
        aneg = fx.f32([16])
        hT = fx.f32([1024])
        hTb = fx.bf16([1024])
        prefix = fx.f32([12, 3])
        ncv = fx.f32([12, 3])
        nT = fx.bf16([16, T])
        YT = fx.bf16([16, T])
        r2_start = fx.off - (16 * T) // 2
        r2_end = fx.off
        xT_s = fx.bf16([12, NS])
        zT_s = fx.f32([8, NS])
        dtT_s = fx.f32([NS])
        Rsel = fx.f32([1024])
        Wfix = [fx.bf16([16, 512]) for _ in range(2)]
        _wflat = [w_.rearrange("p a b -> p (a b)") for w_ in Wfix]
        WG = [w_[:, 0:4096].rearrange("p (a b) -> p a b", a=16) for w_ in _wflat]
        WU = [w_[:, 4096:8192].rearrange("p (a b) -> p a b", a=16) for w_ in _wflat]
        wq = [0]
        S0 = fx.off
        sc = Bump(arena, S0, AW)
        r2 = Bump(arena, r2_start, r2_end)

        KC = ("cst",)

        P.op("sp", lambda e: e.dma_start(out=CST, in_=cst_d[:, :]), w=[KC], dma=True)
        P.op("dve", lambda e: e.tensor_copy(out=identB, in_=identF), r=[KC], w=["identB"])
        P.op("dve", lambda e: e.memset(onesB, 1.0), w=["onesB"])
        P.op("dve", lambda e: e.tensor_copy(out=U32r, in_=UF), r=[KC], w=["U32"])
        P.op("act", lambda e: e.activation(out=aneg, in_=alog_bc, func=AF.Exp), r=[KC], w=["aneg0"])
        P.op("dve", lambda e: e.tensor_scalar(out=aneg, in0=aneg, scalar1=-1.0, scalar2=None, op0=ALU.mult), r=["aneg0"], w=["aneg"])
        P.op("dve", lambda e: e.memset(hT, 0.0), w=["hT"])
        P.op("dve", lambda e: e.memset(hTb, 0.0), w=["hTb"])

        def rms_transpose(xt, kx, rows, xn, kxn, st, kst, gbc, kg, dstT, col0, kdst, width, tag, pre=None):
            nk = width // 128

            def s1():
                if pre is not None:
                    pre()
                P.op("act", lambda e: e.activation(out=xn[:rows, :], in_=xt[:rows, :], func=AF.Square, accum_out=st[:rows, 0:1]),
                     r=[kx] if not isinstance(kx, list) else kx, w=[kxn, kst])
                P.op("act", lambda e: e.activation(out=st[:rows, 1:2], in_=st[:rows, 0:1], func=AF.Ln, scale=1.0 / width, bias=EPS),
                     r=[kst], w=[kst])
                P.op("act", lambda e: e.activation(out=st[:rows, 2:3], in_=st[:rows, 1:2], func=AF.Exp, scale=-0.5),
                     r=[kst], w=[kst])
                P.op("dve", lambda e: e.scalar_tensor_tensor(out=xn[:rows, :], in0=xt[:rows, :], scalar=st[:rows, 2:3], op0=ALU.mult,
                                                             in1=gbc[:rows, :], op1=ALU.mult),
                     r=([kx] if not isinstance(kx, list) else kx) + [kst, kg, kxn], w=[kxn])

            def s2():
                for b8 in range(nk // 8):
                    bank = (tag % 2) * 2 + b8
                    psv = pbb[bank][:, 0:8 * rows].rearrange("p (a b) -> p a b", a=8)

                    def tr(e, b8=b8, psv=psv):
                        ins = None
                        for j in range(8):
                            k = b8 * 8 + j
                            ins = e.transpose(out=psv[:, j, :], in_=xn[:rows, k * 128:(k + 1) * 128], identity=identB[:rows, :rows])
                        return ins
                    P.op("pe", tr, r=[kxn, "identB"], w=[("ps", bank)])
                    dst = dstT[:, b8 * 8:(b8 + 1) * 8, col0:col0 + rows]
                    if b8 == 0:
                        P.op("act", lambda e, dst=dst, psv=psv: e.activation(out=dst, in_=psv, func=AF.Copy), r=[("ps", bank)], w=[kdst])
                    else:
                        P.op("dve", lambda e, dst=dst, psv=psv: e.tensor_copy(out=dst, in_=psv), r=[("ps", bank)], w=[kdst])
            return P.capture(s1), P.capture(s2)

        def rms_pipeline(stages):
            n = len(stages)
            for i in range(n + 1):
                if i < n:
                    P.ops.extend(stages[i][0])
                if i >= 1:
                    P.ops.extend(stages[i - 1][1])

        def mm_group(out, pairs, r, w):
            def fn(e):
                ins = None
                n = len(pairs)
                for i, (l, rh) in enumerate(pairs):
                    ins = e.matmul(out, lhsT=l, rhs=rh, start=(i == 0), stop=(i == n - 1))
                return ins
            P.op("pe", fn, r=r, w=w)

        def load_w(dst, src_ap, key, nobar=False):
            P.op("pool", lambda e: e.dma_start(out=dst, in_=src_ap), w=[key], dma=True, nobar=nobar)

        def wload(src_ap, ncols=512):
            slot = wq[0] % 2
            wq[0] += 1
            buf = Wfix[slot] if ncols == 512 else Wfix[slot][:, :, 0:ncols]
            load_w(buf, src_ap, ("W", slot), nobar=True)
            return buf, ("W", slot)

        def wview(wap, c0, ncols):
            return wap[:, c0:c0 + ncols].rearrange("(kt p) c -> p kt c", p=128)

        def conv_silu(rawpad, krp, acc, kacc, j, ntok, dst, kdst, defer=None):
            for k in (0, 1, 2):
                P.op("dve", lambda e, k=k: e.scalar_tensor_tensor(out=acc[:, 0:ntok], in0=rawpad[:, k:k + ntok], scalar=cwb[:, j, k:k + 1],
                                                                  op0=ALU.mult, in1=acc[:, 0:ntok], op1=ALU.add), r=[krp, kacc, KC], w=[kacc])
            silu_ops = P.capture(lambda: P.op("act", lambda e: e.activation(out=dst, in_=acc[:, 0:ntok], func=AF.Silu), r=[kacc], w=[kdst]))
            if defer is None:
                P.ops.extend(silu_ops)
            else:
                defer.append(silu_ops)

        def ssd_temps(b, main=True):
            t = {}
            t["sm"] = [b.f32([96]) for _ in range(2)]
            t["ex"] = b.f32([48])
            t["xdtw"] = b.bf16([16, 64])
            t["B_tok"] = b.bf16([256])
            if not main:
                return t
            t["LT"] = b.f32([2048])
            t["MT"] = b.bf16([16, 128])
            t["xs_tok"] = b.bf16([16, 64])
            t["xdt"] = b.bf16([16, 64])
            t["cbm"] = b.f32([2, 128])
            t["y1"] = b.f32([16, 64])
            t["t2"] = b.f32([16, 64])
            t["yb"] = b.bf16([1024])
            t["gn"] = b.f32([8])
            return t

        def ssd_chunk(t, xT, c, dtraw_c, kdt, main, ztok_c=None, ssdg_bc=None, banks=(0, 1, 2, 7, 1), ktag=""):
            par = c % 2
            sm, ex = t["sm"][par], t["ex"]
            cs = slice(c * 128, (c + 1) * 128)
            kx = ("xT" + ktag,)
            bs_, bx_, bb_ = banks[0], banks[1], banks[2]
            k0, k1, kdtc, kdta, kw2 = ("sm0", par), ("sm1", par), ("dtc", par), ("dta", par), ("w2", par)
            dtc = sm[:, 32:48]
            dta = sm[:, 48:64]
            w2 = sm[:, 64:80]
            toend, dec, ea = ex[:, 0:16], ex[:, 16:32], ex[:, 32:48]
            psx = pbb[bx_][:, 0:1024].rearrange("p (a b) -> p a b", a=8)
            psx3 = pbb[bx_][:, 0:1024].rearrange("p (a b) -> p a b", a=16)
            psB = pbb[bb_][:, 0:256].rearrange("p (a b) -> p a b", a=2)

            def head():
                P.op("dve", lambda e: e.tensor_tensor(out=sm[:, 0:16], in0=dtraw_c, in1=dtb_bc, op=ALU.add), r=[kdt, KC], w=[k0])
                P.op("act", lambda e: e.activation(out=sm[:, 16:32], in_=sm[:, 0:16], func=AF.Exp), r=[k0], w=[k1])
                P.op("act", lambda e: e.activation(out=dtc, in_=sm[:, 16:32], func=AF.Ln, bias=1.0), r=[k1], w=[kdtc])
                P.op("dve", lambda e: e.tensor_tensor(out=dta, in0=dtc, in1=aneg, op=ALU.mult), r=[kdtc, "aneg"], w=[kdta])

            def part1():
                def trx(e):
                    ins = None
                    for q in range(8):
                        ins = e.transpose(out=psx[:, q, :], in_=xT[:, q, cs], identity=identB)
                    return ins
                P.op("pe", trx, r=[kx, "identB"], w=[("ps", bx_)])
                if main:
                    psc = pbf[7][:, 0:256].rearrange("p (a b) -> p a b", a=2)

                    def cbf(e):
                        ins = None
                        for g in range(2):
                            ins = e.matmul(psc[:, g, :], lhsT=xT[:, 8 + g, cs], rhs=xT[:, 10 + g, cs], start=True, stop=True)
                        return ins
                    P.op("pe", cbf, r=[kx], w=[("ps", 7)])
                    P.op("dve", lambda e: e.tensor_tensor(out=t["cbm"], in0=psc, in1=triF.unsqueeze(1).broadcast_to([128, 2, 128]), op=ALU.mult),
                         r=[("ps", 7), KC], w=["cbm"])
                    for i4 in range(4):
                        P.op("dve", lambda e, i4=i4: e.tensor_tensor(out=rhsEr[:, i4 * 512:(i4 + 1) * 512].rearrange("p (a b) -> p a b", a=4),
                                                                     in0=triF.unsqueeze(1).broadcast_to([128, 4, 128]),
                                                                     in1=dta[:, i4 * 4:(i4 + 1) * 4].unsqueeze(2).broadcast_to([128, 4, 128]), op=ALU.mult),
                             r=[kdta, KC], w=[("rhsE", i4)])

                def small(e):
                    e.matmul(pbf[bs_][:, 0:16], lhsT=UF, rhs=dta, start=True, stop=True)
                    ins = e.matmul(pbf[bs_][:, 16:32], lhsT=onesF, rhs=dta, start=True, stop=True)
                    if main:
                        ins = e.matmul(pbf[bs_][:, 32:48], lhsT=triF, rhs=dta, start=True, stop=True)
                    return ins
                P.op("pe", small, r=[kdta, KC], w=[("ps", bs_)])
                nex = 48 if main else 32
                P.op("act", lambda e: e.activation(out=ex[:, 0:nex], in_=pbf[bs_][:, 0:nex], func=AF.Exp), r=[("ps", bs_)], w=["ex"])
                if main:
                    for i in range(4):
                        P.op("pe", lambda e, i=i: e.matmul(pbf[3 + i], lhsT=U32r, rhs=rhsEr[:, i * 512:(i + 1) * 512], start=True, stop=True),
                             r=[("rhsE", i), "U32"], w=[("ps", 3 + i)])
                        P.op("act", lambda e, i=i: e.activation(out=t["LT"][:, i * 512:(i + 1) * 512], in_=pbf[3 + i], func=AF.Exp),
                             r=[("ps", 3 + i)], w=[("LT", i)])

                def trb(e):
                    ins = None
                    for g in range(2):
                        ins = e.transpose(out=psB[:, g, :], in_=xT[:, 8 + g, cs], identity=identB)
                    return ins
                P.op("pe", trb, r=[kx, "identB"], w=[("ps", bb_)])
                P.op("dve", lambda e: e.tensor_tensor(out=w2, in0=dtc, in1=toend, op=ALU.mult), r=[kdtc, "ex"], w=[kw2])
                P.op("dve", lambda e: e.tensor_tensor(out=t["xdtw"], in0=psx3, in1=w2.unsqueeze(2).broadcast_to([128, 16, 64]), op=ALU.mult),
                     r=[("ps", bx_), kw2], w=["xdtw"])
                P.op("act", lambda e: e.activation(out=t["B_tok"], in_=pbb[bb_][:, 0:256], func=AF.Copy), r=[("ps", bb_)], w=["B_tok"])
                if main:
                    P.op("dve", lambda e: e.tensor_tensor(out=t["xdt"], in0=psx3, in1=dtc.unsqueeze(2).broadcast_to([128, 16, 64]), op=ALU.mult),
                         r=[("ps", bx_), kdtc], w=["xdt"])
                    P.op("act", lambda e: e.activation(out=t["xs_tok"], in_=psx3, func=AF.Copy), r=[("ps", bx_)], w=["xs_tok"])
                    LT3 = t["LT"].rearrange("p (a b) -> p a b", a=16)
                    for g in range(2):
                        P.op("dve", lambda e, g=g: e.tensor_tensor(out=t["MT"][:, g * 8:(g + 1) * 8, :], in0=LT3[:, g * 8:(g + 1) * 8, :],
                                                                   in1=t["cbm"][:, g:g + 1, :].broadcast_to([128, 8, 128]), op=ALU.mult),
                             r=[("LT", 2 * g), ("LT", 2 * g + 1), "cbm"], w=[("MT", g)])

            def part2():
                if main:
                    for g in range(2):
                        def yd(e, g=g):
                            ins = None
                            for j in range(8):
                                e16 = g * 8 + j
                                ins = e.matmul(pbf[3 + g][:, j * 64:(j + 1) * 64], lhsT=t["MT"][:, e16, :], rhs=t["xdt"][:, e16, :], start=True, stop=True)
                            return ins
                        P.op("pe", yd, r=[("MT", g), "xdt"], w=[("ps", 3 + g)])
                        P.op("pe", lambda e, g=g: e.matmul(pbf[5 + g], lhsT=xT[:, 10 + g, cs], rhs=hTb[:, g * 512:(g + 1) * 512], start=True, stop=True),
                             r=[kx, "hTb"], w=[("ps", 5 + g)])
                    y1 = t["y1"]
                    for g in range(2):
                        y1g = y1[:, g * 8:(g + 1) * 8, :]
                        P.op("dve", lambda e, g=g, y1g=y1g: e.tensor_tensor(out=y1g, in0=pbf[5 + g].rearrange("p (a b) -> p a b", a=8),
                                                                            in1=ea[:, g * 8:(g + 1) * 8].unsqueeze(2).broadcast_to([128, 8, 64]), op=ALU.mult),
                             r=[("ps", 5 + g), "ex"], w=[("y1", g)])
                        P.op("dve", lambda e, g=g, y1g=y1g: e.tensor_tensor(out=y1g, in0=pbf[3 + g].rearrange("p (a b) -> p a b", a=8), in1=y1g, op=ALU.add),
                             r=[("ps", 3 + g), ("y1", g)], w=[("y1", g)])
                    P.op("pool", lambda e: e.tensor_tensor(out=t["t2"], in0=t["xs_tok"], in1=D_bc.unsqueeze(2).broadcast_to([128, 16, 64]), op=ALU.mult),
                         r=["xs_tok", KC], w=["t2"])
                    P.op("dve", lambda e: e.tensor_tensor(out=y1, in0=y1, in1=t["t2"], op=ALU.add), r=[("y1", 0), ("y1", 1), "t2"], w=[("y1", 0), ("y1", 1)])
                    y1f = y1.rearrange("p a b -> p (a b)")
                    P.op("dve", lambda e: e.tensor_tensor(out=y1f, in0=y1f, in1=ztok_c, op=ALU.mult), r=[("y1", 0), ("y1", 1), "z_tok"], w=[("y1", 0), ("y1", 1)])
                    gn = t["gn"]
                    t2f = t["t2"].rearrange("p a b -> p (a b)")
                    for g in range(2):
                        P.op("act", lambda e, g=g: e.activation(out=t2f[:, g * 512:(g + 1) * 512], in_=y1f[:, g * 512:(g + 1) * 512], func=AF.Square,
                                                                accum_out=gn[:, g:g + 1]), r=[("y1", g)], w=["t2", ("gn", g)])
                    P.op("act", lambda e: e.activation(out=gn[:, 2:4], in_=gn[:, 0:2], func=AF.Ln, scale=1.0 / 512, bias=EPS), r=[("gn", 0), ("gn", 1)], w=["gn2"])
                    P.op("act", lambda e: e.activation(out=gn[:, 4:6], in_=gn[:, 2:4], func=AF.Exp, scale=-0.5), r=["gn2"], w=["gn4"])
                    for g in range(2):
                        P.op("dve", lambda e, g=g: e.scalar_tensor_tensor(out=t["yb"][:, g * 512:(g + 1) * 512], in0=y1f[:, g * 512:(g + 1) * 512],
                                                                          scalar=gn[:, 4 + g:5 + g], op0=ALU.mult, in1=ssdg_bc[:, g * 512:(g + 1) * 512], op1=ALU.mult),
                             r=[("y1", g), "gn4", "ssdg"], w=[("yb", g)])
                    psy = pbb[2][:, 0:1024].rearrange("p (a b) -> p a b", a=8)

                    def try_(e):
                        ins = None
                        for q in range(8):
                            ins = e.transpose(out=psy[:, q, :], in_=t["yb"][:, q * 128:(q + 1) * 128], identity=identB)
                        return ins
                    P.op("pe", try_, r=[("yb", 0), ("yb", 1), "identB"], w=[("ps", 2)])
                    P.op("act", lambda e: e.activation(out=YT[:, 8:16, cs], in_=psy, func=AF.Copy), r=[("ps", 2)], w=[("YTb", c)])

            def part_state():
                sb = (banks[3], banks[4])
                for g in range(2):
                    P.op("pe", lambda e, g=g: e.matmul(pbf[sb[g]], lhsT=t["B_tok"][:, g * 128:(g + 1) * 128], rhs=t["xdtw"].rearrange("p a b -> p (a b)")[:, g * 512:(g + 1) * 512],
                                                       start=True, stop=True), r=["B_tok", "xdtw"], w=[("ps", sb[g])])
                hT3 = hT.rearrange("p (a b) -> p a b", a=16)
                P.op("dve", lambda e: e.tensor_tensor(out=hT3, in0=hT3, in1=dec.unsqueeze(2).broadcast_to([128, 16, 64]), op=ALU.mult), r=["hT", "ex"], w=["hT"])
                for g in range(2):
                    P.op("dve", lambda e, g=g: e.tensor_tensor(out=hT[:, g * 512:(g + 1) * 512], in0=pbf[sb[g]], in1=hT[:, g * 512:(g + 1) * 512], op=ALU.add),
                         r=[("ps", sb[g]), "hT"], w=["hT"])
                P.op("act", lambda e: e.activation(out=hTb, in_=hT, func=AF.Copy), r=["hT"], w=["hTb"])
            return P.capture(head), P.capture(part1), P.capture(part2) + P.capture(part_state)

        def ssd_emit(chunks):
            n = len(chunks)
            P.ops.extend(chunks[0][0])
            for c in range(n):
                P.ops.extend(chunks[c][1])
                if c + 1 < n:
                    P.ops.extend(chunks[c + 1][0])
                P.ops.extend(chunks[c][2])

        dbg_ops = []

        def dbg_dump(name, ap, keys):
            if name in dbg_d:
                dbg_ops.append((name, ap, keys))

        def finish():
            P.barrier()
            outs = []
            for name, ap, keys in dbg_ops:
                k = ("dbgout", name)
                P.op("sp", lambda e, name=name, ap=ap: e.dma_start(out=dbg_d[name], in_=ap), r=keys, w=[k], dma=True)
                outs.append(k)
            P.op("sp", None, r=outs + OUTKEYS)
            P.resolve()
            P.emit(sems, block)

        OUTKEYS = []

        sc.reset()
        NX = 4
        X = [sc.f32([DM]) for _ in range(NX)]
        XN = [sc.bf16([DM]) for _ in range(2)]
        ST = [sc.f32([4]) for _ in range(2)]
        gbc = sc.f32([DM])
        m_common = sc.mark()
        nPT = nT[:, :, 0:TM]
        P.op("sp", lambda e, gbc=gbc: e.dma_start(out=gbc, in_=g_mix.partition_broadcast(128)), w=["gbc"], dma=True)
        stg = []
        for tt in range(8):
            i = tt % 2
            ix = tt % NX
            pre = (lambda tt=tt, ix=ix: P.op("sp", lambda e: e.dma_start(out=X[ix], in_=xprev[tt * 128:(tt + 1) * 128, :]), w=[("X", ix)], dma=True))
            stg.append(rms_transpose(X[ix], ("X", ix), 128, XN[i], ("XN", i), ST[i], ("ST", i), gbc, "gbc", nPT, tt * 128, ("nPT", tt), DM, tt, pre=pre))
        rms_pipeline(stg)
        Wdt = sc.bf16([16, 16])
        r2.reset()
        xTp = r2.bf16([10, TM])
        dtraw_p = r2.f32([8, 16])
        tP = ssd_temps(r2, main=False)
        rawpad = [sc.f32([TM + 4]) for _ in range(2)]
        accb = [sc.f32([TM]) for _ in range(2)]
        blocks = [(COL_X, 512), (COL_X + 512, 512), (COL_B, 512)]
        load_w(Wdt, wview(w_in, COL_DT, 16), "Wdt")
        wcur = wload(wview(w_in, blocks[0][0], blocks[0][1]))
        for i in range(2):
            P.op("dve", lambda e, rp=rawpad[i]: e.memset(rp[:, 0:3], 0.0), w=[("rawpad", i)])
        jt = 0
        defer_p = []
        for bi, (c0, ncol) in enumerate(blocks):
            Wv, kW = wcur
            if bi + 1 < len(blocks):
                wcur = wload(wview(w_in, blocks[bi + 1][0], blocks[bi + 1][1]))
            for jj in range(4):
                j = (c0 - COL_X) // 128 + jj
                rp = rawpad[jt % 2]
                krp = ("rawpad", jt % 2)
                isC = j >= 10
                pend_p = list(defer_p)
                del defer_p[:]
                for nb in range(2):
                    if isC and nb == 0:
                        continue
                    bank = 3 + (jt * 2 + nb) % 4
                    mm_group(pbf[bank], [(Wv[:, k, jj * 128:(jj + 1) * 128], nPT[:, k, nb * 512:(nb + 1) * 512]) for k in range(16)],
                             r=[kW] + [("nPT", tt) for tt in range(nb * 4, nb * 4 + 4)], w=[("ps", bank)])
                    P.op("act", lambda e, rp=rp, nb=nb, bank=bank: e.activation(out=rp[:, 3 + nb * 512:3 + (nb + 1) * 512], in_=pbf[bank], func=AF.Copy),
                         r=[("ps", bank)], w=[krp])
                    if not isC:
                        P.op("act", lambda e, ac=accb[jt % 2], nb=nb, bank=bank, j=j: e.activation(out=ac[:, nb * 512:(nb + 1) * 512], in_=pbf[bank], func=AF.Identity,
                                                                                                scale=cwb[:, j, 3:4], bias=cwb[:, j, 4:5]),
                             r=[("ps", bank), KC], w=[("acc", jt % 2)])
                for so in pend_p:
                    P.ops.extend(so)
                P.op("dve", lambda e, rp=rp, j=j: e.tensor_copy(out=prefix[:, j, :], in_=rp[:, TM:TM + 3]), r=[krp], w=["prefix"])
                if not isC:
                    conv_silu(rp, krp, accb[jt % 2], ("acc", jt % 2), j, TM, xTp[:, j, :], ("xTp",), defer=defer_p)
                jt += 1
        for so in defer_p:
            P.ops.extend(so)
        for tt in range(8):
            mm_group(pbf[0][:, 0:16], [(nPT[:, k, tt * 128:(tt + 1) * 128], Wdt[:, k, :]) for k in range(16)], r=["Wdt", ("nPT", tt)], w=[("ps", 0)])
            P.op("act", lambda e, tt=tt: e.activation(out=dtraw_p[:, tt, :], in_=pbf[0][:, 0:16], func=AF.Copy), r=[("ps", 0)], w=["dtraw_p"])
        P.barrier()

        def pchunks():
            ssd_emit([ssd_chunk(tP, xTp, c, dtraw_p[:, c, :], "dtraw_p", main=False, banks=(5, 6, 5, 7, 6), ktag="p") for c in range(8)])
            P.op("dve", lambda e: e.tensor_scalar(out=hT, in0=hT, scalar1=flag, scalar2=None, op0=ALU.mult), r=["hT", KC], w=["hT"])
            P.op("act", lambda e: e.activation(out=hTb, in_=hT, func=AF.Copy), r=["hT"], w=["hTb"])
        pch_ops = P.capture(pchunks)
        _saved_ops = P.ops
        P.ops = []
        sc.reset(m_common)
        stgA = []
        for tt in range(9):
            i = tt % 2
            rows = 128 if tt < 8 else NS
            src = xm[tt * 128:(tt + 1) * 128, :] if tt < 8 else xsmp[:, :]
            ix = tt % NX
            pre = (lambda ix=ix, rows=rows, src=src: P.op("sp", lambda e: e.dma_start(out=X[ix][:rows, :], in_=src), w=[("X", ix)], dma=True))
            stgA.append(rms_transpose(X[ix], ("X", ix), rows, XN[i], ("XN", i), ST[i], ("ST", i), gbc, "gbc", nT, tt * 128, ("nT", tt), DM, tt, pre=pre))
        rms_pipeline(stgA)
        P.barrier()
        sc.reset()
        xT = sc.bf16([12, T])
        z_tok = sc.bf16([8, 1024])
        dtraw = sc.f32([8, 16])
        ssdg_bc = sc.f32([1024])
        raw_s = sc.f32([12, NS])
        m_A = sc.mark()
        Wdt = sc.bf16([16, 16])
        rawpad = [sc.f32([TM + 4]) for _ in range(2)]
        accb = [sc.f32([TM]) for _ in range(2)]
        scTok = sc.f32([3 * 1536])
        scT = sc.f32([36, NS])
        acc_s = sc.f32([12, NS])
        tmp_s = sc.f32([12, NS])
        P.op("sp", lambda e: e.dma_start(out=ssdg_bc, in_=ssd_g.partition_broadcast(128)), w=["ssdg"], dma=True)
        P.op("sp", lambda e: e.dma_start(out=Rsel, in_=cst2_d[:, C2_RSEL:C2_RSEL + 1024]), w=["cst2"], dma=True)
        P.op("sp", lambda e: e.dma_start(out=scTok[:NS, :], in_=sconv.rearrange("b r c -> b (r c)")), w=["scTok"], dma=True)
        P.op("sp", lambda e: e.dma_start(out=nconv_s[:, 0:2, :], in_=sconv[:, 1:3, :]), w=["o_ncs01"], dma=True)
        OUTKEYS.append("o_ncs01")
        for half in range(2):
            def trs(e, half=half):
                ins = None
                for idx in range(half * 18, half * 18 + 18):
                    r_, j_ = idx // 12, idx % 12
                    ii = idx - half * 18
                    ins = e.transpose(out=pbf[half][:, ii * NS:(ii + 1) * NS], in_=scTok[:NS, r_ * 1536 + j_ * 128:r_ * 1536 + (j_ + 1) * 128],
                                      identity=identF[:NS, :NS])
                return ins
            P.op("pe", trs, r=["scTok", KC], w=[("ps", half)])
            P.op("dve", lambda e, half=half: e.tensor_copy(out=scT[:, half * 18:half * 18 + 18, :].rearrange("p a b -> p (a b)"), in_=pbf[half][:, 0:18 * NS]),
                 r=[("ps", half)], w=["scT"])
        blocksA = [(COL_Z, "z"), (COL_Z + 512, "z"), (COL_X, "x"), (COL_X + 512, "x"), (COL_B, "x")]
        load_w(Wdt, wview(w_in, COL_DT, 16), "Wdt")
        wcur = wload(wview(w_in, blocksA[0][0], 512))
        jt = 0
        grp = 0
        defer_a = []
        _zwin = [len(P.ops), None]
        for bi, (c0, kind) in enumerate(blocksA):
            Wv, kW = wcur
            if kind == "x" and _zwin[1] is None:
                _zwin[1] = len(P.ops)
            if bi + 1 < len(blocksA):
                wcur = wload(wview(w_in, blocksA[bi + 1][0], 512))
            if kind == "z":
                zb = (c0 - COL_Z) // 512
                for tt in range(8):
                    bank = 1 + grp % 4
                    grp += 1
                    mm_group(pbf[bank], [(nT[:, k, tt * 128:(tt + 1) * 128], Wv[:, k, :]) for k in range(16)], r=[kW, ("nT", tt)], w=[("ps", bank)])
                    P.op("act", lambda e, tt=tt, zb=zb, bank=bank: e.activation(out=z_tok[:, tt, zb * 512:(zb + 1) * 512], in_=pbf[bank], func=AF.Silu),
                         r=[("ps", bank)], w=["z_tok"])
                for jj in range(4):
                    bank = 1 + grp % 4
                    grp += 1
                    j = zb * 4 + jj
                    mm_group(pbf[bank][:, 0:NS], [(Wv[:, k, jj * 128:(jj + 1) * 128], nT[:, k, TM:T]) for k in range(16)], r=[kW, ("nT", 8)], w=[("ps", bank)])
                    P.op("act", lambda e, j=j, bank=bank: e.activation(out=zT_s[:, j, :], in_=pbf[bank][:, 0:NS], func=AF.Silu), r=[("ps", bank)], w=["zT_s"])
            else:
                for jj in range(4):
                    j = (c0 - COL_X) // 128 + jj
                    rp = rawpad[jt % 2]
                    krp = ("rawpad", jt % 2)
                    pend_a = list(defer_a)
                    del defer_a[:]
                    P.op("dve", lambda e, rp=rp, j=j: e.tensor_copy(out=rp[:, 0:3], in_=prefix[:, j, :]), r=["prefix"], w=[krp])
                    for nb in range(3):
                        bank = 1 + grp % 4
                        grp += 1
                        lo, hi = (nb * 512, (nb + 1) * 512) if nb < 2 else (TM, T)
                        n = hi - lo
                        rk = [("nT", tt) for tt in range(nb * 4, nb * 4 + 4)] if nb < 2 else [("nT", 8)]
                        mm_group(pbf[bank][:, 0:n], [(Wv[:, k, jj * 128:(jj + 1) * 128], nT[:, k, lo:hi]) for k in range(16)], r=[kW] + rk, w=[("ps", bank)])
                        if nb < 2:
                            P.op("act", lambda e, rp=rp, lo=lo, hi=hi, bank=bank: e.activation(out=rp[:, 3 + lo:3 + hi], in_=pbf[bank], func=AF.Copy),
                                 r=[("ps", bank)], w=[krp])
                            P.op("act", lambda e, ac=accb[jt % 2], lo=lo, hi=hi, bank=bank, j=j: e.activation(out=ac[:, lo:hi], in_=pbf[bank], func=AF.Identity,
                                                                                                    scale=cwb[:, j, 3:4], bias=cwb[:, j, 4:5]),
                                 r=[("ps", bank), KC], w=[("acc", jt % 2)])
                        else:
                            P.op("act", lambda e, j=j, bank=bank: e.activation(out=raw_s[:, j, :], in_=pbf[bank][:, 0:NS], func=AF.Copy),
                                 r=[("ps", bank)], w=["raw_s"])
                    for so in pend_a:
                        P.ops.extend(so)
                    P.op("dve", lambda e, rp=rp, j=j: e.tensor_copy(out=ncv[:, j, :], in_=rp[:, TM:TM + 3]), r=[krp], w=["ncv"])
                    conv_silu(rp, krp, accb[jt % 2], ("acc", jt % 2), j, TM, xT[:, j, 0:TM], ("xT",), defer=defer_a)
                    jt += 1
        for so in defer_a:
            P.ops.extend(so)
        def cwk(k):
            return cwb[:, :, k:k + 1].broadcast_to([128, 12, NS])
        P.op("dve", lambda e: e.tensor_tensor(out=acc_s, in0=raw_s, in1=cwk(3), op=ALU.mult), r=["raw_s", KC], w=["acc_s"])
        P.op("dve", lambda e: e.tensor_tensor(out=acc_s, in0=acc_s, in1=cwk(4), op=ALU.add), r=["acc_s", KC], w=["acc_s"])
        for r_ in range(3):
            P.op("dve", lambda e, r_=r_: e.tensor_tensor(out=tmp_s, in0=scT[:, r_ * 12:(r_ + 1) * 12, :], in1=cwk(r_), op=ALU.mult), r=["scT", KC, "tmp_s"], w=["tmp_s"])
            P.op("dve", lambda e: e.tensor_tensor(out=acc_s, in0=acc_s, in1=tmp_s, op=ALU.add), r=["acc_s", "tmp_s"], w=["acc_s"])
        P.op("act", lambda e: e.activation(out=xT_s, in_=acc_s, func=AF.Silu), r=["acc_s"], w=["xT_s"])
        for tt in range(8):
            mm_group(pbf[0][:, 0:16], [(nT[:, k, tt * 128:(tt + 1) * 128], Wdt[:, k, :]) for k in range(16)], r=["Wdt", ("nT", tt)], w=[("ps", 0)])
            P.op("act", lambda e, tt=tt: e.activation(out=dtraw[:, tt, :], in_=pbf[0][:, 0:16], func=AF.Copy), r=[("ps", 0)], w=["dtraw"])
        mm_group(pbf[1][:NS, 0:NS], [(Wdt[:, k, :], nT[:, k, TM:T]) for k in range(16)], r=["Wdt", ("nT", 8)], w=[("ps", 1)])
        P.op("act", lambda e: e.activation(out=dtT_s[:NS, :], in_=pbf[1][:NS, 0:NS], func=AF.Copy), r=[("ps", 1)], w=["dtT_s"])

        def rows_out(src3, ksrc, R, stage, kst, dram_ap, okey):
            for b3 in range(3):
                def trr(e, b3=b3):
                    ins = None
                    for jj in range(4):
                        j_ = b3 * 4 + jj
                        ins = e.transpose(out=pbf[2 + b3][:R, jj * 128:(jj + 1) * 128], in_=src3[:, j_, :], identity=identF)
                    return ins
                P.op("pe", trr, r=[ksrc, KC], w=[("ps", 2 + b3)])
                P.op("dve", lambda e, b3=b3: e.tensor_copy(out=stage[:R, b3 * 512:(b3 + 1) * 512], in_=pbf[2 + b3][:R, :]), r=[("ps", 2 + b3)], w=[kst, "scTok"])
            P.op("sp", lambda e: e.dma_start(out=dram_ap, in_=stage[:R, 0:1536]), r=[kst], w=[okey], dma=True)
            OUTKEYS.append(okey)
        rows_out(raw_s, "raw_s", NS, scTok[:, 0:1536], "stg0", nconv_s[:, 2, :], "o_ncs2")
        rows_out(ncv, "ncv", 3, scTok[:, 1536:3072], "stg1", nconv_p[:, :], "o_ncp")
        _main_ops = P.ops
        P.ops = _saved_ops
        P.ops.extend(Prog.merge(_main_ops, pch_ops, _zwin[0], _zwin[1]))
        P.barrier()
        if STOP_AFTER == "A1":
            dbg_dump("xT", xT.rearrange("p a b -> p (a b)"), [("xT",)])
            dbg_dump("z_tok", z_tok.rearrange("p a b -> p (a b)"), ["z_tok"])
            dbg_dump("zT_s", zT_s.rearrange("p a b -> p (a b)"), ["zT_s"])
            dbg_dump("dtraw", dtraw.rearrange("p a b -> p (a b)"), ["dtraw"])
            dbg_dump("dtT_s", dtT_s, ["dtT_s"])
            finish()
            return nc

        sc.reset(m_A)
        tA = ssd_temps(sc)
        ssd_emit([ssd_chunk(tA, xT, c, dtraw[:, c, :], "dtraw", main=True, ztok_c=z_tok[:, c, :], ssdg_bc=ssdg_bc) for c in range(8)])
        if STOP_AFTER == "A2a":
            P.barrier()
            dbg_dump("YT", YT.rearrange("p a b -> p (a b)"), [("YTb", c) for c in range(8)])
            finish()
            return nc
        hout = sc.f32([8, 128])
        for half in range(2):
            def trh(e, half=half):
                ins = None
                for jj in range(4):
                    q = half * 4 + jj
                    ins = e.transpose(out=pbf[3 + half][:, jj * 128:(jj + 1) * 128], in_=hT[:, q * 128:(q + 1) * 128], identity=identF)
                return ins
            P.op("pe", trh, r=["hT", KC], w=[("ps", 3 + half)])
            P.op("dve", lambda e, half=half: e.tensor_copy(out=hout[:, half * 4:(half + 1) * 4, :].rearrange("p a b -> p (a b)"), in_=pbf[3 + half]),
                 r=[("ps", 3 + half)], w=["hout"])
        P.op("sp", lambda e: e.dma_start(out=nssm_p.rearrange("(q r) n -> r q n", r=128), in_=hout), r=["hout"], w=["o_nsp"], dma=True)
        OUTKEYS.append("o_nsp")

        if STOP_AFTER == "A2b":
            P.barrier()
            dbg_dump("YT", YT.rearrange("p a b -> p (a b)"), [("YTb", c) for c in range(8)])
            finish()
            return nc
        P.barrier()
        SMPW = 7300
        scS = Bump(arena, AW - SMPW, AW)
        _saved_main = P.ops
        P.ops = []
        _sc_main = sc
        sc = scS
        sm_s = sc.f32([8, NS])
        cat = sc.f32([32])
        rep = sc.f32([8, 32])
        dtx = sc.f32([8, NS])
        ys = sc.f32([8, NS])
        ysq = sc.f32([8, NS])
        rs = sc.f32([2, NS])
        BC_tok = sc.bf16([512])
        selB = sc.bf16([NS, 128])
        NHS = 3
        hs = [sc.f32([8, 128]) for _ in range(NHS)]
        tmpS = [sc.f32([1024]) for _ in range(2)]
        x0 = sm_s[:NS, 0, :]; e1 = sm_s[:NS, 1, :]; anp = sm_s[:NS, 2, 0:1]
        P.op("act", lambda e: e.activation(out=e1, in_=dtT_s[:NS, :], func=AF.Exp, bias=dtb_pp[:NS, :]), r=["dtT_s", KC], w=["s_e1"])
        P.op("act", lambda e: e.activation(out=cat[:NS, 16:32], in_=e1, func=AF.Ln, bias=1.0), r=["s_e1"], w=["s_dt"])
        P.op("act", lambda e: e.activation(out=anp, in_=al_pp[:NS, :], func=AF.Exp), r=[KC], w=["s_anp0"])
        P.op("dve", lambda e: e.tensor_scalar(out=anp, in0=anp, scalar1=-1.0, scalar2=None, op0=ALU.mult), r=["s_anp0"], w=["s_anp"])
        P.op("act", lambda e: e.activation(out=cat[:NS, 0:16], in_=cat[:NS, 16:32], func=AF.Exp, scale=anp), r=["s_dt", "s_anp"], w=["s_dA"])

        def repf(e):
            ins = None
            for q in range(8):
                ins = e.matmul(pbf[7][:, q * 32:(q + 1) * 32], lhsT=Rsel[:NS, q * 128:(q + 1) * 128], rhs=cat[:NS, :], start=True, stop=True)
            return ins
        P.op("pe", repf, r=["s_dA", "s_dt", "cst2"], w=[("ps", 7)])
        P.op("dve", lambda e: e.tensor_copy(out=rep.rearrange("p a b -> p (a b)"), in_=pbf[7][:, 0:256]), r=[("ps", 7)], w=["rep"])
        xs_s = xT_s[:, 0:8, :]
        P.op("dve", lambda e: e.tensor_tensor(out=dtx, in0=rep[:, :, 16:32], in1=xs_s, op=ALU.mult), r=["rep", "xT_s"], w=["dtx"])
        psbc = pbb[7][:NS, 0:512].rearrange("p (a b) -> p a b", a=4)

        def trbc(e):
            ins = None
            for jj in range(4):
                ins = e.transpose(out=psbc[:, jj, :], in_=xT_s[:, 8 + jj, :], identity=identB)
            return ins
        P.op("pe", trbc, r=["xT_s", "identB"], w=[("ps", 7)])
        P.op("act", lambda e: e.activation(out=BC_tok[:NS, :], in_=pbb[7][:NS, 0:512], func=AF.Copy), r=[("ps", 7)], w=["BC_tok"])
        P.op("dve", lambda e: e.tensor_copy(out=selB[:NS, :, :], in_=identB[:NS, 0:NS].unsqueeze(2).broadcast_to([NS, NS, 128])), r=["identB"], w=["selB"])
        P.op("dve", lambda e: e.memset(ys, 0.0), w=["ys"])

        def hs_load(b):
            P.op("sp", lambda e, b=b: e.dma_start(out=hs[b % NHS], in_=sssm[b].rearrange("(q r) n -> r q n", r=128)), w=[("hs", b % NHS, q) for q in range(8)], dma=True)
        for b in range(min(NHS - 1, NS)):
            hs_load(b)
        for b in range(NS):
            hb = hs[b % NHS]
            bank = 5 + b % 2
            if b + NHS - 1 < NS:
                hs_load(b + NHS - 1)
            P.op("pe", lambda e, b=b, bank=bank: e.matmul(pbf[bank], lhsT=selB[:NS, b, :], rhs=BC_tok[:NS, :], start=True, stop=True),
                 r=["selB", "BC_tok"], w=[("ps", bank)])
            for q in range(8):
                P.op("act", lambda e, b=b, q=q, hb=hb: e.activation(out=hb[:, q, :], in_=hb[:, q, :], func=AF.Copy, scale=rep[:, q, b:b + 1]),
                     r=[("hs", b % NHS, q), "rep"], w=[("hs", b % NHS, q)])

            bc4 = pbf[bank][:, 0:256].rearrange("p (g n) -> p g n", g=2).unsqueeze(2).broadcast_to([128, 2, 4, 128])
            cc4 = pbf[bank][:, 256:512].rearrange("p (g n) -> p g n", g=2).unsqueeze(2).broadcast_to([128, 2, 4, 128])
            hb4 = hb.rearrange("p (g j) n -> p g j n", g=2)
            tm4 = tmpS[b % 2].rearrange("p (g j n) -> p g j n", g=2, j=4)
            dx4 = dtx[:, :, b:b + 1].rearrange("p (g j) o -> p g j o", g=2).broadcast_to([128, 2, 4, 128])
            khs = [("hs", b % NHS, q) for q in range(8)]
            ktm = ("tmpS", b % 2)
            P.op("dve", lambda e, tm4=tm4, bc4=bc4, dx4=dx4: e.tensor_tensor(out=tm4, in0=bc4, in1=dx4, op=ALU.mult), r=[("ps", bank), "dtx"], w=[ktm])
            P.op("dve", lambda e, hb4=hb4, tm4=tm4: e.tensor_tensor(out=hb4, in0=hb4, in1=tm4, op=ALU.add), r=khs + [ktm], w=khs)
            P.op("dve", lambda e, hb4=hb4, tm4=tm4, cc4=cc4: e.tensor_tensor(out=tm4, in0=cc4, in1=hb4, op=ALU.mult), r=khs + [("ps", bank), ktm], w=[ktm])
            P.op("dve", lambda e, b=b, tm=tmpS[b % 2]: e.tensor_reduce(out=ys[:, :, b:b + 1], in_=tm.rearrange("p (q n) -> p q n", q=8), axis=AX.X, op=ALU.add),
                 r=[ktm], w=["ys"])
            ok = ("o_nss", b)
            P.op("sp", lambda e, b=b, hb=hb: e.dma_start(out=nssm_s[b].rearrange("(q r) n -> r q n", r=128), in_=hb), r=[("hs", b % NHS, q) for q in range(8)], w=[ok], dma=True)
            OUTKEYS.append(ok)
        P.op("dve", lambda e: e.tensor_tensor(out=ysq, in0=xs_s, in1=D_pp.unsqueeze(2).broadcast_to([128, 8, NS]), op=ALU.mult), r=["xT_s", KC], w=["ysq"])
        P.op("dve", lambda e: e.tensor_tensor(out=ys, in0=ys, in1=ysq, op=ALU.add), r=["ys", "ysq"], w=["ys"])
        P.op("dve", lambda e: e.tensor_tensor(out=ys, in0=ys, in1=zT_s, op=ALU.mult), r=["ys", "zT_s"], w=["ys"])
        P.op("dve", lambda e: e.tensor_tensor(out=ysq, in0=ys, in1=ys, op=ALU.mult), r=["ys", "ysq"], w=["ysq"])

        def gsum(e):
            ins = None
            for q in range(8):
                g = q // 4
                ins = e.matmul(pbf[7][:, g * NS:(g + 1) * NS], lhsT=onesF, rhs=ysq[:, q, :], start=(q % 4 == 0), stop=(q % 4 == 3))
            return ins
        P.op("pe", gsum, r=["ysq", KC], w=[("ps", 7)])
        rsf = rs.rearrange("p a b -> p (a b)")
        P.op("act", lambda e: e.activation(out=rsf, in_=pbf[7][:, 0:2 * NS], func=AF.Ln, scale=1.0 / 512, bias=EPS), r=[("ps", 7)], w=["rs0"])
        P.op("act", lambda e: e.activation(out=rsf, in_=rsf, func=AF.Exp, scale=-0.5), r=["rs0"], w=["rs"])
        for g in range(2):
            P.op("dve", lambda e, g=g: e.tensor_tensor(out=ys[:, g * 4:(g + 1) * 4, :], in0=ys[:, g * 4:(g + 1) * 4, :],
                                                       in1=rs[:, g:g + 1, :].broadcast_to([128, 4, NS]), op=ALU.mult), r=["ys", "rs"], w=["ys"])
        P.op("dve", lambda e: e.tensor_tensor(out=YT[:, 8:16, TM:T], in0=ys, in1=sg_pp.unsqueeze(2).broadcast_to([128, 8, NS]), op=ALU.mult),
             r=["ys", KC], w=[("YTb", 8)])
        smp_ops = P.ops
        P.ops = _saved_main
        sc = _sc_main

        sc.reset()
        sc.end = AW - SMPW
        _saved_a3 = P.ops
        P.ops = []
        vng_bc = sc.f32([1024])
        bs_bc = sc.f32([8, 128])
        v_tok = sc.bf16([9, 1024])
        WsT = sc.bf16([8, 128])
        sqacc = sc.f32([T])
        gv = [sc.f32([1024]) for _ in range(2)]
        wsraw = gv[0].rearrange("p (a b) -> p a b", a=8)
        vst = [sc.f32([4]) for _ in range(2)]
        ug = [sc.f32([T]) for _ in range(2)]
        _m_tm = sc.mark()
        tmpm = [sc.f32([512]) for _ in range(2)]
        _m_tm2 = sc.mark()
        sc.reset(_m_tm)
        vsf = sc.f32([1024])
        sc.reset(_m_tm2)
        sq = [sc.bf16([T]) for _ in range(2)]
        rstd_bc = sqacc
        w00I = sc.bf16([8, NS])
        P.op("sp", lambda e: e.dma_start(out=vng_bc, in_=vn_g.partition_broadcast(128)), w=["vng"], dma=True)
        P.op("sp", lambda e: e.dma_start(out=bs_bc.rearrange("p a b -> p (a b)"), in_=cst2_d[:, C2_BS:C2_BS + 1024]), w=["bs_bc"], dma=True)
        P.op("sp", lambda e: e.dma_start(out=wsraw, in_=ws_d.rearrange("h t s -> t h s")), w=["wsraw"], dma=True)
        wv2 = [wload(wview(w_in, COL_V, 512)), wload(wview(w_in, COL_V + 512, 512))]
        for half in range(2):
            def trw(e, half=half):
                ins = None
                for jj in range(4):
                    h_ = half * 4 + jj
                    ins = e.transpose(out=pbf[half][:, jj * 128:(jj + 1) * 128], in_=wsraw[:, h_, :], identity=identF)
                return ins
            P.op("pe", trw, r=["wsraw", KC], w=[("ps", half)])
            P.op("dve", lambda e, half=half: e.tensor_tensor(out=WsT[:, half * 4:(half + 1) * 4, :], in0=pbf[half].rearrange("p (a b) -> p a b", a=4),
                                                             in1=triF.unsqueeze(1).broadcast_to([128, 4, 128]), op=ALU.mult), r=[("ps", half), KC], w=["WsT"])
        for h_ in range(8):
            P.op("dve", lambda e, h_=h_: e.tensor_scalar(out=w00I[:NS, h_, :], in0=identF[:NS, 0:NS], scalar1=w00_bc[:NS, h_:h_ + 1], scalar2=None, op0=ALU.mult),
                 r=[KC], w=["w00I"])
        for tt in range(9):
            rows = 128 if tt < 8 else NS
            i = tt % 2
            tcols = slice(tt * 128, tt * 128 + rows)
            for zb in range(2):
                bank = 1 + (tt * 2 + zb) % 4
                mm_group(pbf[bank][:rows, :], [(nT[:, k, tcols], wv2[zb][0][:, k, :]) for k in range(16)], r=[wv2[zb][1], ("nT", tt)], w=[("ps", bank)])
                P.op("act", lambda e, i=i, zb=zb, bank=bank, rows=rows: e.activation(out=gv[i][:rows, zb * 512:(zb + 1) * 512], in_=pbf[bank][:rows, :],
                                                                                    func=AF.Gelu_apprx_tanh), r=[("ps", bank)], w=[("gv", i, zb)])
            P.op("act", lambda e, i=i, rows=rows: e.activation(out=ug[i][:rows, 0:1024], in_=gv[i][:rows, :], func=AF.Square, accum_out=vst[i][:rows, 0:1]),
                 r=[("gv", i, 0), ("gv", i, 1)], w=[("ug", i), ("vst", i)])
            P.op("act", lambda e, i=i, rows=rows: e.activation(out=vst[i][:rows, 1:2], in_=vst[i][:rows, 0:1], func=AF.Ln, scale=1.0 / 1024, bias=EPS),
                 r=[("vst", i)], w=[("vst", i)])
            P.op("act", lambda e, i=i, rows=rows: e.activation(out=vst[i][:rows, 2:3], in_=vst[i][:rows, 1:2], func=AF.Exp, scale=-0.5),
                 r=[("vst", i)], w=[("vst", i)])
            P.op("dve", lambda e, i=i, rows=rows, tt=tt: e.scalar_tensor_tensor(out=v_tok[:rows, tt, :], in0=gv[i][:rows, :], scalar=vst[i][:rows, 2:3], op0=ALU.mult,
                                                                                in1=vng_bc[:rows, :], op1=ALU.mult),
                 r=[("gv", i, 0), ("gv", i, 1), ("vst", i), "vng"], w=[("v_tok", tt)])
            if tt == 8:
                P.op("dve", lambda e, i=i: e.scalar_tensor_tensor(out=vsf[:NS, :], in0=gv[i][:NS, :], scalar=vst[i][:NS, 2:3], op0=ALU.mult,
                                                                  in1=vng_bc[:NS, :], op1=ALU.mult), r=[("gv", i, 0), ("gv", i, 1), ("vst", i), "vng"], w=["vsf", ("tmpm", 0), ("tmpm", 1)])
                P.op("sp", lambda e: e.dma_start(out=v_s[:, :], in_=vsf[:NS, :]), r=["vsf", ("tmpm", 0), ("tmpm", 1)], w=["o_vs"], dma=True)
                OUTKEYS.append("o_vs")
        wu2 = [wload(wview(w_in, COL_U, 512)), wload(wview(w_in, COL_U + 512, 512))]
        tok_blocks = [(0, 348), (348, 696), (696, T)]
        grpc = [0]

        def stage_G(h_):
            ub, jj = h_ // 4, h_ % 4
            Wv, kWu = wu2[ub]
            ui = h_ % 2
            for nb, (lo, hi) in enumerate(tok_blocks):
                bank = grpc[0] % 2
                grpc[0] += 1
                n = hi - lo
                rk = [("nT", tt) for tt in range(lo // 128, min(8, (hi - 1) // 128) + 1)]
                mm_group(pbf[bank][:, 0:n], [(Wv[:, k, jj * 128:(jj + 1) * 128], nT[:, k, lo:hi]) for k in range(16)], r=[kWu] + rk, w=[("ps", bank)])
                P.op("act", lambda e, ui=ui, lo=lo, hi=hi, n=n, bank=bank: e.activation(out=ug[ui][:, lo:hi], in_=pbf[bank][:, 0:n], func=AF.Gelu_apprx_tanh),
                     r=[("ps", bank)], w=[("ug", ui)])

        def stage_M(h_):
            ui = h_ % 2
            for half in range(2):
                def mix(e, half=half, h_=h_):
                    ins = None
                    for cc in range(4):
                        c = half * 4 + cc
                        ins = e.matmul(pbf[2 + half][:, cc * 128:(cc + 1) * 128], lhsT=v_tok[:, c, h_ * 128:(h_ + 1) * 128], rhs=WsT[:, h_, :], start=True, stop=True)
                    return ins
                P.op("pe", mix, r=[("v_tok", c) for c in range(half * 4, half * 4 + 4)] + ["WsT"], w=[("ps", 2 + half)])
                tm = tmpm[half]
                P.op("dve", lambda e, half=half, h_=h_, tm=tm: e.tensor_tensor(out=tm.rearrange("p (a b) -> p a b", a=4), in0=pbf[2 + half].rearrange("p (a b) -> p a b", a=4),
                                                                               in1=bs_bc[:, h_:h_ + 1, :].broadcast_to([128, 4, 128]), op=ALU.add),
                     r=[("ps", 2 + half), "bs_bc"], w=[("tmpm", half)])
                P.op("dve", lambda e, half=half, h_=h_, tm=tm, ui=ui: e.tensor_tensor(out=YT[:, h_, half * 512:(half + 1) * 512], in0=tm, in1=ug[ui][:, half * 512:(half + 1) * 512], op=ALU.mult),
                     r=[("tmpm", half), ("ug", ui)], w=[("ya", h_)])
            P.op("pe", lambda e, h_=h_: e.matmul(pbf[4][:, 0:NS], lhsT=v_tok[:NS, 8, h_ * 128:(h_ + 1) * 128], rhs=w00I[:NS, h_, :], start=True, stop=True),
                 r=[("v_tok", 8), "w00I"], w=[("ps", 4)])
            P.op("dve", lambda e, h_=h_, ui=ui: e.scalar_tensor_tensor(out=YT[:, h_, TM:T], in0=pbf[4][:, 0:NS], scalar=b0_bc[:, h_:h_ + 1], op0=ALU.add, in1=ug[ui][:, TM:T], op1=ALU.mult),
                 r=[("ps", 4), ("ug", ui), KC], w=[("ya", h_)])
            si = h_ % 2
            if h_ == 0:
                P.op("act", lambda e, h_=h_: e.activation(out=sqacc, in_=YT[:, h_, :], func=AF.Square), r=[("ya", h_)], w=["sqacc"])
            else:
                P.op("act", lambda e, h_=h_, si=si: e.activation(out=sq[si], in_=YT[:, h_, :], func=AF.Square), r=[("ya", h_)], w=[("sq", si)])
                P.op("dve", lambda e, si=si: e.tensor_tensor(out=sqacc, in0=sqacc, in1=sq[si], op=ALU.add), r=["sqacc", ("sq", si)], w=["sqacc"])

        stage_G(0)
        for h_ in range(8):
            if h_ + 1 < 8:
                stage_G(h_ + 1)
            stage_M(h_)
        P.op("dve", lambda e: e.tensor_copy(out=sq[0], in_=sqacc), r=["sqacc", ("sq", 0)], w=[("sq", 0)])
        sbank = (0, 1, 4)
        for nb, (lo, hi) in enumerate(tok_blocks):
            n = hi - lo
            P.op("pe", lambda e, nb=nb, lo=lo, hi=hi, n=n: e.matmul(pbf[sbank[nb]][:, 0:n], lhsT=onesB, rhs=sq[0][:, lo:hi], start=True, stop=True),
                 r=[("sq", 0), "onesB"], w=[("ps", sbank[nb])])
        for nb, (lo, hi) in enumerate(tok_blocks):
            n = hi - lo
            P.op("act", lambda e, nb=nb, lo=lo, hi=hi, n=n: e.activation(out=rstd_bc[:, lo:hi], in_=pbf[sbank[nb]][:, 0:n], func=AF.Ln, scale=1.0 / 1024, bias=EPS),
                 r=[("ps", sbank[nb])], w=[("rstd", nb)])
            P.op("act", lambda e, lo=lo, hi=hi: e.activation(out=rstd_bc[:, lo:hi], in_=rstd_bc[:, lo:hi], func=AF.Exp, scale=-0.5), r=[("rstd", nb)], w=[("rstd", nb)])
        for h_ in range(8):
            P.op("dve", lambda e, h_=h_: e.scalar_tensor_tensor(out=YT[:, h_, :], in0=YT[:, h_, :], scalar=gon_pp[:, h_:h_ + 1], op0=ALU.mult, in1=rstd_bc, op1=ALU.mult),
                 r=[("ya", h_), ("rstd", 0), ("rstd", 1), ("rstd", 2), KC], w=[("YTa", h_)])
        _a3_ops = P.ops
        P.ops = _saved_a3
        P.ops.extend(Prog.merge(_a3_ops, smp_ops, 0, int(len(_a3_ops) * 0.8)))
        sc.end = AW
        P.barrier()
        if STOP_AFTER == "A3":
            dbg_dump("YT", YT.rearrange("p a b -> p (a b)"), [("YTa", h_) for h_ in range(8)])
            finish()
            return nc

        sc.reset()
        hres = sc.f32([9, DM])
        m_B = sc.mark()
        for tt in range(9):
            rows = 128 if tt < 8 else NS
            src = xm[tt * 128:(tt + 1) * 128, :] if tt < 8 else xsmp[:, :]
            P.op("sp", lambda e, tt=tt, rows=rows, src=src: e.dma_start(out=hres[:rows, tt, :], in_=src), w=[("h", tt, cb) for cb in range(4)], dma=True)
        wcur = wload(wview(w_out, 0, 512))
        for cb in range(4):
            Wv, kW = wcur
            if cb + 1 < 4:
                wcur = wload(wview(w_out, (cb + 1) * 512, 512))
            for tt in range(9):
                rows = 128 if tt < 8 else NS
                tcols = slice(tt * 128, tt * 128 + rows)
                bank = (cb * 9 + tt) % 6
                mm_group(pbf[bank][:rows, :], [(YT[:, k, tcols], Wv[:, k, :]) for k in range(16)], r=[kW], w=[("ps", bank)])
                P.op("dve", lambda e, rows=rows, tt=tt, cb=cb, bank=bank: e.tensor_tensor(out=hres[:rows, tt, cb * 512:(cb + 1) * 512], in0=pbf[bank][:rows, :],
                                                                                         in1=hres[:rows, tt, cb * 512:(cb + 1) * 512], op=ALU.add),
                     r=[("ps", bank), ("h", tt, cb)], w=[("h", tt, cb)])
        P.barrier()
        if STOP_AFTER == "B":
            dbg_dump("h", hres.rearrange("p a b -> p (a b)"), [("h", tt) for tt in range(9)])
            finish()
            return nc

        sc.reset(m_B)
        gbc = sc.f32([DM])
        stmp = [sc.f32([512]) for _ in range(2)]
        ST = [sc.f32([4]) for _ in range(2)]
        r2.reset()
        actb = [r2.bf16([2, T]) for _ in range(2)]
        Wd = [r2.bf16([2, DM]) for _ in range(2)]
        XN = [r2.bf16([DM]) for _ in range(2)]
        mT = nT
        P.op("sp", lambda e: e.dma_start(out=gbc, in_=g_ffn.partition_broadcast(128)), w=["gbc"], dma=True)
        stgC = []
        for tt in range(9):
            rows = 128 if tt < 8 else NS
            i = tt % 2
            stgC.append(rms_transpose(hres[:, tt, :], [("h", tt, cb) for cb in range(4)], rows, XN[i], ("XN", i), ST[i], ("ST", i), gbc, "gbc", mT, tt * 128, ("mT", tt), DM, tt))
        rms_pipeline(stgC)
        NFB = DFF // 256
        if STOP_AFTER == "C0":
            P.barrier()
            dbg_dump("mT", mT.rearrange("p a b -> p (a b)"), [("mT", tt) for tt in range(9)])
            dbg_dump("XN0", XN[0], [("XN", 0)])
            dbg_dump("ST0", ST[0], [("ST", 0)])
            dbg_dump("gbc", gbc, ["gbc"])
            dbg_dump("h", hres.rearrange("p a b -> p (a b)"), [("h", tt) for tt in range(9)])
            finish()
            return nc

        gu_slot = {}

        def load_gu(fb):
            slot = wq[0] % 2
            wq[0] += 1
            gu_slot[fb] = slot
            load_w(WG[slot], wview(w_gate, fb * 256, 256), ("W", slot), nobar=True)
            load_w(WU[slot], wview(w_up, fb * 256, 256), ("W", slot), nobar=True)

        def load_d(fb):
            load_w(Wd[fb % 2], w_down[fb * 256:(fb + 1) * 256, :].rearrange("(fl p) c -> p fl c", p=128), ("Wd", fb % 2))

        gctr = [0]

        ffn_blocks = [(0, 348), (348, 696), (696, T)]

        def GU_units(fb):
            slot = gu_slot[fb]
            Wg_, Wu_, kWs = WG[slot], WU[slot], ("W", slot)
            units = []
            for fl in range(2):
                for nb, (lo, hi) in enumerate(ffn_blocks):
                    def unit(fl=fl, nb=nb, lo=lo, hi=hi):
                        n = hi - lo
                        pr = gctr[0] % 2
                        gctr[0] += 1
                        bg, bu = pr * 2, pr * 2 + 1
                        rk = [("mT", tt) for tt in range(lo // 128, min(8, (hi - 1) // 128) + 1)]
                        mm_group(pbf[bg][:, 0:n], [(Wg_[:, k, fl * 128:(fl + 1) * 128], mT[:, k, lo:hi]) for k in range(16)], r=[kWs] + rk, w=[("ps", bg)])
                        mm_group(pbf[bu][:, 0:n], [(Wu_[:, k, fl * 128:(fl + 1) * 128], mT[:, k, lo:hi]) for k in range(16)], r=[kWs] + rk, w=[("ps", bu)])
                        P.op("act", lambda e: e.activation(out=stmp[pr][:, 0:n], in_=pbf[bg][:, 0:n], func=AF.Silu), r=[("ps", bg)], w=[("stmp", pr)])
                        P.op("dve", lambda e: e.tensor_tensor(out=actb[fb % 2][:, fl, lo:hi], in0=pbf[bu][:, 0:n], in1=stmp[pr][:, 0:n], op=ALU.mult),
                             r=[("ps", bu), ("stmp", pr)], w=[("act", fb % 2)])
                    units.append(P.capture(unit))
            return units

        dctr = [0]

        def DN_groups(fb):
            groups = []
            for tt in range(9):
                rows = 128 if tt < 8 else NS
                tcols = slice(tt * 128, tt * 128 + rows)
                for cb in range(4):
                    def grp_(tt=tt, rows=rows, tcols=tcols, cb=cb):
                        bank = 4 + dctr[0] % 4
                        dctr[0] += 1
                        mm_group(pbf[bank][:rows, :], [(actb[fb % 2][:, fl, tcols], Wd[fb % 2][:, fl, cb * 512:(cb + 1) * 512]) for fl in range(2)],
                                 r=[("act", fb % 2), ("Wd", fb % 2)], w=[("ps", bank)])
                        P.op("dve", lambda e: e.tensor_tensor(out=hres[:rows, tt, cb * 512:(cb + 1) * 512], in0=pbf[bank][:rows, :],
                                                              in1=hres[:rows, tt, cb * 512:(cb + 1) * 512], op=ALU.add),
                             r=[("ps", bank), ("h", tt, cb)], w=[("h", tt, cb)])
                    groups.append(P.capture(grp_))
            return groups

        load_gu(0)
        load_d(0)
        for i in range(NFB + 1):
            if i + 1 < NFB:
                load_gu(i + 1)
            gu = GU_units(i) if i < NFB else []
            dn = DN_groups(i - 1) if i >= 1 else []
            nslot = max(len(gu), 1)
            per = -(-len(dn) // nslot)
            for u in range(nslot):
                if u < len(gu):
                    P.ops.extend(gu[u])
                for g_ in dn[u * per:(u + 1) * per]:
                    P.ops.extend(g_)
            if i + 1 < NFB:
                load_d(i + 1)
        P.barrier()
        if STOP_AFTER == "C":
            dbg_dump("h", hres.rearrange("p a b -> p (a b)"), [("h", tt) for tt in range(9)])
            finish()
            return nc

        P.op("sp", lambda e: e.dma_start(out=gbc, in_=g_fin.partition_broadcast(128)), w=["gbc"], dma=True)
        for tt in range(9):
            rows = 128 if tt < 8 else NS
            i = tt % 2
            ht = hres[:, tt, :]
            st = ST[i]
            P.op("act", lambda e, rows=rows, ht=ht, st=st, i=i: e.activation(out=XN[i][:rows, :], in_=ht[:rows, :], func=AF.Square, accum_out=st[:rows, 0:1]),
                 r=[("h", tt, cb) for cb in range(4)], w=[("XN", i), ("ST", i)])
            P.op("act", lambda e, rows=rows, st=st: e.activation(out=st[:rows, 1:2], in_=st[:rows, 0:1], func=AF.Ln, scale=1.0 / DM, bias=EPS), r=[("ST", i)], w=[("ST", i)])
            P.op("act", lambda e, rows=rows, st=st: e.activation(out=st[:rows, 2:3], in_=st[:rows, 1:2], func=AF.Exp, scale=-0.5), r=[("ST", i)], w=[("ST", i)])
            P.op("dve", lambda e, rows=rows, ht=ht, st=st: e.scalar_tensor_tensor(out=ht[:rows, :], in0=ht[:rows, :], scalar=st[:rows, 2:3], op0=ALU.mult, in1=gbc[:rows, :], op1=ALU.mult),
                 r=[("h", tt, cb) for cb in range(4)] + [("ST", i), "gbc"], w=[("h", tt, cb) for cb in range(4)])
            dst = y_m[tt * 128:(tt + 1) * 128, :] if tt < 8 else y_s[:, :]
            ok = ("o_y", tt)
            P.op("sp", lambda e, rows=rows, ht=ht, dst=dst: e.dma_start(out=dst, in_=ht[:rows, :]), r=[("h", tt, cb) for cb in range(4)], w=[ok], dma=True)
            OUTKEYS.append(ok)
        finish()
        return nc


def _host_consts(inputs, core):
    hf = core % 2
    cst = np.zeros((128, CSTW), np.float32)
    r = np.arange(128)
    cst[:, C_ID:C_ID + 128] = np.eye(128, dtype=np.float32)
    cst[:, C_TRI:C_TRI + 128] = (r[:, None] <= r[None, :]).astype(np.float32)
    cst[:, C_U:C_U + 128] = (r[:, None] > r[None, :]).astype(np.float32)
    cst[:, C_ONE:C_ONE + 128] = 1.0
    cw = np.asarray(inputs["ssd_conv_w"])[0]
    cb = np.asarray(inputs["ssd_conv_b"])[0]
    cwb = np.concatenate([cw, cb[None]], 0)
    cst[:, C_CWB:C_CWB + 60] = cwb.reshape(5, 12, 128).transpose(2, 1, 0).reshape(128, 60)
    cst[:, C_GON:C_GON + 8] = np.asarray(inputs["chunk_out_norm_g"])[0].reshape(8, 128).T
    cst[:, C_ALOG:C_ALOG + 16] = np.asarray(inputs["ssd_a_log"])[0][None, :]
    cst[:, C_DTB:C_DTB + 16] = np.asarray(inputs["ssd_dt_bias"])[0][None, :]
    cst[:, C_D:C_D + 16] = np.asarray(inputs["ssd_d"])[0][None, :]
    cst[:, C_W00:C_W00 + 8] = np.asarray(inputs["chunk_w_s"])[0][:, 0, 0][None, :]
    cst[:, C_B0:C_B0 + 8] = np.asarray(inputs["chunk_b_s"])[0][:, 0][None, :]
    cst[:, C_FLAG] = float(hf)
    cst[:, C_DPP:C_DPP + 8] = np.repeat(np.asarray(inputs["ssd_d"])[0], 64).reshape(8, 128).T
    cst[:, C_SGPP:C_SGPP + 8] = np.asarray(inputs["ssd_norm_g"])[0].reshape(8, 128).T
    cst[0:16, C_DTBPP] = np.asarray(inputs["ssd_dt_bias"])[0]
    cst[0:16, C_ALPP] = np.asarray(inputs["ssd_a_log"])[0]
    cst2 = np.zeros((128, CST2W), np.float32)
    cst2[:, C2_BS:C2_BS + 1024] = np.asarray(inputs["chunk_b_s"])[0].reshape(1, 1024)
    rs = np.zeros((16, 8, 128), np.float32)
    for q in range(8):
        for rr in range(128):
            rs[2 * q + rr // 64, q, rr] = 1.0
    cst2[0:16, C2_RSEL:C2_RSEL + 1024] = rs.reshape(16, 1024)
    return cst, cst2


def make_in_maps(inputs):
    xp = np.asarray(inputs["x_prompt"], np.float32)
    xs = np.asarray(inputs["x_sample"], np.float32)
    sconv = np.asarray(inputs["state_conv"], np.float32)[0]
    sssm = np.asarray(inputs["state_ssm"], np.float32)[0]
    shared = {
        "w_in": np.ascontiguousarray(np.asarray(inputs["w_in"], np.float32)[0]),
        "w_out": np.ascontiguousarray(np.asarray(inputs["w_out"], np.float32)[0]),
        "w_gate": np.ascontiguousarray(np.asarray(inputs["w_gate"], np.float32)[0]),
        "w_up": np.ascontiguousarray(np.asarray(inputs["w_up"], np.float32)[0]),
        "w_down": np.ascontiguousarray(np.asarray(inputs["w_down"], np.float32)[0]),
        "ws": np.ascontiguousarray(np.asarray(inputs["chunk_w_s"], np.float32)[0]),
        "g_mix": np.ascontiguousarray(np.asarray(inputs["norm_mix_g"], np.float32)[0]),
        "g_ffn": np.ascontiguousarray(np.asarray(inputs["norm_ffn_g"], np.float32)[0]),
        "g_fin": np.ascontiguousarray(np.asarray(inputs["norm_final_g"], np.float32)),
        "vn_g": np.ascontiguousarray(np.asarray(inputs["chunk_v_norm_g"], np.float32)[0]),
        "ssd_g": np.ascontiguousarray(np.asarray(inputs["ssd_norm_g"], np.float32)[0]),
    }
    zeros_prev = np.zeros((TM, DM), np.float32)
    maps = []
    for c in range(8):
        b, hf = c // 2, c % 2
        cst, cst2 = _host_consts(inputs, c)
        m = dict(shared)
        m["xm"] = np.ascontiguousarray(xp[b, hf * TM:(hf + 1) * TM])
        m["xprev"] = np.ascontiguousarray(xp[b, 0:TM]) if hf == 1 else zeros_prev
        m["xsmp"] = np.ascontiguousarray(xs[c * NS:(c + 1) * NS, 0])
        m["sconv"] = np.ascontiguousarray(sconv[c * NS:(c + 1) * NS])
        m["sssm"] = np.ascontiguousarray(sssm[c * NS:(c + 1) * NS].reshape(NS, 1024, 128))
        m["cst"] = cst
        m["cst2"] = cst2
        maps.append(m)
    return maps


def kernel(**inputs):
    nc = build_program()
    maps = make_in_maps(inputs)
    res = run_bass_kernel_spmd(nc, maps, core_ids=list(range(8)))
    R = res.results
    y_prompt = np.zeros((4, 2048, DM), np.float32)
    y_sample = np.zeros((128, 1, DM), np.float32)
    ncp = np.zeros((1, 4, 3, 1536), np.float32)
    nsp = np.zeros((1, 4, 16, 64, 128), np.float32)
    ncs = np.zeros((1, 128, 3, 1536), np.float32)
    nss = np.zeros((1, 128, 16, 64, 128), np.float32)
    vs = np.zeros((1, 128, 1, 1024), np.float32)
    for c in range(8):
        b, hf = c // 2, c % 2
        y_prompt[b, hf * TM:(hf + 1) * TM] = R[c]["y_m"]
        y_sample[c * NS:(c + 1) * NS, 0] = R[c]["y_s"]
        if hf == 1:
            ncp[0, b] = R[c]["nconv_p"]
            nsp[0, b] = R[c]["nssm_p"].reshape(16, 64, 128)
        ncs[0, c * NS:(c + 1) * NS] = R[c]["nconv_s"]
        nss[0, c * NS:(c + 1) * NS] = R[c]["nssm_s"].reshape(NS, 16, 64, 128)
        vs[0, c * NS:(c + 1) * NS, 0] = R[c]["v_s"]
    return (y_prompt, y_sample, ncp, nsp, ncs, nss, vs)
```
